# Optimizing a Trainium2 kernel written in Bass

```python
import functools
import jax
import jax.numpy as jnp
from jax import lax
import numpy as np

D_MODEL = 1024
BATCH = 1
SEQ = 16384
DEPTH = 1
DEC_BATCH = 32
DEC_SEQ = 1
PAST_LEN = 16384
PAGE_SIZE = 128

NSA_HEADS = 8
NSA_KV_HEADS = 2
NSA_GROUP = NSA_HEADS // NSA_KV_HEADS
HEAD_DIM = 64
CMP_LEN = 32
CMP_STRIDE = 16
SEL_BLOCK = 64
N_SEL = 16
WINDOW = 512
Q_BLOCK = 128
RET_HEADS = 4
RET_DK = 128
RET_DV = 256
RET_CHUNK = 128
D_FF = 4 * D_MODEL

ROPE_THETA = 10000.0
RMS_EPS = 1e-6
GN_EPS = 1e-5
NEG_INF = -1e30
FORCE_SCORE = 1e4

NSA_Q_W = NSA_HEADS * HEAD_DIM
NSA_KV_W = 2 * NSA_KV_HEADS * HEAD_DIM
RET_QK_W = RET_HEADS * RET_DK
RET_V_W = RET_HEADS * RET_DV
SPLITS = (NSA_Q_W, NSA_KV_W, NSA_KV_W, NSA_KV_W, 3 * NSA_HEADS, RET_QK_W, RET_QK_W, RET_V_W, RET_V_W, D_MODEL, D_MODEL)
IN_WIDTH = sum(SPLITS)

kernel_name = "nsa_retention_hybrid_step"


def _split_points():
    pts, acc = [], 0
    for w in SPLITS[:-1]:
        acc += w
        pts.append(acc)
    return pts


def rms_norm(x, g):
    xf = x.astype(jnp.float32)
    y = xf * lax.rsqrt(jnp.mean(xf * xf, axis=-1, keepdims=True) + RMS_EPS)
    return (y * g.astype(jnp.float32)).astype(x.dtype)


def rope(x, pos):
    d = x.shape[-1]
    inv = ROPE_THETA ** (-jnp.arange(0, d, 2, dtype=jnp.float32) / d)
    ang = pos.astype(jnp.float32)[:, None] * inv[None, :]
    cos = jnp.cos(ang)[None, :, None, :]
    sin = jnp.sin(ang)[None, :, None, :]
    xf = x.astype(jnp.float32)
    x1, x2 = xf[..., : d // 2], xf[..., d // 2:]
    return jnp.concatenate([x1 * cos - x2 * sin, x1 * sin + x2 * cos], axis=-1).astype(x.dtype)


def rope_kv(kv, pos):
    return jnp.stack([rope(kv[:, :, 0], pos), kv[:, :, 1]], axis=2)


def nsa_compress(rows, w_cmp, b_cmp):
    B, L = rows.shape[:2]
    rb = rows.reshape(B, L // CMP_STRIDE, CMP_STRIDE, 2, NSA_KV_HEADS, HEAD_DIM).astype(jnp.float32)
    first = jnp.einsum('bnjchd,chjd->bnchd', rb, w_cmp[:, :, :CMP_STRIDE].astype(jnp.float32))
    second = jnp.einsum('bnjchd,chjd->bnchd', rb, w_cmp[:, :, CMP_STRIDE:].astype(jnp.float32))
    return first[:, :-1] + second[:, 1:] + b_cmp.astype(jnp.float32)


def nsa_core(q, qpos, cmp, kc_end, gather_sel, kwv, kwpos, gates):
    B, Q = q.shape[:2]
    qg = q.astype(jnp.float32).reshape(B, Q, NSA_KV_HEADS, NSA_GROUP, HEAD_DIM) * (HEAD_DIM ** -0.5)
    mc = (kc_end[None, :] <= qpos[:, None])[None, :, None, None, :]
    s_c = jnp.einsum('bqhgd,bnhd->bqhgn', qg, cmp[:, :, 0])
    p_c = jax.nn.softmax(jnp.where(mc, s_c, NEG_INF), axis=-1) * mc
    o_c = jnp.einsum('bqhgn,bnhd->bqhgd', p_c, cmp[:, :, 1])
    nc = cmp.shape[1]
    ns = (nc + 1) * CMP_STRIDE // SEL_BLOCK
    r = SEL_BLOCK // CMP_STRIDE
    imp = jnp.pad(p_c.sum(axis=3), ((0, 0), (0, 0), (0, 0), (1, r * ns - nc)))
    p_slc = imp[..., : r * ns].reshape(B, Q, NSA_KV_HEADS, ns, r).sum(-1) + imp[..., r::r]
    blk = jnp.arange(ns)
    qblk = (qpos // SEL_BLOCK)[:, None]
    valid = blk[None, :] * SEL_BLOCK <= qpos[:, None]
    forced = (blk[None, :] == 0) | (blk[None, :] == qblk) | (blk[None, :] == qblk - 1)
    score = jnp.where(valid[None, :, None, :], p_slc + jnp.where(forced, FORCE_SCORE, 0.0)[None, :, None, :], NEG_INF)
    _, idx = lax.top_k(score, min(N_SEL, ns))
    kk = idx.shape[-1]
    k_sel, v_sel = gather_sel(idx)
    tpos = idx[..., None] * SEL_BLOCK + jnp.arange(SEL_BLOCK)
    ms = (tpos <= qpos[None, :, None, None, None])[:, :, :, None]
    s_s = jnp.einsum('bqhgd,bqhksd->bqhgks', qg, k_sel)
    s_s = jnp.where(ms, s_s, NEG_INF).reshape(B, Q, NSA_KV_HEADS, NSA_GROUP, kk * SEL_BLOCK)
    p_s = jax.nn.softmax(s_s, axis=-1)
    o_s = jnp.einsum('bqhgm,bqhmd->bqhgd', p_s, v_sel.reshape(B, Q, NSA_KV_HEADS, kk * SEL_BLOCK, HEAD_DIM))
    kp = kwpos[None, :]
    mw = ((kp <= qpos[:, None]) & (kp > qpos[:, None] - WINDOW) & (kp >= 0))[None, :, None, None, :]
    s_w = jnp.einsum('bqhgd,bwhd->bqhgw', qg, kwv[:, :, 0])
    p_w = jax.nn.softmax(jnp.where(mw, s_w, NEG_INF), axis=-1)
    o_w = jnp.einsum('bqhgw,bwhd->bqhgd', p_w, kwv[:, :, 1])
    g = gates.reshape(B, Q, NSA_KV_HEADS, NSA_GROUP, 3)
    o = g[..., 0:1] * o_c + g[..., 1:2] * o_s + g[..., 2:3] * o_w
    return o.reshape(B, Q, NSA_Q_W)


def nsa_prompt(q, gates, kv_c, kv_s, kv_w, w_cmp, b_cmp):
    B, T = q.shape[:2]
    cmp = nsa_compress(kv_c, w_cmp, b_cmp)
    kc_end = jnp.arange(cmp.shape[1]) * CMP_STRIDE + (CMP_LEN - 1)
    sel_blocks = kv_s.reshape(B, T // SEL_BLOCK, SEL_BLOCK, 2, NSA_KV_HEADS, HEAD_DIM)
    bi = jnp.arange(B)[:, None, None, None]
    hi = jnp.arange(NSA_KV_HEADS)[None, None, :, None]

    def gather_sel(idx):
        g = sel_blocks[bi, idx, :, :, hi, :]
        return g[..., 0, :], g[..., 1, :]

    kw_pad = jnp.pad(kv_w, ((0, 0), (WINDOW, 0), (0, 0), (0, 0), (0, 0)))

    def block(s0):
        qb = lax.dynamic_slice_in_dim(q, s0, Q_BLOCK, axis=1)
        gb = lax.dynamic_slice_in_dim(gates, s0, Q_BLOCK, axis=1)
        qpos = s0 + jnp.arange(Q_BLOCK)
        kwb = lax.dynamic_slice_in_dim(kw_pad, s0, WINDOW + Q_BLOCK, axis=1)
        kwpos = s0 - WINDOW + jnp.arange(WINDOW + Q_BLOCK)
        return nsa_core(qb, qpos, cmp, kc_end, gather_sel, kwb, kwpos, gb)

    out = lax.map(block, jnp.arange(T // Q_BLOCK) * Q_BLOCK)
    new_win = kv_w[:, T - min(WINDOW, T):]
    return out.swapaxes(0, 1).reshape(B, T, NSA_Q_W), new_win


def nsa_sample(q, gates, kv_c, kv_s, kv_w, cache_cmp, cache_sel, cache_win, page_table, w_cmp, b_cmp):
    B, S = q.shape[:2]
    P = page_table.shape[1] * PAGE_SIZE
    s_pad = -(-S // SEL_BLOCK) * SEL_BLOCK
    pad = ((0, 0), (0, s_pad - S), (0, 0), (0, 0), (0, 0))
    past_c = cache_cmp[page_table].reshape(B, P, 2, NSA_KV_HEADS, HEAD_DIM)
    cmp = nsa_compress(jnp.concatenate([past_c, jnp.pad(kv_c, pad).astype(past_c.dtype)], axis=1), w_cmp, b_cmp)
    kc_end = jnp.arange(cmp.shape[1]) * CMP_STRIDE + (CMP_LEN - 1)
    n_past_blk = P // SEL_BLOCK
    bpp = PAGE_SIZE // SEL_BLOCK
    pool = cache_sel.reshape(cache_sel.shape[0], bpp, SEL_BLOCK, 2, NSA_KV_HEADS, HEAD_DIM)
    new_blocks = jnp.pad(kv_s, pad).astype(cache_sel.dtype).reshape(B, s_pad // SEL_BLOCK, SEL_BLOCK, 2, NSA_KV_HEADS, HEAD_DIM)
    bi = jnp.arange(B)[:, None, None, None]
    hi = jnp.arange(NSA_KV_HEADS)[None, None, :, None]

    def gather_sel(idx):
        jp = jnp.clip(idx, 0, n_past_blk - 1)
        phys = page_table[bi, jp // bpp]
        gp = pool[phys, jp % bpp, :, :, hi, :]
        jn = jnp.clip(idx - n_past_blk, 0, new_blocks.shape[1] - 1)
        gn = new_blocks[bi, jn, :, :, hi, :]
        g = jnp.where((idx < n_past_blk)[..., None, None, None], gp, gn)
        return g[..., 0, :], g[..., 1, :]

    wb = cache_win.shape[1]
    kwv = jnp.concatenate([cache_win, kv_w.astype(cache_win.dtype)], axis=1)
    kwpos = P - wb + jnp.arange(wb + S)
    qpos = P + jnp.arange(S)
    o = nsa_core(q, qpos, cmp, kc_end, gather_sel, kwv, kwpos, gates)
    return o, kwv[:, S:]


def ret_log_decay():
    return jnp.log1p(-jnp.exp2(-5.0 - jnp.arange(RET_HEADS, dtype=jnp.float32)))


def retention_chunk(S, q, k, v):
    lg = ret_log_decay()
    C = q.shape[1]
    i = jnp.arange(C, dtype=jnp.float32)
    diff = i[:, None] - i[None, :]
    D = jnp.where(diff >= 0, jnp.exp(jnp.maximum(diff, 0.0)[None] * lg[:, None, None]), 0.0)
    qf, kf, vf = q.astype(jnp.float32), k.astype(jnp.float32), v.astype(jnp.float32)
    att = jnp.einsum('bihd,bjhd->bhij', qf, kf) * D
    o = jnp.einsum('bhij,bjhe->bihe', att, vf)
    q_dec = jnp.exp((i[:, None] + 1.0) * lg[None, :])
    o = o + jnp.einsum('bihd,bhde->bihe', qf * q_dec[None, :, :, None], S)
    k_dec = jnp.exp((C - 1.0 - i)[:, None] * lg[None, :])
    S_new = jnp.exp(C * lg)[None, :, None, None] * S + jnp.einsum('bjhd,bjhe->bhde', kf * k_dec[None, :, :, None], vf)
    return S_new, o


def retention_prompt(q, k, v):
    B, T = q.shape[:2]
    n = T // RET_CHUNK

    def chunks(a):
        return a.reshape(B, n, RET_CHUNK, a.shape[2], a.shape[3]).swapaxes(0, 1)

    S0 = jnp.zeros((B, RET_HEADS, RET_DK, RET_DV), jnp.float32)

    def step(S, xs):
        return retention_chunk(S, *xs)

    S, o = lax.scan(step, S0, (chunks(q), chunks(k), chunks(v)))
    return o.swapaxes(0, 1).reshape(B, T, RET_HEADS, RET_DV), S


def retention_sample(q, k, v, state):
    S_new, o = retention_chunk(state.astype(jnp.float32), q, k, v)
    return o, S_new


def retention_out(o, gate, gn_w, gn_b):
    B, T = o.shape[:2]
    mu = jnp.mean(o, axis=-1, keepdims=True)
    var = jnp.mean(jnp.square(o - mu), axis=-1, keepdims=True)
    y = ((o - mu) * lax.rsqrt(var + GN_EPS)).reshape(B, T, RET_V_W)
    y = y * gn_w.astype(jnp.float32) + gn_b.astype(jnp.float32)
    return jax.nn.silu(gate.astype(jnp.float32)) * y


def decoder_layer(x, c, pos, nsa_fn, ret_fn, norm_mix, norm_mlp, w_ada, b_ada, w_in,
                  ret_gn_w, ret_gn_b, w_branch_nsa, w_branch_ret, w_out, w_up, w_down):
    B, T, _ = x.shape
    mod = (jax.nn.silu(c) @ w_ada + b_ada)[:, None, :]
    sh_a, sc_a, gt_a, sh_f, sc_f, gt_f = jnp.split(mod, 6, axis=-1)
    h = rms_norm(x, norm_mix) * (1.0 + sc_a) + sh_a
    z = h @ w_in
    q, kv_c, kv_s, kv_w, g_nsa, rq, rk, rv, rg, ga, gb = jnp.split(z, _split_points(), axis=-1)
    q = rope(q.reshape(B, T, NSA_HEADS, HEAD_DIM), pos)
    kv_c = rope_kv(kv_c.reshape(B, T, 2, NSA_KV_HEADS, HEAD_DIM), pos)
    kv_s = rope_kv(kv_s.reshape(B, T, 2, NSA_KV_HEADS, HEAD_DIM), pos)
    kv_w = rope_kv(kv_w.reshape(B, T, 2, NSA_KV_HEADS, HEAD_DIM), pos)
    g_nsa = jax.nn.sigmoid(g_nsa.reshape(B, T, NSA_HEADS, 3).astype(jnp.float32))
    o_nsa, new_win = nsa_fn(q, g_nsa, kv_c, kv_s, kv_w)
    rq = rope(rq.reshape(B, T, RET_HEADS, RET_DK), pos)
    rk = rope(rk.reshape(B, T, RET_HEADS, RET_DK), pos) * (RET_DK ** -0.5)
    rv = rv.reshape(B, T, RET_HEADS, RET_DV)
    o_ret, new_S = ret_fn(rq, rk, rv)
    y_ret = retention_out(o_ret, rg, ret_gn_w, ret_gn_b)
    y_a = o_nsa.astype(x.dtype) @ w_branch_nsa
    y_b = y_ret.astype(x.dtype) @ w_branch_ret
    mixed = (jax.nn.sigmoid(ga) * y_a + jax.nn.sigmoid(gb) * y_b) @ w_out
    x = x + gt_a * mixed
    h = rms_norm(x, norm_mlp) * (1.0 + sc_f) + sh_f
    x = x + gt_f * (jnp.square(jax.nn.relu(h @ w_up)) @ w_down)
    return x, kv_c, kv_s, new_win, new_S


def setup_inputs(seed: int = 0) -> dict:
    key = jax.random.key(seed)
    ks = jax.random.split(key, 24)
    f32 = jnp.float32
    n_pages = PAST_LEN // PAGE_SIZE
    n_used = DEC_BATCH * n_pages
    n_phys = n_used + n_used // 4
    win_buf = min(WINDOW, PAST_LEN)

    def nrm(k, shape, s=1.0):
        return s * jax.random.normal(k, shape, f32)

    page_table = jax.random.permutation(ks[8], n_phys)[:n_used].reshape(DEC_BATCH, n_pages).astype(jnp.int32)
    return {
        'x_prompt': nrm(ks[0], (BATCH, SEQ, D_MODEL)),
        'x_sample': nrm(ks[1], (DEC_BATCH, DEC_SEQ, D_MODEL)),
        'c_prompt': nrm(ks[2], (BATCH, D_MODEL)),
        'c_sample': nrm(ks[3], (DEC_BATCH, D_MODEL)),
        'cache_cmp_kv': nrm(ks[4], (DEPTH, n_phys, PAGE_SIZE, 2, NSA_KV_HEADS, HEAD_DIM)),
        'cache_sel_kv': nrm(ks[5], (DEPTH, n_phys, PAGE_SIZE, 2, NSA_KV_HEADS, HEAD_DIM)),
        'cache_win_kv': nrm(ks[6], (DEPTH, DEC_BATCH, win_buf, 2, NSA_KV_HEADS, HEAD_DIM)),
        'state_ret': nrm(ks[7], (DEPTH, DEC_BATCH, RET_HEADS, RET_DK, RET_DV), 0.1),
        'page_table': page_table,
        'norm_mix': 1.0 + nrm(ks[9], (DEPTH, D_MODEL), 0.02),
        'norm_mlp': 1.0 + nrm(ks[10], (DEPTH, D_MODEL), 0.02),
        'norm_final': 1.0 + nrm(ks[11], (D_MODEL,), 0.02),
        'w_ada': nrm(ks[12], (DEPTH, D_MODEL, 6 * D_MODEL), 0.2 * D_MODEL ** -0.5),
        'b_ada': nrm(ks[13], (DEPTH, 6 * D_MODEL), 0.02),
        'w_in': nrm(ks[14], (DEPTH, D_MODEL, IN_WIDTH), D_MODEL ** -0.5),
        'w_cmp': nrm(ks[15], (DEPTH, 2, NSA_KV_HEADS, CMP_LEN, HEAD_DIM), CMP_LEN ** -0.5),
        'b_cmp': nrm(ks[16], (DEPTH, 2, NSA_KV_HEADS, HEAD_DIM), 0.02),
        'ret_gn_w': 1.0 + nrm(ks[17], (DEPTH, RET_V_W), 0.02),
        'ret_gn_b': nrm(ks[18], (DEPTH, RET_V_W), 0.02),
        'w_branch_nsa': nrm(ks[19], (DEPTH, NSA_Q_W, D_MODEL), NSA_Q_W ** -0.5),
        'w_branch_ret': nrm(ks[20], (DEPTH, RET_V_W, D_MODEL), RET_V_W ** -0.5),
        'w_out': nrm(ks[21], (DEPTH, D_MODEL, D_MODEL), D_MODEL ** -0.5),
        'w_up': nrm(ks[22], (DEPTH, D_MODEL, D_FF), D_MODEL ** -0.5),
        'w_down': nrm(ks[23], (DEPTH, D_FF, D_MODEL), D_FF ** -0.5),
    }


def reference(x_prompt, x_sample, c_prompt, c_sample, cache_cmp_kv, cache_sel_kv, cache_win_kv, state_ret,
              page_table, norm_mix, norm_mlp, norm_final, w_ada, b_ada, w_in, w_cmp, b_cmp, ret_gn_w, ret_gn_b,
              w_branch_nsa, w_branch_ret, w_out, w_up, w_down):
    T = x_prompt.shape[1]
    S = x_sample.shape[1]
    P = page_table.shape[1] * PAGE_SIZE
    pos_p = jnp.arange(T)
    pos_s = P + jnp.arange(S)
    xp, xs = x_prompt, x_sample
    cmp_p, sel_p, win_p, ret_p = [], [], [], []
    cmp_s, sel_s, win_s, ret_s = [], [], [], []
    for l in range(DEPTH):
        lw = (norm_mix[l], norm_mlp[l], w_ada[l], b_ada[l], w_in[l], ret_gn_w[l], ret_gn_b[l],
              w_branch_nsa[l], w_branch_ret[l], w_out[l], w_up[l], w_down[l])
        nsa_p = functools.partial(nsa_prompt, w_cmp=w_cmp[l], b_cmp=b_cmp[l])
        xp, a, b, c, d = decoder_layer(xp, c_prompt, pos_p, nsa_p, retention_prompt, *lw)
        cmp_p.append(a); sel_p.append(b); win_p.append(c); ret_p.append(d)
        nsa_s = functools.partial(nsa_sample, cache_cmp=cache_cmp_kv[l], cache_sel=cache_sel_kv[l],
                                  cache_win=cache_win_kv[l], page_table=page_table, w_cmp=w_cmp[l], b_cmp=b_cmp[l])
        ret_fn = functools.partial(retention_sample, state=state_ret[l])
        xs, a, b, c, d = decoder_layer(xs, c_sample, pos_s, nsa_s, ret_fn, *lw)
        cmp_s.append(a); sel_s.append(b); win_s.append(c); ret_s.append(d)
    y_prompt = rms_norm(xp, norm_final)
    y_sample = rms_norm(xs, norm_final)
    return (y_prompt, y_sample,
            jnp.stack(cmp_p), jnp.stack(sel_p), jnp.stack(win_p), jnp.stack(ret_p),
            jnp.stack(cmp_s), jnp.stack(sel_s), jnp.stack(win_s), jnp.stack(ret_s))
```

```python
import math
import os
from contextlib import ExitStack

import numpy as np
import ml_dtypes

import concourse.bass as bass
import concourse.mybir as mybir
from concourse.bass_utils import run_bass_kernel_spmd

F32 = mybir.dt.float32
BF16 = mybir.dt.bfloat16
I32 = mybir.dt.int32
ALU = mybir.AluOpType
ACTF = mybir.ActivationFunctionType
AX = mybir.AxisListType

ENGS = ("pe", "act", "dve", "pool", "sp")
NSLOT = 12

D = 1024
T = 16384
NT = T // 128
NB = 16
NCORE = 8
SB = 4
P_PAST = 16384
W_IN = 6424
C_Q, C_KVC, C_KVS, C_KVW, C_G, C_RQ, C_RK, C_RV, C_RG, C_GA, C_GB = (
    0, 512, 768, 1024, 1280, 1304, 1816, 2328, 3352, 4376, 5400)
NEG = -30000.0
RMS_EPS = 1e-6
GN_EPS = 1e-5
LG = [math.log1p(-2.0 ** (-5.0 - h)) for h in range(4)]
RET_SCALE = 128 ** -0.5


class Dep:
    __slots__ = ("w", "r", "excl")

    def __init__(self, excl=False):
        self.w = None
        self.r = []
        self.excl = excl


class Prog:
    def __init__(self, nc, stack):
        self.nc = nc
        self.stack = stack
        self.ops = {e: [] for e in ENGS}
        self.cnt = {e: 0 for e in ENGS}
        self.esem = {e: stack.enter_context(nc.semaphore("es_" + e)) for e in ENGS}
        self.dq = {}
        for q in ("sp", "act", "pool"):
            sl = [stack.enter_context(nc.semaphore("ds_%s_%d" % (q, i))) for i in range(NSLOT)]
            self.dq[q] = {"sems": sl, "cnt": [0] * NSLOT, "next": 0}
        self.waited = {e: {} for e in ENGS}
        self.nuid = 0

    def sb(self, shape, dt=F32, stack=None):
        self.nuid += 1
        return (stack or self.stack).enter_context(self.nc.sbuf_tensor("sb%d" % self.nuid, list(shape), dt))

    def ps(self, shape, dt=F32, stack=None):
        self.nuid += 1
        return (stack or self.stack).enter_context(self.nc.psum_tensor("ps%d" % self.nuid, list(shape), dt))

    def _need(self, eng, tok, waits):
        if tok is None:
            return
        kind, key, val = tok
        if kind == "e" and key == eng and eng == "pe":
            return
        k = (kind, key)
        if self.waited[eng].get(k, 0) >= val:
            return
        self.waited[eng][k] = val
        waits[k] = max(waits.get(k, 0), val)

    def _gather(self, eng, reads, writes):
        waits = {}
        for d in reads:
            self._need(eng, d.w, waits)
            if d.excl:
                for t in d.r:
                    if not (t[0] == "e" and t[1] == eng):
                        self._need(eng, t, waits)
        for d in writes:
            self._need(eng, d.w, waits)
            for t in d.r:
                self._need(eng, t, waits)
        return waits

    def _commit(self, tok, reads, writes):
        for d in reads:
            if tok[0] == "e":
                d.r = [t for t in d.r if not (t[0] == "e" and t[1] == tok[1])]
            d.r.append(tok)
        for d in writes:
            d.w = tok
            d.r = []

    def _semof(self, k):
        kind, key = k
        if kind == "e":
            return self.esem[key]
        q, i = key
        return self.dq[q]["sems"][i]

    def op(self, eng, fn, reads=(), writes=()):
        waits = self._gather(eng, reads, writes)
        self.cnt[eng] += 1
        tok = ("e", eng, self.cnt[eng])
        self.ops[eng].append(([(self._semof(k), v) for k, v in waits.items()], fn, (self.esem[eng], 1)))
        self._commit(tok, reads, writes)
        return tok

    def dma(self, q, fn, reads=(), writes=()):
        st = self.dq[q]
        i = st["next"]
        st["next"] = (i + 1) % NSLOT
        waits = self._gather(q, reads, writes)
        if st["cnt"][i] > 0:
            self._need(q, ("d", (q, i), st["cnt"][i]), waits)
        st["cnt"][i] += 16
        tok = ("d", (q, i), st["cnt"][i])
        self.ops[q].append(([(self._semof(k), v) for k, v in waits.items()], fn, (st["sems"][i], 16)))
        self._commit(tok, reads, writes)
        return tok

    def barrier(self):
        toks = [("e", e, self.cnt[e]) for e in ENGS if self.cnt[e] > 0]
        for q, st in self.dq.items():
            for i in range(NSLOT):
                if st["cnt"][i] > 0:
                    toks.append(("d", (q, i), st["cnt"][i]))
        for eng in ENGS:
            waits = {}
            for t in toks:
                if t[0] == "e" and t[1] == eng:
                    continue
                self._need(eng, t, waits)
            if waits:
                self.ops[eng].append(([(self._semof(k), v) for k, v in waits.items()], None, None))

    def emit(self):
        nc = self.nc
        ops = self.ops
        with nc.Block() as block:
            def run(handle, lst):
                for waits, fn, inc in lst:
                    for sem, v in waits:
                        handle.wait_ge(sem, v)
                    if fn is not None:
                        fn(handle).then_inc(inc[0], inc[1])

            @block.tensor
            def _(e):
                run(e, ops["pe"])

            @block.scalar
            def _(e):
                run(e, ops["act"])

            @block.vector
            def _(e):
                run(e, ops["dve"])

            @block.gpsimd
            def _(e):
                run(e, ops["pool"])

            @block.sync
            def _(e):
                run(e, ops["sp"])


def bc(ap, shape):
    return ap.to_broadcast(list(shape))


def build(stage=99):
    nc = bass.Bass("TRN2", target_bir_lowering=False)

    def din(name, shape, dt=F32):
        return nc.dram_tensor(name, list(shape), dt, kind="ExternalInput").ap()

    def dout(name, shape, dt=F32):
        return nc.dram_tensor(name, list(shape), dt, kind="ExternalOutput").ap()

    def dscr(name, shape, dt=F32):
        return nc.dram_tensor(name, list(shape), dt, kind="Internal").ap()

    x_all = din("x_all", [T, D])
    c_all = din("c_all", [128, 40])
    rope_all = din("rope_all", [T, 192])
    w_ada = din("w_ada", [D, 6 * D])
    b_ada = din("b_ada", [1, 6 * D])
    w_in = din("w_in", [D, W_IN])
    norm_mix = din("norm_mix", [1, D])
    norm_mlp = din("norm_mlp", [1, D])
    ident_d = din("ident", [128, 128])
    sel5_d = din("sel5", [5, 132])
    inda_d = din("inda", [128, 8])
    wrep_d = din("wrep", [128, 512])
    bcol_d = din("bcol", [128, 2])
    kdsc_d = din("kdsc", [128, 4])
    oh_d = din("oh", [128, 8])

    x_own = din("x_own", [NB * 128, D])
    rope_own = din("rope_own", [NB * 128, 192])
    cmask_d = din("cmask", [NB, 128, 1024])
    forceb_d = din("forceb", [NB, 128, 256])
    dmask_d = din("dmask", [128, 8 * 128])
    wmask_d = din("wmask", [128, 24 * 128])
    ovp_d = din("ovp", [128, 8 * 256])
    dtab_d = din("dtab", [128, 512])
    qdt_d = din("qdt", [128, 512])
    gnw_d = din("gnw", [1, D])
    gnb_d = din("gnb", [1, D])
    DBG = (stage == 2)
    DBG_NBLK = int(os.environ.get('K_NBLK', '3'))
    PART = int(os.environ.get('K_PART', '9'))
    SUB = int(os.environ.get('K_SUB', '9'))
    SUB2 = int(os.environ.get('K_SUB2', '9'))
    hT_d = dscr("hT_d", [NB, 128, 8, 128], BF16)
    yrT_d = dscr("yrT_d", [NB, 128, 8, 128], BF16)
    onT_d = dscr("onT_d", [NB, 128, 4, 128], BF16)
    if DBG:
        dbg_selb = dout("dbg_selb", [NB, 2, 128, 256])
        dbg_on = dout("dbg_on", [NB, 128, 512])
        dbg_yr = dout("dbg_yr", [NB, 128, 1024])

    w_bn = din("w_bn", [512, D])
    w_br = din("w_br", [D, D])
    w_out = din("w_out", [D, D])
    w_up = din("w_up", [D, 4 * D])
    w_down = din("w_down", [4 * D, D])
    norm_final_d = din("norm_final", [1, D])
    x_smp = din("x_smp", [SB, D])
    x1_d = dscr("x1_d", [NB * 128, D])
    x1s_d = dscr("x1s_d", [SB, D])
    hTs_d = dscr("hTs_d", [128, 8, SB], BF16)
    onTs_d = dscr("onTs_d", [128, 4, SB], BF16)
    yrTs_d = dscr("yrTs_d", [128, 8, SB], BF16)
    y_own = dout("y_own", [NB * 128, D])
    y_smp = dout("y_smp", [SB, D])

    smallc_d = din("smallc", [1, 1024])
    oh4_d = din("oh4", [SB, 4])
    ohm_d = din("ohm", [128, 16])
    ovs_d = din("ovs", [128, 8 * 257])
    wrow_d = din("wrow", [1, 2 * 16 * 256])
    brow_d = din("brow", [1, 256])
    rope_smp = din("rope_smp", [SB, 192])
    ptab = din("ptab", [SB, 128], I32)
    cache_cmp = din("cache_cmp", [5120 * 8, 4096])
    cache_sel = din("cache_sel", [5120 * 8, 4096])
    cache_win = din("cache_win", [SB, 512, 256])
    state_s = din("state_s", [SB, 4, 128, 256])
    os_d = dscr("os_d", [SB, 3, 8, 64])
    o_cmp_s = dout("o_cmp_s", [SB, 256])
    o_sel_s = dout("o_sel_s", [SB, 256])
    o_win_s = dout("o_win_s", [SB, 512, 256])
    o_state_s = dout("o_state_s", [SB, 4, 128, 256])

    o_cmp = dout("o_cmp", [T, 256])
    o_sel = dout("o_sel", [T, 256])
    o_win = dout("o_win", [512, 256])
    o_state = dout("o_state", [4, 128, 256])

    kwT_d = dscr("kwT_d", [128, T], BF16)
    vw_d = dscr("vw_d", [T, 130], BF16)
    ksT_d = dscr("ksT_d", [128, T], BF16)
    vs_d = dscr("vs_d", [T, 130], BF16)
    sown_d = dscr("sown_d", [NB, 128, 1024])
    mods_d = dscr("mods_d", [SB, 6 * D])

    with ExitStack() as st:
        P = Prog(nc, st)
        ident = P.sb([128, 128]); d_ident = Dep()
        identb = P.sb([128, 128], BF16); d_identb = Dep()
        modP4 = P.sb([128, 4, D]); d_modP = Dep()
        G1P = P.sb([128, D]); d_G1P = Dep()
        G2P = P.sb([128, D]); d_G2P = Dep()
        cmpKT = P.sb([128, 1024], BF16); d_cmpKT = Dep()
        cmpVA = P.sb([128, 8, 2, 65], BF16); d_cmpVA = Dep()
        epsT = P.sb([128, 1]); d_eps = Dep()

        banks = [P.ps([128, 512]) for _ in range(6)]
        pT = P.ps([128, 1024]); d_pTs = [Dep(True), Dep(True)]
        d_bank = [Dep(True) for _ in range(6)]

        P.dma("sp", lambda e: e.dma_start(out=ident[:], in_=ident_d), writes=[d_ident])
        P.op("dve", lambda e: e.tensor_copy(out=identb[:], in_=ident[:]), reads=[d_ident], writes=[d_identb])
        P.op("dve", lambda e: e.memset(epsT[:], RMS_EPS), writes=[d_eps])
        P.op("pool", lambda e: e.memset(cmpVA[:].rearrange("p a b c -> p (a b c)"), 1.0), writes=[d_cmpVA])

        with ExitStack() as s0:
            cT = P.sb([128, 8, 5], F32, s0); d_cT = Dep()
            scT = P.sb([128, 8, 5], BF16, s0); d_scT = Dep()
            sel5 = P.sb([5, 132], F32, s0); d_sel5 = Dep()
            mod5 = P.sb([5, 6 * D], F32, s0); d_mod5 = Dep()
            bada = P.sb([5, 6 * D], F32, s0); d_bada = Dep()
            gmix = P.sb([128, D], F32, s0); d_gmix = Dep()
            gmlp = P.sb([128, D], F32, s0); d_gmlp = Dep()
            wa = [P.sb([128, 8, 512], BF16, s0) for _ in range(2)]; d_wa = [Dep(), Dep()]

            P.dma("sp", lambda e: e.dma_start(out=cT[:].rearrange("p c b -> p (c b)"), in_=c_all), writes=[d_cT])
            P.dma("sp", lambda e: e.dma_start(out=sel5[:], in_=sel5_d), writes=[d_sel5])
            P.dma("sp", lambda e: e.dma_start(out=bada[:], in_=b_ada.partition_broadcast(5)[:, 0, :]), writes=[d_bada])
            P.dma("sp", lambda e: e.dma_start(out=gmix[:], in_=norm_mix.partition_broadcast(128)[:, 0, :]), writes=[d_gmix])
            P.dma("sp", lambda e: e.dma_start(out=gmlp[:], in_=norm_mlp.partition_broadcast(128)[:, 0, :]), writes=[d_gmlp])
            P.op("act", lambda e: e.activation(out=scT[:], in_=cT[:], func=ACTF.Silu), reads=[d_cT], writes=[d_scT])
            for g in range(12):
                b = g % 2
                P.dma("pool", lambda e, g=g, b=b: e.dma_start(
                    out=wa[b][:], in_=w_ada[:, g * 512:(g + 1) * 512].rearrange("(c p) n -> p c n", p=128)),
                    writes=[d_wa[b]])
                for k in range(8):
                    P.op("pe", lambda e, k=k, b=b: e.matmul(out=banks[0][0:5, :], lhsT=scT[:, k, :], rhs=wa[b][:, k, :],
                                                           start=(k == 0), stop=(k == 7)),
                         reads=[d_scT, d_wa[b]], writes=[d_bank[0]])
                P.op("dve", lambda e, g=g: e.tensor_tensor(out=mod5[:, g * 512:(g + 1) * 512], in0=banks[0][0:5, :],
                                                          in1=bada[:, g * 512:(g + 1) * 512], op=ALU.add),
                     reads=[d_bank[0], d_bada], writes=[d_mod5])
            P.dma("sp", lambda e: e.dma_start(out=mods_d, in_=mod5[1:5, :]), reads=[d_mod5])
            slot = {0: 0, 2: 1, 3: 2, 5: 3}
            for g in range(12):
                ch, hf_ = g // 2, (g % 2) * 512
                P.op("pe", lambda e, g=g: e.matmul(out=banks[1][:, :], lhsT=sel5[:, 0:128], rhs=mod5[:, g * 512:(g + 1) * 512],
                                                   start=True, stop=True), reads=[d_sel5, d_mod5], writes=[d_bank[1]])
                if ch in slot:
                    P.op("act", lambda e, ch=ch, hf_=hf_: e.copy(out=modP4[:, slot[ch], hf_:hf_ + 512], in_=banks[1][:, :]),
                         reads=[d_bank[1]], writes=[d_modP])
                else:
                    Gt, dG, gn, dgn = (G1P, d_G1P, gmix, d_gmix) if ch == 1 else (G2P, d_G2P, gmlp, d_gmlp)
                    P.op("dve", lambda e, Gt=Gt, gn=gn, hf_=hf_: e.scalar_tensor_tensor(out=Gt[:, hf_:hf_ + 512], in0=banks[1][:, :], scalar=1.0,
                                                                                         in1=gn[:, hf_:hf_ + 512], op0=ALU.add, op1=ALU.mult),
                         reads=[d_bank[1], dgn], writes=[dG])
            P.barrier()

        def norm_mod_T(xt, d_xt, ntok, G, d_G, shift, d_shift, wk, hT, d_hT):
            junk, d_junk, ss, d_ss, rstd, d_rstd, hm, d_hm, hf, d_hf = wk
            P.op("act", lambda e: e.activation(out=junk[0:ntok, :], in_=xt[0:ntok, :], func=ACTF.Square, accum_out=ss[0:ntok, :]),
                 reads=[d_xt], writes=[d_junk, d_ss])
            P.op("act", lambda e: e.activation(out=rstd[0:ntok, :], in_=ss[0:ntok, :], func=ACTF.Sqrt, bias=epsT[0:ntok, :], scale=1.0 / D),
                 reads=[d_ss, d_eps], writes=[d_rstd])
            P.op("dve", lambda e: e.reciprocal(out=rstd[0:ntok, :], in_=rstd[0:ntok, :]), reads=[d_rstd], writes=[d_rstd])
            P.op("dve", lambda e: e.scalar_tensor_tensor(out=hm[0:ntok, :], in0=xt[0:ntok, :], scalar=rstd[0:ntok, 0:1], in1=G[0:ntok, :],
                                                        op0=ALU.mult, op1=ALU.mult),
                 reads=[d_xt, d_rstd, d_G], writes=[d_hm])
            P.op("pool", lambda e: e.tensor_tensor(out=hf[0:ntok, :], in0=hm[0:ntok, :], in1=shift[0:ntok, :], op=ALU.add),
                 reads=[d_hm, d_shift], writes=[d_hf])
            for c in range(8):
                P.op("pe", lambda e, c=c: e.transpose(out=pT[:, c * 128:c * 128 + ntok], in_=hf[0:ntok, c * 128:(c + 1) * 128],
                                                      identity=ident[0:ntok, 0:ntok]),
                     reads=[d_hf, d_ident], writes=[d_pTs[c // 4]])
            pv = pT[:].rearrange("p (c t) -> p c t", t=128)
            P.op("act", lambda e: e.copy(out=hT[:, 0:4, 0:ntok], in_=pv[:, 0:4, 0:ntok]), reads=[d_pTs[0]], writes=[d_hT])
            P.op("dve", lambda e: e.tensor_copy(out=hT[:, 4:8, 0:ntok], in_=pv[:, 4:8, 0:ntok]), reads=[d_pTs[1]], writes=[d_hT])

        junk_sh = P.sb([128, D], BF16); d_junk_sh = Dep()

        def mk_wk(stack, n=128):
            hm_ = P.sb([n, D], F32, stack); d_hm_ = Dep()
            return (junk_sh, d_junk_sh, P.sb([n, 1], F32, stack), Dep(), P.sb([n, 1], F32, stack), Dep(),
                    hm_, d_hm_, hm_, d_hm_)

        def rope(src4, dst4, cos, sin, shp, tmps, d_src, d_dst, d_rp, d_tmps):
            ta, tb, tc, td = tmps
            cb, sbb = cos, sin
            P.op("dve", lambda e: e.tensor_tensor(out=ta, in0=src4(0), in1=cb, op=ALU.mult), reads=[d_src, d_rp], writes=[d_tmps[0]])
            P.op("dve", lambda e: e.tensor_tensor(out=tb, in0=src4(1), in1=sbb, op=ALU.mult), reads=[d_src, d_rp], writes=[d_tmps[1]])
            P.op("dve", lambda e: e.tensor_tensor(out=tc, in0=src4(0), in1=sbb, op=ALU.mult), reads=[d_src, d_rp], writes=[d_tmps[2]])
            P.op("dve", lambda e: e.tensor_tensor(out=td, in0=src4(1), in1=cb, op=ALU.mult), reads=[d_src, d_rp], writes=[d_tmps[3]])
            P.op("pool", lambda e: e.tensor_tensor(out=dst4(0), in0=ta, in1=tb, op=ALU.subtract), reads=[d_tmps[0], d_tmps[1]], writes=[d_dst])
            P.op("pool", lambda e: e.tensor_tensor(out=dst4(1), in0=tc, in1=td, op=ALU.add), reads=[d_tmps[2], d_tmps[3]], writes=[d_dst])

        def make_attn(pts, d_pts):
            scb = [banks[0], banks[1]]
            d_scb = [d_bank[0], d_bank[1]]
            slot = [0]

            def attn(nq, ng, QTh, d_Q, tiles, o4v, d_o4, pslc=None, first=True, last=True):
                ncols = nq * ng
                nt_ = len(tiles)
                for ti, tl in enumerate(tiles):
                    s_ = slot[0]; slot[0] ^= 1
                    nk = tl["nk"]
                    sc, d_sc, pt, d_pt = scb[s_], d_scb[s_], pts[s_], d_pts[s_]
                    nb = len(tl["biases"])
                    P.op("pe", lambda e, sc=sc, tl=tl, nk=nk, nb=nb: e.matmul(out=sc[0:nk, 0:ncols], lhsT=tl["KT"], rhs=QTh,
                                                                           start=True, stop=(nb == 0)),
                         reads=[d_Q] + tl["deps"], writes=[d_sc])
                    for bi, (bl, br_) in enumerate(tl["biases"]):
                        P.op("pe", lambda e, sc=sc, bl=bl, br_=br_, nk=nk, bi=bi, nb=nb: e.matmul(
                            out=sc[0:nk, 0:ncols], lhsT=bl, rhs=br_, start=False, stop=(bi == nb - 1)),
                            reads=tl["deps"], writes=[d_sc])
                    P.op("act", lambda e, sc=sc, pt=pt, nk=nk: e.activation(out=pt[0:nk, 0:ncols], in_=sc[0:nk, 0:ncols], func=ACTF.Exp),
                         reads=[d_sc], writes=[d_pt])
                    for g in range(ng):
                        P.op("pe", lambda e, pt=pt, tl=tl, nk=nk, g=g, ti=ti: e.matmul(
                            out=o4v[:, g, :], lhsT=pt[0:nk, g * nq:(g + 1) * nq], rhs=tl["V"],
                            start=(first and ti == 0 and g == 0), stop=(last and ti == nt_ - 1), skip_group_check=True),
                            reads=[d_pt] + tl["deps"], writes=[d_o4])
                    if pslc is not None:
                        psv, d_ps, gpb = pslc
                        for g in range(ng):
                            P.op("pe", lambda e, pt=pt, tl=tl, nk=nk, g=g, ti=ti: e.matmul(
                                out=psv(g), lhsT=pt[0:nk, g * nq:(g + 1) * nq], rhs=tl["Ov"],
                                start=(ti == 0 and g % gpb == 0), stop=(ti == nt_ - 1), skip_group_check=True),
                                reads=[d_pt] + tl["deps"], writes=[d_ps[g // gpb]])


            return attn

        def phase_b1():
            with ExitStack() as sB:
                WB = P.sb([128, 8, 4376], BF16, sB); d_WB = Dep()
                wsegs = []
                for g in range(4):
                    wsegs += [(C_Q + g * 64, 64, g * 128), (C_Q + (4 + g) * 64, 64, g * 128 + 64)]
                wsegs += [(C_RQ, 512, 512), (C_RK, 512, 1024), (C_RV, 1024, 1536), (C_RG, 1024, 2560), (C_G, 24, 3584),
                          (C_KVC, 128, 3608), (C_KVS, 128, 3736), (C_KVW, 128, 3864),
                          (C_KVC + 128, 128, 3992), (C_KVS + 128, 128, 4120), (C_KVW + 128, 128, 4248)]
                for (src, n, dst) in wsegs:
                    P.dma("pool", lambda e, src=src, n=n, dst=dst: e.dma_start(
                        out=WB[:, :, dst:dst + n], in_=w_in[:, src:src + n].rearrange("(c p) n -> p c n", p=128)), writes=[d_WB])
                dmaskB = P.sb([128, 8, 128], BF16, sB); d_dmask = Dep()
                wmaskB = P.sb([128, 24, 128], BF16, sB); d_wmask = Dep()
                OvP = P.sb([128, 8, 256], BF16, sB); d_OvP = Dep()
                dtab = P.sb([128, 4, 128], F32, sB); d_dtab = Dep()
                qdt = P.sb([128, 4, 128], F32, sB); d_qdt = Dep()
                gnw = P.sb([128, D], F32, sB); d_gnw = Dep()
                gnb = P.sb([128, D], F32, sB); d_gnb = Dep()
                Irep = P.sb([128, 4, 128], BF16, sB); d_Irep = Dep()
                epsG = P.sb([128, 1], F32, sB); d_epsG = Dep()
                P.dma("pool", lambda e: e.dma_start(out=dmaskB[:].rearrange("p a b -> p (a b)"), in_=dmask_d), writes=[d_dmask])
                P.dma("pool", lambda e: e.dma_start(out=wmaskB[:].rearrange("p a b -> p (a b)"), in_=wmask_d), writes=[d_wmask])
                P.dma("pool", lambda e: e.dma_start(out=OvP[:].rearrange("p a b -> p (a b)"), in_=ovp_d), writes=[d_OvP])
                P.dma("sp", lambda e: e.dma_start(out=dtab[:].rearrange("p a b -> p (a b)"), in_=dtab_d), writes=[d_dtab])
                P.dma("sp", lambda e: e.dma_start(out=qdt[:].rearrange("p a b -> p (a b)"), in_=qdt_d), writes=[d_qdt])
                P.dma("sp", lambda e: e.dma_start(out=gnw[:], in_=gnw_d.partition_broadcast(128)[:, 0, :]), writes=[d_gnw])
                P.dma("sp", lambda e: e.dma_start(out=gnb[:], in_=gnb_d.partition_broadcast(128)[:, 0, :]), writes=[d_gnb])
                for g in range(4):
                    P.op("dve", lambda e, g=g: e.tensor_copy(out=Irep[:, g, :], in_=ident[:]), reads=[d_ident], writes=[d_Irep])
                P.op("dve", lambda e: e.memset(epsG[:], GN_EPS), writes=[d_epsG])
                IrepF = Irep[:].rearrange("p a b -> p (a b)")

                xt = P.sb([128, D], F32, sB); d_xt = Dep()
                rp = P.sb([128, 192], F32, sB); d_rp = Dep()
                wk = mk_wk(sB)
                hT = P.sb([128, 8, 128], BF16, sB); d_hT = Dep()
                tmq = [P.sb([128, 256], F32, sB) for _ in range(4)]; d_tmq = [Dep() for _ in range(4)]
                qr = P.sb([128, 512], F32, sB); d_qr = Dep()
                rqr = P.sb([128, 4, 128], F32, sB); d_rqr = Dep()
                rkr = P.sb([128, 4, 128], F32, sB); d_rkr = Dep()
                sg = P.sb([128, 24], F32, sB); d_sg = Dep()
                QT = P.sb([128, 4, 128], BF16, sB); d_QT = Dep()
                rqT = P.sb([128, 4, 128], BF16, sB); d_rqT = Dep()
                rqdT = P.sb([128, 4, 128], BF16, sB); d_rqdT = Dep()
                rkT = P.sb([128, 4, 128], BF16, sB); d_rkT = Dep()
                vb = P.sb([128, 1024], BF16, sB); d_vb = Dep()
                sgate = P.sb([128, 1024], F32, sB); d_sgate = Dep()
                attD = P.sb([128, 4, 128], BF16, sB); d_attD = Dep()
                SownB = P.sb([128, 1024], BF16, sB); d_SownB = Dep()
                osb = P.sb([128, 4, 256], F32, sB); d_osb = Dep()
                st4 = P.sb([128, 16], F32, sB); d_st4 = Dep()
                yret = P.sb([128, 1024], F32, sB); d_yret = Dep()
                yrT = P.sb([128, 8, 128], BF16, sB); d_yrT = Dep()
                onsa = P.sb([128, 8, 64], F32, sB); d_onsa = Dep()
                tmpo = P.sb([128, 4, 64], F32, sB); d_tmpo = Dep()
                onT = P.sb([128, 4, 128], BF16, sB); d_onT = Dep()
                rden = P.sb([128, 8], F32, sB); d_rden = Dep()
                wgt = P.sb([128, 4], F32, sB); d_wgt = Dep()
                cmaskB = P.sb([128, 1024], BF16, sB); d_cmask = Dep()
                forceb = P.sb([128, 256], F32, sB); d_forceb = Dep()
                pacc = P.sb([128, 256], F32, sB); d_pacc = Dep()
                wk2 = P.sb([128, 256], F32, sB); d_wk2 = Dep()
                m16 = P.sb([128, 16], F32, sB); d_m16 = Dep()
                selb = [P.sb([128, 256], F32, sB) for _ in range(2)]; d_selb = [Dep(), Dep()]
                selbX = [[P.sb([128, 1024], BF16, sB) for _ in range(2)] for _ in range(2)]
                d_selbX = [[Dep(), Dep()], [Dep(), Dep()]]
                KsG = [P.sb([128, 1024], BF16, sB) for _ in range(2)]; d_KsG = [Dep(), Dep()]
                VsG = [P.sb([128, 8, 130], BF16, sB) for _ in range(2)]; d_VsG = [Dep(), Dep()]
                KwWin = P.sb([128, 1536], BF16, sB); d_KwWin = Dep()
                VwWin = P.sb([128, 12, 130], BF16, sB); d_VwWin = Dep()
                pts = [P.sb([128, 512], BF16, sB) for _ in range(2)]; d_pts = [Dep(), Dep()]
                P.op("pool", lambda e: e.memset(KwWin[:], 0.0), writes=[d_KwWin])
                P.op("pool", lambda e: e.memset(VwWin[:].rearrange("p a b -> p (a b)"), 0.0), writes=[d_VwWin])
                attn = make_attn(pts, d_pts)

                o4 = [banks[2][:, 0:260].rearrange("p (g c) -> p g c", g=4), banks[3][:, 0:260].rearrange("p (g c) -> p g c", g=4)]
                d_o4 = [d_bank[2], d_bank[3]]
                sgv = sg[:].rearrange("p (h b) -> p h b", b=3)

                def evac(kvh, br_i, first):
                    o4v, d_o = o4[kvh], d_o4[kvh]
                    P.op("dve", lambda e: e.tensor_scalar(out=rden[:, kvh * 4:(kvh + 1) * 4], in0=o4v[:, :, 64], scalar1=1e-30, scalar2=None, op0=ALU.max),
                         reads=[d_o], writes=[d_rden])
                    P.op("dve", lambda e: e.reciprocal(out=rden[:, kvh * 4:(kvh + 1) * 4], in_=rden[:, kvh * 4:(kvh + 1) * 4]), reads=[d_rden], writes=[d_rden])
                    P.op("dve", lambda e: e.tensor_tensor(out=wgt[:], in0=rden[:, kvh * 4:(kvh + 1) * 4], in1=sgv[:, kvh * 4:(kvh + 1) * 4, br_i], op=ALU.mult),
                         reads=[d_rden, d_sg], writes=[d_wgt])
                    wb_ = wgt[:].unsqueeze(2).to_broadcast([128, 4, 64])
                    if first:
                        P.op("dve", lambda e: e.tensor_tensor(out=onsa[:, kvh * 4:(kvh + 1) * 4, :], in0=o4v[:, :, 0:64], in1=wb_, op=ALU.mult),
                             reads=[d_o, d_wgt], writes=[d_onsa])
                    else:
                        P.op("dve", lambda e: e.tensor_tensor(out=tmpo[:], in0=o4v[:, :, 0:64], in1=wb_, op=ALU.mult),
                             reads=[d_o, d_wgt], writes=[d_tmpo])
                        P.op("pool", lambda e: e.tensor_tensor(out=onsa[:, kvh * 4:(kvh + 1) * 4, :], in0=onsa[:, kvh * 4:(kvh + 1) * 4, :], in1=tmpo[:], op=ALU.add),
                             reads=[d_tmpo, d_onsa], writes=[d_onsa])

                nblk = NB if stage >= 2 else 0
                if stage == 2:
                    nblk = DBG_NBLK
                for i in range(nblk):
                    P.dma("sp", lambda e, i=i: e.dma_start(out=xt[:], in_=x_own[i * 128:(i + 1) * 128, :]), writes=[d_xt])
                    P.dma("sp", lambda e, i=i: e.dma_start(out=rp[:], in_=rope_own[i * 128:(i + 1) * 128, :]), writes=[d_rp])
                    P.dma("pool", lambda e, i=i: e.dma_start(out=cmaskB[:], in_=cmask_d[i]), writes=[d_cmask])
                    P.dma("sp", lambda e, i=i: e.dma_start(out=forceb[:], in_=forceb_d[i]), writes=[d_forceb])
                    P.dma("pool", lambda e, i=i: e.dma_start(out=SownB[:], in_=sown_d[i]), writes=[d_SownB])
                    if i == 0:
                        P.dma("sp", lambda e: e.dma_start(out=KwWin[:, 512:1536], in_=kwT_d[:, 0:1024]), writes=[d_KwWin])
                        P.dma("sp", lambda e: e.dma_start(out=VwWin[:, 4:12, :], in_=vw_d[0:1024, :].rearrange("(m p) f -> p m f", p=128)), writes=[d_VwWin])
                    else:
                        P.dma("sp", lambda e, i=i: e.dma_start(out=KwWin[:], in_=kwT_d[:, (8 * i - 4) * 128:(8 * i + 8) * 128]), writes=[d_KwWin])
                        P.dma("sp", lambda e, i=i: e.dma_start(out=VwWin[:], in_=vw_d[(8 * i - 4) * 128:(8 * i + 8) * 128, :].rearrange("(m p) f -> p m f", p=128)),
                              writes=[d_VwWin])
                    norm_mod_T(xt, d_xt, 128, G1P, d_G1P, modP4[:, 0, :], d_modP, wk, hT, d_hT)
                    P.dma("sp", lambda e, i=i: e.dma_start(out=hT_d[i], in_=hT[:]), reads=[d_hT])
                    if SUB < 1:
                        continue
                    for (bk, c0, n) in ((0, 0, 512), (1, 512, 512), (4, 1024, 512), (5, 3584, 24)):
                        for k in range(8):
                            P.op("pe", lambda e, bk=bk, k=k, c0=c0, n=n: e.matmul(out=banks[bk][:, 0:n], lhsT=hT[:, k, :], rhs=WB[:, k, c0:c0 + n],
                                                                                start=(k == 0), stop=(k == 7)),
                                 reads=[d_hT, d_WB], writes=[d_bank[bk]])
                    P.op("act", lambda e: e.activation(out=sg[:], in_=banks[5][:, 0:24], func=ACTF.Sigmoid), reads=[d_bank[5]], writes=[d_sg])
                    if SUB < 2:
                        continue
                    srcq = banks[0][:, 0:512].rearrange("p (h two e) -> p h two e", h=8, two=2)
                    dstq = qr[:].rearrange("p (h two e) -> p h two e", h=8, two=2)
                    shq = [128, 8, 32]
                    tvq = [t[:, 0:256].rearrange("p (h e) -> p h e", h=8) for t in tmq]
                    rope(lambda hf: srcq[:, :, hf, :], lambda hf: dstq[:, :, hf, :], rp[:, 0:32].unsqueeze(1).to_broadcast(shq),
                         rp[:, 32:64].unsqueeze(1).to_broadcast(shq), shq, tvq, d_bank[0], d_qr, d_rp, d_tmq)
                    shk = [128, 4, 64]
                    tvk = [t[:, 0:256].rearrange("p (h e) -> p h e", h=4) for t in tmq]
                    cosk = rp[:, 64:128].unsqueeze(1).to_broadcast(shk)
                    sink = rp[:, 128:192].unsqueeze(1).to_broadcast(shk)
                    srcrq = banks[1][:, 0:512].rearrange("p (h two e) -> p h two e", h=4, two=2)
                    dstrq = rqr[:].rearrange("p h (two e) -> p h two e", two=2)
                    rope(lambda hf: srcrq[:, :, hf, :], lambda hf: dstrq[:, :, hf, :], cosk, sink, shk, tvk, d_bank[1], d_rqr, d_rp, d_tmq)
                    srcrk = banks[4][:, 0:512].rearrange("p (h two e) -> p h two e", h=4, two=2)
                    dstrk = rkr[:].rearrange("p h (two e) -> p h two e", two=2)
                    rope(lambda hf: srcrk[:, :, hf, :], lambda hf: dstrk[:, :, hf, :], cosk, sink, shk, tvk, d_bank[4], d_rkr, d_rp, d_tmq)
                    if SUB < 3:
                        continue
                    P.op("pool", lambda e: e.tensor_scalar(out=qr[:], in0=qr[:], scalar1=0.125, scalar2=None, op0=ALU.mult), reads=[d_qr], writes=[d_qr])
                    P.op("pool", lambda e: e.tensor_scalar(out=rkr[:].rearrange("p a b -> p (a b)"), in0=rkr[:].rearrange("p a b -> p (a b)"),
                                                           scalar1=RET_SCALE, scalar2=None, op0=ALU.mult), reads=[d_rkr], writes=[d_rkr])
                    for g in range(4):
                        P.op("pe", lambda e, g=g: e.transpose(out=banks[3][:, g * 128:(g + 1) * 128], in_=qr[:, g * 128:(g + 1) * 128], identity=ident[:]),
                             reads=[d_qr, d_ident], writes=[d_bank[3]])
                    P.op("act", lambda e: e.copy(out=QT[:].rearrange("p a b -> p (a b)"), in_=banks[3][:, :]),
                         reads=[d_bank[3]], writes=[d_QT])
                    if SUB2 < 1:
                        continue
                    for h in range(4):
                        P.op("pe", lambda e, h=h: e.transpose(out=banks[3][:, h * 128:(h + 1) * 128], in_=rqr[:, h, :], identity=ident[:]),
                             reads=[d_rqr, d_ident], writes=[d_bank[3]])
                    P.op("act", lambda e: e.copy(out=rqT[:].rearrange("p a b -> p (a b)"), in_=banks[3][:, :]), reads=[d_bank[3]], writes=[d_rqT])
                    P.op("dve", lambda e: e.tensor_tensor(out=rqdT[:].rearrange("p a b -> p (a b)"), in0=banks[3][:, :],
                                                          in1=qdt[:].rearrange("p a b -> p (a b)"), op=ALU.mult),
                         reads=[d_bank[3], d_qdt], writes=[d_rqdT])
                    if SUB2 < 2:
                        continue
                    for h in range(4):
                        P.op("pe", lambda e, h=h: e.transpose(out=banks[3][:, h * 128:(h + 1) * 128], in_=rkr[:, h, :], identity=ident[:]),
                             reads=[d_rkr, d_ident], writes=[d_bank[3]])
                    P.op("act", lambda e: e.copy(out=rkT[:].rearrange("p a b -> p (a b)"), in_=banks[3][:, :]),
                         reads=[d_bank[3]], writes=[d_rkT])
                    if SUB < 4:
                        continue
                    for (bk, c0) in ((0, 1536), (1, 2048), (4, 2560), (5, 3072)):
                        for k in range(8):
                            P.op("pe", lambda e, bk=bk, k=k, c0=c0: e.matmul(out=banks[bk][:, :], lhsT=hT[:, k, :], rhs=WB[:, k, c0:c0 + 512],
                                                                           start=(k == 0), stop=(k == 7)),
                                 reads=[d_hT, d_WB], writes=[d_bank[bk]])
                    P.op("act", lambda e: e.copy(out=vb[:, 0:512], in_=banks[0][:, :]), reads=[d_bank[0]], writes=[d_vb])
                    P.op("act", lambda e: e.copy(out=vb[:, 512:1024], in_=banks[1][:, :]), reads=[d_bank[1]], writes=[d_vb])
                    P.op("act", lambda e: e.activation(out=sgate[:, 0:512], in_=banks[4][:, :], func=ACTF.Silu), reads=[d_bank[4]], writes=[d_sgate])
                    P.op("act", lambda e: e.activation(out=sgate[:, 512:1024], in_=banks[5][:, :], func=ACTF.Silu), reads=[d_bank[5]], writes=[d_sgate])
                    if PART < 2:
                        continue
                    for h in range(4):
                        P.op("pe", lambda e, h=h: e.matmul(out=banks[3][:, h * 128:(h + 1) * 128], lhsT=rkT[:, h, :], rhs=rqT[:, h, :], start=True, stop=True),
                             reads=[d_rkT, d_rqT], writes=[d_bank[3]])
                    P.op("dve", lambda e: e.tensor_tensor(out=attD[:].rearrange("p a b -> p (a b)"), in0=banks[3][:, :],
                                                          in1=dtab[:].rearrange("p a b -> p (a b)"), op=ALU.mult),
                         reads=[d_bank[3], d_dtab], writes=[d_attD])
                    for h in range(4):
                        bk = 4 + h // 2
                        c0 = (h % 2) * 256
                        P.op("pe", lambda e, h=h, bk=bk, c0=c0: e.matmul(out=banks[bk][:, c0:c0 + 256], lhsT=attD[:, h, :], rhs=vb[:, h * 256:(h + 1) * 256],
                                                                        start=True, stop=False),
                             reads=[d_attD, d_vb], writes=[d_bank[bk]])
                        P.op("pe", lambda e, h=h, bk=bk, c0=c0: e.matmul(out=banks[bk][:, c0:c0 + 256], lhsT=rqdT[:, h, :], rhs=SownB[:, h * 256:(h + 1) * 256],
                                                                        start=False, stop=True),
                             reads=[d_rqdT, d_SownB], writes=[d_bank[bk]])
                    for h in range(4):
                        bk = 4 + h // 2
                        c0 = (h % 2) * 256
                        P.op("act", lambda e, h=h, bk=bk, c0=c0: e.activation(out=osb[:, h, :], in_=banks[bk][:, c0:c0 + 256], func=ACTF.Identity,
                                                                             accum_out=st4[:, h:h + 1]),
                             reads=[d_bank[bk]], writes=[d_osb, d_st4])
                        P.op("act", lambda e, h=h, bk=bk, c0=c0: e.activation(out=junk_sh[:, 0:256], in_=banks[bk][:, c0:c0 + 256], func=ACTF.Square,
                                                                             accum_out=st4[:, 4 + h:5 + h]),
                             reads=[d_bank[bk]], writes=[d_junk_sh, d_st4])
                    P.op("dve", lambda e: e.tensor_scalar(out=st4[:, 8:12], in0=st4[:, 0:4], scalar1=1.0 / 256, scalar2=None, op0=ALU.mult),
                         reads=[d_st4], writes=[d_st4])
                    P.op("dve", lambda e: e.tensor_tensor(out=st4[:, 12:16], in0=st4[:, 8:12], in1=st4[:, 8:12], op=ALU.mult), reads=[d_st4], writes=[d_st4])
                    P.op("dve", lambda e: e.scalar_tensor_tensor(out=st4[:, 12:16], in0=st4[:, 4:8], scalar=1.0 / 256, in1=st4[:, 12:16],
                                                                 op0=ALU.mult, op1=ALU.subtract), reads=[d_st4], writes=[d_st4])
                    P.op("act", lambda e: e.activation(out=st4[:, 12:16], in_=st4[:, 12:16], func=ACTF.Sqrt, bias=epsG[:, 0:1], scale=1.0),
                         reads=[d_st4, d_epsG], writes=[d_st4])
                    P.op("dve", lambda e: e.reciprocal(out=st4[:, 12:16], in_=st4[:, 12:16]), reads=[d_st4], writes=[d_st4])
                    for h in range(4):
                        P.op("dve", lambda e, h=h: e.tensor_scalar(out=yret[:, h * 256:(h + 1) * 256], in0=osb[:, h, :], scalar1=st4[:, 8 + h:9 + h],
                                                                  scalar2=st4[:, 12 + h:13 + h], op0=ALU.subtract, op1=ALU.mult),
                             reads=[d_osb, d_st4], writes=[d_yret])
                    P.op("pool", lambda e: e.tensor_tensor(out=yret[:], in0=yret[:], in1=gnw[:], op=ALU.mult), reads=[d_yret, d_gnw], writes=[d_yret])
                    P.op("pool", lambda e: e.tensor_tensor(out=yret[:], in0=yret[:], in1=gnb[:], op=ALU.add), reads=[d_yret, d_gnb], writes=[d_yret])
                    P.op("pool", lambda e: e.tensor_tensor(out=yret[:], in0=yret[:], in1=sgate[:], op=ALU.mult), reads=[d_yret, d_sgate], writes=[d_yret])
                    for c in range(8):
                        P.op("pe", lambda e, c=c: e.transpose(out=pT[:, c * 128:(c + 1) * 128], in_=yret[:, c * 128:(c + 1) * 128], identity=ident[:]),
                             reads=[d_yret, d_ident], writes=[d_pTs[c // 4]])
                    P.op("act", lambda e: e.copy(out=yrT[:].rearrange("p a b -> p (a b)"), in_=pT[:]), reads=d_pTs, writes=[d_yrT])
                    P.dma("sp", lambda e, i=i: e.dma_start(out=yrT_d[i], in_=yrT[:]), reads=[d_yrT])
                    if DBG:
                        P.dma("sp", lambda e, i=i: e.dma_start(out=dbg_yr[i], in_=yret[:]), reads=[d_yret])

                    if PART < 3:
                        continue
                    for kvh in range(2):
                        QTh = QT[kvh * 64:(kvh + 1) * 64, :, :].rearrange("p a b -> p (a b)")
                        tiles = []
                        for jt in range(8):
                            tiles.append({"KT": cmpKT[kvh * 64:(kvh + 1) * 64, jt * 128:(jt + 1) * 128], "V": cmpVA[:, jt, kvh, :], "nk": 128,
                                          "biases": [(cmaskB[:, jt * 128:(jt + 1) * 128], IrepF)], "Ov": OvP[:, jt, :],
                                          "deps": [d_cmpKT, d_cmpVA, d_cmask, d_Irep, d_OvP]})
                        psv = lambda g: banks[4 + g // 2][:, (g % 2) * 256:(g % 2) * 256 + 256]
                        attn(128, 4, QTh, d_QT, tiles, o4[kvh], d_o4[kvh], pslc=(psv, [d_bank[4], d_bank[5]], 2))
                        evac(kvh, 0, True)
                        P.op("dve", lambda e, kvh=kvh: e.tensor_scalar(out=pacc[:], in0=psv(0), scalar1=rden[:, kvh * 4:kvh * 4 + 1], scalar2=None, op0=ALU.mult),
                             reads=[d_bank[4], d_rden], writes=[d_pacc])
                        for g in range(1, 4):
                            P.op("dve", lambda e, kvh=kvh, g=g: e.scalar_tensor_tensor(out=pacc[:], in0=psv(g), scalar=rden[:, kvh * 4 + g:kvh * 4 + g + 1],
                                                                                      in1=pacc[:], op0=ALU.mult, op1=ALU.add),
                                 reads=[d_bank[4 + g // 2], d_rden, d_pacc], writes=[d_pacc])
                        P.op("dve", lambda e: e.tensor_tensor(out=pacc[:], in0=pacc[:], in1=forceb[:], op=ALU.add), reads=[d_pacc, d_forceb], writes=[d_pacc])
                        P.op("dve", lambda e: e.max(out=m16[:, 0:8], in_=pacc[:]), reads=[d_pacc], writes=[d_m16])
                        P.op("dve", lambda e: e.match_replace(out=wk2[:], in_to_replace=m16[:, 0:8], in_values=pacc[:], imm_value=-3.0e38),
                             reads=[d_pacc, d_m16], writes=[d_wk2])
                        P.op("dve", lambda e: e.max(out=m16[:, 8:16], in_=wk2[:]), reads=[d_wk2], writes=[d_m16])
                        P.op("dve", lambda e, kvh=kvh: e.tensor_scalar(out=selb[kvh][:], in0=pacc[:], scalar1=m16[:, 15:16], scalar2=NEG,
                                                                      op0=ALU.is_lt, op1=ALU.mult),
                             reads=[d_pacc, d_m16], writes=[d_selb[kvh]])
                    if DBG:
                        P.dma("sp", lambda e, i=i: e.dma_start(out=dbg_selb[i, 0], in_=selb[0][:]), reads=[d_selb[0]])
                        P.dma("sp", lambda e, i=i: e.dma_start(out=dbg_selb[i, 1], in_=selb[1][:]), reads=[d_selb[1]])
                    if PART < 4:
                        continue
                    ngk = i + 1
                    for gk in range(ngk):
                        b = gk % 2
                        P.dma("sp", lambda e, gk=gk, b=b: e.dma_start(out=KsG[b][:], in_=ksT_d[:, gk * 1024:(gk + 1) * 1024]), writes=[d_KsG[b]])
                        P.dma("sp", lambda e, gk=gk, b=b: e.dma_start(out=VsG[b][:], in_=vs_d[gk * 1024:(gk + 1) * 1024, :].rearrange("(m p) f -> p m f", p=128)),
                              writes=[d_VsG[b]])
                        for kvh in range(2):
                            P.op("pool", lambda e, kvh=kvh, b=b, gk=gk: e.tensor_copy(
                                out=selbX[kvh][b][:].rearrange("p (a c) -> p a c", c=64),
                                in_=selb[kvh][:, gk * 16:(gk + 1) * 16].unsqueeze(2).to_broadcast([128, 16, 64])),
                                reads=[d_selb[kvh]], writes=[d_selbX[kvh][b]])
                        for kvh in range(2):
                            QTh = QT[kvh * 64:(kvh + 1) * 64, :, :].rearrange("p a b -> p (a b)")
                            tiles = []
                            for j in range(8):
                                bs = [(selbX[kvh][b][:, j * 128:(j + 1) * 128], IrepF)]
                                if gk == i:
                                    bs.append((dmaskB[:, j, :], IrepF))
                                tiles.append({"KT": KsG[b][kvh * 64:(kvh + 1) * 64, j * 128:(j + 1) * 128], "V": VsG[b][:, j, kvh * 65:(kvh + 1) * 65],
                                              "nk": 128, "biases": bs, "deps": [d_KsG[b], d_VsG[b], d_selbX[kvh][b], d_Irep, d_dmask]})
                            attn(128, 4, QTh, d_QT, tiles, o4[kvh], d_o4[kvh], first=(gk == 0), last=(gk == ngk - 1))
                    evac(0, 1, False)
                    evac(1, 1, False)
                    if PART < 5:
                        continue
                    for kvh in range(2):
                        QTh = QT[kvh * 64:(kvh + 1) * 64, :, :].rearrange("p a b -> p (a b)")
                        tiles = []
                        for m in range(12):
                            wm = m if i == 0 else 12 + m
                            tiles.append({"KT": KwWin[kvh * 64:(kvh + 1) * 64, m * 128:(m + 1) * 128], "V": VwWin[:, m, kvh * 65:(kvh + 1) * 65], "nk": 128,
                                          "biases": [(wmaskB[:, wm, :], IrepF)], "deps": [d_KwWin, d_VwWin, d_wmask, d_Irep]})
                        attn(128, 4, QTh, d_QT, tiles, o4[kvh], d_o4[kvh])
                        evac(kvh, 2, False)
                    onf = onsa[:].rearrange("p h d -> p (h d)")
                    for c in range(4):
                        P.op("pe", lambda e, c=c: e.transpose(out=pT[:, c * 128:(c + 1) * 128], in_=onf[:, c * 128:(c + 1) * 128], identity=ident[:]),
                             reads=[d_onsa, d_ident], writes=[d_pTs[0]])
                    P.op("act", lambda e: e.copy(out=onT[:].rearrange("p a b -> p (a b)"), in_=pT[:, 0:512]), reads=[d_pTs[0]], writes=[d_onT])
                    P.dma("sp", lambda e, i=i: e.dma_start(out=onT_d[i], in_=onT[:]), reads=[d_onT])
                    if DBG:
                        P.dma("sp", lambda e, i=i: e.dma_start(out=dbg_on[i], in_=onsa[:].rearrange("p h d -> p (h d)")), reads=[d_onsa])
                P.barrier()

        def phase_s1():
            with ExitStack() as sS:
                sR = ExitStack()
                smallc = P.sb([1, 1024], F32, sS); smallb = P.sb([1, 1024], BF16, sS); d_small = Dep()
                P.dma("sp", lambda e: e.dma_start(out=smallc[:], in_=smallc_d), writes=[d_small])
                P.op("dve", lambda e: e.tensor_copy(out=smallb[:], in_=smallc[:]), reads=[d_small], writes=[d_small])
                cm7row, wm0row, ones4 = smallb[0:1, 0:128], smallb[0:1, 128:256], smallb[0:1, 272:276]
                forceS = smallc[0:1, 276:532]
                oh4 = P.sb([SB, 4], F32, sS); d_oh4 = Dep()
                P.dma("sp", lambda e: e.dma_start(out=oh4[:], in_=oh4_d), writes=[d_oh4])
                OvS = P.sb([128, 8, 257], BF16, sS); d_OvS = Dep()
                P.dma("pool", lambda e: e.dma_start(out=OvS[:].rearrange("p a b -> p (a b)"), in_=ovs_d), writes=[d_OvS])
                wbc = P.sb([128, 2, 16, 256], F32, sS); d_wbc = Dep()
                P.dma("sp", lambda e: e.dma_start(out=wbc[:].rearrange("p a b c -> p (a b c)"), in_=wrow_d.partition_broadcast(128)[:, 0, :]), writes=[d_wbc])
                brow = P.sb([128, 256], F32, sS); d_brow = Dep()
                P.dma("sp", lambda e: e.dma_start(out=brow[:], in_=brow_d.partition_broadcast(128)[:, 0, :]), writes=[d_brow])
                QTs = P.sb([128, 4, SB], BF16, sS); d_QTs = Dep()
                KnT = P.sb([128, 2, SB], BF16, sS); d_KnT = Dep()
                VnA = P.sb([SB, 2, 2, 65], BF16, sS); d_VnA = Dep()
                sg = P.sb([SB, 24], F32, sS); d_sg = Dep()
                ohm = P.sb([128, 4, 4], F32, sR); d_ohm = Dep()
                P.dma("sp", lambda e: e.dma_start(out=ohm[:].rearrange("p a b -> p (a b)"), in_=ohm_d), writes=[d_ohm])
                gnw = P.sb([SB, D], F32, sR); d_gnw = Dep()
                gnb = P.sb([SB, D], F32, sR); d_gnb = Dep()
                P.dma("sp", lambda e: e.dma_start(out=gnw[:], in_=gnw_d.partition_broadcast(SB)[:, 0, :]), writes=[d_gnw])
                P.dma("sp", lambda e: e.dma_start(out=gnb[:], in_=gnb_d.partition_broadcast(SB)[:, 0, :]), writes=[d_gnb])
                epsG = P.sb([SB, 1], F32, sR); d_epsG = Dep()
                P.op("dve", lambda e: e.memset(epsG[:], GN_EPS), writes=[d_epsG])
                xs = P.sb([SB, D], F32, sR); d_xs = Dep()
                rps = P.sb([SB, 192], F32, sR); d_rps = Dep()
                msS = P.sb([SB, 2, D], F32, sR); d_msS = Dep()
                gm4 = P.sb([SB, D], F32, sR); d_gm4 = Dep()
                P.dma("sp", lambda e: e.dma_start(out=xs[:], in_=x_smp), writes=[d_xs])
                P.dma("sp", lambda e: e.dma_start(out=rps[:], in_=rope_smp), writes=[d_rps])
                P.dma("sp", lambda e: e.dma_start(out=msS[:], in_=mods_d[:, 0:2 * D].rearrange("s (a d) -> s a d", a=2)), writes=[d_msS])
                P.dma("sp", lambda e: e.dma_start(out=gm4[:], in_=norm_mix.partition_broadcast(SB)[:, 0, :]), writes=[d_gm4])
                P.op("dve", lambda e: e.scalar_tensor_tensor(out=msS[:, 1, :], in0=msS[:, 1, :], scalar=1.0, in1=gm4[:], op0=ALU.add, op1=ALU.mult),
                     reads=[d_msS, d_gm4], writes=[d_msS])
                wkS = mk_wk(sR, SB)
                hTs = P.sb([128, 8, SB], BF16, sR); d_hTs = Dep()
                norm_mod_T(xs, d_xs, SB, msS[:, 1, :], d_msS, msS[:, 0, :], d_msS, wkS, hTs, d_hTs)
                P.dma("sp", lambda e: e.dma_start(out=hTs_d, in_=hTs[:]), reads=[d_hTs])
                zs = P.sb([SB, 4376], F32, sR); d_zs = Dep()
                wch = [P.sb([128, 8, 512], BF16, sR) for _ in range(2)]; d_wch = [Dep(), Dep()]
                segs = []
                for g in range(4):
                    segs += [(C_Q + g * 64, 64), (C_Q + (4 + g) * 64, 64)]
                segs += [(C_RQ, 512), (C_RK, 512), (C_RV, 512), (C_RV + 512, 512), (C_RG, 512), (C_RG + 512, 512), (C_G, 24),
                         (C_KVC, 128), (C_KVS, 128), (C_KVW, 128), (C_KVC + 128, 128), (C_KVS + 128, 128), (C_KVW + 128, 128)]
                chunks = []
                cur, curw = [], 0
                for sg_ in segs:
                    if curw + sg_[1] > 512:
                        chunks.append(cur); cur, curw = [], 0
                    cur.append(sg_); curw += sg_[1]
                chunks.append(cur)
                zoff = 0
                for ci, ch in enumerate(chunks):
                    b = ci % 2
                    o_ = 0
                    for (src, n_) in ch:
                        P.dma("pool", lambda e, b=b, o_=o_, src=src, n_=n_: e.dma_start(
                            out=wch[b][:, :, o_:o_ + n_], in_=w_in[:, src:src + n_].rearrange("(c p) n -> p c n", p=128)), writes=[d_wch[b]])
                        o_ += n_
                    for k in range(8):
                        P.op("pe", lambda e, b=b, k=k, o_=o_: e.matmul(out=banks[b][0:SB, 0:o_], lhsT=hTs[:, k, :], rhs=wch[b][:, k, 0:o_], start=(k == 0), stop=(k == 7)),
                             reads=[d_hTs, d_wch[b]], writes=[d_bank[b]])
                    P.op("act", lambda e, b=b, o_=o_, zoff=zoff: e.copy(out=zs[:, zoff:zoff + o_], in_=banks[b][0:SB, 0:o_]), reads=[d_bank[b]], writes=[d_zs])
                    zoff += o_
                tms = [P.sb([SB, 256], F32, sR) for _ in range(4)]; d_tms = [Dep() for _ in range(4)]
                qr = P.sb([SB, 512], F32, sR); d_qr = Dep()
                rqr = P.sb([SB, 4, 128], F32, sR); d_rqr = Dep()
                rkr = P.sb([SB, 4, 128], F32, sR); d_rkr = Dep()
                kvo = P.sb([SB, 3, 256], F32, sR); d_kvo = Dep()
                sgate = P.sb([SB, 1024], F32, sR); d_sgate = Dep()
                vbs = P.sb([SB, 1024], BF16, sR); d_vbs = Dep()
                shq = [SB, 8, 32]
                srcq = zs[:, 0:512].rearrange("p (h two e) -> p h two e", h=8, two=2)
                dstq = qr[:].rearrange("p (h two e) -> p h two e", h=8, two=2)
                tvq = [t[:, 0:256].rearrange("p (h e) -> p h e", h=8) for t in tms]
                cos32 = rps[:, 0:32]; sin32 = rps[:, 32:64]
                rope(lambda hf: srcq[:, :, hf, :], lambda hf: dstq[:, :, hf, :], cos32.unsqueeze(1).to_broadcast(shq), sin32.unsqueeze(1).to_broadcast(shq),
                     shq, tvq, d_zs, d_qr, d_rps, d_tms)
                P.op("pool", lambda e: e.tensor_scalar(out=qr[:], in0=qr[:], scalar1=0.125, scalar2=None, op0=ALU.mult), reads=[d_qr], writes=[d_qr])
                shk = [SB, 4, 64]
                tvk = [t[:, 0:256].rearrange("p (h e) -> p h e", h=4) for t in tms]
                cosk = rps[:, 64:128].unsqueeze(1).to_broadcast(shk); sink = rps[:, 128:192].unsqueeze(1).to_broadcast(shk)
                for (c0, dstt, d_dst) in ((512, rqr, d_rqr), (1024, rkr, d_rkr)):
                    src_ = zs[:, c0:c0 + 512].rearrange("p (h two e) -> p h two e", h=4, two=2)
                    dst_ = dstt[:].rearrange("p h (two e) -> p h two e", two=2)
                    rope(lambda hf, src_=src_: src_[:, :, hf, :], lambda hf, dst_=dst_: dst_[:, :, hf, :], cosk, sink, shk, tvk, d_zs, d_dst, d_rps, d_tms)
                P.op("pool", lambda e: e.tensor_scalar(out=rkr[:].rearrange("p a b -> p (a b)"), in0=rkr[:].rearrange("p a b -> p (a b)"),
                                                       scalar1=RET_SCALE, scalar2=None, op0=ALU.mult), reads=[d_rkr], writes=[d_rkr])
                shn = [SB, 3, 2, 32]
                srcn = zs[:, 3608:3992].rearrange("p (j h two e) -> p j h two e", j=3, h=2, two=2)
                dstn = kvo[:, :, 0:128].rearrange("p j (h two e) -> p j h two e", h=2, two=2)
                tvn = [t[:, 0:192].rearrange("p (j h e) -> p j h e", j=3, h=2) for t in tms]
                rope(lambda hf: srcn[:, :, :, hf, :], lambda hf: dstn[:, :, :, hf, :], cos32.unsqueeze(1).unsqueeze(1).to_broadcast(shn),
                     sin32.unsqueeze(1).unsqueeze(1).to_broadcast(shn), shn, tvn, d_zs, d_kvo, d_rps, d_tms)
                P.op("act", lambda e: e.copy(out=kvo[:, :, 128:256], in_=zs[:, 3992:4376].rearrange("p (j f) -> p j f", j=3)), reads=[d_zs], writes=[d_kvo])
                P.op("act", lambda e: e.activation(out=sg[:], in_=zs[:, 3584:3608], func=ACTF.Sigmoid), reads=[d_zs], writes=[d_sg])
                P.op("act", lambda e: e.activation(out=sgate[:], in_=zs[:, 2560:3584], func=ACTF.Silu), reads=[d_zs], writes=[d_sgate])
                P.op("act", lambda e: e.copy(out=vbs[:], in_=zs[:, 1536:2560]), reads=[d_zs], writes=[d_vbs])
                P.dma("sp", lambda e: e.dma_start(out=o_cmp_s, in_=kvo[:, 0, :]), reads=[d_kvo])
                P.dma("sp", lambda e: e.dma_start(out=o_sel_s, in_=kvo[:, 1, :]), reads=[d_kvo])
                P.dma("sp", lambda e: e.dma_start(out=o_win_s[:, 511, :], in_=kvo[:, 2, :]), reads=[d_kvo])
                for b in range(SB):
                    P.dma("sp", lambda e, b=b: e.dma_start(out=o_win_s[b, 0:511, :], in_=cache_win[b, 1:512, :]))
                rqTs = P.sb([128, 4, SB], BF16, sR); d_rqTs = Dep()
                rqdTs = P.sb([128, 4, SB], F32, sR); d_rqdTs = Dep()
                rkTs = P.sb([128, 4, SB], BF16, sR); d_rkTs = Dep()
                for (srcs, dstT, d_srcs, d_dstT) in (([qr[:, g * 128:(g + 1) * 128] for g in range(4)], QTs, d_qr, d_QTs),
                                                     ([rqr[:, h, :] for h in range(4)], rqTs, d_rqr, d_rqTs),
                                                     ([rkr[:, h, :] for h in range(4)], rkTs, d_rkr, d_rkTs),
                                                     ([kvo[:, 1, 0:128], kvo[:, 2, 0:128]], KnT, d_kvo, d_KnT)):
                    for j, sap in enumerate(srcs):
                        P.op("pe", lambda e, j=j, sap=sap: e.transpose(out=banks[3][:, j * SB:(j + 1) * SB], in_=sap, identity=ident[0:SB, 0:SB]),
                             reads=[d_srcs, d_ident], writes=[d_bank[3]])
                    nn = len(srcs) * SB
                    P.op("act", lambda e, dstT=dstT, nn=nn: e.copy(out=dstT[:].rearrange("p a b -> p (a b)"), in_=banks[3][:, 0:nn]), reads=[d_bank[3]], writes=[d_dstT])
                for h in range(4):
                    P.op("dve", lambda e, h=h: e.tensor_scalar(out=rqdTs[:, h, :], in0=rqTs[:, h, :], scalar1=float(math.exp(LG[h])), scalar2=None, op0=ALU.mult),
                         reads=[d_rqTs], writes=[d_rqdTs])
                P.op("pool", lambda e: e.memset(VnA[:].rearrange("p a b c -> p (a b c)"), 1.0), writes=[d_VnA])
                P.op("pool", lambda e: e.tensor_copy(out=VnA[:, :, :, 0:64], in_=kvo[:, 1:3, 128:256].rearrange("p j (h d) -> p j h d", h=2)),
                     reads=[d_kvo], writes=[d_VnA])
                attDs = P.sb([SB, 4, SB], BF16, sR); d_attDs = Dep()
                for h in range(4):
                    P.op("pe", lambda e, h=h: e.matmul(out=banks[3][0:SB, h * SB:(h + 1) * SB], lhsT=rkTs[:, h, :], rhs=rqTs[:, h, :], start=True, stop=True),
                         reads=[d_rkTs, d_rqTs], writes=[d_bank[3]])
                P.op("dve", lambda e: e.tensor_tensor(out=attDs[:], in0=banks[3][0:SB, 0:16].rearrange("p (h i) -> p h i", h=4),
                                                      in1=oh4[:].unsqueeze(1).to_broadcast([SB, 4, SB]), op=ALU.mult), reads=[d_bank[3], d_oh4], writes=[d_attDs])
                Sb = [P.sb([128, 4, 256], F32, sR) for _ in range(SB)]; d_Sb = [Dep() for _ in range(SB)]
                Sbb = [P.sb([128, 4, 256], BF16, sR) for _ in range(SB)]; d_Sbb = [Dep() for _ in range(SB)]
                rqm = [P.sb([128, 4, SB], BF16, sR) for _ in range(SB)]; d_rqm = [Dep() for _ in range(SB)]
                kdm = [P.sb([SB, 4, 128], BF16, sR) for _ in range(SB)]; d_kdm = [Dep() for _ in range(SB)]
                for b in range(SB):
                    P.dma("sp", lambda e, b=b: e.dma_start(out=Sb[b][:], in_=state_s[b].rearrange("h k v -> k h v")), writes=[d_Sb[b]])
                    P.op("pool", lambda e, b=b: e.tensor_copy(out=Sbb[b][:].rearrange("p a b -> p (a b)"), in_=Sb[b][:].rearrange("p a b -> p (a b)")),
                         reads=[d_Sb[b]], writes=[d_Sbb[b]])
                    P.op("dve", lambda e, b=b: e.tensor_tensor(out=rqm[b][:], in0=rqdTs[:], in1=ohm[:, b, :].unsqueeze(1).to_broadcast([128, 4, SB]), op=ALU.mult),
                         reads=[d_rqdTs, d_ohm], writes=[d_rqm[b]])
                    P.op("dve", lambda e, b=b: e.tensor_scalar(out=kdm[b][:].rearrange("p a b -> p (a b)"), in0=rkr[:].rearrange("p a b -> p (a b)"),
                                                              scalar1=oh4[:, b:b + 1], scalar2=None, op0=ALU.mult), reads=[d_rkr, d_oh4], writes=[d_kdm[b]])
                for h in range(4):
                    bk = 4 + h // 2
                    c0 = (h % 2) * 256
                    P.op("pe", lambda e, h=h, bk=bk, c0=c0: e.matmul(out=banks[bk][0:SB, c0:c0 + 256], lhsT=attDs[:, h, :], rhs=vbs[:, h * 256:(h + 1) * 256],
                                                                    start=True, stop=False), reads=[d_attDs, d_vbs], writes=[d_bank[bk]])
                    for b in range(SB):
                        P.op("pe", lambda e, h=h, bk=bk, c0=c0, b=b: e.matmul(out=banks[bk][0:SB, c0:c0 + 256], lhsT=rqm[b][:, h, :], rhs=Sbb[b][:, h, :],
                                                                             start=False, stop=(b == SB - 1)), reads=[d_rqm[b], d_Sbb[b]], writes=[d_bank[bk]])
                osb = P.sb([SB, 4, 256], F32, sR); d_osb = Dep()
                st4 = P.sb([SB, 16], F32, sR); d_st4 = Dep()
                yret = P.sb([SB, 1024], F32, sR); d_yret = Dep()
                for h in range(4):
                    bk = 4 + h // 2
                    c0 = (h % 2) * 256
                    P.op("act", lambda e, h=h, bk=bk, c0=c0: e.activation(out=osb[:, h, :], in_=banks[bk][0:SB, c0:c0 + 256], func=ACTF.Identity, accum_out=st4[:, h:h + 1]),
                         reads=[d_bank[bk]], writes=[d_osb, d_st4])
                    P.op("act", lambda e, h=h, bk=bk, c0=c0: e.activation(out=junk_sh[0:SB, 0:256], in_=banks[bk][0:SB, c0:c0 + 256], func=ACTF.Square,
                                                                         accum_out=st4[:, 4 + h:5 + h]), reads=[d_bank[bk]], writes=[d_junk_sh, d_st4])
                P.op("dve", lambda e: e.tensor_scalar(out=st4[:, 8:12], in0=st4[:, 0:4], scalar1=1.0 / 256, scalar2=None, op0=ALU.mult), reads=[d_st4], writes=[d_st4])
                P.op("dve", lambda e: e.tensor_tensor(out=st4[:, 12:16], in0=st4[:, 8:12], in1=st4[:, 8:12], op=ALU.mult), reads=[d_st4], writes=[d_st4])
                P.op("dve", lambda e: e.scalar_tensor_tensor(out=st4[:, 12:16], in0=st4[:, 4:8], scalar=1.0 / 256, in1=st4[:, 12:16], op0=ALU.mult, op1=ALU.subtract),
                     reads=[d_st4], writes=[d_st4])
                P.op("act", lambda e: e.activation(out=st4[:, 12:16], in_=st4[:, 12:16], func=ACTF.Sqrt, bias=epsG[:, 0:1], scale=1.0), reads=[d_st4, d_epsG], writes=[d_st4])
                P.op("dve", lambda e: e.reciprocal(out=st4[:, 12:16], in_=st4[:, 12:16]), reads=[d_st4], writes=[d_st4])
                for h in range(4):
                    P.op("dve", lambda e, h=h: e.tensor_scalar(out=yret[:, h * 256:(h + 1) * 256], in0=osb[:, h, :], scalar1=st4[:, 8 + h:9 + h],
                                                              scalar2=st4[:, 12 + h:13 + h], op0=ALU.subtract, op1=ALU.mult), reads=[d_osb, d_st4], writes=[d_yret])
                P.op("pool", lambda e: e.tensor_tensor(out=yret[:], in0=yret[:], in1=gnw[:], op=ALU.mult), reads=[d_yret, d_gnw], writes=[d_yret])
                P.op("pool", lambda e: e.tensor_tensor(out=yret[:], in0=yret[:], in1=gnb[:], op=ALU.add), reads=[d_yret, d_gnb], writes=[d_yret])
                P.op("pool", lambda e: e.tensor_tensor(out=yret[:], in0=yret[:], in1=sgate[:], op=ALU.mult), reads=[d_yret, d_sgate], writes=[d_yret])
                yrTs = P.sb([128, 8, SB], BF16, sR); d_yrTs = Dep()
                for c in range(8):
                    P.op("pe", lambda e, c=c: e.transpose(out=banks[3][:, c * SB:(c + 1) * SB], in_=yret[:, c * 128:(c + 1) * 128], identity=ident[0:SB, 0:SB]),
                         reads=[d_yret, d_ident], writes=[d_bank[3]])
                P.op("act", lambda e: e.copy(out=yrTs[:].rearrange("p a b -> p (a b)"), in_=banks[3][:, 0:8 * SB]), reads=[d_bank[3]], writes=[d_yrTs])
                P.dma("sp", lambda e: e.dma_start(out=yrTs_d, in_=yrTs[:]), reads=[d_yrTs])
                snew = [P.sb([128, 256], F32, sR) for _ in range(2)]; d_snew = [Dep(), Dep()]
                for b in range(SB):
                    for h in range(4):
                        bb = (b * 4 + h) % 2
                        P.op("pe", lambda e, b=b, h=h, bb=bb: e.matmul(out=banks[4 + bb][:, 0:256], lhsT=kdm[b][:, h, :], rhs=vbs[:, h * 256:(h + 1) * 256],
                                                                      start=True, stop=True), reads=[d_kdm[b], d_vbs], writes=[d_bank[4 + bb]])
                        P.op("dve", lambda e, b=b, h=h, bb=bb: e.scalar_tensor_tensor(out=snew[bb][:], in0=Sb[b][:, h, :], scalar=float(math.exp(LG[h])),
                                                                                     in1=banks[4 + bb][:, 0:256], op0=ALU.mult, op1=ALU.add),
                             reads=[d_Sb[b], d_bank[4 + bb]], writes=[d_snew[bb]])
                        P.dma("sp", lambda e, b=b, h=h, bb=bb: e.dma_start(out=o_state_s[b, h], in_=snew[bb][:]), reads=[d_snew[bb]])

                P.barrier()
                sR.close()
                pts = [P.sb([128, 512], BF16, sS) for _ in range(2)]; d_pts = [Dep(), Dep()]
                attn = make_attn(pts, d_pts)
                G = [P.sb([128, 16, 256], F32, sS) for _ in range(2)]; d_G = [Dep(), Dep()]
                prod = [P.sb([128, 16, 256], F32, sS) for _ in range(2)]; d_prod = [Dep(), Dep()]
                Fs = P.sb([128, 8, 256], F32, sS); d_Fs = Dep()
                S2 = P.sb([128, 9, 256], F32, sS); d_S2 = Dep()
                cmpKTs = P.sb([128, 8, 128], BF16, sS); d_cmpKTs = Dep()
                cmpVAs = P.sb([128, 8, 2, 65], BF16, sS); d_cmpVAs = Dep()
                P.op("pool", lambda e: e.memset(cmpVAs[:].rearrange("p a b c -> p (a b c)"), 1.0), writes=[d_cmpVAs])
                KT16 = [P.sb([128, 16, 128], BF16, sS) for _ in range(2)]; d_KT16 = [Dep(), Dep()]
                V16 = [P.sb([128, 16, 2, 65], BF16, sS) for _ in range(2)]; d_V16 = [Dep(), Dep()]
                for t_ in V16:
                    P.op("pool", lambda e, t_=t_: e.memset(t_[:].rearrange("p a b c -> p (a b c)"), 1.0), writes=d_V16)
                Wt = P.sb([128, 4, 256], F32, sS); d_Wt = Dep()
                KTw = P.sb([128, 4, 128], BF16, sS); d_KTw = Dep()
                VwA = P.sb([128, 4, 2, 65], BF16, sS); d_VwA = Dep()
                P.op("pool", lambda e: e.memset(VwA[:].rearrange("p a b c -> p (a b c)"), 1.0), writes=[d_VwA])
                ptb = P.sb([128, 2], I32, sS); d_ptb = Dep()
                idx8 = P.sb([128, 9], I32, sS); d_idx8 = Dep()
                og = P.sb([SB, 65], F32, sS); d_og = Dep()
                rd1 = P.sb([SB, 2], F32, sS); d_rd1 = Dep()
                pn = P.sb([SB, 256], F32, sS); d_pn = Dep()
                srow = P.sb([1, 256], F32, sS); d_srow = Dep()
                wk2 = P.sb([1, 256], F32, sS); d_wk2 = Dep()
                m16 = P.sb([1, 16], F32, sS); d_m16 = Dep()
                selbB = [P.sb([1, 256], BF16, sS) for _ in range(2)]; d_selbB = [Dep(), Dep()]
                o4s = [banks[2][0:SB, 0:65].rearrange("p (g c) -> p g c", g=1), banks[3][0:SB, 0:65].rearrange("p (g c) -> p g c", g=1)]
                d_o4s = [d_bank[2], d_bank[3]]
                ones4f = oh4
                onec = P.sb([SB, 1], F32, sS); d_onec = Dep()
                P.op("dve", lambda e: e.memset(onec[:], 1.0), writes=[d_onec])

                def evac_s(b, kvh, br_i):
                    o4v, d_o = o4s[kvh], d_o4s[kvh]
                    P.op("dve", lambda e: e.tensor_scalar(out=rd1[:, 0:1], in0=o4v[:, 0, 64:65], scalar1=1e-30, scalar2=None, op0=ALU.max), reads=[d_o], writes=[d_rd1])
                    P.op("dve", lambda e: e.reciprocal(out=rd1[:, 0:1], in_=rd1[:, 0:1]), reads=[d_rd1], writes=[d_rd1])
                    P.op("dve", lambda e: e.tensor_scalar(out=og[:, 0:64], in0=o4v[:, 0, 0:64], scalar1=rd1[:, 0:1], scalar2=None, op0=ALU.mult),
                         reads=[d_o, d_rd1], writes=[d_og])
                    P.dma("sp", lambda e: e.dma_start(out=os_d[b, br_i, kvh * 4:(kvh + 1) * 4, :], in_=og[:, 0:64]), reads=[d_og])

                for b in range(SB):
                    P.dma("sp", lambda e, b=b: e.dma_start(out=ptb[:, 0:1], in_=ptab[b].rearrange("(p o) -> p o", o=1)), writes=[d_ptb])
                    P.dma("sp", lambda e, b=b: e.dma_start(out=ptb[0:127, 1:2], in_=ptab[b, 1:128].rearrange("(p o) -> p o", o=1)), writes=[d_ptb])
                    P.dma("sp", lambda e, b=b: e.dma_start(out=ptb[127:128, 1:2], in_=ptab[b, 127:128].rearrange("(p o) -> p o", o=1)), writes=[d_ptb])
                    for k in range(8):
                        P.op("dve", lambda e, k=k: e.tensor_scalar(out=idx8[:, k:k + 1], in0=ptb[:, 0:1], scalar1=8, scalar2=k, op0=ALU.mult, op1=ALU.add),
                             reads=[d_ptb], writes=[d_idx8])
                    P.op("dve", lambda e: e.tensor_scalar(out=idx8[:, 8:9], in0=ptb[:, 1:2], scalar1=8, scalar2=0, op0=ALU.mult, op1=ALU.add),
                         reads=[d_ptb], writes=[d_idx8])
                    for k in range(9):
                        gb_ = k % 2
                        P.dma("pool", lambda e, k=k, gb_=gb_: e.indirect_dma_start(
                            out=G[gb_][:].rearrange("p a b -> p (a b)"), out_offset=None, in_=cache_cmp[:, :],
                            in_offset=bass.IndirectOffsetOnAxis(ap=idx8[:, k:k + 1], axis=0)), reads=[d_idx8], writes=[d_G[gb_]])
                        for r in range(2):
                            if k == 8 and r == 0:
                                continue
                            P.op("pool", lambda e, gb_=gb_, r=r: e.tensor_tensor(out=prod[r][:], in0=G[gb_][:], in1=wbc[:, r, :, :], op=ALU.mult),
                                 reads=[d_G[gb_], d_wbc], writes=[d_prod[r]])
                            dstF = Fs[:, k, :] if r == 0 else S2[:, k, :]
                            P.op("dve", lambda e, r=r, dstF=dstF: e.tensor_reduce(out=dstF, in_=prod[r][:].rearrange("p j f -> p f j"), axis=AX.X, op=ALU.add),
                                 reads=[d_prod[r]], writes=[d_Fs if r == 0 else d_S2])
                    P.op("dve", lambda e: e.tensor_tensor(out=Fs[:], in0=Fs[:], in1=S2[:, 1:9, :], op=ALU.add), reads=[d_Fs, d_S2], writes=[d_Fs])
                    P.op("dve", lambda e: e.tensor_tensor(out=Fs[:], in0=Fs[:], in1=brow[:].unsqueeze(1).to_broadcast([128, 8, 256]), op=ALU.add),
                         reads=[d_Fs, d_brow], writes=[d_Fs])
                    for s_ in range(8):
                        P.op("pe", lambda e, s_=s_: e.transpose(out=pT[:, s_ * 128:(s_ + 1) * 128], in_=Fs[:, s_, 0:128], identity=ident[:]),
                             reads=[d_Fs, d_ident], writes=[d_pTs[s_ // 4]])
                    P.op("act", lambda e: e.copy(out=cmpKTs[:].rearrange("p a b -> p (a b)"), in_=pT[:]), reads=d_pTs, writes=[d_cmpKTs])
                    P.op("pool", lambda e: e.tensor_copy(out=cmpVAs[:, :, :, 0:64], in_=Fs[:, :, 128:256].rearrange("p s (h d) -> p s h d", h=2)),
                         reads=[d_Fs], writes=[d_cmpVAs])
                    for kvh in range(2):
                        QTh = QTs[kvh * 64:(kvh + 1) * 64, :, b]
                        tiles = []
                        for s_ in range(8):
                            bs = [(cm7row, ones4)] if s_ == 7 else []
                            tiles.append({"KT": cmpKTs[kvh * 64:(kvh + 1) * 64, s_, :], "V": cmpVAs[:, s_, kvh, :], "nk": 128, "biases": bs,
                                          "Ov": OvS[:, s_, :], "deps": [d_cmpKTs, d_cmpVAs, d_small, d_OvS]})
                        psvS = lambda g: banks[4][0:SB, 0:257]
                        attn(SB, 1, QTh, d_QTs, tiles, o4s[kvh], d_o4s[kvh], pslc=(psvS, [d_bank[4]], 1))
                        evac_s(b, kvh, 0)
                        P.op("dve", lambda e: e.tensor_scalar(out=rd1[:, 1:2], in0=banks[4][0:SB, 256:257], scalar1=1e-30, scalar2=None, op0=ALU.max),
                             reads=[d_bank[4]], writes=[d_rd1])
                        P.op("dve", lambda e: e.reciprocal(out=rd1[:, 1:2], in_=rd1[:, 1:2]), reads=[d_rd1], writes=[d_rd1])
                        P.op("dve", lambda e: e.tensor_scalar(out=pn[:], in0=banks[4][0:SB, 0:256], scalar1=rd1[:, 1:2], scalar2=None, op0=ALU.mult),
                             reads=[d_bank[4], d_rd1], writes=[d_pn])
                        P.op("pe", lambda e: e.matmul(out=banks[5][0:1, 0:256], lhsT=onec[:, 0:1], rhs=pn[:], start=True, stop=True),
                             reads=[d_onec, d_pn], writes=[d_bank[5]])
                        P.op("dve", lambda e: e.tensor_tensor(out=srow[:], in0=banks[5][0:1, 0:256], in1=forceS, op=ALU.add), reads=[d_bank[5], d_small], writes=[d_srow])
                        P.op("dve", lambda e: e.max(out=m16[:, 0:8], in_=srow[:]), reads=[d_srow], writes=[d_m16])
                        P.op("dve", lambda e: e.match_replace(out=wk2[:], in_to_replace=m16[:, 0:8], in_values=srow[:], imm_value=-3.0e38),
                             reads=[d_srow, d_m16], writes=[d_wk2])
                        P.op("dve", lambda e: e.max(out=m16[:, 8:16], in_=wk2[:]), reads=[d_wk2], writes=[d_m16])
                        P.op("dve", lambda e, kvh=kvh: e.tensor_scalar(out=selbB[kvh][:], in0=srow[:], scalar1=m16[:, 14:15], scalar2=NEG, op0=ALU.is_lt, op1=ALU.mult),
                             reads=[d_srow, d_m16], writes=[d_selbB[kvh]])
                    for k in range(8):
                        gb_ = k % 2
                        P.dma("pool", lambda e, k=k, gb_=gb_: e.indirect_dma_start(
                            out=G[gb_][:].rearrange("p a b -> p (a b)"), out_offset=None, in_=cache_sel[:, :],
                            in_offset=bass.IndirectOffsetOnAxis(ap=idx8[:, k:k + 1], axis=0)), reads=[d_idx8], writes=[d_G[gb_]])
                        for j in range(16):
                            if j % 4 == 0:
                                tb_, d_tb = (banks[4], d_bank[4]) if (j // 4) % 2 == 0 else (banks[5], d_bank[5])
                            P.op("pe", lambda e, gb_=gb_, j=j, tb_=tb_: e.transpose(out=tb_[:, (j % 4) * 128:(j % 4 + 1) * 128], in_=G[gb_][:, j, 0:128], identity=ident[:]),
                                 reads=[d_G[gb_], d_ident], writes=[d_tb])
                            if j % 4 == 3:
                                P.op("act", lambda e, gb_=gb_, j=j, tb_=tb_: e.copy(out=KT16[gb_][:, j - 3:j + 1, :].rearrange("p a b -> p (a b)"), in_=tb_[:, :]),
                                     reads=[d_tb], writes=[d_KT16[gb_]])
                        P.op("pool", lambda e, gb_=gb_: e.tensor_copy(out=V16[gb_][:, :, :, 0:64], in_=G[gb_][:, :, 128:256].rearrange("p j (h d) -> p j h d", h=2)),
                             reads=[d_G[gb_]], writes=[d_V16[gb_]])
                        for kvh in range(2):
                            QTh = QTs[kvh * 64:(kvh + 1) * 64, :, b]
                            tiles = []
                            for j in range(16):
                                hi = 1 if (16 * k + j) >= 64 else 0
                                tiles.append({"KT": KT16[gb_][kvh * 64:(kvh + 1) * 64, j, :], "V": V16[gb_][:, j, kvh, :], "nk": 128,
                                              "biases": [(selbB[kvh][0:1, hi:256:2], ones4)], "deps": [d_KT16[gb_], d_V16[gb_], d_selbB[kvh], d_small]})
                            attn(SB, 1, QTh, d_QTs, tiles, o4s[kvh], d_o4s[kvh], first=(k == 0), last=False)
                    for kvh in range(2):
                        QTh = QTs[kvh * 64:(kvh + 1) * 64, :, b]
                        tiles = [{"KT": KnT[kvh * 64:(kvh + 1) * 64, 0, :], "V": VnA[:, 0, kvh, :], "nk": SB,
                                  "biases": [(smallb[0:1, 256 + 4 * b:260 + 4 * b], ones4)], "deps": [d_KnT, d_VnA, d_small]}]
                        attn(SB, 1, QTh, d_QTs, tiles, o4s[kvh], d_o4s[kvh], first=False, last=True)
                        evac_s(b, kvh, 1)
                    P.dma("sp", lambda e, b=b: e.dma_start(out=Wt[:], in_=cache_win[b].rearrange("(m p) f -> p m f", p=128)), writes=[d_Wt])
                    for m in range(4):
                        P.op("pe", lambda e, m=m: e.transpose(out=banks[4][:, m * 128:(m + 1) * 128], in_=Wt[:, m, 0:128], identity=ident[:]),
                             reads=[d_Wt, d_ident], writes=[d_bank[4]])
                    P.op("act", lambda e: e.copy(out=KTw[:].rearrange("p a b -> p (a b)"), in_=banks[4][:, :]), reads=[d_bank[4]], writes=[d_KTw])
                    P.op("pool", lambda e: e.tensor_copy(out=VwA[:, :, :, 0:64], in_=Wt[:, :, 128:256].rearrange("p m (h d) -> p m h d", h=2)),
                         reads=[d_Wt], writes=[d_VwA])
                    for kvh in range(2):
                        QTh = QTs[kvh * 64:(kvh + 1) * 64, :, b]
                        tiles = []
                        for m in range(4):
                            bs = [(wm0row, ones4)] if m == 0 else []
                            tiles.append({"KT": KTw[kvh * 64:(kvh + 1) * 64, m, :], "V": VwA[:, m, kvh, :], "nk": 128, "biases": bs,
                                          "deps": [d_KTw, d_VwA, d_small]})
                        tiles.append({"KT": KnT[kvh * 64:(kvh + 1) * 64, 1, :], "V": VnA[:, 1, kvh, :], "nk": SB,
                                      "biases": [(smallb[0:1, 256 + 4 * b:260 + 4 * b], ones4)], "deps": [d_KnT, d_VnA, d_small]})
                        attn(SB, 1, QTh, d_QTs, tiles, o4s[kvh], d_o4s[kvh])
                        evac_s(b, kvh, 2)
                P.barrier()
                osall = P.sb([SB, 3, 8, 64], F32, sS); d_osall = Dep()
                onsa = P.sb([SB, 8, 64], F32, sS); d_onsa = Dep()
                tmpo = P.sb([SB, 8, 64], F32, sS); d_tmpo = Dep()
                P.dma("sp", lambda e: e.dma_start(out=osall[:].rearrange("p a b c -> p (a b c)"), in_=os_d.rearrange("b r h d -> b (r h d)")), writes=[d_osall])
                sgv = sg[:].rearrange("p (h r) -> p h r", r=3)
                for r in range(3):
                    dst_, d_dst_ = (onsa, d_onsa) if r == 0 else (tmpo, d_tmpo)
                    P.op("dve", lambda e, r=r, dst_=dst_: e.tensor_tensor(out=dst_[:], in0=osall[:, r, :, :], in1=sgv[:, :, r].unsqueeze(2).to_broadcast([SB, 8, 64]), op=ALU.mult),
                         reads=[d_osall, d_sg], writes=[d_dst_])
                    if r > 0:
                        P.op("dve", lambda e: e.tensor_tensor(out=onsa[:], in0=onsa[:], in1=tmpo[:], op=ALU.add), reads=[d_onsa, d_tmpo], writes=[d_onsa])
                onTs = P.sb([128, 4, SB], BF16, sS); d_onTs = Dep()
                onf = onsa[:].rearrange("p h d -> p (h d)")
                for c in range(4):
                    P.op("pe", lambda e, c=c: e.transpose(out=banks[3][:, c * SB:(c + 1) * SB], in_=onf[:, c * 128:(c + 1) * 128], identity=ident[0:SB, 0:SB]),
                         reads=[d_onsa, d_ident], writes=[d_bank[3]])
                P.op("act", lambda e: e.copy(out=onTs[:].rearrange("p a b -> p (a b)"), in_=banks[3][:, 0:4 * SB]), reads=[d_bank[3]], writes=[d_onTs])
                P.dma("sp", lambda e: e.dma_start(out=onTs_d, in_=onTs[:]), reads=[d_onTs])
                P.barrier()

        def phase_b2(groups):
            with ExitStack() as s2:
                Wgab = P.sb([128, 8, 2048], BF16, s2); d_Wgab = Dep()
                Wbn = P.sb([128, 4, 1024], BF16, s2); d_Wbn = Dep()
                Wbr = P.sb([128, 8, 1024], BF16, s2); d_Wbr = Dep()
                Wout = P.sb([128, 8, 1024], BF16, s2); d_Wout = Dep()
                P.dma("pool", lambda e: e.dma_start(out=Wgab[:], in_=w_in[:, C_GA:C_GA + 2048].rearrange("(c p) n -> p c n", p=128)), writes=[d_Wgab])
                P.dma("pool", lambda e: e.dma_start(out=Wbn[:], in_=w_bn.rearrange("(c p) n -> p c n", p=128)), writes=[d_Wbn])
                P.dma("pool", lambda e: e.dma_start(out=Wbr[:], in_=w_br.rearrange("(c p) n -> p c n", p=128)), writes=[d_Wbr])
                P.dma("pool", lambda e: e.dma_start(out=Wout[:], in_=w_out.rearrange("(c p) n -> p c n", p=128)), writes=[d_Wout])
                hT4 = P.sb([128, 8, 512], BF16, s2); d_hT4 = Dep()
                onT4 = P.sb([128, 4, 512], BF16, s2); d_onT4 = Dep()
                yrT4 = P.sb([128, 8, 512], BF16, s2); d_yrT4 = Dep()
                mixT = P.sb([128, 8, 512], BF16, s2); d_mixT = Dep()
                sga = [P.sb([128, 512], F32, s2) for _ in range(2)]; d_sga = [Dep(), Dep()]
                sgb = [P.sb([128, 512], F32, s2) for _ in range(2)]; d_sgb = [Dep(), Dep()]
                t1 = [P.sb([128, 512], F32, s2) for _ in range(2)]; d_t1 = [Dep(), Dep()]
                t2 = [P.sb([128, 512], F32, s2) for _ in range(2)]; d_t2 = [Dep(), Dep()]
                xa = [P.sb([128, D], F32, s2) for _ in range(2)]; d_xa = [Dep(), Dep()]
                x1t = [P.sb([128, D], F32, s2) for _ in range(2)]; d_x1t = [Dep(), Dep()]
                gtaS = P.sb([SB, D], F32, s2); d_gtaS = Dep()
                P.dma("sp", lambda e: e.dma_start(out=gtaS[:], in_=mods_d[:, 2 * D:3 * D]), writes=[d_gtaS])
                for grp in groups:
                    n = grp["ntok"]
                    grp["load_a"](hT4, d_hT4, onT4, d_onT4, yrT4, d_yrT4)
                    for fc in range(8):
                        b = fc % 2
                        for (bk, W, dW, kc, c0, src, dsrc) in ((0, Wgab, d_Wgab, 8, fc * 128, hT4, d_hT4), (1, Wgab, d_Wgab, 8, 1024 + fc * 128, hT4, d_hT4),
                                                              (2, Wbn, d_Wbn, 4, fc * 128, onT4, d_onT4), (3, Wbr, d_Wbr, 8, fc * 128, yrT4, d_yrT4)):
                            for k in range(kc):
                                P.op("pe", lambda e, bk=bk, W=W, k=k, c0=c0, src=src, kc=kc, n=n: e.matmul(
                                    out=banks[bk][:, 0:n], lhsT=W[:, k, c0:c0 + 128], rhs=src[:, k, 0:n], start=(k == 0), stop=(k == kc - 1)),
                                    reads=[dW, dsrc], writes=[d_bank[bk]])
                        P.op("act", lambda e, b=b, n=n: e.activation(out=sga[b][:, 0:n], in_=banks[0][:, 0:n], func=ACTF.Sigmoid), reads=[d_bank[0]], writes=[d_sga[b]])
                        P.op("act", lambda e, b=b, n=n: e.activation(out=sgb[b][:, 0:n], in_=banks[1][:, 0:n], func=ACTF.Sigmoid), reads=[d_bank[1]], writes=[d_sgb[b]])
                        P.op("dve", lambda e, b=b, n=n: e.tensor_tensor(out=t1[b][:, 0:n], in0=banks[2][:, 0:n], in1=sga[b][:, 0:n], op=ALU.mult),
                             reads=[d_bank[2], d_sga[b]], writes=[d_t1[b]])
                        P.op("dve", lambda e, b=b, n=n: e.tensor_tensor(out=t2[b][:, 0:n], in0=banks[3][:, 0:n], in1=sgb[b][:, 0:n], op=ALU.mult),
                             reads=[d_bank[3], d_sgb[b]], writes=[d_t2[b]])
                        P.op("pool", lambda e, b=b, fc=fc, n=n: e.tensor_tensor(out=mixT[:, fc, 0:n], in0=t1[b][:, 0:n], in1=t2[b][:, 0:n], op=ALU.add),
                             reads=[d_t1[b], d_t2[b]], writes=[d_mixT])
                    off = 0
                    for ti, (xsrc, nt_, gta, d_gta, x1dst) in enumerate(grp["tiles_a"](gtaS, d_gtaS)):
                        b = ti % 2
                        P.dma("sp", lambda e, b=b, xsrc=xsrc, nt_=nt_: e.dma_start(out=xa[b][0:nt_, :], in_=xsrc), writes=[d_xa[b]])
                        for half in range(2):
                            bk = 4 + half
                            for k in range(8):
                                P.op("pe", lambda e, bk=bk, k=k, off=off, nt_=nt_, half=half: e.matmul(
                                    out=banks[bk][0:nt_, :], lhsT=mixT[:, k, off:off + nt_], rhs=Wout[:, k, half * 512:(half + 1) * 512],
                                    start=(k == 0), stop=(k == 7)), reads=[d_mixT, d_Wout], writes=[d_bank[bk]])
                            P.op("dve", lambda e, bk=bk, b=b, nt_=nt_, half=half, gta=gta: e.tensor_tensor(
                                out=x1t[b][0:nt_, half * 512:(half + 1) * 512], in0=banks[bk][0:nt_, :], in1=gta[0:nt_, half * 512:(half + 1) * 512], op=ALU.mult),
                                reads=[d_bank[bk], d_gta], writes=[d_x1t[b]])
                        P.op("pool", lambda e, b=b, nt_=nt_: e.tensor_tensor(out=x1t[b][0:nt_, :], in0=x1t[b][0:nt_, :], in1=xa[b][0:nt_, :], op=ALU.add),
                             reads=[d_x1t[b], d_xa[b]], writes=[d_x1t[b]])
                        P.dma("sp", lambda e, b=b, nt_=nt_, x1dst=x1dst: e.dma_start(out=x1dst, in_=x1t[b][0:nt_, :]), reads=[d_x1t[b]])
                        off += nt_
                P.barrier()
            with ExitStack() as s3:
                Wup = P.sb([128, 8, 4096], BF16, s3); d_Wup = Dep()
                Wdn = P.sb([128, 32, 1024], BF16, s3); d_Wdn = Dep()
                for q4 in range(4):
                    P.dma("pool", lambda e, q4=q4: e.dma_start(out=Wup[:, :, q4 * 1024:(q4 + 1) * 1024],
                                                              in_=w_up[:, q4 * 1024:(q4 + 1) * 1024].rearrange("(c p) n -> p c n", p=128)), writes=[d_Wup])
                    P.dma("pool", lambda e, q4=q4: e.dma_start(out=Wdn[:, q4 * 8:(q4 + 1) * 8, :],
                                                              in_=w_down[q4 * 1024:(q4 + 1) * 1024, :].rearrange("(c p) n -> p c n", p=128)), writes=[d_Wdn])
                nfb = P.sb([128, D], F32, s3); d_nfb = Dep()
                P.dma("sp", lambda e: e.dma_start(out=nfb[:], in_=norm_final_d.partition_broadcast(128)[:, 0, :]), writes=[d_nfb])
                h2T4 = P.sb([128, 8, 128], BF16, s3); d_h2T4 = Dep()
                uT = P.sb([128, 32, 128], BF16, s3); d_uT = Dep()
                x1in = [P.sb([128, D], F32, s3) for _ in range(2)]; d_x1in = [Dep(), Dep()]
                rl = [P.sb([128, 128], BF16, s3) for _ in range(2)]; d_rl = [Dep(), Dep()]
                x2t = [P.sb([128, D], F32, s3) for _ in range(2)]; d_x2t = [Dep(), Dep()]
                ss2 = P.sb([128, 2], F32, s3); d_ss2 = Dep()
                wk3 = mk_wk(s3)
                modS3 = P.sb([SB, 3, D], F32, s3); d_modS3 = Dep()
                P.dma("sp", lambda e: e.dma_start(out=modS3[:], in_=mods_d[:, 3 * D:6 * D].rearrange("s (a d) -> s a d", a=3)), writes=[d_modS3])
                P.dma("sp", lambda e: e.dma_start(out=x2t[1][0:SB, :], in_=norm_mlp.partition_broadcast(SB)[:, 0, :]), writes=[d_x2t[1]])
                P.op("dve", lambda e: e.scalar_tensor_tensor(out=modS3[:, 1, :], in0=modS3[:, 1, :], scalar=1.0, in1=x2t[1][0:SB, :], op0=ALU.add, op1=ALU.mult),
                     reads=[d_modS3, d_x2t[1]], writes=[d_modS3])
                for grp in groups:
                    tl_all = grp["tiles_b"](modS3, d_modS3)
                    for s0_ in range(0, len(tl_all), 1):
                        tl_b = tl_all[s0_:s0_ + 1]
                        n = sum(t_[1] for t_ in tl_b)
                        off = 0
                        for ti, (x1src, nt_, G2, d_G2, shf, d_shf, gtf, d_gtf, ydst) in enumerate(tl_b):
                            P.dma("sp", lambda e, ti=ti, x1src=x1src, nt_=nt_: e.dma_start(out=x1in[ti][0:nt_, :], in_=x1src), writes=[d_x1in[ti]])
                            norm_mod_T(x1in[ti], d_x1in[ti], nt_, G2, d_G2, shf, d_shf, wk3, h2T4[:, :, off:off + nt_], d_h2T4)
                            off += nt_
                        for fc in range(32):
                            b = fc % 2
                            bk = fc % 4
                            for k in range(8):
                                P.op("pe", lambda e, bk=bk, k=k, fc=fc, n=n: e.matmul(out=banks[bk][:, 0:n], lhsT=Wup[:, k, fc * 128:(fc + 1) * 128], rhs=h2T4[:, k, 0:n],
                                                                                    start=(k == 0), stop=(k == 7)), reads=[d_Wup, d_h2T4], writes=[d_bank[bk]])
                            P.op("act", lambda e, bk=bk, b=b, n=n: e.activation(out=rl[b][:, 0:n], in_=banks[bk][:, 0:n], func=ACTF.Relu), reads=[d_bank[bk]], writes=[d_rl[b]])
                            P.op("pool", lambda e, b=b, fc=fc, n=n: e.tensor_tensor(out=uT[:, fc, 0:n], in0=rl[b][:, 0:n], in1=rl[b][:, 0:n], op=ALU.mult),
                                 reads=[d_rl[b]], writes=[d_uT])
                        off = 0
                        for ti, (x1src, nt_, G2, d_G2, shf, d_shf, gtf, d_gtf, ydst) in enumerate(tl_b):
                            b = ti % 2
                            for half in range(2):
                                bk = 4 + half
                                for k in range(32):
                                    P.op("pe", lambda e, bk=bk, k=k, off=off, nt_=nt_, half=half: e.matmul(
                                        out=banks[bk][0:nt_, :], lhsT=uT[:, k, off:off + nt_], rhs=Wdn[:, k, half * 512:(half + 1) * 512],
                                        start=(k == 0), stop=(k == 31)), reads=[d_uT, d_Wdn], writes=[d_bank[bk]])
                                P.op("dve", lambda e, bk=bk, b=b, nt_=nt_, half=half, gtf=gtf: e.tensor_tensor(
                                    out=x2t[b][0:nt_, half * 512:(half + 1) * 512], in0=banks[bk][0:nt_, :], in1=gtf[0:nt_, half * 512:(half + 1) * 512], op=ALU.mult),
                                    reads=[d_bank[bk], d_gtf], writes=[d_x2t[b]])
                            P.op("pool", lambda e, b=b, ti=ti, nt_=nt_: e.tensor_tensor(out=x2t[b][0:nt_, :], in0=x2t[b][0:nt_, :], in1=x1in[ti][0:nt_, :], op=ALU.add),
                                 reads=[d_x2t[b], d_x1in[ti]], writes=[d_x2t[b]])
                            P.op("act", lambda e, b=b, nt_=nt_: e.activation(out=junk_sh[0:nt_, :], in_=x2t[b][0:nt_, :], func=ACTF.Square, accum_out=ss2[0:nt_, 0:1]),
                                 reads=[d_x2t[b]], writes=[d_junk_sh, d_ss2])
                            P.op("act", lambda e, nt_=nt_: e.activation(out=ss2[0:nt_, 1:2], in_=ss2[0:nt_, 0:1], func=ACTF.Sqrt, bias=epsT[0:nt_, :], scale=1.0 / D),
                                 reads=[d_ss2, d_eps], writes=[d_ss2])
                            P.op("dve", lambda e, nt_=nt_: e.reciprocal(out=ss2[0:nt_, 1:2], in_=ss2[0:nt_, 1:2]), reads=[d_ss2], writes=[d_ss2])
                            P.op("dve", lambda e, b=b, nt_=nt_: e.scalar_tensor_tensor(out=x2t[b][0:nt_, :], in0=x2t[b][0:nt_, :], scalar=ss2[0:nt_, 1:2], in1=nfb[0:nt_, :],
                                                                                     op0=ALU.mult, op1=ALU.mult), reads=[d_x2t[b], d_ss2, d_nfb], writes=[d_x2t[b]])
                            P.dma("sp", lambda e, b=b, nt_=nt_, ydst=ydst: e.dma_start(out=ydst, in_=x2t[b][0:nt_, :]), reads=[d_x2t[b]])
                            off += nt_
                P.barrier()

        with ExitStack() as sA:
            WA = P.sb([128, 8, 2304], BF16, sA); d_WA = Dep()
            segs = [(C_KVC, 128, 0), (C_KVS, 128, 128), (C_KVW, 128, 256), (C_KVC + 128, 128, 384),
                    (C_KVS + 128, 128, 512), (C_KVW + 128, 128, 640), (C_RK, 512, 768), (C_RV, 1024, 1280)]
            for (src, n, dst) in segs:
                P.dma("pool", lambda e, src=src, n=n, dst=dst: e.dma_start(
                    out=WA[:, :, dst:dst + n], in_=w_in[:, src:src + n].rearrange("(c p) n -> p c n", p=128)), writes=[d_WA])
            inda = P.sb([128, 8], F32, sA); indab = P.sb([128, 8], BF16, sA); d_inda = Dep()
            wrep = P.sb([128, 512], F32, sA); d_wrep = Dep()
            bcol = P.sb([128, 2], F32, sA); d_bcol = Dep()
            kdsc = P.sb([128, 4], F32, sA); d_kdsc = Dep()
            oh = P.sb([128, 8], F32, sA); d_oh = Dep()
            P.dma("sp", lambda e: e.dma_start(out=inda[:], in_=inda_d), writes=[d_inda])
            P.op("dve", lambda e: e.tensor_copy(out=indab[:], in_=inda[:]), reads=[d_inda], writes=[d_inda])
            P.dma("sp", lambda e: e.dma_start(out=wrep[:], in_=wrep_d), writes=[d_wrep])
            P.dma("sp", lambda e: e.dma_start(out=bcol[:], in_=bcol_d), writes=[d_bcol])
            P.dma("sp", lambda e: e.dma_start(out=kdsc[:], in_=kdsc_d), writes=[d_kdsc])
            P.dma("sp", lambda e: e.dma_start(out=oh[:], in_=oh_d), writes=[d_oh])
            FS = P.sb([128, 4, 1024], F32, sA); d_FS = Dep()
            S = P.sb([128, 4, 256], F32, sA); d_S = Dep()
            Sown = P.sb([128, 1024], F32, sA); d_Sown = Dep()
            tmpS = P.sb([128, 1024], F32, sA); d_tmpS = Dep()
            P.op("pool", lambda e: e.memset(S[:].rearrange("p a b -> p (a b)"), 0.0), writes=[d_S])
            NBUF = 2
            xts = [P.sb([128, D], F32, sA) for _ in range(NBUF)]; d_xts = [Dep() for _ in range(NBUF)]
            rps = [P.sb([128, 192], F32, sA) for _ in range(NBUF)]; d_rps = [Dep() for _ in range(NBUF)]
            wks = [mk_wk(sA) for _ in range(NBUF)]
            hTs = [P.sb([128, 8, 128], BF16, sA) for _ in range(NBUF)]; d_hTs = [Dep() for _ in range(NBUF)]
            kvos = [P.sb([128, 3, 256], F32, sA) for _ in range(NBUF)]; d_kvos = [Dep() for _ in range(NBUF)]
            tm = [[P.sb([128, 256], F32, sA) for _ in range(4)] for _ in range(NBUF)]
            d_tm = [[Dep() for _ in range(4)] for _ in range(NBUF)]
            rkr = [P.sb([128, 4, 128], F32, sA) for _ in range(NBUF)]; d_rkr = [Dep() for _ in range(NBUF)]
            kd = [P.sb([128, 4, 128], BF16, sA) for _ in range(NBUF)]; d_kd = [Dep() for _ in range(NBUF)]
            vb = [P.sb([128, 1024], BF16, sA) for _ in range(NBUF)]; d_vb = [Dep() for _ in range(NBUF)]
            cw = [P.sb([128, 2, 256], BF16, sA) for _ in range(NBUF)]; d_cw = [Dep() for _ in range(NBUF)]
            kwt = [P.sb([128, 128], BF16, sA) for _ in range(NBUF)]; d_kwt = [Dep() for _ in range(NBUF)]
            kst = [P.sb([128, 128], BF16, sA) for _ in range(NBUF)]; d_kst = [Dep() for _ in range(NBUF)]
            vwb = [P.sb([128, 2, 65], BF16, sA) for _ in range(NBUF)]; d_vwb = [Dep() for _ in range(NBUF)]
            vsb = [P.sb([128, 2, 65], BF16, sA) for _ in range(NBUF)]; d_vsb = [Dep() for _ in range(NBUF)]
            for b_ in range(NBUF):
                P.op("pool", lambda e, b_=b_: e.memset(vwb[b_][:].rearrange("p a b -> p (a b)"), 1.0), writes=[d_vwb[b_]])
                P.op("pool", lambda e, b_=b_: e.memset(vsb[b_][:].rearrange("p a b -> p (a b)"), 1.0), writes=[d_vsb[b_]])
            zA, zB, zK, zV0, zV1, zC = banks
            dzA, dzB, dzK, dzV0, dzV1, dzC = d_bank
            d_zBt = dzB

            ntA = NT if stage >= 1 else 0
            if 'K_NTA' in os.environ:
                ntA = int(os.environ['K_NTA'])
            if stage == 2:
                ntA = 8 * DBG_NBLK
                P.op("pool", lambda e: e.memset(FS[:].rearrange("p a b -> p (a b)"), 0.0), writes=[d_FS])
            for tt in range(ntA):
                b = tt % NBUF
                xt, d_xt, rp, d_rp, hT, d_hT, kvo, d_kvo = xts[b], d_xts[b], rps[b], d_rps[b], hTs[b], d_hTs[b], kvos[b], d_kvos[b]
                P.dma("sp", lambda e, tt=tt, xt=xt: e.dma_start(out=xt[:], in_=x_all[tt * 128:(tt + 1) * 128, :]), writes=[d_xt])
                P.dma("sp", lambda e, tt=tt, rp=rp: e.dma_start(out=rp[:], in_=rope_all[tt * 128:(tt + 1) * 128, :]), writes=[d_rp])
                norm_mod_T(xt, d_xt, 128, G1P, d_G1P, modP4[:, 0, :], d_modP, wks[b], hT, d_hT)
                for (zz, dz, c0, n) in ((zA, dzA, 0, 512), (zB, dzB, 512, 256), (zK, dzK, 768, 512), (zV0, dzV0, 1280, 512), (zV1, dzV1, 1792, 512)):
                    for k in range(8):
                        P.op("pe", lambda e, zz=zz, k=k, c0=c0, n=n, hT=hT: e.matmul(out=zz[:, 0:n], lhsT=hT[:, k, :], rhs=WA[:, k, c0:c0 + n],
                                                                                    start=(k == 0), stop=(k == 7)),
                             reads=[d_hT, d_WA], writes=[dz])
                src = zA[:, 0:384].rearrange("p (j h two e) -> p j h two e", j=3, h=2, two=2)
                dst = kvo[:, :, 0:128].rearrange("p j (h two e) -> p j h two e", h=2, two=2)
                shp = [128, 3, 2, 32]
                tv = [t[:, 0:192].rearrange("p (j h e) -> p j h e", j=3, h=2) for t in tm[b]]
                cosn = rp[:, 0:32].unsqueeze(1).unsqueeze(1).to_broadcast(shp)
                sinn = rp[:, 32:64].unsqueeze(1).unsqueeze(1).to_broadcast(shp)
                rope(lambda hf, src=src: src[:, :, :, hf, :], lambda hf, dst=dst: dst[:, :, :, hf, :], cosn, sinn, shp, tv,
                     dzA, d_kvo, d_rp, d_tm[b])
                P.op("act", lambda e, kvo=kvo: e.copy(out=kvo[:, 0, 128:256], in_=zA[:, 384:512]), reads=[dzA], writes=[d_kvo])
                P.op("act", lambda e, kvo=kvo: e.copy(out=kvo[:, 1:3, 128:256], in_=zB[:, 0:256].rearrange("p (j f) -> p j f", j=2)),
                     reads=[dzB], writes=[d_kvo])
                P.dma("sp", lambda e, tt=tt, kvo=kvo: e.dma_start(out=o_cmp[tt * 128:(tt + 1) * 128, :], in_=kvo[:, 0, :]), reads=[d_kvo])
                P.dma("sp", lambda e, tt=tt, kvo=kvo: e.dma_start(out=o_sel[tt * 128:(tt + 1) * 128, :], in_=kvo[:, 1, :]), reads=[d_kvo])
                if tt >= NT - 4:
                    P.dma("sp", lambda e, tt=tt, kvo=kvo: e.dma_start(out=o_win[(tt - NT + 4) * 128:(tt - NT + 5) * 128, :], in_=kvo[:, 2, :]),
                          reads=[d_kvo])
                P.op("pool", lambda e, kvo=kvo, b=b: e.tensor_copy(out=vsb[b][:, :, 0:64], in_=kvo[:, 1, 128:256].rearrange("p (h d) -> p h d", h=2)),
                     reads=[d_kvo], writes=[d_vsb[b]])
                P.dma("sp", lambda e, tt=tt, b=b: e.dma_start(out=vs_d[tt * 128:(tt + 1) * 128, :], in_=vsb[b][:].rearrange("p a b -> p (a b)")),
                      reads=[d_vsb[b]])
                P.op("pool", lambda e, kvo=kvo, b=b: e.tensor_copy(out=vwb[b][:, :, 0:64], in_=kvo[:, 2, 128:256].rearrange("p (h d) -> p h d", h=2)),
                     reads=[d_kvo], writes=[d_vwb[b]])
                P.dma("sp", lambda e, tt=tt, b=b: e.dma_start(out=vw_d[tt * 128:(tt + 1) * 128, :], in_=vwb[b][:].rearrange("p a b -> p (a b)")),
                      reads=[d_vwb[b]])
                P.op("pe", lambda e, kvo=kvo: e.transpose(out=zB[:, 256:384], in_=kvo[:, 1, 0:128], identity=ident[:]),
                     reads=[d_kvo, d_ident], writes=[d_zBt])
                P.op("pe", lambda e, kvo=kvo: e.transpose(out=zB[:, 384:512], in_=kvo[:, 2, 0:128], identity=ident[:]),
                     reads=[d_kvo, d_ident], writes=[d_zBt])
                P.op("act", lambda e, b=b: e.copy(out=kst[b][:], in_=zB[:, 256:384]), reads=[d_zBt], writes=[d_kst[b]])
                P.dma("sp", lambda e, tt=tt, b=b: e.dma_start(out=ksT_d[:, tt * 128:(tt + 1) * 128], in_=kst[b][:]), reads=[d_kst[b]])
                P.op("act", lambda e, b=b: e.copy(out=kwt[b][:], in_=zB[:, 384:512]), reads=[d_zBt], writes=[d_kwt[b]])
                P.dma("sp", lambda e, tt=tt, b=b: e.dma_start(out=kwT_d[:, tt * 128:(tt + 1) * 128], in_=kwt[b][:]), reads=[d_kwt[b]])
                for r in range(2):
                    P.op("pool", lambda e, r=r, b=b, kvo=kvo: e.tensor_tensor(out=cw[b][:, r, :], in0=kvo[:, 0, :], in1=wrep[:, r * 256:(r + 1) * 256],
                                                                             op=ALU.mult), reads=[d_kvo, d_wrep], writes=[d_cw[b]])
                for r in range(2):
                    for kv in range(2):
                        i4 = r * 2 + kv
                        P.op("pe", lambda e, r=r, kv=kv, i4=i4, b=b: e.matmul(out=zC[:, i4 * 8:(i4 + 1) * 8], lhsT=cw[b][:, r, kv * 128:(kv + 1) * 128],
                                                                             rhs=indab[:], start=True, stop=True),
                             reads=[d_cw[b], d_inda], writes=[dzC])
                P.op("act", lambda e, tt=tt: e.copy(out=FS[:, :, tt * 8:(tt + 1) * 8], in_=zC[:, 0:32].rearrange("p (a s) -> p a s", a=4)),
                     reads=[dzC], writes=[d_FS])
                srck = zK[:, 0:512].rearrange("p (h two e) -> p h two e", h=4, two=2)
                dstk = rkr[b][:].rearrange("p h (two e) -> p h two e", two=2)
                shpk = [128, 4, 64]
                tvk = [t[:, 0:256].rearrange("p (h e) -> p h e", h=4) for t in tm[b]]
                cosk = rp[:, 64:128].unsqueeze(1).to_broadcast(shpk)
                sink = rp[:, 128:192].unsqueeze(1).to_broadcast(shpk)
                rope(lambda hf, srck=srck: srck[:, :, hf, :], lambda hf, dstk=dstk: dstk[:, :, hf, :], cosk, sink, shpk, tvk,
                     dzK, d_rkr[b], d_rp, d_tm[b])
                P.op("pool", lambda e, b=b: e.tensor_tensor(out=kd[b][:], in0=rkr[b][:], in1=kdsc[:].unsqueeze(2).to_broadcast([128, 4, 128]),
                                                           op=ALU.mult), reads=[d_rkr[b], d_kdsc], writes=[d_kd[b]])
                P.op("act", lambda e, b=b: e.copy(out=vb[b][:, 0:512], in_=zV0[:, :]), reads=[dzV0], writes=[d_vb[b]])
                P.op("act", lambda e, b=b: e.copy(out=vb[b][:, 512:1024], in_=zV1[:, :]), reads=[dzV1], writes=[d_vb[b]])
                j = tt % 8
                if j == 0:
                    P.op("pool", lambda e: e.tensor_scalar(out=Sown[:], in0=S[:].rearrange("p a b -> p (a b)"), scalar1=oh[:, 0:1], scalar2=None,
                                                           op0=ALU.mult), reads=[d_S, d_oh], writes=[d_Sown])
                else:
                    P.op("pool", lambda e, j=j: e.tensor_scalar(out=tmpS[:], in0=S[:].rearrange("p a b -> p (a b)"), scalar1=oh[:, j:j + 1], scalar2=None,
                                                               op0=ALU.mult), reads=[d_S, d_oh], writes=[d_tmpS])
                    P.op("pool", lambda e: e.tensor_tensor(out=Sown[:], in0=Sown[:], in1=tmpS[:], op=ALU.add),
                         reads=[d_tmpS, d_Sown], writes=[d_Sown])
                if j == 7:
                    P.dma("sp", lambda e, i=tt // 8: e.dma_start(out=sown_d[i], in_=Sown[:]), reads=[d_Sown])
                for h in range(4):
                    zz, dz = (zV0, dzV0) if h < 2 else (zV1, dzV1)
                    c0 = (h % 2) * 256
                    P.op("pe", lambda e, h=h, zz=zz, c0=c0, b=b: e.matmul(out=zz[:, c0:c0 + 256], lhsT=kd[b][:, h, :], rhs=vb[b][:, h * 256:(h + 1) * 256],
                                                                         start=True, stop=True),
                         reads=[d_kd[b], d_vb[b]], writes=[dz])
                    P.op("dve", lambda e, h=h, zz=zz, c0=c0: e.scalar_tensor_tensor(out=S[:, h, :], in0=S[:, h, :], scalar=float(math.exp(128 * LG[h])),
                                                                                   in1=zz[:, c0:c0 + 256], op0=ALU.mult, op1=ALU.add),
                         reads=[d_S, dz], writes=[d_S])
            if ntA:
                P.dma("sp", lambda e: e.dma_start(out=o_state.rearrange("h k v -> k h v"), in_=S[:]), reads=[d_S])
                ctmp, cv = tmpS, Sown
                P.op("pool", lambda e: e.memset(cmpKT[:], 0.0), writes=[d_cmpKT])
                P.op("pool", lambda e: e.memset(cv[:], 0.0), reads=[], writes=[d_Sown])
                P.op("dve", lambda e: e.tensor_tensor(out=ctmp[:, 0:1023], in0=FS[:, 0, 0:1023], in1=FS[:, 2, 1:1024], op=ALU.add),
                     reads=[d_FS], writes=[d_tmpS])
                P.op("dve", lambda e: e.tensor_scalar(out=cmpKT[:, 0:1023], in0=ctmp[:, 0:1023], scalar1=bcol[:, 0:1], scalar2=None, op0=ALU.add),
                     reads=[d_tmpS, d_bcol], writes=[d_cmpKT])
                P.op("dve", lambda e: e.tensor_tensor(out=ctmp[:, 0:1023], in0=FS[:, 1, 0:1023], in1=FS[:, 3, 1:1024], op=ALU.add),
                     reads=[d_FS], writes=[d_tmpS])
                P.op("dve", lambda e: e.tensor_scalar(out=cv[:, 0:1023], in0=ctmp[:, 0:1023], scalar1=bcol[:, 1:2], scalar2=None, op0=ALU.add),
                     reads=[d_tmpS, d_bcol], writes=[d_Sown])
                for jt in range(8):
                    P.op("pe", lambda e, jt=jt: e.transpose(out=pT[:, jt * 128:(jt + 1) * 128], in_=cv[:, jt * 128:(jt + 1) * 128], identity=ident[:]),
                         reads=[d_Sown, d_ident], writes=[d_pTs[jt // 4]])
                P.op("act", lambda e: e.copy(out=cmpVA[:, :, :, 0:64], in_=pT[:].rearrange("p (j h d) -> p j h d", j=8, h=2)),
                     reads=d_pTs, writes=[d_cmpVA])
            P.barrier()

        if stage >= 2:
            phase_b1()
        if stage >= 4:
            phase_s1()
        if stage >= 3:
            groups = []
            for gq in range(4):
                def load_a(hT4, d_hT4, onT4, d_onT4, yrT4, d_yrT4, gq=gq):
                    for j in range(4):
                        i_ = 4 * gq + j
                        P.dma("sp", lambda e, i_=i_, j=j: e.dma_start(out=hT4[:, :, j * 128:(j + 1) * 128], in_=hT_d[i_]), writes=[d_hT4])
                        P.dma("sp", lambda e, i_=i_, j=j: e.dma_start(out=onT4[:, :, j * 128:(j + 1) * 128], in_=onT_d[i_]), writes=[d_onT4])
                        P.dma("sp", lambda e, i_=i_, j=j: e.dma_start(out=yrT4[:, :, j * 128:(j + 1) * 128], in_=yrT_d[i_]), writes=[d_yrT4])

                def tiles_a(gtaS, d_gtaS, gq=gq):
                    return [(x_own[(4 * gq + j) * 128:(4 * gq + j + 1) * 128, :], 128, modP4[:, 1, :], d_modP,
                             x1_d[(4 * gq + j) * 128:(4 * gq + j + 1) * 128, :]) for j in range(4)]

                def tiles_b(modS3, d_modS3, gq=gq):
                    return [(x1_d[(4 * gq + j) * 128:(4 * gq + j + 1) * 128, :], 128, G2P, d_G2P, modP4[:, 2, :], d_modP, modP4[:, 3, :], d_modP,
                             y_own[(4 * gq + j) * 128:(4 * gq + j + 1) * 128, :]) for j in range(4)]
                groups.append({"ntok": 512, "load_a": load_a, "tiles_a": tiles_a, "tiles_b": tiles_b})
            if stage >= 4:
                def load_as(hT4, d_hT4, onT4, d_onT4, yrT4, d_yrT4):
                    P.dma("sp", lambda e: e.dma_start(out=hT4[:, :, 0:SB], in_=hTs_d), writes=[d_hT4])
                    P.dma("sp", lambda e: e.dma_start(out=onT4[:, :, 0:SB], in_=onTs_d), writes=[d_onT4])
                    P.dma("sp", lambda e: e.dma_start(out=yrT4[:, :, 0:SB], in_=yrTs_d), writes=[d_yrT4])
                groups.append({"ntok": SB, "load_a": load_as,
                               "tiles_a": lambda gtaS, d_gtaS: [(x_smp, SB, gtaS[:], d_gtaS, x1s_d)],
                               "tiles_b": lambda modS3, d_modS3: [(x1s_d, SB, modS3[:, 1, :], d_modS3, modS3[:, 0, :], d_modS3, modS3[:, 2, :], d_modS3, y_smp)]})
            phase_b2(groups)

        P.barrier()
        P.emit()
    return nc


def _rope_tables(pos):
    pos = np.asarray(pos, np.float32)
    out = np.zeros((len(pos), 192), np.float32)
    inv32 = (10000.0 ** (-np.arange(0, 64, 2, dtype=np.float32) / 64)).astype(np.float32)
    inv64 = (10000.0 ** (-np.arange(0, 128, 2, dtype=np.float32) / 128)).astype(np.float32)
    a32 = pos[:, None] * inv32[None, :]
    a64 = pos[:, None] * inv64[None, :]
    out[:, 0:32] = np.cos(a32)
    out[:, 32:64] = np.sin(a32)
    out[:, 64:128] = np.cos(a64)
    out[:, 128:192] = np.sin(a64)
    return out


def _consts(core):
    cs = {}
    cs["ident"] = np.eye(128, dtype=np.float32)
    sel5 = np.zeros((5, 132), np.float32)
    sel5[0, 0:128] = 1.0
    for s in range(4):
        sel5[1 + s, 128 + s] = 1.0
    cs["sel5"] = sel5
    inda = np.zeros((128, 8), np.float32)
    inda[np.arange(128), np.arange(128) // 16] = 1.0
    cs["inda"] = inda
    i = np.arange(128, dtype=np.float64)
    kdsc = np.zeros((128, 4), np.float32)
    for h in range(4):
        kdsc[:, h] = RET_SCALE * np.exp((127.0 - i) * LG[h])
    cs["kdsc"] = kdsc
    oh = np.zeros((128, 8), np.float32)
    oh[:, core] = 1.0
    cs["oh"] = oh
    return cs


def _core_tables(c):
    q = np.arange(128)
    n = np.arange(1024)
    j = np.arange(256)
    k = np.arange(128)
    cmask = np.zeros((NB, 128, 1024), np.float32)
    forceb = np.zeros((NB, 128, 256), np.float32)
    for i in range(NB):
        t = 8 * i + c
        qpos = 128 * t + q
        ok = (16 * n[None, :] + 31 <= qpos[:, None]) & (n[None, :] <= 1022)
        cmask[i] = np.where(ok, 0.0, NEG)
        valid = 64 * j[None, :] <= qpos[:, None]
        qb = qpos[:, None] // 64
        forced = (j[None, :] == 0) | (j[None, :] == qb) | (j[None, :] == qb - 1)
        forceb[i] = np.where(valid, np.where(forced, 1.0e4, 0.0), -1.0e30)
    dmask = np.zeros((128, 8, 128), np.float32)
    for jj in range(8):
        if jj == c:
            dmask[:, jj, :] = np.where(k[None, :] <= q[:, None], 0.0, NEG)
        elif jj > c:
            dmask[:, jj, :] = NEG
    wmask = np.zeros((128, 24, 128), np.float32)
    for v in range(2):
        for m in range(12):
            diff = 128 * (m - 4 - c) + k[None, :] - q[:, None]
            ok = (diff <= 0) & (diff > -512)
            if v == 0 and m < 4:
                ok = np.zeros_like(ok)
            wmask[:, v * 12 + m, :] = np.where(ok, 0.0, NEG)
    return {"cmask": cmask, "forceb": forceb, "dmask": dmask.reshape(128, 1024), "wmask": wmask.reshape(128, 24 * 128)}


def _shared_tables():
    nl = np.arange(128)
    blk = np.arange(256)
    ovp = np.zeros((128, 8, 256), np.float32)
    for jt in range(8):
        n = 128 * jt + nl
        ovp[:, jt, :] = ((4 * blk[None, :] - 1 <= n[:, None]) & (n[:, None] <= 4 * blk[None, :] + 3)).astype(np.float32)
    i = np.arange(128)
    dtab = np.zeros((128, 4, 128), np.float32)
    qdt = np.zeros((128, 4, 128), np.float32)
    for h in range(4):
        diff = i[None, :] - i[:, None]
        dtab[:, h, :] = np.where(diff >= 0, np.exp(np.maximum(diff, 0) * LG[h]), 0.0)
        qdt[:, h, :] = np.exp((i[None, :] + 1.0) * LG[h])
    p = np.arange(128)
    ovs = np.zeros((128, 8, 257), np.float32)
    for s_ in range(8):
        n = 8 * p + s_
        ovs[:, s_, 0:256] = ((4 * blk[None, :] - 1 <= n[:, None]) & (n[:, None] <= 4 * blk[None, :] + 3)).astype(np.float32)
    ovs[:, :, 256] = 1.0
    smallc = np.zeros((1, 1024), np.float32)
    smallc[0, 127] = NEG
    smallc[0, 128] = NEG
    for b in range(4):
        for s_ in range(4):
            smallc[0, 256 + 4 * b + s_] = 0.0 if s_ == b else NEG
    smallc[0, 272:276] = 1.0
    smallc[0, 276 + 0] = 1.0e4
    smallc[0, 276 + 255] = 1.0e4
    ohm = np.zeros((128, 4, 4), np.float32)
    for b in range(4):
        ohm[:, b, b] = 1.0
    return {"ovp": ovp.reshape(128, 2048), "dtab": dtab.reshape(128, 512), "qdt": qdt.reshape(128, 512),
            "ovs": ovs.reshape(128, 8 * 257), "smallc": smallc, "ohm": ohm.reshape(128, 16), "oh4": np.eye(4, dtype=np.float32),
            "rope_smp": _rope_tables(np.full(4, P_PAST))}


def _prep(inputs, stage=99):
    f = lambda a: np.ascontiguousarray(np.asarray(a, dtype=np.float32))
    x_all = f(inputs["x_prompt"]).reshape(T, D)
    w_cmp = f(inputs["w_cmp"])[0]
    b_cmp = f(inputs["b_cmp"])[0]
    wrep = np.zeros((128, 512), np.float32)
    pm = np.arange(128) % 16
    for r in range(2):
        blk = w_cmp[:, :, r * 16 + pm, :]
        wrep[:, r * 256:(r + 1) * 256] = blk.transpose(2, 0, 1, 3).reshape(128, 256)
    bcol = np.ascontiguousarray(b_cmp.reshape(2, 128).T)
    rope_all = _rope_tables(np.arange(T))
    shared = {
        "x_all": x_all, "rope_all": rope_all,
        "w_ada": f(inputs["w_ada"])[0], "b_ada": f(inputs["b_ada"]).reshape(1, 6 * D),
        "w_in": f(inputs["w_in"])[0],
        "norm_mix": f(inputs["norm_mix"]).reshape(1, D), "norm_mlp": f(inputs["norm_mlp"]).reshape(1, D),
        "wrep": wrep, "bcol": bcol,
        "w_bn": f(inputs["w_branch_nsa"])[0], "w_br": f(inputs["w_branch_ret"])[0], "w_out": f(inputs["w_out"])[0],
        "w_up": f(inputs["w_up"])[0], "w_down": f(inputs["w_down"])[0], "norm_final": f(inputs["norm_final"]).reshape(1, D),
        "gnw": f(inputs["ret_gn_w"]).reshape(1, D), "gnb": f(inputs["ret_gn_b"]).reshape(1, D),
    }
    shared.update(_shared_tables())
    shared["wrow"] = np.ascontiguousarray(w_cmp.reshape(2, 2, 2, 16, 64).transpose(2, 3, 0, 1, 4).reshape(1, 2 * 16 * 256))
    shared["brow"] = np.ascontiguousarray(b_cmp.reshape(1, 256))
    shared["cache_cmp"] = f(inputs["cache_cmp_kv"]).reshape(5120 * 8, 4096)
    shared["cache_sel"] = f(inputs["cache_sel_kv"]).reshape(5120 * 8, 4096)
    ptab_all = np.ascontiguousarray(np.asarray(inputs["page_table"], dtype=np.int32))
    cwin = f(inputs["cache_win_kv"])[0].reshape(32, 512, 256)
    sret = f(inputs["state_ret"])[0]
    maps = []
    cp = f(inputs["c_prompt"]).reshape(1, D)
    cs_ = f(inputs["c_sample"])
    for c in range(NCORE):
        m = dict(shared)
        m.update(_consts(c))
        m.update(_core_tables(c))
        tiles = [8 * i + c for i in range(NB)]
        m["x_own"] = np.ascontiguousarray(x_all.reshape(NT, 128, D)[tiles].reshape(NB * 128, D))
        m["rope_own"] = np.ascontiguousarray(rope_all.reshape(NT, 128, 192)[tiles].reshape(NB * 128, 192))
        m["ptab"] = ptab_all[4 * c:4 * c + 4]
        m["cache_win"] = cwin[4 * c:4 * c + 4]
        m["state_s"] = sret[4 * c:4 * c + 4]
        m["x_smp"] = f(inputs["x_sample"]).reshape(32, D)[4 * c:4 * c + 4]
        m["c_all"] = np.ascontiguousarray(np.concatenate([cp, cs_[4 * c:4 * c + 4]], axis=0).reshape(5, 8, 128).transpose(2, 1, 0).reshape(128, 40))
        maps.append(m)
    return maps


_NC_CACHE = {}


def run(inputs, stage=99):
    if stage not in _NC_CACHE:
        _NC_CACHE[stage] = build(stage)
    nc = _NC_CACHE[stage]
    maps = _prep(inputs, stage)
    res = run_bass_kernel_spmd(nc, maps, core_ids=list(range(NCORE)))
    return res.results


def kernel(**inputs):
    r = run(inputs)
    r0 = r[0]
    y = np.zeros((NT, 128, D), np.float32)
    for c in range(NCORE):
        yo = np.asarray(r[c]["y_own"]).reshape(NB, 128, D)
        for i in range(NB):
            y[8 * i + c] = yo[i]
    y_prompt = y.reshape(1, T, D)
    y_sample = np.concatenate([np.asarray(r[c]["y_smp"]) for c in range(NCORE)], axis=0).reshape(32, 1, D)
    new_cmp_p = np.asarray(r0["o_cmp"]).reshape(1, 1, T, 2, 2, 64)
    new_sel_p = np.asarray(r0["o_sel"]).reshape(1, 1, T, 2, 2, 64)
    new_win_p = np.asarray(r0["o_win"]).reshape(1, 1, 512, 2, 2, 64)
    new_state_p = np.asarray(r0["o_state"]).reshape(1, 1, 4, 128, 256)
    cat = lambda k: np.concatenate([np.asarray(r[c][k]) for c in range(NCORE)], axis=0)
    new_cmp_s = cat("o_cmp_s").reshape(1, 32, 1, 2, 2, 64)
    new_sel_s = cat("o_sel_s").reshape(1, 32, 1, 2, 2, 64)
    new_win_s = cat("o_win_s").reshape(1, 32, 512, 2, 2, 64)
    new_state_s = cat("o_state_s").reshape(1, 32, 4, 128, 256)
    return (y_prompt, y_sample, new_cmp_p, new_sel_p, new_win_p, new_state_p, new_cmp_s, new_sel_s, new_win_s, new_state_s)
```

```python
import math
import os
from contextlib import ExitStack

import numpy as np
import ml_dtypes

import concourse.bass as bass
import concourse.mybir as mybir
from concourse.bass_utils import run_bass_kernel_spmd

F32 = mybir.dt.float32
BF16 = mybir.dt.bfloat16
I32 = mybir.dt.int32
ALU = mybir.AluOpType
ACTF = mybir.ActivationFunctionType
AX = mybir.AxisListType

ENGS = ("pe", "act", "dve", "pool", "sp")
NSLOT = 12

D = 1024
T = 16384
NT = T // 128
NB = 16
NCORE = 8
SB = 4
P_PAST = 16384
W_IN = 6424
C_Q, C_KVC, C_KVS, C_KVW, C_G, C_RQ, C_RK, C_RV, C_RG, C_GA, C_GB = (
    0, 512, 768, 1024, 1280, 1304, 1816, 2328, 3352, 4376, 5400)
NEG = -30000.0
RMS_EPS = 1e-6
GN_EPS = 1e-5
LG = [math.log1p(-2.0 ** (-5.0 - h)) for h in range(4)]
RET_SCALE = 128 ** -0.5


class Dep:
    __slots__ = ("w", "r", "excl")

    def __init__(self, excl=False):
        self.w = None
        self.r = []
        self.excl = excl


class Prog:
    def __init__(self, nc, stack):
        self.nc = nc
        self.stack = stack
        self.ops = {e: [] for e in ENGS}
        self.cnt = {e: 0 for e in ENGS}
        self.esem = {e: stack.enter_context(nc.semaphore("es_" + e)) for e in ENGS}
        self.dq = {}
        for q in ("sp", "act", "pool"):
            sl = [stack.enter_context(nc.semaphore("ds_%s_%d" % (q, i))) for i in range(NSLOT)]
            self.dq[q] = {"sems": sl, "cnt": [0] * NSLOT, "next": 0}
        self.waited = {e: {} for e in ENGS}
        self.nuid = 0

    def sb(self, shape, dt=F32, stack=None):
        self.nuid += 1
        return (stack or self.stack).enter_context(self.nc.sbuf_tensor("sb%d" % self.nuid, list(shape), dt))

    def ps(self, shape, dt=F32, stack=None):
        self.nuid += 1
        return (stack or self.stack).enter_context(self.nc.psum_tensor("ps%d" % self.nuid, list(shape), dt))

    def _need(self, eng, tok, waits):
        if tok is None:
            return
        kind, key, val = tok
        if kind == "e" and key == eng and eng == "pe":
            return
        k = (kind, key)
        if self.waited[eng].get(k, 0) >= val:
            return
        self.waited[eng][k] = val
        waits[k] = max(waits.get(k, 0), val)

    def _gather(self, eng, reads, writes):
        waits = {}
        for d in reads:
            self._need(eng, d.w, waits)
            if d.excl:
                for t in d.r:
                    if not (t[0] == "e" and t[1] == eng):
                        self._need(eng, t, waits)
        for d in writes:
            self._need(eng, d.w, waits)
            for t in d.r:
                self._need(eng, t, waits)
        return waits

    def _commit(self, tok, reads, writes):
        for d in reads:
            if tok[0] == "e":
                d.r = [t for t in d.r if not (t[0] == "e" and t[1] == tok[1])]
            d.r.append(tok)
        for d in writes:
            d.w = tok
            d.r = []

    def _semof(self, k):
        kind, key = k
        if kind == "e":
            return self.esem[key]
        q, i = key
        return self.dq[q]["sems"][i]

    def op(self, eng, fn, reads=(), writes=()):
        waits = self._gather(eng, reads, writes)
        self.cnt[eng] += 1
        tok = ("e", eng, self.cnt[eng])
        self.ops[eng].append(([(self._semof(k), v) for k, v in waits.items()], fn, (self.esem[eng], 1)))
        self._commit(tok, reads, writes)
        return tok

    def dma(self, q, fn, reads=(), writes=()):
        st = self.dq[q]
        i = st["next"]
        st["next"] = (i + 1) % NSLOT
        waits = self._gather(q, reads, writes)
        if st["cnt"][i] > 0:
            self._need(q, ("d", (q, i), st["cnt"][i]), waits)
        st["cnt"][i] += 16
        tok = ("d", (q, i), st["cnt"][i])
        self.ops[q].append(([(self._semof(k), v) for k, v in waits.items()], fn, (st["sems"][i], 16)))
        self._commit(tok, reads, writes)
        return tok

    def barrier(self):
        toks = [("e", e, self.cnt[e]) for e in ENGS if self.cnt[e] > 0]
        for q, st in self.dq.items():
            for i in range(NSLOT):
                if st["cnt"][i] > 0:
                    toks.append(("d", (q, i), st["cnt"][i]))
        for eng in ENGS:
            waits = {}
            for t in toks:
                if t[0] == "e" and t[1] == eng:
                    continue
                self._need(eng, t, waits)
            if waits:
                self.ops[eng].append(([(self._semof(k), v) for k, v in waits.items()], None, None))

    def emit(self):
        nc = self.nc
        ops = self.ops
        with nc.Block() as block:
            def run(handle, lst):
                for waits, fn, inc in lst:
                    for sem, v in waits:
                        handle.wait_ge(sem, v)
                    if fn is not None:
                        fn(handle).then_inc(inc[0], inc[1])

            @block.tensor
            def _(e):
                run(e, ops["pe"])

            @block.scalar
            def _(e):
                run(e, ops["act"])

            @block.vector
            def _(e):
                run(e, ops["dve"])

            @block.gpsimd
            def _(e):
                run(e, ops["pool"])

            @block.sync
            def _(e):
                run(e, ops["sp"])


def bc(ap, shape):
    return ap.to_broadcast(list(shape))


def build(stage=99):
    nc = bass.Bass("TRN2", target_bir_lowering=False)

    def din(name, shape, dt=F32):
        return nc.dram_tensor(name, list(shape), dt, kind="ExternalInput").ap()

    def dout(name, shape, dt=F32):
        return nc.dram_tensor(name, list(shape), dt, kind="ExternalOutput").ap()

    def dscr(name, shape, dt=F32):
        return nc.dram_tensor(name, list(shape), dt, kind="Internal").ap()

    x_all = din("x_all", [T, D])
    c_all = din("c_all", [128, 40])
    rope_all = din("rope_all", [T, 192])
    w_ada = din("w_ada", [D, 6 * D])
    b_ada = din("b_ada", [1, 6 * D])
    w_in = din("w_in", [D, W_IN])
    norm_mix = din("norm_mix", [1, D])
    norm_mlp = din("norm_mlp", [1, D])
    ident_d = din("ident", [128, 128])
    sel5_d = din("sel5", [5, 132])
    inda_d = din("inda", [128, 8])
    wrep_d = din("wrep", [128, 512])
    bcol_d = din("bcol", [128, 2])
    kdsc_d = din("kdsc", [128, 4])
    oh_d = din("oh", [128, 8])

    x_own = din("x_own", [NB * 128, D])
    rope_own = din("rope_own", [NB * 128, 192])
    cmask_d = din("cmask", [NB, 128, 1024])
    forceb_d = din("forceb", [NB, 128, 256])
    dmask_d = din("dmask", [128, 8 * 128])
    wmask_d = din("wmask", [128, 24 * 128])
    ovp_d = din("ovp", [128, 8 * 256])
    dtab_d = din("dtab", [128, 512])
    qdt_d = din("qdt", [128, 512])
    gnw_d = din("gnw", [1, D])
    gnb_d = din("gnb", [1, D])
    DBG = (stage == 2)
    DBG_NBLK = int(os.environ.get('K_NBLK', '3'))
    PART = int(os.environ.get('K_PART', '9'))
    SUB = int(os.environ.get('K_SUB', '9'))
    SUB2 = int(os.environ.get('K_SUB2', '9'))
    hT_d = dscr("hT_d", [NB, 128, 8, 128], BF16)
    yrT_d = dscr("yrT_d", [NB, 128, 8, 128], BF16)
    onT_d = dscr("onT_d", [NB, 128, 4, 128], BF16)
    if DBG:
        dbg_selb = dout("dbg_selb", [NB, 2, 128, 256])
        dbg_on = dout("dbg_on", [NB, 128, 512])
        dbg_yr = dout("dbg_yr", [NB, 128, 1024])

    w_bn = din("w_bn", [512, D])
    w_br = din("w_br", [D, D])
    w_out = din("w_out", [D, D])
    w_up = din("w_up", [D, 4 * D])
    w_down = din("w_down", [4 * D, D])
    norm_final_d = din("norm_final", [1, D])
    x_smp = din("x_smp", [SB, D])
    x1_d = dscr("x1_d", [NB * 128, D])
    x1s_d = dscr("x1s_d", [SB, D])
    hTs_d = dscr("hTs_d", [128, 8, SB], BF16)
    onTs_d = dscr("onTs_d", [128, 4, SB], BF16)
    yrTs_d = dscr("yrTs_d", [128, 8, SB], BF16)
    y_own = dout("y_own", [NB * 128, D])
    y_smp = dout("y_smp", [SB, D])

    smallc_d = din("smallc", [1, 1024])
    oh4_d = din("oh4", [SB, 4])
    ohm_d = din("ohm", [128, 16])
    ovs_d = din("ovs", [128, 8 * 257])
    wrow_d = din("wrow", [1, 2 * 16 * 256])
    brow_d = din("brow", [1, 256])
    rope_smp = din("rope_smp", [SB, 192])
    ptab = din("ptab", [SB, 128], I32)
    cache_cmp = din("cache_cmp", [5120 * 8, 4096])
    cache_sel = din("cache_sel", [5120 * 8, 4096])
    cache_win = din("cache_win", [SB, 512, 256])
    state_s = din("state_s", [SB, 4, 128, 256])
    os_d = dscr("os_d", [SB, 3, 8, 64])
    dbg_ons = dout("dbg_ons", [SB, 512])
    o_cmp_s = dout("o_cmp_s", [SB, 256])
    o_sel_s = dout("o_sel_s", [SB, 256])
    o_win_s = dout("o_win_s", [SB, 512, 256])
    o_state_s = dout("o_state_s", [SB, 4, 128, 256])

    o_cmp = dout("o_cmp", [T, 256])
    o_sel = dout("o_sel", [T, 256])
    o_win = dout("o_win", [512, 256])
    o_state = dout("o_state", [4, 128, 256])

    kwT_d = dscr("kwT_d", [128, T], BF16)
    vw_d = dscr("vw_d", [T, 130], BF16)
    ksT_d = dscr("ksT_d", [128, T], BF16)
    vs_d = dscr("vs_d", [T, 130], BF16)
    sown_d = dscr("sown_d", [NB, 128, 1024])
    mods_d = dscr("mods_d", [SB, 6 * D])

    with ExitStack() as st:
        P = Prog(nc, st)
        ident = P.sb([128, 128]); d_ident = Dep()
        identb = P.sb([128, 128], BF16); d_identb = Dep()
        modP4 = P.sb([128, 4, D]); d_modP = Dep()
        G1P = P.sb([128, D]); d_G1P = Dep()
        G2P = P.sb([128, D]); d_G2P = Dep()
        cmpKT = P.sb([128, 1024], BF16); d_cmpKT = Dep()
        cmpVA = P.sb([128, 8, 2, 65], BF16); d_cmpVA = Dep()
        epsT = P.sb([128, 1]); d_eps = Dep()

        banks = [P.ps([128, 512]) for _ in range(6)]
        pT = P.ps([128, 1024]); d_pTs = [Dep(True), Dep(True)]
        d_bank = [Dep(True) for _ in range(6)]

        P.dma("sp", lambda e: e.dma_start(out=ident[:], in_=ident_d), writes=[d_ident])
        P.op("dve", lambda e: e.tensor_copy(out=identb[:], in_=ident[:]), reads=[d_ident], writes=[d_identb])
        P.op("dve", lambda e: e.memset(epsT[:], RMS_EPS), writes=[d_eps])
        P.op("pool", lambda e: e.memset(cmpVA[:].rearrange("p a b c -> p (a b c)"), 1.0), writes=[d_cmpVA])

        with ExitStack() as s0:
            cT = P.sb([128, 8, 5], F32, s0); d_cT = Dep()
            scT = P.sb([128, 8, 5], BF16, s0); d_scT = Dep()
            sel5 = P.sb([5, 132], F32, s0); d_sel5 = Dep()
            mod5 = P.sb([5, 6 * D], F32, s0); d_mod5 = Dep()
            bada = P.sb([5, 6 * D], F32, s0); d_bada = Dep()
            gmix = P.sb([128, D], F32, s0); d_gmix = Dep()
            gmlp = P.sb([128, D], F32, s0); d_gmlp = Dep()
            wa = [P.sb([128, 8, 512], BF16, s0) for _ in range(2)]; d_wa = [Dep(), Dep()]

            P.dma("sp", lambda e: e.dma_start(out=cT[:].rearrange("p c b -> p (c b)"), in_=c_all), writes=[d_cT])
            P.dma("sp", lambda e: e.dma_start(out=sel5[:], in_=sel5_d), writes=[d_sel5])
            P.dma("sp", lambda e: e.dma_start(out=bada[:], in_=b_ada.partition_broadcast(5)[:, 0, :]), writes=[d_bada])
            P.dma("sp", lambda e: e.dma_start(out=gmix[:], in_=norm_mix.partition_broadcast(128)[:, 0, :]), writes=[d_gmix])
            P.dma("sp", lambda e: e.dma_start(out=gmlp[:], in_=norm_mlp.partition_broadcast(128)[:, 0, :]), writes=[d_gmlp])
            P.op("act", lambda e: e.activation(out=scT[:], in_=cT[:], func=ACTF.Silu), reads=[d_cT], writes=[d_scT])
            for g in range(12):
                b = g % 2
                P.dma("pool", lambda e, g=g, b=b: e.dma_start(
                    out=wa[b][:], in_=w_ada[:, g * 512:(g + 1) * 512].rearrange("(c p) n -> p c n", p=128)),
                    writes=[d_wa[b]])
                for k in range(8):
                    P.op("pe", lambda e, k=k, b=b: e.matmul(out=banks[0][0:5, :], lhsT=scT[:, k, :], rhs=wa[b][:, k, :],
                                                           start=(k == 0), stop=(k == 7)),
                         reads=[d_scT, d_wa[b]], writes=[d_bank[0]])
                P.op("dve", lambda e, g=g: e.tensor_tensor(out=mod5[:, g * 512:(g + 1) * 512], in0=banks[0][0:5, :],
                                                          in1=bada[:, g * 512:(g + 1) * 512], op=ALU.add),
                     reads=[d_bank[0], d_bada], writes=[d_mod5])
            P.dma("sp", lambda e: e.dma_start(out=mods_d, in_=mod5[1:5, :]), reads=[d_mod5])
            slot = {0: 0, 2: 1, 3: 2, 5: 3}
            for g in range(12):
                ch, hf_ = g // 2, (g % 2) * 512
                P.op("pe", lambda e, g=g: e.matmul(out=banks[1][:, :], lhsT=sel5[:, 0:128], rhs=mod5[:, g * 512:(g + 1) * 512],
                                                   start=True, stop=True), reads=[d_sel5, d_mod5], writes=[d_bank[1]])
                if ch in slot:
                    P.op("act", lambda e, ch=ch, hf_=hf_: e.copy(out=modP4[:, slot[ch], hf_:hf_ + 512], in_=banks[1][:, :]),
                         reads=[d_bank[1]], writes=[d_modP])
                else:
                    Gt, dG, gn, dgn = (G1P, d_G1P, gmix, d_gmix) if ch == 1 else (G2P, d_G2P, gmlp, d_gmlp)
                    P.op("dve", lambda e, Gt=Gt, gn=gn, hf_=hf_: e.scalar_tensor_tensor(out=Gt[:, hf_:hf_ + 512], in0=banks[1][:, :], scalar=1.0,
                                                                                         in1=gn[:, hf_:hf_ + 512], op0=ALU.add, op1=ALU.mult),
                         reads=[d_bank[1], dgn], writes=[dG])
            P.barrier()

        shcol = P.sb([128, 2, 8], F32); d_shcol = Dep()
        for si, mi in enumerate((0, 2)):
            for c in range(8):
                P.op("pe", lambda e, c=c, mi=mi: e.transpose(out=pT[:, c * 128:(c + 1) * 128], in_=modP4[:, mi, c * 128:(c + 1) * 128], identity=ident[:]),
                     reads=[d_modP, d_ident], writes=[d_pTs[c // 4]])
            P.op("act", lambda e, si=si: e.copy(out=shcol[:, si, :], in_=pT[:].rearrange("p (c t) -> p c t", t=128)[:, :, 0]), reads=d_pTs, writes=[d_shcol])

        def norm_mod_T(xt, d_xt, ntok, G, d_G, shift, d_shift, wk, hT, d_hT, shift_col=None, d_shift_col=None):
            junk, d_junk, ss, d_ss, rstd, d_rstd, hm, d_hm, hf, d_hf = wk
            P.op("act", lambda e: e.activation(out=junk[0:ntok, :], in_=xt[0:ntok, :], func=ACTF.Square, accum_out=ss[0:ntok, :]),
                 reads=[d_xt], writes=[d_junk, d_ss])
            P.op("act", lambda e: e.activation(out=rstd[0:ntok, :], in_=ss[0:ntok, :], func=ACTF.Sqrt, bias=epsT[0:ntok, :], scale=1.0 / D),
                 reads=[d_ss, d_eps], writes=[d_rstd])
            P.op("dve", lambda e: e.reciprocal(out=rstd[0:ntok, :], in_=rstd[0:ntok, :]), reads=[d_rstd], writes=[d_rstd])
            P.op("dve", lambda e: e.scalar_tensor_tensor(out=hm[0:ntok, :], in0=xt[0:ntok, :], scalar=rstd[0:ntok, 0:1], in1=G[0:ntok, :],
                                                        op0=ALU.mult, op1=ALU.mult),
                 reads=[d_xt, d_rstd, d_G], writes=[d_hm])
            if shift_col is None:
                P.op("pool", lambda e: e.tensor_tensor(out=hf[0:ntok, :], in0=hm[0:ntok, :], in1=shift[0:ntok, :], op=ALU.add),
                     reads=[d_hm, d_shift], writes=[d_hf])
            for c in range(8):
                P.op("pe", lambda e, c=c: e.transpose(out=pT[:, c * 128:c * 128 + ntok], in_=hf[0:ntok, c * 128:(c + 1) * 128],
                                                      identity=ident[0:ntok, 0:ntok]),
                     reads=[d_hf, d_ident], writes=[d_pTs[c // 4]])
            pv = pT[:].rearrange("p (c t) -> p c t", t=128)
            if shift_col is None:
                P.op("act", lambda e: e.copy(out=hT[:, 0:4, 0:ntok], in_=pv[:, 0:4, 0:ntok]), reads=[d_pTs[0]], writes=[d_hT])
                P.op("dve", lambda e: e.tensor_copy(out=hT[:, 4:8, 0:ntok], in_=pv[:, 4:8, 0:ntok]), reads=[d_pTs[1]], writes=[d_hT])
            else:
                for c in range(8):
                    P.op("act", lambda e, c=c: e.activation(out=hT[:, c, 0:ntok], in_=pv[:, c, 0:ntok], func=ACTF.Identity, bias=shift_col[:, c:c + 1]),
                         reads=[d_pTs[c // 4], d_shift_col], writes=[d_hT])

        junk_sh = P.sb([128, D], BF16); d_junk_sh = Dep()

        def mk_wk(stack, n=128):
            hm_ = P.sb([n, D], F32, stack); d_hm_ = Dep()
            return (junk_sh, d_junk_sh, P.sb([n, 1], F32, stack), Dep(), P.sb([n, 1], F32, stack), Dep(),
                    hm_, d_hm_, hm_, d_hm_)

        def rope(src4, dst4, cos, sin, shp, tmps, d_src, d_dst, d_rp, d_tmps):
            ta, tb, tc, td = tmps
            cb, sbb = cos, sin
            P.op("dve", lambda e: e.tensor_tensor(out=ta, in0=src4(0), in1=cb, op=ALU.mult), reads=[d_src, d_rp], writes=[d_tmps[0]])
            P.op("dve", lambda e: e.tensor_tensor(out=tb, in0=src4(1), in1=sbb, op=ALU.mult), reads=[d_src, d_rp], writes=[d_tmps[1]])
            P.op("dve", lambda e: e.tensor_tensor(out=tc, in0=src4(0), in1=sbb, op=ALU.mult), reads=[d_src, d_rp], writes=[d_tmps[2]])
            P.op("dve", lambda e: e.tensor_tensor(out=td, in0=src4(1), in1=cb, op=ALU.mult), reads=[d_src, d_rp], writes=[d_tmps[3]])
            P.op("pool", lambda e: e.tensor_tensor(out=dst4(0), in0=ta, in1=tb, op=ALU.subtract), reads=[d_tmps[0], d_tmps[1]], writes=[d_dst])
            P.op("pool", lambda e: e.tensor_tensor(out=dst4(1), in0=tc, in1=td, op=ALU.add), reads=[d_tmps[2], d_tmps[3]], writes=[d_dst])

        def make_attn(pts, d_pts):
            scb = [banks[0], banks[1]]
            d_scb = [d_bank[0], d_bank[1]]
            slot = [0]

            def attn(nq, ng, QTh, d_Q, tiles, o4v, d_o4, pslc=None, first=True, last=True, pvT=False):
                ncols = nq * ng
                nt_ = len(tiles)
                for ti, tl in enumerate(tiles):
                    s_ = slot[0]; slot[0] ^= 1
                    nk = tl["nk"]
                    sc, d_sc, pt, d_pt = scb[s_], d_scb[s_], pts[s_], d_pts[s_]
                    nb = len(tl["biases"])
                    P.op("pe", lambda e, sc=sc, tl=tl, nk=nk, nb=nb: e.matmul(out=sc[0:nk, 0:ncols], lhsT=tl["KT"], rhs=QTh,
                                                                           start=True, stop=(nb == 0)),
                         reads=[d_Q] + tl["deps"], writes=[d_sc])
                    for bi, (bl, br_) in enumerate(tl["biases"]):
                        P.op("pe", lambda e, sc=sc, bl=bl, br_=br_, nk=nk, bi=bi, nb=nb: e.matmul(
                            out=sc[0:nk, 0:ncols], lhsT=bl, rhs=br_, start=False, stop=(bi == nb - 1)),
                            reads=tl["deps"], writes=[d_sc])
                    P.op("act", lambda e, sc=sc, pt=pt, nk=nk: e.activation(out=pt[0:nk, 0:ncols], in_=sc[0:nk, 0:ncols], func=ACTF.Exp),
                         reads=[d_sc], writes=[d_pt])
                    if pvT:
                        P.op("pe", lambda e, pt=pt, tl=tl, nk=nk, ti=ti: e.matmul(
                            out=o4v, lhsT=tl["V"], rhs=pt[0:nk, 0:ncols], start=(first and ti == 0), stop=(last and ti == nt_ - 1)),
                            reads=[d_pt] + tl["deps"], writes=[d_o4])
                    else:
                        for g in range(ng):
                            P.op("pe", lambda e, pt=pt, tl=tl, nk=nk, g=g, ti=ti: e.matmul(
                                out=o4v[:, g, :], lhsT=pt[0:nk, g * nq:(g + 1) * nq], rhs=tl["V"],
                                start=(first and ti == 0 and g == 0), stop=(last and ti == nt_ - 1), skip_group_check=True),
                                reads=[d_pt] + tl["deps"], writes=[d_o4])
                    if pslc is not None:
                        psv, d_ps, gpb = pslc
                        for g in range(ng):
                            P.op("pe", lambda e, pt=pt, tl=tl, nk=nk, g=g, ti=ti: e.matmul(
                                out=psv(g), lhsT=pt[0:nk, g * nq:(g + 1) * nq], rhs=tl["Ov"],
                                start=(ti == 0 and g % gpb == 0), stop=(ti == nt_ - 1), skip_group_check=True),
                                reads=[d_pt] + tl["deps"], writes=[d_ps[g // gpb]])


            return attn

        def phase_b1():
            with ExitStack() as sB:
                WB = P.sb([128, 8, 4376], BF16, sB); d_WB = Dep()
                wsegs = []
                for g in range(4):
                    wsegs += [(C_Q + g * 64, 64, g * 128), (C_Q + (4 + g) * 64, 64, g * 128 + 64)]
                wsegs += [(C_RQ, 512, 512), (C_RK, 512, 1024), (C_RV, 1024, 1536), (C_RG, 1024, 2560), (C_G, 24, 3584),
                          (C_KVC, 128, 3608), (C_KVS, 128, 3736), (C_KVW, 128, 3864),
                          (C_KVC + 128, 128, 3992), (C_KVS + 128, 128, 4120), (C_KVW + 128, 128, 4248)]
                for (src, n, dst) in wsegs:
                    P.dma("pool", lambda e, src=src, n=n, dst=dst: e.dma_start(
                        out=WB[:, :, dst:dst + n], in_=w_in[:, src:src + n].rearrange("(c p) n -> p c n", p=128)), writes=[d_WB])
                dmaskB = P.sb([128, 8, 128], BF16, sB); d_dmask = Dep()
                wmaskB = P.sb([128, 24, 128], BF16, sB); d_wmask = Dep()
                OvP = P.sb([128, 8, 256], BF16, sB); d_OvP = Dep()
                dtab = P.sb([128, 4, 128], F32, sB); d_dtab = Dep()
                qdt = P.sb([128, 4, 128], F32, sB); d_qdt = Dep()
                gnw = P.sb([128, D], F32, sB); d_gnw = Dep()
                gnb = P.sb([128, D], F32, sB); d_gnb = Dep()
                Irep = P.sb([128, 4, 128], BF16, sB); d_Irep = Dep()
                epsG = P.sb([128, 1], F32, sB); d_epsG = Dep()
                P.dma("pool", lambda e: e.dma_start(out=dmaskB[:].rearrange("p a b -> p (a b)"), in_=dmask_d), writes=[d_dmask])
                P.dma("pool", lambda e: e.dma_start(out=wmaskB[:].rearrange("p a b -> p (a b)"), in_=wmask_d), writes=[d_wmask])
                P.dma("pool", lambda e: e.dma_start(out=OvP[:].rearrange("p a b -> p (a b)"), in_=ovp_d), writes=[d_OvP])
                P.dma("sp", lambda e: e.dma_start(out=dtab[:].rearrange("p a b -> p (a b)"), in_=dtab_d), writes=[d_dtab])
                P.dma("sp", lambda e: e.dma_start(out=qdt[:].rearrange("p a b -> p (a b)"), in_=qdt_d), writes=[d_qdt])
                P.dma("sp", lambda e: e.dma_start(out=gnw[:], in_=gnw_d.partition_broadcast(128)[:, 0, :]), writes=[d_gnw])
                P.dma("sp", lambda e: e.dma_start(out=gnb[:], in_=gnb_d.partition_broadcast(128)[:, 0, :]), writes=[d_gnb])
                for g in range(4):
                    P.op("dve", lambda e, g=g: e.tensor_copy(out=Irep[:, g, :], in_=ident[:]), reads=[d_ident], writes=[d_Irep])
                P.op("dve", lambda e: e.memset(epsG[:], GN_EPS), writes=[d_epsG])
                IrepF = Irep[:].rearrange("p a b -> p (a b)")

                xt = P.sb([128, D], F32, sB); d_xt = Dep()
                rp = P.sb([128, 192], F32, sB); d_rp = Dep()
                wk = mk_wk(sB)
                hT = P.sb([128, 8, 128], BF16, sB); d_hT = Dep()
                tmq = [P.sb([128, 256], F32, sB) for _ in range(4)]; d_tmq = [Dep() for _ in range(4)]
                qr = P.sb([128, 512], F32, sB); d_qr = Dep()
                rqr = P.sb([128, 4, 128], F32, sB); d_rqr = Dep()
                rkr = P.sb([128, 4, 128], F32, sB); d_rkr = Dep()
                sg = P.sb([128, 24], F32, sB); d_sg = Dep()
                QT = P.sb([128, 4, 128], BF16, sB); d_QT = Dep()
                rqT = P.sb([128, 4, 128], BF16, sB); d_rqT = Dep()
                rqdT = P.sb([128, 4, 128], BF16, sB); d_rqdT = Dep()
                rkT = P.sb([128, 4, 128], BF16, sB); d_rkT = Dep()
                vb = P.sb([128, 1024], BF16, sB); d_vb = Dep()
                sgate = P.sb([128, 1024], F32, sB); d_sgate = Dep()
                attD = P.sb([128, 4, 128], BF16, sB); d_attD = Dep()
                SownB = P.sb([128, 1024], BF16, sB); d_SownB = Dep()
                osb = P.sb([128, 4, 256], F32, sB); d_osb = Dep()
                st4 = P.sb([128, 16], F32, sB); d_st4 = Dep()
                yret = P.sb([128, 1024], F32, sB); d_yret = Dep()
                yrT = P.sb([128, 8, 128], BF16, sB); d_yrT = Dep()
                onsa = P.sb([128, 8, 64], F32, sB); d_onsa = Dep()
                tmpo = P.sb([128, 4, 64], F32, sB); d_tmpo = Dep()
                onT = P.sb([128, 4, 128], BF16, sB); d_onT = Dep()
                rden = P.sb([128, 8], F32, sB); d_rden = Dep()
                wgt = P.sb([128, 4], F32, sB); d_wgt = Dep()
                cmaskB = P.sb([128, 1024], BF16, sB); d_cmask = Dep()
                forceb = P.sb([128, 256], F32, sB); d_forceb = Dep()
                pacc = P.sb([128, 256], F32, sB); d_pacc = Dep()
                wk2 = P.sb([128, 256], F32, sB); d_wk2 = Dep()
                m16 = P.sb([128, 16], F32, sB); d_m16 = Dep()
                selb = [P.sb([128, 256], F32, sB) for _ in range(2)]; d_selb = [Dep(), Dep()]
                selbX = [[P.sb([128, 1024], BF16, sB) for _ in range(2)] for _ in range(2)]
                d_selbX = [[Dep(), Dep()], [Dep(), Dep()]]
                KsG = [P.sb([128, 1024], BF16, sB) for _ in range(2)]; d_KsG = [Dep(), Dep()]
                VsG = [P.sb([128, 8, 130], BF16, sB) for _ in range(2)]; d_VsG = [Dep(), Dep()]
                KwWin = P.sb([128, 1536], BF16, sB); d_KwWin = Dep()
                VwWin = P.sb([128, 12, 130], BF16, sB); d_VwWin = Dep()
                pts = [P.sb([128, 512], BF16, sB) for _ in range(2)]; d_pts = [Dep(), Dep()]
                P.op("pool", lambda e: e.memset(KwWin[:], 0.0), writes=[d_KwWin])
                P.op("pool", lambda e: e.memset(VwWin[:].rearrange("p a b -> p (a b)"), 0.0), writes=[d_VwWin])
                attn = make_attn(pts, d_pts)

                o4 = [banks[2][0:65, :], banks[3][0:65, :]]
                d_o4 = [d_bank[2], d_bank[3]]
                oTs = [P.sb([65, 512], F32, sB) for _ in range(2)]; d_oTs = [Dep(), Dep()]
                sgv = sg[:].rearrange("p (h b) -> p h b", b=3)

                def evac(kvh, br_i, first):
                    P.op("act", lambda e: e.copy(out=oTs[kvh][:], in_=o4[kvh]), reads=[d_o4[kvh]], writes=[d_oTs[kvh]])
                    for g in range(4):
                        P.op("pe", lambda e, g=g: e.transpose(out=pT[:, kvh * 512 + g * 65:kvh * 512 + (g + 1) * 65], in_=oTs[kvh][0:65, g * 128:(g + 1) * 128],
                                                              identity=ident[0:65, 0:65]), reads=[d_oTs[kvh], d_ident], writes=[d_pTs[kvh]])
                    o4v = pT[:, kvh * 512:kvh * 512 + 260].rearrange("p (g c) -> p g c", g=4)
                    d_o = d_pTs[kvh]
                    P.op("dve", lambda e: e.tensor_scalar(out=rden[:, kvh * 4:(kvh + 1) * 4], in0=o4v[:, :, 64], scalar1=1e-30, scalar2=None, op0=ALU.max),
                         reads=[d_o], writes=[d_rden])
                    P.op("dve", lambda e: e.reciprocal(out=rden[:, kvh * 4:(kvh + 1) * 4], in_=rden[:, kvh * 4:(kvh + 1) * 4]), reads=[d_rden], writes=[d_rden])
                    P.op("dve", lambda e: e.tensor_tensor(out=wgt[:], in0=rden[:, kvh * 4:(kvh + 1) * 4], in1=sgv[:, kvh * 4:(kvh + 1) * 4, br_i], op=ALU.mult),
                         reads=[d_rden, d_sg], writes=[d_wgt])
                    wb_ = wgt[:].unsqueeze(2).to_broadcast([128, 4, 64])
                    if first:
                        P.op("dve", lambda e: e.tensor_tensor(out=onsa[:, kvh * 4:(kvh + 1) * 4, :], in0=o4v[:, :, 0:64], in1=wb_, op=ALU.mult),
                             reads=[d_o, d_wgt], writes=[d_onsa])
                    else:
                        P.op("dve", lambda e: e.tensor_tensor(out=tmpo[:], in0=o4v[:, :, 0:64], in1=wb_, op=ALU.mult),
                             reads=[d_o, d_wgt], writes=[d_tmpo])
                        P.op("pool", lambda e: e.tensor_tensor(out=onsa[:, kvh * 4:(kvh + 1) * 4, :], in0=onsa[:, kvh * 4:(kvh + 1) * 4, :], in1=tmpo[:], op=ALU.add),
                             reads=[d_tmpo, d_onsa], writes=[d_onsa])

                nblk = NB if stage >= 2 else 0
                if stage == 2:
                    nblk = DBG_NBLK
                for i in range(nblk):
                    P.dma("sp", lambda e, i=i: e.dma_start(out=xt[:], in_=x_own[i * 128:(i + 1) * 128, :]), writes=[d_xt])
                    P.dma("sp", lambda e, i=i: e.dma_start(out=rp[:], in_=rope_own[i * 128:(i + 1) * 128, :]), writes=[d_rp])
                    P.dma("pool", lambda e, i=i: e.dma_start(out=cmaskB[:], in_=cmask_d[i]), writes=[d_cmask])
                    P.dma("sp", lambda e, i=i: e.dma_start(out=forceb[:], in_=forceb_d[i]), writes=[d_forceb])
                    P.dma("pool", lambda e, i=i: e.dma_start(out=SownB[:], in_=sown_d[i]), writes=[d_SownB])
                    if i == 0:
                        P.dma("sp", lambda e: e.dma_start(out=KwWin[:, 512:1536], in_=kwT_d[:, 0:1024]), writes=[d_KwWin])
                        P.dma("sp", lambda e: e.dma_start(out=VwWin[:, 4:12, :], in_=vw_d[0:1024, :].rearrange("(m p) f -> p m f", p=128)), writes=[d_VwWin])
                    else:
                        P.dma("sp", lambda e, i=i: e.dma_start(out=KwWin[:], in_=kwT_d[:, (8 * i - 4) * 128:(8 * i + 8) * 128]), writes=[d_KwWin])
                        P.dma("sp", lambda e, i=i: e.dma_start(out=VwWin[:], in_=vw_d[(8 * i - 4) * 128:(8 * i + 8) * 128, :].rearrange("(m p) f -> p m f", p=128)),
                              writes=[d_VwWin])
                    norm_mod_T(xt, d_xt, 128, G1P, d_G1P, modP4[:, 0, :], d_modP, wk, hT, d_hT, shcol[:, 0, :], d_shcol)
                    P.dma("sp", lambda e, i=i: e.dma_start(out=hT_d[i], in_=hT[:]), reads=[d_hT])
                    if SUB < 1:
                        continue
                    for (bk, c0, n) in ((0, 0, 512), (1, 512, 512), (4, 1024, 512), (5, 3584, 24)):
                        for k in range(8):
                            P.op("pe", lambda e, bk=bk, k=k, c0=c0, n=n: e.matmul(out=banks[bk][:, 0:n], lhsT=hT[:, k, :], rhs=WB[:, k, c0:c0 + n],
                                                                                start=(k == 0), stop=(k == 7)),
                                 reads=[d_hT, d_WB], writes=[d_bank[bk]])
                    P.op("act", lambda e: e.activation(out=sg[:], in_=banks[5][:, 0:24], func=ACTF.Sigmoid), reads=[d_bank[5]], writes=[d_sg])
                    if SUB < 2:
                        continue
                    srcq = banks[0][:, 0:512].rearrange("p (h two e) -> p h two e", h=8, two=2)
                    dstq = qr[:].rearrange("p (h two e) -> p h two e", h=8, two=2)
                    shq = [128, 8, 32]
                    tvq = [t[:, 0:256].rearrange("p (h e) -> p h e", h=8) for t in tmq]
                    rope(lambda hf: srcq[:, :, hf, :], lambda hf: dstq[:, :, hf, :], rp[:, 0:32].unsqueeze(1).to_broadcast(shq),
                         rp[:, 32:64].unsqueeze(1).to_broadcast(shq), shq, tvq, d_bank[0], d_qr, d_rp, d_tmq)
                    shk = [128, 4, 64]
                    tvk = [t[:, 0:256].rearrange("p (h e) -> p h e", h=4) for t in tmq]
                    cosk = rp[:, 64:128].unsqueeze(1).to_broadcast(shk)
                    sink = rp[:, 128:192].unsqueeze(1).to_broadcast(shk)
                    srcrq = banks[1][:, 0:512].rearrange("p (h two e) -> p h two e", h=4, two=2)
                    dstrq = rqr[:].rearrange("p h (two e) -> p h two e", two=2)
                    rope(lambda hf: srcrq[:, :, hf, :], lambda hf: dstrq[:, :, hf, :], cosk, sink, shk, tvk, d_bank[1], d_rqr, d_rp, d_tmq)
                    srcrk = banks[4][:, 0:512].rearrange("p (h two e) -> p h two e", h=4, two=2)
                    dstrk = rkr[:].rearrange("p h (two e) -> p h two e", two=2)
                    rope(lambda hf: srcrk[:, :, hf, :], lambda hf: dstrk[:, :, hf, :], cosk, sink, shk, tvk, d_bank[4], d_rkr, d_rp, d_tmq)
                    if SUB < 3:
                        continue
                    P.op("pool", lambda e: e.tensor_scalar(out=qr[:], in0=qr[:], scalar1=0.125, scalar2=None, op0=ALU.mult), reads=[d_qr], writes=[d_qr])
                    P.op("pool", lambda e: e.tensor_scalar(out=rkr[:].rearrange("p a b -> p (a b)"), in0=rkr[:].rearrange("p a b -> p (a b)"),
                                                           scalar1=RET_SCALE, scalar2=None, op0=ALU.mult), reads=[d_rkr], writes=[d_rkr])
                    for g in range(4):
                        P.op("pe", lambda e, g=g: e.transpose(out=banks[3][:, g * 128:(g + 1) * 128], in_=qr[:, g * 128:(g + 1) * 128], identity=ident[:]),
                             reads=[d_qr, d_ident], writes=[d_bank[3]])
                    P.op("act", lambda e: e.copy(out=QT[:].rearrange("p a b -> p (a b)"), in_=banks[3][:, :]),
                         reads=[d_bank[3]], writes=[d_QT])
                    if SUB2 < 1:
                        continue
                    for h in range(4):
                        P.op("pe", lambda e, h=h: e.transpose(out=banks[3][:, h * 128:(h + 1) * 128], in_=rqr[:, h, :], identity=ident[:]),
                             reads=[d_rqr, d_ident], writes=[d_bank[3]])
                    P.op("act", lambda e: e.copy(out=rqT[:].rearrange("p a b -> p (a b)"), in_=banks[3][:, :]), reads=[d_bank[3]], writes=[d_rqT])
                    P.op("dve", lambda e: e.tensor_tensor(out=rqdT[:].rearrange("p a b -> p (a b)"), in0=banks[3][:, :],
                                                          in1=qdt[:].rearrange("p a b -> p (a b)"), op=ALU.mult),
                         reads=[d_bank[3], d_qdt], writes=[d_rqdT])
                    if SUB2 < 2:
                        continue
                    for h in range(4):
                        P.op("pe", lambda e, h=h: e.transpose(out=banks[3][:, h * 128:(h + 1) * 128], in_=rkr[:, h, :], identity=ident[:]),
                             reads=[d_rkr, d_ident], writes=[d_bank[3]])
                    P.op("act", lambda e: e.copy(out=rkT[:].rearrange("p a b -> p (a b)"), in_=banks[3][:, :]),
                         reads=[d_bank[3]], writes=[d_rkT])
                    if SUB < 4:
                        continue
                    for (bk, c0) in ((0, 1536), (1, 2048), (4, 2560), (5, 3072)):
                        for k in range(8):
                            P.op("pe", lambda e, bk=bk, k=k, c0=c0: e.matmul(out=banks[bk][:, :], lhsT=hT[:, k, :], rhs=WB[:, k, c0:c0 + 512],
                                                                           start=(k == 0), stop=(k == 7)),
                                 reads=[d_hT, d_WB], writes=[d_bank[bk]])
                    P.op("act", lambda e: e.copy(out=vb[:, 0:512], in_=banks[0][:, :]), reads=[d_bank[0]], writes=[d_vb])
                    P.op("act", lambda e: e.copy(out=vb[:, 512:1024], in_=banks[1][:, :]), reads=[d_bank[1]], writes=[d_vb])
                    P.op("act", lambda e: e.activation(out=sgate[:, 0:512], in_=banks[4][:, :], func=ACTF.Silu), reads=[d_bank[4]], writes=[d_sgate])
                    P.op("act", lambda e: e.activation(out=sgate[:, 512:1024], in_=banks[5][:, :], func=ACTF.Silu), reads=[d_bank[5]], writes=[d_sgate])
                    if PART < 2:
                        continue
                    for h in range(4):
                        P.op("pe", lambda e, h=h: e.matmul(out=banks[3][:, h * 128:(h + 1) * 128], lhsT=rkT[:, h, :], rhs=rqT[:, h, :], start=True, stop=True),
                             reads=[d_rkT, d_rqT], writes=[d_bank[3]])
                    P.op("dve", lambda e: e.tensor_tensor(out=attD[:].rearrange("p a b -> p (a b)"), in0=banks[3][:, :],
                                                          in1=dtab[:].rearrange("p a b -> p (a b)"), op=ALU.mult),
                         reads=[d_bank[3], d_dtab], writes=[d_attD])
                    for h in range(4):
                        bk = 4 + h // 2
                        c0 = (h % 2) * 256
                        P.op("pe", lambda e, h=h, bk=bk, c0=c0: e.matmul(out=banks[bk][:, c0:c0 + 256], lhsT=attD[:, h, :], rhs=vb[:, h * 256:(h + 1) * 256],
                                                                        start=True, stop=False),
                             reads=[d_attD, d_vb], writes=[d_bank[bk]])
                        P.op("pe", lambda e, h=h, bk=bk, c0=c0: e.matmul(out=banks[bk][:, c0:c0 + 256], lhsT=rqdT[:, h, :], rhs=SownB[:, h * 256:(h + 1) * 256],
                                                                        start=False, stop=True),
                             reads=[d_rqdT, d_SownB], writes=[d_bank[bk]])
                    for h in range(4):
                        bk = 4 + h // 2
                        c0 = (h % 2) * 256
                        P.op("act", lambda e, h=h, bk=bk, c0=c0: e.activation(out=osb[:, h, :], in_=banks[bk][:, c0:c0 + 256], func=ACTF.Identity,
                                                                             accum_out=st4[:, h:h + 1]),
                             reads=[d_bank[bk]], writes=[d_osb, d_st4])
                        P.op("act", lambda e, h=h, bk=bk, c0=c0: e.activation(out=junk_sh[:, 0:256], in_=banks[bk][:, c0:c0 + 256], func=ACTF.Square,
                                                                             accum_out=st4[:, 4 + h:5 + h]),
                             reads=[d_bank[bk]], writes=[d_junk_sh, d_st4])
                    P.op("dve", lambda e: e.tensor_scalar(out=st4[:, 8:12], in0=st4[:, 0:4], scalar1=1.0 / 256, scalar2=None, op0=ALU.mult),
                         reads=[d_st4], writes=[d_st4])
                    P.op("dve", lambda e: e.tensor_tensor(out=st4[:, 12:16], in0=st4[:, 8:12], in1=st4[:, 8:12], op=ALU.mult), reads=[d_st4], writes=[d_st4])
                    P.op("dve", lambda e: e.scalar_tensor_tensor(out=st4[:, 12:16], in0=st4[:, 4:8], scalar=1.0 / 256, in1=st4[:, 12:16],
                                                                 op0=ALU.mult, op1=ALU.subtract), reads=[d_st4], writes=[d_st4])
                    P.op("act", lambda e: e.activation(out=st4[:, 12:16], in_=st4[:, 12:16], func=ACTF.Sqrt, bias=epsG[:, 0:1], scale=1.0),
                         reads=[d_st4, d_epsG], writes=[d_st4])
                    P.op("dve", lambda e: e.reciprocal(out=st4[:, 12:16], in_=st4[:, 12:16]), reads=[d_st4], writes=[d_st4])
                    for h in range(4):
                        P.op("dve", lambda e, h=h: e.tensor_scalar(out=yret[:, h * 256:(h + 1) * 256], in0=osb[:, h, :], scalar1=st4[:, 8 + h:9 + h],
                                                                  scalar2=st4[:, 12 + h:13 + h], op0=ALU.subtract, op1=ALU.mult),
                             reads=[d_osb, d_st4], writes=[d_yret])
                    P.op("pool", lambda e: e.tensor_tensor(out=yret[:], in0=yret[:], in1=gnw[:], op=ALU.mult), reads=[d_yret, d_gnw], writes=[d_yret])
                    P.op("pool", lambda e: e.tensor_tensor(out=yret[:], in0=yret[:], in1=gnb[:], op=ALU.add), reads=[d_yret, d_gnb], writes=[d_yret])
                    P.op("pool", lambda e: e.tensor_tensor(out=yret[:], in0=yret[:], in1=sgate[:], op=ALU.mult), reads=[d_yret, d_sgate], writes=[d_yret])
                    for c in range(8):
                        P.op("pe", lambda e, c=c: e.transpose(out=pT[:, c * 128:(c + 1) * 128], in_=yret[:, c * 128:(c + 1) * 128], identity=ident[:]),
                             reads=[d_yret, d_ident], writes=[d_pTs[c // 4]])
                    P.op("act", lambda e: e.copy(out=yrT[:].rearrange("p a b -> p (a b)"), in_=pT[:]), reads=d_pTs, writes=[d_yrT])
                    P.dma("sp", lambda e, i=i: e.dma_start(out=yrT_d[i], in_=yrT[:]), reads=[d_yrT])
                    if DBG:
                        P.dma("sp", lambda e, i=i: e.dma_start(out=dbg_yr[i], in_=yret[:]), reads=[d_yret])

                    if PART < 3:
                        continue
                    for kvh in range(2):
                        QTh = QT[kvh * 64:(kvh + 1) * 64, :, :].rearrange("p a b -> p (a b)")
                        tiles = []
                        for jt in range(8):
                            tiles.append({"KT": cmpKT[kvh * 64:(kvh + 1) * 64, jt * 128:(jt + 1) * 128], "V": cmpVA[:, jt, kvh, :], "nk": 128,
                                          "biases": [(cmaskB[:, jt * 128:(jt + 1) * 128], IrepF)], "Ov": OvP[:, jt, :],
                                          "deps": [d_cmpKT, d_cmpVA, d_cmask, d_Irep, d_OvP]})
                        psv = lambda g: banks[4 + g // 2][:, (g % 2) * 256:(g % 2) * 256 + 256]
                        attn(128, 4, QTh, d_QT, tiles, o4[kvh], d_o4[kvh], pslc=(psv, [d_bank[4], d_bank[5]], 2), pvT=True)
                        evac(kvh, 0, True)
                        P.op("dve", lambda e, kvh=kvh: e.tensor_scalar(out=pacc[:], in0=psv(0), scalar1=rden[:, kvh * 4:kvh * 4 + 1], scalar2=None, op0=ALU.mult),
                             reads=[d_bank[4], d_rden], writes=[d_pacc])
                        for g in range(1, 4):
                            P.op("dve", lambda e, kvh=kvh, g=g: e.scalar_tensor_tensor(out=pacc[:], in0=psv(g), scalar=rden[:, kvh * 4 + g:kvh * 4 + g + 1],
                                                                                      in1=pacc[:], op0=ALU.mult, op1=ALU.add),
                                 reads=[d_bank[4 + g // 2], d_rden, d_pacc], writes=[d_pacc])
                        P.op("dve", lambda e: e.tensor_tensor(out=pacc[:], in0=pacc[:], in1=forceb[:], op=ALU.add), reads=[d_pacc, d_forceb], writes=[d_pacc])
                        P.op("dve", lambda e: e.max(out=m16[:, 0:8], in_=pacc[:]), reads=[d_pacc], writes=[d_m16])
                        P.op("dve", lambda e: e.match_replace(out=wk2[:], in_to_replace=m16[:, 0:8], in_values=pacc[:], imm_value=-3.0e38),
                             reads=[d_pacc, d_m16], writes=[d_wk2])
                        P.op("dve", lambda e: e.max(out=m16[:, 8:16], in_=wk2[:]), reads=[d_wk2], writes=[d_m16])
                        P.op("dve", lambda e, kvh=kvh: e.tensor_scalar(out=selb[kvh][:], in0=pacc[:], scalar1=m16[:, 15:16], scalar2=NEG,
                                                                      op0=ALU.is_lt, op1=ALU.mult),
                             reads=[d_pacc, d_m16], writes=[d_selb[kvh]])
                    if DBG:
                        P.dma("sp", lambda e, i=i: e.dma_start(out=dbg_selb[i, 0], in_=selb[0][:]), reads=[d_selb[0]])
                        P.dma("sp", lambda e, i=i: e.dma_start(out=dbg_selb[i, 1], in_=selb[1][:]), reads=[d_selb[1]])
                    if PART < 4:
                        continue
                    ngk = i + 1
                    for gk in range(ngk):
                        b = gk % 2
                        P.dma("sp", lambda e, gk=gk, b=b: e.dma_start(out=KsG[b][:], in_=ksT_d[:, gk * 1024:(gk + 1) * 1024]), writes=[d_KsG[b]])
                        P.dma("sp", lambda e, gk=gk, b=b: e.dma_start(out=VsG[b][:], in_=vs_d[gk * 1024:(gk + 1) * 1024, :].rearrange("(m p) f -> p m f", p=128)),
                              writes=[d_VsG[b]])
                        for kvh in range(2):
                            P.op("pool", lambda e, kvh=kvh, b=b, gk=gk: e.tensor_copy(
                                out=selbX[kvh][b][:].rearrange("p (a c) -> p a c", c=64),
                                in_=selb[kvh][:, gk * 16:(gk + 1) * 16].unsqueeze(2).to_broadcast([128, 16, 64])),
                                reads=[d_selb[kvh]], writes=[d_selbX[kvh][b]])
                        for kvh in range(2):
                            QTh = QT[kvh * 64:(kvh + 1) * 64, :, :].rearrange("p a b -> p (a b)")
                            tiles = []
                            for j in range(8):
                                bs = [(selbX[kvh][b][:, j * 128:(j + 1) * 128], IrepF)]
                                if gk == i:
                                    bs.append((dmaskB[:, j, :], IrepF))
                                tiles.append({"KT": KsG[b][kvh * 64:(kvh + 1) * 64, j * 128:(j + 1) * 128], "V": VsG[b][:, j, kvh * 65:(kvh + 1) * 65],
                                              "nk": 128, "biases": bs, "deps": [d_KsG[b], d_VsG[b], d_selbX[kvh][b], d_Irep, d_dmask]})
                            attn(128, 4, QTh, d_QT, tiles, o4[kvh], d_o4[kvh], first=(gk == 0), last=(gk == ngk - 1), pvT=True)
                    evac(0, 1, False)
                    evac(1, 1, False)
                    if PART < 5:
                        continue
                    for kvh in range(2):
                        QTh = QT[kvh * 64:(kvh + 1) * 64, :, :].rearrange("p a b -> p (a b)")
                        tiles = []
                        for m in range(12):
                            wm = m if i == 0 else 12 + m
                            tiles.append({"KT": KwWin[kvh * 64:(kvh + 1) * 64, m * 128:(m + 1) * 128], "V": VwWin[:, m, kvh * 65:(kvh + 1) * 65], "nk": 128,
                                          "biases": [(wmaskB[:, wm, :], IrepF)], "deps": [d_KwWin, d_VwWin, d_wmask, d_Irep]})
                        attn(128, 4, QTh, d_QT, tiles, o4[kvh], d_o4[kvh], pvT=True)
                        evac(kvh, 2, False)
                    onf = onsa[:].rearrange("p h d -> p (h d)")
                    for c in range(4):
                        P.op("pe", lambda e, c=c: e.transpose(out=pT[:, c * 128:(c + 1) * 128], in_=onf[:, c * 128:(c + 1) * 128], identity=ident[:]),
                             reads=[d_onsa, d_ident], writes=[d_pTs[0]])
                    P.op("act", lambda e: e.copy(out=onT[:].rearrange("p a b -> p (a b)"), in_=pT[:, 0:512]), reads=[d_pTs[0]], writes=[d_onT])
                    P.dma("sp", lambda e, i=i: e.dma_start(out=onT_d[i], in_=onT[:]), reads=[d_onT])
                    if DBG:
                        P.dma("sp", lambda e, i=i: e.dma_start(out=dbg_on[i], in_=onsa[:].rearrange("p h d -> p (h d)")), reads=[d_onsa])
                P.barrier()

        def phase_s1():
            with ExitStack() as sS:
                sR = ExitStack()
                smallc = P.sb([1, 1024], F32, sS); smallb = P.sb([1, 1024], BF16, sS); d_small = Dep()
                P.dma("sp", lambda e: e.dma_start(out=smallc[:], in_=smallc_d), writes=[d_small])
                P.op("dve", lambda e: e.tensor_copy(out=smallb[:], in_=smallc[:]), reads=[d_small], writes=[d_small])
                cm7row, wm0row, ones4 = smallb[0:1, 0:128], smallb[0:1, 128:256], smallb[0:1, 272:276]
                forceS = smallc[0:1, 276:532]
                oh4 = P.sb([SB, 4], F32, sS); d_oh4 = Dep()
                P.dma("sp", lambda e: e.dma_start(out=oh4[:], in_=oh4_d), writes=[d_oh4])
                OvS = P.sb([128, 8, 257], BF16, sS); d_OvS = Dep()
                P.dma("pool", lambda e: e.dma_start(out=OvS[:].rearrange("p a b -> p (a b)"), in_=ovs_d), writes=[d_OvS])
                wbc = P.sb([128, 2, 16, 256], F32, sS); d_wbc = Dep()
                P.dma("sp", lambda e: e.dma_start(out=wbc[:].rearrange("p a b c -> p (a b c)"), in_=wrow_d.partition_broadcast(128)[:, 0, :]), writes=[d_wbc])
                brow = P.sb([128, 256], F32, sS); d_brow = Dep()
                P.dma("sp", lambda e: e.dma_start(out=brow[:], in_=brow_d.partition_broadcast(128)[:, 0, :]), writes=[d_brow])
                QTs = P.sb([128, 4, SB], BF16, sS); d_QTs = Dep()
                KnT = P.sb([128, 2, SB], BF16, sS); d_KnT = Dep()
                VnA = P.sb([SB, 2, 2, 65], BF16, sS); d_VnA = Dep()
                sg = P.sb([SB, 24], F32, sS); d_sg = Dep()
                ohm = P.sb([128, 4, 4], F32, sR); d_ohm = Dep()
                P.dma("sp", lambda e: e.dma_start(out=ohm[:].rearrange("p a b -> p (a b)"), in_=ohm_d), writes=[d_ohm])
                gnw = P.sb([SB, D], F32, sR); d_gnw = Dep()
                gnb = P.sb([SB, D], F32, sR); d_gnb = Dep()
                P.dma("sp", lambda e: e.dma_start(out=gnw[:], in_=gnw_d.partition_broadcast(SB)[:, 0, :]), writes=[d_gnw])
                P.dma("sp", lambda e: e.dma_start(out=gnb[:], in_=gnb_d.partition_broadcast(SB)[:, 0, :]), writes=[d_gnb])
                epsG = P.sb([SB, 1], F32, sR); d_epsG = Dep()
                P.op("dve", lambda e: e.memset(epsG[:], GN_EPS), writes=[d_epsG])
                xs = P.sb([SB, D], F32, sR); d_xs = Dep()
                rps = P.sb([SB, 192], F32, sR); d_rps = Dep()
                msS = P.sb([SB, 2, D], F32, sR); d_msS = Dep()
                gm4 = P.sb([SB, D], F32, sR); d_gm4 = Dep()
                P.dma("sp", lambda e: e.dma_start(out=xs[:], in_=x_smp), writes=[d_xs])
                P.dma("sp", lambda e: e.dma_start(out=rps[:], in_=rope_smp), writes=[d_rps])
                P.dma("sp", lambda e: e.dma_start(out=msS[:], in_=mods_d[:, 0:2 * D].rearrange("s (a d) -> s a d", a=2)), writes=[d_msS])
                P.dma("sp", lambda e: e.dma_start(out=gm4[:], in_=norm_mix.partition_broadcast(SB)[:, 0, :]), writes=[d_gm4])
                P.op("dve", lambda e: e.scalar_tensor_tensor(out=msS[:, 1, :], in0=msS[:, 1, :], scalar=1.0, in1=gm4[:], op0=ALU.add, op1=ALU.mult),
                     reads=[d_msS, d_gm4], writes=[d_msS])
                wkS = mk_wk(sR, SB)
                hTs = P.sb([128, 8, SB], BF16, sR); d_hTs = Dep()
                norm_mod_T(xs, d_xs, SB, msS[:, 1, :], d_msS, msS[:, 0, :], d_msS, wkS, hTs, d_hTs)
                P.dma("sp", lambda e: e.dma_start(out=hTs_d, in_=hTs[:]), reads=[d_hTs])
                zs = P.sb([SB, 4376], F32, sR); d_zs = Dep()
                wch = [P.sb([128, 8, 512], BF16, sR) for _ in range(2)]; d_wch = [Dep(), Dep()]
                segs = []
                for g in range(4):
                    segs += [(C_Q + g * 64, 64), (C_Q + (4 + g) * 64, 64)]
                segs += [(C_RQ, 512), (C_RK, 512), (C_RV, 512), (C_RV + 512, 512), (C_RG, 512), (C_RG + 512, 512), (C_G, 24),
                         (C_KVC, 128), (C_KVS, 128), (C_KVW, 128), (C_KVC + 128, 128), (C_KVS + 128, 128), (C_KVW + 128, 128)]
                chunks = []
                cur, curw = [], 0
                for sg_ in segs:
                    if curw + sg_[1] > 512:
                        chunks.append(cur); cur, curw = [], 0
                    cur.append(sg_); curw += sg_[1]
                chunks.append(cur)
                zoff = 0
                for ci, ch in enumerate(chunks):
                    b = ci % 2
                    o_ = 0
                    for (src, n_) in ch:
                        P.dma("pool", lambda e, b=b, o_=o_, src=src, n_=n_: e.dma_start(
                            out=wch[b][:, :, o_:o_ + n_], in_=w_in[:, src:src + n_].rearrange("(c p) n -> p c n", p=128)), writes=[d_wch[b]])
                        o_ += n_
                    for k in range(8):
                        P.op("pe", lambda e, b=b, k=k, o_=o_: e.matmul(out=banks[b][0:SB, 0:o_], lhsT=hTs[:, k, :], rhs=wch[b][:, k, 0:o_], start=(k == 0), stop=(k == 7)),
                             reads=[d_hTs, d_wch[b]], writes=[d_bank[b]])
                    P.op("act", lambda e, b=b, o_=o_, zoff=zoff: e.copy(out=zs[:, zoff:zoff + o_], in_=banks[b][0:SB, 0:o_]), reads=[d_bank[b]], writes=[d_zs])
                    zoff += o_
                tms = [P.sb([SB, 256], F32, sR) for _ in range(4)]; d_tms = [Dep() for _ in range(4)]
                qr = P.sb([SB, 512], F32, sR); d_qr = Dep()
                rqr = P.sb([SB, 4, 128], F32, sR); d_rqr = Dep()
                rkr = P.sb([SB, 4, 128], F32, sR); d_rkr = Dep()
                kvo = P.sb([SB, 3, 256], F32, sR); d_kvo = Dep()
                sgate = P.sb([SB, 1024], F32, sR); d_sgate = Dep()
                vbs = P.sb([SB, 1024], BF16, sR); d_vbs = Dep()
                shq = [SB, 8, 32]
                srcq = zs[:, 0:512].rearrange("p (h two e) -> p h two e", h=8, two=2)
                dstq = qr[:].rearrange("p (h two e) -> p h two e", h=8, two=2)
                tvq = [t[:, 0:256].rearrange("p (h e) -> p h e", h=8) for t in tms]
                cos32 = rps[:, 0:32]; sin32 = rps[:, 32:64]
                rope(lambda hf: srcq[:, :, hf, :], lambda hf: dstq[:, :, hf, :], cos32.unsqueeze(1).to_broadcast(shq), sin32.unsqueeze(1).to_broadcast(shq),
                     shq, tvq, d_zs, d_qr, d_rps, d_tms)
                P.op("pool", lambda e: e.tensor_scalar(out=qr[:], in0=qr[:], scalar1=0.125, scalar2=None, op0=ALU.mult), reads=[d_qr], writes=[d_qr])
                shk = [SB, 4, 64]
                tvk = [t[:, 0:256].rearrange("p (h e) -> p h e", h=4) for t in tms]
                cosk = rps[:, 64:128].unsqueeze(1).to_broadcast(shk); sink = rps[:, 128:192].unsqueeze(1).to_broadcast(shk)
                for (c0, dstt, d_dst) in ((512, rqr, d_rqr), (1024, rkr, d_rkr)):
                    src_ = zs[:, c0:c0 + 512].rearrange("p (h two e) -> p h two e", h=4, two=2)
                    dst_ = dstt[:].rearrange("p h (two e) -> p h two e", two=2)
                    rope(lambda hf, src_=src_: src_[:, :, hf, :], lambda hf, dst_=dst_: dst_[:, :, hf, :], cosk, sink, shk, tvk, d_zs, d_dst, d_rps, d_tms)
                P.op("pool", lambda e: e.tensor_scalar(out=rkr[:].rearrange("p a b -> p (a b)"), in0=rkr[:].rearrange("p a b -> p (a b)"),
                                                       scalar1=RET_SCALE, scalar2=None, op0=ALU.mult), reads=[d_rkr], writes=[d_rkr])
                shn = [SB, 3, 2, 32]
                srcn = zs[:, 3608:3992].rearrange("p (j h two e) -> p j h two e", j=3, h=2, two=2)
                dstn = kvo[:, :, 0:128].rearrange("p j (h two e) -> p j h two e", h=2, two=2)
                tvn = [t[:, 0:192].rearrange("p (j h e) -> p j h e", j=3, h=2) for t in tms]
                rope(lambda hf: srcn[:, :, :, hf, :], lambda hf: dstn[:, :, :, hf, :], cos32.unsqueeze(1).unsqueeze(1).to_broadcast(shn),
                     sin32.unsqueeze(1).unsqueeze(1).to_broadcast(shn), shn, tvn, d_zs, d_kvo, d_rps, d_tms)
                P.op("act", lambda e: e.copy(out=kvo[:, :, 128:256], in_=zs[:, 3992:4376].rearrange("p (j f) -> p j f", j=3)), reads=[d_zs], writes=[d_kvo])
                P.op("act", lambda e: e.activation(out=sg[:], in_=zs[:, 3584:3608], func=ACTF.Sigmoid), reads=[d_zs], writes=[d_sg])
                P.op("act", lambda e: e.activation(out=sgate[:], in_=zs[:, 2560:3584], func=ACTF.Silu), reads=[d_zs], writes=[d_sgate])
                P.op("act", lambda e: e.copy(out=vbs[:], in_=zs[:, 1536:2560]), reads=[d_zs], writes=[d_vbs])
                P.dma("sp", lambda e: e.dma_start(out=o_cmp_s, in_=kvo[:, 0, :]), reads=[d_kvo])
                P.dma("sp", lambda e: e.dma_start(out=o_sel_s, in_=kvo[:, 1, :]), reads=[d_kvo])
                P.dma("sp", lambda e: e.dma_start(out=o_win_s[:, 511, :], in_=kvo[:, 2, :]), reads=[d_kvo])
                for b in range(SB):
                    P.dma("sp", lambda e, b=b: e.dma_start(out=o_win_s[b, 0:511, :], in_=cache_win[b, 1:512, :]))
                rqTs = P.sb([128, 4, SB], BF16, sR); d_rqTs = Dep()
                rqdTs = P.sb([128, 4, SB], F32, sR); d_rqdTs = Dep()
                rkTs = P.sb([128, 4, SB], BF16, sR); d_rkTs = Dep()
                for (srcs, dstT, d_srcs, d_dstT) in (([qr[:, g * 128:(g + 1) * 128] for g in range(4)], QTs, d_qr, d_QTs),
                                                     ([rqr[:, h, :] for h in range(4)], rqTs, d_rqr, d_rqTs),
                                                     ([rkr[:, h, :] for h in range(4)], rkTs, d_rkr, d_rkTs),
                                                     ([kvo[:, 1, 0:128], kvo[:, 2, 0:128]], KnT, d_kvo, d_KnT)):
                    for j, sap in enumerate(srcs):
                        P.op("pe", lambda e, j=j, sap=sap: e.transpose(out=banks[3][:, j * SB:(j + 1) * SB], in_=sap, identity=ident[0:SB, 0:SB]),
                             reads=[d_srcs, d_ident], writes=[d_bank[3]])
                    nn = len(srcs) * SB
                    P.op("act", lambda e, dstT=dstT, nn=nn: e.copy(out=dstT[:].rearrange("p a b -> p (a b)"), in_=banks[3][:, 0:nn]), reads=[d_bank[3]], writes=[d_dstT])
                for h in range(4):
                    P.op("dve", lambda e, h=h: e.tensor_scalar(out=rqdTs[:, h, :], in0=rqTs[:, h, :], scalar1=float(math.exp(LG[h])), scalar2=None, op0=ALU.mult),
                         reads=[d_rqTs], writes=[d_rqdTs])
                P.op("pool", lambda e: e.memset(VnA[:].rearrange("p a b c -> p (a b c)"), 1.0), writes=[d_VnA])
                P.op("pool", lambda e: e.tensor_copy(out=VnA[:, :, :, 0:64], in_=kvo[:, 1:3, 128:256].rearrange("p j (h d) -> p j h d", h=2)),
                     reads=[d_kvo], writes=[d_VnA])
                attDs = P.sb([SB, 4, SB], BF16, sR); d_attDs = Dep()
                for h in range(4):
                    P.op("pe", lambda e, h=h: e.matmul(out=banks[3][0:SB, h * SB:(h + 1) * SB], lhsT=rkTs[:, h, :], rhs=rqTs[:, h, :], start=True, stop=True),
                         reads=[d_rkTs, d_rqTs], writes=[d_bank[3]])
                P.op("dve", lambda e: e.tensor_tensor(out=attDs[:], in0=banks[3][0:SB, 0:16].rearrange("p (h i) -> p h i", h=4),
                                                      in1=oh4[:].unsqueeze(1).to_broadcast([SB, 4, SB]), op=ALU.mult), reads=[d_bank[3], d_oh4], writes=[d_attDs])
                Sb = [P.sb([128, 4, 256], F32, sR) for _ in range(SB)]; d_Sb = [Dep() for _ in range(SB)]
                Sbb = [P.sb([128, 4, 256], BF16, sR) for _ in range(SB)]; d_Sbb = [Dep() for _ in range(SB)]
                rqm = [P.sb([128, 4, SB], BF16, sR) for _ in range(SB)]; d_rqm = [Dep() for _ in range(SB)]
                kdm = [P.sb([SB, 4, 128], BF16, sR) for _ in range(SB)]; d_kdm = [Dep() for _ in range(SB)]
                for b in range(SB):
                    P.dma("sp", lambda e, b=b: e.dma_start(out=Sb[b][:], in_=state_s[b].rearrange("h k v -> k h v")), writes=[d_Sb[b]])
                    P.op("pool", lambda e, b=b: e.tensor_copy(out=Sbb[b][:].rearrange("p a b -> p (a b)"), in_=Sb[b][:].rearrange("p a b -> p (a b)")),
                         reads=[d_Sb[b]], writes=[d_Sbb[b]])
                    P.op("dve", lambda e, b=b: e.tensor_tensor(out=rqm[b][:], in0=rqdTs[:], in1=ohm[:, b, :].unsqueeze(1).to_broadcast([128, 4, SB]), op=ALU.mult),
                         reads=[d_rqdTs, d_ohm], writes=[d_rqm[b]])
                    P.op("dve", lambda e, b=b: e.tensor_scalar(out=kdm[b][:].rearrange("p a b -> p (a b)"), in0=rkr[:].rearrange("p a b -> p (a b)"),
                                                              scalar1=oh4[:, b:b + 1], scalar2=None, op0=ALU.mult), reads=[d_rkr, d_oh4], writes=[d_kdm[b]])
                for h in range(4):
                    bk = 4 + h // 2
                    c0 = (h % 2) * 256
                    P.op("pe", lambda e, h=h, bk=bk, c0=c0: e.matmul(out=banks[bk][0:SB, c0:c0 + 256], lhsT=attDs[:, h, :], rhs=vbs[:, h * 256:(h + 1) * 256],
                                                                    start=True, stop=False), reads=[d_attDs, d_vbs], writes=[d_bank[bk]])
                    for b in range(SB):
                        P.op("pe", lambda e, h=h, bk=bk, c0=c0, b=b: e.matmul(out=banks[bk][0:SB, c0:c0 + 256], lhsT=rqm[b][:, h, :], rhs=Sbb[b][:, h, :],
                                                                             start=False, stop=(b == SB - 1)), reads=[d_rqm[b], d_Sbb[b]], writes=[d_bank[bk]])
                osb = P.sb([SB, 4, 256], F32, sR); d_osb = Dep()
                st4 = P.sb([SB, 16], F32, sR); d_st4 = Dep()
                yret = P.sb([SB, 1024], F32, sR); d_yret = Dep()
                for h in range(4):
                    bk = 4 + h // 2
                    c0 = (h % 2) * 256
                    P.op("act", lambda e, h=h, bk=bk, c0=c0: e.activation(out=osb[:, h, :], in_=banks[bk][0:SB, c0:c0 + 256], func=ACTF.Identity, accum_out=st4[:, h:h + 1]),
                         reads=[d_bank[bk]], writes=[d_osb, d_st4])
                    P.op("act", lambda e, h=h, bk=bk, c0=c0: e.activation(out=junk_sh[0:SB, 0:256], in_=banks[bk][0:SB, c0:c0 + 256], func=ACTF.Square,
                                                                         accum_out=st4[:, 4 + h:5 + h]), reads=[d_bank[bk]], writes=[d_junk_sh, d_st4])
                P.op("dve", lambda e: e.tensor_scalar(out=st4[:, 8:12], in0=st4[:, 0:4], scalar1=1.0 / 256, scalar2=None, op0=ALU.mult), reads=[d_st4], writes=[d_st4])
                P.op("dve", lambda e: e.tensor_tensor(out=st4[:, 12:16], in0=st4[:, 8:12], in1=st4[:, 8:12], op=ALU.mult), reads=[d_st4], writes=[d_st4])
                P.op("dve", lambda e: e.scalar_tensor_tensor(out=st4[:, 12:16], in0=st4[:, 4:8], scalar=1.0 / 256, in1=st4[:, 12:16], op0=ALU.mult, op1=ALU.subtract),
                     reads=[d_st4], writes=[d_st4])
                P.op("act", lambda e: e.activation(out=st4[:, 12:16], in_=st4[:, 12:16], func=ACTF.Sqrt, bias=epsG[:, 0:1], scale=1.0), reads=[d_st4, d_epsG], writes=[d_st4])
                P.op("dve", lambda e: e.reciprocal(out=st4[:, 12:16], in_=st4[:, 12:16]), reads=[d_st4], writes=[d_st4])
                for h in range(4):
                    P.op("dve", lambda e, h=h: e.tensor_scalar(out=yret[:, h * 256:(h + 1) * 256], in0=osb[:, h, :], scalar1=st4[:, 8 + h:9 + h],
                                                              scalar2=st4[:, 12 + h:13 + h], op0=ALU.subtract, op1=ALU.mult), reads=[d_osb, d_st4], writes=[d_yret])
                P.op("pool", lambda e: e.tensor_tensor(out=yret[:], in0=yret[:], in1=gnw[:], op=ALU.mult), reads=[d_yret, d_gnw], writes=[d_yret])
                P.op("pool", lambda e: e.tensor_tensor(out=yret[:], in0=yret[:], in1=gnb[:], op=ALU.add), reads=[d_yret, d_gnb], writes=[d_yret])
                P.op("pool", lambda e: e.tensor_tensor(out=yret[:], in0=yret[:], in1=sgate[:], op=ALU.mult), reads=[d_yret, d_sgate], writes=[d_yret])
                yrTs = P.sb([128, 8, SB], BF16, sR); d_yrTs = Dep()
                for c in range(8):
                    P.op("pe", lambda e, c=c: e.transpose(out=banks[3][:, c * SB:(c + 1) * SB], in_=yret[:, c * 128:(c + 1) * 128], identity=ident[0:SB, 0:SB]),
                         reads=[d_yret, d_ident], writes=[d_bank[3]])
                P.op("act", lambda e: e.copy(out=yrTs[:].rearrange("p a b -> p (a b)"), in_=banks[3][:, 0:8 * SB]), reads=[d_bank[3]], writes=[d_yrTs])
                P.dma("sp", lambda e: e.dma_start(out=yrTs_d, in_=yrTs[:]), reads=[d_yrTs])
                snew = [P.sb([128, 256], F32, sR) for _ in range(2)]; d_snew = [Dep(), Dep()]
                for b in range(SB):
                    for h in range(4):
                        bb = (b * 4 + h) % 2
                        P.op("pe", lambda e, b=b, h=h, bb=bb: e.matmul(out=banks[4 + bb][:, 0:256], lhsT=kdm[b][:, h, :], rhs=vbs[:, h * 256:(h + 1) * 256],
                                                                      start=True, stop=True), reads=[d_kdm[b], d_vbs], writes=[d_bank[4 + bb]])
                        P.op("dve", lambda e, b=b, h=h, bb=bb: e.scalar_tensor_tensor(out=snew[bb][:], in0=Sb[b][:, h, :], scalar=float(math.exp(LG[h])),
                                                                                     in1=banks[4 + bb][:, 0:256], op0=ALU.mult, op1=ALU.add),
                             reads=[d_Sb[b], d_bank[4 + bb]], writes=[d_snew[bb]])
                        P.dma("sp", lambda e, b=b, h=h, bb=bb: e.dma_start(out=o_state_s[b, h], in_=snew[bb][:]), reads=[d_snew[bb]])

                P.barrier()
                sR.close()
                pts = [P.sb([128, 512], BF16, sS) for _ in range(2)]; d_pts = [Dep(), Dep()]
                attn = make_attn(pts, d_pts)
                G = [P.sb([128, 16, 256], F32, sS) for _ in range(2)]; d_G = [Dep(), Dep()]
                prod = [P.sb([128, 16, 256], F32, sS) for _ in range(2)]; d_prod = [Dep(), Dep()]
                Fs = P.sb([128, 8, 256], F32, sS); d_Fs = Dep()
                S2 = P.sb([128, 9, 256], F32, sS); d_S2 = Dep()
                cmpKTs = P.sb([128, 8, 128], BF16, sS); d_cmpKTs = Dep()
                cmpVAs = P.sb([128, 8, 2, 65], BF16, sS); d_cmpVAs = Dep()
                P.op("pool", lambda e: e.memset(cmpVAs[:].rearrange("p a b c -> p (a b c)"), 1.0), writes=[d_cmpVAs])
                KT16 = [P.sb([128, 16, 128], BF16, sS) for _ in range(2)]; d_KT16 = [Dep(), Dep()]
                V16 = [P.sb([128, 16, 2, 65], BF16, sS) for _ in range(2)]; d_V16 = [Dep(), Dep()]
                for t_ in V16:
                    P.op("pool", lambda e, t_=t_: e.memset(t_[:].rearrange("p a b c -> p (a b c)"), 1.0), writes=d_V16)
                Wt = P.sb([128, 4, 256], F32, sS); d_Wt = Dep()
                KTw = P.sb([128, 4, 128], BF16, sS); d_KTw = Dep()
                VwA = P.sb([128, 4, 2, 65], BF16, sS); d_VwA = Dep()
                P.op("pool", lambda e: e.memset(VwA[:].rearrange("p a b c -> p (a b c)"), 1.0), writes=[d_VwA])
                ptb = P.sb([128, 2], I32, sS); d_ptb = Dep()
                idx8 = P.sb([128, 9], I32, sS); d_idx8 = Dep()
                og = P.sb([SB, 65], F32, sS); d_og = Dep()
                rd1 = P.sb([SB, 2], F32, sS); d_rd1 = Dep()
                pn = P.sb([SB, 256], F32, sS); d_pn = Dep()
                srow = P.sb([1, 256], F32, sS); d_srow = Dep()
                wk2 = P.sb([1, 256], F32, sS); d_wk2 = Dep()
                m16 = P.sb([1, 16], F32, sS); d_m16 = Dep()
                selbB = [P.sb([1, 256], BF16, sS) for _ in range(2)]; d_selbB = [Dep(), Dep()]
                o4s = [banks[2][0:SB, 0:65].rearrange("p (g c) -> p g c", g=1), banks[3][0:SB, 0:65].rearrange("p (g c) -> p g c", g=1)]
                d_o4s = [d_bank[2], d_bank[3]]
                ones4f = oh4
                onec = P.sb([SB, 1], F32, sS); d_onec = Dep()
                P.op("dve", lambda e: e.memset(onec[:], 1.0), writes=[d_onec])

                def evac_s(b, kvh, br_i):
                    o4v, d_o = o4s[kvh], d_o4s[kvh]
                    P.op("dve", lambda e: e.tensor_scalar(out=rd1[:, 0:1], in0=o4v[:, 0, 64:65], scalar1=1e-30, scalar2=None, op0=ALU.max), reads=[d_o], writes=[d_rd1])
                    P.op("dve", lambda e: e.reciprocal(out=rd1[:, 0:1], in_=rd1[:, 0:1]), reads=[d_rd1], writes=[d_rd1])
                    P.op("dve", lambda e: e.tensor_scalar(out=og[:, 0:64], in0=o4v[:, 0, 0:64], scalar1=rd1[:, 0:1], scalar2=None, op0=ALU.mult),
                         reads=[d_o, d_rd1], writes=[d_og])
                    P.dma("sp", lambda e: e.dma_start(out=os_d[b, br_i, kvh * 4:(kvh + 1) * 4, :], in_=og[:, 0:64]), reads=[d_og])

                for b in range(SB):
                    P.dma("sp", lambda e, b=b: e.dma_start(out=ptb[:, 0:1], in_=ptab[b].rearrange("(p o) -> p o", o=1)), writes=[d_ptb])
                    P.dma("sp", lambda e, b=b: e.dma_start(out=ptb[0:127, 1:2], in_=ptab[b, 1:128].rearrange("(p o) -> p o", o=1)), writes=[d_ptb])
                    P.dma("sp", lambda e, b=b: e.dma_start(out=ptb[127:128, 1:2], in_=ptab[b, 127:128].rearrange("(p o) -> p o", o=1)), writes=[d_ptb])
                    for k in range(8):
                        P.op("dve", lambda e, k=k: e.tensor_scalar(out=idx8[:, k:k + 1], in0=ptb[:, 0:1], scalar1=8, scalar2=k, op0=ALU.mult, op1=ALU.add),
                             reads=[d_ptb], writes=[d_idx8])
                    P.op("dve", lambda e: e.tensor_scalar(out=idx8[:, 8:9], in0=ptb[:, 1:2], scalar1=8, scalar2=0, op0=ALU.mult, op1=ALU.add),
                         reads=[d_ptb], writes=[d_idx8])
                    for k in range(9):
                        gb_ = k % 2
                        P.dma("pool", lambda e, k=k, gb_=gb_: e.indirect_dma_start(
                            out=G[gb_][:].rearrange("p a b -> p (a b)"), out_offset=None, in_=cache_cmp[:, :],
                            in_offset=bass.IndirectOffsetOnAxis(ap=idx8[:, k:k + 1], axis=0)), reads=[d_idx8], writes=[d_G[gb_]])
                        for r in range(2):
                            if k == 8 and r == 0:
                                continue
                            P.op("pool", lambda e, gb_=gb_, r=r: e.tensor_tensor(out=prod[r][:], in0=G[gb_][:], in1=wbc[:, r, :, :], op=ALU.mult),
                                 reads=[d_G[gb_], d_wbc], writes=[d_prod[r]])
                            dstF = Fs[:, k, :] if r == 0 else S2[:, k, :]
                            P.op("dve", lambda e, r=r, dstF=dstF: e.tensor_reduce(out=dstF, in_=prod[r][:].rearrange("p j f -> p f j"), axis=AX.X, op=ALU.add),
                                 reads=[d_prod[r]], writes=[d_Fs if r == 0 else d_S2])
                    P.op("dve", lambda e: e.tensor_tensor(out=Fs[:], in0=Fs[:], in1=S2[:, 1:9, :], op=ALU.add), reads=[d_Fs, d_S2], writes=[d_Fs])
                    P.op("dve", lambda e: e.tensor_tensor(out=Fs[:], in0=Fs[:], in1=brow[:].unsqueeze(1).to_broadcast([128, 8, 256]), op=ALU.add),
                         reads=[d_Fs, d_brow], writes=[d_Fs])
                    for s_ in range(8):
                        P.op("pe", lambda e, s_=s_: e.transpose(out=pT[:, s_ * 128:(s_ + 1) * 128], in_=Fs[:, s_, 0:128], identity=ident[:]),
                             reads=[d_Fs, d_ident], writes=[d_pTs[s_ // 4]])
                    P.op("act", lambda e: e.copy(out=cmpKTs[:].rearrange("p a b -> p (a b)"), in_=pT[:]), reads=d_pTs, writes=[d_cmpKTs])
                    P.op("pool", lambda e: e.tensor_copy(out=cmpVAs[:, :, :, 0:64], in_=Fs[:, :, 128:256].rearrange("p s (h d) -> p s h d", h=2)),
                         reads=[d_Fs], writes=[d_cmpVAs])
                    for kvh in range(2):
                        QTh = QTs[kvh * 64:(kvh + 1) * 64, :, b]
                        tiles = []
                        for s_ in range(8):
                            bs = [(cm7row, ones4)] if s_ == 7 else []
                            tiles.append({"KT": cmpKTs[kvh * 64:(kvh + 1) * 64, s_, :], "V": cmpVAs[:, s_, kvh, :], "nk": 128, "biases": bs,
                                          "Ov": OvS[:, s_, :], "deps": [d_cmpKTs, d_cmpVAs, d_small, d_OvS]})
                        psvS = lambda g: banks[4][0:SB, 0:257]
                        attn(SB, 1, QTh, d_QTs, tiles, o4s[kvh], d_o4s[kvh], pslc=(psvS, [d_bank[4]], 1))
                        evac_s(b, kvh, 0)
                        P.op("dve", lambda e: e.tensor_scalar(out=rd1[:, 1:2], in0=banks[4][0:SB, 256:257], scalar1=1e-30, scalar2=None, op0=ALU.max),
                             reads=[d_bank[4]], writes=[d_rd1])
                        P.op("dve", lambda e: e.reciprocal(out=rd1[:, 1:2], in_=rd1[:, 1:2]), reads=[d_rd1], writes=[d_rd1])
                        P.op("dve", lambda e: e.tensor_scalar(out=pn[:], in0=banks[4][0:SB, 0:256], scalar1=rd1[:, 1:2], scalar2=None, op0=ALU.mult),
                             reads=[d_bank[4], d_rd1], writes=[d_pn])
                        P.op("pe", lambda e: e.matmul(out=banks[5][0:1, 0:256], lhsT=onec[:, 0:1], rhs=pn[:], start=True, stop=True),
                             reads=[d_onec, d_pn], writes=[d_bank[5]])
                        P.op("dve", lambda e: e.tensor_tensor(out=srow[:], in0=banks[5][0:1, 0:256], in1=forceS, op=ALU.add), reads=[d_bank[5], d_small], writes=[d_srow])
                        P.op("dve", lambda e: e.max(out=m16[:, 0:8], in_=srow[:]), reads=[d_srow], writes=[d_m16])
                        P.op("dve", lambda e: e.match_replace(out=wk2[:], in_to_replace=m16[:, 0:8], in_values=srow[:], imm_value=-3.0e38),
                             reads=[d_srow, d_m16], writes=[d_wk2])
                        P.op("dve", lambda e: e.max(out=m16[:, 8:16], in_=wk2[:]), reads=[d_wk2], writes=[d_m16])
                        P.op("dve", lambda e, kvh=kvh: e.tensor_scalar(out=selbB[kvh][:], in0=srow[:], scalar1=m16[:, 14:15], scalar2=NEG, op0=ALU.is_lt, op1=ALU.mult),
                             reads=[d_srow, d_m16], writes=[d_selbB[kvh]])
                    for k in range(8):
                        gb_ = k % 2
                        P.dma("pool", lambda e, k=k, gb_=gb_: e.indirect_dma_start(
                            out=G[gb_][:].rearrange("p a b -> p (a b)"), out_offset=None, in_=cache_sel[:, :],
                            in_offset=bass.IndirectOffsetOnAxis(ap=idx8[:, k:k + 1], axis=0)), reads=[d_idx8], writes=[d_G[gb_]])
                        for j in range(16):
                            if j % 4 == 0:
                                tb_, d_tb = (banks[4], d_bank[4]) if (j // 4) % 2 == 0 else (banks[5], d_bank[5])
                            P.op("pe", lambda e, gb_=gb_, j=j, tb_=tb_: e.transpose(out=tb_[:, (j % 4) * 128:(j % 4 + 1) * 128], in_=G[gb_][:, j, 0:128], identity=ident[:]),
                                 reads=[d_G[gb_], d_ident], writes=[d_tb])
                            if j % 4 == 3:
                                P.op("act", lambda e, gb_=gb_, j=j, tb_=tb_: e.copy(out=KT16[gb_][:, j - 3:j + 1, :].rearrange("p a b -> p (a b)"), in_=tb_[:, :]),
                                     reads=[d_tb], writes=[d_KT16[gb_]])
                        P.op("pool", lambda e, gb_=gb_: e.tensor_copy(out=V16[gb_][:, :, :, 0:64], in_=G[gb_][:, :, 128:256].rearrange("p j (h d) -> p j h d", h=2)),
                             reads=[d_G[gb_]], writes=[d_V16[gb_]])
                        for kvh in range(2):
                            QTh = QTs[kvh * 64:(kvh + 1) * 64, :, b]
                            tiles = []
                            for j in range(16):
                                hi = 1 if (16 * k + j) >= 64 else 0
                                tiles.append({"KT": KT16[gb_][kvh * 64:(kvh + 1) * 64, j, :], "V": V16[gb_][:, j, kvh, :], "nk": 128,
                                              "biases": [(selbB[kvh][0:1, hi:256:2], ones4)], "deps": [d_KT16[gb_], d_V16[gb_], d_selbB[kvh], d_small]})
                            attn(SB, 1, QTh, d_QTs, tiles, o4s[kvh], d_o4s[kvh], first=(k == 0), last=False)
                    for kvh in range(2):
                        QTh = QTs[kvh * 64:(kvh + 1) * 64, :, b]
                        tiles = [{"KT": KnT[kvh * 64:(kvh + 1) * 64, 0, :], "V": VnA[:, 0, kvh, :], "nk": SB,
                                  "biases": [(smallb[0:1, 256 + 4 * b:260 + 4 * b], ones4)], "deps": [d_KnT, d_VnA, d_small]}]
                        attn(SB, 1, QTh, d_QTs, tiles, o4s[kvh], d_o4s[kvh], first=False, last=True)
                        evac_s(b, kvh, 1)
                    P.dma("sp", lambda e, b=b: e.dma_start(out=Wt[:], in_=cache_win[b].rearrange("(m p) f -> p m f", p=128)), writes=[d_Wt])
                    for m in range(4):
                        P.op("pe", lambda e, m=m: e.transpose(out=banks[4][:, m * 128:(m + 1) * 128], in_=Wt[:, m, 0:128], identity=ident[:]),
                             reads=[d_Wt, d_ident], writes=[d_bank[4]])
                    P.op("act", lambda e: e.copy(out=KTw[:].rearrange("p a b -> p (a b)"), in_=banks[4][:, :]), reads=[d_bank[4]], writes=[d_KTw])
                    P.op("pool", lambda e: e.tensor_copy(out=VwA[:, :, :, 0:64], in_=Wt[:, :, 128:256].rearrange("p m (h d) -> p m h d", h=2)),
                         reads=[d_Wt], writes=[d_VwA])
                    for kvh in range(2):
                        QTh = QTs[kvh * 64:(kvh + 1) * 64, :, b]
                        tiles = []
                        for m in range(4):
                            bs = [(wm0row, ones4)] if m == 0 else []
                            tiles.append({"KT": KTw[kvh * 64:(kvh + 1) * 64, m, :], "V": VwA[:, m, kvh, :], "nk": 128, "biases": bs,
                                          "deps": [d_KTw, d_VwA, d_small]})
                        tiles.append({"KT": KnT[kvh * 64:(kvh + 1) * 64, 1, :], "V": VnA[:, 1, kvh, :], "nk": SB,
                                      "biases": [(smallb[0:1, 256 + 4 * b:260 + 4 * b], ones4)], "deps": [d_KnT, d_VnA, d_small]})
                        attn(SB, 1, QTh, d_QTs, tiles, o4s[kvh], d_o4s[kvh])
                        evac_s(b, kvh, 2)
                P.barrier()
                osall = P.sb([SB, 3, 8, 64], F32, sS); d_osall = Dep()
                onsa = P.sb([SB, 8, 64], F32, sS); d_onsa = Dep()
                tmpo = P.sb([SB, 8, 64], F32, sS); d_tmpo = Dep()
                P.dma("sp", lambda e: e.dma_start(out=osall[:].rearrange("p a b c -> p (a b c)"), in_=os_d.rearrange("b r h d -> b (r h d)")), writes=[d_osall])
                sgv = sg[:].rearrange("p (h r) -> p h r", r=3)
                for r in range(3):
                    dst_, d_dst_ = (onsa, d_onsa) if r == 0 else (tmpo, d_tmpo)
                    P.op("dve", lambda e, r=r, dst_=dst_: e.tensor_tensor(out=dst_[:], in0=osall[:, r, :, :], in1=sgv[:, :, r].unsqueeze(2).to_broadcast([SB, 8, 64]), op=ALU.mult),
                         reads=[d_osall, d_sg], writes=[d_dst_])
                    if r > 0:
                        P.op("dve", lambda e: e.tensor_tensor(out=onsa[:], in0=onsa[:], in1=tmpo[:], op=ALU.add), reads=[d_onsa, d_tmpo], writes=[d_onsa])
                onTs = P.sb([128, 4, SB], BF16, sS); d_onTs = Dep()
                onf = onsa[:].rearrange("p h d -> p (h d)")
                P.dma("sp", lambda e: e.dma_start(out=dbg_ons, in_=onf), reads=[d_onsa])
                for c in range(4):
                    P.op("pe", lambda e, c=c: e.transpose(out=banks[3][:, c * SB:(c + 1) * SB], in_=onf[:, c * 128:(c + 1) * 128], identity=ident[0:SB, 0:SB]),
                         reads=[d_onsa, d_ident], writes=[d_bank[3]])
                P.op("act", lambda e: e.copy(out=onTs[:].rearrange("p a b -> p (a b)"), in_=banks[3][:, 0:4 * SB]), reads=[d_bank[3]], writes=[d_onTs])
                P.dma("sp", lambda e: e.dma_start(out=onTs_d, in_=onTs[:]), reads=[d_onTs])
                P.barrier()

        def phase_b2(groups):
            with ExitStack() as s2:
                Wgab = P.sb([128, 8, 2048], BF16, s2); d_Wgab = Dep()
                Wbn = P.sb([128, 4, 1024], BF16, s2); d_Wbn = Dep()
                Wbr = P.sb([128, 8, 1024], BF16, s2); d_Wbr = Dep()
                Wout = P.sb([128, 8, 1024], BF16, s2); d_Wout = Dep()
                P.dma("pool", lambda e: e.dma_start(out=Wgab[:], in_=w_in[:, C_GA:C_GA + 2048].rearrange("(c p) n -> p c n", p=128)), writes=[d_Wgab])
                P.dma("pool", lambda e: e.dma_start(out=Wbn[:], in_=w_bn.rearrange("(c p) n -> p c n", p=128)), writes=[d_Wbn])
                P.dma("pool", lambda e: e.dma_start(out=Wbr[:], in_=w_br.rearrange("(c p) n -> p c n", p=128)), writes=[d_Wbr])
                P.dma("pool", lambda e: e.dma_start(out=Wout[:], in_=w_out.rearrange("(c p) n -> p c n", p=128)), writes=[d_Wout])
                hT4 = P.sb([128, 8, 512], BF16, s2); d_hT4 = Dep()
                onT4 = P.sb([128, 4, 512], BF16, s2); d_onT4 = Dep()
                yrT4 = P.sb([128, 8, 512], BF16, s2); d_yrT4 = Dep()
                mixT = P.sb([128, 8, 512], BF16, s2); d_mixT = Dep()
                sga = [P.sb([128, 512], F32, s2) for _ in range(2)]; d_sga = [Dep(), Dep()]
                sgb = [P.sb([128, 512], F32, s2) for _ in range(2)]; d_sgb = [Dep(), Dep()]
                t1 = [P.sb([128, 512], F32, s2) for _ in range(2)]; d_t1 = [Dep(), Dep()]
                t2 = [P.sb([128, 512], F32, s2) for _ in range(2)]; d_t2 = [Dep(), Dep()]
                xa = [P.sb([128, D], F32, s2) for _ in range(2)]; d_xa = [Dep(), Dep()]
                x1t = [P.sb([128, D], F32, s2) for _ in range(2)]; d_x1t = [Dep(), Dep()]
                gtaS = P.sb([SB, D], F32, s2); d_gtaS = Dep()
                P.dma("sp", lambda e: e.dma_start(out=gtaS[:], in_=mods_d[:, 2 * D:3 * D]), writes=[d_gtaS])
                for grp in groups:
                    n = grp["ntok"]
                    grp["load_a"](hT4, d_hT4, onT4, d_onT4, yrT4, d_yrT4)
                    for fc in range(8):
                        b = fc % 2
                        for (bk, W, dW, kc, c0, src, dsrc) in ((0, Wgab, d_Wgab, 8, fc * 128, hT4, d_hT4), (1, Wgab, d_Wgab, 8, 1024 + fc * 128, hT4, d_hT4),
                                                              (2, Wbn, d_Wbn, 4, fc * 128, onT4, d_onT4), (3, Wbr, d_Wbr, 8, fc * 128, yrT4, d_yrT4)):
                            for k in range(kc):
                                P.op("pe", lambda e, bk=bk, W=W, k=k, c0=c0, src=src, kc=kc, n=n: e.matmul(
                                    out=banks[bk][:, 0:n], lhsT=W[:, k, c0:c0 + 128], rhs=src[:, k, 0:n], start=(k == 0), stop=(k == kc - 1)),
                                    reads=[dW, dsrc], writes=[d_bank[bk]])
                        P.op("act", lambda e, b=b, n=n: e.activation(out=sga[b][:, 0:n], in_=banks[0][:, 0:n], func=ACTF.Sigmoid), reads=[d_bank[0]], writes=[d_sga[b]])
                        P.op("act", lambda e, b=b, n=n: e.activation(out=sgb[b][:, 0:n], in_=banks[1][:, 0:n], func=ACTF.Sigmoid), reads=[d_bank[1]], writes=[d_sgb[b]])
                        P.op("dve", lambda e, b=b, n=n: e.tensor_tensor(out=t1[b][:, 0:n], in0=banks[2][:, 0:n], in1=sga[b][:, 0:n], op=ALU.mult),
                             reads=[d_bank[2], d_sga[b]], writes=[d_t1[b]])
                        P.op("dve", lambda e, b=b, n=n: e.tensor_tensor(out=t2[b][:, 0:n], in0=banks[3][:, 0:n], in1=sgb[b][:, 0:n], op=ALU.mult),
                             reads=[d_bank[3], d_sgb[b]], writes=[d_t2[b]])
                        P.op("pool", lambda e, b=b, fc=fc, n=n: e.tensor_tensor(out=mixT[:, fc, 0:n], in0=t1[b][:, 0:n], in1=t2[b][:, 0:n], op=ALU.add),
                             reads=[d_t1[b], d_t2[b]], writes=[d_mixT])
                    off = 0
                    for ti, (xsrc, nt_, gta, d_gta, x1dst) in enumerate(grp["tiles_a"](gtaS, d_gtaS)):
                        b = ti % 2
                        P.dma("sp", lambda e, b=b, xsrc=xsrc, nt_=nt_: e.dma_start(out=xa[b][0:nt_, :], in_=xsrc), writes=[d_xa[b]])
                        for half in range(2):
                            bk = 4 + half
                            for k in range(8):
                                P.op("pe", lambda e, bk=bk, k=k, off=off, nt_=nt_, half=half: e.matmul(
                                    out=banks[bk][0:nt_, :], lhsT=mixT[:, k, off:off + nt_], rhs=Wout[:, k, half * 512:(half + 1) * 512],
                                    start=(k == 0), stop=(k == 7)), reads=[d_mixT, d_Wout], writes=[d_bank[bk]])
                            P.op("dve", lambda e, bk=bk, b=b, nt_=nt_, half=half, gta=gta: e.tensor_tensor(
                                out=x1t[b][0:nt_, half * 512:(half + 1) * 512], in0=banks[bk][0:nt_, :], in1=gta[0:nt_, half * 512:(half + 1) * 512], op=ALU.mult),
                                reads=[d_bank[bk], d_gta], writes=[d_x1t[b]])
                        P.op("pool", lambda e, b=b, nt_=nt_: e.tensor_tensor(out=x1t[b][0:nt_, :], in0=x1t[b][0:nt_, :], in1=xa[b][0:nt_, :], op=ALU.add),
                             reads=[d_x1t[b], d_xa[b]], writes=[d_x1t[b]])
                        P.dma("sp", lambda e, b=b, nt_=nt_, x1dst=x1dst: e.dma_start(out=x1dst, in_=x1t[b][0:nt_, :]), reads=[d_x1t[b]])
                        off += nt_
                P.barrier()
            with ExitStack() as s3:
                Wup = P.sb([128, 8, 4096], BF16, s3); d_Wup = Dep()
                Wdn = P.sb([128, 32, 1024], BF16, s3); d_Wdn = Dep()
                for q4 in range(4):
                    P.dma("pool", lambda e, q4=q4: e.dma_start(out=Wup[:, :, q4 * 1024:(q4 + 1) * 1024],
                                                              in_=w_up[:, q4 * 1024:(q4 + 1) * 1024].rearrange("(c p) n -> p c n", p=128)), writes=[d_Wup])
                    P.dma("pool", lambda e, q4=q4: e.dma_start(out=Wdn[:, q4 * 8:(q4 + 1) * 8, :],
                                                              in_=w_down[q4 * 1024:(q4 + 1) * 1024, :].rearrange("(c p) n -> p c n", p=128)), writes=[d_Wdn])
                nfb = P.sb([128, D], F32, s3); d_nfb = Dep()
                P.dma("sp", lambda e: e.dma_start(out=nfb[:], in_=norm_final_d.partition_broadcast(128)[:, 0, :]), writes=[d_nfb])
                h2T4 = P.sb([128, 8, 128], BF16, s3); d_h2T4 = Dep()
                uT = P.sb([128, 32, 128], BF16, s3); d_uT = Dep()
                x1in = [P.sb([128, D], F32, s3) for _ in range(2)]; d_x1in = [Dep(), Dep()]
                rl = [P.sb([128, 128], BF16, s3) for _ in range(2)]; d_rl = [Dep(), Dep()]
                x2t = [P.sb([128, D], F32, s3) for _ in range(2)]; d_x2t = [Dep(), Dep()]
                ss2 = P.sb([128, 2], F32, s3); d_ss2 = Dep()
                wk3 = mk_wk(s3)
                modS3 = P.sb([SB, 3, D], F32, s3); d_modS3 = Dep()
                P.dma("sp", lambda e: e.dma_start(out=modS3[:], in_=mods_d[:, 3 * D:6 * D].rearrange("s (a d) -> s a d", a=3)), writes=[d_modS3])
                P.dma("sp", lambda e: e.dma_start(out=x2t[1][0:SB, :], in_=norm_mlp.partition_broadcast(SB)[:, 0, :]), writes=[d_x2t[1]])
                P.op("dve", lambda e: e.scalar_tensor_tensor(out=modS3[:, 1, :], in0=modS3[:, 1, :], scalar=1.0, in1=x2t[1][0:SB, :], op0=ALU.add, op1=ALU.mult),
                     reads=[d_modS3, d_x2t[1]], writes=[d_modS3])
                for grp in groups:
                    tl_all = grp["tiles_b"](modS3, d_modS3)
                    for s0_ in range(0, len(tl_all), 1):
                        tl_b = tl_all[s0_:s0_ + 1]
                        n = sum(t_[1] for t_ in tl_b)
                        off = 0
                        for ti, (x1src, nt_, G2, d_G2, shf, d_shf, gtf, d_gtf, ydst, shc) in enumerate(tl_b):
                            P.dma("sp", lambda e, ti=ti, x1src=x1src, nt_=nt_: e.dma_start(out=x1in[ti][0:nt_, :], in_=x1src), writes=[d_x1in[ti]])
                            norm_mod_T(x1in[ti], d_x1in[ti], nt_, G2, d_G2, shf, d_shf, wk3, h2T4[:, :, off:off + nt_], d_h2T4, shc, d_shcol)
                            off += nt_
                        for fc in range(32):
                            b = fc % 2
                            bk = fc % 4
                            for k in range(8):
                                P.op("pe", lambda e, bk=bk, k=k, fc=fc, n=n: e.matmul(out=banks[bk][:, 0:n], lhsT=Wup[:, k, fc * 128:(fc + 1) * 128], rhs=h2T4[:, k, 0:n],
                                                                                    start=(k == 0), stop=(k == 7)), reads=[d_Wup, d_h2T4], writes=[d_bank[bk]])
                            P.op("act", lambda e, bk=bk, b=b, n=n: e.activation(out=rl[b][:, 0:n], in_=banks[bk][:, 0:n], func=ACTF.Relu), reads=[d_bank[bk]], writes=[d_rl[b]])
                            P.op("pool", lambda e, b=b, fc=fc, n=n: e.tensor_tensor(out=uT[:, fc, 0:n], in0=rl[b][:, 0:n], in1=rl[b][:, 0:n], op=ALU.mult),
                                 reads=[d_rl[b]], writes=[d_uT])
                        off = 0
                        for ti, (x1src, nt_, G2, d_G2, shf, d_shf, gtf, d_gtf, ydst, shc) in enumerate(tl_b):
                            b = ti % 2
                            for half in range(2):
                                bk = 4 + half
                                for k in range(32):
                                    P.op("pe", lambda e, bk=bk, k=k, off=off, nt_=nt_, half=half: e.matmul(
                                        out=banks[bk][0:nt_, :], lhsT=uT[:, k, off:off + nt_], rhs=Wdn[:, k, half * 512:(half + 1) * 512],
                                        start=(k == 0), stop=(k == 31)), reads=[d_uT, d_Wdn], writes=[d_bank[bk]])
                                P.op("dve", lambda e, bk=bk, b=b, nt_=nt_, half=half, gtf=gtf: e.tensor_tensor(
                                    out=x2t[b][0:nt_, half * 512:(half + 1) * 512], in0=banks[bk][0:nt_, :], in1=gtf[0:nt_, half * 512:(half + 1) * 512], op=ALU.mult),
                                    reads=[d_bank[bk], d_gtf], writes=[d_x2t[b]])
                            P.op("pool", lambda e, b=b, ti=ti, nt_=nt_: e.tensor_tensor(out=x2t[b][0:nt_, :], in0=x2t[b][0:nt_, :], in1=x1in[ti][0:nt_, :], op=ALU.add),
                                 reads=[d_x2t[b], d_x1in[ti]], writes=[d_x2t[b]])
                            P.op("act", lambda e, b=b, nt_=nt_: e.activation(out=junk_sh[0:nt_, :], in_=x2t[b][0:nt_, :], func=ACTF.Square, accum_out=ss2[0:nt_, 0:1]),
                                 reads=[d_x2t[b]], writes=[d_junk_sh, d_ss2])
                            P.op("act", lambda e, nt_=nt_: e.activation(out=ss2[0:nt_, 1:2], in_=ss2[0:nt_, 0:1], func=ACTF.Sqrt, bias=epsT[0:nt_, :], scale=1.0 / D),
                                 reads=[d_ss2, d_eps], writes=[d_ss2])
                            P.op("dve", lambda e, nt_=nt_: e.reciprocal(out=ss2[0:nt_, 1:2], in_=ss2[0:nt_, 1:2]), reads=[d_ss2], writes=[d_ss2])
                            P.op("dve", lambda e, b=b, nt_=nt_: e.scalar_tensor_tensor(out=x2t[b][0:nt_, :], in0=x2t[b][0:nt_, :], scalar=ss2[0:nt_, 1:2], in1=nfb[0:nt_, :],
                                                                                     op0=ALU.mult, op1=ALU.mult), reads=[d_x2t[b], d_ss2, d_nfb], writes=[d_x2t[b]])
                            P.dma("sp", lambda e, b=b, nt_=nt_, ydst=ydst: e.dma_start(out=ydst, in_=x2t[b][0:nt_, :]), reads=[d_x2t[b]])
                            off += nt_
                P.barrier()

        with ExitStack() as sA:
            WA = P.sb([128, 8, 2304], BF16, sA); d_WA = Dep()
            segs = [(C_KVC, 128, 0), (C_KVS, 128, 128), (C_KVW, 128, 256), (C_KVC + 128, 128, 384),
                    (C_KVS + 128, 128, 512), (C_KVW + 128, 128, 640), (C_RK, 512, 768), (C_RV, 1024, 1280)]
            for (src, n, dst) in segs:
                P.dma("pool", lambda e, src=src, n=n, dst=dst: e.dma_start(
                    out=WA[:, :, dst:dst + n], in_=w_in[:, src:src + n].rearrange("(c p) n -> p c n", p=128)), writes=[d_WA])
            inda = P.sb([128, 8], F32, sA); indab = P.sb([128, 8], BF16, sA); d_inda = Dep()
            wrep = P.sb([128, 512], F32, sA); d_wrep = Dep()
            bcol = P.sb([128, 2], F32, sA); d_bcol = Dep()
            kdsc = P.sb([128, 4], F32, sA); d_kdsc = Dep()
            oh = P.sb([128, 8], F32, sA); d_oh = Dep()
            P.dma("sp", lambda e: e.dma_start(out=inda[:], in_=inda_d), writes=[d_inda])
            P.op("dve", lambda e: e.tensor_copy(out=indab[:], in_=inda[:]), reads=[d_inda], writes=[d_inda])
            P.dma("sp", lambda e: e.dma_start(out=wrep[:], in_=wrep_d), writes=[d_wrep])
            P.dma("sp", lambda e: e.dma_start(out=bcol[:], in_=bcol_d), writes=[d_bcol])
            P.dma("sp", lambda e: e.dma_start(out=kdsc[:], in_=kdsc_d), writes=[d_kdsc])
            P.dma("sp", lambda e: e.dma_start(out=oh[:], in_=oh_d), writes=[d_oh])
            FS = P.sb([128, 4, 1024], F32, sA); d_FS = Dep()
            S = P.sb([128, 4, 256], F32, sA); d_S = Dep()
            Sown = P.sb([128, 1024], F32, sA); d_Sown = Dep()
            tmpS = P.sb([128, 1024], F32, sA); d_tmpS = Dep()
            P.op("pool", lambda e: e.memset(S[:].rearrange("p a b -> p (a b)"), 0.0), writes=[d_S])
            NBUF = 2
            xts = [P.sb([128, D], F32, sA) for _ in range(NBUF)]; d_xts = [Dep() for _ in range(NBUF)]
            rps = [P.sb([128, 192], F32, sA) for _ in range(NBUF)]; d_rps = [Dep() for _ in range(NBUF)]
            wks = [mk_wk(sA) for _ in range(NBUF)]
            hTs = [P.sb([128, 8, 128], BF16, sA) for _ in range(NBUF)]; d_hTs = [Dep() for _ in range(NBUF)]
            kvos = [P.sb([128, 3, 256], F32, sA) for _ in range(NBUF)]; d_kvos = [Dep() for _ in range(NBUF)]
            kraw = [P.sb([128, 896], F32, sA) for _ in range(NBUF)]; d_kraw = [Dep() for _ in range(NBUF)]
            tm = [[P.sb([128, 256], F32, sA) for _ in range(4)] for _ in range(NBUF)]
            d_tm = [[Dep() for _ in range(4)] for _ in range(NBUF)]
            rkr = [P.sb([128, 4, 128], F32, sA) for _ in range(NBUF)]; d_rkr = [Dep() for _ in range(NBUF)]
            kd = [P.sb([128, 4, 128], BF16, sA) for _ in range(NBUF)]; d_kd = [Dep() for _ in range(NBUF)]
            vb = [P.sb([128, 1024], BF16, sA) for _ in range(NBUF)]; d_vb = [Dep() for _ in range(NBUF)]
            cw = [P.sb([128, 2, 256], BF16, sA) for _ in range(NBUF)]; d_cw = [Dep() for _ in range(NBUF)]
            kwt = [P.sb([128, 128], BF16, sA) for _ in range(NBUF)]; d_kwt = [Dep() for _ in range(NBUF)]
            kst = [P.sb([128, 128], BF16, sA) for _ in range(NBUF)]; d_kst = [Dep() for _ in range(NBUF)]
            vwb = [P.sb([128, 2, 65], BF16, sA) for _ in range(NBUF)]; d_vwb = [Dep() for _ in range(NBUF)]
            vsb = [P.sb([128, 2, 65], BF16, sA) for _ in range(NBUF)]; d_vsb = [Dep() for _ in range(NBUF)]
            for b_ in range(NBUF):
                P.op("pool", lambda e, b_=b_: e.memset(vwb[b_][:].rearrange("p a b -> p (a b)"), 1.0), writes=[d_vwb[b_]])
                P.op("pool", lambda e, b_=b_: e.memset(vsb[b_][:].rearrange("p a b -> p (a b)"), 1.0), writes=[d_vsb[b_]])
            zA, zB, zK, zV0, zV1, zC = banks
            dzA, dzB, dzK, dzV0, dzV1, dzC = d_bank
            d_zBt = dzB

            ntA = NT if stage >= 1 else 0
            if 'K_NTA' in os.environ:
                ntA = int(os.environ['K_NTA'])
            if stage == 2:
                ntA = 8 * DBG_NBLK
                P.op("pool", lambda e: e.memset(FS[:].rearrange("p a b -> p (a b)"), 0.0), writes=[d_FS])
            for tt in range(ntA):
                b = tt % NBUF
                xt, d_xt, rp, d_rp, hT, d_hT, kvo, d_kvo = xts[b], d_xts[b], rps[b], d_rps[b], hTs[b], d_hTs[b], kvos[b], d_kvos[b]
                P.dma("sp", lambda e, tt=tt, xt=xt: e.dma_start(out=xt[:], in_=x_all[tt * 128:(tt + 1) * 128, :]), writes=[d_xt])
                P.dma("sp", lambda e, tt=tt, rp=rp: e.dma_start(out=rp[:], in_=rope_all[tt * 128:(tt + 1) * 128, :]), writes=[d_rp])
                norm_mod_T(xt, d_xt, 128, G1P, d_G1P, modP4[:, 0, :], d_modP, wks[b], hT, d_hT, shcol[:, 0, :], d_shcol)
                for (zz, dz, c0, n) in ((zA, dzA, 0, 512), (zB, dzB, 512, 256), (zK, dzK, 768, 512), (zV0, dzV0, 1280, 512), (zV1, dzV1, 1792, 512)):
                    for k in range(8):
                        P.op("pe", lambda e, zz=zz, k=k, c0=c0, n=n, hT=hT: e.matmul(out=zz[:, 0:n], lhsT=hT[:, k, :], rhs=WA[:, k, c0:c0 + n],
                                                                                    start=(k == 0), stop=(k == 7)),
                             reads=[d_hT, d_WA], writes=[dz])
                P.op("act", lambda e, b=b: e.copy(out=kraw[b][:, 0:384], in_=zA[:, 0:384]), reads=[dzA], writes=[d_kraw[b]])
                P.op("act", lambda e, b=b: e.copy(out=kraw[b][:, 384:896], in_=zK[:, 0:512]), reads=[dzK], writes=[d_kraw[b]])
                src = kraw[b][:, 0:384].rearrange("p (j h two e) -> p j h two e", j=3, h=2, two=2)
                dst = kvo[:, :, 0:128].rearrange("p j (h two e) -> p j h two e", h=2, two=2)
                shp = [128, 3, 2, 32]
                tv = [t[:, 0:192].rearrange("p (j h e) -> p j h e", j=3, h=2) for t in tm[b]]
                cosn = rp[:, 0:32].unsqueeze(1).unsqueeze(1).to_broadcast(shp)
                sinn = rp[:, 32:64].unsqueeze(1).unsqueeze(1).to_broadcast(shp)
                rope(lambda hf, src=src: src[:, :, :, hf, :], lambda hf, dst=dst: dst[:, :, :, hf, :], cosn, sinn, shp, tv,
                     d_kraw[b], d_kvo, d_rp, d_tm[b])
                P.op("act", lambda e, kvo=kvo: e.copy(out=kvo[:, 0, 128:256], in_=zA[:, 384:512]), reads=[dzA], writes=[d_kvo])
                P.op("act", lambda e, kvo=kvo: e.copy(out=kvo[:, 1:3, 128:256], in_=zB[:, 0:256].rearrange("p (j f) -> p j f", j=2)),
                     reads=[dzB], writes=[d_kvo])
                P.dma("sp", lambda e, tt=tt, kvo=kvo: e.dma_start(out=o_cmp[tt * 128:(tt + 1) * 128, :], in_=kvo[:, 0, :]), reads=[d_kvo])
                P.dma("sp", lambda e, tt=tt, kvo=kvo: e.dma_start(out=o_sel[tt * 128:(tt + 1) * 128, :], in_=kvo[:, 1, :]), reads=[d_kvo])
                if tt >= NT - 4:
                    P.dma("sp", lambda e, tt=tt, kvo=kvo: e.dma_start(out=o_win[(tt - NT + 4) * 128:(tt - NT + 5) * 128, :], in_=kvo[:, 2, :]),
                          reads=[d_kvo])
                P.op("pool", lambda e, kvo=kvo, b=b: e.tensor_copy(out=vsb[b][:, :, 0:64], in_=kvo[:, 1, 128:256].rearrange("p (h d) -> p h d", h=2)),
                     reads=[d_kvo], writes=[d_vsb[b]])
                P.dma("sp", lambda e, tt=tt, b=b: e.dma_start(out=vs_d[tt * 128:(tt + 1) * 128, :], in_=vsb[b][:].rearrange("p a b -> p (a b)")),
                      reads=[d_vsb[b]])
                P.op("pool", lambda e, kvo=kvo, b=b: e.tensor_copy(out=vwb[b][:, :, 0:64], in_=kvo[:, 2, 128:256].rearrange("p (h d) -> p h d", h=2)),
                     reads=[d_kvo], writes=[d_vwb[b]])
                P.dma("sp", lambda e, tt=tt, b=b: e.dma_start(out=vw_d[tt * 128:(tt + 1) * 128, :], in_=vwb[b][:].rearrange("p a b -> p (a b)")),
                      reads=[d_vwb[b]])
                P.op("pe", lambda e, kvo=kvo: e.transpose(out=zB[:, 256:384], in_=kvo[:, 1, 0:128], identity=ident[:]),
                     reads=[d_kvo, d_ident], writes=[d_zBt])
                P.op("pe", lambda e, kvo=kvo: e.transpose(out=zB[:, 384:512], in_=kvo[:, 2, 0:128], identity=ident[:]),
                     reads=[d_kvo, d_ident], writes=[d_zBt])
                P.op("act", lambda e, b=b: e.copy(out=kst[b][:], in_=zB[:, 256:384]), reads=[d_zBt], writes=[d_kst[b]])
                P.dma("sp", lambda e, tt=tt, b=b: e.dma_start(out=ksT_d[:, tt * 128:(tt + 1) * 128], in_=kst[b][:]), reads=[d_kst[b]])
                P.op("act", lambda e, b=b: e.copy(out=kwt[b][:], in_=zB[:, 384:512]), reads=[d_zBt], writes=[d_kwt[b]])
                P.dma("sp", lambda e, tt=tt, b=b: e.dma_start(out=kwT_d[:, tt * 128:(tt + 1) * 128], in_=kwt[b][:]), reads=[d_kwt[b]])
                for r in range(2):
                    P.op("pool", lambda e, r=r, b=b, kvo=kvo: e.tensor_tensor(out=cw[b][:, r, :], in0=kvo[:, 0, :], in1=wrep[:, r * 256:(r + 1) * 256],
                                                                             op=ALU.mult), reads=[d_kvo, d_wrep], writes=[d_cw[b]])
                for r in range(2):
                    for kv in range(2):
                        i4 = r * 2 + kv
                        P.op("pe", lambda e, r=r, kv=kv, i4=i4, b=b: e.matmul(out=zC[:, i4 * 8:(i4 + 1) * 8], lhsT=cw[b][:, r, kv * 128:(kv + 1) * 128],
                                                                             rhs=indab[:], start=True, stop=True),
                             reads=[d_cw[b], d_inda], writes=[dzC])
                P.op("act", lambda e, tt=tt: e.copy(out=FS[:, :, tt * 8:(tt + 1) * 8], in_=zC[:, 0:32].rearrange("p (a s) -> p a s", a=4)),
                     reads=[dzC], writes=[d_FS])
                srck = kraw[b][:, 384:896].rearrange("p (h two e) -> p h two e", h=4, two=2)
                dstk = rkr[b][:].rearrange("p h (two e) -> p h two e", two=2)
                shpk = [128, 4, 64]
                tvk = [t[:, 0:256].rearrange("p (h e) -> p h e", h=4) for t in tm[b]]
                cosk = rp[:, 64:128].unsqueeze(1).to_broadcast(shpk)
                sink = rp[:, 128:192].unsqueeze(1).to_broadcast(shpk)
                rope(lambda hf, srck=srck: srck[:, :, hf, :], lambda hf, dstk=dstk: dstk[:, :, hf, :], cosk, sink, shpk, tvk,
                     d_kraw[b], d_rkr[b], d_rp, d_tm[b])
                P.op("pool", lambda e, b=b: e.tensor_tensor(out=kd[b][:], in0=rkr[b][:], in1=kdsc[:].unsqueeze(2).to_broadcast([128, 4, 128]),
                                                           op=ALU.mult), reads=[d_rkr[b], d_kdsc], writes=[d_kd[b]])
                P.op("act", lambda e, b=b: e.copy(out=vb[b][:, 0:512], in_=zV0[:, :]), reads=[dzV0], writes=[d_vb[b]])
                P.op("act", lambda e, b=b: e.copy(out=vb[b][:, 512:1024], in_=zV1[:, :]), reads=[dzV1], writes=[d_vb[b]])
                j = tt % 8
                if j == 0:
                    P.op("dve", lambda e: e.tensor_scalar(out=Sown[:], in0=S[:].rearrange("p a b -> p (a b)"), scalar1=oh[:, 0:1], scalar2=None,
                                                          op0=ALU.mult), reads=[d_S, d_oh], writes=[d_Sown])
                else:
                    P.op("dve", lambda e, j=j: e.scalar_tensor_tensor(out=Sown[:], in0=S[:].rearrange("p a b -> p (a b)"), scalar=oh[:, j:j + 1],
                                                                     in1=Sown[:], op0=ALU.mult, op1=ALU.add),
                         reads=[d_S, d_oh, d_Sown], writes=[d_Sown])
                if j == 7:
                    P.dma("sp", lambda e, i=tt // 8: e.dma_start(out=sown_d[i], in_=Sown[:]), reads=[d_Sown])
                for h in range(4):
                    zz, dz = (zV0, dzV0) if h < 2 else (zV1, dzV1)
                    c0 = (h % 2) * 256
                    P.op("pe", lambda e, h=h, zz=zz, c0=c0, b=b: e.matmul(out=zz[:, c0:c0 + 256], lhsT=kd[b][:, h, :], rhs=vb[b][:, h * 256:(h + 1) * 256],
                                                                         start=True, stop=True),
                         reads=[d_kd[b], d_vb[b]], writes=[dz])
                    P.op("dve", lambda e, h=h, zz=zz, c0=c0: e.scalar_tensor_tensor(out=S[:, h, :], in0=S[:, h, :], scalar=float(math.exp(128 * LG[h])),
                                                                                   in1=zz[:, c0:c0 + 256], op0=ALU.mult, op1=ALU.add),
                         reads=[d_S, dz], writes=[d_S])
            if ntA:
                P.dma("sp", lambda e: e.dma_start(out=o_state.rearrange("h k v -> k h v"), in_=S[:]), reads=[d_S])
                ctmp, cv = tmpS, Sown
                P.op("pool", lambda e: e.memset(cmpKT[:], 0.0), writes=[d_cmpKT])
                P.op("pool", lambda e: e.memset(cv[:], 0.0), reads=[], writes=[d_Sown])
                P.op("dve", lambda e: e.tensor_tensor(out=ctmp[:, 0:1023], in0=FS[:, 0, 0:1023], in1=FS[:, 2, 1:1024], op=ALU.add),
                     reads=[d_FS], writes=[d_tmpS])
                P.op("dve", lambda e: e.tensor_scalar(out=cmpKT[:, 0:1023], in0=ctmp[:, 0:1023], scalar1=bcol[:, 0:1], scalar2=None, op0=ALU.add),
                     reads=[d_tmpS, d_bcol], writes=[d_cmpKT])
                P.op("dve", lambda e: e.tensor_tensor(out=ctmp[:, 0:1023], in0=FS[:, 1, 0:1023], in1=FS[:, 3, 1:1024], op=ALU.add),
                     reads=[d_FS], writes=[d_tmpS])
                P.op("dve", lambda e: e.tensor_scalar(out=cv[:, 0:1023], in0=ctmp[:, 0:1023], scalar1=bcol[:, 1:2], scalar2=None, op0=ALU.add),
                     reads=[d_tmpS, d_bcol], writes=[d_Sown])
                for jt in range(8):
                    P.op("pe", lambda e, jt=jt: e.transpose(out=pT[:, jt * 128:(jt + 1) * 128], in_=cv[:, jt * 128:(jt + 1) * 128], identity=ident[:]),
                         reads=[d_Sown, d_ident], writes=[d_pTs[jt // 4]])
                P.op("act", lambda e: e.copy(out=cmpVA[:, :, :, 0:64], in_=pT[:].rearrange("p (j h d) -> p j h d", j=8, h=2)),
                     reads=d_pTs, writes=[d_cmpVA])
            P.barrier()

        if stage >= 2:
            phase_b1()
        if stage >= 4:
            phase_s1()
        if stage >= 3:
            groups = []
            for gq in range(4):
                def load_a(hT4, d_hT4, onT4, d_onT4, yrT4, d_yrT4, gq=gq):
                    for j in range(4):
                        i_ = 4 * gq + j
                        P.dma("sp", lambda e, i_=i_, j=j: e.dma_start(out=hT4[:, :, j * 128:(j + 1) * 128], in_=hT_d[i_]), writes=[d_hT4])
                        P.dma("sp", lambda e, i_=i_, j=j: e.dma_start(out=onT4[:, :, j * 128:(j + 1) * 128], in_=onT_d[i_]), writes=[d_onT4])
                        P.dma("sp", lambda e, i_=i_, j=j: e.dma_start(out=yrT4[:, :, j * 128:(j + 1) * 128], in_=yrT_d[i_]), writes=[d_yrT4])

                def tiles_a(gtaS, d_gtaS, gq=gq):
                    return [(x_own[(4 * gq + j) * 128:(4 * gq + j + 1) * 128, :], 128, modP4[:, 1, :], d_modP,
                             x1_d[(4 * gq + j) * 128:(4 * gq + j + 1) * 128, :]) for j in range(4)]

                def tiles_b(modS3, d_modS3, gq=gq):
                    return [(x1_d[(4 * gq + j) * 128:(4 * gq + j + 1) * 128, :], 128, G2P, d_G2P, modP4[:, 2, :], d_modP, modP4[:, 3, :], d_modP,
                             y_own[(4 * gq + j) * 128:(4 * gq + j + 1) * 128, :], shcol[:, 1, :]) for j in range(4)]
                groups.append({"ntok": 512, "load_a": load_a, "tiles_a": tiles_a, "tiles_b": tiles_b})
            if stage >= 4:
                def load_as(hT4, d_hT4, onT4, d_onT4, yrT4, d_yrT4):
                    P.dma("sp", lambda e: e.dma_start(out=hT4[:, :, 0:SB], in_=hTs_d), writes=[d_hT4])
                    P.dma("sp", lambda e: e.dma_start(out=onT4[:, :, 0:SB], in_=onTs_d), writes=[d_onT4])
                    P.dma("sp", lambda e: e.dma_start(out=yrT4[:, :, 0:SB], in_=yrTs_d), writes=[d_yrT4])
                groups.append({"ntok": SB, "load_a": load_as,
                               "tiles_a": lambda gtaS, d_gtaS: [(x_smp, SB, gtaS[:], d_gtaS, x1s_d)],
                               "tiles_b": lambda modS3, d_modS3: [(x1s_d, SB, modS3[:, 1, :], d_modS3, modS3[:, 0, :], d_modS3, modS3[:, 2, :], d_modS3, y_smp, None)]})
            phase_b2(groups)

        P.barrier()
        P.emit()
    return nc


def _rope_tables(pos):
    pos = np.asarray(pos, np.float32)
    out = np.zeros((len(pos), 192), np.float32)
    inv32 = (10000.0 ** (-np.arange(0, 64, 2, dtype=np.float32) / 64)).astype(np.float32)
    inv64 = (10000.0 ** (-np.arange(0, 128, 2, dtype=np.float32) / 128)).astype(np.float32)
    a32 = pos[:, None] * inv32[None, :]
    a64 = pos[:, None] * inv64[None, :]
    out[:, 0:32] = np.cos(a32)
    out[:, 32:64] = np.sin(a32)
    out[:, 64:128] = np.cos(a64)
    out[:, 128:192] = np.sin(a64)
    return out


def _consts(core):
    cs = {}
    cs["ident"] = np.eye(128, dtype=np.float32)
    sel5 = np.zeros((5, 132), np.float32)
    sel5[0, 0:128] = 1.0
    for s in range(4):
        sel5[1 + s, 128 + s] = 1.0
    cs["sel5"] = sel5
    inda = np.zeros((128, 8), np.float32)
    inda[np.arange(128), np.arange(128) // 16] = 1.0
    cs["inda"] = inda
    i = np.arange(128, dtype=np.float64)
    kdsc = np.zeros((128, 4), np.float32)
    for h in range(4):
        kdsc[:, h] = RET_SCALE * np.exp((127.0 - i) * LG[h])
    cs["kdsc"] = kdsc
    oh = np.zeros((128, 8), np.float32)
    oh[:, core] = 1.0
    cs["oh"] = oh
    return cs


def _core_tables(c):
    q = np.arange(128)
    n = np.arange(1024)
    j = np.arange(256)
    k = np.arange(128)
    cmask = np.zeros((NB, 128, 1024), np.float32)
    forceb = np.zeros((NB, 128, 256), np.float32)
    for i in range(NB):
        t = 8 * i + c
        qpos = 128 * t + q
        ok = (16 * n[None, :] + 31 <= qpos[:, None]) & (n[None, :] <= 1022)
        cmask[i] = np.where(ok, 0.0, NEG)
        valid = 64 * j[None, :] <= qpos[:, None]
        qb = qpos[:, None] // 64
        forced = (j[None, :] == 0) | (j[None, :] == qb) | (j[None, :] == qb - 1)
        forceb[i] = np.where(valid, np.where(forced, 1.0e4, 0.0), -1.0e30)
    dmask = np.zeros((128, 8, 128), np.float32)
    for jj in range(8):
        if jj == c:
            dmask[:, jj, :] = np.where(k[None, :] <= q[:, None], 0.0, NEG)
        elif jj > c:
            dmask[:, jj, :] = NEG
    wmask = np.zeros((128, 24, 128), np.float32)
    for v in range(2):
        for m in range(12):
            diff = 128 * (m - 4 - c) + k[None, :] - q[:, None]
            ok = (diff <= 0) & (diff > -512)
            if v == 0 and m < 4:
                ok = np.zeros_like(ok)
            wmask[:, v * 12 + m, :] = np.where(ok, 0.0, NEG)
    return {"cmask": cmask, "forceb": forceb, "dmask": dmask.reshape(128, 1024), "wmask": wmask.reshape(128, 24 * 128)}


def _shared_tables():
    nl = np.arange(128)
    blk = np.arange(256)
    ovp = np.zeros((128, 8, 256), np.float32)
    for jt in range(8):
        n = 128 * jt + nl
        ovp[:, jt, :] = ((4 * blk[None, :] - 1 <= n[:, None]) & (n[:, None] <= 4 * blk[None, :] + 3)).astype(np.float32)
    i = np.arange(128)
    dtab = np.zeros((128, 4, 128), np.float32)
    qdt = np.zeros((128, 4, 128), np.float32)
    for h in range(4):
        diff = i[None, :] - i[:, None]
        dtab[:, h, :] = np.where(diff >= 0, np.exp(np.maximum(diff, 0) * LG[h]), 0.0)
        qdt[:, h, :] = np.exp((i[None, :] + 1.0) * LG[h])
    p = np.arange(128)
    ovs = np.zeros((128, 8, 257), np.float32)
    for s_ in range(8):
        n = 8 * p + s_
        ovs[:, s_, 0:256] = ((4 * blk[None, :] - 1 <= n[:, None]) & (n[:, None] <= 4 * blk[None, :] + 3)).astype(np.float32)
    ovs[:, :, 256] = 1.0
    smallc = np.zeros((1, 1024), np.float32)
    smallc[0, 127] = NEG
    smallc[0, 128] = NEG
    for b in range(4):
        for s_ in range(4):
            smallc[0, 256 + 4 * b + s_] = 0.0 if s_ == b else NEG
    smallc[0, 272:276] = 1.0
    smallc[0, 276 + 0] = 1.0e4
    smallc[0, 276 + 255] = 1.0e4
    ohm = np.zeros((128, 4, 4), np.float32)
    for b in range(4):
        ohm[:, b, b] = 1.0
    return {"ovp": ovp.reshape(128, 2048), "dtab": dtab.reshape(128, 512), "qdt": qdt.reshape(128, 512),
            "ovs": ovs.reshape(128, 8 * 257), "smallc": smallc, "ohm": ohm.reshape(128, 16), "oh4": np.eye(4, dtype=np.float32),
            "rope_smp": _rope_tables(np.full(4, P_PAST))}


def _prep(inputs, stage=99):
    f = lambda a: np.ascontiguousarray(np.asarray(a, dtype=np.float32))
    x_all = f(inputs["x_prompt"]).reshape(T, D)
    w_cmp = f(inputs["w_cmp"])[0]
    b_cmp = f(inputs["b_cmp"])[0]
    wrep = np.zeros((128, 512), np.float32)
    pm = np.arange(128) % 16
    for r in range(2):
        blk = w_cmp[:, :, r * 16 + pm, :]
        wrep[:, r * 256:(r + 1) * 256] = blk.transpose(2, 0, 1, 3).reshape(128, 256)
    bcol = np.ascontiguousarray(b_cmp.reshape(2, 128).T)
    rope_all = _rope_tables(np.arange(T))
    shared = {
        "x_all": x_all, "rope_all": rope_all,
        "w_ada": f(inputs["w_ada"])[0], "b_ada": f(inputs["b_ada"]).reshape(1, 6 * D),
        "w_in": f(inputs["w_in"])[0],
        "norm_mix": f(inputs["norm_mix"]).reshape(1, D), "norm_mlp": f(inputs["norm_mlp"]).reshape(1, D),
        "wrep": wrep, "bcol": bcol,
        "w_bn": f(inputs["w_branch_nsa"])[0], "w_br": f(inputs["w_branch_ret"])[0], "w_out": f(inputs["w_out"])[0],
        "w_up": f(inputs["w_up"])[0], "w_down": f(inputs["w_down"])[0], "norm_final": f(inputs["norm_final"]).reshape(1, D),
        "gnw": f(inputs["ret_gn_w"]).reshape(1, D), "gnb": f(inputs["ret_gn_b"]).reshape(1, D),
    }
    shared.update(_shared_tables())
    shared["wrow"] = np.ascontiguousarray(w_cmp.reshape(2, 2, 2, 16, 64).transpose(2, 3, 0, 1, 4).reshape(1, 2 * 16 * 256))
    shared["brow"] = np.ascontiguousarray(b_cmp.reshape(1, 256))
    shared["cache_cmp"] = f(inputs["cache_cmp_kv"]).reshape(5120 * 8, 4096)
    shared["cache_sel"] = f(inputs["cache_sel_kv"]).reshape(5120 * 8, 4096)
    ptab_all = np.ascontiguousarray(np.asarray(inputs["page_table"], dtype=np.int32))
    cwin = f(inputs["cache_win_kv"])[0].reshape(32, 512, 256)
    sret = f(inputs["state_ret"])[0]
    maps = []
    cp = f(inputs["c_prompt"]).reshape(1, D)
    cs_ = f(inputs["c_sample"])
    for c in range(NCORE):
        m = dict(shared)
        m.update(_consts(c))
        m.update(_core_tables(c))
        tiles = [8 * i + c for i in range(NB)]
        m["x_own"] = np.ascontiguousarray(x_all.reshape(NT, 128, D)[tiles].reshape(NB * 128, D))
        m["rope_own"] = np.ascontiguousarray(rope_all.reshape(NT, 128, 192)[tiles].reshape(NB * 128, 192))
        m["ptab"] = ptab_all[4 * c:4 * c + 4]
        m["cache_win"] = cwin[4 * c:4 * c + 4]
        m["state_s"] = sret[4 * c:4 * c + 4]
        m["x_smp"] = f(inputs["x_sample"]).reshape(32, D)[4 * c:4 * c + 4]
        m["c_all"] = np.ascontiguousarray(np.concatenate([cp, cs_[4 * c:4 * c + 4]], axis=0).reshape(5, 8, 128).transpose(2, 1, 0).reshape(128, 40))
        maps.append(m)
    return maps


_NC_CACHE = {}


def run(inputs, stage=99):
    if stage not in _NC_CACHE:
        _NC_CACHE[stage] = build(stage)
    nc = _NC_CACHE[stage]
    maps = _prep(inputs, stage)
    res = run_bass_kernel_spmd(nc, maps, core_ids=list(range(NCORE)))
    return res.results


def kernel(**inputs):
    r = run(inputs)
    r0 = r[0]
    y = np.zeros((NT, 128, D), np.float32)
    for c in range(NCORE):
        yo = np.asarray(r[c]["y_own"]).reshape(NB, 128, D)
        for i in range(NB):
            y[8 * i + c] = yo[i]
    y_prompt = y.reshape(1, T, D)
    y_sample = np.concatenate([np.asarray(r[c]["y_smp"]) for c in range(NCORE)], axis=0).reshape(32, 1, D)
    new_cmp_p = np.asarray(r0["o_cmp"]).reshape(1, 1, T, 2, 2, 64)
    new_sel_p = np.asarray(r0["o_sel"]).reshape(1, 1, T, 2, 2, 64)
    new_win_p = np.asarray(r0["o_win"]).reshape(1, 1, 512, 2, 2, 64)
    new_state_p = np.asarray(r0["o_state"]).reshape(1, 1, 4, 128, 256)
    cat = lambda k: np.concatenate([np.asarray(r[c][k]) for c in range(NCORE)], axis=0)
    new_cmp_s = cat("o_cmp_s").reshape(1, 32, 1, 2, 2, 64)
    new_sel_s = cat("o_sel_s").reshape(1, 32, 1, 2, 2, 64)
    new_win_s = cat("o_win_s").reshape(1, 32, 512, 2, 2, 64)
    new_state_s = cat("o_state_s").reshape(1, 32, 4, 128, 256)
    return (y_prompt, y_sample, new_cmp_p, new_sel_p, new_win_p, new_state_p, new_cmp_s, new_sel_s, new_win_s, new_state_s)
```

```python
import math
import os
from contextlib import ExitStack

import numpy as np
import ml_dtypes

import concourse.bass as bass
import concourse.mybir as mybir
from concourse.bass_utils import run_bass_kernel_spmd

F32 = mybir.dt.float32
BF16 = mybir.dt.bfloat16
I32 = mybir.dt.int32
ALU = mybir.AluOpType
ACTF = mybir.ActivationFunctionType
AX = mybir.AxisListType

ENGS = ("pe", "act", "dve", "pool", "sp")
NSLOT = 12

D = 1024
T = 16384
NT = T // 128
NB = 16
NCORE = 8
SB = 4
P_PAST = 16384
W_IN = 6424
C_Q, C_KVC, C_KVS, C_KVW, C_G, C_RQ, C_RK, C_RV, C_RG, C_GA, C_GB = (
    0, 512, 768, 1024, 1280, 1304, 1816, 2328, 3352, 4376, 5400)
NEG = -30000.0
RMS_EPS = 1e-6
GN_EPS = 1e-5
LG = [math.log1p(-2.0 ** (-5.0 - h)) for h in range(4)]
RET_SCALE = 128 ** -0.5


class Dep:
    __slots__ = ("w", "r", "excl")

    def __init__(self, excl=False):
        self.w = None
        self.r = []
        self.excl = excl


class Prog:
    def __init__(self, nc, stack):
        self.nc = nc
        self.stack = stack
        self.ops = {e: [] for e in ENGS}
        self.cnt = {e: 0 for e in ENGS}
        self.esem = {e: stack.enter_context(nc.semaphore("es_" + e)) for e in ENGS}
        self.dq = {}
        for q in ("sp", "act", "pool"):
            sl = [stack.enter_context(nc.semaphore("ds_%s_%d" % (q, i))) for i in range(NSLOT)]
            self.dq[q] = {"sems": sl, "cnt": [0] * NSLOT, "next": 0}
        self.waited = {e: {} for e in ENGS}
        self.nuid = 0

    def sb(self, shape, dt=F32, stack=None):
        self.nuid += 1
        return (stack or self.stack).enter_context(self.nc.sbuf_tensor("sb%d" % self.nuid, list(shape), dt))

    def ps(self, shape, dt=F32, stack=None):
        self.nuid += 1
        return (stack or self.stack).enter_context(self.nc.psum_tensor("ps%d" % self.nuid, list(shape), dt))

    def _need(self, eng, tok, waits):
        if tok is None:
            return
        kind, key, val = tok
        if kind == "e" and key == eng and eng == "pe":
            return
        k = (kind, key)
        if self.waited[eng].get(k, 0) >= val:
            return
        self.waited[eng][k] = val
        waits[k] = max(waits.get(k, 0), val)

    def _gather(self, eng, reads, writes):
        waits = {}
        for d in reads:
            self._need(eng, d.w, waits)
            if d.excl:
                for t in d.r:
                    if not (t[0] == "e" and t[1] == eng):
                        self._need(eng, t, waits)
        for d in writes:
            self._need(eng, d.w, waits)
            for t in d.r:
                self._need(eng, t, waits)
        return waits

    def _commit(self, tok, reads, writes):
        for d in reads:
            if tok[0] == "e":
                d.r = [t for t in d.r if not (t[0] == "e" and t[1] == tok[1])]
            d.r.append(tok)
        for d in writes:
            d.w = tok
            d.r = []

    def _semof(self, k):
        kind, key = k
        if kind == "e":
            return self.esem[key]
        q, i = key
        return self.dq[q]["sems"][i]

    def op(self, eng, fn, reads=(), writes=()):
        waits = self._gather(eng, reads, writes)
        self.cnt[eng] += 1
        tok = ("e", eng, self.cnt[eng])
        self.ops[eng].append(([(self._semof(k), v) for k, v in waits.items()], fn, (self.esem[eng], 1)))
        self._commit(tok, reads, writes)
        return tok

    def dma(self, q, fn, reads=(), writes=()):
        st = self.dq[q]
        i = st["next"]
        st["next"] = (i + 1) % NSLOT
        waits = self._gather(q, reads, writes)
        if st["cnt"][i] > 0:
            self._need(q, ("d", (q, i), st["cnt"][i]), waits)
        st["cnt"][i] += 16
        tok = ("d", (q, i), st["cnt"][i])
        self.ops[q].append(([(self._semof(k), v) for k, v in waits.items()], fn, (st["sems"][i], 16)))
        self._commit(tok, reads, writes)
        return tok

    def barrier(self):
        toks = [("e", e, self.cnt[e]) for e in ENGS if self.cnt[e] > 0]
        for q, st in self.dq.items():
            for i in range(NSLOT):
                if st["cnt"][i] > 0:
                    toks.append(("d", (q, i), st["cnt"][i]))
        for eng in ENGS:
            waits = {}
            for t in toks:
                if t[0] == "e" and t[1] == eng:
                    continue
                self._need(eng, t, waits)
            if waits:
                self.ops[eng].append(([(self._semof(k), v) for k, v in waits.items()], None, None))

    def emit(self):
        nc = self.nc
        ops = self.ops
        with nc.Block() as block:
            def run(handle, lst):
                for waits, fn, inc in lst:
                    for sem, v in waits:
                        handle.wait_ge(sem, v)
                    if fn is not None:
                        fn(handle).then_inc(inc[0], inc[1])

            @block.tensor
            def _(e):
                run(e, ops["pe"])

            @block.scalar
            def _(e):
                run(e, ops["act"])

            @block.vector
            def _(e):
                run(e, ops["dve"])

            @block.gpsimd
            def _(e):
                run(e, ops["pool"])

            @block.sync
            def _(e):
                run(e, ops["sp"])


def bc(ap, shape):
    return ap.to_broadcast(list(shape))


def build(stage=99):
    nc = bass.Bass("TRN2", target_bir_lowering=False)

    def din(name, shape, dt=F32):
        return nc.dram_tensor(name, list(shape), dt, kind="ExternalInput").ap()

    def dout(name, shape, dt=F32):
        return nc.dram_tensor(name, list(shape), dt, kind="ExternalOutput").ap()

    def dscr(name, shape, dt=F32):
        return nc.dram_tensor(name, list(shape), dt, kind="Internal").ap()

    x_all = din("x_all", [T, D])
    c_all = din("c_all", [128, 40])
    rope_all = din("rope_all", [T, 192])
    w_ada = din("w_ada", [D, 6 * D])
    b_ada = din("b_ada", [1, 6 * D])
    w_in = din("w_in", [D, W_IN])
    norm_mix = din("norm_mix", [1, D])
    norm_mlp = din("norm_mlp", [1, D])
    ident_d = din("ident", [128, 128])
    sel5_d = din("sel5", [5, 132])
    inda_d = din("inda", [128, 8])
    wrep_d = din("wrep", [128, 512])
    bcol_d = din("bcol", [128, 2])
    kdsc_d = din("kdsc", [128, 4])
    oh_d = din("oh", [128, 8])

    x_own = din("x_own", [NB * 128, D])
    rope_own = din("rope_own", [NB * 128, 192])
    cmask_d = din("cmask", [NB, 128, 1024])
    forceb_d = din("forceb", [NB, 128, 256])
    dmask_d = din("dmask", [128, 8 * 128])
    wmask_d = din("wmask", [128, 24 * 128])
    ovp_d = din("ovp", [128, 8 * 256])
    dtab_d = din("dtab", [128, 512])
    qdt_d = din("qdt", [128, 512])
    gnw_d = din("gnw", [1, D])
    gnb_d = din("gnb", [1, D])
    DBG = (stage == 2)
    DBG_NBLK = int(os.environ.get('K_NBLK', '3'))
    PART = int(os.environ.get('K_PART', '9'))
    SUB = int(os.environ.get('K_SUB', '9'))
    SUB2 = int(os.environ.get('K_SUB2', '9'))
    hT_d = dscr("hT_d", [NB, 128, 8, 128], BF16)
    yrT_d = dscr("yrT_d", [NB, 128, 8, 128], BF16)
    onT_d = dscr("onT_d", [NB, 128, 4, 128], BF16)
    if DBG:
        dbg_selb = dout("dbg_selb", [NB, 2, 128, 256])
        dbg_on = dout("dbg_on", [NB, 128, 512])
        dbg_yr = dout("dbg_yr", [NB, 128, 1024])

    w_bn = din("w_bn", [512, D])
    w_br = din("w_br", [D, D])
    w_out = din("w_out", [D, D])
    w_up = din("w_up", [D, 4 * D])
    w_down = din("w_down", [4 * D, D])
    norm_final_d = din("norm_final", [1, D])
    x_smp = din("x_smp", [SB, D])
    x1_d = dscr("x1_d", [NB * 128, D])
    x1s_d = dscr("x1s_d", [SB, D])
    hTs_d = dscr("hTs_d", [128, 8, SB], BF16)
    onTs_d = dscr("onTs_d", [128, 4, SB], BF16)
    yrTs_d = dscr("yrTs_d", [128, 8, SB], BF16)
    y_own = dout("y_own", [NB * 128, D])
    y_smp = dout("y_smp", [SB, D])

    smallc_d = din("smallc", [1, 1024])
    oh4_d = din("oh4", [SB, 4])
    ohm_d = din("ohm", [128, 16])
    ovs_d = din("ovs", [128, 8 * 257])
    wrow_d = din("wrow", [1, 2 * 16 * 256])
    brow_d = din("brow", [1, 256])
    rope_smp = din("rope_smp", [SB, 192])
    ptab = din("ptab", [SB, 128], I32)
    cache_cmp = din("cache_cmp", [5120 * 8, 4096])
    cache_sel = din("cache_sel", [5120 * 8, 4096])
    cache_win = din("cache_win", [SB, 512, 256])
    state_s = din("state_s", [SB, 4, 128, 256])
    os_d = dscr("os_d", [SB, 3, 8, 64])
    dbg_ons = dout("dbg_ons", [SB, 512])
    o_cmp_s = dout("o_cmp_s", [SB, 256])
    o_sel_s = dout("o_sel_s", [SB, 256])
    o_win_s = dout("o_win_s", [SB, 512, 256])
    o_state_s = dout("o_state_s", [SB, 4, 128, 256])

    o_cmp = dout("o_cmp", [T, 256])
    o_sel = dout("o_sel", [T, 256])
    o_win = dout("o_win", [512, 256])
    o_state = dout("o_state", [4, 128, 256])

    kwT_d = dscr("kwT_d", [128, T], BF16)
    vw_d = dscr("vw_d", [T, 130], BF16)
    ksT_d = dscr("ksT_d", [128, T], BF16)
    vs_d = dscr("vs_d", [T, 130], BF16)
    sown_d = dscr("sown_d", [NB, 128, 1024])
    mods_d = dscr("mods_d", [SB, 6 * D])

    with ExitStack() as st:
        P = Prog(nc, st)
        ident = P.sb([128, 128]); d_ident = Dep()
        identb = P.sb([128, 128], BF16); d_identb = Dep()
        modP4 = P.sb([128, 4, D]); d_modP = Dep()
        G1P = P.sb([128, D]); d_G1P = Dep()
        G2P = P.sb([128, D]); d_G2P = Dep()
        cmpKT = P.sb([128, 1024], BF16); d_cmpKT = Dep()
        cmpVA = P.sb([128, 8, 2, 65], BF16); d_cmpVA = Dep()
        epsT = P.sb([128, 1]); d_eps = Dep()

        banks = [P.ps([128, 512]) for _ in range(6)]
        pT = P.ps([128, 1024]); d_pTs = [Dep(True), Dep(True)]
        d_bank = [Dep(True) for _ in range(6)]

        P.dma("sp", lambda e: e.dma_start(out=ident[:], in_=ident_d), writes=[d_ident])
        P.op("dve", lambda e: e.tensor_copy(out=identb[:], in_=ident[:]), reads=[d_ident], writes=[d_identb])
        P.op("dve", lambda e: e.memset(epsT[:], RMS_EPS), writes=[d_eps])
        P.op("pool", lambda e: e.memset(cmpVA[:].rearrange("p a b c -> p (a b c)"), 1.0), writes=[d_cmpVA])

        with ExitStack() as s0:
            cT = P.sb([128, 8, 5], F32, s0); d_cT = Dep()
            scT = P.sb([128, 8, 5], BF16, s0); d_scT = Dep()
            sel5 = P.sb([5, 132], F32, s0); d_sel5 = Dep()
            mod5 = P.sb([5, 6 * D], F32, s0); d_mod5 = Dep()
            bada = P.sb([5, 6 * D], F32, s0); d_bada = Dep()
            gmix = P.sb([128, D], F32, s0); d_gmix = Dep()
            gmlp = P.sb([128, D], F32, s0); d_gmlp = Dep()
            wa = [P.sb([128, 8, 512], BF16, s0) for _ in range(2)]; d_wa = [Dep(), Dep()]

            P.dma("sp", lambda e: e.dma_start(out=cT[:].rearrange("p c b -> p (c b)"), in_=c_all), writes=[d_cT])
            P.dma("sp", lambda e: e.dma_start(out=sel5[:], in_=sel5_d), writes=[d_sel5])
            P.dma("sp", lambda e: e.dma_start(out=bada[:], in_=b_ada.partition_broadcast(5)[:, 0, :]), writes=[d_bada])
            P.dma("sp", lambda e: e.dma_start(out=gmix[:], in_=norm_mix.partition_broadcast(128)[:, 0, :]), writes=[d_gmix])
            P.dma("sp", lambda e: e.dma_start(out=gmlp[:], in_=norm_mlp.partition_broadcast(128)[:, 0, :]), writes=[d_gmlp])
            P.op("act", lambda e: e.activation(out=scT[:], in_=cT[:], func=ACTF.Silu), reads=[d_cT], writes=[d_scT])
            for g in range(12):
                b = g % 2
                P.dma("pool", lambda e, g=g, b=b: e.dma_start(
                    out=wa[b][:], in_=w_ada[:, g * 512:(g + 1) * 512].rearrange("(c p) n -> p c n", p=128)),
                    writes=[d_wa[b]])
                for k in range(8):
                    P.op("pe", lambda e, k=k, b=b: e.matmul(out=banks[0][0:5, :], lhsT=scT[:, k, :], rhs=wa[b][:, k, :],
                                                           start=(k == 0), stop=(k == 7)),
                         reads=[d_scT, d_wa[b]], writes=[d_bank[0]])
                P.op("dve", lambda e, g=g: e.tensor_tensor(out=mod5[:, g * 512:(g + 1) * 512], in0=banks[0][0:5, :],
                                                          in1=bada[:, g * 512:(g + 1) * 512], op=ALU.add),
                     reads=[d_bank[0], d_bada], writes=[d_mod5])
            P.dma("sp", lambda e: e.dma_start(out=mods_d, in_=mod5[1:5, :]), reads=[d_mod5])
            slot = {0: 0, 2: 1, 3: 2, 5: 3}
            for g in range(12):
                ch, hf_ = g // 2, (g % 2) * 512
                P.op("pe", lambda e, g=g: e.matmul(out=banks[1][:, :], lhsT=sel5[:, 0:128], rhs=mod5[:, g * 512:(g + 1) * 512],
                                                   start=True, stop=True), reads=[d_sel5, d_mod5], writes=[d_bank[1]])
                if ch in slot:
                    P.op("act", lambda e, ch=ch, hf_=hf_: e.copy(out=modP4[:, slot[ch], hf_:hf_ + 512], in_=banks[1][:, :]),
                         reads=[d_bank[1]], writes=[d_modP])
                else:
                    Gt, dG, gn, dgn = (G1P, d_G1P, gmix, d_gmix) if ch == 1 else (G2P, d_G2P, gmlp, d_gmlp)
                    P.op("dve", lambda e, Gt=Gt, gn=gn, hf_=hf_: e.scalar_tensor_tensor(out=Gt[:, hf_:hf_ + 512], in0=banks[1][:, :], scalar=1.0,
                                                                                         in1=gn[:, hf_:hf_ + 512], op0=ALU.add, op1=ALU.mult),
                         reads=[d_bank[1], dgn], writes=[dG])
            P.barrier()

        shcol = P.sb([128, 2, 8], F32); d_shcol = Dep()
        for si, mi in enumerate((0, 2)):
            for c in range(8):
                P.op("pe", lambda e, c=c, mi=mi: e.transpose(out=pT[:, c * 128:(c + 1) * 128], in_=modP4[:, mi, c * 128:(c + 1) * 128], identity=ident[:]),
                     reads=[d_modP, d_ident], writes=[d_pTs[c // 4]])
            P.op("act", lambda e, si=si: e.copy(out=shcol[:, si, :], in_=pT[:].rearrange("p (c t) -> p c t", t=128)[:, :, 0]), reads=d_pTs, writes=[d_shcol])

        def norm_mod_T(xt, d_xt, ntok, G, d_G, shift, d_shift, wk, hT, d_hT, shift_col=None, d_shift_col=None):
            junk, d_junk, ss, d_ss, rstd, d_rstd, hm, d_hm, hf, d_hf = wk
            P.op("act", lambda e: e.activation(out=junk[0:ntok, :], in_=xt[0:ntok, :], func=ACTF.Square, accum_out=ss[0:ntok, :]),
                 reads=[d_xt], writes=[d_junk, d_ss])
            P.op("act", lambda e: e.activation(out=rstd[0:ntok, :], in_=ss[0:ntok, :], func=ACTF.Sqrt, bias=epsT[0:ntok, :], scale=1.0 / D),
                 reads=[d_ss, d_eps], writes=[d_rstd])
            P.op("dve", lambda e: e.reciprocal(out=rstd[0:ntok, :], in_=rstd[0:ntok, :]), reads=[d_rstd], writes=[d_rstd])
            P.op("dve", lambda e: e.scalar_tensor_tensor(out=hm[0:ntok, :], in0=xt[0:ntok, :], scalar=rstd[0:ntok, 0:1], in1=G[0:ntok, :],
                                                        op0=ALU.mult, op1=ALU.mult),
                 reads=[d_xt, d_rstd, d_G], writes=[d_hm])
            if shift_col is None:
                P.op("pool", lambda e: e.tensor_tensor(out=hf[0:ntok, :], in0=hm[0:ntok, :], in1=shift[0:ntok, :], op=ALU.add),
                     reads=[d_hm, d_shift], writes=[d_hf])
            for c in range(8):
                P.op("pe", lambda e, c=c: e.transpose(out=pT[:, c * 128:c * 128 + ntok], in_=hf[0:ntok, c * 128:(c + 1) * 128],
                                                      identity=ident[0:ntok, 0:ntok]),
                     reads=[d_hf, d_ident], writes=[d_pTs[c // 4]])
            pv = pT[:].rearrange("p (c t) -> p c t", t=128)
            if shift_col is None:
                P.op("act", lambda e: e.copy(out=hT[:, 0:4, 0:ntok], in_=pv[:, 0:4, 0:ntok]), reads=[d_pTs[0]], writes=[d_hT])
                P.op("dve", lambda e: e.tensor_copy(out=hT[:, 4:8, 0:ntok], in_=pv[:, 4:8, 0:ntok]), reads=[d_pTs[1]], writes=[d_hT])
            else:
                for c in range(8):
                    P.op("act", lambda e, c=c: e.activation(out=hT[:, c, 0:ntok], in_=pv[:, c, 0:ntok], func=ACTF.Identity, bias=shift_col[:, c:c + 1]),
                         reads=[d_pTs[c // 4], d_shift_col], writes=[d_hT])

        junk_sh = P.sb([128, D], BF16); d_junk_sh = Dep()

        def mk_wk(stack, n=128):
            hm_ = P.sb([n, D], F32, stack); d_hm_ = Dep()
            return (junk_sh, d_junk_sh, P.sb([n, 1], F32, stack), Dep(), P.sb([n, 1], F32, stack), Dep(),
                    hm_, d_hm_, hm_, d_hm_)

        def rope(src4, dst4, cos, sin, shp, tmps, d_src, d_dst, d_rp, d_tmps):
            ta, tb, tc, td = tmps
            cb, sbb = cos, sin
            P.op("dve", lambda e: e.tensor_tensor(out=ta, in0=src4(0), in1=cb, op=ALU.mult), reads=[d_src, d_rp], writes=[d_tmps[0]])
            P.op("dve", lambda e: e.tensor_tensor(out=tb, in0=src4(1), in1=sbb, op=ALU.mult), reads=[d_src, d_rp], writes=[d_tmps[1]])
            P.op("dve", lambda e: e.tensor_tensor(out=tc, in0=src4(0), in1=sbb, op=ALU.mult), reads=[d_src, d_rp], writes=[d_tmps[2]])
            P.op("dve", lambda e: e.tensor_tensor(out=td, in0=src4(1), in1=cb, op=ALU.mult), reads=[d_src, d_rp], writes=[d_tmps[3]])
            P.op("pool", lambda e: e.tensor_tensor(out=dst4(0), in0=ta, in1=tb, op=ALU.subtract), reads=[d_tmps[0], d_tmps[1]], writes=[d_dst])
            P.op("pool", lambda e: e.tensor_tensor(out=dst4(1), in0=tc, in1=td, op=ALU.add), reads=[d_tmps[2], d_tmps[3]], writes=[d_dst])

        def make_attn(pts, d_pts):
            scb = [banks[0], banks[1]]
            d_scb = [d_bank[0], d_bank[1]]
            slot = [0]
            pend = []

            def flush():
                while pend:
                    pend.pop(0)()

            def attn(nq, ng, QTh, d_Q, tiles, o4v, d_o4, pslc=None, first=True, last=True, pvT=False):
                ncols = nq * ng
                nt_ = len(tiles)
                for ti, tl in enumerate(tiles):
                    s_ = slot[0]; slot[0] ^= 1
                    nk = tl["nk"]
                    sc, d_sc, pt, d_pt = scb[s_], d_scb[s_], pts[s_], d_pts[s_]
                    nb = len(tl["biases"])
                    P.op("pe", lambda e, sc=sc, tl=tl, nk=nk, nb=nb: e.matmul(out=sc[0:nk, 0:ncols], lhsT=tl["KT"], rhs=QTh,
                                                                           start=True, stop=(nb == 0)),
                         reads=[d_Q] + tl["deps"], writes=[d_sc])
                    for bi, (bl, br_) in enumerate(tl["biases"]):
                        P.op("pe", lambda e, sc=sc, bl=bl, br_=br_, nk=nk, bi=bi, nb=nb: e.matmul(
                            out=sc[0:nk, 0:ncols], lhsT=bl, rhs=br_, start=False, stop=(bi == nb - 1)),
                            reads=tl["deps"], writes=[d_sc])
                    P.op("act", lambda e, sc=sc, pt=pt, nk=nk: e.activation(out=pt[0:nk, 0:ncols], in_=sc[0:nk, 0:ncols], func=ACTF.Exp),
                         reads=[d_sc], writes=[d_pt])

                    def pv(pt=pt, d_pt=d_pt, tl=tl, nk=nk, ti=ti):
                        if pvT:
                            P.op("pe", lambda e: e.matmul(out=o4v, lhsT=tl["V"], rhs=pt[0:nk, 0:ncols], start=(first and ti == 0), stop=(last and ti == nt_ - 1)),
                                 reads=[d_pt] + tl["deps"], writes=[d_o4])
                        else:
                            for g in range(ng):
                                P.op("pe", lambda e, g=g: e.matmul(
                                    out=o4v[:, g, :], lhsT=pt[0:nk, g * nq:(g + 1) * nq], rhs=tl["V"],
                                    start=(first and ti == 0 and g == 0), stop=(last and ti == nt_ - 1), skip_group_check=True),
                                    reads=[d_pt] + tl["deps"], writes=[d_o4])
                        if pslc is not None:
                            psv, d_ps, gpb = pslc
                            for g in range(ng):
                                P.op("pe", lambda e, g=g: e.matmul(
                                    out=psv(g), lhsT=pt[0:nk, g * nq:(g + 1) * nq], rhs=tl["Ov"],
                                    start=(ti == 0 and g % gpb == 0), stop=(ti == nt_ - 1), skip_group_check=True),
                                    reads=[d_pt] + tl["deps"], writes=[d_ps[g // gpb]])
                    flush()
                    pend.append(pv)

            attn.flush = flush
            return attn

        def phase_b1():
            with ExitStack() as sB:
                WB = P.sb([128, 8, 4376], BF16, sB); d_WB = Dep()
                wsegs = []
                for g in range(4):
                    wsegs += [(C_Q + g * 64, 64, g * 128), (C_Q + (4 + g) * 64, 64, g * 128 + 64)]
                wsegs += [(C_RQ, 512, 512), (C_RK, 512, 1024), (C_RV, 1024, 1536), (C_RG, 1024, 2560), (C_G, 24, 3584),
                          (C_KVC, 128, 3608), (C_KVS, 128, 3736), (C_KVW, 128, 3864),
                          (C_KVC + 128, 128, 3992), (C_KVS + 128, 128, 4120), (C_KVW + 128, 128, 4248)]
                for (src, n, dst) in wsegs:
                    P.dma("pool", lambda e, src=src, n=n, dst=dst: e.dma_start(
                        out=WB[:, :, dst:dst + n], in_=w_in[:, src:src + n].rearrange("(c p) n -> p c n", p=128)), writes=[d_WB])
                dmaskB = P.sb([128, 8, 128], BF16, sB); d_dmask = Dep()
                wmaskB = P.sb([128, 24, 128], BF16, sB); d_wmask = Dep()
                OvP = P.sb([128, 8, 256], BF16, sB); d_OvP = Dep()
                dtab = P.sb([128, 4, 128], F32, sB); d_dtab = Dep()
                qdt = P.sb([128, 4, 128], F32, sB); d_qdt = Dep()
                gnw = P.sb([128, D], F32, sB); d_gnw = Dep()
                gnb = P.sb([128, D], F32, sB); d_gnb = Dep()
                Irep = P.sb([128, 4, 128], BF16, sB); d_Irep = Dep()
                epsG = P.sb([128, 1], F32, sB); d_epsG = Dep()
                P.dma("pool", lambda e: e.dma_start(out=dmaskB[:].rearrange("p a b -> p (a b)"), in_=dmask_d), writes=[d_dmask])
                P.dma("pool", lambda e: e.dma_start(out=wmaskB[:].rearrange("p a b -> p (a b)"), in_=wmask_d), writes=[d_wmask])
                P.dma("pool", lambda e: e.dma_start(out=OvP[:].rearrange("p a b -> p (a b)"), in_=ovp_d), writes=[d_OvP])
                P.dma("sp", lambda e: e.dma_start(out=dtab[:].rearrange("p a b -> p (a b)"), in_=dtab_d), writes=[d_dtab])
                P.dma("sp", lambda e: e.dma_start(out=qdt[:].rearrange("p a b -> p (a b)"), in_=qdt_d), writes=[d_qdt])
                P.dma("sp", lambda e: e.dma_start(out=gnw[:], in_=gnw_d.partition_broadcast(128)[:, 0, :]), writes=[d_gnw])
                P.dma("sp", lambda e: e.dma_start(out=gnb[:], in_=gnb_d.partition_broadcast(128)[:, 0, :]), writes=[d_gnb])
                for g in range(4):
                    P.op("dve", lambda e, g=g: e.tensor_copy(out=Irep[:, g, :], in_=ident[:]), reads=[d_ident], writes=[d_Irep])
                P.op("dve", lambda e: e.memset(epsG[:], GN_EPS), writes=[d_epsG])
                IrepF = Irep[:].rearrange("p a b -> p (a b)")

                xt = P.sb([128, D], F32, sB); d_xt = Dep()
                rp = P.sb([128, 192], F32, sB); d_rp = Dep()
                wk = mk_wk(sB)
                hT = P.sb([128, 8, 128], BF16, sB); d_hT = Dep()
                tmq = [P.sb([128, 256], F32, sB) for _ in range(4)]; d_tmq = [Dep() for _ in range(4)]
                qr = P.sb([128, 512], F32, sB); d_qr = Dep()
                rqr = P.sb([128, 4, 128], F32, sB); d_rqr = Dep()
                rkr = P.sb([128, 4, 128], F32, sB); d_rkr = Dep()
                sg = P.sb([128, 24], F32, sB); d_sg = Dep()
                QT = P.sb([128, 4, 128], BF16, sB); d_QT = Dep()
                rqT = P.sb([128, 4, 128], BF16, sB); d_rqT = Dep()
                rqdT = P.sb([128, 4, 128], BF16, sB); d_rqdT = Dep()
                rkT = P.sb([128, 4, 128], BF16, sB); d_rkT = Dep()
                vb = P.sb([128, 1024], BF16, sB); d_vb = Dep()
                sgate = P.sb([128, 1024], F32, sB); d_sgate = Dep()
                attD = P.sb([128, 4, 128], BF16, sB); d_attD = Dep()
                SownB = P.sb([128, 1024], BF16, sB); d_SownB = Dep()
                osb = P.sb([128, 4, 256], F32, sB); d_osb = Dep()
                st4 = P.sb([128, 16], F32, sB); d_st4 = Dep()
                yret = P.sb([128, 1024], F32, sB); d_yret = Dep()
                yrT = P.sb([128, 8, 128], BF16, sB); d_yrT = Dep()
                onsa = P.sb([128, 8, 64], F32, sB); d_onsa = Dep()
                tmpo = P.sb([128, 4, 64], F32, sB); d_tmpo = Dep()
                onT = P.sb([128, 4, 128], BF16, sB); d_onT = Dep()
                rden = P.sb([128, 8], F32, sB); d_rden = Dep()
                wgt = P.sb([128, 4], F32, sB); d_wgt = Dep()
                cmaskB = P.sb([128, 1024], BF16, sB); d_cmask = Dep()
                forceb = P.sb([128, 256], F32, sB); d_forceb = Dep()
                pacc = P.sb([128, 256], F32, sB); d_pacc = Dep()
                wk2 = P.sb([128, 256], F32, sB); d_wk2 = Dep()
                m16 = P.sb([128, 16], F32, sB); d_m16 = Dep()
                selb = [P.sb([128, 256], F32, sB) for _ in range(2)]; d_selb = [Dep(), Dep()]
                selbX = [[P.sb([128, 1024], BF16, sB) for _ in range(2)] for _ in range(2)]
                d_selbX = [[Dep(), Dep()], [Dep(), Dep()]]
                KsG = [P.sb([128, 1024], BF16, sB) for _ in range(2)]; d_KsG = [Dep(), Dep()]
                VsG = [P.sb([128, 8, 130], BF16, sB) for _ in range(2)]; d_VsG = [Dep(), Dep()]
                KwWin = P.sb([128, 1536], BF16, sB); d_KwWin = Dep()
                VwWin = P.sb([128, 12, 130], BF16, sB); d_VwWin = Dep()
                pts = [P.sb([128, 512], BF16, sB) for _ in range(2)]; d_pts = [Dep(), Dep()]
                P.op("pool", lambda e: e.memset(KwWin[:], 0.0), writes=[d_KwWin])
                P.op("pool", lambda e: e.memset(VwWin[:].rearrange("p a b -> p (a b)"), 0.0), writes=[d_VwWin])
                attn = make_attn(pts, d_pts)

                o4 = [banks[2][0:65, :], banks[3][0:65, :]]
                d_o4 = [d_bank[2], d_bank[3]]
                oTs = [P.sb([65, 512], F32, sB) for _ in range(2)]; d_oTs = [Dep(), Dep()]
                sgv = sg[:].rearrange("p (h b) -> p h b", b=3)

                def evac(kvh, br_i, first):
                    attn.flush()
                    P.op("act", lambda e: e.copy(out=oTs[kvh][:], in_=o4[kvh]), reads=[d_o4[kvh]], writes=[d_oTs[kvh]])
                    for g in range(4):
                        P.op("pe", lambda e, g=g: e.transpose(out=pT[:, kvh * 512 + g * 65:kvh * 512 + (g + 1) * 65], in_=oTs[kvh][0:65, g * 128:(g + 1) * 128],
                                                              identity=ident[0:65, 0:65]), reads=[d_oTs[kvh], d_ident], writes=[d_pTs[kvh]])
                    o4v = pT[:, kvh * 512:kvh * 512 + 260].rearrange("p (g c) -> p g c", g=4)
                    d_o = d_pTs[kvh]
                    P.op("dve", lambda e: e.tensor_scalar(out=rden[:, kvh * 4:(kvh + 1) * 4], in0=o4v[:, :, 64], scalar1=1e-30, scalar2=None, op0=ALU.max),
                         reads=[d_o], writes=[d_rden])
                    P.op("dve", lambda e: e.reciprocal(out=rden[:, kvh * 4:(kvh + 1) * 4], in_=rden[:, kvh * 4:(kvh + 1) * 4]), reads=[d_rden], writes=[d_rden])
                    P.op("dve", lambda e: e.tensor_tensor(out=wgt[:], in0=rden[:, kvh * 4:(kvh + 1) * 4], in1=sgv[:, kvh * 4:(kvh + 1) * 4, br_i], op=ALU.mult),
                         reads=[d_rden, d_sg], writes=[d_wgt])
                    wb_ = wgt[:].unsqueeze(2).to_broadcast([128, 4, 64])
                    if first:
                        P.op("dve", lambda e: e.tensor_tensor(out=onsa[:, kvh * 4:(kvh + 1) * 4, :], in0=o4v[:, :, 0:64], in1=wb_, op=ALU.mult),
                             reads=[d_o, d_wgt], writes=[d_onsa])
                    else:
                        P.op("dve", lambda e: e.tensor_tensor(out=tmpo[:], in0=o4v[:, :, 0:64], in1=wb_, op=ALU.mult),
                             reads=[d_o, d_wgt], writes=[d_tmpo])
                        P.op("pool", lambda e: e.tensor_tensor(out=onsa[:, kvh * 4:(kvh + 1) * 4, :], in0=onsa[:, kvh * 4:(kvh + 1) * 4, :], in1=tmpo[:], op=ALU.add),
                             reads=[d_tmpo, d_onsa], writes=[d_onsa])

                nblk = NB if stage >= 2 else 0
                if stage == 2:
                    nblk = DBG_NBLK
                for i in range(nblk):
                    P.dma("sp", lambda e, i=i: e.dma_start(out=xt[:], in_=x_own[i * 128:(i + 1) * 128, :]), writes=[d_xt])
                    P.dma("sp", lambda e, i=i: e.dma_start(out=rp[:], in_=rope_own[i * 128:(i + 1) * 128, :]), writes=[d_rp])
                    P.dma("pool", lambda e, i=i: e.dma_start(out=cmaskB[:], in_=cmask_d[i]), writes=[d_cmask])
                    P.dma("sp", lambda e, i=i: e.dma_start(out=forceb[:], in_=forceb_d[i]), writes=[d_forceb])
                    P.dma("pool", lambda e, i=i: e.dma_start(out=SownB[:], in_=sown_d[i]), writes=[d_SownB])
                    if i == 0:
                        P.dma("sp", lambda e: e.dma_start(out=KwWin[:, 512:1536], in_=kwT_d[:, 0:1024]), writes=[d_KwWin])
                        P.dma("sp", lambda e: e.dma_start(out=VwWin[:, 4:12, :], in_=vw_d[0:1024, :].rearrange("(m p) f -> p m f", p=128)), writes=[d_VwWin])
                    else:
                        P.dma("sp", lambda e, i=i: e.dma_start(out=KwWin[:], in_=kwT_d[:, (8 * i - 4) * 128:(8 * i + 8) * 128]), writes=[d_KwWin])
                        P.dma("sp", lambda e, i=i: e.dma_start(out=VwWin[:], in_=vw_d[(8 * i - 4) * 128:(8 * i + 8) * 128, :].rearrange("(m p) f -> p m f", p=128)),
                              writes=[d_VwWin])
                    norm_mod_T(xt, d_xt, 128, G1P, d_G1P, modP4[:, 0, :], d_modP, wk, hT, d_hT, shcol[:, 0, :], d_shcol)
                    P.dma("sp", lambda e, i=i: e.dma_start(out=hT_d[i], in_=hT[:]), reads=[d_hT])
                    if SUB < 1:
                        continue
                    for (bk, c0, n) in ((0, 0, 512), (1, 512, 512), (4, 1024, 512), (5, 3584, 24)):
                        for k in range(8):
                            P.op("pe", lambda e, bk=bk, k=k, c0=c0, n=n: e.matmul(out=banks[bk][:, 0:n], lhsT=hT[:, k, :], rhs=WB[:, k, c0:c0 + n],
                                                                                start=(k == 0), stop=(k == 7)),
                                 reads=[d_hT, d_WB], writes=[d_bank[bk]])
                    P.op("act", lambda e: e.activation(out=sg[:], in_=banks[5][:, 0:24], func=ACTF.Sigmoid), reads=[d_bank[5]], writes=[d_sg])
                    if SUB < 2:
                        continue
                    srcq = banks[0][:, 0:512].rearrange("p (h two e) -> p h two e", h=8, two=2)
                    dstq = qr[:].rearrange("p (h two e) -> p h two e", h=8, two=2)
                    shq = [128, 8, 32]
                    tvq = [t[:, 0:256].rearrange("p (h e) -> p h e", h=8) for t in tmq]
                    rope(lambda hf: srcq[:, :, hf, :], lambda hf: dstq[:, :, hf, :], rp[:, 0:32].unsqueeze(1).to_broadcast(shq),
                         rp[:, 32:64].unsqueeze(1).to_broadcast(shq), shq, tvq, d_bank[0], d_qr, d_rp, d_tmq)
                    shk = [128, 4, 64]
                    tvk = [t[:, 0:256].rearrange("p (h e) -> p h e", h=4) for t in tmq]
                    cosk = rp[:, 64:128].unsqueeze(1).to_broadcast(shk)
                    sink = rp[:, 128:192].unsqueeze(1).to_broadcast(shk)
                    srcrq = banks[1][:, 0:512].rearrange("p (h two e) -> p h two e", h=4, two=2)
                    dstrq = rqr[:].rearrange("p h (two e) -> p h two e", two=2)
                    rope(lambda hf: srcrq[:, :, hf, :], lambda hf: dstrq[:, :, hf, :], cosk, sink, shk, tvk, d_bank[1], d_rqr, d_rp, d_tmq)
                    srcrk = banks[4][:, 0:512].rearrange("p (h two e) -> p h two e", h=4, two=2)
                    dstrk = rkr[:].rearrange("p h (two e) -> p h two e", two=2)
                    rope(lambda hf: srcrk[:, :, hf, :], lambda hf: dstrk[:, :, hf, :], cosk, sink, shk, tvk, d_bank[4], d_rkr, d_rp, d_tmq)
                    if SUB < 3:
                        continue
                    P.op("pool", lambda e: e.tensor_scalar(out=qr[:], in0=qr[:], scalar1=0.125, scalar2=None, op0=ALU.mult), reads=[d_qr], writes=[d_qr])
                    P.op("pool", lambda e: e.tensor_scalar(out=rkr[:].rearrange("p a b -> p (a b)"), in0=rkr[:].rearrange("p a b -> p (a b)"),
                                                           scalar1=RET_SCALE, scalar2=None, op0=ALU.mult), reads=[d_rkr], writes=[d_rkr])
                    for g in range(4):
                        P.op("pe", lambda e, g=g: e.transpose(out=banks[3][:, g * 128:(g + 1) * 128], in_=qr[:, g * 128:(g + 1) * 128], identity=ident[:]),
                             reads=[d_qr, d_ident], writes=[d_bank[3]])
                    P.op("act", lambda e: e.copy(out=QT[:].rearrange("p a b -> p (a b)"), in_=banks[3][:, :]),
                         reads=[d_bank[3]], writes=[d_QT])
                    if SUB2 < 1:
                        continue
                    for h in range(4):
                        P.op("pe", lambda e, h=h: e.transpose(out=banks[3][:, h * 128:(h + 1) * 128], in_=rqr[:, h, :], identity=ident[:]),
                             reads=[d_rqr, d_ident], writes=[d_bank[3]])
                    P.op("act", lambda e: e.copy(out=rqT[:].rearrange("p a b -> p (a b)"), in_=banks[3][:, :]), reads=[d_bank[3]], writes=[d_rqT])
                    P.op("dve", lambda e: e.tensor_tensor(out=rqdT[:].rearrange("p a b -> p (a b)"), in0=banks[3][:, :],
                                                          in1=qdt[:].rearrange("p a b -> p (a b)"), op=ALU.mult),
                         reads=[d_bank[3], d_qdt], writes=[d_rqdT])
                    if SUB2 < 2:
                        continue
                    for h in range(4):
                        P.op("pe", lambda e, h=h: e.transpose(out=banks[3][:, h * 128:(h + 1) * 128], in_=rkr[:, h, :], identity=ident[:]),
                             reads=[d_rkr, d_ident], writes=[d_bank[3]])
                    P.op("act", lambda e: e.copy(out=rkT[:].rearrange("p a b -> p (a b)"), in_=banks[3][:, :]),
                         reads=[d_bank[3]], writes=[d_rkT])
                    if SUB < 4:
                        continue
                    for (bk, c0) in ((0, 1536), (1, 2048), (4, 2560), (5, 3072)):
                        for k in range(8):
                            P.op("pe", lambda e, bk=bk, k=k, c0=c0: e.matmul(out=banks[bk][:, :], lhsT=hT[:, k, :], rhs=WB[:, k, c0:c0 + 512],
                                                                           start=(k == 0), stop=(k == 7)),
                                 reads=[d_hT, d_WB], writes=[d_bank[bk]])
                    P.op("act", lambda e: e.copy(out=vb[:, 0:512], in_=banks[0][:, :]), reads=[d_bank[0]], writes=[d_vb])
                    P.op("act", lambda e: e.copy(out=vb[:, 512:1024], in_=banks[1][:, :]), reads=[d_bank[1]], writes=[d_vb])
                    P.op("act", lambda e: e.activation(out=sgate[:, 0:512], in_=banks[4][:, :], func=ACTF.Silu), reads=[d_bank[4]], writes=[d_sgate])
                    P.op("act", lambda e: e.activation(out=sgate[:, 512:1024], in_=banks[5][:, :], func=ACTF.Silu), reads=[d_bank[5]], writes=[d_sgate])
                    if PART < 2:
                        continue
                    for h in range(4):
                        P.op("pe", lambda e, h=h: e.matmul(out=banks[3][:, h * 128:(h + 1) * 128], lhsT=rkT[:, h, :], rhs=rqT[:, h, :], start=True, stop=True),
                             reads=[d_rkT, d_rqT], writes=[d_bank[3]])
                    P.op("dve", lambda e: e.tensor_tensor(out=attD[:].rearrange("p a b -> p (a b)"), in0=banks[3][:, :],
                                                          in1=dtab[:].rearrange("p a b -> p (a b)"), op=ALU.mult),
                         reads=[d_bank[3], d_dtab], writes=[d_attD])
                    for h in range(4):
                        bk = 4 + h // 2
                        c0 = (h % 2) * 256
                        P.op("pe", lambda e, h=h, bk=bk, c0=c0: e.matmul(out=banks[bk][:, c0:c0 + 256], lhsT=attD[:, h, :], rhs=vb[:, h * 256:(h + 1) * 256],
                                                                        start=True, stop=False),
                             reads=[d_attD, d_vb], writes=[d_bank[bk]])
                        P.op("pe", lambda e, h=h, bk=bk, c0=c0: e.matmul(out=banks[bk][:, c0:c0 + 256], lhsT=rqdT[:, h, :], rhs=SownB[:, h * 256:(h + 1) * 256],
                                                                        start=False, stop=True),
                             reads=[d_rqdT, d_SownB], writes=[d_bank[bk]])
                    for h in range(4):
                        bk = 4 + h // 2
                        c0 = (h % 2) * 256
                        P.op("act", lambda e, h=h, bk=bk, c0=c0: e.activation(out=osb[:, h, :], in_=banks[bk][:, c0:c0 + 256], func=ACTF.Identity,
                                                                             accum_out=st4[:, h:h + 1]),
                             reads=[d_bank[bk]], writes=[d_osb, d_st4])
                        P.op("act", lambda e, h=h, bk=bk, c0=c0: e.activation(out=junk_sh[:, 0:256], in_=banks[bk][:, c0:c0 + 256], func=ACTF.Square,
                                                                             accum_out=st4[:, 4 + h:5 + h]),
                             reads=[d_bank[bk]], writes=[d_junk_sh, d_st4])
                    P.op("dve", lambda e: e.tensor_scalar(out=st4[:, 8:12], in0=st4[:, 0:4], scalar1=1.0 / 256, scalar2=None, op0=ALU.mult),
                         reads=[d_st4], writes=[d_st4])
                    P.op("dve", lambda e: e.tensor_tensor(out=st4[:, 12:16], in0=st4[:, 8:12], in1=st4[:, 8:12], op=ALU.mult), reads=[d_st4], writes=[d_st4])
                    P.op("dve", lambda e: e.scalar_tensor_tensor(out=st4[:, 12:16], in0=st4[:, 4:8], scalar=1.0 / 256, in1=st4[:, 12:16],
                                                                 op0=ALU.mult, op1=ALU.subtract), reads=[d_st4], writes=[d_st4])
                    P.op("act", lambda e: e.activation(out=st4[:, 12:16], in_=st4[:, 12:16], func=ACTF.Sqrt, bias=epsG[:, 0:1], scale=1.0),
                         reads=[d_st4, d_epsG], writes=[d_st4])
                    P.op("dve", lambda e: e.reciprocal(out=st4[:, 12:16], in_=st4[:, 12:16]), reads=[d_st4], writes=[d_st4])
                    for h in range(4):
                        P.op("dve", lambda e, h=h: e.tensor_scalar(out=yret[:, h * 256:(h + 1) * 256], in0=osb[:, h, :], scalar1=st4[:, 8 + h:9 + h],
                                                                  scalar2=st4[:, 12 + h:13 + h], op0=ALU.subtract, op1=ALU.mult),
                             reads=[d_osb, d_st4], writes=[d_yret])
                    P.op("pool", lambda e: e.tensor_tensor(out=yret[:], in0=yret[:], in1=gnw[:], op=ALU.mult), reads=[d_yret, d_gnw], writes=[d_yret])
                    P.op("pool", lambda e: e.tensor_tensor(out=yret[:], in0=yret[:], in1=gnb[:], op=ALU.add), reads=[d_yret, d_gnb], writes=[d_yret])
                    P.op("pool", lambda e: e.tensor_tensor(out=yret[:], in0=yret[:], in1=sgate[:], op=ALU.mult), reads=[d_yret, d_sgate], writes=[d_yret])
                    for c in range(8):
                        P.op("pe", lambda e, c=c: e.transpose(out=pT[:, c * 128:(c + 1) * 128], in_=yret[:, c * 128:(c + 1) * 128], identity=ident[:]),
                             reads=[d_yret, d_ident], writes=[d_pTs[c // 4]])
                    P.op("act", lambda e: e.copy(out=yrT[:].rearrange("p a b -> p (a b)"), in_=pT[:]), reads=d_pTs, writes=[d_yrT])
                    P.dma("sp", lambda e, i=i: e.dma_start(out=yrT_d[i], in_=yrT[:]), reads=[d_yrT])
                    if DBG:
                        P.dma("sp", lambda e, i=i: e.dma_start(out=dbg_yr[i], in_=yret[:]), reads=[d_yret])

                    if PART < 3:
                        continue
                    for kvh in range(2):
                        QTh = QT[kvh * 64:(kvh + 1) * 64, :, :].rearrange("p a b -> p (a b)")
                        tiles = []
                        for jt in range(8):
                            tiles.append({"KT": cmpKT[kvh * 64:(kvh + 1) * 64, jt * 128:(jt + 1) * 128], "V": cmpVA[:, jt, kvh, :], "nk": 128,
                                          "biases": [(cmaskB[:, jt * 128:(jt + 1) * 128], IrepF)], "Ov": OvP[:, jt, :],
                                          "deps": [d_cmpKT, d_cmpVA, d_cmask, d_Irep, d_OvP]})
                        psv = lambda g: banks[4 + g // 2][:, (g % 2) * 256:(g % 2) * 256 + 256]
                        attn(128, 4, QTh, d_QT, tiles, o4[kvh], d_o4[kvh], pslc=(psv, [d_bank[4], d_bank[5]], 2), pvT=True)
                        evac(kvh, 0, True)
                        P.op("dve", lambda e, kvh=kvh: e.tensor_scalar(out=pacc[:], in0=psv(0), scalar1=rden[:, kvh * 4:kvh * 4 + 1], scalar2=None, op0=ALU.mult),
                             reads=[d_bank[4], d_rden], writes=[d_pacc])
                        for g in range(1, 4):
                            P.op("dve", lambda e, kvh=kvh, g=g: e.scalar_tensor_tensor(out=pacc[:], in0=psv(g), scalar=rden[:, kvh * 4 + g:kvh * 4 + g + 1],
                                                                                      in1=pacc[:], op0=ALU.mult, op1=ALU.add),
                                 reads=[d_bank[4 + g // 2], d_rden, d_pacc], writes=[d_pacc])
                        P.op("dve", lambda e: e.tensor_tensor(out=pacc[:], in0=pacc[:], in1=forceb[:], op=ALU.add), reads=[d_pacc, d_forceb], writes=[d_pacc])
                        P.op("dve", lambda e: e.max(out=m16[:, 0:8], in_=pacc[:]), reads=[d_pacc], writes=[d_m16])
                        P.op("dve", lambda e: e.match_replace(out=wk2[:], in_to_replace=m16[:, 0:8], in_values=pacc[:], imm_value=-3.0e38),
                             reads=[d_pacc, d_m16], writes=[d_wk2])
                        P.op("dve", lambda e: e.max(out=m16[:, 8:16], in_=wk2[:]), reads=[d_wk2], writes=[d_m16])
                        P.op("dve", lambda e, kvh=kvh: e.tensor_scalar(out=selb[kvh][:], in0=pacc[:], scalar1=m16[:, 15:16], scalar2=NEG,
                                                                      op0=ALU.is_lt, op1=ALU.mult),
                             reads=[d_pacc, d_m16], writes=[d_selb[kvh]])
                    if DBG:
                        P.dma("sp", lambda e, i=i: e.dma_start(out=dbg_selb[i, 0], in_=selb[0][:]), reads=[d_selb[0]])
                        P.dma("sp", lambda e, i=i: e.dma_start(out=dbg_selb[i, 1], in_=selb[1][:]), reads=[d_selb[1]])
                    if PART < 4:
                        continue
                    ngk = i + 1
                    for gk in range(ngk):
                        b = gk % 2
                        P.dma("sp", lambda e, gk=gk, b=b: e.dma_start(out=KsG[b][:], in_=ksT_d[:, gk * 1024:(gk + 1) * 1024]), writes=[d_KsG[b]])
                        P.dma("sp", lambda e, gk=gk, b=b: e.dma_start(out=VsG[b][:], in_=vs_d[gk * 1024:(gk + 1) * 1024, :].rearrange("(m p) f -> p m f", p=128)),
                              writes=[d_VsG[b]])
                        for kvh in range(2):
                            P.op("pool", lambda e, kvh=kvh, b=b, gk=gk: e.tensor_copy(
                                out=selbX[kvh][b][:].rearrange("p (a c) -> p a c", c=64),
                                in_=selb[kvh][:, gk * 16:(gk + 1) * 16].unsqueeze(2).to_broadcast([128, 16, 64])),
                                reads=[d_selb[kvh]], writes=[d_selbX[kvh][b]])
                        for kvh in range(2):
                            QTh = QT[kvh * 64:(kvh + 1) * 64, :, :].rearrange("p a b -> p (a b)")
                            tiles = []
                            for j in range(8):
                                bs = [(selbX[kvh][b][:, j * 128:(j + 1) * 128], IrepF)]
                                if gk == i:
                                    bs.append((dmaskB[:, j, :], IrepF))
                                tiles.append({"KT": KsG[b][kvh * 64:(kvh + 1) * 64, j * 128:(j + 1) * 128], "V": VsG[b][:, j, kvh * 65:(kvh + 1) * 65],
                                              "nk": 128, "biases": bs, "deps": [d_KsG[b], d_VsG[b], d_selbX[kvh][b], d_Irep, d_dmask]})
                            attn(128, 4, QTh, d_QT, tiles, o4[kvh], d_o4[kvh], first=(gk == 0), last=(gk == ngk - 1), pvT=True)
                    evac(0, 1, False)
                    evac(1, 1, False)
                    if PART < 5:
                        continue
                    for kvh in range(2):
                        QTh = QT[kvh * 64:(kvh + 1) * 64, :, :].rearrange("p a b -> p (a b)")
                        tiles = []
                        for m in range(12):
                            wm = m if i == 0 else 12 + m
                            tiles.append({"KT": KwWin[kvh * 64:(kvh + 1) * 64, m * 128:(m + 1) * 128], "V": VwWin[:, m, kvh * 65:(kvh + 1) * 65], "nk": 128,
                                          "biases": [(wmaskB[:, wm, :], IrepF)], "deps": [d_KwWin, d_VwWin, d_wmask, d_Irep]})
                        attn(128, 4, QTh, d_QT, tiles, o4[kvh], d_o4[kvh], pvT=True)
                        evac(kvh, 2, False)
                    onf = onsa[:].rearrange("p h d -> p (h d)")
                    for c in range(4):
                        P.op("pe", lambda e, c=c: e.transpose(out=pT[:, c * 128:(c + 1) * 128], in_=onf[:, c * 128:(c + 1) * 128], identity=ident[:]),
                             reads=[d_onsa, d_ident], writes=[d_pTs[0]])
                    P.op("act", lambda e: e.copy(out=onT[:].rearrange("p a b -> p (a b)"), in_=pT[:, 0:512]), reads=[d_pTs[0]], writes=[d_onT])
                    P.dma("sp", lambda e, i=i: e.dma_start(out=onT_d[i], in_=onT[:]), reads=[d_onT])
                    if DBG:
                        P.dma("sp", lambda e, i=i: e.dma_start(out=dbg_on[i], in_=onsa[:].rearrange("p h d -> p (h d)")), reads=[d_onsa])
                P.barrier()

        def phase_s1():
            with ExitStack() as sS:
                sR = ExitStack()
                smallc = P.sb([1, 1024], F32, sS); smallb = P.sb([1, 1024], BF16, sS); d_small = Dep()
                P.dma("sp", lambda e: e.dma_start(out=smallc[:], in_=smallc_d), writes=[d_small])
                P.op("dve", lambda e: e.tensor_copy(out=smallb[:], in_=smallc[:]), reads=[d_small], writes=[d_small])
                cm7row, wm0row, ones4 = smallb[0:1, 0:128], smallb[0:1, 128:256], smallb[0:1, 272:276]
                forceS = smallc[0:1, 276:532]
                oh4 = P.sb([SB, 4], F32, sS); d_oh4 = Dep()
                P.dma("sp", lambda e: e.dma_start(out=oh4[:], in_=oh4_d), writes=[d_oh4])
                OvS = P.sb([128, 8, 257], BF16, sS); d_OvS = Dep()
                P.dma("pool", lambda e: e.dma_start(out=OvS[:].rearrange("p a b -> p (a b)"), in_=ovs_d), writes=[d_OvS])
                wbc = P.sb([128, 2, 16, 256], F32, sS); d_wbc = Dep()
                P.dma("sp", lambda e: e.dma_start(out=wbc[:].rearrange("p a b c -> p (a b c)"), in_=wrow_d.partition_broadcast(128)[:, 0, :]), writes=[d_wbc])
                brow = P.sb([128, 256], F32, sS); d_brow = Dep()
                P.dma("sp", lambda e: e.dma_start(out=brow[:], in_=brow_d.partition_broadcast(128)[:, 0, :]), writes=[d_brow])
                QTs = P.sb([128, 4, SB], BF16, sS); d_QTs = Dep()
                KnT = P.sb([128, 2, SB], BF16, sS); d_KnT = Dep()
                VnA = P.sb([SB, 2, 2, 65], BF16, sS); d_VnA = Dep()
                sg = P.sb([SB, 24], F32, sS); d_sg = Dep()
                ohm = P.sb([128, 4, 4], F32, sR); d_ohm = Dep()
                P.dma("sp", lambda e: e.dma_start(out=ohm[:].rearrange("p a b -> p (a b)"), in_=ohm_d), writes=[d_ohm])
                gnw = P.sb([SB, D], F32, sR); d_gnw = Dep()
                gnb = P.sb([SB, D], F32, sR); d_gnb = Dep()
                P.dma("sp", lambda e: e.dma_start(out=gnw[:], in_=gnw_d.partition_broadcast(SB)[:, 0, :]), writes=[d_gnw])
                P.dma("sp", lambda e: e.dma_start(out=gnb[:], in_=gnb_d.partition_broadcast(SB)[:, 0, :]), writes=[d_gnb])
                epsG = P.sb([SB, 1], F32, sR); d_epsG = Dep()
                P.op("dve", lambda e: e.memset(epsG[:], GN_EPS), writes=[d_epsG])
                xs = P.sb([SB, D], F32, sR); d_xs = Dep()
                rps = P.sb([SB, 192], F32, sR); d_rps = Dep()
                msS = P.sb([SB, 2, D], F32, sR); d_msS = Dep()
                gm4 = P.sb([SB, D], F32, sR); d_gm4 = Dep()
                P.dma("sp", lambda e: e.dma_start(out=xs[:], in_=x_smp), writes=[d_xs])
                P.dma("sp", lambda e: e.dma_start(out=rps[:], in_=rope_smp), writes=[d_rps])
                P.dma("sp", lambda e: e.dma_start(out=msS[:], in_=mods_d[:, 0:2 * D].rearrange("s (a d) -> s a d", a=2)), writes=[d_msS])
                P.dma("sp", lambda e: e.dma_start(out=gm4[:], in_=norm_mix.partition_broadcast(SB)[:, 0, :]), writes=[d_gm4])
                P.op("dve", lambda e: e.scalar_tensor_tensor(out=msS[:, 1, :], in0=msS[:, 1, :], scalar=1.0, in1=gm4[:], op0=ALU.add, op1=ALU.mult),
                     reads=[d_msS, d_gm4], writes=[d_msS])
                wkS = mk_wk(sR, SB)
                hTs = P.sb([128, 8, SB], BF16, sR); d_hTs = Dep()
                norm_mod_T(xs, d_xs, SB, msS[:, 1, :], d_msS, msS[:, 0, :], d_msS, wkS, hTs, d_hTs)
                P.dma("sp", lambda e: e.dma_start(out=hTs_d, in_=hTs[:]), reads=[d_hTs])
                zs = P.sb([SB, 4376], F32, sR); d_zs = Dep()
                wch = [P.sb([128, 8, 512], BF16, sR) for _ in range(2)]; d_wch = [Dep(), Dep()]
                segs = []
                for g in range(4):
                    segs += [(C_Q + g * 64, 64), (C_Q + (4 + g) * 64, 64)]
                segs += [(C_RQ, 512), (C_RK, 512), (C_RV, 512), (C_RV + 512, 512), (C_RG, 512), (C_RG + 512, 512), (C_G, 24),
                         (C_KVC, 128), (C_KVS, 128), (C_KVW, 128), (C_KVC + 128, 128), (C_KVS + 128, 128), (C_KVW + 128, 128)]
                chunks = []
                cur, curw = [], 0
                for sg_ in segs:
                    if curw + sg_[1] > 512:
                        chunks.append(cur); cur, curw = [], 0
                    cur.append(sg_); curw += sg_[1]
                chunks.append(cur)
                zoff = 0
                for ci, ch in enumerate(chunks):
                    b = ci % 2
                    o_ = 0
                    for (src, n_) in ch:
                        P.dma("pool", lambda e, b=b, o_=o_, src=src, n_=n_: e.dma_start(
                            out=wch[b][:, :, o_:o_ + n_], in_=w_in[:, src:src + n_].rearrange("(c p) n -> p c n", p=128)), writes=[d_wch[b]])
                        o_ += n_
                    for k in range(8):
                        P.op("pe", lambda e, b=b, k=k, o_=o_: e.matmul(out=banks[b][0:SB, 0:o_], lhsT=hTs[:, k, :], rhs=wch[b][:, k, 0:o_], start=(k == 0), stop=(k == 7)),
                             reads=[d_hTs, d_wch[b]], writes=[d_bank[b]])
                    P.op("act", lambda e, b=b, o_=o_, zoff=zoff: e.copy(out=zs[:, zoff:zoff + o_], in_=banks[b][0:SB, 0:o_]), reads=[d_bank[b]], writes=[d_zs])
                    zoff += o_
                tms = [P.sb([SB, 256], F32, sR) for _ in range(4)]; d_tms = [Dep() for _ in range(4)]
                qr = P.sb([SB, 512], F32, sR); d_qr = Dep()
                rqr = P.sb([SB, 4, 128], F32, sR); d_rqr = Dep()
                rkr = P.sb([SB, 4, 128], F32, sR); d_rkr = Dep()
                kvo = P.sb([SB, 3, 256], F32, sR); d_kvo = Dep()
                sgate = P.sb([SB, 1024], F32, sR); d_sgate = Dep()
                vbs = P.sb([SB, 1024], BF16, sR); d_vbs = Dep()
                shq = [SB, 8, 32]
                srcq = zs[:, 0:512].rearrange("p (h two e) -> p h two e", h=8, two=2)
                dstq = qr[:].rearrange("p (h two e) -> p h two e", h=8, two=2)
                tvq = [t[:, 0:256].rearrange("p (h e) -> p h e", h=8) for t in tms]
                cos32 = rps[:, 0:32]; sin32 = rps[:, 32:64]
                rope(lambda hf: srcq[:, :, hf, :], lambda hf: dstq[:, :, hf, :], cos32.unsqueeze(1).to_broadcast(shq), sin32.unsqueeze(1).to_broadcast(shq),
                     shq, tvq, d_zs, d_qr, d_rps, d_tms)
                P.op("pool", lambda e: e.tensor_scalar(out=qr[:], in0=qr[:], scalar1=0.125, scalar2=None, op0=ALU.mult), reads=[d_qr], writes=[d_qr])
                shk = [SB, 4, 64]
                tvk = [t[:, 0:256].rearrange("p (h e) -> p h e", h=4) for t in tms]
                cosk = rps[:, 64:128].unsqueeze(1).to_broadcast(shk); sink = rps[:, 128:192].unsqueeze(1).to_broadcast(shk)
                for (c0, dstt, d_dst) in ((512, rqr, d_rqr), (1024, rkr, d_rkr)):
                    src_ = zs[:, c0:c0 + 512].rearrange("p (h two e) -> p h two e", h=4, two=2)
                    dst_ = dstt[:].rearrange("p h (two e) -> p h two e", two=2)
                    rope(lambda hf, src_=src_: src_[:, :, hf, :], lambda hf, dst_=dst_: dst_[:, :, hf, :], cosk, sink, shk, tvk, d_zs, d_dst, d_rps, d_tms)
                P.op("pool", lambda e: e.tensor_scalar(out=rkr[:].rearrange("p a b -> p (a b)"), in0=rkr[:].rearrange("p a b -> p (a b)"),
                                                       scalar1=RET_SCALE, scalar2=None, op0=ALU.mult), reads=[d_rkr], writes=[d_rkr])
                shn = [SB, 3, 2, 32]
                srcn = zs[:, 3608:3992].rearrange("p (j h two e) -> p j h two e", j=3, h=2, two=2)
                dstn = kvo[:, :, 0:128].rearrange("p j (h two e) -> p j h two e", h=2, two=2)
                tvn = [t[:, 0:192].rearrange("p (j h e) -> p j h e", j=3, h=2) for t in tms]
                rope(lambda hf: srcn[:, :, :, hf, :], lambda hf: dstn[:, :, :, hf, :], cos32.unsqueeze(1).unsqueeze(1).to_broadcast(shn),
                     sin32.unsqueeze(1).unsqueeze(1).to_broadcast(shn), shn, tvn, d_zs, d_kvo, d_rps, d_tms)
                P.op("act", lambda e: e.copy(out=kvo[:, :, 128:256], in_=zs[:, 3992:4376].rearrange("p (j f) -> p j f", j=3)), reads=[d_zs], writes=[d_kvo])
                P.op("act", lambda e: e.activation(out=sg[:], in_=zs[:, 3584:3608], func=ACTF.Sigmoid), reads=[d_zs], writes=[d_sg])
                P.op("act", lambda e: e.activation(out=sgate[:], in_=zs[:, 2560:3584], func=ACTF.Silu), reads=[d_zs], writes=[d_sgate])
                P.op("act", lambda e: e.copy(out=vbs[:], in_=zs[:, 1536:2560]), reads=[d_zs], writes=[d_vbs])
                P.dma("sp", lambda e: e.dma_start(out=o_cmp_s, in_=kvo[:, 0, :]), reads=[d_kvo])
                P.dma("sp", lambda e: e.dma_start(out=o_sel_s, in_=kvo[:, 1, :]), reads=[d_kvo])
                P.dma("sp", lambda e: e.dma_start(out=o_win_s[:, 511, :], in_=kvo[:, 2, :]), reads=[d_kvo])
                for b in range(SB):
                    P.dma("sp", lambda e, b=b: e.dma_start(out=o_win_s[b, 0:511, :], in_=cache_win[b, 1:512, :]))
                rqTs = P.sb([128, 4, SB], BF16, sR); d_rqTs = Dep()
                rqdTs = P.sb([128, 4, SB], F32, sR); d_rqdTs = Dep()
                rkTs = P.sb([128, 4, SB], BF16, sR); d_rkTs = Dep()
                for (srcs, dstT, d_srcs, d_dstT) in (([qr[:, g * 128:(g + 1) * 128] for g in range(4)], QTs, d_qr, d_QTs),
                                                     ([rqr[:, h, :] for h in range(4)], rqTs, d_rqr, d_rqTs),
                                                     ([rkr[:, h, :] for h in range(4)], rkTs, d_rkr, d_rkTs),
                                                     ([kvo[:, 1, 0:128], kvo[:, 2, 0:128]], KnT, d_kvo, d_KnT)):
                    for j, sap in enumerate(srcs):
                        P.op("pe", lambda e, j=j, sap=sap: e.transpose(out=banks[3][:, j * SB:(j + 1) * SB], in_=sap, identity=ident[0:SB, 0:SB]),
                             reads=[d_srcs, d_ident], writes=[d_bank[3]])
                    nn = len(srcs) * SB
                    P.op("act", lambda e, dstT=dstT, nn=nn: e.copy(out=dstT[:].rearrange("p a b -> p (a b)"), in_=banks[3][:, 0:nn]), reads=[d_bank[3]], writes=[d_dstT])
                for h in range(4):
                    P.op("dve", lambda e, h=h: e.tensor_scalar(out=rqdTs[:, h, :], in0=rqTs[:, h, :], scalar1=float(math.exp(LG[h])), scalar2=None, op0=ALU.mult),
                         reads=[d_rqTs], writes=[d_rqdTs])
                P.op("pool", lambda e: e.memset(VnA[:].rearrange("p a b c -> p (a b c)"), 1.0), writes=[d_VnA])
                P.op("pool", lambda e: e.tensor_copy(out=VnA[:, :, :, 0:64], in_=kvo[:, 1:3, 128:256].rearrange("p j (h d) -> p j h d", h=2)),
                     reads=[d_kvo], writes=[d_VnA])
                attDs = P.sb([SB, 4, SB], BF16, sR); d_attDs = Dep()
                for h in range(4):
                    P.op("pe", lambda e, h=h: e.matmul(out=banks[3][0:SB, h * SB:(h + 1) * SB], lhsT=rkTs[:, h, :], rhs=rqTs[:, h, :], start=True, stop=True),
                         reads=[d_rkTs, d_rqTs], writes=[d_bank[3]])
                P.op("dve", lambda e: e.tensor_tensor(out=attDs[:], in0=banks[3][0:SB, 0:16].rearrange("p (h i) -> p h i", h=4),
                                                      in1=oh4[:].unsqueeze(1).to_broadcast([SB, 4, SB]), op=ALU.mult), reads=[d_bank[3], d_oh4], writes=[d_attDs])
                Sb = [P.sb([128, 4, 256], F32, sR) for _ in range(SB)]; d_Sb = [Dep() for _ in range(SB)]
                Sbb = [P.sb([128, 4, 256], BF16, sR) for _ in range(SB)]; d_Sbb = [Dep() for _ in range(SB)]
                rqm = [P.sb([128, 4, SB], BF16, sR) for _ in range(SB)]; d_rqm = [Dep() for _ in range(SB)]
                kdm = [P.sb([SB, 4, 128], BF16, sR) for _ in range(SB)]; d_kdm = [Dep() for _ in range(SB)]
                for b in range(SB):
                    P.dma("sp", lambda e, b=b: e.dma_start(out=Sb[b][:], in_=state_s[b].rearrange("h k v -> k h v")), writes=[d_Sb[b]])
                    P.op("pool", lambda e, b=b: e.tensor_copy(out=Sbb[b][:].rearrange("p a b -> p (a b)"), in_=Sb[b][:].rearrange("p a b -> p (a b)")),
                         reads=[d_Sb[b]], writes=[d_Sbb[b]])
                    P.op("dve", lambda e, b=b: e.tensor_tensor(out=rqm[b][:], in0=rqdTs[:], in1=ohm[:, b, :].unsqueeze(1).to_broadcast([128, 4, SB]), op=ALU.mult),
                         reads=[d_rqdTs, d_ohm], writes=[d_rqm[b]])
                    P.op("dve", lambda e, b=b: e.tensor_scalar(out=kdm[b][:].rearrange("p a b -> p (a b)"), in0=rkr[:].rearrange("p a b -> p (a b)"),
                                                              scalar1=oh4[:, b:b + 1], scalar2=None, op0=ALU.mult), reads=[d_rkr, d_oh4], writes=[d_kdm[b]])
                for h in range(4):
                    bk = 4 + h // 2
                    c0 = (h % 2) * 256
                    P.op("pe", lambda e, h=h, bk=bk, c0=c0: e.matmul(out=banks[bk][0:SB, c0:c0 + 256], lhsT=attDs[:, h, :], rhs=vbs[:, h * 256:(h + 1) * 256],
                                                                    start=True, stop=False), reads=[d_attDs, d_vbs], writes=[d_bank[bk]])
                    for b in range(SB):
                        P.op("pe", lambda e, h=h, bk=bk, c0=c0, b=b: e.matmul(out=banks[bk][0:SB, c0:c0 + 256], lhsT=rqm[b][:, h, :], rhs=Sbb[b][:, h, :],
                                                                             start=False, stop=(b == SB - 1)), reads=[d_rqm[b], d_Sbb[b]], writes=[d_bank[bk]])
                osb = P.sb([SB, 4, 256], F32, sR); d_osb = Dep()
                st4 = P.sb([SB, 16], F32, sR); d_st4 = Dep()
                yret = P.sb([SB, 1024], F32, sR); d_yret = Dep()
                for h in range(4):
                    bk = 4 + h // 2
                    c0 = (h % 2) * 256
                    P.op("act", lambda e, h=h, bk=bk, c0=c0: e.activation(out=osb[:, h, :], in_=banks[bk][0:SB, c0:c0 + 256], func=ACTF.Identity, accum_out=st4[:, h:h + 1]),
                         reads=[d_bank[bk]], writes=[d_osb, d_st4])
                    P.op("act", lambda e, h=h, bk=bk, c0=c0: e.activation(out=junk_sh[0:SB, 0:256], in_=banks[bk][0:SB, c0:c0 + 256], func=ACTF.Square,
                                                                         accum_out=st4[:, 4 + h:5 + h]), reads=[d_bank[bk]], writes=[d_junk_sh, d_st4])
                P.op("dve", lambda e: e.tensor_scalar(out=st4[:, 8:12], in0=st4[:, 0:4], scalar1=1.0 / 256, scalar2=None, op0=ALU.mult), reads=[d_st4], writes=[d_st4])
                P.op("dve", lambda e: e.tensor_tensor(out=st4[:, 12:16], in0=st4[:, 8:12], in1=st4[:, 8:12], op=ALU.mult), reads=[d_st4], writes=[d_st4])
                P.op("dve", lambda e: e.scalar_tensor_tensor(out=st4[:, 12:16], in0=st4[:, 4:8], scalar=1.0 / 256, in1=st4[:, 12:16], op0=ALU.mult, op1=ALU.subtract),
                     reads=[d_st4], writes=[d_st4])
                P.op("act", lambda e: e.activation(out=st4[:, 12:16], in_=st4[:, 12:16], func=ACTF.Sqrt, bias=epsG[:, 0:1], scale=1.0), reads=[d_st4, d_epsG], writes=[d_st4])
                P.op("dve", lambda e: e.reciprocal(out=st4[:, 12:16], in_=st4[:, 12:16]), reads=[d_st4], writes=[d_st4])
                for h in range(4):
                    P.op("dve", lambda e, h=h: e.tensor_scalar(out=yret[:, h * 256:(h + 1) * 256], in0=osb[:, h, :], scalar1=st4[:, 8 + h:9 + h],
                                                              scalar2=st4[:, 12 + h:13 + h], op0=ALU.subtract, op1=ALU.mult), reads=[d_osb, d_st4], writes=[d_yret])
                P.op("pool", lambda e: e.tensor_tensor(out=yret[:], in0=yret[:], in1=gnw[:], op=ALU.mult), reads=[d_yret, d_gnw], writes=[d_yret])
                P.op("pool", lambda e: e.tensor_tensor(out=yret[:], in0=yret[:], in1=gnb[:], op=ALU.add), reads=[d_yret, d_gnb], writes=[d_yret])
                P.op("pool", lambda e: e.tensor_tensor(out=yret[:], in0=yret[:], in1=sgate[:], op=ALU.mult), reads=[d_yret, d_sgate], writes=[d_yret])
                yrTs = P.sb([128, 8, SB], BF16, sR); d_yrTs = Dep()
                for c in range(8):
                    P.op("pe", lambda e, c=c: e.transpose(out=banks[3][:, c * SB:(c + 1) * SB], in_=yret[:, c * 128:(c + 1) * 128], identity=ident[0:SB, 0:SB]),
                         reads=[d_yret, d_ident], writes=[d_bank[3]])
                P.op("act", lambda e: e.copy(out=yrTs[:].rearrange("p a b -> p (a b)"), in_=banks[3][:, 0:8 * SB]), reads=[d_bank[3]], writes=[d_yrTs])
                P.dma("sp", lambda e: e.dma_start(out=yrTs_d, in_=yrTs[:]), reads=[d_yrTs])
                snew = [P.sb([128, 256], F32, sR) for _ in range(2)]; d_snew = [Dep(), Dep()]
                for b in range(SB):
                    for h in range(4):
                        bb = (b * 4 + h) % 2
                        P.op("pe", lambda e, b=b, h=h, bb=bb: e.matmul(out=banks[4 + bb][:, 0:256], lhsT=kdm[b][:, h, :], rhs=vbs[:, h * 256:(h + 1) * 256],
                                                                      start=True, stop=True), reads=[d_kdm[b], d_vbs], writes=[d_bank[4 + bb]])
                        P.op("dve", lambda e, b=b, h=h, bb=bb: e.scalar_tensor_tensor(out=snew[bb][:], in0=Sb[b][:, h, :], scalar=float(math.exp(LG[h])),
                                                                                     in1=banks[4 + bb][:, 0:256], op0=ALU.mult, op1=ALU.add),
                             reads=[d_Sb[b], d_bank[4 + bb]], writes=[d_snew[bb]])
                        P.dma("sp", lambda e, b=b, h=h, bb=bb: e.dma_start(out=o_state_s[b, h], in_=snew[bb][:]), reads=[d_snew[bb]])

                P.barrier()
                sR.close()
                pts = [P.sb([128, 512], BF16, sS) for _ in range(2)]; d_pts = [Dep(), Dep()]
                attn = make_attn(pts, d_pts)
                G = [P.sb([128, 16, 256], F32, sS) for _ in range(2)]; d_G = [Dep(), Dep()]
                prod = [P.sb([128, 16, 256], F32, sS) for _ in range(2)]; d_prod = [Dep(), Dep()]
                Fs = P.sb([128, 8, 256], F32, sS); d_Fs = Dep()
                S2 = P.sb([128, 9, 256], F32, sS); d_S2 = Dep()
                cmpKTs = P.sb([128, 8, 128], BF16, sS); d_cmpKTs = Dep()
                cmpVAs = P.sb([128, 8, 2, 65], BF16, sS); d_cmpVAs = Dep()
                P.op("pool", lambda e: e.memset(cmpVAs[:].rearrange("p a b c -> p (a b c)"), 1.0), writes=[d_cmpVAs])
                KT16 = [P.sb([128, 16, 128], BF16, sS) for _ in range(2)]; d_KT16 = [Dep(), Dep()]
                V16 = [P.sb([128, 16, 2, 65], BF16, sS) for _ in range(2)]; d_V16 = [Dep(), Dep()]
                for t_ in V16:
                    P.op("pool", lambda e, t_=t_: e.memset(t_[:].rearrange("p a b c -> p (a b c)"), 1.0), writes=d_V16)
                Wt = P.sb([128, 4, 256], F32, sS); d_Wt = Dep()
                KTw = P.sb([128, 4, 128], BF16, sS); d_KTw = Dep()
                VwA = P.sb([128, 4, 2, 65], BF16, sS); d_VwA = Dep()
                P.op("pool", lambda e: e.memset(VwA[:].rearrange("p a b c -> p (a b c)"), 1.0), writes=[d_VwA])
                ptb = P.sb([128, 2], I32, sS); d_ptb = Dep()
                idx8 = P.sb([128, 9], I32, sS); d_idx8 = Dep()
                og = P.sb([SB, 65], F32, sS); d_og = Dep()
                rd1 = P.sb([SB, 2], F32, sS); d_rd1 = Dep()
                pn = P.sb([SB, 256], F32, sS); d_pn = Dep()
                srow = P.sb([1, 256], F32, sS); d_srow = Dep()
                wk2 = P.sb([1, 256], F32, sS); d_wk2 = Dep()
                m16 = P.sb([1, 16], F32, sS); d_m16 = Dep()
                selbB = [P.sb([1, 256], BF16, sS) for _ in range(2)]; d_selbB = [Dep(), Dep()]
                o4s = [banks[2][0:SB, 0:65].rearrange("p (g c) -> p g c", g=1), banks[3][0:SB, 0:65].rearrange("p (g c) -> p g c", g=1)]
                d_o4s = [d_bank[2], d_bank[3]]
                ones4f = oh4
                onec = P.sb([SB, 1], F32, sS); d_onec = Dep()
                P.op("dve", lambda e: e.memset(onec[:], 1.0), writes=[d_onec])

                def evac_s(b, kvh, br_i):
                    attn.flush()
                    o4v, d_o = o4s[kvh], d_o4s[kvh]
                    P.op("dve", lambda e: e.tensor_scalar(out=rd1[:, 0:1], in0=o4v[:, 0, 64:65], scalar1=1e-30, scalar2=None, op0=ALU.max), reads=[d_o], writes=[d_rd1])
                    P.op("dve", lambda e: e.reciprocal(out=rd1[:, 0:1], in_=rd1[:, 0:1]), reads=[d_rd1], writes=[d_rd1])
                    P.op("dve", lambda e: e.tensor_scalar(out=og[:, 0:64], in0=o4v[:, 0, 0:64], scalar1=rd1[:, 0:1], scalar2=None, op0=ALU.mult),
                         reads=[d_o, d_rd1], writes=[d_og])
                    P.dma("sp", lambda e: e.dma_start(out=os_d[b, br_i, kvh * 4:(kvh + 1) * 4, :], in_=og[:, 0:64]), reads=[d_og])

                for b in range(SB):
                    P.dma("sp", lambda e, b=b: e.dma_start(out=ptb[:, 0:1], in_=ptab[b].rearrange("(p o) -> p o", o=1)), writes=[d_ptb])
                    P.dma("sp", lambda e, b=b: e.dma_start(out=ptb[0:127, 1:2], in_=ptab[b, 1:128].rearrange("(p o) -> p o", o=1)), writes=[d_ptb])
                    P.dma("sp", lambda e, b=b: e.dma_start(out=ptb[127:128, 1:2], in_=ptab[b, 127:128].rearrange("(p o) -> p o", o=1)), writes=[d_ptb])
                    for k in range(8):
                        P.op("dve", lambda e, k=k: e.tensor_scalar(out=idx8[:, k:k + 1], in0=ptb[:, 0:1], scalar1=8, scalar2=k, op0=ALU.mult, op1=ALU.add),
                             reads=[d_ptb], writes=[d_idx8])
                    P.op("dve", lambda e: e.tensor_scalar(out=idx8[:, 8:9], in0=ptb[:, 1:2], scalar1=8, scalar2=0, op0=ALU.mult, op1=ALU.add),
                         reads=[d_ptb], writes=[d_idx8])
                    for k in range(9):
                        gb_ = k % 2
                        P.dma("pool", lambda e, k=k, gb_=gb_: e.indirect_dma_start(
                            out=G[gb_][:].rearrange("p a b -> p (a b)"), out_offset=None, in_=cache_cmp[:, :],
                            in_offset=bass.IndirectOffsetOnAxis(ap=idx8[:, k:k + 1], axis=0)), reads=[d_idx8], writes=[d_G[gb_]])
                        for r in range(2):
                            if k == 8 and r == 0:
                                continue
                            P.op("pool", lambda e, gb_=gb_, r=r: e.tensor_tensor(out=prod[r][:], in0=G[gb_][:], in1=wbc[:, r, :, :], op=ALU.mult),
                                 reads=[d_G[gb_], d_wbc], writes=[d_prod[r]])
                            dstF = Fs[:, k, :] if r == 0 else S2[:, k, :]
                            P.op("dve", lambda e, r=r, dstF=dstF: e.tensor_reduce(out=dstF, in_=prod[r][:].rearrange("p j f -> p f j"), axis=AX.X, op=ALU.add),
                                 reads=[d_prod[r]], writes=[d_Fs if r == 0 else d_S2])
                    P.op("dve", lambda e: e.tensor_tensor(out=Fs[:], in0=Fs[:], in1=S2[:, 1:9, :], op=ALU.add), reads=[d_Fs, d_S2], writes=[d_Fs])
                    P.op("dve", lambda e: e.tensor_tensor(out=Fs[:], in0=Fs[:], in1=brow[:].unsqueeze(1).to_broadcast([128, 8, 256]), op=ALU.add),
                         reads=[d_Fs, d_brow], writes=[d_Fs])
                    for s_ in range(8):
                        P.op("pe", lambda e, s_=s_: e.transpose(out=pT[:, s_ * 128:(s_ + 1) * 128], in_=Fs[:, s_, 0:128], identity=ident[:]),
                             reads=[d_Fs, d_ident], writes=[d_pTs[s_ // 4]])
                    P.op("act", lambda e: e.copy(out=cmpKTs[:].rearrange("p a b -> p (a b)"), in_=pT[:]), reads=d_pTs, writes=[d_cmpKTs])
                    P.op("pool", lambda e: e.tensor_copy(out=cmpVAs[:, :, :, 0:64], in_=Fs[:, :, 128:256].rearrange("p s (h d) -> p s h d", h=2)),
                         reads=[d_Fs], writes=[d_cmpVAs])
                    for kvh in range(2):
                        QTh = QTs[kvh * 64:(kvh + 1) * 64, :, b]
                        tiles = []
                        for s_ in range(8):
                            bs = [(cm7row, ones4)] if s_ == 7 else []
                            tiles.append({"KT": cmpKTs[kvh * 64:(kvh + 1) * 64, s_, :], "V": cmpVAs[:, s_, kvh, :], "nk": 128, "biases": bs,
                                          "Ov": OvS[:, s_, :], "deps": [d_cmpKTs, d_cmpVAs, d_small, d_OvS]})
                        psvS = lambda g: banks[4][0:SB, 0:257]
                        attn(SB, 1, QTh, d_QTs, tiles, o4s[kvh], d_o4s[kvh], pslc=(psvS, [d_bank[4]], 1))
                        evac_s(b, kvh, 0)
                        P.op("dve", lambda e: e.tensor_scalar(out=rd1[:, 1:2], in0=banks[4][0:SB, 256:257], scalar1=1e-30, scalar2=None, op0=ALU.max),
                             reads=[d_bank[4]], writes=[d_rd1])
                        P.op("dve", lambda e: e.reciprocal(out=rd1[:, 1:2], in_=rd1[:, 1:2]), reads=[d_rd1], writes=[d_rd1])
                        P.op("dve", lambda e: e.tensor_scalar(out=pn[:], in0=banks[4][0:SB, 0:256], scalar1=rd1[:, 1:2], scalar2=None, op0=ALU.mult),
                             reads=[d_bank[4], d_rd1], writes=[d_pn])
                        P.op("pe", lambda e: e.matmul(out=banks[5][0:1, 0:256], lhsT=onec[:, 0:1], rhs=pn[:], start=True, stop=True),
                             reads=[d_onec, d_pn], writes=[d_bank[5]])
                        P.op("dve", lambda e: e.tensor_tensor(out=srow[:], in0=banks[5][0:1, 0:256], in1=forceS, op=ALU.add), reads=[d_bank[5], d_small], writes=[d_srow])
                        P.op("dve", lambda e: e.max(out=m16[:, 0:8], in_=srow[:]), reads=[d_srow], writes=[d_m16])
                        P.op("dve", lambda e: e.match_replace(out=wk2[:], in_to_replace=m16[:, 0:8], in_values=srow[:], imm_value=-3.0e38),
                             reads=[d_srow, d_m16], writes=[d_wk2])
                        P.op("dve", lambda e: e.max(out=m16[:, 8:16], in_=wk2[:]), reads=[d_wk2], writes=[d_m16])
                        P.op("dve", lambda e, kvh=kvh: e.tensor_scalar(out=selbB[kvh][:], in0=srow[:], scalar1=m16[:, 14:15], scalar2=NEG, op0=ALU.is_lt, op1=ALU.mult),
                             reads=[d_srow, d_m16], writes=[d_selbB[kvh]])
                    for k in range(8):
                        gb_ = k % 2
                        P.dma("pool", lambda e, k=k, gb_=gb_: e.indirect_dma_start(
                            out=G[gb_][:].rearrange("p a b -> p (a b)"), out_offset=None, in_=cache_sel[:, :],
                            in_offset=bass.IndirectOffsetOnAxis(ap=idx8[:, k:k + 1], axis=0)), reads=[d_idx8], writes=[d_G[gb_]])
                        for j in range(16):
                            if j % 4 == 0:
                                tb_, d_tb = (banks[4], d_bank[4]) if (j // 4) % 2 == 0 else (banks[5], d_bank[5])
                            P.op("pe", lambda e, gb_=gb_, j=j, tb_=tb_: e.transpose(out=tb_[:, (j % 4) * 128:(j % 4 + 1) * 128], in_=G[gb_][:, j, 0:128], identity=ident[:]),
                                 reads=[d_G[gb_], d_ident], writes=[d_tb])
                            if j % 4 == 3:
                                P.op("act", lambda e, gb_=gb_, j=j, tb_=tb_: e.copy(out=KT16[gb_][:, j - 3:j + 1, :].rearrange("p a b -> p (a b)"), in_=tb_[:, :]),
                                     reads=[d_tb], writes=[d_KT16[gb_]])
                        P.op("pool", lambda e, gb_=gb_: e.tensor_copy(out=V16[gb_][:, :, :, 0:64], in_=G[gb_][:, :, 128:256].rearrange("p j (h d) -> p j h d", h=2)),
                             reads=[d_G[gb_]], writes=[d_V16[gb_]])
                        for kvh in range(2):
                            QTh = QTs[kvh * 64:(kvh + 1) * 64, :, b]
                            tiles = []
                            for j in range(16):
                                hi = 1 if (16 * k + j) >= 64 else 0
                                tiles.append({"KT": KT16[gb_][kvh * 64:(kvh + 1) * 64, j, :], "V": V16[gb_][:, j, kvh, :], "nk": 128,
                                              "biases": [(selbB[kvh][0:1, hi:256:2], ones4)], "deps": [d_KT16[gb_], d_V16[gb_], d_selbB[kvh], d_small]})
                            attn(SB, 1, QTh, d_QTs, tiles, o4s[kvh], d_o4s[kvh], first=(k == 0), last=False)
                    for kvh in range(2):
                        QTh = QTs[kvh * 64:(kvh + 1) * 64, :, b]
                        tiles = [{"KT": KnT[kvh * 64:(kvh + 1) * 64, 0, :], "V": VnA[:, 0, kvh, :], "nk": SB,
                                  "biases": [(smallb[0:1, 256 + 4 * b:260 + 4 * b], ones4)], "deps": [d_KnT, d_VnA, d_small]}]
                        attn(SB, 1, QTh, d_QTs, tiles, o4s[kvh], d_o4s[kvh], first=False, last=True)
                        evac_s(b, kvh, 1)
                    P.dma("sp", lambda e, b=b: e.dma_start(out=Wt[:], in_=cache_win[b].rearrange("(m p) f -> p m f", p=128)), writes=[d_Wt])
                    for m in range(4):
                        P.op("pe", lambda e, m=m: e.transpose(out=banks[4][:, m * 128:(m + 1) * 128], in_=Wt[:, m, 0:128], identity=ident[:]),
                             reads=[d_Wt, d_ident], writes=[d_bank[4]])
                    P.op("act", lambda e: e.copy(out=KTw[:].rearrange("p a b -> p (a b)"), in_=banks[4][:, :]), reads=[d_bank[4]], writes=[d_KTw])
                    P.op("pool", lambda e: e.tensor_copy(out=VwA[:, :, :, 0:64], in_=Wt[:, :, 128:256].rearrange("p m (h d) -> p m h d", h=2)),
                         reads=[d_Wt], writes=[d_VwA])
                    for kvh in range(2):
                        QTh = QTs[kvh * 64:(kvh + 1) * 64, :, b]
                        tiles = []
                        for m in range(4):
                            bs = [(wm0row, ones4)] if m == 0 else []
                            tiles.append({"KT": KTw[kvh * 64:(kvh + 1) * 64, m, :], "V": VwA[:, m, kvh, :], "nk": 128, "biases": bs,
                                          "deps": [d_KTw, d_VwA, d_small]})
                        tiles.append({"KT": KnT[kvh * 64:(kvh + 1) * 64, 1, :], "V": VnA[:, 1, kvh, :], "nk": SB,
                                      "biases": [(smallb[0:1, 256 + 4 * b:260 + 4 * b], ones4)], "deps": [d_KnT, d_VnA, d_small]})
                        attn(SB, 1, QTh, d_QTs, tiles, o4s[kvh], d_o4s[kvh])
                        evac_s(b, kvh, 2)
                P.barrier()
                osall = P.sb([SB, 3, 8, 64], F32, sS); d_osall = Dep()
                onsa = P.sb([SB, 8, 64], F32, sS); d_onsa = Dep()
                tmpo = P.sb([SB, 8, 64], F32, sS); d_tmpo = Dep()
                P.dma("sp", lambda e: e.dma_start(out=osall[:].rearrange("p a b c -> p (a b c)"), in_=os_d.rearrange("b r h d -> b (r h d)")), writes=[d_osall])
                sgv = sg[:].rearrange("p (h r) -> p h r", r=3)
                for r in range(3):
                    dst_, d_dst_ = (onsa, d_onsa) if r == 0 else (tmpo, d_tmpo)
                    P.op("dve", lambda e, r=r, dst_=dst_: e.tensor_tensor(out=dst_[:], in0=osall[:, r, :, :], in1=sgv[:, :, r].unsqueeze(2).to_broadcast([SB, 8, 64]), op=ALU.mult),
                         reads=[d_osall, d_sg], writes=[d_dst_])
                    if r > 0:
                        P.op("dve", lambda e: e.tensor_tensor(out=onsa[:], in0=onsa[:], in1=tmpo[:], op=ALU.add), reads=[d_onsa, d_tmpo], writes=[d_onsa])
                onTs = P.sb([128, 4, SB], BF16, sS); d_onTs = Dep()
                onf = onsa[:].rearrange("p h d -> p (h d)")
                P.dma("sp", lambda e: e.dma_start(out=dbg_ons, in_=onf), reads=[d_onsa])
                for c in range(4):
                    P.op("pe", lambda e, c=c: e.transpose(out=banks[3][:, c * SB:(c + 1) * SB], in_=onf[:, c * 128:(c + 1) * 128], identity=ident[0:SB, 0:SB]),
                         reads=[d_onsa, d_ident], writes=[d_bank[3]])
                P.op("act", lambda e: e.copy(out=onTs[:].rearrange("p a b -> p (a b)"), in_=banks[3][:, 0:4 * SB]), reads=[d_bank[3]], writes=[d_onTs])
                P.dma("sp", lambda e: e.dma_start(out=onTs_d, in_=onTs[:]), reads=[d_onTs])
                P.barrier()

        def phase_b2(groups):
            with ExitStack() as s2:
                Wgab = P.sb([128, 8, 2048], BF16, s2); d_Wgab = Dep()
                Wbn = P.sb([128, 4, 1024], BF16, s2); d_Wbn = Dep()
                Wbr = P.sb([128, 8, 1024], BF16, s2); d_Wbr = Dep()
                Wout = P.sb([128, 8, 1024], BF16, s2); d_Wout = Dep()
                P.dma("pool", lambda e: e.dma_start(out=Wgab[:], in_=w_in[:, C_GA:C_GA + 2048].rearrange("(c p) n -> p c n", p=128)), writes=[d_Wgab])
                P.dma("pool", lambda e: e.dma_start(out=Wbn[:], in_=w_bn.rearrange("(c p) n -> p c n", p=128)), writes=[d_Wbn])
                P.dma("pool", lambda e: e.dma_start(out=Wbr[:], in_=w_br.rearrange("(c p) n -> p c n", p=128)), writes=[d_Wbr])
                P.dma("pool", lambda e: e.dma_start(out=Wout[:], in_=w_out.rearrange("(c p) n -> p c n", p=128)), writes=[d_Wout])
                hT4 = P.sb([128, 8, 512], BF16, s2); d_hT4 = Dep()
                onT4 = P.sb([128, 4, 512], BF16, s2); d_onT4 = Dep()
                yrT4 = P.sb([128, 8, 512], BF16, s2); d_yrT4 = Dep()
                mixT = P.sb([128, 8, 512], BF16, s2); d_mixT = Dep()
                sga = [P.sb([128, 512], F32, s2) for _ in range(2)]; d_sga = [Dep(), Dep()]
                sgb = [P.sb([128, 512], F32, s2) for _ in range(2)]; d_sgb = [Dep(), Dep()]
                t1 = [P.sb([128, 512], F32, s2) for _ in range(2)]; d_t1 = [Dep(), Dep()]
                t2 = [P.sb([128, 512], F32, s2) for _ in range(2)]; d_t2 = [Dep(), Dep()]
                xa = [P.sb([128, D], F32, s2) for _ in range(2)]; d_xa = [Dep(), Dep()]
                x1t = [P.sb([128, D], F32, s2) for _ in range(2)]; d_x1t = [Dep(), Dep()]
                gtaS = P.sb([SB, D], F32, s2); d_gtaS = Dep()
                P.dma("sp", lambda e: e.dma_start(out=gtaS[:], in_=mods_d[:, 2 * D:3 * D]), writes=[d_gtaS])
                for grp in groups:
                    n = grp["ntok"]
                    grp["load_a"](hT4, d_hT4, onT4, d_onT4, yrT4, d_yrT4)
                    for fc in range(8):
                        b = fc % 2
                        for (bk, W, dW, kc, c0, src, dsrc) in ((0, Wgab, d_Wgab, 8, fc * 128, hT4, d_hT4), (1, Wgab, d_Wgab, 8, 1024 + fc * 128, hT4, d_hT4),
                                                              (2, Wbn, d_Wbn, 4, fc * 128, onT4, d_onT4), (3, Wbr, d_Wbr, 8, fc * 128, yrT4, d_yrT4)):
                            for k in range(kc):
                                P.op("pe", lambda e, bk=bk, W=W, k=k, c0=c0, src=src, kc=kc, n=n: e.matmul(
                                    out=banks[bk][:, 0:n], lhsT=W[:, k, c0:c0 + 128], rhs=src[:, k, 0:n], start=(k == 0), stop=(k == kc - 1)),
                                    reads=[dW, dsrc], writes=[d_bank[bk]])
                        P.op("act", lambda e, b=b, n=n: e.activation(out=sga[b][:, 0:n], in_=banks[0][:, 0:n], func=ACTF.Sigmoid), reads=[d_bank[0]], writes=[d_sga[b]])
                        P.op("act", lambda e, b=b, n=n: e.activation(out=sgb[b][:, 0:n], in_=banks[1][:, 0:n], func=ACTF.Sigmoid), reads=[d_bank[1]], writes=[d_sgb[b]])
                        P.op("dve", lambda e, b=b, n=n: e.tensor_tensor(out=t1[b][:, 0:n], in0=banks[2][:, 0:n], in1=sga[b][:, 0:n], op=ALU.mult),
                             reads=[d_bank[2], d_sga[b]], writes=[d_t1[b]])
                        P.op("dve", lambda e, b=b, n=n: e.tensor_tensor(out=t2[b][:, 0:n], in0=banks[3][:, 0:n], in1=sgb[b][:, 0:n], op=ALU.mult),
                             reads=[d_bank[3], d_sgb[b]], writes=[d_t2[b]])
                        P.op("pool", lambda e, b=b, fc=fc, n=n: e.tensor_tensor(out=mixT[:, fc, 0:n], in0=t1[b][:, 0:n], in1=t2[b][:, 0:n], op=ALU.add),
                             reads=[d_t1[b], d_t2[b]], writes=[d_mixT])
                    off = 0
                    for ti, (xsrc, nt_, gta, d_gta, x1dst) in enumerate(grp["tiles_a"](gtaS, d_gtaS)):
                        b = ti % 2
                        P.dma("sp", lambda e, b=b, xsrc=xsrc, nt_=nt_: e.dma_start(out=xa[b][0:nt_, :], in_=xsrc), writes=[d_xa[b]])
                        for half in range(2):
                            bk = 4 + half
                            for k in range(8):
                                P.op("pe", lambda e, bk=bk, k=k, off=off, nt_=nt_, half=half: e.matmul(
                                    out=banks[bk][0:nt_, :], lhsT=mixT[:, k, off:off + nt_], rhs=Wout[:, k, half * 512:(half + 1) * 512],
                                    start=(k == 0), stop=(k == 7)), reads=[d_mixT, d_Wout], writes=[d_bank[bk]])
                            P.op("dve", lambda e, bk=bk, b=b, nt_=nt_, half=half, gta=gta: e.tensor_tensor(
                                out=x1t[b][0:nt_, half * 512:(half + 1) * 512], in0=banks[bk][0:nt_, :], in1=gta[0:nt_, half * 512:(half + 1) * 512], op=ALU.mult),
                                reads=[d_bank[bk], d_gta], writes=[d_x1t[b]])
                        P.op("pool", lambda e, b=b, nt_=nt_: e.tensor_tensor(out=x1t[b][0:nt_, :], in0=x1t[b][0:nt_, :], in1=xa[b][0:nt_, :], op=ALU.add),
                             reads=[d_x1t[b], d_xa[b]], writes=[d_x1t[b]])
                        P.dma("sp", lambda e, b=b, nt_=nt_, x1dst=x1dst: e.dma_start(out=x1dst, in_=x1t[b][0:nt_, :]), reads=[d_x1t[b]])
                        off += nt_
                P.barrier()
            with ExitStack() as s3:
                Wup = P.sb([128, 8, 4096], BF16, s3); d_Wup = Dep()
                Wdn = P.sb([128, 32, 1024], BF16, s3); d_Wdn = Dep()
                for q4 in range(4):
                    P.dma("pool", lambda e, q4=q4: e.dma_start(out=Wup[:, :, q4 * 1024:(q4 + 1) * 1024],
                                                              in_=w_up[:, q4 * 1024:(q4 + 1) * 1024].rearrange("(c p) n -> p c n", p=128)), writes=[d_Wup])
                    P.dma("pool", lambda e, q4=q4: e.dma_start(out=Wdn[:, q4 * 8:(q4 + 1) * 8, :],
                                                              in_=w_down[q4 * 1024:(q4 + 1) * 1024, :].rearrange("(c p) n -> p c n", p=128)), writes=[d_Wdn])
                nfb = P.sb([128, D], F32, s3); d_nfb = Dep()
                P.dma("sp", lambda e: e.dma_start(out=nfb[:], in_=norm_final_d.partition_broadcast(128)[:, 0, :]), writes=[d_nfb])
                h2T4 = P.sb([128, 8, 256], BF16, s3); d_h2T4 = Dep()
                uT = P.sb([128, 32, 256], BF16, s3); d_uT = Dep()
                x1in = [P.sb([128, D], F32, s3) for _ in range(2)]; d_x1in = [Dep(), Dep()]
                rl = [P.sb([128, 256], BF16, s3) for _ in range(2)]; d_rl = [Dep(), Dep()]
                x2t_ = P.sb([128, D], F32, s3); x2t = [x2t_, x2t_]; d_x2t_ = Dep(); d_x2t = [d_x2t_, d_x2t_]
                ss2 = P.sb([128, 2], F32, s3); d_ss2 = Dep()
                wk3 = mk_wk(s3)
                modS3 = P.sb([SB, 3, D], BF16, s3); d_modS3 = Dep()
                P.dma("pool", lambda e: e.dma_start(out=modS3[:], in_=mods_d[:, 3 * D:6 * D].rearrange("s (a d) -> s a d", a=3)), writes=[d_modS3])
                P.dma("sp", lambda e: e.dma_start(out=x2t[1][0:SB, :], in_=norm_mlp.partition_broadcast(SB)[:, 0, :]), writes=[d_x2t[1]])
                P.op("dve", lambda e: e.scalar_tensor_tensor(out=modS3[:, 1, :], in0=modS3[:, 1, :], scalar=1.0, in1=x2t[1][0:SB, :], op0=ALU.add, op1=ALU.mult),
                     reads=[d_modS3, d_x2t[1]], writes=[d_modS3])
                for grp in groups:
                    tl_all = grp["tiles_b"](modS3, d_modS3)
                    for s0_ in range(0, len(tl_all), 2):
                        tl_b = tl_all[s0_:s0_ + 2]
                        n = sum(t_[1] for t_ in tl_b)
                        off = 0
                        for ti, (x1src, nt_, G2, d_G2, shf, d_shf, gtf, d_gtf, ydst, shc) in enumerate(tl_b):
                            P.dma("sp", lambda e, ti=ti, x1src=x1src, nt_=nt_: e.dma_start(out=x1in[ti][0:nt_, :], in_=x1src), writes=[d_x1in[ti]])
                            norm_mod_T(x1in[ti], d_x1in[ti], nt_, G2, d_G2, shf, d_shf, wk3, h2T4[:, :, off:off + nt_], d_h2T4, shc, d_shcol)
                            off += nt_
                        for fc in range(32):
                            b = fc % 2
                            bk = fc % 4
                            for k in range(8):
                                P.op("pe", lambda e, bk=bk, k=k, fc=fc, n=n: e.matmul(out=banks[bk][:, 0:n], lhsT=Wup[:, k, fc * 128:(fc + 1) * 128], rhs=h2T4[:, k, 0:n],
                                                                                    start=(k == 0), stop=(k == 7)), reads=[d_Wup, d_h2T4], writes=[d_bank[bk]])
                            P.op("act", lambda e, bk=bk, b=b, n=n: e.activation(out=rl[b][:, 0:n], in_=banks[bk][:, 0:n], func=ACTF.Relu), reads=[d_bank[bk]], writes=[d_rl[b]])
                            P.op("pool", lambda e, b=b, fc=fc, n=n: e.tensor_tensor(out=uT[:, fc, 0:n], in0=rl[b][:, 0:n], in1=rl[b][:, 0:n], op=ALU.mult),
                                 reads=[d_rl[b]], writes=[d_uT])
                        off = 0
                        for ti, (x1src, nt_, G2, d_G2, shf, d_shf, gtf, d_gtf, ydst, shc) in enumerate(tl_b):
                            b = ti % 2
                            for half in range(2):
                                bk = 4 + half
                                for k in range(32):
                                    P.op("pe", lambda e, bk=bk, k=k, off=off, nt_=nt_, half=half: e.matmul(
                                        out=banks[bk][0:nt_, :], lhsT=uT[:, k, off:off + nt_], rhs=Wdn[:, k, half * 512:(half + 1) * 512],
                                        start=(k == 0), stop=(k == 31)), reads=[d_uT, d_Wdn], writes=[d_bank[bk]])
                                P.op("dve", lambda e, bk=bk, b=b, nt_=nt_, half=half, gtf=gtf: e.tensor_tensor(
                                    out=x2t[b][0:nt_, half * 512:(half + 1) * 512], in0=banks[bk][0:nt_, :], in1=gtf[0:nt_, half * 512:(half + 1) * 512], op=ALU.mult),
                                    reads=[d_bank[bk], d_gtf], writes=[d_x2t[b]])
                            P.op("pool", lambda e, b=b, ti=ti, nt_=nt_: e.tensor_tensor(out=x2t[b][0:nt_, :], in0=x2t[b][0:nt_, :], in1=x1in[ti][0:nt_, :], op=ALU.add),
                                 reads=[d_x2t[b], d_x1in[ti]], writes=[d_x2t[b]])
                            P.op("act", lambda e, b=b, nt_=nt_: e.activation(out=junk_sh[0:nt_, :], in_=x2t[b][0:nt_, :], func=ACTF.Square, accum_out=ss2[0:nt_, 0:1]),
                                 reads=[d_x2t[b]], writes=[d_junk_sh, d_ss2])
                            P.op("act", lambda e, nt_=nt_: e.activation(out=ss2[0:nt_, 1:2], in_=ss2[0:nt_, 0:1], func=ACTF.Sqrt, bias=epsT[0:nt_, :], scale=1.0 / D),
                                 reads=[d_ss2, d_eps], writes=[d_ss2])
                            P.op("dve", lambda e, nt_=nt_: e.reciprocal(out=ss2[0:nt_, 1:2], in_=ss2[0:nt_, 1:2]), reads=[d_ss2], writes=[d_ss2])
                            P.op("dve", lambda e, b=b, nt_=nt_: e.scalar_tensor_tensor(out=x2t[b][0:nt_, :], in0=x2t[b][0:nt_, :], scalar=ss2[0:nt_, 1:2], in1=nfb[0:nt_, :],
                                                                                     op0=ALU.mult, op1=ALU.mult), reads=[d_x2t[b], d_ss2, d_nfb], writes=[d_x2t[b]])
                            P.dma("sp", lambda e, b=b, nt_=nt_, ydst=ydst: e.dma_start(out=ydst, in_=x2t[b][0:nt_, :]), reads=[d_x2t[b]])
                            off += nt_
                P.barrier()

        with ExitStack() as sA:
            WA = P.sb([128, 8, 2304], BF16, sA); d_WA = Dep()
            segs = [(C_KVC, 128, 0), (C_KVS, 128, 128), (C_KVW, 128, 256), (C_KVC + 128, 128, 384),
                    (C_KVS + 128, 128, 512), (C_KVW + 128, 128, 640), (C_RK, 512, 768), (C_RV, 1024, 1280)]
            for (src, n, dst) in segs:
                P.dma("pool", lambda e, src=src, n=n, dst=dst: e.dma_start(
                    out=WA[:, :, dst:dst + n], in_=w_in[:, src:src + n].rearrange("(c p) n -> p c n", p=128)), writes=[d_WA])
            inda = P.sb([128, 8], F32, sA); indab = P.sb([128, 8], BF16, sA); d_inda = Dep()
            wrep = P.sb([128, 512], F32, sA); d_wrep = Dep()
            bcol = P.sb([128, 2], F32, sA); d_bcol = Dep()
            kdsc = P.sb([128, 4], F32, sA); d_kdsc = Dep()
            oh = P.sb([128, 8], F32, sA); d_oh = Dep()
            P.dma("sp", lambda e: e.dma_start(out=inda[:], in_=inda_d), writes=[d_inda])
            P.op("dve", lambda e: e.tensor_copy(out=indab[:], in_=inda[:]), reads=[d_inda], writes=[d_inda])
            P.dma("sp", lambda e: e.dma_start(out=wrep[:], in_=wrep_d), writes=[d_wrep])
            P.dma("sp", lambda e: e.dma_start(out=bcol[:], in_=bcol_d), writes=[d_bcol])
            P.dma("sp", lambda e: e.dma_start(out=kdsc[:], in_=kdsc_d), writes=[d_kdsc])
            P.dma("sp", lambda e: e.dma_start(out=oh[:], in_=oh_d), writes=[d_oh])
            FS = P.sb([128, 4, 1024], F32, sA); d_FS = Dep()
            S = P.sb([128, 4, 256], F32, sA); d_S = Dep()
            Sown = P.sb([128, 1024], F32, sA); d_Sown = Dep()
            tmpS = P.sb([128, 1024], F32, sA); d_tmpS = Dep()
            P.op("pool", lambda e: e.memset(S[:].rearrange("p a b -> p (a b)"), 0.0), writes=[d_S])
            NBUF = 2
            xts = [P.sb([128, D], F32, sA) for _ in range(NBUF)]; d_xts = [Dep() for _ in range(NBUF)]
            rps = [P.sb([128, 192], F32, sA) for _ in range(NBUF)]; d_rps = [Dep() for _ in range(NBUF)]
            wks = [mk_wk(sA) for _ in range(NBUF)]
            hTs = [P.sb([128, 8, 128], BF16, sA) for _ in range(NBUF)]; d_hTs = [Dep() for _ in range(NBUF)]
            kvos = [P.sb([128, 3, 256], F32, sA) for _ in range(NBUF)]; d_kvos = [Dep() for _ in range(NBUF)]
            kraw = [P.sb([128, 896], F32, sA) for _ in range(NBUF)]; d_kraw = [Dep() for _ in range(NBUF)]
            tm = [[P.sb([128, 256], F32, sA) for _ in range(4)] for _ in range(NBUF)]
            d_tm = [[Dep() for _ in range(4)] for _ in range(NBUF)]
            rkr = [P.sb([128, 4, 128], F32, sA) for _ in range(NBUF)]; d_rkr = [Dep() for _ in range(NBUF)]
            kd = [P.sb([128, 4, 128], BF16, sA) for _ in range(NBUF)]; d_kd = [Dep() for _ in range(NBUF)]
            vb = [P.sb([128, 1024], BF16, sA) for _ in range(NBUF)]; d_vb = [Dep() for _ in range(NBUF)]
            cw = [P.sb([128, 2, 256], BF16, sA) for _ in range(NBUF)]; d_cw = [Dep() for _ in range(NBUF)]
            kwt = [P.sb([128, 128], BF16, sA) for _ in range(NBUF)]; d_kwt = [Dep() for _ in range(NBUF)]
            kst = [P.sb([128, 128], BF16, sA) for _ in range(NBUF)]; d_kst = [Dep() for _ in range(NBUF)]
            vwb = [P.sb([128, 2, 65], BF16, sA) for _ in range(NBUF)]; d_vwb = [Dep() for _ in range(NBUF)]
            vsb = [P.sb([128, 2, 65], BF16, sA) for _ in range(NBUF)]; d_vsb = [Dep() for _ in range(NBUF)]
            for b_ in range(NBUF):
                P.op("pool", lambda e, b_=b_: e.memset(vwb[b_][:].rearrange("p a b -> p (a b)"), 1.0), writes=[d_vwb[b_]])
                P.op("pool", lambda e, b_=b_: e.memset(vsb[b_][:].rearrange("p a b -> p (a b)"), 1.0), writes=[d_vsb[b_]])
            zA, zB, zK, zV0, zV1, zC = banks
            dzA, dzB, dzK, dzV0, dzV1, dzC = d_bank
            d_zBt = dzB

            ntA = NT if stage >= 1 else 0
            if 'K_NTA' in os.environ:
                ntA = int(os.environ['K_NTA'])
            if stage == 2:
                ntA = 8 * DBG_NBLK
                P.op("pool", lambda e: e.memset(FS[:].rearrange("p a b -> p (a b)"), 0.0), writes=[d_FS])
            late = []
            for tt in range(ntA):
                b = tt % NBUF
                xt, d_xt, rp, d_rp, hT, d_hT, kvo, d_kvo = xts[b], d_xts[b], rps[b], d_rps[b], hTs[b], d_hTs[b], kvos[b], d_kvos[b]
                P.dma("sp", lambda e, tt=tt, xt=xt: e.dma_start(out=xt[:], in_=x_all[tt * 128:(tt + 1) * 128, :]), writes=[d_xt])
                P.dma("sp", lambda e, tt=tt, rp=rp: e.dma_start(out=rp[:], in_=rope_all[tt * 128:(tt + 1) * 128, :]), writes=[d_rp])
                norm_mod_T(xt, d_xt, 128, G1P, d_G1P, modP4[:, 0, :], d_modP, wks[b], hT, d_hT, shcol[:, 0, :], d_shcol)
                for (zz, dz, c0, n) in ((zA, dzA, 0, 512), (zB, dzB, 512, 256), (zK, dzK, 768, 512), (zV0, dzV0, 1280, 512), (zV1, dzV1, 1792, 512)):
                    for k in range(8):
                        P.op("pe", lambda e, zz=zz, k=k, c0=c0, n=n, hT=hT: e.matmul(out=zz[:, 0:n], lhsT=hT[:, k, :], rhs=WA[:, k, c0:c0 + n],
                                                                                    start=(k == 0), stop=(k == 7)),
                             reads=[d_hT, d_WA], writes=[dz])
                while late:
                    late.pop(0)()
                P.op("act", lambda e, b=b: e.copy(out=kraw[b][:, 0:384], in_=zA[:, 0:384]), reads=[dzA], writes=[d_kraw[b]])
                P.op("act", lambda e, b=b: e.copy(out=kraw[b][:, 384:896], in_=zK[:, 0:512]), reads=[dzK], writes=[d_kraw[b]])
                src = kraw[b][:, 0:384].rearrange("p (j h two e) -> p j h two e", j=3, h=2, two=2)
                dst = kvo[:, :, 0:128].rearrange("p j (h two e) -> p j h two e", h=2, two=2)
                shp = [128, 3, 2, 32]
                tv = [t[:, 0:192].rearrange("p (j h e) -> p j h e", j=3, h=2) for t in tm[b]]
                cosn = rp[:, 0:32].unsqueeze(1).unsqueeze(1).to_broadcast(shp)
                sinn = rp[:, 32:64].unsqueeze(1).unsqueeze(1).to_broadcast(shp)
                rope(lambda hf, src=src: src[:, :, :, hf, :], lambda hf, dst=dst: dst[:, :, :, hf, :], cosn, sinn, shp, tv,
                     d_kraw[b], d_kvo, d_rp, d_tm[b])
                P.op("act", lambda e, kvo=kvo: e.copy(out=kvo[:, 0, 128:256], in_=zA[:, 384:512]), reads=[dzA], writes=[d_kvo])
                P.op("act", lambda e, kvo=kvo: e.copy(out=kvo[:, 1:3, 128:256], in_=zB[:, 0:256].rearrange("p (j f) -> p j f", j=2)),
                     reads=[dzB], writes=[d_kvo])
                P.dma("sp", lambda e, tt=tt, kvo=kvo: e.dma_start(out=o_cmp[tt * 128:(tt + 1) * 128, :], in_=kvo[:, 0, :]), reads=[d_kvo])
                P.dma("sp", lambda e, tt=tt, kvo=kvo: e.dma_start(out=o_sel[tt * 128:(tt + 1) * 128, :], in_=kvo[:, 1, :]), reads=[d_kvo])
                if tt >= NT - 4:
                    P.dma("sp", lambda e, tt=tt, kvo=kvo: e.dma_start(out=o_win[(tt - NT + 4) * 128:(tt - NT + 5) * 128, :], in_=kvo[:, 2, :]),
                          reads=[d_kvo])
                P.op("pool", lambda e, kvo=kvo, b=b: e.tensor_copy(out=vsb[b][:, :, 0:64], in_=kvo[:, 1, 128:256].rearrange("p (h d) -> p h d", h=2)),
                     reads=[d_kvo], writes=[d_vsb[b]])
                P.dma("sp", lambda e, tt=tt, b=b: e.dma_start(out=vs_d[tt * 128:(tt + 1) * 128, :], in_=vsb[b][:].rearrange("p a b -> p (a b)")),
                      reads=[d_vsb[b]])
                P.op("pool", lambda e, kvo=kvo, b=b: e.tensor_copy(out=vwb[b][:, :, 0:64], in_=kvo[:, 2, 128:256].rearrange("p (h d) -> p h d", h=2)),
                     reads=[d_kvo], writes=[d_vwb[b]])
                P.dma("sp", lambda e, tt=tt, b=b: e.dma_start(out=vw_d[tt * 128:(tt + 1) * 128, :], in_=vwb[b][:].rearrange("p a b -> p (a b)")),
                      reads=[d_vwb[b]])
                for r in range(2):
                    P.op("pool", lambda e, r=r, b=b, kvo=kvo: e.tensor_tensor(out=cw[b][:, r, :], in0=kvo[:, 0, :], in1=wrep[:, r * 256:(r + 1) * 256],
                                                                             op=ALU.mult), reads=[d_kvo, d_wrep], writes=[d_cw[b]])
                srck = kraw[b][:, 384:896].rearrange("p (h two e) -> p h two e", h=4, two=2)
                dstk = rkr[b][:].rearrange("p h (two e) -> p h two e", two=2)
                shpk = [128, 4, 64]
                tvk = [t[:, 0:256].rearrange("p (h e) -> p h e", h=4) for t in tm[b]]
                cosk = rp[:, 64:128].unsqueeze(1).to_broadcast(shpk)
                sink = rp[:, 128:192].unsqueeze(1).to_broadcast(shpk)
                rope(lambda hf, srck=srck: srck[:, :, hf, :], lambda hf, dstk=dstk: dstk[:, :, hf, :], cosk, sink, shpk, tvk,
                     d_kraw[b], d_rkr[b], d_rp, d_tm[b])
                P.op("pool", lambda e, b=b: e.tensor_tensor(out=kd[b][:], in0=rkr[b][:], in1=kdsc[:].unsqueeze(2).to_broadcast([128, 4, 128]),
                                                           op=ALU.mult), reads=[d_rkr[b], d_kdsc], writes=[d_kd[b]])
                P.op("act", lambda e, b=b: e.copy(out=vb[b][:, 0:512], in_=zV0[:, :]), reads=[dzV0], writes=[d_vb[b]])
                P.op("act", lambda e, b=b: e.copy(out=vb[b][:, 512:1024], in_=zV1[:, :]), reads=[dzV1], writes=[d_vb[b]])
                j = tt % 8
                if j == 0:
                    P.op("dve", lambda e: e.tensor_scalar(out=Sown[:], in0=S[:].rearrange("p a b -> p (a b)"), scalar1=oh[:, 0:1], scalar2=None,
                                                          op0=ALU.mult), reads=[d_S, d_oh], writes=[d_Sown])
                else:
                    P.op("dve", lambda e, j=j: e.scalar_tensor_tensor(out=Sown[:], in0=S[:].rearrange("p a b -> p (a b)"), scalar=oh[:, j:j + 1],
                                                                     in1=Sown[:], op0=ALU.mult, op1=ALU.add),
                         reads=[d_S, d_oh, d_Sown], writes=[d_Sown])
                if j == 7:
                    P.dma("sp", lambda e, i=tt // 8: e.dma_start(out=sown_d[i], in_=Sown[:]), reads=[d_Sown])
                def late_ops(tt=tt, b=b, kvo=kvo, d_kvo=d_kvo):
                    P.op("pe", lambda e, kvo=kvo: e.transpose(out=zB[:, 256:384], in_=kvo[:, 1, 0:128], identity=ident[:]),
                         reads=[d_kvo, d_ident], writes=[d_zBt])
                    P.op("pe", lambda e, kvo=kvo: e.transpose(out=zB[:, 384:512], in_=kvo[:, 2, 0:128], identity=ident[:]),
                         reads=[d_kvo, d_ident], writes=[d_zBt])
                    P.op("act", lambda e, b=b: e.copy(out=kst[b][:], in_=zB[:, 256:384]), reads=[d_zBt], writes=[d_kst[b]])
                    P.dma("sp", lambda e, tt=tt, b=b: e.dma_start(out=ksT_d[:, tt * 128:(tt + 1) * 128], in_=kst[b][:]), reads=[d_kst[b]])
                    P.op("act", lambda e, b=b: e.copy(out=kwt[b][:], in_=zB[:, 384:512]), reads=[d_zBt], writes=[d_kwt[b]])
                    P.dma("sp", lambda e, tt=tt, b=b: e.dma_start(out=kwT_d[:, tt * 128:(tt + 1) * 128], in_=kwt[b][:]), reads=[d_kwt[b]])
                    for r in range(2):
                        for kv in range(2):
                            i4 = r * 2 + kv
                            P.op("pe", lambda e, r=r, kv=kv, i4=i4, b=b: e.matmul(out=zC[:, i4 * 8:(i4 + 1) * 8], lhsT=cw[b][:, r, kv * 128:(kv + 1) * 128],
                                                                                 rhs=indab[:], start=True, stop=True),
                                 reads=[d_cw[b], d_inda], writes=[dzC])
                    P.op("act", lambda e, tt=tt: e.copy(out=FS[:, :, tt * 8:(tt + 1) * 8], in_=zC[:, 0:32].rearrange("p (a s) -> p a s", a=4)),
                         reads=[dzC], writes=[d_FS])
                    for h in range(4):
                        zz, dz = pT, d_pTs[h // 2]
                        c0 = h * 256
                        P.op("pe", lambda e, h=h, zz=zz, c0=c0, b=b: e.matmul(out=zz[:, c0:c0 + 256], lhsT=kd[b][:, h, :], rhs=vb[b][:, h * 256:(h + 1) * 256],
                                                                             start=True, stop=True),
                             reads=[d_kd[b], d_vb[b]], writes=[dz])
                        P.op("dve", lambda e, h=h, zz=zz, c0=c0: e.scalar_tensor_tensor(out=S[:, h, :], in0=S[:, h, :], scalar=float(math.exp(128 * LG[h])),
                                                                                       in1=zz[:, c0:c0 + 256], op0=ALU.mult, op1=ALU.add),
                             reads=[d_S, dz], writes=[d_S])

                late.append(late_ops)
            while late:
                late.pop(0)()
            if ntA:
                P.dma("sp", lambda e: e.dma_start(out=o_state.rearrange("h k v -> k h v"), in_=S[:]), reads=[d_S])
                ctmp, cv = tmpS, Sown
                P.op("pool", lambda e: e.memset(cmpKT[:], 0.0), writes=[d_cmpKT])
                P.op("pool", lambda e: e.memset(cv[:], 0.0), reads=[], writes=[d_Sown])
                P.op("dve", lambda e: e.tensor_tensor(out=ctmp[:, 0:1023], in0=FS[:, 0, 0:1023], in1=FS[:, 2, 1:1024], op=ALU.add),
                     reads=[d_FS], writes=[d_tmpS])
                P.op("dve", lambda e: e.tensor_scalar(out=cmpKT[:, 0:1023], in0=ctmp[:, 0:1023], scalar1=bcol[:, 0:1], scalar2=None, op0=ALU.add),
                     reads=[d_tmpS, d_bcol], writes=[d_cmpKT])
                P.op("dve", lambda e: e.tensor_tensor(out=ctmp[:, 0:1023], in0=FS[:, 1, 0:1023], in1=FS[:, 3, 1:1024], op=ALU.add),
                     reads=[d_FS], writes=[d_tmpS])
                P.op("dve", lambda e: e.tensor_scalar(out=cv[:, 0:1023], in0=ctmp[:, 0:1023], scalar1=bcol[:, 1:2], scalar2=None, op0=ALU.add),
                     reads=[d_tmpS, d_bcol], writes=[d_Sown])
                for jt in range(8):
                    P.op("pe", lambda e, jt=jt: e.transpose(out=pT[:, jt * 128:(jt + 1) * 128], in_=cv[:, jt * 128:(jt + 1) * 128], identity=ident[:]),
                         reads=[d_Sown, d_ident], writes=[d_pTs[jt // 4]])
                P.op("act", lambda e: e.copy(out=cmpVA[:, :, :, 0:64], in_=pT[:].rearrange("p (j h d) -> p j h d", j=8, h=2)),
                     reads=d_pTs, writes=[d_cmpVA])
            P.barrier()

        if stage >= 2:
            phase_b1()
        if stage >= 4:
            phase_s1()
        if stage >= 3:
            groups = []
            for gq in range(4):
                def load_a(hT4, d_hT4, onT4, d_onT4, yrT4, d_yrT4, gq=gq):
                    for j in range(4):
                        i_ = 4 * gq + j
                        P.dma("sp", lambda e, i_=i_, j=j: e.dma_start(out=hT4[:, :, j * 128:(j + 1) * 128], in_=hT_d[i_]), writes=[d_hT4])
                        P.dma("sp", lambda e, i_=i_, j=j: e.dma_start(out=onT4[:, :, j * 128:(j + 1) * 128], in_=onT_d[i_]), writes=[d_onT4])
                        P.dma("sp", lambda e, i_=i_, j=j: e.dma_start(out=yrT4[:, :, j * 128:(j + 1) * 128], in_=yrT_d[i_]), writes=[d_yrT4])

                def tiles_a(gtaS, d_gtaS, gq=gq):
                    return [(x_own[(4 * gq + j) * 128:(4 * gq + j + 1) * 128, :], 128, modP4[:, 1, :], d_modP,
                             x1_d[(4 * gq + j) * 128:(4 * gq + j + 1) * 128, :]) for j in range(4)]

                def tiles_b(modS3, d_modS3, gq=gq):
                    return [(x1_d[(4 * gq + j) * 128:(4 * gq + j + 1) * 128, :], 128, G2P, d_G2P, modP4[:, 2, :], d_modP, modP4[:, 3, :], d_modP,
                             y_own[(4 * gq + j) * 128:(4 * gq + j + 1) * 128, :], shcol[:, 1, :]) for j in range(4)]
                groups.append({"ntok": 512, "load_a": load_a, "tiles_a": tiles_a, "tiles_b": tiles_b})
            if stage >= 4:
                def load_as(hT4, d_hT4, onT4, d_onT4, yrT4, d_yrT4):
                    P.dma("sp", lambda e: e.dma_start(out=hT4[:, :, 0:SB], in_=hTs_d), writes=[d_hT4])
                    P.dma("sp", lambda e: e.dma_start(out=onT4[:, :, 0:SB], in_=onTs_d), writes=[d_onT4])
                    P.dma("sp", lambda e: e.dma_start(out=yrT4[:, :, 0:SB], in_=yrTs_d), writes=[d_yrT4])
                groups.append({"ntok": SB, "load_a": load_as,
                               "tiles_a": lambda gtaS, d_gtaS: [(x_smp, SB, gtaS[:], d_gtaS, x1s_d)],
                               "tiles_b": lambda modS3, d_modS3: [(x1s_d, SB, modS3[:, 1, :], d_modS3, modS3[:, 0, :], d_modS3, modS3[:, 2, :], d_modS3, y_smp, None)]})
            phase_b2(groups)

        P.barrier()
        P.emit()
    return nc


def _rope_tables(pos):
    pos = np.asarray(pos, np.float32)
    out = np.zeros((len(pos), 192), np.float32)
    inv32 = (10000.0 ** (-np.arange(0, 64, 2, dtype=np.float32) / 64)).astype(np.float32)
    inv64 = (10000.0 ** (-np.arange(0, 128, 2, dtype=np.float32) / 128)).astype(np.float32)
    a32 = pos[:, None] * inv32[None, :]
    a64 = pos[:, None] * inv64[None, :]
    out[:, 0:32] = np.cos(a32)
    out[:, 32:64] = np.sin(a32)
    out[:, 64:128] = np.cos(a64)
    out[:, 128:192] = np.sin(a64)
    return out


def _consts(core):
    cs = {}
    cs["ident"] = np.eye(128, dtype=np.float32)
    sel5 = np.zeros((5, 132), np.float32)
    sel5[0, 0:128] = 1.0
    for s in range(4):
        sel5[1 + s, 128 + s] = 1.0
    cs["sel5"] = sel5
    inda = np.zeros((128, 8), np.float32)
    inda[np.arange(128), np.arange(128) // 16] = 1.0
    cs["inda"] = inda
    i = np.arange(128, dtype=np.float64)
    kdsc = np.zeros((128, 4), np.float32)
    for h in range(4):
        kdsc[:, h] = RET_SCALE * np.exp((127.0 - i) * LG[h])
    cs["kdsc"] = kdsc
    oh = np.zeros((128, 8), np.float32)
    oh[:, core] = 1.0
    cs["oh"] = oh
    return cs


def _core_tables(c):
    q = np.arange(128)
    n = np.arange(1024)
    j = np.arange(256)
    k = np.arange(128)
    cmask = np.zeros((NB, 128, 1024), np.float32)
    forceb = np.zeros((NB, 128, 256), np.float32)
    for i in range(NB):
        t = 8 * i + c
        qpos = 128 * t + q
        ok = (16 * n[None, :] + 31 <= qpos[:, None]) & (n[None, :] <= 1022)
        cmask[i] = np.where(ok, 0.0, NEG)
        valid = 64 * j[None, :] <= qpos[:, None]
        qb = qpos[:, None] // 64
        forced = (j[None, :] == 0) | (j[None, :] == qb) | (j[None, :] == qb - 1)
        forceb[i] = np.where(valid, np.where(forced, 1.0e4, 0.0), -1.0e30)
    dmask = np.zeros((128, 8, 128), np.float32)
    for jj in range(8):
        if jj == c:
            dmask[:, jj, :] = np.where(k[None, :] <= q[:, None], 0.0, NEG)
        elif jj > c:
            dmask[:, jj, :] = NEG
    wmask = np.zeros((128, 24, 128), np.float32)
    for v in range(2):
        for m in range(12):
            diff = 128 * (m - 4 - c) + k[None, :] - q[:, None]
            ok = (diff <= 0) & (diff > -512)
            if v == 0 and m < 4:
                ok = np.zeros_like(ok)
            wmask[:, v * 12 + m, :] = np.where(ok, 0.0, NEG)
    return {"cmask": cmask, "forceb": forceb, "dmask": dmask.reshape(128, 1024), "wmask": wmask.reshape(128, 24 * 128)}


def _shared_tables():
    nl = np.arange(128)
    blk = np.arange(256)
    ovp = np.zeros((128, 8, 256), np.float32)
    for jt in range(8):
        n = 128 * jt + nl
        ovp[:, jt, :] = ((4 * blk[None, :] - 1 <= n[:, None]) & (n[:, None] <= 4 * blk[None, :] + 3)).astype(np.float32)
    i = np.arange(128)
    dtab = np.zeros((128, 4, 128), np.float32)
    qdt = np.zeros((128, 4, 128), np.float32)
    for h in range(4):
        diff = i[None, :] - i[:, None]
        dtab[:, h, :] = np.where(diff >= 0, np.exp(np.maximum(diff, 0) * LG[h]), 0.0)
        qdt[:, h, :] = np.exp((i[None, :] + 1.0) * LG[h])
    p = np.arange(128)
    ovs = np.zeros((128, 8, 257), np.float32)
    for s_ in range(8):
        n = 8 * p + s_
        ovs[:, s_, 0:256] = ((4 * blk[None, :] - 1 <= n[:, None]) & (n[:, None] <= 4 * blk[None, :] + 3)).astype(np.float32)
    ovs[:, :, 256] = 1.0
    smallc = np.zeros((1, 1024), np.float32)
    smallc[0, 127] = NEG
    smallc[0, 128] = NEG
    for b in range(4):
        for s_ in range(4):
            smallc[0, 256 + 4 * b + s_] = 0.0 if s_ == b else NEG
    smallc[0, 272:276] = 1.0
    smallc[0, 276 + 0] = 1.0e4
    smallc[0, 276 + 255] = 1.0e4
    ohm = np.zeros((128, 4, 4), np.float32)
    for b in range(4):
        ohm[:, b, b] = 1.0
    return {"ovp": ovp.reshape(128, 2048), "dtab": dtab.reshape(128, 512), "qdt": qdt.reshape(128, 512),
            "ovs": ovs.reshape(128, 8 * 257), "smallc": smallc, "ohm": ohm.reshape(128, 16), "oh4": np.eye(4, dtype=np.float32),
            "rope_smp": _rope_tables(np.full(4, P_PAST))}


def _prep(inputs, stage=99):
    f = lambda a: np.ascontiguousarray(np.asarray(a, dtype=np.float32))
    x_all = f(inputs["x_prompt"]).reshape(T, D)
    w_cmp = f(inputs["w_cmp"])[0]
    b_cmp = f(inputs["b_cmp"])[0]
    wrep = np.zeros((128, 512), np.float32)
    pm = np.arange(128) % 16
    for r in range(2):
        blk = w_cmp[:, :, r * 16 + pm, :]
        wrep[:, r * 256:(r + 1) * 256] = blk.transpose(2, 0, 1, 3).reshape(128, 256)
    bcol = np.ascontiguousarray(b_cmp.reshape(2, 128).T)
    rope_all = _rope_tables(np.arange(T))
    shared = {
        "x_all": x_all, "rope_all": rope_all,
        "w_ada": f(inputs["w_ada"])[0], "b_ada": f(inputs["b_ada"]).reshape(1, 6 * D),
        "w_in": f(inputs["w_in"])[0],
        "norm_mix": f(inputs["norm_mix"]).reshape(1, D), "norm_mlp": f(inputs["norm_mlp"]).reshape(1, D),
        "wrep": wrep, "bcol": bcol,
        "w_bn": f(inputs["w_branch_nsa"])[0], "w_br": f(inputs["w_branch_ret"])[0], "w_out": f(inputs["w_out"])[0],
        "w_up": f(inputs["w_up"])[0], "w_down": f(inputs["w_down"])[0], "norm_final": f(inputs["norm_final"]).reshape(1, D),
        "gnw": f(inputs["ret_gn_w"]).reshape(1, D), "gnb": f(inputs["ret_gn_b"]).reshape(1, D),
    }
    shared.update(_shared_tables())
    shared["wrow"] = np.ascontiguousarray(w_cmp.reshape(2, 2, 2, 16, 64).transpose(2, 3, 0, 1, 4).reshape(1, 2 * 16 * 256))
    shared["brow"] = np.ascontiguousarray(b_cmp.reshape(1, 256))
    shared["cache_cmp"] = f(inputs["cache_cmp_kv"]).reshape(5120 * 8, 4096)
    shared["cache_sel"] = f(inputs["cache_sel_kv"]).reshape(5120 * 8, 4096)
    ptab_all = np.ascontiguousarray(np.asarray(inputs["page_table"], dtype=np.int32))
    cwin = f(inputs["cache_win_kv"])[0].reshape(32, 512, 256)
    sret = f(inputs["state_ret"])[0]
    maps = []
    cp = f(inputs["c_prompt"]).reshape(1, D)
    cs_ = f(inputs["c_sample"])
    for c in range(NCORE):
        m = dict(shared)
        m.update(_consts(c))
        m.update(_core_tables(c))
        tiles = [8 * i + c for i in range(NB)]
        m["x_own"] = np.ascontiguousarray(x_all.reshape(NT, 128, D)[tiles].reshape(NB * 128, D))
        m["rope_own"] = np.ascontiguousarray(rope_all.reshape(NT, 128, 192)[tiles].reshape(NB * 128, 192))
        m["ptab"] = ptab_all[4 * c:4 * c + 4]
        m["cache_win"] = cwin[4 * c:4 * c + 4]
        m["state_s"] = sret[4 * c:4 * c + 4]
        m["x_smp"] = f(inputs["x_sample"]).reshape(32, D)[4 * c:4 * c + 4]
        m["c_all"] = np.ascontiguousarray(np.concatenate([cp, cs_[4 * c:4 * c + 4]], axis=0).reshape(5, 8, 128).transpose(2, 1, 0).reshape(128, 40))
        maps.append(m)
    return maps


_NC_CACHE = {}


def run(inputs, stage=99):
    if stage not in _NC_CACHE:
        _NC_CACHE[stage] = build(stage)
    nc = _NC_CACHE[stage]
    maps = _prep(inputs, stage)
    res = run_bass_kernel_spmd(nc, maps, core_ids=list(range(NCORE)))
    return res.results


def kernel(**inputs):
    r = run(inputs)
    r0 = r[0]
    y = np.zeros((NT, 128, D), np.float32)
    for c in range(NCORE):
        yo = np.asarray(r[c]["y_own"]).reshape(NB, 128, D)
        for i in range(NB):
            y[8 * i + c] = yo[i]
    y_prompt = y.reshape(1, T, D)
    y_sample = np.concatenate([np.asarray(r[c]["y_smp"]) for c in range(NCORE)], axis=0).reshape(32, 1, D)
    new_cmp_p = np.asarray(r0["o_cmp"]).reshape(1, 1, T, 2, 2, 64)
    new_sel_p = np.asarray(r0["o_sel"]).reshape(1, 1, T, 2, 2, 64)
    new_win_p = np.asarray(r0["o_win"]).reshape(1, 1, 512, 2, 2, 64)
    new_state_p = np.asarray(r0["o_state"]).reshape(1, 1, 4, 128, 256)
    cat = lambda k: np.concatenate([np.asarray(r[c][k]) for c in range(NCORE)], axis=0)
    new_cmp_s = cat("o_cmp_s").reshape(1, 32, 1, 2, 2, 64)
    new_sel_s = cat("o_sel_s").reshape(1, 32, 1, 2, 2, 64)
    new_win_s = cat("o_win_s").reshape(1, 32, 512, 2, 2, 64)
    new_state_s = cat("o_state_s").reshape(1, 32, 4, 128, 256)
    return (y_prompt, y_sample, new_cmp_p, new_sel_p, new_win_p, new_state_p, new_cmp_s, new_sel_s, new_win_s, new_state_s)
```

```python
import math
import os
from contextlib import ExitStack

import numpy as np
import ml_dtypes

import concourse.bass as bass
import concourse.mybir as mybir
from concourse.bass_utils import run_bass_kernel_spmd

F32 = mybir.dt.float32
BF16 = mybir.dt.bfloat16
I32 = mybir.dt.int32
ALU = mybir.AluOpType
ACTF = mybir.ActivationFunctionType
AX = mybir.AxisListType

ENGS = ("pe", "act", "dve", "pool", "sp")
NSLOT = 12

D = 1024
T = 16384
NT = T // 128
NB = 16
NCORE = 8
SB = 4
P_PAST = 16384
W_IN = 6424
C_Q, C_KVC, C_KVS, C_KVW, C_G, C_RQ, C_RK, C_RV, C_RG, C_GA, C_GB = (
    0, 512, 768, 1024, 1280, 1304, 1816, 2328, 3352, 4376, 5400)
NEG = -30000.0
RMS_EPS = 1e-6
GN_EPS = 1e-5
LG = [math.log1p(-2.0 ** (-5.0 - h)) for h in range(4)]
RET_SCALE = 128 ** -0.5


class Dep:
    __slots__ = ("w", "r", "excl")

    def __init__(self, excl=False):
        self.w = None
        self.r = []
        self.excl = excl


class Prog:
    def __init__(self, nc, stack):
        self.nc = nc
        self.stack = stack
        self.ops = {e: [] for e in ENGS}
        self.cnt = {e: 0 for e in ENGS}
        self.esem = {e: stack.enter_context(nc.semaphore("es_" + e)) for e in ENGS}
        self.dq = {}
        for q in ("sp", "act", "pool"):
            sl = [stack.enter_context(nc.semaphore("ds_%s_%d" % (q, i))) for i in range(NSLOT)]
            self.dq[q] = {"sems": sl, "cnt": [0] * NSLOT, "next": 0}
        self.waited = {e: {} for e in ENGS}
        self.nuid = 0

    def sb(self, shape, dt=F32, stack=None):
        self.nuid += 1
        return (stack or self.stack).enter_context(self.nc.sbuf_tensor("sb%d" % self.nuid, list(shape), dt))

    def ps(self, shape, dt=F32, stack=None):
        self.nuid += 1
        return (stack or self.stack).enter_context(self.nc.psum_tensor("ps%d" % self.nuid, list(shape), dt))

    def _need(self, eng, tok, waits):
        if tok is None:
            return
        kind, key, val = tok
        if kind == "e" and key == eng and eng == "pe":
            return
        k = (kind, key)
        if self.waited[eng].get(k, 0) >= val:
            return
        self.waited[eng][k] = val
        waits[k] = max(waits.get(k, 0), val)

    def _gather(self, eng, reads, writes):
        waits = {}
        for d in reads:
            self._need(eng, d.w, waits)
            if d.excl:
                for t in d.r:
                    if not (t[0] == "e" and t[1] == eng):
                        self._need(eng, t, waits)
        for d in writes:
            self._need(eng, d.w, waits)
            for t in d.r:
                self._need(eng, t, waits)
        return waits

    def _commit(self, tok, reads, writes):
        for d in reads:
            if tok[0] == "e":
                d.r = [t for t in d.r if not (t[0] == "e" and t[1] == tok[1])]
            d.r.append(tok)
        for d in writes:
            d.w = tok
            d.r = []

    def _semof(self, k):
        kind, key = k
        if kind == "e":
            return self.esem[key]
        q, i = key
        return self.dq[q]["sems"][i]

    def op(self, eng, fn, reads=(), writes=()):
        waits = self._gather(eng, reads, writes)
        self.cnt[eng] += 1
        tok = ("e", eng, self.cnt[eng])
        self.ops[eng].append(([(self._semof(k), v) for k, v in waits.items()], fn, (self.esem[eng], 1)))
        self._commit(tok, reads, writes)
        return tok

    def dma(self, q, fn, reads=(), writes=()):
        st = self.dq[q]
        i = st["next"]
        st["next"] = (i + 1) % NSLOT
        waits = self._gather(q, reads, writes)
        if st["cnt"][i] > 0:
            self._need(q, ("d", (q, i), st["cnt"][i]), waits)
        st["cnt"][i] += 16
        tok = ("d", (q, i), st["cnt"][i])
        self.ops[q].append(([(self._semof(k), v) for k, v in waits.items()], fn, (st["sems"][i], 16)))
        self._commit(tok, reads, writes)
        return tok

    def barrier(self):
        toks = [("e", e, self.cnt[e]) for e in ENGS if self.cnt[e] > 0]
        for q, st in self.dq.items():
            for i in range(NSLOT):
                if st["cnt"][i] > 0:
                    toks.append(("d", (q, i), st["cnt"][i]))
        for eng in ENGS:
            waits = {}
            for t in toks:
                if t[0] == "e" and t[1] == eng:
                    continue
                self._need(eng, t, waits)
            if waits:
                self.ops[eng].append(([(self._semof(k), v) for k, v in waits.items()], None, None))

    def emit(self):
        nc = self.nc
        ops = self.ops
        with nc.Block() as block:
            def run(handle, lst):
                for waits, fn, inc in lst:
                    for sem, v in waits:
                        handle.wait_ge(sem, v)
                    if fn is not None:
                        fn(handle).then_inc(inc[0], inc[1])

            @block.tensor
            def _(e):
                run(e, ops["pe"])

            @block.scalar
            def _(e):
                run(e, ops["act"])

            @block.vector
            def _(e):
                run(e, ops["dve"])

            @block.gpsimd
            def _(e):
                run(e, ops["pool"])

            @block.sync
            def _(e):
                run(e, ops["sp"])


def bc(ap, shape):
    return ap.to_broadcast(list(shape))


def build(stage=99):
    nc = bass.Bass("TRN2", target_bir_lowering=False)

    def din(name, shape, dt=F32):
        return nc.dram_tensor(name, list(shape), dt, kind="ExternalInput").ap()

    def dout(name, shape, dt=F32):
        return nc.dram_tensor(name, list(shape), dt, kind="ExternalOutput").ap()

    def dscr(name, shape, dt=F32):
        return nc.dram_tensor(name, list(shape), dt, kind="Internal").ap()

    x_all = din("x_all", [T, D])
    c_all = din("c_all", [128, 40])
    rope_all = din("rope_all", [T, 192])
    w_ada = din("w_ada", [D, 6 * D])
    b_ada = din("b_ada", [1, 6 * D])
    w_in = din("w_in", [D, W_IN])
    norm_mix = din("norm_mix", [1, D])
    norm_mlp = din("norm_mlp", [1, D])
    ident_d = din("ident", [128, 128])
    sel5_d = din("sel5", [5, 132])
    inda_d = din("inda", [128, 8])
    wrep_d = din("wrep", [128, 512])
    bcol_d = din("bcol", [128, 2])
    kdsc_d = din("kdsc", [128, 4])
    oh_d = din("oh", [128, 8])

    x_own = din("x_own", [NB * 128, D])
    rope_own = din("rope_own", [NB * 128, 192])
    cmask_d = din("cmask", [NB, 128, 1024])
    forceb_d = din("forceb", [NB, 128, 256])
    dmask_d = din("dmask", [128, 8 * 128])
    wmask_d = din("wmask", [128, 24 * 128])
    ovp_d = din("ovp", [128, 8 * 256])
    dtab_d = din("dtab", [128, 512])
    qdt_d = din("qdt", [128, 512])
    gnw_d = din("gnw", [1, D])
    gnb_d = din("gnb", [1, D])
    DBG = (stage == 2)
    DBG_NBLK = int(os.environ.get('K_NBLK', '3'))
    PART = int(os.environ.get('K_PART', '9'))
    SUB = int(os.environ.get('K_SUB', '9'))
    SUB2 = int(os.environ.get('K_SUB2', '9'))
    hT_d = dscr("hT_d", [NB, 128, 8, 128], BF16)
    yrT_d = dscr("yrT_d", [NB, 128, 8, 128], BF16)
    onT_d = dscr("onT_d", [NB, 128, 4, 128], BF16)
    if DBG:
        dbg_selb = dout("dbg_selb", [NB, 2, 128, 256])
        dbg_on = dout("dbg_on", [NB, 128, 512])
        dbg_yr = dout("dbg_yr", [NB, 128, 1024])

    w_bn = din("w_bn", [512, D])
    w_br = din("w_br", [D, D])
    w_out = din("w_out", [D, D])
    w_up = din("w_up", [D, 4 * D])
    w_down = din("w_down", [4 * D, D])
    norm_final_d = din("norm_final", [1, D])
    x_smp = din("x_smp", [SB, D])
    x1_d = dscr("x1_d", [NB * 128, D])
    x1s_d = dscr("x1s_d", [SB, D])
    hTs_d = dscr("hTs_d", [128, 8, SB], BF16)
    onTs_d = dscr("onTs_d", [128, 4, SB], BF16)
    yrTs_d = dscr("yrTs_d", [128, 8, SB], BF16)
    y_own = dout("y_own", [NB * 128, D])
    y_smp = dout("y_smp", [SB, D])

    smallc_d = din("smallc", [1, 1024])
    oh4_d = din("oh4", [SB, 4])
    ohm_d = din("ohm", [128, 16])
    ovs_d = din("ovs", [128, 8 * 257])
    wrow_d = din("wrow", [1, 2 * 16 * 256])
    brow_d = din("brow", [1, 256])
    rope_smp = din("rope_smp", [SB, 192])
    ptab = din("ptab", [SB, 128], I32)
    cache_cmp = din("cache_cmp", [5120 * 8, 4096])
    cache_sel = din("cache_sel", [5120 * 8, 4096])
    cache_win = din("cache_win", [SB, 512, 256])
    state_s = din("state_s", [SB, 4, 128, 256])
    os_d = dscr("os_d", [SB, 3, 8, 64])
    dbg_ons = dout("dbg_ons", [SB, 512])
    o_cmp_s = dout("o_cmp_s", [SB, 256])
    o_sel_s = dout("o_sel_s", [SB, 256])
    o_win_s = dout("o_win_s", [SB, 512, 256])
    o_state_s = dout("o_state_s", [SB, 4, 128, 256])

    o_cmp = dout("o_cmp", [T, 256])
    o_sel = dout("o_sel", [T, 256])
    o_win = dout("o_win", [512, 256])
    o_state = dout("o_state", [4, 128, 256])

    kwT_d = dscr("kwT_d", [128, T], BF16)
    vw_d = dscr("vw_d", [T, 130], BF16)
    ksT_d = dscr("ksT_d", [128, T], BF16)
    vs_d = dscr("vs_d", [T, 130], BF16)
    sown_d = dscr("sown_d", [NB, 128, 1024])
    mods_d = dscr("mods_d", [SB, 6 * D])

    with ExitStack() as st:
        P = Prog(nc, st)
        ident = P.sb([128, 128]); d_ident = Dep()
        identb = P.sb([128, 128], BF16); d_identb = Dep()
        modP4 = P.sb([128, 4, D]); d_modP = Dep()
        G1P = P.sb([128, D]); d_G1P = Dep()
        G2P = P.sb([128, D]); d_G2P = Dep()
        cmpKT = P.sb([128, 1024], BF16); d_cmpKT = Dep()
        cmpVA = P.sb([128, 8, 2, 65], BF16); d_cmpVA = Dep()
        epsT = P.sb([128, 1]); d_eps = Dep()

        banks = [P.ps([128, 512]) for _ in range(6)]
        pT = P.ps([128, 1024]); d_pTs = [Dep(True), Dep(True)]
        d_bank = [Dep(True) for _ in range(6)]

        P.dma("sp", lambda e: e.dma_start(out=ident[:], in_=ident_d), writes=[d_ident])
        P.op("dve", lambda e: e.tensor_copy(out=identb[:], in_=ident[:]), reads=[d_ident], writes=[d_identb])
        P.op("dve", lambda e: e.memset(epsT[:], RMS_EPS), writes=[d_eps])
        P.op("pool", lambda e: e.memset(cmpVA[:].rearrange("p a b c -> p (a b c)"), 1.0), writes=[d_cmpVA])

        with ExitStack() as s0:
            cT = P.sb([128, 8, 5], F32, s0); d_cT = Dep()
            scT = P.sb([128, 8, 5], BF16, s0); d_scT = Dep()
            sel5 = P.sb([5, 132], F32, s0); d_sel5 = Dep()
            mod5 = P.sb([5, 6 * D], F32, s0); d_mod5 = Dep()
            bada = P.sb([5, 6 * D], F32, s0); d_bada = Dep()
            gmix = P.sb([128, D], F32, s0); d_gmix = Dep()
            gmlp = P.sb([128, D], F32, s0); d_gmlp = Dep()
            wa = [P.sb([128, 8, 512], BF16, s0) for _ in range(2)]; d_wa = [Dep(), Dep()]

            P.dma("sp", lambda e: e.dma_start(out=cT[:].rearrange("p c b -> p (c b)"), in_=c_all), writes=[d_cT])
            P.dma("sp", lambda e: e.dma_start(out=sel5[:], in_=sel5_d), writes=[d_sel5])
            P.dma("sp", lambda e: e.dma_start(out=bada[:], in_=b_ada.partition_broadcast(5)[:, 0, :]), writes=[d_bada])
            P.dma("sp", lambda e: e.dma_start(out=gmix[:], in_=norm_mix.partition_broadcast(128)[:, 0, :]), writes=[d_gmix])
            P.dma("sp", lambda e: e.dma_start(out=gmlp[:], in_=norm_mlp.partition_broadcast(128)[:, 0, :]), writes=[d_gmlp])
            P.op("act", lambda e: e.activation(out=scT[:], in_=cT[:], func=ACTF.Silu), reads=[d_cT], writes=[d_scT])
            for g in range(12):
                b = g % 2
                P.dma("pool", lambda e, g=g, b=b: e.dma_start(
                    out=wa[b][:], in_=w_ada[:, g * 512:(g + 1) * 512].rearrange("(c p) n -> p c n", p=128)),
                    writes=[d_wa[b]])
                for k in range(8):
                    P.op("pe", lambda e, k=k, b=b: e.matmul(out=banks[0][0:5, :], lhsT=scT[:, k, :], rhs=wa[b][:, k, :],
                                                           start=(k == 0), stop=(k == 7)),
                         reads=[d_scT, d_wa[b]], writes=[d_bank[0]])
                P.op("dve", lambda e, g=g: e.tensor_tensor(out=mod5[:, g * 512:(g + 1) * 512], in0=banks[0][0:5, :],
                                                          in1=bada[:, g * 512:(g + 1) * 512], op=ALU.add),
                     reads=[d_bank[0], d_bada], writes=[d_mod5])
            P.dma("sp", lambda e: e.dma_start(out=mods_d, in_=mod5[1:5, :]), reads=[d_mod5])
            slot = {0: 0, 2: 1, 3: 2, 5: 3}
            for g in range(12):
                ch, hf_ = g // 2, (g % 2) * 512
                P.op("pe", lambda e, g=g: e.matmul(out=banks[1][:, :], lhsT=sel5[:, 0:128], rhs=mod5[:, g * 512:(g + 1) * 512],
                                                   start=True, stop=True), reads=[d_sel5, d_mod5], writes=[d_bank[1]])
                if ch in slot:
                    P.op("act", lambda e, ch=ch, hf_=hf_: e.copy(out=modP4[:, slot[ch], hf_:hf_ + 512], in_=banks[1][:, :]),
                         reads=[d_bank[1]], writes=[d_modP])
                else:
                    Gt, dG, gn, dgn = (G1P, d_G1P, gmix, d_gmix) if ch == 1 else (G2P, d_G2P, gmlp, d_gmlp)
                    P.op("dve", lambda e, Gt=Gt, gn=gn, hf_=hf_: e.scalar_tensor_tensor(out=Gt[:, hf_:hf_ + 512], in0=banks[1][:, :], scalar=1.0,
                                                                                         in1=gn[:, hf_:hf_ + 512], op0=ALU.add, op1=ALU.mult),
                         reads=[d_bank[1], dgn], writes=[dG])
            P.barrier()

        shcol = P.sb([128, 2, 8], F32); d_shcol = Dep()
        for si, mi in enumerate((0, 2)):
            for c in range(8):
                P.op("pe", lambda e, c=c, mi=mi: e.transpose(out=pT[:, c * 128:(c + 1) * 128], in_=modP4[:, mi, c * 128:(c + 1) * 128], identity=ident[:]),
                     reads=[d_modP, d_ident], writes=[d_pTs[c // 4]])
            P.op("act", lambda e, si=si: e.copy(out=shcol[:, si, :], in_=pT[:].rearrange("p (c t) -> p c t", t=128)[:, :, 0]), reads=d_pTs, writes=[d_shcol])

        def norm_mod_T(xt, d_xt, ntok, G, d_G, shift, d_shift, wk, hT, d_hT, shift_col=None, d_shift_col=None):
            junk, d_junk, ss, d_ss, rstd, d_rstd, hm, d_hm, hf, d_hf = wk
            P.op("act", lambda e: e.activation(out=junk[0:ntok, :], in_=xt[0:ntok, :], func=ACTF.Square, accum_out=ss[0:ntok, :]),
                 reads=[d_xt], writes=[d_junk, d_ss])
            P.op("act", lambda e: e.activation(out=rstd[0:ntok, :], in_=ss[0:ntok, :], func=ACTF.Sqrt, bias=epsT[0:ntok, :], scale=1.0 / D),
                 reads=[d_ss, d_eps], writes=[d_rstd])
            P.op("dve", lambda e: e.reciprocal(out=rstd[0:ntok, :], in_=rstd[0:ntok, :]), reads=[d_rstd], writes=[d_rstd])
            P.op("dve", lambda e: e.scalar_tensor_tensor(out=hm[0:ntok, :], in0=xt[0:ntok, :], scalar=rstd[0:ntok, 0:1], in1=G[0:ntok, :],
                                                        op0=ALU.mult, op1=ALU.mult),
                 reads=[d_xt, d_rstd, d_G], writes=[d_hm])
            if shift_col is None:
                P.op("pool", lambda e: e.tensor_tensor(out=hf[0:ntok, :], in0=hm[0:ntok, :], in1=shift[0:ntok, :], op=ALU.add),
                     reads=[d_hm, d_shift], writes=[d_hf])
            for c in range(8):
                P.op("pe", lambda e, c=c: e.transpose(out=pT[:, c * 128:c * 128 + ntok], in_=hf[0:ntok, c * 128:(c + 1) * 128],
                                                      identity=ident[0:ntok, 0:ntok]),
                     reads=[d_hf, d_ident], writes=[d_pTs[c // 4]])
            pv = pT[:].rearrange("p (c t) -> p c t", t=128)
            if shift_col is None:
                P.op("act", lambda e: e.copy(out=hT[:, 0:4, 0:ntok], in_=pv[:, 0:4, 0:ntok]), reads=[d_pTs[0]], writes=[d_hT])
                P.op("dve", lambda e: e.tensor_copy(out=hT[:, 4:8, 0:ntok], in_=pv[:, 4:8, 0:ntok]), reads=[d_pTs[1]], writes=[d_hT])
            else:
                for c in range(8):
                    P.op("act", lambda e, c=c: e.activation(out=hT[:, c, 0:ntok], in_=pv[:, c, 0:ntok], func=ACTF.Identity, bias=shift_col[:, c:c + 1]),
                         reads=[d_pTs[c // 4], d_shift_col], writes=[d_hT])

        junk_sh = P.sb([128, D], BF16); d_junk_sh = Dep()

        def mk_wk(stack, n=128):
            hm_ = P.sb([n, D], F32, stack); d_hm_ = Dep()
            return (junk_sh, d_junk_sh, P.sb([n, 1], F32, stack), Dep(), P.sb([n, 1], F32, stack), Dep(),
                    hm_, d_hm_, hm_, d_hm_)

        def rope(src4, dst4, cos, sin, shp, tmps, d_src, d_dst, d_rp, d_tmps):
            ta, tb, tc, td = tmps
            cb, sbb = cos, sin
            P.op("dve", lambda e: e.tensor_tensor(out=ta, in0=src4(0), in1=cb, op=ALU.mult), reads=[d_src, d_rp], writes=[d_tmps[0]])
            P.op("dve", lambda e: e.tensor_tensor(out=tb, in0=src4(1), in1=sbb, op=ALU.mult), reads=[d_src, d_rp], writes=[d_tmps[1]])
            P.op("dve", lambda e: e.tensor_tensor(out=tc, in0=src4(0), in1=sbb, op=ALU.mult), reads=[d_src, d_rp], writes=[d_tmps[2]])
            P.op("dve", lambda e: e.tensor_tensor(out=td, in0=src4(1), in1=cb, op=ALU.mult), reads=[d_src, d_rp], writes=[d_tmps[3]])
            P.op("pool", lambda e: e.tensor_tensor(out=dst4(0), in0=ta, in1=tb, op=ALU.subtract), reads=[d_tmps[0], d_tmps[1]], writes=[d_dst])
            P.op("pool", lambda e: e.tensor_tensor(out=dst4(1), in0=tc, in1=td, op=ALU.add), reads=[d_tmps[2], d_tmps[3]], writes=[d_dst])

        def make_attn(pts, d_pts):
            scb = [banks[0], banks[1]]
            d_scb = [d_bank[0], d_bank[1]]
            slot = [0]
            pend = []

            def flush():
                while pend:
                    pend.pop(0)()

            def attn(nq, ng, QTh, d_Q, tiles, o4v, d_o4, pslc=None, first=True, last=True, pvT=False):
                ncols = nq * ng
                nt_ = len(tiles)
                for ti, tl in enumerate(tiles):
                    s_ = slot[0]; slot[0] ^= 1
                    nk = tl["nk"]
                    sc, d_sc, pt, d_pt = scb[s_], d_scb[s_], pts[s_], d_pts[s_]
                    nb = len(tl["biases"])
                    P.op("pe", lambda e, sc=sc, tl=tl, nk=nk, nb=nb: e.matmul(out=sc[0:nk, 0:ncols], lhsT=tl["KT"], rhs=QTh,
                                                                           start=True, stop=(nb == 0)),
                         reads=[d_Q] + tl["deps"], writes=[d_sc])
                    for bi, (bl, br_) in enumerate(tl["biases"]):
                        P.op("pe", lambda e, sc=sc, bl=bl, br_=br_, nk=nk, bi=bi, nb=nb: e.matmul(
                            out=sc[0:nk, 0:ncols], lhsT=bl, rhs=br_, start=False, stop=(bi == nb - 1)),
                            reads=tl["deps"], writes=[d_sc])
                    P.op("act", lambda e, sc=sc, pt=pt, nk=nk: e.activation(out=pt[0:nk, 0:ncols], in_=sc[0:nk, 0:ncols], func=ACTF.Exp),
                         reads=[d_sc], writes=[d_pt])

                    def pv(pt=pt, d_pt=d_pt, tl=tl, nk=nk, ti=ti):
                        if pvT:
                            P.op("pe", lambda e: e.matmul(out=o4v, lhsT=tl["V"], rhs=pt[0:nk, 0:ncols], start=(first and ti == 0), stop=(last and ti == nt_ - 1)),
                                 reads=[d_pt] + tl["deps"], writes=[d_o4])
                        else:
                            for g in range(ng):
                                P.op("pe", lambda e, g=g: e.matmul(
                                    out=o4v[:, g, :], lhsT=pt[0:nk, g * nq:(g + 1) * nq], rhs=tl["V"],
                                    start=(first and ti == 0 and g == 0), stop=(last and ti == nt_ - 1), skip_group_check=True),
                                    reads=[d_pt] + tl["deps"], writes=[d_o4])
                        if pslc is not None:
                            psv, d_ps, gpb = pslc
                            for g in range(ng):
                                P.op("pe", lambda e, g=g: e.matmul(
                                    out=psv(g), lhsT=pt[0:nk, g * nq:(g + 1) * nq], rhs=tl["Ov"],
                                    start=(ti == 0 and g % gpb == 0), stop=(ti == nt_ - 1), skip_group_check=True),
                                    reads=[d_pt] + tl["deps"], writes=[d_ps[g // gpb]])
                    flush()
                    pend.append(pv)

            attn.flush = flush
            return attn

        def phase_b1():
            with ExitStack() as sB:
                WB = P.sb([128, 8, 4376], BF16, sB); d_WB = Dep()
                wsegs = []
                for g in range(4):
                    wsegs += [(C_Q + g * 64, 64, g * 128), (C_Q + (4 + g) * 64, 64, g * 128 + 64)]
                wsegs += [(C_RQ, 512, 512), (C_RK, 512, 1024), (C_RV, 1024, 1536), (C_RG, 1024, 2560), (C_G, 24, 3584),
                          (C_KVC, 128, 3608), (C_KVS, 128, 3736), (C_KVW, 128, 3864),
                          (C_KVC + 128, 128, 3992), (C_KVS + 128, 128, 4120), (C_KVW + 128, 128, 4248)]
                for (src, n, dst) in wsegs:
                    P.dma("pool", lambda e, src=src, n=n, dst=dst: e.dma_start(
                        out=WB[:, :, dst:dst + n], in_=w_in[:, src:src + n].rearrange("(c p) n -> p c n", p=128)), writes=[d_WB])
                dmaskB = P.sb([128, 8, 128], BF16, sB); d_dmask = Dep()
                wmaskB = P.sb([128, 24, 128], BF16, sB); d_wmask = Dep()
                OvP = P.sb([128, 8, 256], BF16, sB); d_OvP = Dep()
                dtab = P.sb([128, 4, 128], F32, sB); d_dtab = Dep()
                qdt = P.sb([128, 4, 128], F32, sB); d_qdt = Dep()
                gnw = P.sb([128, D], F32, sB); d_gnw = Dep()
                gnb = P.sb([128, D], F32, sB); d_gnb = Dep()
                Irep = P.sb([128, 4, 128], BF16, sB); d_Irep = Dep()
                epsG = P.sb([128, 1], F32, sB); d_epsG = Dep()
                P.dma("pool", lambda e: e.dma_start(out=dmaskB[:].rearrange("p a b -> p (a b)"), in_=dmask_d), writes=[d_dmask])
                P.dma("pool", lambda e: e.dma_start(out=wmaskB[:].rearrange("p a b -> p (a b)"), in_=wmask_d), writes=[d_wmask])
                P.dma("pool", lambda e: e.dma_start(out=OvP[:].rearrange("p a b -> p (a b)"), in_=ovp_d), writes=[d_OvP])
                P.dma("sp", lambda e: e.dma_start(out=dtab[:].rearrange("p a b -> p (a b)"), in_=dtab_d), writes=[d_dtab])
                P.dma("sp", lambda e: e.dma_start(out=qdt[:].rearrange("p a b -> p (a b)"), in_=qdt_d), writes=[d_qdt])
                P.dma("sp", lambda e: e.dma_start(out=gnw[:], in_=gnw_d.partition_broadcast(128)[:, 0, :]), writes=[d_gnw])
                P.dma("sp", lambda e: e.dma_start(out=gnb[:], in_=gnb_d.partition_broadcast(128)[:, 0, :]), writes=[d_gnb])
                for g in range(4):
                    P.op("dve", lambda e, g=g: e.tensor_copy(out=Irep[:, g, :], in_=ident[:]), reads=[d_ident], writes=[d_Irep])
                P.op("dve", lambda e: e.memset(epsG[:], GN_EPS), writes=[d_epsG])
                IrepF = Irep[:].rearrange("p a b -> p (a b)")

                xt = P.sb([128, D], F32, sB); d_xt = Dep()
                rp = P.sb([128, 192], F32, sB); d_rp = Dep()
                wk = mk_wk(sB)
                hT = P.sb([128, 8, 128], BF16, sB); d_hT = Dep()
                tmq = [P.sb([128, 256], F32, sB) for _ in range(4)]; d_tmq = [Dep() for _ in range(4)]
                qr = P.sb([128, 512], F32, sB); d_qr = Dep()
                rqr = P.sb([128, 4, 128], F32, sB); d_rqr = Dep()
                rkr = P.sb([128, 4, 128], F32, sB); d_rkr = Dep()
                sg = P.sb([128, 24], F32, sB); d_sg = Dep()
                QT = P.sb([128, 4, 128], BF16, sB); d_QT = Dep()
                rqT = P.sb([128, 4, 128], BF16, sB); d_rqT = Dep()
                rqdT = P.sb([128, 4, 128], BF16, sB); d_rqdT = Dep()
                rkT = P.sb([128, 4, 128], BF16, sB); d_rkT = Dep()
                vb = P.sb([128, 1024], BF16, sB); d_vb = Dep()
                sgate = P.sb([128, 1024], F32, sB); d_sgate = Dep()
                attD = P.sb([128, 4, 128], BF16, sB); d_attD = Dep()
                SownB = P.sb([128, 1024], BF16, sB); d_SownB = Dep()
                osb = P.sb([128, 4, 256], F32, sB); d_osb = Dep()
                st4 = P.sb([128, 16], F32, sB); d_st4 = Dep()
                yret = P.sb([128, 1024], F32, sB); d_yret = Dep()
                yrT = P.sb([128, 8, 128], BF16, sB); d_yrT = Dep()
                onsa = P.sb([128, 8, 64], F32, sB); d_onsa = Dep()
                tmpo = P.sb([128, 4, 64], F32, sB); d_tmpo = Dep()
                onT = P.sb([128, 4, 128], BF16, sB); d_onT = Dep()
                rden = P.sb([128, 8], F32, sB); d_rden = Dep()
                wgt = P.sb([128, 4], F32, sB); d_wgt = Dep()
                cmaskB = P.sb([128, 1024], BF16, sB); d_cmask = Dep()
                forceb = P.sb([128, 256], F32, sB); d_forceb = Dep()
                pacc = P.sb([128, 256], F32, sB); d_pacc = Dep()
                wk2 = P.sb([128, 256], F32, sB); d_wk2 = Dep()
                m16 = P.sb([128, 16], F32, sB); d_m16 = Dep()
                selb = [P.sb([128, 256], F32, sB) for _ in range(2)]; d_selb = [Dep(), Dep()]
                selbX = [[P.sb([128, 1024], BF16, sB) for _ in range(2)] for _ in range(2)]
                d_selbX = [[Dep(), Dep()], [Dep(), Dep()]]
                KsG = [P.sb([128, 1024], BF16, sB) for _ in range(2)]; d_KsG = [Dep(), Dep()]
                VsG = [P.sb([128, 8, 130], BF16, sB) for _ in range(2)]; d_VsG = [Dep(), Dep()]
                KwWin = P.sb([128, 1536], BF16, sB); d_KwWin = Dep()
                VwWin = P.sb([128, 12, 130], BF16, sB); d_VwWin = Dep()
                pts = [P.sb([128, 512], BF16, sB) for _ in range(2)]; d_pts = [Dep(), Dep()]
                P.op("pool", lambda e: e.memset(KwWin[:], 0.0), writes=[d_KwWin])
                P.op("pool", lambda e: e.memset(VwWin[:].rearrange("p a b -> p (a b)"), 0.0), writes=[d_VwWin])
                attn = make_attn(pts, d_pts)

                o4 = [banks[2][0:65, :], banks[3][0:65, :]]
                d_o4 = [d_bank[2], d_bank[3]]
                oTs = [P.sb([65, 512], F32, sB) for _ in range(2)]; d_oTs = [Dep(), Dep()]
                sgv = sg[:].rearrange("p (h b) -> p h b", b=3)

                def evac(kvh, br_i, first):
                    attn.flush()
                    P.op("act", lambda e: e.copy(out=oTs[kvh][:], in_=o4[kvh]), reads=[d_o4[kvh]], writes=[d_oTs[kvh]])
                    for g in range(4):
                        P.op("pe", lambda e, g=g: e.transpose(out=pT[:, kvh * 512 + g * 65:kvh * 512 + (g + 1) * 65], in_=oTs[kvh][0:65, g * 128:(g + 1) * 128],
                                                              identity=ident[0:65, 0:65]), reads=[d_oTs[kvh], d_ident], writes=[d_pTs[kvh]])
                    o4v = pT[:, kvh * 512:kvh * 512 + 260].rearrange("p (g c) -> p g c", g=4)
                    d_o = d_pTs[kvh]
                    P.op("dve", lambda e: e.tensor_scalar(out=rden[:, kvh * 4:(kvh + 1) * 4], in0=o4v[:, :, 64], scalar1=1e-30, scalar2=None, op0=ALU.max),
                         reads=[d_o], writes=[d_rden])
                    P.op("dve", lambda e: e.reciprocal(out=rden[:, kvh * 4:(kvh + 1) * 4], in_=rden[:, kvh * 4:(kvh + 1) * 4]), reads=[d_rden], writes=[d_rden])
                    P.op("dve", lambda e: e.tensor_tensor(out=wgt[:], in0=rden[:, kvh * 4:(kvh + 1) * 4], in1=sgv[:, kvh * 4:(kvh + 1) * 4, br_i], op=ALU.mult),
                         reads=[d_rden, d_sg], writes=[d_wgt])
                    wb_ = wgt[:].unsqueeze(2).to_broadcast([128, 4, 64])
                    if first:
                        P.op("dve", lambda e: e.tensor_tensor(out=onsa[:, kvh * 4:(kvh + 1) * 4, :], in0=o4v[:, :, 0:64], in1=wb_, op=ALU.mult),
                             reads=[d_o, d_wgt], writes=[d_onsa])
                    else:
                        P.op("dve", lambda e: e.tensor_tensor(out=tmpo[:], in0=o4v[:, :, 0:64], in1=wb_, op=ALU.mult),
                             reads=[d_o, d_wgt], writes=[d_tmpo])
                        P.op("pool", lambda e: e.tensor_tensor(out=onsa[:, kvh * 4:(kvh + 1) * 4, :], in0=onsa[:, kvh * 4:(kvh + 1) * 4, :], in1=tmpo[:], op=ALU.add),
                             reads=[d_tmpo, d_onsa], writes=[d_onsa])

                nblk = NB if stage >= 2 else 0
                if stage == 2:
                    nblk = DBG_NBLK
                for i in range(nblk):
                    P.dma("sp", lambda e, i=i: e.dma_start(out=xt[:], in_=x_own[i * 128:(i + 1) * 128, :]), writes=[d_xt])
                    P.dma("sp", lambda e, i=i: e.dma_start(out=rp[:], in_=rope_own[i * 128:(i + 1) * 128, :]), writes=[d_rp])
                    P.dma("pool", lambda e, i=i: e.dma_start(out=cmaskB[:], in_=cmask_d[i]), writes=[d_cmask])
                    P.dma("sp", lambda e, i=i: e.dma_start(out=forceb[:], in_=forceb_d[i]), writes=[d_forceb])
                    P.dma("pool", lambda e, i=i: e.dma_start(out=SownB[:], in_=sown_d[i]), writes=[d_SownB])
                    if i == 0:
                        P.dma("sp", lambda e: e.dma_start(out=KwWin[:, 512:1536], in_=kwT_d[:, 0:1024]), writes=[d_KwWin])
                        P.dma("sp", lambda e: e.dma_start(out=VwWin[:, 4:12, :], in_=vw_d[0:1024, :].rearrange("(m p) f -> p m f", p=128)), writes=[d_VwWin])
                    else:
                        P.dma("sp", lambda e, i=i: e.dma_start(out=KwWin[:], in_=kwT_d[:, (8 * i - 4) * 128:(8 * i + 8) * 128]), writes=[d_KwWin])
                        P.dma("sp", lambda e, i=i: e.dma_start(out=VwWin[:], in_=vw_d[(8 * i - 4) * 128:(8 * i + 8) * 128, :].rearrange("(m p) f -> p m f", p=128)),
                              writes=[d_VwWin])
                    norm_mod_T(xt, d_xt, 128, G1P, d_G1P, modP4[:, 0, :], d_modP, wk, hT, d_hT, shcol[:, 0, :], d_shcol)
                    P.dma("sp", lambda e, i=i: e.dma_start(out=hT_d[i], in_=hT[:]), reads=[d_hT])
                    if SUB < 1:
                        continue
                    for (bk, c0, n) in ((0, 0, 512), (1, 512, 512), (4, 1024, 512), (5, 3584, 24)):
                        for k in range(8):
                            P.op("pe", lambda e, bk=bk, k=k, c0=c0, n=n: e.matmul(out=banks[bk][:, 0:n], lhsT=hT[:, k, :], rhs=WB[:, k, c0:c0 + n],
                                                                                start=(k == 0), stop=(k == 7)),
                                 reads=[d_hT, d_WB], writes=[d_bank[bk]])
                    P.op("act", lambda e: e.activation(out=sg[:], in_=banks[5][:, 0:24], func=ACTF.Sigmoid), reads=[d_bank[5]], writes=[d_sg])
                    if SUB < 2:
                        continue
                    srcq = banks[0][:, 0:512].rearrange("p (h two e) -> p h two e", h=8, two=2)
                    dstq = qr[:].rearrange("p (h two e) -> p h two e", h=8, two=2)
                    shq = [128, 8, 32]
                    tvq = [t[:, 0:256].rearrange("p (h e) -> p h e", h=8) for t in tmq]
                    rope(lambda hf: srcq[:, :, hf, :], lambda hf: dstq[:, :, hf, :], rp[:, 0:32].unsqueeze(1).to_broadcast(shq),
                         rp[:, 32:64].unsqueeze(1).to_broadcast(shq), shq, tvq, d_bank[0], d_qr, d_rp, d_tmq)
                    shk = [128, 4, 64]
                    tvk = [t[:, 0:256].rearrange("p (h e) -> p h e", h=4) for t in tmq]
                    cosk = rp[:, 64:128].unsqueeze(1).to_broadcast(shk)
                    sink = rp[:, 128:192].unsqueeze(1).to_broadcast(shk)
                    srcrq = banks[1][:, 0:512].rearrange("p (h two e) -> p h two e", h=4, two=2)
                    dstrq = rqr[:].rearrange("p h (two e) -> p h two e", two=2)
                    rope(lambda hf: srcrq[:, :, hf, :], lambda hf: dstrq[:, :, hf, :], cosk, sink, shk, tvk, d_bank[1], d_rqr, d_rp, d_tmq)
                    srcrk = banks[4][:, 0:512].rearrange("p (h two e) -> p h two e", h=4, two=2)
                    dstrk = rkr[:].rearrange("p h (two e) -> p h two e", two=2)
                    rope(lambda hf: srcrk[:, :, hf, :], lambda hf: dstrk[:, :, hf, :], cosk, sink, shk, tvk, d_bank[4], d_rkr, d_rp, d_tmq)
                    if SUB < 3:
                        continue
                    P.op("pool", lambda e: e.tensor_scalar(out=qr[:], in0=qr[:], scalar1=0.125, scalar2=None, op0=ALU.mult), reads=[d_qr], writes=[d_qr])
                    P.op("pool", lambda e: e.tensor_scalar(out=rkr[:].rearrange("p a b -> p (a b)"), in0=rkr[:].rearrange("p a b -> p (a b)"),
                                                           scalar1=RET_SCALE, scalar2=None, op0=ALU.mult), reads=[d_rkr], writes=[d_rkr])
                    for g in range(4):
                        P.op("pe", lambda e, g=g: e.transpose(out=banks[3][:, g * 128:(g + 1) * 128], in_=qr[:, g * 128:(g + 1) * 128], identity=ident[:]),
                             reads=[d_qr, d_ident], writes=[d_bank[3]])
                    P.op("act", lambda e: e.copy(out=QT[:].rearrange("p a b -> p (a b)"), in_=banks[3][:, :]),
                         reads=[d_bank[3]], writes=[d_QT])
                    if SUB2 < 1:
                        continue
                    for h in range(4):
                        P.op("pe", lambda e, h=h: e.transpose(out=banks[3][:, h * 128:(h + 1) * 128], in_=rqr[:, h, :], identity=ident[:]),
                             reads=[d_rqr, d_ident], writes=[d_bank[3]])
                    P.op("act", lambda e: e.copy(out=rqT[:].rearrange("p a b -> p (a b)"), in_=banks[3][:, :]), reads=[d_bank[3]], writes=[d_rqT])
                    P.op("dve", lambda e: e.tensor_tensor(out=rqdT[:].rearrange("p a b -> p (a b)"), in0=banks[3][:, :],
                                                          in1=qdt[:].rearrange("p a b -> p (a b)"), op=ALU.mult),
                         reads=[d_bank[3], d_qdt], writes=[d_rqdT])
                    if SUB2 < 2:
                        continue
                    for h in range(4):
                        P.op("pe", lambda e, h=h: e.transpose(out=banks[3][:, h * 128:(h + 1) * 128], in_=rkr[:, h, :], identity=ident[:]),
                             reads=[d_rkr, d_ident], writes=[d_bank[3]])
                    P.op("act", lambda e: e.copy(out=rkT[:].rearrange("p a b -> p (a b)"), in_=banks[3][:, :]),
                         reads=[d_bank[3]], writes=[d_rkT])
                    if SUB < 4:
                        continue
                    for (bk, c0) in ((0, 1536), (1, 2048), (4, 2560), (5, 3072)):
                        for k in range(8):
                            P.op("pe", lambda e, bk=bk, k=k, c0=c0: e.matmul(out=banks[bk][:, :], lhsT=hT[:, k, :], rhs=WB[:, k, c0:c0 + 512],
                                                                           start=(k == 0), stop=(k == 7)),
                                 reads=[d_hT, d_WB], writes=[d_bank[bk]])
                    P.op("act", lambda e: e.copy(out=vb[:, 0:512], in_=banks[0][:, :]), reads=[d_bank[0]], writes=[d_vb])
                    P.op("act", lambda e: e.copy(out=vb[:, 512:1024], in_=banks[1][:, :]), reads=[d_bank[1]], writes=[d_vb])
                    P.op("act", lambda e: e.activation(out=sgate[:, 0:512], in_=banks[4][:, :], func=ACTF.Silu), reads=[d_bank[4]], writes=[d_sgate])
                    P.op("act", lambda e: e.activation(out=sgate[:, 512:1024], in_=banks[5][:, :], func=ACTF.Silu), reads=[d_bank[5]], writes=[d_sgate])
                    if PART < 2:
                        continue
                    for h in range(4):
                        P.op("pe", lambda e, h=h: e.matmul(out=banks[3][:, h * 128:(h + 1) * 128], lhsT=rkT[:, h, :], rhs=rqT[:, h, :], start=True, stop=True),
                             reads=[d_rkT, d_rqT], writes=[d_bank[3]])
                    P.op("dve", lambda e: e.tensor_tensor(out=attD[:].rearrange("p a b -> p (a b)"), in0=banks[3][:, :],
                                                          in1=dtab[:].rearrange("p a b -> p (a b)"), op=ALU.mult),
                         reads=[d_bank[3], d_dtab], writes=[d_attD])
                    for h in range(4):
                        bk = 4 + h // 2
                        c0 = (h % 2) * 256
                        P.op("pe", lambda e, h=h, bk=bk, c0=c0: e.matmul(out=banks[bk][:, c0:c0 + 256], lhsT=attD[:, h, :], rhs=vb[:, h * 256:(h + 1) * 256],
                                                                        start=True, stop=False),
                             reads=[d_attD, d_vb], writes=[d_bank[bk]])
                        P.op("pe", lambda e, h=h, bk=bk, c0=c0: e.matmul(out=banks[bk][:, c0:c0 + 256], lhsT=rqdT[:, h, :], rhs=SownB[:, h * 256:(h + 1) * 256],
                                                                        start=False, stop=True),
                             reads=[d_rqdT, d_SownB], writes=[d_bank[bk]])
                    for h in range(4):
                        bk = 4 + h // 2
                        c0 = (h % 2) * 256
                        P.op("act", lambda e, h=h, bk=bk, c0=c0: e.activation(out=osb[:, h, :], in_=banks[bk][:, c0:c0 + 256], func=ACTF.Identity,
                                                                             accum_out=st4[:, h:h + 1]),
                             reads=[d_bank[bk]], writes=[d_osb, d_st4])
                        P.op("act", lambda e, h=h, bk=bk, c0=c0: e.activation(out=junk_sh[:, 0:256], in_=banks[bk][:, c0:c0 + 256], func=ACTF.Square,
                                                                             accum_out=st4[:, 4 + h:5 + h]),
                             reads=[d_bank[bk]], writes=[d_junk_sh, d_st4])
                    P.op("dve", lambda e: e.tensor_scalar(out=st4[:, 8:12], in0=st4[:, 0:4], scalar1=1.0 / 256, scalar2=None, op0=ALU.mult),
                         reads=[d_st4], writes=[d_st4])
                    P.op("dve", lambda e: e.tensor_tensor(out=st4[:, 12:16], in0=st4[:, 8:12], in1=st4[:, 8:12], op=ALU.mult), reads=[d_st4], writes=[d_st4])
                    P.op("dve", lambda e: e.scalar_tensor_tensor(out=st4[:, 12:16], in0=st4[:, 4:8], scalar=1.0 / 256, in1=st4[:, 12:16],
                                                                 op0=ALU.mult, op1=ALU.subtract), reads=[d_st4], writes=[d_st4])
                    P.op("act", lambda e: e.activation(out=st4[:, 12:16], in_=st4[:, 12:16], func=ACTF.Sqrt, bias=epsG[:, 0:1], scale=1.0),
                         reads=[d_st4, d_epsG], writes=[d_st4])
                    P.op("dve", lambda e: e.reciprocal(out=st4[:, 12:16], in_=st4[:, 12:16]), reads=[d_st4], writes=[d_st4])
                    for h in range(4):
                        P.op("dve", lambda e, h=h: e.tensor_scalar(out=yret[:, h * 256:(h + 1) * 256], in0=osb[:, h, :], scalar1=st4[:, 8 + h:9 + h],
                                                                  scalar2=st4[:, 12 + h:13 + h], op0=ALU.subtract, op1=ALU.mult),
                             reads=[d_osb, d_st4], writes=[d_yret])
                    P.op("pool", lambda e: e.tensor_tensor(out=yret[:], in0=yret[:], in1=gnw[:], op=ALU.mult), reads=[d_yret, d_gnw], writes=[d_yret])
                    P.op("pool", lambda e: e.tensor_tensor(out=yret[:], in0=yret[:], in1=gnb[:], op=ALU.add), reads=[d_yret, d_gnb], writes=[d_yret])
                    P.op("pool", lambda e: e.tensor_tensor(out=yret[:], in0=yret[:], in1=sgate[:], op=ALU.mult), reads=[d_yret, d_sgate], writes=[d_yret])
                    for c in range(8):
                        P.op("pe", lambda e, c=c: e.transpose(out=pT[:, c * 128:(c + 1) * 128], in_=yret[:, c * 128:(c + 1) * 128], identity=ident[:]),
                             reads=[d_yret, d_ident], writes=[d_pTs[c // 4]])
                    P.op("act", lambda e: e.copy(out=yrT[:].rearrange("p a b -> p (a b)"), in_=pT[:]), reads=d_pTs, writes=[d_yrT])
                    P.dma("sp", lambda e, i=i: e.dma_start(out=yrT_d[i], in_=yrT[:]), reads=[d_yrT])
                    if DBG:
                        P.dma("sp", lambda e, i=i: e.dma_start(out=dbg_yr[i], in_=yret[:]), reads=[d_yret])

                    if PART < 3:
                        continue
                    for kvh in range(2):
                        QTh = QT[kvh * 64:(kvh + 1) * 64, :, :].rearrange("p a b -> p (a b)")
                        tiles = []
                        for jt in range(8):
                            tiles.append({"KT": cmpKT[kvh * 64:(kvh + 1) * 64, jt * 128:(jt + 1) * 128], "V": cmpVA[:, jt, kvh, :], "nk": 128,
                                          "biases": [(cmaskB[:, jt * 128:(jt + 1) * 128], IrepF)], "Ov": OvP[:, jt, :],
                                          "deps": [d_cmpKT, d_cmpVA, d_cmask, d_Irep, d_OvP]})
                        psv = lambda g: banks[4 + g // 2][:, (g % 2) * 256:(g % 2) * 256 + 256]
                        attn(128, 4, QTh, d_QT, tiles, o4[kvh], d_o4[kvh], pslc=(psv, [d_bank[4], d_bank[5]], 2), pvT=True)
                        evac(kvh, 0, True)
                        P.op("dve", lambda e, kvh=kvh: e.tensor_scalar(out=pacc[:], in0=psv(0), scalar1=rden[:, kvh * 4:kvh * 4 + 1], scalar2=None, op0=ALU.mult),
                             reads=[d_bank[4], d_rden], writes=[d_pacc])
                        for g in range(1, 4):
                            P.op("dve", lambda e, kvh=kvh, g=g: e.scalar_tensor_tensor(out=pacc[:], in0=psv(g), scalar=rden[:, kvh * 4 + g:kvh * 4 + g + 1],
                                                                                      in1=pacc[:], op0=ALU.mult, op1=ALU.add),
                                 reads=[d_bank[4 + g // 2], d_rden, d_pacc], writes=[d_pacc])
                        P.op("dve", lambda e: e.tensor_tensor(out=pacc[:], in0=pacc[:], in1=forceb[:], op=ALU.add), reads=[d_pacc, d_forceb], writes=[d_pacc])
                        P.op("dve", lambda e: e.max(out=m16[:, 0:8], in_=pacc[:]), reads=[d_pacc], writes=[d_m16])
                        P.op("dve", lambda e: e.match_replace(out=wk2[:], in_to_replace=m16[:, 0:8], in_values=pacc[:], imm_value=-3.0e38),
                             reads=[d_pacc, d_m16], writes=[d_wk2])
                        P.op("dve", lambda e: e.max(out=m16[:, 8:16], in_=wk2[:]), reads=[d_wk2], writes=[d_m16])
                        P.op("dve", lambda e, kvh=kvh: e.tensor_scalar(out=selb[kvh][:], in0=pacc[:], scalar1=m16[:, 15:16], scalar2=NEG,
                                                                      op0=ALU.is_lt, op1=ALU.mult),
                             reads=[d_pacc, d_m16], writes=[d_selb[kvh]])
                    if DBG:
                        P.dma("sp", lambda e, i=i: e.dma_start(out=dbg_selb[i, 0], in_=selb[0][:]), reads=[d_selb[0]])
                        P.dma("sp", lambda e, i=i: e.dma_start(out=dbg_selb[i, 1], in_=selb[1][:]), reads=[d_selb[1]])
                    if PART < 4:
                        continue
                    ngk = i + 1
                    for gk in range(ngk):
                        b = gk % 2
                        P.dma("sp", lambda e, gk=gk, b=b: e.dma_start(out=KsG[b][:], in_=ksT_d[:, gk * 1024:(gk + 1) * 1024]), writes=[d_KsG[b]])
                        P.dma("sp", lambda e, gk=gk, b=b: e.dma_start(out=VsG[b][:], in_=vs_d[gk * 1024:(gk + 1) * 1024, :].rearrange("(m p) f -> p m f", p=128)),
                              writes=[d_VsG[b]])
                        for kvh in range(2):
                            P.op("pool", lambda e, kvh=kvh, b=b, gk=gk: e.tensor_copy(
                                out=selbX[kvh][b][:].rearrange("p (a c) -> p a c", c=64),
                                in_=selb[kvh][:, gk * 16:(gk + 1) * 16].unsqueeze(2).to_broadcast([128, 16, 64])),
                                reads=[d_selb[kvh]], writes=[d_selbX[kvh][b]])
                        for kvh in range(2):
                            QTh = QT[kvh * 64:(kvh + 1) * 64, :, :].rearrange("p a b -> p (a b)")
                            tiles = []
                            for j in range(8):
                                bs = [(selbX[kvh][b][:, j * 128:(j + 1) * 128], IrepF)]
                                if gk == i:
                                    bs.append((dmaskB[:, j, :], IrepF))
                                tiles.append({"KT": KsG[b][kvh * 64:(kvh + 1) * 64, j * 128:(j + 1) * 128], "V": VsG[b][:, j, kvh * 65:(kvh + 1) * 65],
                                              "nk": 128, "biases": bs, "deps": [d_KsG[b], d_VsG[b], d_selbX[kvh][b], d_Irep, d_dmask]})
                            attn(128, 4, QTh, d_QT, tiles, o4[kvh], d_o4[kvh], first=(gk == 0), last=(gk == ngk - 1), pvT=True)
                    evac(0, 1, False)
                    evac(1, 1, False)
                    if PART < 5:
                        continue
                    for kvh in range(2):
                        QTh = QT[kvh * 64:(kvh + 1) * 64, :, :].rearrange("p a b -> p (a b)")
                        tiles = []
                        for m in range(12):
                            wm = m if i == 0 else 12 + m
                            tiles.append({"KT": KwWin[kvh * 64:(kvh + 1) * 64, m * 128:(m + 1) * 128], "V": VwWin[:, m, kvh * 65:(kvh + 1) * 65], "nk": 128,
                                          "biases": [(wmaskB[:, wm, :], IrepF)], "deps": [d_KwWin, d_VwWin, d_wmask, d_Irep]})
                        attn(128, 4, QTh, d_QT, tiles, o4[kvh], d_o4[kvh], pvT=True)
                        evac(kvh, 2, False)
                    onf = onsa[:].rearrange("p h d -> p (h d)")
                    for c in range(4):
                        P.op("pe", lambda e, c=c: e.transpose(out=pT[:, c * 128:(c + 1) * 128], in_=onf[:, c * 128:(c + 1) * 128], identity=ident[:]),
                             reads=[d_onsa, d_ident], writes=[d_pTs[0]])
                    P.op("act", lambda e: e.copy(out=onT[:].rearrange("p a b -> p (a b)"), in_=pT[:, 0:512]), reads=[d_pTs[0]], writes=[d_onT])
                    P.dma("sp", lambda e, i=i: e.dma_start(out=onT_d[i], in_=onT[:]), reads=[d_onT])
                    if DBG:
                        P.dma("sp", lambda e, i=i: e.dma_start(out=dbg_on[i], in_=onsa[:].rearrange("p h d -> p (h d)")), reads=[d_onsa])
                P.barrier()

        def phase_s1():
            with ExitStack() as sS:
                sR = ExitStack()
                smallc = P.sb([1, 1024], F32, sS); smallb = P.sb([1, 1024], BF16, sS); d_small = Dep()
                P.dma("sp", lambda e: e.dma_start(out=smallc[:], in_=smallc_d), writes=[d_small])
                P.op("dve", lambda e: e.tensor_copy(out=smallb[:], in_=smallc[:]), reads=[d_small], writes=[d_small])
                cm7row, wm0row, ones4 = smallb[0:1, 0:128], smallb[0:1, 128:256], smallb[0:1, 272:276]
                forceS = smallc[0:1, 276:532]
                oh4 = P.sb([SB, 4], F32, sS); d_oh4 = Dep()
                P.dma("sp", lambda e: e.dma_start(out=oh4[:], in_=oh4_d), writes=[d_oh4])
                OvS = P.sb([128, 8, 257], BF16, sS); d_OvS = Dep()
                P.dma("pool", lambda e: e.dma_start(out=OvS[:].rearrange("p a b -> p (a b)"), in_=ovs_d), writes=[d_OvS])
                wbc = P.sb([128, 2, 16, 256], F32, sS); d_wbc = Dep()
                P.dma("sp", lambda e: e.dma_start(out=wbc[:].rearrange("p a b c -> p (a b c)"), in_=wrow_d.partition_broadcast(128)[:, 0, :]), writes=[d_wbc])
                brow = P.sb([128, 256], F32, sS); d_brow = Dep()
                P.dma("sp", lambda e: e.dma_start(out=brow[:], in_=brow_d.partition_broadcast(128)[:, 0, :]), writes=[d_brow])
                QTs = P.sb([128, 4, SB], BF16, sS); d_QTs = Dep()
                KnT = P.sb([128, 2, SB], BF16, sS); d_KnT = Dep()
                VnA = P.sb([SB, 2, 2, 65], BF16, sS); d_VnA = Dep()
                sg = P.sb([SB, 24], F32, sS); d_sg = Dep()
                ohm = P.sb([128, 4, 4], F32, sR); d_ohm = Dep()
                P.dma("sp", lambda e: e.dma_start(out=ohm[:].rearrange("p a b -> p (a b)"), in_=ohm_d), writes=[d_ohm])
                gnw = P.sb([SB, D], F32, sR); d_gnw = Dep()
                gnb = P.sb([SB, D], F32, sR); d_gnb = Dep()
                P.dma("sp", lambda e: e.dma_start(out=gnw[:], in_=gnw_d.partition_broadcast(SB)[:, 0, :]), writes=[d_gnw])
                P.dma("sp", lambda e: e.dma_start(out=gnb[:], in_=gnb_d.partition_broadcast(SB)[:, 0, :]), writes=[d_gnb])
                epsG = P.sb([SB, 1], F32, sR); d_epsG = Dep()
                P.op("dve", lambda e: e.memset(epsG[:], GN_EPS), writes=[d_epsG])
                xs = P.sb([SB, D], F32, sR); d_xs = Dep()
                rps = P.sb([SB, 192], F32, sR); d_rps = Dep()
                msS = P.sb([SB, 2, D], F32, sR); d_msS = Dep()
                gm4 = P.sb([SB, D], F32, sR); d_gm4 = Dep()
                P.dma("sp", lambda e: e.dma_start(out=xs[:], in_=x_smp), writes=[d_xs])
                P.dma("sp", lambda e: e.dma_start(out=rps[:], in_=rope_smp), writes=[d_rps])
                P.dma("sp", lambda e: e.dma_start(out=msS[:], in_=mods_d[:, 0:2 * D].rearrange("s (a d) -> s a d", a=2)), writes=[d_msS])
                P.dma("sp", lambda e: e.dma_start(out=gm4[:], in_=norm_mix.partition_broadcast(SB)[:, 0, :]), writes=[d_gm4])
                P.op("dve", lambda e: e.scalar_tensor_tensor(out=msS[:, 1, :], in0=msS[:, 1, :], scalar=1.0, in1=gm4[:], op0=ALU.add, op1=ALU.mult),
                     reads=[d_msS, d_gm4], writes=[d_msS])
                wkS = mk_wk(sR, SB)
                hTs = P.sb([128, 8, SB], BF16, sR); d_hTs = Dep()
                norm_mod_T(xs, d_xs, SB, msS[:, 1, :], d_msS, msS[:, 0, :], d_msS, wkS, hTs, d_hTs)
                P.dma("sp", lambda e: e.dma_start(out=hTs_d, in_=hTs[:]), reads=[d_hTs])
                zs = P.sb([SB, 4376], F32, sR); d_zs = Dep()
                wch = [P.sb([128, 8, 512], BF16, sR) for _ in range(2)]; d_wch = [Dep(), Dep()]
                segs = []
                for g in range(4):
                    segs += [(C_Q + g * 64, 64), (C_Q + (4 + g) * 64, 64)]
                segs += [(C_RQ, 512), (C_RK, 512), (C_RV, 512), (C_RV + 512, 512), (C_RG, 512), (C_RG + 512, 512), (C_G, 24),
                         (C_KVC, 128), (C_KVS, 128), (C_KVW, 128), (C_KVC + 128, 128), (C_KVS + 128, 128), (C_KVW + 128, 128)]
                chunks = []
                cur, curw = [], 0
                for sg_ in segs:
                    if curw + sg_[1] > 512:
                        chunks.append(cur); cur, curw = [], 0
                    cur.append(sg_); curw += sg_[1]
                chunks.append(cur)
                zoff = 0
                for ci, ch in enumerate(chunks):
                    b = ci % 2
                    o_ = 0
                    for (src, n_) in ch:
                        P.dma("pool", lambda e, b=b, o_=o_, src=src, n_=n_: e.dma_start(
                            out=wch[b][:, :, o_:o_ + n_], in_=w_in[:, src:src + n_].rearrange("(c p) n -> p c n", p=128)), writes=[d_wch[b]])
                        o_ += n_
                    for k in range(8):
                        P.op("pe", lambda e, b=b, k=k, o_=o_: e.matmul(out=banks[b][0:SB, 0:o_], lhsT=hTs[:, k, :], rhs=wch[b][:, k, 0:o_], start=(k == 0), stop=(k == 7)),
                             reads=[d_hTs, d_wch[b]], writes=[d_bank[b]])
                    P.op("act", lambda e, b=b, o_=o_, zoff=zoff: e.copy(out=zs[:, zoff:zoff + o_], in_=banks[b][0:SB, 0:o_]), reads=[d_bank[b]], writes=[d_zs])
                    zoff += o_
                tms = [P.sb([SB, 256], F32, sR) for _ in range(4)]; d_tms = [Dep() for _ in range(4)]
                qr = P.sb([SB, 512], F32, sR); d_qr = Dep()
                rqr = P.sb([SB, 4, 128], F32, sR); d_rqr = Dep()
                rkr = P.sb([SB, 4, 128], F32, sR); d_rkr = Dep()
                kvo = P.sb([SB, 3, 256], F32, sR); d_kvo = Dep()
                sgate = P.sb([SB, 1024], F32, sR); d_sgate = Dep()
                vbs = P.sb([SB, 1024], BF16, sR); d_vbs = Dep()
                shq = [SB, 8, 32]
                srcq = zs[:, 0:512].rearrange("p (h two e) -> p h two e", h=8, two=2)
                dstq = qr[:].rearrange("p (h two e) -> p h two e", h=8, two=2)
                tvq = [t[:, 0:256].rearrange("p (h e) -> p h e", h=8) for t in tms]
                cos32 = rps[:, 0:32]; sin32 = rps[:, 32:64]
                rope(lambda hf: srcq[:, :, hf, :], lambda hf: dstq[:, :, hf, :], cos32.unsqueeze(1).to_broadcast(shq), sin32.unsqueeze(1).to_broadcast(shq),
                     shq, tvq, d_zs, d_qr, d_rps, d_tms)
                P.op("pool", lambda e: e.tensor_scalar(out=qr[:], in0=qr[:], scalar1=0.125, scalar2=None, op0=ALU.mult), reads=[d_qr], writes=[d_qr])
                shk = [SB, 4, 64]
                tvk = [t[:, 0:256].rearrange("p (h e) -> p h e", h=4) for t in tms]
                cosk = rps[:, 64:128].unsqueeze(1).to_broadcast(shk); sink = rps[:, 128:192].unsqueeze(1).to_broadcast(shk)
                for (c0, dstt, d_dst) in ((512, rqr, d_rqr), (1024, rkr, d_rkr)):
                    src_ = zs[:, c0:c0 + 512].rearrange("p (h two e) -> p h two e", h=4, two=2)
                    dst_ = dstt[:].rearrange("p h (two e) -> p h two e", two=2)
                    rope(lambda hf, src_=src_: src_[:, :, hf, :], lambda hf, dst_=dst_: dst_[:, :, hf, :], cosk, sink, shk, tvk, d_zs, d_dst, d_rps, d_tms)
                P.op("pool", lambda e: e.tensor_scalar(out=rkr[:].rearrange("p a b -> p (a b)"), in0=rkr[:].rearrange("p a b -> p (a b)"),
                                                       scalar1=RET_SCALE, scalar2=None, op0=ALU.mult), reads=[d_rkr], writes=[d_rkr])
                shn = [SB, 3, 2, 32]
                srcn = zs[:, 3608:3992].rearrange("p (j h two e) -> p j h two e", j=3, h=2, two=2)
                dstn = kvo[:, :, 0:128].rearrange("p j (h two e) -> p j h two e", h=2, two=2)
                tvn = [t[:, 0:192].rearrange("p (j h e) -> p j h e", j=3, h=2) for t in tms]
                rope(lambda hf: srcn[:, :, :, hf, :], lambda hf: dstn[:, :, :, hf, :], cos32.unsqueeze(1).unsqueeze(1).to_broadcast(shn),
                     sin32.unsqueeze(1).unsqueeze(1).to_broadcast(shn), shn, tvn, d_zs, d_kvo, d_rps, d_tms)
                P.op("act", lambda e: e.copy(out=kvo[:, :, 128:256], in_=zs[:, 3992:4376].rearrange("p (j f) -> p j f", j=3)), reads=[d_zs], writes=[d_kvo])
                P.op("act", lambda e: e.activation(out=sg[:], in_=zs[:, 3584:3608], func=ACTF.Sigmoid), reads=[d_zs], writes=[d_sg])
                P.op("act", lambda e: e.activation(out=sgate[:], in_=zs[:, 2560:3584], func=ACTF.Silu), reads=[d_zs], writes=[d_sgate])
                P.op("act", lambda e: e.copy(out=vbs[:], in_=zs[:, 1536:2560]), reads=[d_zs], writes=[d_vbs])
                P.dma("sp", lambda e: e.dma_start(out=o_cmp_s, in_=kvo[:, 0, :]), reads=[d_kvo])
                P.dma("sp", lambda e: e.dma_start(out=o_sel_s, in_=kvo[:, 1, :]), reads=[d_kvo])
                P.dma("sp", lambda e: e.dma_start(out=o_win_s[:, 511, :], in_=kvo[:, 2, :]), reads=[d_kvo])
                for b in range(SB):
                    P.dma("sp", lambda e, b=b: e.dma_start(out=o_win_s[b, 0:511, :], in_=cache_win[b, 1:512, :]))
                rqTs = P.sb([128, 4, SB], BF16, sR); d_rqTs = Dep()
                rqdTs = P.sb([128, 4, SB], F32, sR); d_rqdTs = Dep()
                rkTs = P.sb([128, 4, SB], BF16, sR); d_rkTs = Dep()
                for (srcs, dstT, d_srcs, d_dstT) in (([qr[:, g * 128:(g + 1) * 128] for g in range(4)], QTs, d_qr, d_QTs),
                                                     ([rqr[:, h, :] for h in range(4)], rqTs, d_rqr, d_rqTs),
                                                     ([rkr[:, h, :] for h in range(4)], rkTs, d_rkr, d_rkTs),
                                                     ([kvo[:, 1, 0:128], kvo[:, 2, 0:128]], KnT, d_kvo, d_KnT)):
                    for j, sap in enumerate(srcs):
                        P.op("pe", lambda e, j=j, sap=sap: e.transpose(out=banks[3][:, j * SB:(j + 1) * SB], in_=sap, identity=ident[0:SB, 0:SB]),
                             reads=[d_srcs, d_ident], writes=[d_bank[3]])
                    nn = len(srcs) * SB
                    P.op("act", lambda e, dstT=dstT, nn=nn: e.copy(out=dstT[:].rearrange("p a b -> p (a b)"), in_=banks[3][:, 0:nn]), reads=[d_bank[3]], writes=[d_dstT])
                for h in range(4):
                    P.op("dve", lambda e, h=h: e.tensor_scalar(out=rqdTs[:, h, :], in0=rqTs[:, h, :], scalar1=float(math.exp(LG[h])), scalar2=None, op0=ALU.mult),
                         reads=[d_rqTs], writes=[d_rqdTs])
                P.op("pool", lambda e: e.memset(VnA[:].rearrange("p a b c -> p (a b c)"), 1.0), writes=[d_VnA])
                P.op("pool", lambda e: e.tensor_copy(out=VnA[:, :, :, 0:64], in_=kvo[:, 1:3, 128:256].rearrange("p j (h d) -> p j h d", h=2)),
                     reads=[d_kvo], writes=[d_VnA])
                attDs = P.sb([SB, 4, SB], BF16, sR); d_attDs = Dep()
                for h in range(4):
                    P.op("pe", lambda e, h=h: e.matmul(out=banks[3][0:SB, h * SB:(h + 1) * SB], lhsT=rkTs[:, h, :], rhs=rqTs[:, h, :], start=True, stop=True),
                         reads=[d_rkTs, d_rqTs], writes=[d_bank[3]])
                P.op("dve", lambda e: e.tensor_tensor(out=attDs[:], in0=banks[3][0:SB, 0:16].rearrange("p (h i) -> p h i", h=4),
                                                      in1=oh4[:].unsqueeze(1).to_broadcast([SB, 4, SB]), op=ALU.mult), reads=[d_bank[3], d_oh4], writes=[d_attDs])
                Sb = [P.sb([128, 4, 256], F32, sR) for _ in range(SB)]; d_Sb = [Dep() for _ in range(SB)]
                Sbb = [P.sb([128, 4, 256], BF16, sR) for _ in range(SB)]; d_Sbb = [Dep() for _ in range(SB)]
                rqm = [P.sb([128, 4, SB], BF16, sR) for _ in range(SB)]; d_rqm = [Dep() for _ in range(SB)]
                kdm = [P.sb([SB, 4, 128], BF16, sR) for _ in range(SB)]; d_kdm = [Dep() for _ in range(SB)]
                for b in range(SB):
                    P.dma("sp", lambda e, b=b: e.dma_start(out=Sb[b][:], in_=state_s[b].rearrange("h k v -> k h v")), writes=[d_Sb[b]])
                    P.op("pool", lambda e, b=b: e.tensor_copy(out=Sbb[b][:].rearrange("p a b -> p (a b)"), in_=Sb[b][:].rearrange("p a b -> p (a b)")),
                         reads=[d_Sb[b]], writes=[d_Sbb[b]])
                    P.op("dve", lambda e, b=b: e.tensor_tensor(out=rqm[b][:], in0=rqdTs[:], in1=ohm[:, b, :].unsqueeze(1).to_broadcast([128, 4, SB]), op=ALU.mult),
                         reads=[d_rqdTs, d_ohm], writes=[d_rqm[b]])
                    P.op("dve", lambda e, b=b: e.tensor_scalar(out=kdm[b][:].rearrange("p a b -> p (a b)"), in0=rkr[:].rearrange("p a b -> p (a b)"),
                                                              scalar1=oh4[:, b:b + 1], scalar2=None, op0=ALU.mult), reads=[d_rkr, d_oh4], writes=[d_kdm[b]])
                for h in range(4):
                    bk = 4 + h // 2
                    c0 = (h % 2) * 256
                    P.op("pe", lambda e, h=h, bk=bk, c0=c0: e.matmul(out=banks[bk][0:SB, c0:c0 + 256], lhsT=attDs[:, h, :], rhs=vbs[:, h * 256:(h + 1) * 256],
                                                                    start=True, stop=False), reads=[d_attDs, d_vbs], writes=[d_bank[bk]])
                    for b in range(SB):
                        P.op("pe", lambda e, h=h, bk=bk, c0=c0, b=b: e.matmul(out=banks[bk][0:SB, c0:c0 + 256], lhsT=rqm[b][:, h, :], rhs=Sbb[b][:, h, :],
                                                                             start=False, stop=(b == SB - 1)), reads=[d_rqm[b], d_Sbb[b]], writes=[d_bank[bk]])
                osb = P.sb([SB, 4, 256], F32, sR); d_osb = Dep()
                st4 = P.sb([SB, 16], F32, sR); d_st4 = Dep()
                yret = P.sb([SB, 1024], F32, sR); d_yret = Dep()
                for h in range(4):
                    bk = 4 + h // 2
                    c0 = (h % 2) * 256
                    P.op("act", lambda e, h=h, bk=bk, c0=c0: e.activation(out=osb[:, h, :], in_=banks[bk][0:SB, c0:c0 + 256], func=ACTF.Identity, accum_out=st4[:, h:h + 1]),
                         reads=[d_bank[bk]], writes=[d_osb, d_st4])
                    P.op("act", lambda e, h=h, bk=bk, c0=c0: e.activation(out=junk_sh[0:SB, 0:256], in_=banks[bk][0:SB, c0:c0 + 256], func=ACTF.Square,
                                                                         accum_out=st4[:, 4 + h:5 + h]), reads=[d_bank[bk]], writes=[d_junk_sh, d_st4])
                P.op("dve", lambda e: e.tensor_scalar(out=st4[:, 8:12], in0=st4[:, 0:4], scalar1=1.0 / 256, scalar2=None, op0=ALU.mult), reads=[d_st4], writes=[d_st4])
                P.op("dve", lambda e: e.tensor_tensor(out=st4[:, 12:16], in0=st4[:, 8:12], in1=st4[:, 8:12], op=ALU.mult), reads=[d_st4], writes=[d_st4])
                P.op("dve", lambda e: e.scalar_tensor_tensor(out=st4[:, 12:16], in0=st4[:, 4:8], scalar=1.0 / 256, in1=st4[:, 12:16], op0=ALU.mult, op1=ALU.subtract),
                     reads=[d_st4], writes=[d_st4])
                P.op("act", lambda e: e.activation(out=st4[:, 12:16], in_=st4[:, 12:16], func=ACTF.Sqrt, bias=epsG[:, 0:1], scale=1.0), reads=[d_st4, d_epsG], writes=[d_st4])
                P.op("dve", lambda e: e.reciprocal(out=st4[:, 12:16], in_=st4[:, 12:16]), reads=[d_st4], writes=[d_st4])
                for h in range(4):
                    P.op("dve", lambda e, h=h: e.tensor_scalar(out=yret[:, h * 256:(h + 1) * 256], in0=osb[:, h, :], scalar1=st4[:, 8 + h:9 + h],
                                                              scalar2=st4[:, 12 + h:13 + h], op0=ALU.subtract, op1=ALU.mult), reads=[d_osb, d_st4], writes=[d_yret])
                P.op("pool", lambda e: e.tensor_tensor(out=yret[:], in0=yret[:], in1=gnw[:], op=ALU.mult), reads=[d_yret, d_gnw], writes=[d_yret])
                P.op("pool", lambda e: e.tensor_tensor(out=yret[:], in0=yret[:], in1=gnb[:], op=ALU.add), reads=[d_yret, d_gnb], writes=[d_yret])
                P.op("pool", lambda e: e.tensor_tensor(out=yret[:], in0=yret[:], in1=sgate[:], op=ALU.mult), reads=[d_yret, d_sgate], writes=[d_yret])
                yrTs = P.sb([128, 8, SB], BF16, sR); d_yrTs = Dep()
                for c in range(8):
                    P.op("pe", lambda e, c=c: e.transpose(out=banks[3][:, c * SB:(c + 1) * SB], in_=yret[:, c * 128:(c + 1) * 128], identity=ident[0:SB, 0:SB]),
                         reads=[d_yret, d_ident], writes=[d_bank[3]])
                P.op("act", lambda e: e.copy(out=yrTs[:].rearrange("p a b -> p (a b)"), in_=banks[3][:, 0:8 * SB]), reads=[d_bank[3]], writes=[d_yrTs])
                P.dma("sp", lambda e: e.dma_start(out=yrTs_d, in_=yrTs[:]), reads=[d_yrTs])
                snew = [P.sb([128, 256], F32, sR) for _ in range(2)]; d_snew = [Dep(), Dep()]
                for b in range(SB):
                    for h in range(4):
                        bb = (b * 4 + h) % 2
                        P.op("pe", lambda e, b=b, h=h, bb=bb: e.matmul(out=banks[4 + bb][:, 0:256], lhsT=kdm[b][:, h, :], rhs=vbs[:, h * 256:(h + 1) * 256],
                                                                      start=True, stop=True), reads=[d_kdm[b], d_vbs], writes=[d_bank[4 + bb]])
                        P.op("dve", lambda e, b=b, h=h, bb=bb: e.scalar_tensor_tensor(out=snew[bb][:], in0=Sb[b][:, h, :], scalar=float(math.exp(LG[h])),
                                                                                     in1=banks[4 + bb][:, 0:256], op0=ALU.mult, op1=ALU.add),
                             reads=[d_Sb[b], d_bank[4 + bb]], writes=[d_snew[bb]])
                        P.dma("sp", lambda e, b=b, h=h, bb=bb: e.dma_start(out=o_state_s[b, h], in_=snew[bb][:]), reads=[d_snew[bb]])

                P.barrier()
                sR.close()
                pts = [P.sb([128, 512], BF16, sS) for _ in range(2)]; d_pts = [Dep(), Dep()]
                attn = make_attn(pts, d_pts)
                G = [P.sb([128, 16, 256], F32, sS) for _ in range(2)]; d_G = [Dep(), Dep()]
                prod = [P.sb([128, 16, 256], F32, sS) for _ in range(2)]; d_prod = [Dep(), Dep()]
                Fs = P.sb([128, 8, 256], F32, sS); d_Fs = Dep()
                S2 = P.sb([128, 9, 256], F32, sS); d_S2 = Dep()
                cmpKTs = P.sb([128, 8, 128], BF16, sS); d_cmpKTs = Dep()
                cmpVAs = P.sb([128, 8, 2, 65], BF16, sS); d_cmpVAs = Dep()
                P.op("pool", lambda e: e.memset(cmpVAs[:].rearrange("p a b c -> p (a b c)"), 1.0), writes=[d_cmpVAs])
                KT16 = [P.sb([128, 16, 128], BF16, sS) for _ in range(2)]; d_KT16 = [Dep(), Dep()]
                V16 = [P.sb([128, 16, 2, 65], BF16, sS) for _ in range(2)]; d_V16 = [Dep(), Dep()]
                for t_ in V16:
                    P.op("pool", lambda e, t_=t_: e.memset(t_[:].rearrange("p a b c -> p (a b c)"), 1.0), writes=d_V16)
                Wt = P.sb([128, 4, 256], F32, sS); d_Wt = Dep()
                KTw = P.sb([128, 4, 128], BF16, sS); d_KTw = Dep()
                VwA = P.sb([128, 4, 2, 65], BF16, sS); d_VwA = Dep()
                P.op("pool", lambda e: e.memset(VwA[:].rearrange("p a b c -> p (a b c)"), 1.0), writes=[d_VwA])
                ptb = P.sb([128, 2], I32, sS); d_ptb = Dep()
                idx8 = P.sb([128, 9], I32, sS); d_idx8 = Dep()
                og = P.sb([SB, 65], F32, sS); d_og = Dep()
                rd1 = P.sb([SB, 2], F32, sS); d_rd1 = Dep()
                pn = P.sb([SB, 256], F32, sS); d_pn = Dep()
                srow = P.sb([1, 256], F32, sS); d_srow = Dep()
                wk2 = P.sb([1, 256], F32, sS); d_wk2 = Dep()
                m16 = P.sb([1, 16], F32, sS); d_m16 = Dep()
                selbB = [P.sb([1, 256], BF16, sS) for _ in range(2)]; d_selbB = [Dep(), Dep()]
                o4s = [banks[2][0:SB, 0:65].rearrange("p (g c) -> p g c", g=1), banks[3][0:SB, 0:65].rearrange("p (g c) -> p g c", g=1)]
                d_o4s = [d_bank[2], d_bank[3]]
                ones4f = oh4
                onec = P.sb([SB, 1], F32, sS); d_onec = Dep()
                P.op("dve", lambda e: e.memset(onec[:], 1.0), writes=[d_onec])

                def evac_s(b, kvh, br_i):
                    attn.flush()
                    o4v, d_o = o4s[kvh], d_o4s[kvh]
                    P.op("dve", lambda e: e.tensor_scalar(out=rd1[:, 0:1], in0=o4v[:, 0, 64:65], scalar1=1e-30, scalar2=None, op0=ALU.max), reads=[d_o], writes=[d_rd1])
                    P.op("dve", lambda e: e.reciprocal(out=rd1[:, 0:1], in_=rd1[:, 0:1]), reads=[d_rd1], writes=[d_rd1])
                    P.op("dve", lambda e: e.tensor_scalar(out=og[:, 0:64], in0=o4v[:, 0, 0:64], scalar1=rd1[:, 0:1], scalar2=None, op0=ALU.mult),
                         reads=[d_o, d_rd1], writes=[d_og])
                    P.dma("sp", lambda e: e.dma_start(out=os_d[b, br_i, kvh * 4:(kvh + 1) * 4, :], in_=og[:, 0:64]), reads=[d_og])

                for b in range(SB):
                    P.dma("sp", lambda e, b=b: e.dma_start(out=ptb[:, 0:1], in_=ptab[b].rearrange("(p o) -> p o", o=1)), writes=[d_ptb])
                    P.dma("sp", lambda e, b=b: e.dma_start(out=ptb[0:127, 1:2], in_=ptab[b, 1:128].rearrange("(p o) -> p o", o=1)), writes=[d_ptb])
                    P.dma("sp", lambda e, b=b: e.dma_start(out=ptb[127:128, 1:2], in_=ptab[b, 127:128].rearrange("(p o) -> p o", o=1)), writes=[d_ptb])
                    for k in range(8):
                        P.op("dve", lambda e, k=k: e.tensor_scalar(out=idx8[:, k:k + 1], in0=ptb[:, 0:1], scalar1=8, scalar2=k, op0=ALU.mult, op1=ALU.add),
                             reads=[d_ptb], writes=[d_idx8])
                    P.op("dve", lambda e: e.tensor_scalar(out=idx8[:, 8:9], in0=ptb[:, 1:2], scalar1=8, scalar2=0, op0=ALU.mult, op1=ALU.add),
                         reads=[d_ptb], writes=[d_idx8])
                    for k in range(9):
                        gb_ = k % 2
                        P.dma("pool", lambda e, k=k, gb_=gb_: e.indirect_dma_start(
                            out=G[gb_][:].rearrange("p a b -> p (a b)"), out_offset=None, in_=cache_cmp[:, :],
                            in_offset=bass.IndirectOffsetOnAxis(ap=idx8[:, k:k + 1], axis=0)), reads=[d_idx8], writes=[d_G[gb_]])
                        for r in range(2):
                            if k == 8 and r == 0:
                                continue
                            P.op("pool", lambda e, gb_=gb_, r=r: e.tensor_tensor(out=prod[r][:], in0=G[gb_][:], in1=wbc[:, r, :, :], op=ALU.mult),
                                 reads=[d_G[gb_], d_wbc], writes=[d_prod[r]])
                            dstF = Fs[:, k, :] if r == 0 else S2[:, k, :]
                            P.op("dve", lambda e, r=r, dstF=dstF: e.tensor_reduce(out=dstF, in_=prod[r][:].rearrange("p j f -> p f j"), axis=AX.X, op=ALU.add),
                                 reads=[d_prod[r]], writes=[d_Fs if r == 0 else d_S2])
                    P.op("dve", lambda e: e.tensor_tensor(out=Fs[:], in0=Fs[:], in1=S2[:, 1:9, :], op=ALU.add), reads=[d_Fs, d_S2], writes=[d_Fs])
                    P.op("dve", lambda e: e.tensor_tensor(out=Fs[:], in0=Fs[:], in1=brow[:].unsqueeze(1).to_broadcast([128, 8, 256]), op=ALU.add),
                         reads=[d_Fs, d_brow], writes=[d_Fs])
                    for s_ in range(8):
                        P.op("pe", lambda e, s_=s_: e.transpose(out=pT[:, s_ * 128:(s_ + 1) * 128], in_=Fs[:, s_, 0:128], identity=ident[:]),
                             reads=[d_Fs, d_ident], writes=[d_pTs[s_ // 4]])
                    P.op("act", lambda e: e.copy(out=cmpKTs[:].rearrange("p a b -> p (a b)"), in_=pT[:]), reads=d_pTs, writes=[d_cmpKTs])
                    P.op("pool", lambda e: e.tensor_copy(out=cmpVAs[:, :, :, 0:64], in_=Fs[:, :, 128:256].rearrange("p s (h d) -> p s h d", h=2)),
                         reads=[d_Fs], writes=[d_cmpVAs])
                    for kvh in range(2):
                        QTh = QTs[kvh * 64:(kvh + 1) * 64, :, b]
                        tiles = []
                        for s_ in range(8):
                            bs = [(cm7row, ones4)] if s_ == 7 else []
                            tiles.append({"KT": cmpKTs[kvh * 64:(kvh + 1) * 64, s_, :], "V": cmpVAs[:, s_, kvh, :], "nk": 128, "biases": bs,
                                          "Ov": OvS[:, s_, :], "deps": [d_cmpKTs, d_cmpVAs, d_small, d_OvS]})
                        psvS = lambda g: banks[4][0:SB, 0:257]
                        attn(SB, 1, QTh, d_QTs, tiles, o4s[kvh], d_o4s[kvh], pslc=(psvS, [d_bank[4]], 1))
                        evac_s(b, kvh, 0)
                        P.op("dve", lambda e: e.tensor_scalar(out=rd1[:, 1:2], in0=banks[4][0:SB, 256:257], scalar1=1e-30, scalar2=None, op0=ALU.max),
                             reads=[d_bank[4]], writes=[d_rd1])
                        P.op("dve", lambda e: e.reciprocal(out=rd1[:, 1:2], in_=rd1[:, 1:2]), reads=[d_rd1], writes=[d_rd1])
                        P.op("dve", lambda e: e.tensor_scalar(out=pn[:], in0=banks[4][0:SB, 0:256], scalar1=rd1[:, 1:2], scalar2=None, op0=ALU.mult),
                             reads=[d_bank[4], d_rd1], writes=[d_pn])
                        P.op("pe", lambda e: e.matmul(out=banks[5][0:1, 0:256], lhsT=onec[:, 0:1], rhs=pn[:], start=True, stop=True),
                             reads=[d_onec, d_pn], writes=[d_bank[5]])
                        P.op("dve", lambda e: e.tensor_tensor(out=srow[:], in0=banks[5][0:1, 0:256], in1=forceS, op=ALU.add), reads=[d_bank[5], d_small], writes=[d_srow])
                        P.op("dve", lambda e: e.max(out=m16[:, 0:8], in_=srow[:]), reads=[d_srow], writes=[d_m16])
                        P.op("dve", lambda e: e.match_replace(out=wk2[:], in_to_replace=m16[:, 0:8], in_values=srow[:], imm_value=-3.0e38),
                             reads=[d_srow, d_m16], writes=[d_wk2])
                        P.op("dve", lambda e: e.max(out=m16[:, 8:16], in_=wk2[:]), reads=[d_wk2], writes=[d_m16])
                        P.op("dve", lambda e, kvh=kvh: e.tensor_scalar(out=selbB[kvh][:], in0=srow[:], scalar1=m16[:, 14:15], scalar2=NEG, op0=ALU.is_lt, op1=ALU.mult),
                             reads=[d_srow, d_m16], writes=[d_selbB[kvh]])
                    for k in range(8):
                        gb_ = k % 2
                        P.dma("pool", lambda e, k=k, gb_=gb_: e.indirect_dma_start(
                            out=G[gb_][:].rearrange("p a b -> p (a b)"), out_offset=None, in_=cache_sel[:, :],
                            in_offset=bass.IndirectOffsetOnAxis(ap=idx8[:, k:k + 1], axis=0)), reads=[d_idx8], writes=[d_G[gb_]])
                        for j in range(16):
                            if j % 4 == 0:
                                tb_, d_tb = (banks[4], d_bank[4]) if (j // 4) % 2 == 0 else (banks[5], d_bank[5])
                            P.op("pe", lambda e, gb_=gb_, j=j, tb_=tb_: e.transpose(out=tb_[:, (j % 4) * 128:(j % 4 + 1) * 128], in_=G[gb_][:, j, 0:128], identity=ident[:]),
                                 reads=[d_G[gb_], d_ident], writes=[d_tb])
                            if j % 4 == 3:
                                P.op("act", lambda e, gb_=gb_, j=j, tb_=tb_: e.copy(out=KT16[gb_][:, j - 3:j + 1, :].rearrange("p a b -> p (a b)"), in_=tb_[:, :]),
                                     reads=[d_tb], writes=[d_KT16[gb_]])
                        P.op("pool", lambda e, gb_=gb_: e.tensor_copy(out=V16[gb_][:, :, :, 0:64], in_=G[gb_][:, :, 128:256].rearrange("p j (h d) -> p j h d", h=2)),
                             reads=[d_G[gb_]], writes=[d_V16[gb_]])
                        for kvh in range(2):
                            QTh = QTs[kvh * 64:(kvh + 1) * 64, :, b]
                            tiles = []
                            for j in range(16):
                                hi = 1 if (16 * k + j) >= 64 else 0
                                tiles.append({"KT": KT16[gb_][kvh * 64:(kvh + 1) * 64, j, :], "V": V16[gb_][:, j, kvh, :], "nk": 128,
                                              "biases": [(selbB[kvh][0:1, hi:256:2], ones4)], "deps": [d_KT16[gb_], d_V16[gb_], d_selbB[kvh], d_small]})
                            attn(SB, 1, QTh, d_QTs, tiles, o4s[kvh], d_o4s[kvh], first=(k == 0), last=False)
                    for kvh in range(2):
                        QTh = QTs[kvh * 64:(kvh + 1) * 64, :, b]
                        tiles = [{"KT": KnT[kvh * 64:(kvh + 1) * 64, 0, :], "V": VnA[:, 0, kvh, :], "nk": SB,
                                  "biases": [(smallb[0:1, 256 + 4 * b:260 + 4 * b], ones4)], "deps": [d_KnT, d_VnA, d_small]}]
                        attn(SB, 1, QTh, d_QTs, tiles, o4s[kvh], d_o4s[kvh], first=False, last=True)
                        evac_s(b, kvh, 1)
                    P.dma("sp", lambda e, b=b: e.dma_start(out=Wt[:], in_=cache_win[b].rearrange("(m p) f -> p m f", p=128)), writes=[d_Wt])
                    for m in range(4):
                        P.op("pe", lambda e, m=m: e.transpose(out=banks[4][:, m * 128:(m + 1) * 128], in_=Wt[:, m, 0:128], identity=ident[:]),
                             reads=[d_Wt, d_ident], writes=[d_bank[4]])
                    P.op("act", lambda e: e.copy(out=KTw[:].rearrange("p a b -> p (a b)"), in_=banks[4][:, :]), reads=[d_bank[4]], writes=[d_KTw])
                    P.op("pool", lambda e: e.tensor_copy(out=VwA[:, :, :, 0:64], in_=Wt[:, :, 128:256].rearrange("p m (h d) -> p m h d", h=2)),
                         reads=[d_Wt], writes=[d_VwA])
                    for kvh in range(2):
                        QTh = QTs[kvh * 64:(kvh + 1) * 64, :, b]
                        tiles = []
                        for m in range(4):
                            bs = [(wm0row, ones4)] if m == 0 else []
                            tiles.append({"KT": KTw[kvh * 64:(kvh + 1) * 64, m, :], "V": VwA[:, m, kvh, :], "nk": 128, "biases": bs,
                                          "deps": [d_KTw, d_VwA, d_small]})
                        tiles.append({"KT": KnT[kvh * 64:(kvh + 1) * 64, 1, :], "V": VnA[:, 1, kvh, :], "nk": SB,
                                      "biases": [(smallb[0:1, 256 + 4 * b:260 + 4 * b], ones4)], "deps": [d_KnT, d_VnA, d_small]})
                        attn(SB, 1, QTh, d_QTs, tiles, o4s[kvh], d_o4s[kvh])
                        evac_s(b, kvh, 2)
                P.barrier()
                osall = P.sb([SB, 3, 8, 64], F32, sS); d_osall = Dep()
                onsa = P.sb([SB, 8, 64], F32, sS); d_onsa = Dep()
                tmpo = P.sb([SB, 8, 64], F32, sS); d_tmpo = Dep()
                P.dma("sp", lambda e: e.dma_start(out=osall[:].rearrange("p a b c -> p (a b c)"), in_=os_d.rearrange("b r h d -> b (r h d)")), writes=[d_osall])
                sgv = sg[:].rearrange("p (h r) -> p h r", r=3)
                for r in range(3):
                    dst_, d_dst_ = (onsa, d_onsa) if r == 0 else (tmpo, d_tmpo)
                    P.op("dve", lambda e, r=r, dst_=dst_: e.tensor_tensor(out=dst_[:], in0=osall[:, r, :, :], in1=sgv[:, :, r].unsqueeze(2).to_broadcast([SB, 8, 64]), op=ALU.mult),
                         reads=[d_osall, d_sg], writes=[d_dst_])
                    if r > 0:
                        P.op("dve", lambda e: e.tensor_tensor(out=onsa[:], in0=onsa[:], in1=tmpo[:], op=ALU.add), reads=[d_onsa, d_tmpo], writes=[d_onsa])
                onTs = P.sb([128, 4, SB], BF16, sS); d_onTs = Dep()
                onf = onsa[:].rearrange("p h d -> p (h d)")
                P.dma("sp", lambda e: e.dma_start(out=dbg_ons, in_=onf), reads=[d_onsa])
                for c in range(4):
                    P.op("pe", lambda e, c=c: e.transpose(out=banks[3][:, c * SB:(c + 1) * SB], in_=onf[:, c * 128:(c + 1) * 128], identity=ident[0:SB, 0:SB]),
                         reads=[d_onsa, d_ident], writes=[d_bank[3]])
                P.op("act", lambda e: e.copy(out=onTs[:].rearrange("p a b -> p (a b)"), in_=banks[3][:, 0:4 * SB]), reads=[d_bank[3]], writes=[d_onTs])
                P.dma("sp", lambda e: e.dma_start(out=onTs_d, in_=onTs[:]), reads=[d_onTs])
                P.barrier()

        def phase_b2(groups):
            with ExitStack() as s2:
                Wgab = P.sb([128, 8, 2048], BF16, s2); d_Wgab = Dep()
                Wbn = P.sb([128, 4, 1024], BF16, s2); d_Wbn = Dep()
                Wbr = P.sb([128, 8, 1024], BF16, s2); d_Wbr = Dep()
                Wout = P.sb([128, 8, 1024], BF16, s2); d_Wout = Dep()
                P.dma("pool", lambda e: e.dma_start(out=Wgab[:], in_=w_in[:, C_GA:C_GA + 2048].rearrange("(c p) n -> p c n", p=128)), writes=[d_Wgab])
                P.dma("pool", lambda e: e.dma_start(out=Wbn[:], in_=w_bn.rearrange("(c p) n -> p c n", p=128)), writes=[d_Wbn])
                P.dma("pool", lambda e: e.dma_start(out=Wbr[:], in_=w_br.rearrange("(c p) n -> p c n", p=128)), writes=[d_Wbr])
                P.dma("pool", lambda e: e.dma_start(out=Wout[:], in_=w_out.rearrange("(c p) n -> p c n", p=128)), writes=[d_Wout])
                hT4 = P.sb([128, 8, 512], BF16, s2); d_hT4 = Dep()
                onT4 = P.sb([128, 4, 512], BF16, s2); d_onT4 = Dep()
                yrT4 = P.sb([128, 8, 512], BF16, s2); d_yrT4 = Dep()
                mixT = P.sb([128, 8, 512], BF16, s2); d_mixT = Dep()
                sga = [P.sb([128, 512], F32, s2) for _ in range(2)]; d_sga = [Dep(), Dep()]
                sgb = [P.sb([128, 512], F32, s2) for _ in range(2)]; d_sgb = [Dep(), Dep()]
                t1 = [P.sb([128, 512], F32, s2) for _ in range(2)]; d_t1 = [Dep(), Dep()]
                t2 = [P.sb([128, 512], F32, s2) for _ in range(2)]; d_t2 = [Dep(), Dep()]
                xa = [P.sb([128, D], F32, s2) for _ in range(2)]; d_xa = [Dep(), Dep()]
                x1t = [P.sb([128, D], F32, s2) for _ in range(2)]; d_x1t = [Dep(), Dep()]
                gtaS = P.sb([SB, D], F32, s2); d_gtaS = Dep()
                P.dma("sp", lambda e: e.dma_start(out=gtaS[:], in_=mods_d[:, 2 * D:3 * D]), writes=[d_gtaS])
                for grp in groups:
                    n = grp["ntok"]
                    grp["load_a"](hT4, d_hT4, onT4, d_onT4, yrT4, d_yrT4)
                    for fc in range(8):
                        b = fc % 2
                        for (bk, W, dW, kc, c0, src, dsrc) in ((0, Wgab, d_Wgab, 8, fc * 128, hT4, d_hT4), (1, Wgab, d_Wgab, 8, 1024 + fc * 128, hT4, d_hT4),
                                                              (2, Wbn, d_Wbn, 4, fc * 128, onT4, d_onT4), (3, Wbr, d_Wbr, 8, fc * 128, yrT4, d_yrT4)):
                            for k in range(kc):
                                P.op("pe", lambda e, bk=bk, W=W, k=k, c0=c0, src=src, kc=kc, n=n: e.matmul(
                                    out=banks[bk][:, 0:n], lhsT=W[:, k, c0:c0 + 128], rhs=src[:, k, 0:n], start=(k == 0), stop=(k == kc - 1)),
                                    reads=[dW, dsrc], writes=[d_bank[bk]])
                        P.op("act", lambda e, b=b, n=n: e.activation(out=sga[b][:, 0:n], in_=banks[0][:, 0:n], func=ACTF.Sigmoid), reads=[d_bank[0]], writes=[d_sga[b]])
                        P.op("act", lambda e, b=b, n=n: e.activation(out=sgb[b][:, 0:n], in_=banks[1][:, 0:n], func=ACTF.Sigmoid), reads=[d_bank[1]], writes=[d_sgb[b]])
                        P.op("dve", lambda e, b=b, n=n: e.tensor_tensor(out=t1[b][:, 0:n], in0=banks[2][:, 0:n], in1=sga[b][:, 0:n], op=ALU.mult),
                             reads=[d_bank[2], d_sga[b]], writes=[d_t1[b]])
                        P.op("dve", lambda e, b=b, n=n: e.tensor_tensor(out=t2[b][:, 0:n], in0=banks[3][:, 0:n], in1=sgb[b][:, 0:n], op=ALU.mult),
                             reads=[d_bank[3], d_sgb[b]], writes=[d_t2[b]])
                        P.op("pool", lambda e, b=b, fc=fc, n=n: e.tensor_tensor(out=mixT[:, fc, 0:n], in0=t1[b][:, 0:n], in1=t2[b][:, 0:n], op=ALU.add),
                             reads=[d_t1[b], d_t2[b]], writes=[d_mixT])
                    off = 0
                    for ti, (xsrc, nt_, gta, d_gta, x1dst) in enumerate(grp["tiles_a"](gtaS, d_gtaS)):
                        b = ti % 2
                        P.dma("sp", lambda e, b=b, xsrc=xsrc, nt_=nt_: e.dma_start(out=xa[b][0:nt_, :], in_=xsrc), writes=[d_xa[b]])
                        for half in range(2):
                            bk = 4 + half
                            for k in range(8):
                                P.op("pe", lambda e, bk=bk, k=k, off=off, nt_=nt_, half=half: e.matmul(
                                    out=banks[bk][0:nt_, :], lhsT=mixT[:, k, off:off + nt_], rhs=Wout[:, k, half * 512:(half + 1) * 512],
                                    start=(k == 0), stop=(k == 7)), reads=[d_mixT, d_Wout], writes=[d_bank[bk]])
                            P.op("dve", lambda e, bk=bk, b=b, nt_=nt_, half=half, gta=gta: e.tensor_tensor(
                                out=x1t[b][0:nt_, half * 512:(half + 1) * 512], in0=banks[bk][0:nt_, :], in1=gta[0:nt_, half * 512:(half + 1) * 512], op=ALU.mult),
                                reads=[d_bank[bk], d_gta], writes=[d_x1t[b]])
                        P.op("pool", lambda e, b=b, nt_=nt_: e.tensor_tensor(out=x1t[b][0:nt_, :], in0=x1t[b][0:nt_, :], in1=xa[b][0:nt_, :], op=ALU.add),
                             reads=[d_x1t[b], d_xa[b]], writes=[d_x1t[b]])
                        P.dma("sp", lambda e, b=b, nt_=nt_, x1dst=x1dst: e.dma_start(out=x1dst, in_=x1t[b][0:nt_, :]), reads=[d_x1t[b]])
                        off += nt_
                P.barrier()
            with ExitStack() as s3:
                Wup = P.sb([128, 8, 4096], BF16, s3); d_Wup = Dep()
                Wdn = P.sb([128, 32, 1024], BF16, s3); d_Wdn = Dep()
                for q4 in range(4):
                    P.dma("pool", lambda e, q4=q4: e.dma_start(out=Wup[:, :, q4 * 1024:(q4 + 1) * 1024],
                                                              in_=w_up[:, q4 * 1024:(q4 + 1) * 1024].rearrange("(c p) n -> p c n", p=128)), writes=[d_Wup])
                    P.dma("pool", lambda e, q4=q4: e.dma_start(out=Wdn[:, q4 * 8:(q4 + 1) * 8, :],
                                                              in_=w_down[q4 * 1024:(q4 + 1) * 1024, :].rearrange("(c p) n -> p c n", p=128)), writes=[d_Wdn])
                nfb = P.sb([128, D], F32, s3); d_nfb = Dep()
                P.dma("sp", lambda e: e.dma_start(out=nfb[:], in_=norm_final_d.partition_broadcast(128)[:, 0, :]), writes=[d_nfb])
                h2T4 = P.sb([128, 8, 256], BF16, s3); d_h2T4 = Dep()
                uT = P.sb([128, 32, 256], BF16, s3); d_uT = Dep()
                x1in = [P.sb([128, D], F32, s3) for _ in range(2)]; d_x1in = [Dep(), Dep()]
                rl = [P.sb([128, 256], BF16, s3) for _ in range(2)]; d_rl = [Dep(), Dep()]
                x2t_ = P.sb([128, D], F32, s3); x2t = [x2t_, x2t_]; d_x2t_ = Dep(); d_x2t = [d_x2t_, d_x2t_]
                ss2 = P.sb([128, 2], F32, s3); d_ss2 = Dep()
                wk3 = mk_wk(s3)
                modS3 = P.sb([SB, 3, D], BF16, s3); d_modS3 = Dep()
                P.dma("pool", lambda e: e.dma_start(out=modS3[:], in_=mods_d[:, 3 * D:6 * D].rearrange("s (a d) -> s a d", a=3)), writes=[d_modS3])
                P.dma("sp", lambda e: e.dma_start(out=x2t[1][0:SB, :], in_=norm_mlp.partition_broadcast(SB)[:, 0, :]), writes=[d_x2t[1]])
                P.op("dve", lambda e: e.scalar_tensor_tensor(out=modS3[:, 1, :], in0=modS3[:, 1, :], scalar=1.0, in1=x2t[1][0:SB, :], op0=ALU.add, op1=ALU.mult),
                     reads=[d_modS3, d_x2t[1]], writes=[d_modS3])
                for grp in groups:
                    tl_all = grp["tiles_b"](modS3, d_modS3)
                    for s0_ in range(0, len(tl_all), 2):
                        tl_b = tl_all[s0_:s0_ + 2]
                        n = sum(t_[1] for t_ in tl_b)
                        off = 0
                        for ti, (x1src, nt_, G2, d_G2, shf, d_shf, gtf, d_gtf, ydst, shc) in enumerate(tl_b):
                            P.dma("sp", lambda e, ti=ti, x1src=x1src, nt_=nt_: e.dma_start(out=x1in[ti][0:nt_, :], in_=x1src), writes=[d_x1in[ti]])
                            norm_mod_T(x1in[ti], d_x1in[ti], nt_, G2, d_G2, shf, d_shf, wk3, h2T4[:, :, off:off + nt_], d_h2T4, shc, d_shcol)
                            off += nt_
                        for fc in range(32):
                            b = fc % 2
                            bk = fc % 4
                            for k in range(8):
                                P.op("pe", lambda e, bk=bk, k=k, fc=fc, n=n: e.matmul(out=banks[bk][:, 0:n], lhsT=Wup[:, k, fc * 128:(fc + 1) * 128], rhs=h2T4[:, k, 0:n],
                                                                                    start=(k == 0), stop=(k == 7)), reads=[d_Wup, d_h2T4], writes=[d_bank[bk]])
                            P.op("act", lambda e, bk=bk, b=b, n=n: e.activation(out=rl[b][:, 0:n], in_=banks[bk][:, 0:n], func=ACTF.Relu), reads=[d_bank[bk]], writes=[d_rl[b]])
                            P.op("pool", lambda e, b=b, fc=fc, n=n: e.tensor_tensor(out=uT[:, fc, 0:n], in0=rl[b][:, 0:n], in1=rl[b][:, 0:n], op=ALU.mult),
                                 reads=[d_rl[b]], writes=[d_uT])
                        off = 0
                        for ti, (x1src, nt_, G2, d_G2, shf, d_shf, gtf, d_gtf, ydst, shc) in enumerate(tl_b):
                            b = ti % 2
                            for half in range(2):
                                bk = 4 + half
                                for k in range(32):
                                    P.op("pe", lambda e, bk=bk, k=k, off=off, nt_=nt_, half=half: e.matmul(
                                        out=banks[bk][0:nt_, :], lhsT=uT[:, k, off:off + nt_], rhs=Wdn[:, k, half * 512:(half + 1) * 512],
                                        start=(k == 0), stop=(k == 31)), reads=[d_uT, d_Wdn], writes=[d_bank[bk]])
                                P.op("dve", lambda e, bk=bk, b=b, nt_=nt_, half=half, gtf=gtf: e.tensor_tensor(
                                    out=x2t[b][0:nt_, half * 512:(half + 1) * 512], in0=banks[bk][0:nt_, :], in1=gtf[0:nt_, half * 512:(half + 1) * 512], op=ALU.mult),
                                    reads=[d_bank[bk], d_gtf], writes=[d_x2t[b]])
                            P.op("pool", lambda e, b=b, ti=ti, nt_=nt_: e.tensor_tensor(out=x2t[b][0:nt_, :], in0=x2t[b][0:nt_, :], in1=x1in[ti][0:nt_, :], op=ALU.add),
                                 reads=[d_x2t[b], d_x1in[ti]], writes=[d_x2t[b]])
                            P.op("act", lambda e, b=b, nt_=nt_: e.activation(out=junk_sh[0:nt_, :], in_=x2t[b][0:nt_, :], func=ACTF.Square, accum_out=ss2[0:nt_, 0:1]),
                                 reads=[d_x2t[b]], writes=[d_junk_sh, d_ss2])
                            P.op("act", lambda e, nt_=nt_: e.activation(out=ss2[0:nt_, 1:2], in_=ss2[0:nt_, 0:1], func=ACTF.Sqrt, bias=epsT[0:nt_, :], scale=1.0 / D),
                                 reads=[d_ss2, d_eps], writes=[d_ss2])
                            P.op("dve", lambda e, nt_=nt_: e.reciprocal(out=ss2[0:nt_, 1:2], in_=ss2[0:nt_, 1:2]), reads=[d_ss2], writes=[d_ss2])
                            P.op("dve", lambda e, b=b, nt_=nt_: e.scalar_tensor_tensor(out=x2t[b][0:nt_, :], in0=x2t[b][0:nt_, :], scalar=ss2[0:nt_, 1:2], in1=nfb[0:nt_, :],
                                                                                     op0=ALU.mult, op1=ALU.mult), reads=[d_x2t[b], d_ss2, d_nfb], writes=[d_x2t[b]])
                            P.dma("sp", lambda e, b=b, nt_=nt_, ydst=ydst: e.dma_start(out=ydst, in_=x2t[b][0:nt_, :]), reads=[d_x2t[b]])
                            off += nt_
                P.barrier()

        with ExitStack() as sA:
            WA = P.sb([128, 8, 2304], BF16, sA); d_WA = Dep()
            segs = [(C_KVC, 128, 0), (C_KVS, 128, 128), (C_KVW, 128, 256), (C_KVC + 128, 128, 384),
                    (C_KVS + 128, 128, 512), (C_KVW + 128, 128, 640), (C_RK, 512, 768), (C_RV, 1024, 1280)]
            for (src, n, dst) in segs:
                P.dma("pool", lambda e, src=src, n=n, dst=dst: e.dma_start(
                    out=WA[:, :, dst:dst + n], in_=w_in[:, src:src + n].rearrange("(c p) n -> p c n", p=128)), writes=[d_WA])
            inda = P.sb([128, 8], F32, sA); indab = P.sb([128, 8], BF16, sA); d_inda = Dep()
            wrep = P.sb([128, 512], F32, sA); d_wrep = Dep()
            bcol = P.sb([128, 2], F32, sA); d_bcol = Dep()
            kdsc = P.sb([128, 4], F32, sA); d_kdsc = Dep()
            oh = P.sb([128, 8], F32, sA); d_oh = Dep()
            P.dma("sp", lambda e: e.dma_start(out=inda[:], in_=inda_d), writes=[d_inda])
            P.op("dve", lambda e: e.tensor_copy(out=indab[:], in_=inda[:]), reads=[d_inda], writes=[d_inda])
            P.dma("sp", lambda e: e.dma_start(out=wrep[:], in_=wrep_d), writes=[d_wrep])
            P.dma("sp", lambda e: e.dma_start(out=bcol[:], in_=bcol_d), writes=[d_bcol])
            P.dma("sp", lambda e: e.dma_start(out=kdsc[:], in_=kdsc_d), writes=[d_kdsc])
            P.dma("sp", lambda e: e.dma_start(out=oh[:], in_=oh_d), writes=[d_oh])
            FS = P.sb([128, 4, 1024], F32, sA); d_FS = Dep()
            S = P.sb([128, 4, 256], F32, sA); d_S = Dep()
            Sown = P.sb([128, 1024], F32, sA); d_Sown = Dep()
            tmpS = P.sb([128, 1024], F32, sA); d_tmpS = Dep()
            P.op("pool", lambda e: e.memset(S[:].rearrange("p a b -> p (a b)"), 0.0), writes=[d_S])
            NBUF = 3
            xts = [P.sb([128, D], F32, sA) for _ in range(NBUF)]; d_xts = [Dep() for _ in range(NBUF)]
            rps = [P.sb([128, 192], F32, sA) for _ in range(NBUF)]; d_rps = [Dep() for _ in range(NBUF)]
            wks = [mk_wk(sA) for _ in range(NBUF)]
            hTs = [P.sb([128, 8, 128], BF16, sA) for _ in range(NBUF)]; d_hTs = [Dep() for _ in range(NBUF)]
            kvos = [P.sb([128, 3, 256], F32, sA) for _ in range(NBUF)]; d_kvos = [Dep() for _ in range(NBUF)]
            kraw = [P.sb([128, 896], F32, sA) for _ in range(NBUF)]; d_kraw = [Dep() for _ in range(NBUF)]
            tm = [[P.sb([128, 256], F32, sA) for _ in range(4)] for _ in range(NBUF)]
            d_tm = [[Dep() for _ in range(4)] for _ in range(NBUF)]
            rkr = [P.sb([128, 4, 128], F32, sA) for _ in range(NBUF)]; d_rkr = [Dep() for _ in range(NBUF)]
            kd = [P.sb([128, 4, 128], BF16, sA) for _ in range(NBUF)]; d_kd = [Dep() for _ in range(NBUF)]
            vb = [P.sb([128, 1024], BF16, sA) for _ in range(NBUF)]; d_vb = [Dep() for _ in range(NBUF)]
            cw = [P.sb([128, 2, 256], BF16, sA) for _ in range(NBUF)]; d_cw = [Dep() for _ in range(NBUF)]
            kwt = [P.sb([128, 128], BF16, sA) for _ in range(NBUF)]; d_kwt = [Dep() for _ in range(NBUF)]
            kst = [P.sb([128, 128], BF16, sA) for _ in range(NBUF)]; d_kst = [Dep() for _ in range(NBUF)]
            vwb = [P.sb([128, 2, 65], BF16, sA) for _ in range(NBUF)]; d_vwb = [Dep() for _ in range(NBUF)]
            vsb = [P.sb([128, 2, 65], BF16, sA) for _ in range(NBUF)]; d_vsb = [Dep() for _ in range(NBUF)]
            for b_ in range(NBUF):
                P.op("pool", lambda e, b_=b_: e.memset(vwb[b_][:].rearrange("p a b -> p (a b)"), 1.0), writes=[d_vwb[b_]])
                P.op("pool", lambda e, b_=b_: e.memset(vsb[b_][:].rearrange("p a b -> p (a b)"), 1.0), writes=[d_vsb[b_]])
            zA, zB, zK, zV0, zV1, zC = banks
            dzA, dzB, dzK, dzV0, dzV1, dzC = d_bank
            d_zBt = dzB

            ntA = NT if stage >= 1 else 0
            if 'K_NTA' in os.environ:
                ntA = int(os.environ['K_NTA'])
            if stage == 2:
                ntA = 8 * DBG_NBLK
                P.op("pool", lambda e: e.memset(FS[:].rearrange("p a b -> p (a b)"), 0.0), writes=[d_FS])
            for tt in range(ntA):
                b = tt % NBUF
                xt, d_xt, rp, d_rp, hT, d_hT, kvo, d_kvo = xts[b], d_xts[b], rps[b], d_rps[b], hTs[b], d_hTs[b], kvos[b], d_kvos[b]
                P.dma("sp", lambda e, tt=tt, xt=xt: e.dma_start(out=xt[:], in_=x_all[tt * 128:(tt + 1) * 128, :]), writes=[d_xt])
                P.dma("sp", lambda e, tt=tt, rp=rp: e.dma_start(out=rp[:], in_=rope_all[tt * 128:(tt + 1) * 128, :]), writes=[d_rp])
                norm_mod_T(xt, d_xt, 128, G1P, d_G1P, modP4[:, 0, :], d_modP, wks[b], hT, d_hT, shcol[:, 0, :], d_shcol)
                for (zz, dz, c0, n) in ((zA, dzA, 0, 512), (zB, dzB, 512, 256), (zK, dzK, 768, 512), (zV0, dzV0, 1280, 512), (zV1, dzV1, 1792, 512)):
                    for k in range(8):
                        P.op("pe", lambda e, zz=zz, k=k, c0=c0, n=n, hT=hT: e.matmul(out=zz[:, 0:n], lhsT=hT[:, k, :], rhs=WA[:, k, c0:c0 + n],
                                                                                    start=(k == 0), stop=(k == 7)),
                             reads=[d_hT, d_WA], writes=[dz])
                P.op("act", lambda e, b=b: e.copy(out=kraw[b][:, 0:384], in_=zA[:, 0:384]), reads=[dzA], writes=[d_kraw[b]])
                P.op("act", lambda e, b=b: e.copy(out=kraw[b][:, 384:896], in_=zK[:, 0:512]), reads=[dzK], writes=[d_kraw[b]])
                src = kraw[b][:, 0:384].rearrange("p (j h two e) -> p j h two e", j=3, h=2, two=2)
                dst = kvo[:, :, 0:128].rearrange("p j (h two e) -> p j h two e", h=2, two=2)
                shp = [128, 3, 2, 32]
                tv = [t[:, 0:192].rearrange("p (j h e) -> p j h e", j=3, h=2) for t in tm[b]]
                cosn = rp[:, 0:32].unsqueeze(1).unsqueeze(1).to_broadcast(shp)
                sinn = rp[:, 32:64].unsqueeze(1).unsqueeze(1).to_broadcast(shp)
                rope(lambda hf, src=src: src[:, :, :, hf, :], lambda hf, dst=dst: dst[:, :, :, hf, :], cosn, sinn, shp, tv,
                     d_kraw[b], d_kvo, d_rp, d_tm[b])
                P.op("act", lambda e, kvo=kvo: e.copy(out=kvo[:, 0, 128:256], in_=zA[:, 384:512]), reads=[dzA], writes=[d_kvo])
                P.op("act", lambda e, kvo=kvo: e.copy(out=kvo[:, 1:3, 128:256], in_=zB[:, 0:256].rearrange("p (j f) -> p j f", j=2)),
                     reads=[dzB], writes=[d_kvo])
                P.dma("sp", lambda e, tt=tt, kvo=kvo: e.dma_start(out=o_cmp[tt * 128:(tt + 1) * 128, :], in_=kvo[:, 0, :]), reads=[d_kvo])
                P.dma("sp", lambda e, tt=tt, kvo=kvo: e.dma_start(out=o_sel[tt * 128:(tt + 1) * 128, :], in_=kvo[:, 1, :]), reads=[d_kvo])
                if tt >= NT - 4:
                    P.dma("sp", lambda e, tt=tt, kvo=kvo: e.dma_start(out=o_win[(tt - NT + 4) * 128:(tt - NT + 5) * 128, :], in_=kvo[:, 2, :]),
                          reads=[d_kvo])
                P.op("pool", lambda e, kvo=kvo, b=b: e.tensor_copy(out=vsb[b][:, :, 0:64], in_=kvo[:, 1, 128:256].rearrange("p (h d) -> p h d", h=2)),
                     reads=[d_kvo], writes=[d_vsb[b]])
                P.dma("sp", lambda e, tt=tt, b=b: e.dma_start(out=vs_d[tt * 128:(tt + 1) * 128, :], in_=vsb[b][:].rearrange("p a b -> p (a b)")),
                      reads=[d_vsb[b]])
                P.op("pool", lambda e, kvo=kvo, b=b: e.tensor_copy(out=vwb[b][:, :, 0:64], in_=kvo[:, 2, 128:256].rearrange("p (h d) -> p h d", h=2)),
                     reads=[d_kvo], writes=[d_vwb[b]])
                P.dma("sp", lambda e, tt=tt, b=b: e.dma_start(out=vw_d[tt * 128:(tt + 1) * 128, :], in_=vwb[b][:].rearrange("p a b -> p (a b)")),
                      reads=[d_vwb[b]])
                P.op("pe", lambda e, kvo=kvo: e.transpose(out=zB[:, 256:384], in_=kvo[:, 1, 0:128], identity=ident[:]),
                     reads=[d_kvo, d_ident], writes=[d_zBt])
                P.op("pe", lambda e, kvo=kvo: e.transpose(out=zB[:, 384:512], in_=kvo[:, 2, 0:128], identity=ident[:]),
                     reads=[d_kvo, d_ident], writes=[d_zBt])
                P.op("act", lambda e, b=b: e.copy(out=kst[b][:], in_=zB[:, 256:384]), reads=[d_zBt], writes=[d_kst[b]])
                P.dma("sp", lambda e, tt=tt, b=b: e.dma_start(out=ksT_d[:, tt * 128:(tt + 1) * 128], in_=kst[b][:]), reads=[d_kst[b]])
                P.op("act", lambda e, b=b: e.copy(out=kwt[b][:], in_=zB[:, 384:512]), reads=[d_zBt], writes=[d_kwt[b]])
                P.dma("sp", lambda e, tt=tt, b=b: e.dma_start(out=kwT_d[:, tt * 128:(tt + 1) * 128], in_=kwt[b][:]), reads=[d_kwt[b]])
                for r in range(2):
                    P.op("pool", lambda e, r=r, b=b, kvo=kvo: e.tensor_tensor(out=cw[b][:, r, :], in0=kvo[:, 0, :], in1=wrep[:, r * 256:(r + 1) * 256],
                                                                             op=ALU.mult), reads=[d_kvo, d_wrep], writes=[d_cw[b]])
                for r in range(2):
                    for kv in range(2):
                        i4 = r * 2 + kv
                        P.op("pe", lambda e, r=r, kv=kv, i4=i4, b=b: e.matmul(out=zC[:, i4 * 8:(i4 + 1) * 8], lhsT=cw[b][:, r, kv * 128:(kv + 1) * 128],
                                                                             rhs=indab[:], start=True, stop=True),
                             reads=[d_cw[b], d_inda], writes=[dzC])
                P.op("act", lambda e, tt=tt: e.copy(out=FS[:, :, tt * 8:(tt + 1) * 8], in_=zC[:, 0:32].rearrange("p (a s) -> p a s", a=4)),
                     reads=[dzC], writes=[d_FS])
                srck = kraw[b][:, 384:896].rearrange("p (h two e) -> p h two e", h=4, two=2)
                dstk = rkr[b][:].rearrange("p h (two e) -> p h two e", two=2)
                shpk = [128, 4, 64]
                tvk = [t[:, 0:256].rearrange("p (h e) -> p h e", h=4) for t in tm[b]]
                cosk = rp[:, 64:128].unsqueeze(1).to_broadcast(shpk)
                sink = rp[:, 128:192].unsqueeze(1).to_broadcast(shpk)
                rope(lambda hf, srck=srck: srck[:, :, hf, :], lambda hf, dstk=dstk: dstk[:, :, hf, :], cosk, sink, shpk, tvk,
                     d_kraw[b], d_rkr[b], d_rp, d_tm[b])
                P.op("pool", lambda e, b=b: e.tensor_tensor(out=kd[b][:], in0=rkr[b][:], in1=kdsc[:].unsqueeze(2).to_broadcast([128, 4, 128]),
                                                           op=ALU.mult), reads=[d_rkr[b], d_kdsc], writes=[d_kd[b]])
                P.op("act", lambda e, b=b: e.copy(out=vb[b][:, 0:512], in_=zV0[:, :]), reads=[dzV0], writes=[d_vb[b]])
                P.op("act", lambda e, b=b: e.copy(out=vb[b][:, 512:1024], in_=zV1[:, :]), reads=[dzV1], writes=[d_vb[b]])
                j = tt % 8
                if j == 0:
                    P.op("dve", lambda e: e.tensor_scalar(out=Sown[:], in0=S[:].rearrange("p a b -> p (a b)"), scalar1=oh[:, 0:1], scalar2=None,
                                                          op0=ALU.mult), reads=[d_S, d_oh], writes=[d_Sown])
                else:
                    P.op("dve", lambda e, j=j: e.scalar_tensor_tensor(out=Sown[:], in0=S[:].rearrange("p a b -> p (a b)"), scalar=oh[:, j:j + 1],
                                                                     in1=Sown[:], op0=ALU.mult, op1=ALU.add),
                         reads=[d_S, d_oh, d_Sown], writes=[d_Sown])
                if j == 7:
                    P.dma("sp", lambda e, i=tt // 8: e.dma_start(out=sown_d[i], in_=Sown[:]), reads=[d_Sown])
                for h in range(4):
                    zz, dz = (zV0, dzV0) if h < 2 else (zV1, dzV1)
                    c0 = (h % 2) * 256
                    P.op("pe", lambda e, h=h, zz=zz, c0=c0, b=b: e.matmul(out=zz[:, c0:c0 + 256], lhsT=kd[b][:, h, :], rhs=vb[b][:, h * 256:(h + 1) * 256],
                                                                         start=True, stop=True),
                         reads=[d_kd[b], d_vb[b]], writes=[dz])
                    P.op("dve", lambda e, h=h, zz=zz, c0=c0: e.scalar_tensor_tensor(out=S[:, h, :], in0=S[:, h, :], scalar=float(math.exp(128 * LG[h])),
                                                                                   in1=zz[:, c0:c0 + 256], op0=ALU.mult, op1=ALU.add),
                         reads=[d_S, dz], writes=[d_S])
            if ntA:
                P.dma("sp", lambda e: e.dma_start(out=o_state.rearrange("h k v -> k h v"), in_=S[:]), reads=[d_S])
                ctmp, cv = tmpS, Sown
                P.op("pool", lambda e: e.memset(cmpKT[:], 0.0), writes=[d_cmpKT])
                P.op("pool", lambda e: e.memset(cv[:], 0.0), reads=[], writes=[d_Sown])
                P.op("dve", lambda e: e.tensor_tensor(out=ctmp[:, 0:1023], in0=FS[:, 0, 0:1023], in1=FS[:, 2, 1:1024], op=ALU.add),
                     reads=[d_FS], writes=[d_tmpS])
                P.op("dve", lambda e: e.tensor_scalar(out=cmpKT[:, 0:1023], in0=ctmp[:, 0:1023], scalar1=bcol[:, 0:1], scalar2=None, op0=ALU.add),
                     reads=[d_tmpS, d_bcol], writes=[d_cmpKT])
                P.op("dve", lambda e: e.tensor_tensor(out=ctmp[:, 0:1023], in0=FS[:, 1, 0:1023], in1=FS[:, 3, 1:1024], op=ALU.add),
                     reads=[d_FS], writes=[d_tmpS])
                P.op("dve", lambda e: e.tensor_scalar(out=cv[:, 0:1023], in0=ctmp[:, 0:1023], scalar1=bcol[:, 1:2], scalar2=None, op0=ALU.add),
                     reads=[d_tmpS, d_bcol], writes=[d_Sown])
                for jt in range(8):
                    P.op("pe", lambda e, jt=jt: e.transpose(out=pT[:, jt * 128:(jt + 1) * 128], in_=cv[:, jt * 128:(jt + 1) * 128], identity=ident[:]),
                         reads=[d_Sown, d_ident], writes=[d_pTs[jt // 4]])
                P.op("act", lambda e: e.copy(out=cmpVA[:, :, :, 0:64], in_=pT[:].rearrange("p (j h d) -> p j h d", j=8, h=2)),
                     reads=d_pTs, writes=[d_cmpVA])
            P.barrier()

        if stage >= 2:
            phase_b1()
        if stage >= 4:
            phase_s1()
        if stage >= 3:
            groups = []
            for gq in range(4):
                def load_a(hT4, d_hT4, onT4, d_onT4, yrT4, d_yrT4, gq=gq):
                    for j in range(4):
                        i_ = 4 * gq + j
                        P.dma("sp", lambda e, i_=i_, j=j: e.dma_start(out=hT4[:, :, j * 128:(j + 1) * 128], in_=hT_d[i_]), writes=[d_hT4])
                        P.dma("sp", lambda e, i_=i_, j=j: e.dma_start(out=onT4[:, :, j * 128:(j + 1) * 128], in_=onT_d[i_]), writes=[d_onT4])
                        P.dma("sp", lambda e, i_=i_, j=j: e.dma_start(out=yrT4[:, :, j * 128:(j + 1) * 128], in_=yrT_d[i_]), writes=[d_yrT4])

                def tiles_a(gtaS, d_gtaS, gq=gq):
                    return [(x_own[(4 * gq + j) * 128:(4 * gq + j + 1) * 128, :], 128, modP4[:, 1, :], d_modP,
                             x1_d[(4 * gq + j) * 128:(4 * gq + j + 1) * 128, :]) for j in range(4)]

                def tiles_b(modS3, d_modS3, gq=gq):
                    return [(x1_d[(4 * gq + j) * 128:(4 * gq + j + 1) * 128, :], 128, G2P, d_G2P, modP4[:, 2, :], d_modP, modP4[:, 3, :], d_modP,
                             y_own[(4 * gq + j) * 128:(4 * gq + j + 1) * 128, :], shcol[:, 1, :]) for j in range(4)]
                groups.append({"ntok": 512, "load_a": load_a, "tiles_a": tiles_a, "tiles_b": tiles_b})
            if stage >= 4:
                def load_as(hT4, d_hT4, onT4, d_onT4, yrT4, d_yrT4):
                    P.dma("sp", lambda e: e.dma_start(out=hT4[:, :, 0:SB], in_=hTs_d), writes=[d_hT4])
                    P.dma("sp", lambda e: e.dma_start(out=onT4[:, :, 0:SB], in_=onTs_d), writes=[d_onT4])
                    P.dma("sp", lambda e: e.dma_start(out=yrT4[:, :, 0:SB], in_=yrTs_d), writes=[d_yrT4])
                groups.append({"ntok": SB, "load_a": load_as,
                               "tiles_a": lambda gtaS, d_gtaS: [(x_smp, SB, gtaS[:], d_gtaS, x1s_d)],
                               "tiles_b": lambda modS3, d_modS3: [(x1s_d, SB, modS3[:, 1, :], d_modS3, modS3[:, 0, :], d_modS3, modS3[:, 2, :], d_modS3, y_smp, None)]})
            phase_b2(groups)

        P.barrier()
        P.emit()
    return nc


def _rope_tables(pos):
    pos = np.asarray(pos, np.float32)
    out = np.zeros((len(pos), 192), np.float32)
    inv32 = (10000.0 ** (-np.arange(0, 64, 2, dtype=np.float32) / 64)).astype(np.float32)
    inv64 = (10000.0 ** (-np.arange(0, 128, 2, dtype=np.float32) / 128)).astype(np.float32)
    a32 = pos[:, None] * inv32[None, :]
    a64 = pos[:, None] * inv64[None, :]
    out[:, 0:32] = np.cos(a32)
    out[:, 32:64] = np.sin(a32)
    out[:, 64:128] = np.cos(a64)
    out[:, 128:192] = np.sin(a64)
    return out


def _consts(core):
    cs = {}
    cs["ident"] = np.eye(128, dtype=np.float32)
    sel5 = np.zeros((5, 132), np.float32)
    sel5[0, 0:128] = 1.0
    for s in range(4):
        sel5[1 + s, 128 + s] = 1.0
    cs["sel5"] = sel5
    inda = np.zeros((128, 8), np.float32)
    inda[np.arange(128), np.arange(128) // 16] = 1.0
    cs["inda"] = inda
    i = np.arange(128, dtype=np.float64)
    kdsc = np.zeros((128, 4), np.float32)
    for h in range(4):
        kdsc[:, h] = RET_SCALE * np.exp((127.0 - i) * LG[h])
    cs["kdsc"] = kdsc
    oh = np.zeros((128, 8), np.float32)
    oh[:, core] = 1.0
    cs["oh"] = oh
    return cs


def _core_tables(c):
    q = np.arange(128)
    n = np.arange(1024)
    j = np.arange(256)
    k = np.arange(128)
    cmask = np.zeros((NB, 128, 1024), np.float32)
    forceb = np.zeros((NB, 128, 256), np.float32)
    for i in range(NB):
        t = 8 * i + c
        qpos = 128 * t + q
        ok = (16 * n[None, :] + 31 <= qpos[:, None]) & (n[None, :] <= 1022)
        cmask[i] = np.where(ok, 0.0, NEG)
        valid = 64 * j[None, :] <= qpos[:, None]
        qb = qpos[:, None] // 64
        forced = (j[None, :] == 0) | (j[None, :] == qb) | (j[None, :] == qb - 1)
        forceb[i] = np.where(valid, np.where(forced, 1.0e4, 0.0), -1.0e30)
    dmask = np.zeros((128, 8, 128), np.float32)
    for jj in range(8):
        if jj == c:
            dmask[:, jj, :] = np.where(k[None, :] <= q[:, None], 0.0, NEG)
        elif jj > c:
            dmask[:, jj, :] = NEG
    wmask = np.zeros((128, 24, 128), np.float32)
    for v in range(2):
        for m in range(12):
            diff = 128 * (m - 4 - c) + k[None, :] - q[:, None]
            ok = (diff <= 0) & (diff > -512)
            if v == 0 and m < 4:
                ok = np.zeros_like(ok)
            wmask[:, v * 12 + m, :] = np.where(ok, 0.0, NEG)
    return {"cmask": cmask, "forceb": forceb, "dmask": dmask.reshape(128, 1024), "wmask": wmask.reshape(128, 24 * 128)}


def _shared_tables():
    nl = np.arange(128)
    blk = np.arange(256)
    ovp = np.zeros((128, 8, 256), np.float32)
    for jt in range(8):
        n = 128 * jt + nl
        ovp[:, jt, :] = ((4 * blk[None, :] - 1 <= n[:, None]) & (n[:, None] <= 4 * blk[None, :] + 3)).astype(np.float32)
    i = np.arange(128)
    dtab = np.zeros((128, 4, 128), np.float32)
    qdt = np.zeros((128, 4, 128), np.float32)
    for h in range(4):
        diff = i[None, :] - i[:, None]
        dtab[:, h, :] = np.where(diff >= 0, np.exp(np.maximum(diff, 0) * LG[h]), 0.0)
        qdt[:, h, :] = np.exp((i[None, :] + 1.0) * LG[h])
    p = np.arange(128)
    ovs = np.zeros((128, 8, 257), np.float32)
    for s_ in range(8):
        n = 8 * p + s_
        ovs[:, s_, 0:256] = ((4 * blk[None, :] - 1 <= n[:, None]) & (n[:, None] <= 4 * blk[None, :] + 3)).astype(np.float32)
    ovs[:, :, 256] = 1.0
    smallc = np.zeros((1, 1024), np.float32)
    smallc[0, 127] = NEG
    smallc[0, 128] = NEG
    for b in range(4):
        for s_ in range(4):
            smallc[0, 256 + 4 * b + s_] = 0.0 if s_ == b else NEG
    smallc[0, 272:276] = 1.0
    smallc[0, 276 + 0] = 1.0e4
    smallc[0, 276 + 255] = 1.0e4
    ohm = np.zeros((128, 4, 4), np.float32)
    for b in range(4):
        ohm[:, b, b] = 1.0
    return {"ovp": ovp.reshape(128, 2048), "dtab": dtab.reshape(128, 512), "qdt": qdt.reshape(128, 512),
            "ovs": ovs.reshape(128, 8 * 257), "smallc": smallc, "ohm": ohm.reshape(128, 16), "oh4": np.eye(4, dtype=np.float32),
            "rope_smp": _rope_tables(np.full(4, P_PAST))}


def _prep(inputs, stage=99):
    f = lambda a: np.ascontiguousarray(np.asarray(a, dtype=np.float32))
    x_all = f(inputs["x_prompt"]).reshape(T, D)
    w_cmp = f(inputs["w_cmp"])[0]
    b_cmp = f(inputs["b_cmp"])[0]
    wrep = np.zeros((128, 512), np.float32)
    pm = np.arange(128) % 16
    for r in range(2):
        blk = w_cmp[:, :, r * 16 + pm, :]
        wrep[:, r * 256:(r + 1) * 256] = blk.transpose(2, 0, 1, 3).reshape(128, 256)
    bcol = np.ascontiguousarray(b_cmp.reshape(2, 128).T)
    rope_all = _rope_tables(np.arange(T))
    shared = {
        "x_all": x_all, "rope_all": rope_all,
        "w_ada": f(inputs["w_ada"])[0], "b_ada": f(inputs["b_ada"]).reshape(1, 6 * D),
        "w_in": f(inputs["w_in"])[0],
        "norm_mix": f(inputs["norm_mix"]).reshape(1, D), "norm_mlp": f(inputs["norm_mlp"]).reshape(1, D),
        "wrep": wrep, "bcol": bcol,
        "w_bn": f(inputs["w_branch_nsa"])[0], "w_br": f(inputs["w_branch_ret"])[0], "w_out": f(inputs["w_out"])[0],
        "w_up": f(inputs["w_up"])[0], "w_down": f(inputs["w_down"])[0], "norm_final": f(inputs["norm_final"]).reshape(1, D),
        "gnw": f(inputs["ret_gn_w"]).reshape(1, D), "gnb": f(inputs["ret_gn_b"]).reshape(1, D),
    }
    shared.update(_shared_tables())
    shared["wrow"] = np.ascontiguousarray(w_cmp.reshape(2, 2, 2, 16, 64).transpose(2, 3, 0, 1, 4).reshape(1, 2 * 16 * 256))
    shared["brow"] = np.ascontiguousarray(b_cmp.reshape(1, 256))
    shared["cache_cmp"] = f(inputs["cache_cmp_kv"]).reshape(5120 * 8, 4096)
    shared["cache_sel"] = f(inputs["cache_sel_kv"]).reshape(5120 * 8, 4096)
    ptab_all = np.ascontiguousarray(np.asarray(inputs["page_table"], dtype=np.int32))
    cwin = f(inputs["cache_win_kv"])[0].reshape(32, 512, 256)
    sret = f(inputs["state_ret"])[0]
    maps = []
    cp = f(inputs["c_prompt"]).reshape(1, D)
    cs_ = f(inputs["c_sample"])
    for c in range(NCORE):
        m = dict(shared)
        m.update(_consts(c))
        m.update(_core_tables(c))
        tiles = [8 * i + c for i in range(NB)]
        m["x_own"] = np.ascontiguousarray(x_all.reshape(NT, 128, D)[tiles].reshape(NB * 128, D))
        m["rope_own"] = np.ascontiguousarray(rope_all.reshape(NT, 128, 192)[tiles].reshape(NB * 128, 192))
        m["ptab"] = ptab_all[4 * c:4 * c + 4]
        m["cache_win"] = cwin[4 * c:4 * c + 4]
        m["state_s"] = sret[4 * c:4 * c + 4]
        m["x_smp"] = f(inputs["x_sample"]).reshape(32, D)[4 * c:4 * c + 4]
        m["c_all"] = np.ascontiguousarray(np.concatenate([cp, cs_[4 * c:4 * c + 4]], axis=0).reshape(5, 8, 128).transpose(2, 1, 0).reshape(128, 40))
        maps.append(m)
    return maps


_NC_CACHE = {}


def run(inputs, stage=99):
    if stage not in _NC_CACHE:
        _NC_CACHE[stage] = build(stage)
    nc = _NC_CACHE[stage]
    maps = _prep(inputs, stage)
    res = run_bass_kernel_spmd(nc, maps, core_ids=list(range(NCORE)))
    return res.results


def kernel(**inputs):
    r = run(inputs)
    r0 = r[0]
    y = np.zeros((NT, 128, D), np.float32)
    for c in range(NCORE):
        yo = np.asarray(r[c]["y_own"]).reshape(NB, 128, D)
        for i in range(NB):
            y[8 * i + c] = yo[i]
    y_prompt = y.reshape(1, T, D)
    y_sample = np.concatenate([np.asarray(r[c]["y_smp"]) for c in range(NCORE)], axis=0).reshape(32, 1, D)
    new_cmp_p = np.asarray(r0["o_cmp"]).reshape(1, 1, T, 2, 2, 64)
    new_sel_p = np.asarray(r0["o_sel"]).reshape(1, 1, T, 2, 2, 64)
    new_win_p = np.asarray(r0["o_win"]).reshape(1, 1, 512, 2, 2, 64)
    new_state_p = np.asarray(r0["o_state"]).reshape(1, 1, 4, 128, 256)
    cat = lambda k: np.concatenate([np.asarray(r[c][k]) for c in range(NCORE)], axis=0)
    new_cmp_s = cat("o_cmp_s").reshape(1, 32, 1, 2, 2, 64)
    new_sel_s = cat("o_sel_s").reshape(1, 32, 1, 2, 2, 64)
    new_win_s = cat("o_win_s").reshape(1, 32, 512, 2, 2, 64)
    new_state_s = cat("o_state_s").reshape(1, 32, 4, 128, 256)
    return (y_prompt, y_sample, new_cmp_p, new_sel_p, new_win_p, new_state_p, new_cmp_s, new_sel_s, new_win_s, new_state_s)
```

```python
import math
import os
from contextlib import ExitStack

import numpy as np
import ml_dtypes

import concourse.bass as bass
import concourse.mybir as mybir
from concourse.bass_utils import run_bass_kernel_spmd

F32 = mybir.dt.float32
BF16 = mybir.dt.bfloat16
I32 = mybir.dt.int32
ALU = mybir.AluOpType
ACTF = mybir.ActivationFunctionType
AX = mybir.AxisListType

ENGS = ("pe", "act", "dve", "pool", "sp")
NSLOT = 12

D = 1024
T = 16384
NT = T // 128
NB = 16
NCORE = 8
SB = 4
P_PAST = 16384
W_IN = 6424
C_Q, C_KVC, C_KVS, C_KVW, C_G, C_RQ, C_RK, C_RV, C_RG, C_GA, C_GB = (
    0, 512, 768, 1024, 1280, 1304, 1816, 2328, 3352, 4376, 5400)
NEG = -30000.0
RMS_EPS = 1e-6
GN_EPS = 1e-5
LG = [math.log1p(-2.0 ** (-5.0 - h)) for h in range(4)]
RET_SCALE = 128 ** -0.5


class Dep:
    __slots__ = ("w", "r", "excl")

    def __init__(self, excl=False):
        self.w = None
        self.r = []
        self.excl = excl


class Prog:
    def __init__(self, nc, stack):
        self.nc = nc
        self.stack = stack
        self.ops = {e: [] for e in ENGS}
        self.cnt = {e: 0 for e in ENGS}
        self.esem = {e: stack.enter_context(nc.semaphore("es_" + e)) for e in ENGS}
        self.dq = {}
        for q in ("sp", "act", "pool"):
            sl = [stack.enter_context(nc.semaphore("ds_%s_%d" % (q, i))) for i in range(NSLOT)]
            self.dq[q] = {"sems": sl, "cnt": [0] * NSLOT, "next": 0}
        self.waited = {e: {} for e in ENGS}
        self.nuid = 0

    def sb(self, shape, dt=F32, stack=None):
        self.nuid += 1
        return (stack or self.stack).enter_context(self.nc.sbuf_tensor("sb%d" % self.nuid, list(shape), dt))

    def ps(self, shape, dt=F32, stack=None):
        self.nuid += 1
        return (stack or self.stack).enter_context(self.nc.psum_tensor("ps%d" % self.nuid, list(shape), dt))

    def _need(self, eng, tok, waits):
        if tok is None:
            return
        kind, key, val = tok
        if kind == "e" and key == eng and eng == "pe":
            return
        k = (kind, key)
        if self.waited[eng].get(k, 0) >= val:
            return
        self.waited[eng][k] = val
        waits[k] = max(waits.get(k, 0), val)

    def _gather(self, eng, reads, writes):
        waits = {}
        for d in reads:
            self._need(eng, d.w, waits)
            if d.excl:
                for t in d.r:
                    if not (t[0] == "e" and t[1] == eng):
                        self._need(eng, t, waits)
        for d in writes:
            self._need(eng, d.w, waits)
            for t in d.r:
                self._need(eng, t, waits)
        return waits

    def _commit(self, tok, reads, writes):
        for d in reads:
            if tok[0] == "e":
                d.r = [t for t in d.r if not (t[0] == "e" and t[1] == tok[1])]
            d.r.append(tok)
        for d in writes:
            d.w = tok
            d.r = []

    def _semof(self, k):
        kind, key = k
        if kind == "e":
            return self.esem[key]
        q, i = key
        return self.dq[q]["sems"][i]

    def op(self, eng, fn, reads=(), writes=()):
        waits = self._gather(eng, reads, writes)
        self.cnt[eng] += 1
        tok = ("e", eng, self.cnt[eng])
        self.ops[eng].append(([(self._semof(k), v) for k, v in waits.items()], fn, (self.esem[eng], 1)))
        self._commit(tok, reads, writes)
        return tok

    def dma(self, q, fn, reads=(), writes=()):
        st = self.dq[q]
        i = st["next"]
        st["next"] = (i + 1) % NSLOT
        waits = self._gather(q, reads, writes)
        if st["cnt"][i] > 0:
            self._need(q, ("d", (q, i), st["cnt"][i]), waits)
        st["cnt"][i] += 16
        tok = ("d", (q, i), st["cnt"][i])
        self.ops[q].append(([(self._semof(k), v) for k, v in waits.items()], fn, (st["sems"][i], 16)))
        self._commit(tok, reads, writes)
        return tok

    def barrier(self):
        toks = [("e", e, self.cnt[e]) for e in ENGS if self.cnt[e] > 0]
        for q, st in self.dq.items():
            for i in range(NSLOT):
                if st["cnt"][i] > 0:
                    toks.append(("d", (q, i), st["cnt"][i]))
        for eng in ENGS:
            waits = {}
            for t in toks:
                if t[0] == "e" and t[1] == eng:
                    continue
                self._need(eng, t, waits)
            if waits:
                self.ops[eng].append(([(self._semof(k), v) for k, v in waits.items()], None, None))

    def emit(self):
        nc = self.nc
        ops = self.ops
        with nc.Block() as block:
            def run(handle, lst):
                for waits, fn, inc in lst:
                    for sem, v in waits:
                        handle.wait_ge(sem, v)
                    if fn is not None:
                        fn(handle).then_inc(inc[0], inc[1])

            @block.tensor
            def _(e):
                run(e, ops["pe"])

            @block.scalar
            def _(e):
                run(e, ops["act"])

            @block.vector
            def _(e):
                run(e, ops["dve"])

            @block.gpsimd
            def _(e):
                run(e, ops["pool"])

            @block.sync
            def _(e):
                run(e, ops["sp"])


def bc(ap, shape):
    return ap.to_broadcast(list(shape))


def build(stage=99):
    nc = bass.Bass("TRN2", target_bir_lowering=False)

    def din(name, shape, dt=F32):
        return nc.dram_tensor(name, list(shape), dt, kind="ExternalInput").ap()

    def dout(name, shape, dt=F32):
        return nc.dram_tensor(name, list(shape), dt, kind="ExternalOutput").ap()

    def dscr(name, shape, dt=F32):
        return nc.dram_tensor(name, list(shape), dt, kind="Internal").ap()

    x_all = din("x_all", [T, D])
    c_all = din("c_all", [128, 40])
    rope_all = din("rope_all", [T, 192])
    w_ada = din("w_ada", [D, 6 * D])
    b_ada = din("b_ada", [1, 6 * D])
    w_in = din("w_in", [D, W_IN])
    norm_mix = din("norm_mix", [1, D])
    norm_mlp = din("norm_mlp", [1, D])
    ident_d = din("ident", [128, 128])
    sel5_d = din("sel5", [5, 132])
    inda_d = din("inda", [128, 8])
    wrep_d = din("wrep", [128, 512])
    bcol_d = din("bcol", [128, 2])
    kdsc_d = din("kdsc", [128, 4])
    oh_d = din("oh", [128, 8])

    x_own = din("x_own", [NB * 128, D])
    rope_own = din("rope_own", [NB * 128, 192])
    cmask_d = din("cmask", [NB, 128, 1024])
    forceb_d = din("forceb", [NB, 128, 256])
    dmask_d = din("dmask", [128, 8 * 128])
    wmask_d = din("wmask", [128, 24 * 128])
    ovp_d = din("ovp", [128, 8 * 256])
    dtab_d = din("dtab", [128, 512])
    qdt_d = din("qdt", [128, 512])
    gnw_d = din("gnw", [1, D])
    gnb_d = din("gnb", [1, D])
    DBG = (stage == 2)
    DBG_NBLK = int(os.environ.get('K_NBLK', '3'))
    PART = int(os.environ.get('K_PART', '9'))
    SUB = int(os.environ.get('K_SUB', '9'))
    SUB2 = int(os.environ.get('K_SUB2', '9'))
    hT_d = dscr("hT_d", [NB, 128, 8, 128], BF16)
    yrT_d = dscr("yrT_d", [NB, 128, 8, 128], BF16)
    onT_d = dscr("onT_d", [NB, 128, 4, 128], BF16)
    if DBG:
        dbg_selb = dout("dbg_selb", [NB, 2, 128, 256])
        dbg_on = dout("dbg_on", [NB, 128, 512])
        dbg_yr = dout("dbg_yr", [NB, 128, 1024])

    w_bn = din("w_bn", [512, D])
    w_br = din("w_br", [D, D])
    w_out = din("w_out", [D, D])
    w_up = din("w_up", [D, 4 * D])
    w_down = din("w_down", [4 * D, D])
    norm_final_d = din("norm_final", [1, D])
    x_smp = din("x_smp", [SB, D])
    x1_d = dscr("x1_d", [NB * 128, D])
    x1s_d = dscr("x1s_d", [SB, D])
    hTs_d = dscr("hTs_d", [128, 8, SB], BF16)
    onTs_d = dscr("onTs_d", [128, 4, SB], BF16)
    yrTs_d = dscr("yrTs_d", [128, 8, SB], BF16)
    y_own = dout("y_own", [NB * 128, D])
    y_smp = dout("y_smp", [SB, D])

    smallc_d = din("smallc", [1, 1024])
    oh4_d = din("oh4", [SB, 4])
    ohm_d = din("ohm", [128, 16])
    ovs_d = din("ovs", [128, 8 * 257])
    wrow_d = din("wrow", [1, 2 * 16 * 256])
    brow_d = din("brow", [1, 256])
    rope_smp = din("rope_smp", [SB, 192])
    ptab = din("ptab", [SB, 128], I32)
    cache_cmp = din("cache_cmp", [5120 * 8, 4096])
    cache_sel = din("cache_sel", [5120 * 8, 4096])
    cache_win = din("cache_win", [SB, 512, 256])
    state_s = din("state_s", [SB, 4, 128, 256])
    os_d = dscr("os_d", [SB, 3, 8, 64])
    dbg_ons = dout("dbg_ons", [SB, 512])
    o_cmp_s = dout("o_cmp_s", [SB, 256])
    o_sel_s = dout("o_sel_s", [SB, 256])
    o_win_s = dout("o_win_s", [SB, 512, 256])
    o_state_s = dout("o_state_s", [SB, 4, 128, 256])

    o_cmp = dout("o_cmp", [T, 256])
    o_sel = dout("o_sel", [T, 256])
    o_win = dout("o_win", [512, 256])
    o_state = dout("o_state", [4, 128, 256])

    kwT_d = dscr("kwT_d", [128, T], BF16)
    vw_d = dscr("vw_d", [T, 130], BF16)
    ksT_d = dscr("ksT_d", [128, T], BF16)
    vs_d = dscr("vs_d", [T, 130], BF16)
    sown_d = dscr("sown_d", [NB, 128, 1024])
    mods_d = dscr("mods_d", [SB, 6 * D])

    with ExitStack() as st:
        P = Prog(nc, st)
        ident = P.sb([128, 128]); d_ident = Dep()
        identb = P.sb([128, 128], BF16); d_identb = Dep()
        modP4 = P.sb([128, 4, D]); d_modP = Dep()
        G1P = P.sb([128, D]); d_G1P = Dep()
        G2P = P.sb([128, D]); d_G2P = Dep()
        cmpKT = P.sb([128, 1024], BF16); d_cmpKT = Dep()
        cmpVA = P.sb([128, 8, 2, 65], BF16); d_cmpVA = Dep()
        epsT = P.sb([128, 1]); d_eps = Dep()

        banks = [P.ps([128, 512]) for _ in range(6)]
        pT = P.ps([128, 1024]); d_pTs = [Dep(True), Dep(True)]
        d_bank = [Dep(True) for _ in range(6)]

        P.dma("sp", lambda e: e.dma_start(out=ident[:], in_=ident_d), writes=[d_ident])
        P.op("dve", lambda e: e.tensor_copy(out=identb[:], in_=ident[:]), reads=[d_ident], writes=[d_identb])
        P.op("dve", lambda e: e.memset(epsT[:], RMS_EPS), writes=[d_eps])
        P.op("pool", lambda e: e.memset(cmpVA[:].rearrange("p a b c -> p (a b c)"), 1.0), writes=[d_cmpVA])

        with ExitStack() as s0:
            cT = P.sb([128, 8, 5], F32, s0); d_cT = Dep()
            scT = P.sb([128, 8, 5], BF16, s0); d_scT = Dep()
            sel5 = P.sb([5, 132], F32, s0); d_sel5 = Dep()
            mod5 = P.sb([5, 6 * D], F32, s0); d_mod5 = Dep()
            bada = P.sb([5, 6 * D], F32, s0); d_bada = Dep()
            gmix = P.sb([128, D], F32, s0); d_gmix = Dep()
            gmlp = P.sb([128, D], F32, s0); d_gmlp = Dep()
            wa = [P.sb([128, 8, 512], BF16, s0) for _ in range(2)]; d_wa = [Dep(), Dep()]

            P.dma("sp", lambda e: e.dma_start(out=cT[:].rearrange("p c b -> p (c b)"), in_=c_all), writes=[d_cT])
            P.dma("sp", lambda e: e.dma_start(out=sel5[:], in_=sel5_d), writes=[d_sel5])
            P.dma("sp", lambda e: e.dma_start(out=bada[:], in_=b_ada.partition_broadcast(5)[:, 0, :]), writes=[d_bada])
            P.dma("sp", lambda e: e.dma_start(out=gmix[:], in_=norm_mix.partition_broadcast(128)[:, 0, :]), writes=[d_gmix])
            P.dma("sp", lambda e: e.dma_start(out=gmlp[:], in_=norm_mlp.partition_broadcast(128)[:, 0, :]), writes=[d_gmlp])
            P.op("act", lambda e: e.activation(out=scT[:], in_=cT[:], func=ACTF.Silu), reads=[d_cT], writes=[d_scT])
            for g in range(12):
                b = g % 2
                P.dma("pool", lambda e, g=g, b=b: e.dma_start(
                    out=wa[b][:], in_=w_ada[:, g * 512:(g + 1) * 512].rearrange("(c p) n -> p c n", p=128)),
                    writes=[d_wa[b]])
                for k in range(8):
                    P.op("pe", lambda e, k=k, b=b: e.matmul(out=banks[0][0:5, :], lhsT=scT[:, k, :], rhs=wa[b][:, k, :],
                                                           start=(k == 0), stop=(k == 7)),
                         reads=[d_scT, d_wa[b]], writes=[d_bank[0]])
                P.op("dve", lambda e, g=g: e.tensor_tensor(out=mod5[:, g * 512:(g + 1) * 512], in0=banks[0][0:5, :],
                                                          in1=bada[:, g * 512:(g + 1) * 512], op=ALU.add),
                     reads=[d_bank[0], d_bada], writes=[d_mod5])
            P.dma("sp", lambda e: e.dma_start(out=mods_d, in_=mod5[1:5, :]), reads=[d_mod5])
            slot = {0: 0, 2: 1, 3: 2, 5: 3}
            for g in range(12):
                ch, hf_ = g // 2, (g % 2) * 512
                P.op("pe", lambda e, g=g: e.matmul(out=banks[1][:, :], lhsT=sel5[:, 0:128], rhs=mod5[:, g * 512:(g + 1) * 512],
                                                   start=True, stop=True), reads=[d_sel5, d_mod5], writes=[d_bank[1]])
                if ch in slot:
                    P.op("act", lambda e, ch=ch, hf_=hf_: e.copy(out=modP4[:, slot[ch], hf_:hf_ + 512], in_=banks[1][:, :]),
                         reads=[d_bank[1]], writes=[d_modP])
                else:
                    Gt, dG, gn, dgn = (G1P, d_G1P, gmix, d_gmix) if ch == 1 else (G2P, d_G2P, gmlp, d_gmlp)
                    P.op("dve", lambda e, Gt=Gt, gn=gn, hf_=hf_: e.scalar_tensor_tensor(out=Gt[:, hf_:hf_ + 512], in0=banks[1][:, :], scalar=1.0,
                                                                                         in1=gn[:, hf_:hf_ + 512], op0=ALU.add, op1=ALU.mult),
                         reads=[d_bank[1], dgn], writes=[dG])
            P.barrier()

        shcol = P.sb([128, 2, 8], F32); d_shcol = Dep()
        for si, mi in enumerate((0, 2)):
            for c in range(8):
                P.op("pe", lambda e, c=c, mi=mi: e.transpose(out=pT[:, c * 128:(c + 1) * 128], in_=modP4[:, mi, c * 128:(c + 1) * 128], identity=ident[:]),
                     reads=[d_modP, d_ident], writes=[d_pTs[c // 4]])
            P.op("act", lambda e, si=si: e.copy(out=shcol[:, si, :], in_=pT[:].rearrange("p (c t) -> p c t", t=128)[:, :, 0]), reads=d_pTs, writes=[d_shcol])

        def norm_mod_T(xt, d_xt, ntok, G, d_G, shift, d_shift, wk, hT, d_hT, shift_col=None, d_shift_col=None):
            junk, d_junk, ss, d_ss, rstd, d_rstd, hm, d_hm, hf, d_hf = wk
            P.op("act", lambda e: e.activation(out=junk[0:ntok, :], in_=xt[0:ntok, :], func=ACTF.Square, accum_out=ss[0:ntok, :]),
                 reads=[d_xt], writes=[d_junk, d_ss])
            P.op("act", lambda e: e.activation(out=rstd[0:ntok, :], in_=ss[0:ntok, :], func=ACTF.Sqrt, bias=epsT[0:ntok, :], scale=1.0 / D),
                 reads=[d_ss, d_eps], writes=[d_rstd])
            P.op("dve", lambda e: e.reciprocal(out=rstd[0:ntok, :], in_=rstd[0:ntok, :]), reads=[d_rstd], writes=[d_rstd])
            P.op("dve", lambda e: e.scalar_tensor_tensor(out=hm[0:ntok, :], in0=xt[0:ntok, :], scalar=rstd[0:ntok, 0:1], in1=G[0:ntok, :],
                                                        op0=ALU.mult, op1=ALU.mult),
                 reads=[d_xt, d_rstd, d_G], writes=[d_hm])
            if shift_col is None:
                P.op("pool", lambda e: e.tensor_tensor(out=hf[0:ntok, :], in0=hm[0:ntok, :], in1=shift[0:ntok, :], op=ALU.add),
                     reads=[d_hm, d_shift], writes=[d_hf])
            for c in range(8):
                P.op("pe", lambda e, c=c: e.transpose(out=pT[:, c * 128:c * 128 + ntok], in_=hf[0:ntok, c * 128:(c + 1) * 128],
                                                      identity=ident[0:ntok, 0:ntok]),
                     reads=[d_hf, d_ident], writes=[d_pTs[c // 4]])
            pv = pT[:].rearrange("p (c t) -> p c t", t=128)
            if shift_col is None:
                P.op("act", lambda e: e.copy(out=hT[:, 0:4, 0:ntok], in_=pv[:, 0:4, 0:ntok]), reads=[d_pTs[0]], writes=[d_hT])
                P.op("dve", lambda e: e.tensor_copy(out=hT[:, 4:8, 0:ntok], in_=pv[:, 4:8, 0:ntok]), reads=[d_pTs[1]], writes=[d_hT])
            else:
                for c in range(8):
                    P.op("act", lambda e, c=c: e.activation(out=hT[:, c, 0:ntok], in_=pv[:, c, 0:ntok], func=ACTF.Identity, bias=shift_col[:, c:c + 1]),
                         reads=[d_pTs[c // 4], d_shift_col], writes=[d_hT])

        junk_sh = P.sb([128, D], BF16); d_junk_sh = Dep()

        def mk_wk(stack, n=128):
            hm_ = P.sb([n, D], F32, stack); d_hm_ = Dep()
            return (junk_sh, d_junk_sh, P.sb([n, 1], F32, stack), Dep(), P.sb([n, 1], F32, stack), Dep(),
                    hm_, d_hm_, hm_, d_hm_)

        def rope(src4, dst4, cos, sin, shp, tmps, d_src, d_dst, d_rp, d_tmps):
            ta, tb, tc, td = tmps
            cb, sbb = cos, sin
            P.op("dve", lambda e: e.tensor_tensor(out=ta, in0=src4(0), in1=cb, op=ALU.mult), reads=[d_src, d_rp], writes=[d_tmps[0]])
            P.op("dve", lambda e: e.tensor_tensor(out=tb, in0=src4(1), in1=sbb, op=ALU.mult), reads=[d_src, d_rp], writes=[d_tmps[1]])
            P.op("dve", lambda e: e.tensor_tensor(out=tc, in0=src4(0), in1=sbb, op=ALU.mult), reads=[d_src, d_rp], writes=[d_tmps[2]])
            P.op("dve", lambda e: e.tensor_tensor(out=td, in0=src4(1), in1=cb, op=ALU.mult), reads=[d_src, d_rp], writes=[d_tmps[3]])
            P.op("pool", lambda e: e.tensor_tensor(out=dst4(0), in0=ta, in1=tb, op=ALU.subtract), reads=[d_tmps[0], d_tmps[1]], writes=[d_dst])
            P.op("pool", lambda e: e.tensor_tensor(out=dst4(1), in0=tc, in1=td, op=ALU.add), reads=[d_tmps[2], d_tmps[3]], writes=[d_dst])

        def make_attn(pts, d_pts):
            scb = [banks[0], banks[1], banks[4], banks[5]]
            d_scb = [d_bank[0], d_bank[1], d_bank[4], d_bank[5]]
            slot = [0]
            pend = []
            cfg = {"nslots": 2, "depth": 1}

            def flush(keep=0):
                while len(pend) > keep:
                    pend.pop(0)()

            def set_mode(nslots, depth):
                flush()
                cfg["nslots"] = min(nslots, len(pts))
                cfg["depth"] = depth
                slot[0] = 0

            def attn(nq, ng, QTh, d_Q, tiles, o4v, d_o4, pslc=None, first=True, last=True, pvT=False):
                ncols = nq * ng
                nt_ = len(tiles)
                for ti, tl in enumerate(tiles):
                    s_ = slot[0]; slot[0] = (slot[0] + 1) % cfg["nslots"]
                    nk = tl["nk"]
                    sc, d_sc, pt, d_pt = scb[s_], d_scb[s_], pts[s_], d_pts[s_]
                    nb = len(tl["biases"])
                    P.op("pe", lambda e, sc=sc, tl=tl, nk=nk, nb=nb: e.matmul(out=sc[0:nk, 0:ncols], lhsT=tl["KT"], rhs=QTh,
                                                                           start=True, stop=(nb == 0)),
                         reads=[d_Q] + tl["deps"], writes=[d_sc])
                    for bi, (bl, br_) in enumerate(tl["biases"]):
                        P.op("pe", lambda e, sc=sc, bl=bl, br_=br_, nk=nk, bi=bi, nb=nb: e.matmul(
                            out=sc[0:nk, 0:ncols], lhsT=bl, rhs=br_, start=False, stop=(bi == nb - 1)),
                            reads=tl["deps"], writes=[d_sc])
                    P.op("act", lambda e, sc=sc, pt=pt, nk=nk: e.activation(out=pt[0:nk, 0:ncols], in_=sc[0:nk, 0:ncols], func=ACTF.Exp),
                         reads=[d_sc], writes=[d_pt])

                    def pv(pt=pt, d_pt=d_pt, tl=tl, nk=nk, ti=ti):
                        if pvT:
                            P.op("pe", lambda e: e.matmul(out=o4v, lhsT=tl["V"], rhs=pt[0:nk, 0:ncols], start=(first and ti == 0), stop=(last and ti == nt_ - 1)),
                                 reads=[d_pt] + tl["deps"], writes=[d_o4])
                        else:
                            for g in range(ng):
                                P.op("pe", lambda e, g=g: e.matmul(
                                    out=o4v[:, g, :], lhsT=pt[0:nk, g * nq:(g + 1) * nq], rhs=tl["V"],
                                    start=(first and ti == 0 and g == 0), stop=(last and ti == nt_ - 1), skip_group_check=True),
                                    reads=[d_pt] + tl["deps"], writes=[d_o4])
                        if pslc is not None:
                            psv, d_ps, gpb = pslc
                            for g in range(ng):
                                P.op("pe", lambda e, g=g: e.matmul(
                                    out=psv(g), lhsT=pt[0:nk, g * nq:(g + 1) * nq], rhs=tl["Ov"],
                                    start=(ti == 0 and g % gpb == 0), stop=(ti == nt_ - 1), skip_group_check=True),
                                    reads=[d_pt] + tl["deps"], writes=[d_ps[g // gpb]])
                    flush(cfg["depth"] - 1)
                    pend.append(pv)

            attn.flush = flush
            attn.set_mode = set_mode
            return attn

        def phase_b1():
            with ExitStack() as sB:
                WB = P.sb([128, 8, 4376], BF16, sB); d_WB = Dep()
                wsegs = []
                for g in range(4):
                    wsegs += [(C_Q + g * 64, 64, g * 128), (C_Q + (4 + g) * 64, 64, g * 128 + 64)]
                wsegs += [(C_RQ, 512, 512), (C_RK, 512, 1024), (C_RV, 1024, 1536), (C_RG, 1024, 2560), (C_G, 24, 3584),
                          (C_KVC, 128, 3608), (C_KVS, 128, 3736), (C_KVW, 128, 3864),
                          (C_KVC + 128, 128, 3992), (C_KVS + 128, 128, 4120), (C_KVW + 128, 128, 4248)]
                for (src, n, dst) in wsegs:
                    P.dma("pool", lambda e, src=src, n=n, dst=dst: e.dma_start(
                        out=WB[:, :, dst:dst + n], in_=w_in[:, src:src + n].rearrange("(c p) n -> p c n", p=128)), writes=[d_WB])
                dmaskB = P.sb([128, 8, 128], BF16, sB); d_dmask = Dep()
                wmaskB = P.sb([128, 24, 128], BF16, sB); d_wmask = Dep()
                OvP = P.sb([128, 8, 256], BF16, sB); d_OvP = Dep()
                dtab = P.sb([128, 4, 128], F32, sB); d_dtab = Dep()
                qdt = P.sb([128, 4, 128], F32, sB); d_qdt = Dep()
                gnw = P.sb([128, D], F32, sB); d_gnw = Dep()
                gnb = P.sb([128, D], F32, sB); d_gnb = Dep()
                Irep = P.sb([128, 4, 128], BF16, sB); d_Irep = Dep()
                epsG = P.sb([128, 1], F32, sB); d_epsG = Dep()
                P.dma("pool", lambda e: e.dma_start(out=dmaskB[:].rearrange("p a b -> p (a b)"), in_=dmask_d), writes=[d_dmask])
                P.dma("pool", lambda e: e.dma_start(out=wmaskB[:].rearrange("p a b -> p (a b)"), in_=wmask_d), writes=[d_wmask])
                P.dma("pool", lambda e: e.dma_start(out=OvP[:].rearrange("p a b -> p (a b)"), in_=ovp_d), writes=[d_OvP])
                P.dma("sp", lambda e: e.dma_start(out=dtab[:].rearrange("p a b -> p (a b)"), in_=dtab_d), writes=[d_dtab])
                P.dma("sp", lambda e: e.dma_start(out=qdt[:].rearrange("p a b -> p (a b)"), in_=qdt_d), writes=[d_qdt])
                P.dma("sp", lambda e: e.dma_start(out=gnw[:], in_=gnw_d.partition_broadcast(128)[:, 0, :]), writes=[d_gnw])
                P.dma("sp", lambda e: e.dma_start(out=gnb[:], in_=gnb_d.partition_broadcast(128)[:, 0, :]), writes=[d_gnb])
                for g in range(4):
                    P.op("dve", lambda e, g=g: e.tensor_copy(out=Irep[:, g, :], in_=ident[:]), reads=[d_ident], writes=[d_Irep])
                P.op("dve", lambda e: e.memset(epsG[:], GN_EPS), writes=[d_epsG])
                IrepF = Irep[:].rearrange("p a b -> p (a b)")

                xt = P.sb([128, D], F32, sB); d_xt = Dep()
                rp = P.sb([128, 192], F32, sB); d_rp = Dep()
                wk = mk_wk(sB)
                hT = P.sb([128, 8, 128], BF16, sB); d_hT = Dep()
                tmq = [P.sb([128, 256], F32, sB) for _ in range(4)]; d_tmq = [Dep() for _ in range(4)]
                qr = P.sb([128, 512], F32, sB); d_qr = Dep()
                rqr = P.sb([128, 4, 128], F32, sB); d_rqr = Dep()
                rkr = P.sb([128, 4, 128], F32, sB); d_rkr = Dep()
                sg = P.sb([128, 24], F32, sB); d_sg = Dep()
                QT = P.sb([128, 4, 128], BF16, sB); d_QT = Dep()
                rqT = P.sb([128, 4, 128], BF16, sB); d_rqT = Dep()
                rqdT = P.sb([128, 4, 128], BF16, sB); d_rqdT = Dep()
                rkT = P.sb([128, 4, 128], BF16, sB); d_rkT = Dep()
                vb = P.sb([128, 1024], BF16, sB); d_vb = Dep()
                sgate = P.sb([128, 1024], F32, sB); d_sgate = Dep()
                attD = P.sb([128, 4, 128], BF16, sB); d_attD = Dep()
                SownB = P.sb([128, 1024], BF16, sB); d_SownB = Dep()
                osb = P.sb([128, 4, 256], F32, sB); d_osb = Dep()
                st4 = P.sb([128, 16], F32, sB); d_st4 = Dep()
                yret = osb[:].rearrange("p a b -> p (a b)"); d_yret = d_osb
                yrT = P.sb([128, 8, 128], BF16, sB); d_yrT = Dep()
                onsa = P.sb([128, 8, 64], F32, sB); d_onsa = Dep()
                tmpo = P.sb([128, 4, 64], F32, sB); d_tmpo = Dep()
                onT = P.sb([128, 4, 128], BF16, sB); d_onT = Dep()
                rden = P.sb([128, 8], F32, sB); d_rden = Dep()
                wgt = P.sb([128, 4], F32, sB); d_wgt = Dep()
                cmaskB = P.sb([128, 1024], BF16, sB); d_cmask = Dep()
                forceb = P.sb([128, 256], F32, sB); d_forceb = Dep()
                pacc = P.sb([128, 256], F32, sB); d_pacc = Dep()
                wk2 = P.sb([128, 256], F32, sB); d_wk2 = Dep()
                m16 = P.sb([128, 16], F32, sB); d_m16 = Dep()
                selb = [P.sb([128, 256], F32, sB) for _ in range(2)]; d_selb = [Dep(), Dep()]
                selbX = [[P.sb([128, 1024], BF16, sB) for _ in range(2)] for _ in range(2)]
                d_selbX = [[Dep(), Dep()], [Dep(), Dep()]]
                KsG = [P.sb([128, 1024], BF16, sB) for _ in range(2)]; d_KsG = [Dep(), Dep()]
                VsG = [P.sb([128, 8, 130], BF16, sB) for _ in range(2)]; d_VsG = [Dep(), Dep()]
                KwWin = P.sb([128, 1536], BF16, sB); d_KwWin = Dep()
                VwWin = P.sb([128, 12, 130], BF16, sB); d_VwWin = Dep()
                pts = [P.sb([128, 512], BF16, sB) for _ in range(4)]; d_pts = [Dep() for _ in range(4)]
                P.op("pool", lambda e: e.memset(KwWin[:], 0.0), writes=[d_KwWin])
                P.op("pool", lambda e: e.memset(VwWin[:].rearrange("p a b -> p (a b)"), 0.0), writes=[d_VwWin])
                attn = make_attn(pts, d_pts)

                o4 = [banks[2][0:65, :], banks[3][0:65, :]]
                d_o4 = [d_bank[2], d_bank[3]]
                oTs = [P.sb([65, 512], F32, sB) for _ in range(2)]; d_oTs = [Dep(), Dep()]
                sgv = sg[:].rearrange("p (h b) -> p h b", b=3)

                def evac(kvh, br_i, first):
                    attn.flush()
                    P.op("act", lambda e: e.copy(out=oTs[kvh][:], in_=o4[kvh]), reads=[d_o4[kvh]], writes=[d_oTs[kvh]])
                    for g in range(4):
                        P.op("pe", lambda e, g=g: e.transpose(out=pT[:, kvh * 512 + g * 65:kvh * 512 + (g + 1) * 65], in_=oTs[kvh][0:65, g * 128:(g + 1) * 128],
                                                              identity=ident[0:65, 0:65]), reads=[d_oTs[kvh], d_ident], writes=[d_pTs[kvh]])
                    o4v = pT[:, kvh * 512:kvh * 512 + 260].rearrange("p (g c) -> p g c", g=4)
                    d_o = d_pTs[kvh]
                    P.op("dve", lambda e: e.tensor_scalar(out=rden[:, kvh * 4:(kvh + 1) * 4], in0=o4v[:, :, 64], scalar1=1e-30, scalar2=None, op0=ALU.max),
                         reads=[d_o], writes=[d_rden])
                    P.op("dve", lambda e: e.reciprocal(out=rden[:, kvh * 4:(kvh + 1) * 4], in_=rden[:, kvh * 4:(kvh + 1) * 4]), reads=[d_rden], writes=[d_rden])
                    P.op("dve", lambda e: e.tensor_tensor(out=wgt[:], in0=rden[:, kvh * 4:(kvh + 1) * 4], in1=sgv[:, kvh * 4:(kvh + 1) * 4, br_i], op=ALU.mult),
                         reads=[d_rden, d_sg], writes=[d_wgt])
                    wb_ = wgt[:].unsqueeze(2).to_broadcast([128, 4, 64])
                    if first:
                        P.op("dve", lambda e: e.tensor_tensor(out=onsa[:, kvh * 4:(kvh + 1) * 4, :], in0=o4v[:, :, 0:64], in1=wb_, op=ALU.mult),
                             reads=[d_o, d_wgt], writes=[d_onsa])
                    else:
                        P.op("dve", lambda e: e.tensor_tensor(out=tmpo[:], in0=o4v[:, :, 0:64], in1=wb_, op=ALU.mult),
                             reads=[d_o, d_wgt], writes=[d_tmpo])
                        P.op("pool", lambda e: e.tensor_tensor(out=onsa[:, kvh * 4:(kvh + 1) * 4, :], in0=onsa[:, kvh * 4:(kvh + 1) * 4, :], in1=tmpo[:], op=ALU.add),
                             reads=[d_tmpo, d_onsa], writes=[d_onsa])

                nblk = NB if stage >= 2 else 0
                if stage == 2:
                    nblk = DBG_NBLK
                for i in range(nblk):
                    P.dma("sp", lambda e, i=i: e.dma_start(out=xt[:], in_=x_own[i * 128:(i + 1) * 128, :]), writes=[d_xt])
                    P.dma("sp", lambda e, i=i: e.dma_start(out=rp[:], in_=rope_own[i * 128:(i + 1) * 128, :]), writes=[d_rp])
                    P.dma("pool", lambda e, i=i: e.dma_start(out=cmaskB[:], in_=cmask_d[i]), writes=[d_cmask])
                    P.dma("sp", lambda e, i=i: e.dma_start(out=forceb[:], in_=forceb_d[i]), writes=[d_forceb])
                    P.dma("pool", lambda e, i=i: e.dma_start(out=SownB[:], in_=sown_d[i]), writes=[d_SownB])
                    if i == 0:
                        P.dma("sp", lambda e: e.dma_start(out=KwWin[:, 512:1536], in_=kwT_d[:, 0:1024]), writes=[d_KwWin])
                        P.dma("sp", lambda e: e.dma_start(out=VwWin[:, 4:12, :], in_=vw_d[0:1024, :].rearrange("(m p) f -> p m f", p=128)), writes=[d_VwWin])
                    else:
                        P.dma("sp", lambda e, i=i: e.dma_start(out=KwWin[:], in_=kwT_d[:, (8 * i - 4) * 128:(8 * i + 8) * 128]), writes=[d_KwWin])
                        P.dma("sp", lambda e, i=i: e.dma_start(out=VwWin[:], in_=vw_d[(8 * i - 4) * 128:(8 * i + 8) * 128, :].rearrange("(m p) f -> p m f", p=128)),
                              writes=[d_VwWin])
                    norm_mod_T(xt, d_xt, 128, G1P, d_G1P, modP4[:, 0, :], d_modP, wk, hT, d_hT, shcol[:, 0, :], d_shcol)
                    P.dma("sp", lambda e, i=i: e.dma_start(out=hT_d[i], in_=hT[:]), reads=[d_hT])
                    if SUB < 1:
                        continue
                    for (bk, c0, n) in ((0, 0, 512), (1, 512, 512), (4, 1024, 512), (5, 3584, 24)):
                        for k in range(8):
                            P.op("pe", lambda e, bk=bk, k=k, c0=c0, n=n: e.matmul(out=banks[bk][:, 0:n], lhsT=hT[:, k, :], rhs=WB[:, k, c0:c0 + n],
                                                                                start=(k == 0), stop=(k == 7)),
                                 reads=[d_hT, d_WB], writes=[d_bank[bk]])
                    P.op("act", lambda e: e.activation(out=sg[:], in_=banks[5][:, 0:24], func=ACTF.Sigmoid), reads=[d_bank[5]], writes=[d_sg])
                    if SUB < 2:
                        continue
                    srcq = banks[0][:, 0:512].rearrange("p (h two e) -> p h two e", h=8, two=2)
                    dstq = qr[:].rearrange("p (h two e) -> p h two e", h=8, two=2)
                    shq = [128, 8, 32]
                    tvq = [t[:, 0:256].rearrange("p (h e) -> p h e", h=8) for t in tmq]
                    rope(lambda hf: srcq[:, :, hf, :], lambda hf: dstq[:, :, hf, :], rp[:, 0:32].unsqueeze(1).to_broadcast(shq),
                         rp[:, 32:64].unsqueeze(1).to_broadcast(shq), shq, tvq, d_bank[0], d_qr, d_rp, d_tmq)
                    shk = [128, 4, 64]
                    tvk = [t[:, 0:256].rearrange("p (h e) -> p h e", h=4) for t in tmq]
                    cosk = rp[:, 64:128].unsqueeze(1).to_broadcast(shk)
                    sink = rp[:, 128:192].unsqueeze(1).to_broadcast(shk)
                    srcrq = banks[1][:, 0:512].rearrange("p (h two e) -> p h two e", h=4, two=2)
                    dstrq = rqr[:].rearrange("p h (two e) -> p h two e", two=2)
                    rope(lambda hf: srcrq[:, :, hf, :], lambda hf: dstrq[:, :, hf, :], cosk, sink, shk, tvk, d_bank[1], d_rqr, d_rp, d_tmq)
                    srcrk = banks[4][:, 0:512].rearrange("p (h two e) -> p h two e", h=4, two=2)
                    dstrk = rkr[:].rearrange("p h (two e) -> p h two e", two=2)
                    rope(lambda hf: srcrk[:, :, hf, :], lambda hf: dstrk[:, :, hf, :], cosk, sink, shk, tvk, d_bank[4], d_rkr, d_rp, d_tmq)
                    if SUB < 3:
                        continue
                    P.op("pool", lambda e: e.tensor_scalar(out=qr[:], in0=qr[:], scalar1=0.125, scalar2=None, op0=ALU.mult), reads=[d_qr], writes=[d_qr])
                    P.op("pool", lambda e: e.tensor_scalar(out=rkr[:].rearrange("p a b -> p (a b)"), in0=rkr[:].rearrange("p a b -> p (a b)"),
                                                           scalar1=RET_SCALE, scalar2=None, op0=ALU.mult), reads=[d_rkr], writes=[d_rkr])
                    for g in range(4):
                        P.op("pe", lambda e, g=g: e.transpose(out=banks[3][:, g * 128:(g + 1) * 128], in_=qr[:, g * 128:(g + 1) * 128], identity=ident[:]),
                             reads=[d_qr, d_ident], writes=[d_bank[3]])
                    P.op("act", lambda e: e.copy(out=QT[:].rearrange("p a b -> p (a b)"), in_=banks[3][:, :]),
                         reads=[d_bank[3]], writes=[d_QT])
                    if SUB2 < 1:
                        continue
                    for h in range(4):
                        P.op("pe", lambda e, h=h: e.transpose(out=banks[3][:, h * 128:(h + 1) * 128], in_=rqr[:, h, :], identity=ident[:]),
                             reads=[d_rqr, d_ident], writes=[d_bank[3]])
                    P.op("act", lambda e: e.copy(out=rqT[:].rearrange("p a b -> p (a b)"), in_=banks[3][:, :]), reads=[d_bank[3]], writes=[d_rqT])
                    P.op("dve", lambda e: e.tensor_tensor(out=rqdT[:].rearrange("p a b -> p (a b)"), in0=banks[3][:, :],
                                                          in1=qdt[:].rearrange("p a b -> p (a b)"), op=ALU.mult),
                         reads=[d_bank[3], d_qdt], writes=[d_rqdT])
                    if SUB2 < 2:
                        continue
                    for h in range(4):
                        P.op("pe", lambda e, h=h: e.transpose(out=banks[3][:, h * 128:(h + 1) * 128], in_=rkr[:, h, :], identity=ident[:]),
                             reads=[d_rkr, d_ident], writes=[d_bank[3]])
                    P.op("act", lambda e: e.copy(out=rkT[:].rearrange("p a b -> p (a b)"), in_=banks[3][:, :]),
                         reads=[d_bank[3]], writes=[d_rkT])
                    if SUB < 4:
                        continue
                    for (bk, c0) in ((0, 1536), (1, 2048), (4, 2560), (5, 3072)):
                        for k in range(8):
                            P.op("pe", lambda e, bk=bk, k=k, c0=c0: e.matmul(out=banks[bk][:, :], lhsT=hT[:, k, :], rhs=WB[:, k, c0:c0 + 512],
                                                                           start=(k == 0), stop=(k == 7)),
                                 reads=[d_hT, d_WB], writes=[d_bank[bk]])
                    P.op("act", lambda e: e.copy(out=vb[:, 0:512], in_=banks[0][:, :]), reads=[d_bank[0]], writes=[d_vb])
                    P.op("act", lambda e: e.copy(out=vb[:, 512:1024], in_=banks[1][:, :]), reads=[d_bank[1]], writes=[d_vb])
                    P.op("act", lambda e: e.activation(out=sgate[:, 0:512], in_=banks[4][:, :], func=ACTF.Silu), reads=[d_bank[4]], writes=[d_sgate])
                    P.op("act", lambda e: e.activation(out=sgate[:, 512:1024], in_=banks[5][:, :], func=ACTF.Silu), reads=[d_bank[5]], writes=[d_sgate])
                    if PART < 2:
                        continue
                    for h in range(4):
                        P.op("pe", lambda e, h=h: e.matmul(out=banks[3][:, h * 128:(h + 1) * 128], lhsT=rkT[:, h, :], rhs=rqT[:, h, :], start=True, stop=True),
                             reads=[d_rkT, d_rqT], writes=[d_bank[3]])
                    P.op("dve", lambda e: e.tensor_tensor(out=attD[:].rearrange("p a b -> p (a b)"), in0=banks[3][:, :],
                                                          in1=dtab[:].rearrange("p a b -> p (a b)"), op=ALU.mult),
                         reads=[d_bank[3], d_dtab], writes=[d_attD])
                    for h in range(4):
                        bk = 4 + h // 2
                        c0 = (h % 2) * 256
                        P.op("pe", lambda e, h=h, bk=bk, c0=c0: e.matmul(out=banks[bk][:, c0:c0 + 256], lhsT=attD[:, h, :], rhs=vb[:, h * 256:(h + 1) * 256],
                                                                        start=True, stop=False),
                             reads=[d_attD, d_vb], writes=[d_bank[bk]])
                        P.op("pe", lambda e, h=h, bk=bk, c0=c0: e.matmul(out=banks[bk][:, c0:c0 + 256], lhsT=rqdT[:, h, :], rhs=SownB[:, h * 256:(h + 1) * 256],
                                                                        start=False, stop=True),
                             reads=[d_rqdT, d_SownB], writes=[d_bank[bk]])
                    for h in range(4):
                        bk = 4 + h // 2
                        c0 = (h % 2) * 256
                        P.op("act", lambda e, h=h, bk=bk, c0=c0: e.activation(out=osb[:, h, :], in_=banks[bk][:, c0:c0 + 256], func=ACTF.Identity,
                                                                             accum_out=st4[:, h:h + 1]),
                             reads=[d_bank[bk]], writes=[d_osb, d_st4])
                        P.op("act", lambda e, h=h, bk=bk, c0=c0: e.activation(out=junk_sh[:, 0:256], in_=banks[bk][:, c0:c0 + 256], func=ACTF.Square,
                                                                             accum_out=st4[:, 4 + h:5 + h]),
                             reads=[d_bank[bk]], writes=[d_junk_sh, d_st4])
                    P.op("dve", lambda e: e.tensor_scalar(out=st4[:, 8:12], in0=st4[:, 0:4], scalar1=1.0 / 256, scalar2=None, op0=ALU.mult),
                         reads=[d_st4], writes=[d_st4])
                    P.op("dve", lambda e: e.tensor_tensor(out=st4[:, 12:16], in0=st4[:, 8:12], in1=st4[:, 8:12], op=ALU.mult), reads=[d_st4], writes=[d_st4])
                    P.op("dve", lambda e: e.scalar_tensor_tensor(out=st4[:, 12:16], in0=st4[:, 4:8], scalar=1.0 / 256, in1=st4[:, 12:16],
                                                                 op0=ALU.mult, op1=ALU.subtract), reads=[d_st4], writes=[d_st4])
                    P.op("act", lambda e: e.activation(out=st4[:, 12:16], in_=st4[:, 12:16], func=ACTF.Sqrt, bias=epsG[:, 0:1], scale=1.0),
                         reads=[d_st4, d_epsG], writes=[d_st4])
                    P.op("dve", lambda e: e.reciprocal(out=st4[:, 12:16], in_=st4[:, 12:16]), reads=[d_st4], writes=[d_st4])
                    for h in range(4):
                        P.op("dve", lambda e, h=h: e.tensor_scalar(out=yret[:, h * 256:(h + 1) * 256], in0=osb[:, h, :], scalar1=st4[:, 8 + h:9 + h],
                                                                  scalar2=st4[:, 12 + h:13 + h], op0=ALU.subtract, op1=ALU.mult),
                             reads=[d_osb, d_st4], writes=[d_yret])
                    P.op("pool", lambda e: e.tensor_tensor(out=yret[:], in0=yret[:], in1=gnw[:], op=ALU.mult), reads=[d_yret, d_gnw], writes=[d_yret])
                    P.op("pool", lambda e: e.tensor_tensor(out=yret[:], in0=yret[:], in1=gnb[:], op=ALU.add), reads=[d_yret, d_gnb], writes=[d_yret])
                    P.op("pool", lambda e: e.tensor_tensor(out=yret[:], in0=yret[:], in1=sgate[:], op=ALU.mult), reads=[d_yret, d_sgate], writes=[d_yret])
                    for c in range(8):
                        P.op("pe", lambda e, c=c: e.transpose(out=pT[:, c * 128:(c + 1) * 128], in_=yret[:, c * 128:(c + 1) * 128], identity=ident[:]),
                             reads=[d_yret, d_ident], writes=[d_pTs[c // 4]])
                    P.op("act", lambda e: e.copy(out=yrT[:].rearrange("p a b -> p (a b)"), in_=pT[:]), reads=d_pTs, writes=[d_yrT])
                    P.dma("sp", lambda e, i=i: e.dma_start(out=yrT_d[i], in_=yrT[:]), reads=[d_yrT])
                    if DBG:
                        P.dma("sp", lambda e, i=i: e.dma_start(out=dbg_yr[i], in_=yret[:]), reads=[d_yret])

                    if PART < 3:
                        continue
                    attn.set_mode(2, 1)
                    for kvh in range(2):
                        QTh = QT[kvh * 64:(kvh + 1) * 64, :, :].rearrange("p a b -> p (a b)")
                        tiles = []
                        for jt in range(8):
                            tiles.append({"KT": cmpKT[kvh * 64:(kvh + 1) * 64, jt * 128:(jt + 1) * 128], "V": cmpVA[:, jt, kvh, :], "nk": 128,
                                          "biases": [(cmaskB[:, jt * 128:(jt + 1) * 128], IrepF)], "Ov": OvP[:, jt, :],
                                          "deps": [d_cmpKT, d_cmpVA, d_cmask, d_Irep, d_OvP]})
                        psv = lambda g: banks[4 + g // 2][:, (g % 2) * 256:(g % 2) * 256 + 256]
                        attn(128, 4, QTh, d_QT, tiles, o4[kvh], d_o4[kvh], pslc=(psv, [d_bank[4], d_bank[5]], 2), pvT=True)
                        evac(kvh, 0, True)
                        P.op("dve", lambda e, kvh=kvh: e.tensor_scalar(out=pacc[:], in0=psv(0), scalar1=rden[:, kvh * 4:kvh * 4 + 1], scalar2=None, op0=ALU.mult),
                             reads=[d_bank[4], d_rden], writes=[d_pacc])
                        for g in range(1, 4):
                            P.op("dve", lambda e, kvh=kvh, g=g: e.scalar_tensor_tensor(out=pacc[:], in0=psv(g), scalar=rden[:, kvh * 4 + g:kvh * 4 + g + 1],
                                                                                      in1=pacc[:], op0=ALU.mult, op1=ALU.add),
                                 reads=[d_bank[4 + g // 2], d_rden, d_pacc], writes=[d_pacc])
                        P.op("dve", lambda e: e.tensor_tensor(out=pacc[:], in0=pacc[:], in1=forceb[:], op=ALU.add), reads=[d_pacc, d_forceb], writes=[d_pacc])
                        P.op("dve", lambda e: e.max(out=m16[:, 0:8], in_=pacc[:]), reads=[d_pacc], writes=[d_m16])
                        P.op("dve", lambda e: e.match_replace(out=wk2[:], in_to_replace=m16[:, 0:8], in_values=pacc[:], imm_value=-3.0e38),
                             reads=[d_pacc, d_m16], writes=[d_wk2])
                        P.op("dve", lambda e: e.max(out=m16[:, 8:16], in_=wk2[:]), reads=[d_wk2], writes=[d_m16])
                        P.op("dve", lambda e, kvh=kvh: e.tensor_scalar(out=selb[kvh][:], in0=pacc[:], scalar1=m16[:, 15:16], scalar2=NEG,
                                                                      op0=ALU.is_lt, op1=ALU.mult),
                             reads=[d_pacc, d_m16], writes=[d_selb[kvh]])
                    if DBG:
                        P.dma("sp", lambda e, i=i: e.dma_start(out=dbg_selb[i, 0], in_=selb[0][:]), reads=[d_selb[0]])
                        P.dma("sp", lambda e, i=i: e.dma_start(out=dbg_selb[i, 1], in_=selb[1][:]), reads=[d_selb[1]])
                    if PART < 4:
                        continue
                    attn.set_mode(4, 2)
                    ngk = i + 1
                    for gk in range(ngk):
                        b = gk % 2
                        P.dma("sp", lambda e, gk=gk, b=b: e.dma_start(out=KsG[b][:], in_=ksT_d[:, gk * 1024:(gk + 1) * 1024]), writes=[d_KsG[b]])
                        P.dma("sp", lambda e, gk=gk, b=b: e.dma_start(out=VsG[b][:], in_=vs_d[gk * 1024:(gk + 1) * 1024, :].rearrange("(m p) f -> p m f", p=128)),
                              writes=[d_VsG[b]])
                        for kvh in range(2):
                            P.op("pool", lambda e, kvh=kvh, b=b, gk=gk: e.tensor_copy(
                                out=selbX[kvh][b][:].rearrange("p (a c) -> p a c", c=64),
                                in_=selb[kvh][:, gk * 16:(gk + 1) * 16].unsqueeze(2).to_broadcast([128, 16, 64])),
                                reads=[d_selb[kvh]], writes=[d_selbX[kvh][b]])
                        for kvh in range(2):
                            QTh = QT[kvh * 64:(kvh + 1) * 64, :, :].rearrange("p a b -> p (a b)")
                            tiles = []
                            for j in range(8):
                                bs = [(selbX[kvh][b][:, j * 128:(j + 1) * 128], IrepF)]
                                if gk == i:
                                    bs.append((dmaskB[:, j, :], IrepF))
                                tiles.append({"KT": KsG[b][kvh * 64:(kvh + 1) * 64, j * 128:(j + 1) * 128], "V": VsG[b][:, j, kvh * 65:(kvh + 1) * 65],
                                              "nk": 128, "biases": bs, "deps": [d_KsG[b], d_VsG[b], d_selbX[kvh][b], d_Irep, d_dmask]})
                            attn(128, 4, QTh, d_QT, tiles, o4[kvh], d_o4[kvh], first=(gk == 0), last=(gk == ngk - 1), pvT=True)
                    evac(0, 1, False)
                    evac(1, 1, False)
                    if PART < 5:
                        continue
                    for kvh in range(2):
                        QTh = QT[kvh * 64:(kvh + 1) * 64, :, :].rearrange("p a b -> p (a b)")
                        tiles = []
                        for m in range(12):
                            wm = m if i == 0 else 12 + m
                            tiles.append({"KT": KwWin[kvh * 64:(kvh + 1) * 64, m * 128:(m + 1) * 128], "V": VwWin[:, m, kvh * 65:(kvh + 1) * 65], "nk": 128,
                                          "biases": [(wmaskB[:, wm, :], IrepF)], "deps": [d_KwWin, d_VwWin, d_wmask, d_Irep]})
                        attn(128, 4, QTh, d_QT, tiles, o4[kvh], d_o4[kvh], pvT=True)
                        evac(kvh, 2, False)
                    onf = onsa[:].rearrange("p h d -> p (h d)")
                    for c in range(4):
                        P.op("pe", lambda e, c=c: e.transpose(out=pT[:, c * 128:(c + 1) * 128], in_=onf[:, c * 128:(c + 1) * 128], identity=ident[:]),
                             reads=[d_onsa, d_ident], writes=[d_pTs[0]])
                    P.op("act", lambda e: e.copy(out=onT[:].rearrange("p a b -> p (a b)"), in_=pT[:, 0:512]), reads=[d_pTs[0]], writes=[d_onT])
                    P.dma("sp", lambda e, i=i: e.dma_start(out=onT_d[i], in_=onT[:]), reads=[d_onT])
                    if DBG:
                        P.dma("sp", lambda e, i=i: e.dma_start(out=dbg_on[i], in_=onsa[:].rearrange("p h d -> p (h d)")), reads=[d_onsa])
                P.barrier()

        def phase_s1():
            with ExitStack() as sS:
                sR = ExitStack()
                smallc = P.sb([1, 1024], F32, sS); smallb = P.sb([1, 1024], BF16, sS); d_small = Dep()
                P.dma("sp", lambda e: e.dma_start(out=smallc[:], in_=smallc_d), writes=[d_small])
                P.op("dve", lambda e: e.tensor_copy(out=smallb[:], in_=smallc[:]), reads=[d_small], writes=[d_small])
                cm7row, wm0row, ones4 = smallb[0:1, 0:128], smallb[0:1, 128:256], smallb[0:1, 272:276]
                forceS = smallc[0:1, 276:532]
                oh4 = P.sb([SB, 4], F32, sS); d_oh4 = Dep()
                P.dma("sp", lambda e: e.dma_start(out=oh4[:], in_=oh4_d), writes=[d_oh4])
                OvS = P.sb([128, 8, 257], BF16, sS); d_OvS = Dep()
                P.dma("pool", lambda e: e.dma_start(out=OvS[:].rearrange("p a b -> p (a b)"), in_=ovs_d), writes=[d_OvS])
                wbc = P.sb([128, 2, 16, 256], F32, sS); d_wbc = Dep()
                P.dma("sp", lambda e: e.dma_start(out=wbc[:].rearrange("p a b c -> p (a b c)"), in_=wrow_d.partition_broadcast(128)[:, 0, :]), writes=[d_wbc])
                brow = P.sb([128, 256], F32, sS); d_brow = Dep()
                P.dma("sp", lambda e: e.dma_start(out=brow[:], in_=brow_d.partition_broadcast(128)[:, 0, :]), writes=[d_brow])
                QTs = P.sb([128, 4, SB], BF16, sS); d_QTs = Dep()
                KnT = P.sb([128, 2, SB], BF16, sS); d_KnT = Dep()
                VnA = P.sb([SB, 2, 2, 65], BF16, sS); d_VnA = Dep()
                sg = P.sb([SB, 24], F32, sS); d_sg = Dep()
                ohm = P.sb([128, 4, 4], F32, sR); d_ohm = Dep()
                P.dma("sp", lambda e: e.dma_start(out=ohm[:].rearrange("p a b -> p (a b)"), in_=ohm_d), writes=[d_ohm])
                gnw = P.sb([SB, D], F32, sR); d_gnw = Dep()
                gnb = P.sb([SB, D], F32, sR); d_gnb = Dep()
                P.dma("sp", lambda e: e.dma_start(out=gnw[:], in_=gnw_d.partition_broadcast(SB)[:, 0, :]), writes=[d_gnw])
                P.dma("sp", lambda e: e.dma_start(out=gnb[:], in_=gnb_d.partition_broadcast(SB)[:, 0, :]), writes=[d_gnb])
                epsG = P.sb([SB, 1], F32, sR); d_epsG = Dep()
                P.op("dve", lambda e: e.memset(epsG[:], GN_EPS), writes=[d_epsG])
                xs = P.sb([SB, D], F32, sR); d_xs = Dep()
                rps = P.sb([SB, 192], F32, sR); d_rps = Dep()
                msS = P.sb([SB, 2, D], F32, sR); d_msS = Dep()
                gm4 = P.sb([SB, D], F32, sR); d_gm4 = Dep()
                P.dma("sp", lambda e: e.dma_start(out=xs[:], in_=x_smp), writes=[d_xs])
                P.dma("sp", lambda e: e.dma_start(out=rps[:], in_=rope_smp), writes=[d_rps])
                P.dma("sp", lambda e: e.dma_start(out=msS[:], in_=mods_d[:, 0:2 * D].rearrange("s (a d) -> s a d", a=2)), writes=[d_msS])
                P.dma("sp", lambda e: e.dma_start(out=gm4[:], in_=norm_mix.partition_broadcast(SB)[:, 0, :]), writes=[d_gm4])
                P.op("dve", lambda e: e.scalar_tensor_tensor(out=msS[:, 1, :], in0=msS[:, 1, :], scalar=1.0, in1=gm4[:], op0=ALU.add, op1=ALU.mult),
                     reads=[d_msS, d_gm4], writes=[d_msS])
                wkS = mk_wk(sR, SB)
                hTs = P.sb([128, 8, SB], BF16, sR); d_hTs = Dep()
                norm_mod_T(xs, d_xs, SB, msS[:, 1, :], d_msS, msS[:, 0, :], d_msS, wkS, hTs, d_hTs)
                P.dma("sp", lambda e: e.dma_start(out=hTs_d, in_=hTs[:]), reads=[d_hTs])
                zs = P.sb([SB, 4376], F32, sR); d_zs = Dep()
                wch = [P.sb([128, 8, 512], BF16, sR) for _ in range(2)]; d_wch = [Dep(), Dep()]
                segs = []
                for g in range(4):
                    segs += [(C_Q + g * 64, 64), (C_Q + (4 + g) * 64, 64)]
                segs += [(C_RQ, 512), (C_RK, 512), (C_RV, 512), (C_RV + 512, 512), (C_RG, 512), (C_RG + 512, 512), (C_G, 24),
                         (C_KVC, 128), (C_KVS, 128), (C_KVW, 128), (C_KVC + 128, 128), (C_KVS + 128, 128), (C_KVW + 128, 128)]
                chunks = []
                cur, curw = [], 0
                for sg_ in segs:
                    if curw + sg_[1] > 512:
                        chunks.append(cur); cur, curw = [], 0
                    cur.append(sg_); curw += sg_[1]
                chunks.append(cur)
                zoff = 0
                for ci, ch in enumerate(chunks):
                    b = ci % 2
                    o_ = 0
                    for (src, n_) in ch:
                        P.dma("pool", lambda e, b=b, o_=o_, src=src, n_=n_: e.dma_start(
                            out=wch[b][:, :, o_:o_ + n_], in_=w_in[:, src:src + n_].rearrange("(c p) n -> p c n", p=128)), writes=[d_wch[b]])
                        o_ += n_
                    for k in range(8):
                        P.op("pe", lambda e, b=b, k=k, o_=o_: e.matmul(out=banks[b][0:SB, 0:o_], lhsT=hTs[:, k, :], rhs=wch[b][:, k, 0:o_], start=(k == 0), stop=(k == 7)),
                             reads=[d_hTs, d_wch[b]], writes=[d_bank[b]])
                    P.op("act", lambda e, b=b, o_=o_, zoff=zoff: e.copy(out=zs[:, zoff:zoff + o_], in_=banks[b][0:SB, 0:o_]), reads=[d_bank[b]], writes=[d_zs])
                    zoff += o_
                tms = [P.sb([SB, 256], F32, sR) for _ in range(4)]; d_tms = [Dep() for _ in range(4)]
                qr = P.sb([SB, 512], F32, sR); d_qr = Dep()
                rqr = P.sb([SB, 4, 128], F32, sR); d_rqr = Dep()
                rkr = P.sb([SB, 4, 128], F32, sR); d_rkr = Dep()
                kvo = P.sb([SB, 3, 256], F32, sR); d_kvo = Dep()
                sgate = P.sb([SB, 1024], F32, sR); d_sgate = Dep()
                vbs = P.sb([SB, 1024], BF16, sR); d_vbs = Dep()
                shq = [SB, 8, 32]
                srcq = zs[:, 0:512].rearrange("p (h two e) -> p h two e", h=8, two=2)
                dstq = qr[:].rearrange("p (h two e) -> p h two e", h=8, two=2)
                tvq = [t[:, 0:256].rearrange("p (h e) -> p h e", h=8) for t in tms]
                cos32 = rps[:, 0:32]; sin32 = rps[:, 32:64]
                rope(lambda hf: srcq[:, :, hf, :], lambda hf: dstq[:, :, hf, :], cos32.unsqueeze(1).to_broadcast(shq), sin32.unsqueeze(1).to_broadcast(shq),
                     shq, tvq, d_zs, d_qr, d_rps, d_tms)
                P.op("pool", lambda e: e.tensor_scalar(out=qr[:], in0=qr[:], scalar1=0.125, scalar2=None, op0=ALU.mult), reads=[d_qr], writes=[d_qr])
                shk = [SB, 4, 64]
                tvk = [t[:, 0:256].rearrange("p (h e) -> p h e", h=4) for t in tms]
                cosk = rps[:, 64:128].unsqueeze(1).to_broadcast(shk); sink = rps[:, 128:192].unsqueeze(1).to_broadcast(shk)
                for (c0, dstt, d_dst) in ((512, rqr, d_rqr), (1024, rkr, d_rkr)):
                    src_ = zs[:, c0:c0 + 512].rearrange("p (h two e) -> p h two e", h=4, two=2)
                    dst_ = dstt[:].rearrange("p h (two e) -> p h two e", two=2)
                    rope(lambda hf, src_=src_: src_[:, :, hf, :], lambda hf, dst_=dst_: dst_[:, :, hf, :], cosk, sink, shk, tvk, d_zs, d_dst, d_rps, d_tms)
                P.op("pool", lambda e: e.tensor_scalar(out=rkr[:].rearrange("p a b -> p (a b)"), in0=rkr[:].rearrange("p a b -> p (a b)"),
                                                       scalar1=RET_SCALE, scalar2=None, op0=ALU.mult), reads=[d_rkr], writes=[d_rkr])
                shn = [SB, 3, 2, 32]
                srcn = zs[:, 3608:3992].rearrange("p (j h two e) -> p j h two e", j=3, h=2, two=2)
                dstn = kvo[:, :, 0:128].rearrange("p j (h two e) -> p j h two e", h=2, two=2)
                tvn = [t[:, 0:192].rearrange("p (j h e) -> p j h e", j=3, h=2) for t in tms]
                rope(lambda hf: srcn[:, :, :, hf, :], lambda hf: dstn[:, :, :, hf, :], cos32.unsqueeze(1).unsqueeze(1).to_broadcast(shn),
                     sin32.unsqueeze(1).unsqueeze(1).to_broadcast(shn), shn, tvn, d_zs, d_kvo, d_rps, d_tms)
                P.op("act", lambda e: e.copy(out=kvo[:, :, 128:256], in_=zs[:, 3992:4376].rearrange("p (j f) -> p j f", j=3)), reads=[d_zs], writes=[d_kvo])
                P.op("act", lambda e: e.activation(out=sg[:], in_=zs[:, 3584:3608], func=ACTF.Sigmoid), reads=[d_zs], writes=[d_sg])
                P.op("act", lambda e: e.activation(out=sgate[:], in_=zs[:, 2560:3584], func=ACTF.Silu), reads=[d_zs], writes=[d_sgate])
                P.op("act", lambda e: e.copy(out=vbs[:], in_=zs[:, 1536:2560]), reads=[d_zs], writes=[d_vbs])
                P.dma("sp", lambda e: e.dma_start(out=o_cmp_s, in_=kvo[:, 0, :]), reads=[d_kvo])
                P.dma("sp", lambda e: e.dma_start(out=o_sel_s, in_=kvo[:, 1, :]), reads=[d_kvo])
                P.dma("sp", lambda e: e.dma_start(out=o_win_s[:, 511, :], in_=kvo[:, 2, :]), reads=[d_kvo])
                for b in range(SB):
                    P.dma("sp", lambda e, b=b: e.dma_start(out=o_win_s[b, 0:511, :], in_=cache_win[b, 1:512, :]))
                rqTs = P.sb([128, 4, SB], BF16, sR); d_rqTs = Dep()
                rqdTs = P.sb([128, 4, SB], F32, sR); d_rqdTs = Dep()
                rkTs = P.sb([128, 4, SB], BF16, sR); d_rkTs = Dep()
                for (srcs, dstT, d_srcs, d_dstT) in (([qr[:, g * 128:(g + 1) * 128] for g in range(4)], QTs, d_qr, d_QTs),
                                                     ([rqr[:, h, :] for h in range(4)], rqTs, d_rqr, d_rqTs),
                                                     ([rkr[:, h, :] for h in range(4)], rkTs, d_rkr, d_rkTs),
                                                     ([kvo[:, 1, 0:128], kvo[:, 2, 0:128]], KnT, d_kvo, d_KnT)):
                    for j, sap in enumerate(srcs):
                        P.op("pe", lambda e, j=j, sap=sap: e.transpose(out=banks[3][:, j * SB:(j + 1) * SB], in_=sap, identity=ident[0:SB, 0:SB]),
                             reads=[d_srcs, d_ident], writes=[d_bank[3]])
                    nn = len(srcs) * SB
                    P.op("act", lambda e, dstT=dstT, nn=nn: e.copy(out=dstT[:].rearrange("p a b -> p (a b)"), in_=banks[3][:, 0:nn]), reads=[d_bank[3]], writes=[d_dstT])
                for h in range(4):
                    P.op("dve", lambda e, h=h: e.tensor_scalar(out=rqdTs[:, h, :], in0=rqTs[:, h, :], scalar1=float(math.exp(LG[h])), scalar2=None, op0=ALU.mult),
                         reads=[d_rqTs], writes=[d_rqdTs])
                P.op("pool", lambda e: e.memset(VnA[:].rearrange("p a b c -> p (a b c)"), 1.0), writes=[d_VnA])
                P.op("pool", lambda e: e.tensor_copy(out=VnA[:, :, :, 0:64], in_=kvo[:, 1:3, 128:256].rearrange("p j (h d) -> p j h d", h=2)),
                     reads=[d_kvo], writes=[d_VnA])
                attDs = P.sb([SB, 4, SB], BF16, sR); d_attDs = Dep()
                for h in range(4):
                    P.op("pe", lambda e, h=h: e.matmul(out=banks[3][0:SB, h * SB:(h + 1) * SB], lhsT=rkTs[:, h, :], rhs=rqTs[:, h, :], start=True, stop=True),
                         reads=[d_rkTs, d_rqTs], writes=[d_bank[3]])
                P.op("dve", lambda e: e.tensor_tensor(out=attDs[:], in0=banks[3][0:SB, 0:16].rearrange("p (h i) -> p h i", h=4),
                                                      in1=oh4[:].unsqueeze(1).to_broadcast([SB, 4, SB]), op=ALU.mult), reads=[d_bank[3], d_oh4], writes=[d_attDs])
                Sb = [P.sb([128, 4, 256], F32, sR) for _ in range(SB)]; d_Sb = [Dep() for _ in range(SB)]
                Sbb = [P.sb([128, 4, 256], BF16, sR) for _ in range(SB)]; d_Sbb = [Dep() for _ in range(SB)]
                rqm = [P.sb([128, 4, SB], BF16, sR) for _ in range(SB)]; d_rqm = [Dep() for _ in range(SB)]
                kdm = [P.sb([SB, 4, 128], BF16, sR) for _ in range(SB)]; d_kdm = [Dep() for _ in range(SB)]
                for b in range(SB):
                    P.dma("sp", lambda e, b=b: e.dma_start(out=Sb[b][:], in_=state_s[b].rearrange("h k v -> k h v")), writes=[d_Sb[b]])
                    P.op("pool", lambda e, b=b: e.tensor_copy(out=Sbb[b][:].rearrange("p a b -> p (a b)"), in_=Sb[b][:].rearrange("p a b -> p (a b)")),
                         reads=[d_Sb[b]], writes=[d_Sbb[b]])
                    P.op("dve", lambda e, b=b: e.tensor_tensor(out=rqm[b][:], in0=rqdTs[:], in1=ohm[:, b, :].unsqueeze(1).to_broadcast([128, 4, SB]), op=ALU.mult),
                         reads=[d_rqdTs, d_ohm], writes=[d_rqm[b]])
                    P.op("dve", lambda e, b=b: e.tensor_scalar(out=kdm[b][:].rearrange("p a b -> p (a b)"), in0=rkr[:].rearrange("p a b -> p (a b)"),
                                                              scalar1=oh4[:, b:b + 1], scalar2=None, op0=ALU.mult), reads=[d_rkr, d_oh4], writes=[d_kdm[b]])
                for h in range(4):
                    bk = 4 + h // 2
                    c0 = (h % 2) * 256
                    P.op("pe", lambda e, h=h, bk=bk, c0=c0: e.matmul(out=banks[bk][0:SB, c0:c0 + 256], lhsT=attDs[:, h, :], rhs=vbs[:, h * 256:(h + 1) * 256],
                                                                    start=True, stop=False), reads=[d_attDs, d_vbs], writes=[d_bank[bk]])
                    for b in range(SB):
                        P.op("pe", lambda e, h=h, bk=bk, c0=c0, b=b: e.matmul(out=banks[bk][0:SB, c0:c0 + 256], lhsT=rqm[b][:, h, :], rhs=Sbb[b][:, h, :],
                                                                             start=False, stop=(b == SB - 1)), reads=[d_rqm[b], d_Sbb[b]], writes=[d_bank[bk]])
                osb = P.sb([SB, 4, 256], F32, sR); d_osb = Dep()
                st4 = P.sb([SB, 16], F32, sR); d_st4 = Dep()
                yret = P.sb([SB, 1024], F32, sR); d_yret = Dep()
                for h in range(4):
                    bk = 4 + h // 2
                    c0 = (h % 2) * 256
                    P.op("act", lambda e, h=h, bk=bk, c0=c0: e.activation(out=osb[:, h, :], in_=banks[bk][0:SB, c0:c0 + 256], func=ACTF.Identity, accum_out=st4[:, h:h + 1]),
                         reads=[d_bank[bk]], writes=[d_osb, d_st4])
                    P.op("act", lambda e, h=h, bk=bk, c0=c0: e.activation(out=junk_sh[0:SB, 0:256], in_=banks[bk][0:SB, c0:c0 + 256], func=ACTF.Square,
                                                                         accum_out=st4[:, 4 + h:5 + h]), reads=[d_bank[bk]], writes=[d_junk_sh, d_st4])
                P.op("dve", lambda e: e.tensor_scalar(out=st4[:, 8:12], in0=st4[:, 0:4], scalar1=1.0 / 256, scalar2=None, op0=ALU.mult), reads=[d_st4], writes=[d_st4])
                P.op("dve", lambda e: e.tensor_tensor(out=st4[:, 12:16], in0=st4[:, 8:12], in1=st4[:, 8:12], op=ALU.mult), reads=[d_st4], writes=[d_st4])
                P.op("dve", lambda e: e.scalar_tensor_tensor(out=st4[:, 12:16], in0=st4[:, 4:8], scalar=1.0 / 256, in1=st4[:, 12:16], op0=ALU.mult, op1=ALU.subtract),
                     reads=[d_st4], writes=[d_st4])
                P.op("act", lambda e: e.activation(out=st4[:, 12:16], in_=st4[:, 12:16], func=ACTF.Sqrt, bias=epsG[:, 0:1], scale=1.0), reads=[d_st4, d_epsG], writes=[d_st4])
                P.op("dve", lambda e: e.reciprocal(out=st4[:, 12:16], in_=st4[:, 12:16]), reads=[d_st4], writes=[d_st4])
                for h in range(4):
                    P.op("dve", lambda e, h=h: e.tensor_scalar(out=yret[:, h * 256:(h + 1) * 256], in0=osb[:, h, :], scalar1=st4[:, 8 + h:9 + h],
                                                              scalar2=st4[:, 12 + h:13 + h], op0=ALU.subtract, op1=ALU.mult), reads=[d_osb, d_st4], writes=[d_yret])
                P.op("pool", lambda e: e.tensor_tensor(out=yret[:], in0=yret[:], in1=gnw[:], op=ALU.mult), reads=[d_yret, d_gnw], writes=[d_yret])
                P.op("pool", lambda e: e.tensor_tensor(out=yret[:], in0=yret[:], in1=gnb[:], op=ALU.add), reads=[d_yret, d_gnb], writes=[d_yret])
                P.op("pool", lambda e: e.tensor_tensor(out=yret[:], in0=yret[:], in1=sgate[:], op=ALU.mult), reads=[d_yret, d_sgate], writes=[d_yret])
                yrTs = P.sb([128, 8, SB], BF16, sR); d_yrTs = Dep()
                for c in range(8):
                    P.op("pe", lambda e, c=c: e.transpose(out=banks[3][:, c * SB:(c + 1) * SB], in_=yret[:, c * 128:(c + 1) * 128], identity=ident[0:SB, 0:SB]),
                         reads=[d_yret, d_ident], writes=[d_bank[3]])
                P.op("act", lambda e: e.copy(out=yrTs[:].rearrange("p a b -> p (a b)"), in_=banks[3][:, 0:8 * SB]), reads=[d_bank[3]], writes=[d_yrTs])
                P.dma("sp", lambda e: e.dma_start(out=yrTs_d, in_=yrTs[:]), reads=[d_yrTs])
                snew = [P.sb([128, 256], F32, sR) for _ in range(2)]; d_snew = [Dep(), Dep()]
                for b in range(SB):
                    for h in range(4):
                        bb = (b * 4 + h) % 2
                        P.op("pe", lambda e, b=b, h=h, bb=bb: e.matmul(out=banks[4 + bb][:, 0:256], lhsT=kdm[b][:, h, :], rhs=vbs[:, h * 256:(h + 1) * 256],
                                                                      start=True, stop=True), reads=[d_kdm[b], d_vbs], writes=[d_bank[4 + bb]])
                        P.op("dve", lambda e, b=b, h=h, bb=bb: e.scalar_tensor_tensor(out=snew[bb][:], in0=Sb[b][:, h, :], scalar=float(math.exp(LG[h])),
                                                                                     in1=banks[4 + bb][:, 0:256], op0=ALU.mult, op1=ALU.add),
                             reads=[d_Sb[b], d_bank[4 + bb]], writes=[d_snew[bb]])
                        P.dma("sp", lambda e, b=b, h=h, bb=bb: e.dma_start(out=o_state_s[b, h], in_=snew[bb][:]), reads=[d_snew[bb]])

                P.barrier()
                sR.close()
                pts = [P.sb([128, 512], BF16, sS) for _ in range(2)]; d_pts = [Dep(), Dep()]
                attn = make_attn(pts, d_pts)
                G = [P.sb([128, 16, 256], F32, sS) for _ in range(2)]; d_G = [Dep(), Dep()]
                prod = [P.sb([128, 16, 256], F32, sS) for _ in range(2)]; d_prod = [Dep(), Dep()]
                Fs = P.sb([128, 8, 256], F32, sS); d_Fs = Dep()
                S2 = P.sb([128, 9, 256], F32, sS); d_S2 = Dep()
                cmpKTs = P.sb([128, 8, 128], BF16, sS); d_cmpKTs = Dep()
                cmpVAs = P.sb([128, 8, 2, 65], BF16, sS); d_cmpVAs = Dep()
                P.op("pool", lambda e: e.memset(cmpVAs[:].rearrange("p a b c -> p (a b c)"), 1.0), writes=[d_cmpVAs])
                KT16 = [P.sb([128, 16, 128], BF16, sS) for _ in range(2)]; d_KT16 = [Dep(), Dep()]
                V16 = [P.sb([128, 16, 2, 65], BF16, sS) for _ in range(2)]; d_V16 = [Dep(), Dep()]
                for t_ in V16:
                    P.op("pool", lambda e, t_=t_: e.memset(t_[:].rearrange("p a b c -> p (a b c)"), 1.0), writes=d_V16)
                Wt = P.sb([128, 4, 256], F32, sS); d_Wt = Dep()
                KTw = P.sb([128, 4, 128], BF16, sS); d_KTw = Dep()
                VwA = P.sb([128, 4, 2, 65], BF16, sS); d_VwA = Dep()
                P.op("pool", lambda e: e.memset(VwA[:].rearrange("p a b c -> p (a b c)"), 1.0), writes=[d_VwA])
                ptb = P.sb([128, 2], I32, sS); d_ptb = Dep()
                idx8 = P.sb([128, 9], I32, sS); d_idx8 = Dep()
                og = P.sb([SB, 65], F32, sS); d_og = Dep()
                rd1 = P.sb([SB, 2], F32, sS); d_rd1 = Dep()
                pn = P.sb([SB, 256], F32, sS); d_pn = Dep()
                srow = P.sb([1, 256], F32, sS); d_srow = Dep()
                wk2 = P.sb([1, 256], F32, sS); d_wk2 = Dep()
                m16 = P.sb([1, 16], F32, sS); d_m16 = Dep()
                selbB = [P.sb([1, 256], BF16, sS) for _ in range(2)]; d_selbB = [Dep(), Dep()]
                o4s = [banks[2][0:SB, 0:65].rearrange("p (g c) -> p g c", g=1), banks[3][0:SB, 0:65].rearrange("p (g c) -> p g c", g=1)]
                d_o4s = [d_bank[2], d_bank[3]]
                ones4f = oh4
                onec = P.sb([SB, 1], F32, sS); d_onec = Dep()
                P.op("dve", lambda e: e.memset(onec[:], 1.0), writes=[d_onec])

                def evac_s(b, kvh, br_i):
                    attn.flush()
                    o4v, d_o = o4s[kvh], d_o4s[kvh]
                    P.op("dve", lambda e: e.tensor_scalar(out=rd1[:, 0:1], in0=o4v[:, 0, 64:65], scalar1=1e-30, scalar2=None, op0=ALU.max), reads=[d_o], writes=[d_rd1])
                    P.op("dve", lambda e: e.reciprocal(out=rd1[:, 0:1], in_=rd1[:, 0:1]), reads=[d_rd1], writes=[d_rd1])
                    P.op("dve", lambda e: e.tensor_scalar(out=og[:, 0:64], in0=o4v[:, 0, 0:64], scalar1=rd1[:, 0:1], scalar2=None, op0=ALU.mult),
                         reads=[d_o, d_rd1], writes=[d_og])
                    P.dma("sp", lambda e: e.dma_start(out=os_d[b, br_i, kvh * 4:(kvh + 1) * 4, :], in_=og[:, 0:64]), reads=[d_og])

                for b in range(SB):
                    P.dma("sp", lambda e, b=b: e.dma_start(out=ptb[:, 0:1], in_=ptab[b].rearrange("(p o) -> p o", o=1)), writes=[d_ptb])
                    P.dma("sp", lambda e, b=b: e.dma_start(out=ptb[0:127, 1:2], in_=ptab[b, 1:128].rearrange("(p o) -> p o", o=1)), writes=[d_ptb])
                    P.dma("sp", lambda e, b=b: e.dma_start(out=ptb[127:128, 1:2], in_=ptab[b, 127:128].rearrange("(p o) -> p o", o=1)), writes=[d_ptb])
                    for k in range(8):
                        P.op("dve", lambda e, k=k: e.tensor_scalar(out=idx8[:, k:k + 1], in0=ptb[:, 0:1], scalar1=8, scalar2=k, op0=ALU.mult, op1=ALU.add),
                             reads=[d_ptb], writes=[d_idx8])
                    P.op("dve", lambda e: e.tensor_scalar(out=idx8[:, 8:9], in0=ptb[:, 1:2], scalar1=8, scalar2=0, op0=ALU.mult, op1=ALU.add),
                         reads=[d_ptb], writes=[d_idx8])
                    for k in range(9):
                        gb_ = k % 2
                        P.dma("pool", lambda e, k=k, gb_=gb_: e.indirect_dma_start(
                            out=G[gb_][:].rearrange("p a b -> p (a b)"), out_offset=None, in_=cache_cmp[:, :],
                            in_offset=bass.IndirectOffsetOnAxis(ap=idx8[:, k:k + 1], axis=0)), reads=[d_idx8], writes=[d_G[gb_]])
                        for r in range(2):
                            if k == 8 and r == 0:
                                continue
                            P.op("pool", lambda e, gb_=gb_, r=r: e.tensor_tensor(out=prod[r][:], in0=G[gb_][:], in1=wbc[:, r, :, :], op=ALU.mult),
                                 reads=[d_G[gb_], d_wbc], writes=[d_prod[r]])
                            dstF = Fs[:, k, :] if r == 0 else S2[:, k, :]
                            P.op("dve", lambda e, r=r, dstF=dstF: e.tensor_reduce(out=dstF, in_=prod[r][:].rearrange("p j f -> p f j"), axis=AX.X, op=ALU.add),
                                 reads=[d_prod[r]], writes=[d_Fs if r == 0 else d_S2])
                    P.op("dve", lambda e: e.tensor_tensor(out=Fs[:], in0=Fs[:], in1=S2[:, 1:9, :], op=ALU.add), reads=[d_Fs, d_S2], writes=[d_Fs])
                    P.op("dve", lambda e: e.tensor_tensor(out=Fs[:], in0=Fs[:], in1=brow[:].unsqueeze(1).to_broadcast([128, 8, 256]), op=ALU.add),
                         reads=[d_Fs, d_brow], writes=[d_Fs])
                    for s_ in range(8):
                        P.op("pe", lambda e, s_=s_: e.transpose(out=pT[:, s_ * 128:(s_ + 1) * 128], in_=Fs[:, s_, 0:128], identity=ident[:]),
                             reads=[d_Fs, d_ident], writes=[d_pTs[s_ // 4]])
                    P.op("act", lambda e: e.copy(out=cmpKTs[:].rearrange("p a b -> p (a b)"), in_=pT[:]), reads=d_pTs, writes=[d_cmpKTs])
                    P.op("pool", lambda e: e.tensor_copy(out=cmpVAs[:, :, :, 0:64], in_=Fs[:, :, 128:256].rearrange("p s (h d) -> p s h d", h=2)),
                         reads=[d_Fs], writes=[d_cmpVAs])
                    for kvh in range(2):
                        QTh = QTs[kvh * 64:(kvh + 1) * 64, :, b]
                        tiles = []
                        for s_ in range(8):
                            bs = [(cm7row, ones4)] if s_ == 7 else []
                            tiles.append({"KT": cmpKTs[kvh * 64:(kvh + 1) * 64, s_, :], "V": cmpVAs[:, s_, kvh, :], "nk": 128, "biases": bs,
                                          "Ov": OvS[:, s_, :], "deps": [d_cmpKTs, d_cmpVAs, d_small, d_OvS]})
                        psvS = lambda g: banks[4][0:SB, 0:257]
                        attn(SB, 1, QTh, d_QTs, tiles, o4s[kvh], d_o4s[kvh], pslc=(psvS, [d_bank[4]], 1))
                        evac_s(b, kvh, 0)
                        P.op("dve", lambda e: e.tensor_scalar(out=rd1[:, 1:2], in0=banks[4][0:SB, 256:257], scalar1=1e-30, scalar2=None, op0=ALU.max),
                             reads=[d_bank[4]], writes=[d_rd1])
                        P.op("dve", lambda e: e.reciprocal(out=rd1[:, 1:2], in_=rd1[:, 1:2]), reads=[d_rd1], writes=[d_rd1])
                        P.op("dve", lambda e: e.tensor_scalar(out=pn[:], in0=banks[4][0:SB, 0:256], scalar1=rd1[:, 1:2], scalar2=None, op0=ALU.mult),
                             reads=[d_bank[4], d_rd1], writes=[d_pn])
                        P.op("pe", lambda e: e.matmul(out=banks[5][0:1, 0:256], lhsT=onec[:, 0:1], rhs=pn[:], start=True, stop=True),
                             reads=[d_onec, d_pn], writes=[d_bank[5]])
                        P.op("dve", lambda e: e.tensor_tensor(out=srow[:], in0=banks[5][0:1, 0:256], in1=forceS, op=ALU.add), reads=[d_bank[5], d_small], writes=[d_srow])
                        P.op("dve", lambda e: e.max(out=m16[:, 0:8], in_=srow[:]), reads=[d_srow], writes=[d_m16])
                        P.op("dve", lambda e: e.match_replace(out=wk2[:], in_to_replace=m16[:, 0:8], in_values=srow[:], imm_value=-3.0e38),
                             reads=[d_srow, d_m16], writes=[d_wk2])
                        P.op("dve", lambda e: e.max(out=m16[:, 8:16], in_=wk2[:]), reads=[d_wk2], writes=[d_m16])
                        P.op("dve", lambda e, kvh=kvh: e.tensor_scalar(out=selbB[kvh][:], in0=srow[:], scalar1=m16[:, 14:15], scalar2=NEG, op0=ALU.is_lt, op1=ALU.mult),
                             reads=[d_srow, d_m16], writes=[d_selbB[kvh]])
                    for k in range(8):
                        gb_ = k % 2
                        P.dma("pool", lambda e, k=k, gb_=gb_: e.indirect_dma_start(
                            out=G[gb_][:].rearrange("p a b -> p (a b)"), out_offset=None, in_=cache_sel[:, :],
                            in_offset=bass.IndirectOffsetOnAxis(ap=idx8[:, k:k + 1], axis=0)), reads=[d_idx8], writes=[d_G[gb_]])
                        for j in range(16):
                            if j % 4 == 0:
                                tb_, d_tb = (banks[4], d_bank[4]) if (j // 4) % 2 == 0 else (banks[5], d_bank[5])
                            P.op("pe", lambda e, gb_=gb_, j=j, tb_=tb_: e.transpose(out=tb_[:, (j % 4) * 128:(j % 4 + 1) * 128], in_=G[gb_][:, j, 0:128], identity=ident[:]),
                                 reads=[d_G[gb_], d_ident], writes=[d_tb])
                            if j % 4 == 3:
                                P.op("act", lambda e, gb_=gb_, j=j, tb_=tb_: e.copy(out=KT16[gb_][:, j - 3:j + 1, :].rearrange("p a b -> p (a b)"), in_=tb_[:, :]),
                                     reads=[d_tb], writes=[d_KT16[gb_]])
                        P.op("pool", lambda e, gb_=gb_: e.tensor_copy(out=V16[gb_][:, :, :, 0:64], in_=G[gb_][:, :, 128:256].rearrange("p j (h d) -> p j h d", h=2)),
                             reads=[d_G[gb_]], writes=[d_V16[gb_]])
                        for kvh in range(2):
                            QTh = QTs[kvh * 64:(kvh + 1) * 64, :, b]
                            tiles = []
                            for j in range(16):
                                hi = 1 if (16 * k + j) >= 64 else 0
                                tiles.append({"KT": KT16[gb_][kvh * 64:(kvh + 1) * 64, j, :], "V": V16[gb_][:, j, kvh, :], "nk": 128,
                                              "biases": [(selbB[kvh][0:1, hi:256:2], ones4)], "deps": [d_KT16[gb_], d_V16[gb_], d_selbB[kvh], d_small]})
                            attn(SB, 1, QTh, d_QTs, tiles, o4s[kvh], d_o4s[kvh], first=(k == 0), last=False)
                    for kvh in range(2):
                        QTh = QTs[kvh * 64:(kvh + 1) * 64, :, b]
                        tiles = [{"KT": KnT[kvh * 64:(kvh + 1) * 64, 0, :], "V": VnA[:, 0, kvh, :], "nk": SB,
                                  "biases": [(smallb[0:1, 256 + 4 * b:260 + 4 * b], ones4)], "deps": [d_KnT, d_VnA, d_small]}]
                        attn(SB, 1, QTh, d_QTs, tiles, o4s[kvh], d_o4s[kvh], first=False, last=True)
                        evac_s(b, kvh, 1)
                    P.dma("sp", lambda e, b=b: e.dma_start(out=Wt[:], in_=cache_win[b].rearrange("(m p) f -> p m f", p=128)), writes=[d_Wt])
                    for m in range(4):
                        P.op("pe", lambda e, m=m: e.transpose(out=banks[4][:, m * 128:(m + 1) * 128], in_=Wt[:, m, 0:128], identity=ident[:]),
                             reads=[d_Wt, d_ident], writes=[d_bank[4]])
                    P.op("act", lambda e: e.copy(out=KTw[:].rearrange("p a b -> p (a b)"), in_=banks[4][:, :]), reads=[d_bank[4]], writes=[d_KTw])
                    P.op("pool", lambda e: e.tensor_copy(out=VwA[:, :, :, 0:64], in_=Wt[:, :, 128:256].rearrange("p m (h d) -> p m h d", h=2)),
                         reads=[d_Wt], writes=[d_VwA])
                    for kvh in range(2):
                        QTh = QTs[kvh * 64:(kvh + 1) * 64, :, b]
                        tiles = []
                        for m in range(4):
                            bs = [(wm0row, ones4)] if m == 0 else []
                            tiles.append({"KT": KTw[kvh * 64:(kvh + 1) * 64, m, :], "V": VwA[:, m, kvh, :], "nk": 128, "biases": bs,
                                          "deps": [d_KTw, d_VwA, d_small]})
                        tiles.append({"KT": KnT[kvh * 64:(kvh + 1) * 64, 1, :], "V": VnA[:, 1, kvh, :], "nk": SB,
                                      "biases": [(smallb[0:1, 256 + 4 * b:260 + 4 * b], ones4)], "deps": [d_KnT, d_VnA, d_small]})
                        attn(SB, 1, QTh, d_QTs, tiles, o4s[kvh], d_o4s[kvh])
                        evac_s(b, kvh, 2)
                P.barrier()
                osall = P.sb([SB, 3, 8, 64], F32, sS); d_osall = Dep()
                onsa = P.sb([SB, 8, 64], F32, sS); d_onsa = Dep()
                tmpo = P.sb([SB, 8, 64], F32, sS); d_tmpo = Dep()
                P.dma("sp", lambda e: e.dma_start(out=osall[:].rearrange("p a b c -> p (a b c)"), in_=os_d.rearrange("b r h d -> b (r h d)")), writes=[d_osall])
                sgv = sg[:].rearrange("p (h r) -> p h r", r=3)
                for r in range(3):
                    dst_, d_dst_ = (onsa, d_onsa) if r == 0 else (tmpo, d_tmpo)
                    P.op("dve", lambda e, r=r, dst_=dst_: e.tensor_tensor(out=dst_[:], in0=osall[:, r, :, :], in1=sgv[:, :, r].unsqueeze(2).to_broadcast([SB, 8, 64]), op=ALU.mult),
                         reads=[d_osall, d_sg], writes=[d_dst_])
                    if r > 0:
                        P.op("dve", lambda e: e.tensor_tensor(out=onsa[:], in0=onsa[:], in1=tmpo[:], op=ALU.add), reads=[d_onsa, d_tmpo], writes=[d_onsa])
                onTs = P.sb([128, 4, SB], BF16, sS); d_onTs = Dep()
                onf = onsa[:].rearrange("p h d -> p (h d)")
                P.dma("sp", lambda e: e.dma_start(out=dbg_ons, in_=onf), reads=[d_onsa])
                for c in range(4):
                    P.op("pe", lambda e, c=c: e.transpose(out=banks[3][:, c * SB:(c + 1) * SB], in_=onf[:, c * 128:(c + 1) * 128], identity=ident[0:SB, 0:SB]),
                         reads=[d_onsa, d_ident], writes=[d_bank[3]])
                P.op("act", lambda e: e.copy(out=onTs[:].rearrange("p a b -> p (a b)"), in_=banks[3][:, 0:4 * SB]), reads=[d_bank[3]], writes=[d_onTs])
                P.dma("sp", lambda e: e.dma_start(out=onTs_d, in_=onTs[:]), reads=[d_onTs])
                P.barrier()

        def phase_b2(groups):
            with ExitStack() as s2:
                Wgab = P.sb([128, 8, 2048], BF16, s2); d_Wgab = Dep()
                Wbn = P.sb([128, 4, 1024], BF16, s2); d_Wbn = Dep()
                Wbr = P.sb([128, 8, 1024], BF16, s2); d_Wbr = Dep()
                Wout = P.sb([128, 8, 1024], BF16, s2); d_Wout = Dep()
                P.dma("pool", lambda e: e.dma_start(out=Wgab[:], in_=w_in[:, C_GA:C_GA + 2048].rearrange("(c p) n -> p c n", p=128)), writes=[d_Wgab])
                P.dma("pool", lambda e: e.dma_start(out=Wbn[:], in_=w_bn.rearrange("(c p) n -> p c n", p=128)), writes=[d_Wbn])
                P.dma("pool", lambda e: e.dma_start(out=Wbr[:], in_=w_br.rearrange("(c p) n -> p c n", p=128)), writes=[d_Wbr])
                P.dma("pool", lambda e: e.dma_start(out=Wout[:], in_=w_out.rearrange("(c p) n -> p c n", p=128)), writes=[d_Wout])
                hT4 = P.sb([128, 8, 512], BF16, s2); d_hT4 = Dep()
                onT4 = P.sb([128, 4, 512], BF16, s2); d_onT4 = Dep()
                yrT4 = P.sb([128, 8, 512], BF16, s2); d_yrT4 = Dep()
                mixT = P.sb([128, 8, 512], BF16, s2); d_mixT = Dep()
                sga = [P.sb([128, 512], F32, s2) for _ in range(2)]; d_sga = [Dep(), Dep()]
                sgb = [P.sb([128, 512], F32, s2) for _ in range(2)]; d_sgb = [Dep(), Dep()]
                t1 = [P.sb([128, 512], F32, s2) for _ in range(2)]; d_t1 = [Dep(), Dep()]
                t2 = [P.sb([128, 512], F32, s2) for _ in range(2)]; d_t2 = [Dep(), Dep()]
                xa = [P.sb([128, D], F32, s2) for _ in range(2)]; d_xa = [Dep(), Dep()]
                x1t = [P.sb([128, D], F32, s2) for _ in range(2)]; d_x1t = [Dep(), Dep()]
                gtaS = P.sb([SB, D], F32, s2); d_gtaS = Dep()
                P.dma("sp", lambda e: e.dma_start(out=gtaS[:], in_=mods_d[:, 2 * D:3 * D]), writes=[d_gtaS])
                for grp in groups:
                    n = grp["ntok"]
                    grp["load_a"](hT4, d_hT4, onT4, d_onT4, yrT4, d_yrT4)
                    for fc in range(8):
                        b = fc % 2
                        for (bk, W, dW, kc, c0, src, dsrc) in ((0, Wgab, d_Wgab, 8, fc * 128, hT4, d_hT4), (1, Wgab, d_Wgab, 8, 1024 + fc * 128, hT4, d_hT4),
                                                              (2, Wbn, d_Wbn, 4, fc * 128, onT4, d_onT4), (3, Wbr, d_Wbr, 8, fc * 128, yrT4, d_yrT4)):
                            for k in range(kc):
                                P.op("pe", lambda e, bk=bk, W=W, k=k, c0=c0, src=src, kc=kc, n=n: e.matmul(
                                    out=banks[bk][:, 0:n], lhsT=W[:, k, c0:c0 + 128], rhs=src[:, k, 0:n], start=(k == 0), stop=(k == kc - 1)),
                                    reads=[dW, dsrc], writes=[d_bank[bk]])
                        P.op("act", lambda e, b=b, n=n: e.activation(out=sga[b][:, 0:n], in_=banks[0][:, 0:n], func=ACTF.Sigmoid), reads=[d_bank[0]], writes=[d_sga[b]])
                        P.op("act", lambda e, b=b, n=n: e.activation(out=sgb[b][:, 0:n], in_=banks[1][:, 0:n], func=ACTF.Sigmoid), reads=[d_bank[1]], writes=[d_sgb[b]])
                        P.op("dve", lambda e, b=b, n=n: e.tensor_tensor(out=t1[b][:, 0:n], in0=banks[2][:, 0:n], in1=sga[b][:, 0:n], op=ALU.mult),
                             reads=[d_bank[2], d_sga[b]], writes=[d_t1[b]])
                        P.op("dve", lambda e, b=b, n=n: e.tensor_tensor(out=t2[b][:, 0:n], in0=banks[3][:, 0:n], in1=sgb[b][:, 0:n], op=ALU.mult),
                             reads=[d_bank[3], d_sgb[b]], writes=[d_t2[b]])
                        P.op("pool", lambda e, b=b, fc=fc, n=n: e.tensor_tensor(out=mixT[:, fc, 0:n], in0=t1[b][:, 0:n], in1=t2[b][:, 0:n], op=ALU.add),
                             reads=[d_t1[b], d_t2[b]], writes=[d_mixT])
                    off = 0
                    for ti, (xsrc, nt_, gta, d_gta, x1dst) in enumerate(grp["tiles_a"](gtaS, d_gtaS)):
                        b = ti % 2
                        P.dma("sp", lambda e, b=b, xsrc=xsrc, nt_=nt_: e.dma_start(out=xa[b][0:nt_, :], in_=xsrc), writes=[d_xa[b]])
                        for half in range(2):
                            bk = 4 + half
                            for k in range(8):
                                P.op("pe", lambda e, bk=bk, k=k, off=off, nt_=nt_, half=half: e.matmul(
                                    out=banks[bk][0:nt_, :], lhsT=mixT[:, k, off:off + nt_], rhs=Wout[:, k, half * 512:(half + 1) * 512],
                                    start=(k == 0), stop=(k == 7)), reads=[d_mixT, d_Wout], writes=[d_bank[bk]])
                            P.op("dve", lambda e, bk=bk, b=b, nt_=nt_, half=half, gta=gta: e.tensor_tensor(
                                out=x1t[b][0:nt_, half * 512:(half + 1) * 512], in0=banks[bk][0:nt_, :], in1=gta[0:nt_, half * 512:(half + 1) * 512], op=ALU.mult),
                                reads=[d_bank[bk], d_gta], writes=[d_x1t[b]])
                        P.op("pool", lambda e, b=b, nt_=nt_: e.tensor_tensor(out=x1t[b][0:nt_, :], in0=x1t[b][0:nt_, :], in1=xa[b][0:nt_, :], op=ALU.add),
                             reads=[d_x1t[b], d_xa[b]], writes=[d_x1t[b]])
                        P.dma("sp", lambda e, b=b, nt_=nt_, x1dst=x1dst: e.dma_start(out=x1dst, in_=x1t[b][0:nt_, :]), reads=[d_x1t[b]])
                        off += nt_
                P.barrier()
            with ExitStack() as s3:
                Wup = P.sb([128, 8, 4096], BF16, s3); d_Wup = Dep()
                Wdn = P.sb([128, 32, 1024], BF16, s3); d_Wdn = Dep()
                for q4 in range(4):
                    P.dma("pool", lambda e, q4=q4: e.dma_start(out=Wup[:, :, q4 * 1024:(q4 + 1) * 1024],
                                                              in_=w_up[:, q4 * 1024:(q4 + 1) * 1024].rearrange("(c p) n -> p c n", p=128)), writes=[d_Wup])
                    P.dma("pool", lambda e, q4=q4: e.dma_start(out=Wdn[:, q4 * 8:(q4 + 1) * 8, :],
                                                              in_=w_down[q4 * 1024:(q4 + 1) * 1024, :].rearrange("(c p) n -> p c n", p=128)), writes=[d_Wdn])
                nfb = P.sb([128, D], F32, s3); d_nfb = Dep()
                P.dma("sp", lambda e: e.dma_start(out=nfb[:], in_=norm_final_d.partition_broadcast(128)[:, 0, :]), writes=[d_nfb])
                h2T4 = P.sb([128, 8, 256], BF16, s3); d_h2T4 = Dep()
                uT = P.sb([128, 32, 256], BF16, s3); d_uT = Dep()
                x1in = [P.sb([128, D], F32, s3) for _ in range(2)]; d_x1in = [Dep(), Dep()]
                rl = [P.sb([128, 256], BF16, s3) for _ in range(2)]; d_rl = [Dep(), Dep()]
                x2t_ = P.sb([128, D], F32, s3); x2t = [x2t_, x2t_]; d_x2t_ = Dep(); d_x2t = [d_x2t_, d_x2t_]
                ss2 = P.sb([128, 2], F32, s3); d_ss2 = Dep()
                wk3 = mk_wk(s3)
                modS3 = P.sb([SB, 3, D], BF16, s3); d_modS3 = Dep()
                P.dma("pool", lambda e: e.dma_start(out=modS3[:], in_=mods_d[:, 3 * D:6 * D].rearrange("s (a d) -> s a d", a=3)), writes=[d_modS3])
                P.dma("sp", lambda e: e.dma_start(out=x2t[1][0:SB, :], in_=norm_mlp.partition_broadcast(SB)[:, 0, :]), writes=[d_x2t[1]])
                P.op("dve", lambda e: e.scalar_tensor_tensor(out=modS3[:, 1, :], in0=modS3[:, 1, :], scalar=1.0, in1=x2t[1][0:SB, :], op0=ALU.add, op1=ALU.mult),
                     reads=[d_modS3, d_x2t[1]], writes=[d_modS3])
                for grp in groups:
                    tl_all = grp["tiles_b"](modS3, d_modS3)
                    for s0_ in range(0, len(tl_all), 2):
                        tl_b = tl_all[s0_:s0_ + 2]
                        n = sum(t_[1] for t_ in tl_b)
                        off = 0
                        for ti, (x1src, nt_, G2, d_G2, shf, d_shf, gtf, d_gtf, ydst, shc) in enumerate(tl_b):
                            P.dma("sp", lambda e, ti=ti, x1src=x1src, nt_=nt_: e.dma_start(out=x1in[ti][0:nt_, :], in_=x1src), writes=[d_x1in[ti]])
                            norm_mod_T(x1in[ti], d_x1in[ti], nt_, G2, d_G2, shf, d_shf, wk3, h2T4[:, :, off:off + nt_], d_h2T4, shc, d_shcol)
                            off += nt_
                        for fc in range(32):
                            b = fc % 2
                            bk = fc % 4
                            for k in range(8):
                                P.op("pe", lambda e, bk=bk, k=k, fc=fc, n=n: e.matmul(out=banks[bk][:, 0:n], lhsT=Wup[:, k, fc * 128:(fc + 1) * 128], rhs=h2T4[:, k, 0:n],
                                                                                    start=(k == 0), stop=(k == 7)), reads=[d_Wup, d_h2T4], writes=[d_bank[bk]])
                            P.op("act", lambda e, bk=bk, b=b, n=n: e.activation(out=rl[b][:, 0:n], in_=banks[bk][:, 0:n], func=ACTF.Relu), reads=[d_bank[bk]], writes=[d_rl[b]])
                            P.op("pool", lambda e, b=b, fc=fc, n=n: e.tensor_tensor(out=uT[:, fc, 0:n], in0=rl[b][:, 0:n], in1=rl[b][:, 0:n], op=ALU.mult),
                                 reads=[d_rl[b]], writes=[d_uT])
                        off = 0
                        for ti, (x1src, nt_, G2, d_G2, shf, d_shf, gtf, d_gtf, ydst, shc) in enumerate(tl_b):
                            b = ti % 2
                            for half in range(2):
                                bk = 4 + half
                                for k in range(32):
                                    P.op("pe", lambda e, bk=bk, k=k, off=off, nt_=nt_, half=half: e.matmul(
                                        out=banks[bk][0:nt_, :], lhsT=uT[:, k, off:off + nt_], rhs=Wdn[:, k, half * 512:(half + 1) * 512],
                                        start=(k == 0), stop=(k == 31)), reads=[d_uT, d_Wdn], writes=[d_bank[bk]])
                                P.op("dve", lambda e, bk=bk, b=b, nt_=nt_, half=half, gtf=gtf: e.tensor_tensor(
                                    out=x2t[b][0:nt_, half * 512:(half + 1) * 512], in0=banks[bk][0:nt_, :], in1=gtf[0:nt_, half * 512:(half + 1) * 512], op=ALU.mult),
                                    reads=[d_bank[bk], d_gtf], writes=[d_x2t[b]])
                            P.op("pool", lambda e, b=b, ti=ti, nt_=nt_: e.tensor_tensor(out=x2t[b][0:nt_, :], in0=x2t[b][0:nt_, :], in1=x1in[ti][0:nt_, :], op=ALU.add),
                                 reads=[d_x2t[b], d_x1in[ti]], writes=[d_x2t[b]])
                            P.op("act", lambda e, b=b, nt_=nt_: e.activation(out=junk_sh[0:nt_, :], in_=x2t[b][0:nt_, :], func=ACTF.Square, accum_out=ss2[0:nt_, 0:1]),
                                 reads=[d_x2t[b]], writes=[d_junk_sh, d_ss2])
                            P.op("act", lambda e, nt_=nt_: e.activation(out=ss2[0:nt_, 1:2], in_=ss2[0:nt_, 0:1], func=ACTF.Sqrt, bias=epsT[0:nt_, :], scale=1.0 / D),
                                 reads=[d_ss2, d_eps], writes=[d_ss2])
                            P.op("dve", lambda e, nt_=nt_: e.reciprocal(out=ss2[0:nt_, 1:2], in_=ss2[0:nt_, 1:2]), reads=[d_ss2], writes=[d_ss2])
                            P.op("dve", lambda e, b=b, nt_=nt_: e.scalar_tensor_tensor(out=x2t[b][0:nt_, :], in0=x2t[b][0:nt_, :], scalar=ss2[0:nt_, 1:2], in1=nfb[0:nt_, :],
                                                                                     op0=ALU.mult, op1=ALU.mult), reads=[d_x2t[b], d_ss2, d_nfb], writes=[d_x2t[b]])
                            P.dma("sp", lambda e, b=b, nt_=nt_, ydst=ydst: e.dma_start(out=ydst, in_=x2t[b][0:nt_, :]), reads=[d_x2t[b]])
                            off += nt_
                P.barrier()

        with ExitStack() as sA:
            WA = P.sb([128, 8, 2304], BF16, sA); d_WA = Dep()
            segs = [(C_KVC, 128, 0), (C_KVS, 128, 128), (C_KVW, 128, 256), (C_KVC + 128, 128, 384),
                    (C_KVS + 128, 128, 512), (C_KVW + 128, 128, 640), (C_RK, 512, 768), (C_RV, 1024, 1280)]
            for (src, n, dst) in segs:
                P.dma("pool", lambda e, src=src, n=n, dst=dst: e.dma_start(
                    out=WA[:, :, dst:dst + n], in_=w_in[:, src:src + n].rearrange("(c p) n -> p c n", p=128)), writes=[d_WA])
            inda = P.sb([128, 8], F32, sA); indab = P.sb([128, 8], BF16, sA); d_inda = Dep()
            wrep = P.sb([128, 512], F32, sA); d_wrep = Dep()
            bcol = P.sb([128, 2], F32, sA); d_bcol = Dep()
            kdsc = P.sb([128, 4], F32, sA); d_kdsc = Dep()
            oh = P.sb([128, 8], F32, sA); d_oh = Dep()
            P.dma("sp", lambda e: e.dma_start(out=inda[:], in_=inda_d), writes=[d_inda])
            P.op("dve", lambda e: e.tensor_copy(out=indab[:], in_=inda[:]), reads=[d_inda], writes=[d_inda])
            P.dma("sp", lambda e: e.dma_start(out=wrep[:], in_=wrep_d), writes=[d_wrep])
            P.dma("sp", lambda e: e.dma_start(out=bcol[:], in_=bcol_d), writes=[d_bcol])
            P.dma("sp", lambda e: e.dma_start(out=kdsc[:], in_=kdsc_d), writes=[d_kdsc])
            P.dma("sp", lambda e: e.dma_start(out=oh[:], in_=oh_d), writes=[d_oh])
            FS = P.sb([128, 4, 1024], F32, sA); d_FS = Dep()
            S = P.sb([128, 4, 256], F32, sA); d_S = Dep()
            Sown = P.sb([128, 1024], F32, sA); d_Sown = Dep()
            tmpS = P.sb([128, 1024], F32, sA); d_tmpS = Dep()
            P.op("pool", lambda e: e.memset(S[:].rearrange("p a b -> p (a b)"), 0.0), writes=[d_S])
            NBUF = 2
            xts = [P.sb([128, D], F32, sA) for _ in range(NBUF)]; d_xts = [Dep() for _ in range(NBUF)]
            rps = [P.sb([128, 192], F32, sA) for _ in range(NBUF)]; d_rps = [Dep() for _ in range(NBUF)]
            wks = [mk_wk(sA) for _ in range(NBUF)]
            hTs = [P.sb([128, 8, 128], BF16, sA) for _ in range(NBUF)]; d_hTs = [Dep() for _ in range(NBUF)]
            kvos = [P.sb([128, 3, 256], F32, sA) for _ in range(NBUF)]; d_kvos = [Dep() for _ in range(NBUF)]
            kraw = [P.sb([128, 896], F32, sA) for _ in range(NBUF)]; d_kraw = [Dep() for _ in range(NBUF)]
            tm = [[P.sb([128, 256], F32, sA) for _ in range(4)] for _ in range(NBUF)]
            d_tm = [[Dep() for _ in range(4)] for _ in range(NBUF)]
            rkr = [P.sb([128, 4, 128], F32, sA) for _ in range(NBUF)]; d_rkr = [Dep() for _ in range(NBUF)]
            kd = [P.sb([128, 4, 128], BF16, sA) for _ in range(NBUF)]; d_kd = [Dep() for _ in range(NBUF)]
            vb = [P.sb([128, 1024], BF16, sA) for _ in range(NBUF)]; d_vb = [Dep() for _ in range(NBUF)]
            cw = [P.sb([128, 2, 256], BF16, sA) for _ in range(NBUF)]; d_cw = [Dep() for _ in range(NBUF)]
            kwt = [P.sb([128, 128], BF16, sA) for _ in range(NBUF)]; d_kwt = [Dep() for _ in range(NBUF)]
            kst = [P.sb([128, 128], BF16, sA) for _ in range(NBUF)]; d_kst = [Dep() for _ in range(NBUF)]
            vwb = [P.sb([128, 2, 65], BF16, sA) for _ in range(NBUF)]; d_vwb = [Dep() for _ in range(NBUF)]
            vsb = [P.sb([128, 2, 65], BF16, sA) for _ in range(NBUF)]; d_vsb = [Dep() for _ in range(NBUF)]
            for b_ in range(NBUF):
                P.op("pool", lambda e, b_=b_: e.memset(vwb[b_][:].rearrange("p a b -> p (a b)"), 1.0), writes=[d_vwb[b_]])
                P.op("pool", lambda e, b_=b_: e.memset(vsb[b_][:].rearrange("p a b -> p (a b)"), 1.0), writes=[d_vsb[b_]])
            zA, zB, zK, zV0, zV1, zC = banks
            dzA, dzB, dzK, dzV0, dzV1, dzC = d_bank
            d_zBt = dzB

            ntA = NT if stage >= 1 else 0
            if 'K_NTA' in os.environ:
                ntA = int(os.environ['K_NTA'])
            if stage == 2:
                ntA = 8 * DBG_NBLK
                P.op("pool", lambda e: e.memset(FS[:].rearrange("p a b -> p (a b)"), 0.0), writes=[d_FS])
            for tt in range(ntA):
                b = tt % NBUF
                xt, d_xt, rp, d_rp, hT, d_hT, kvo, d_kvo = xts[b], d_xts[b], rps[b], d_rps[b], hTs[b], d_hTs[b], kvos[b], d_kvos[b]
                P.dma("sp", lambda e, tt=tt, xt=xt: e.dma_start(out=xt[:], in_=x_all[tt * 128:(tt + 1) * 128, :]), writes=[d_xt])
                P.dma("sp", lambda e, tt=tt, rp=rp: e.dma_start(out=rp[:], in_=rope_all[tt * 128:(tt + 1) * 128, :]), writes=[d_rp])
                norm_mod_T(xt, d_xt, 128, G1P, d_G1P, modP4[:, 0, :], d_modP, wks[b], hT, d_hT, shcol[:, 0, :], d_shcol)
                for (zz, dz, c0, n) in ((zA, dzA, 0, 512), (zB, dzB, 512, 256), (zK, dzK, 768, 512), (zV0, dzV0, 1280, 512), (zV1, dzV1, 1792, 512)):
                    for k in range(8):
                        P.op("pe", lambda e, zz=zz, k=k, c0=c0, n=n, hT=hT: e.matmul(out=zz[:, 0:n], lhsT=hT[:, k, :], rhs=WA[:, k, c0:c0 + n],
                                                                                    start=(k == 0), stop=(k == 7)),
                             reads=[d_hT, d_WA], writes=[dz])
                P.op("act", lambda e, b=b: e.copy(out=kraw[b][:, 0:384], in_=zA[:, 0:384]), reads=[dzA], writes=[d_kraw[b]])
                P.op("act", lambda e, b=b: e.copy(out=kraw[b][:, 384:896], in_=zK[:, 0:512]), reads=[dzK], writes=[d_kraw[b]])
                src = kraw[b][:, 0:384].rearrange("p (j h two e) -> p j h two e", j=3, h=2, two=2)
                dst = kvo[:, :, 0:128].rearrange("p j (h two e) -> p j h two e", h=2, two=2)
                shp = [128, 3, 2, 32]
                tv = [t[:, 0:192].rearrange("p (j h e) -> p j h e", j=3, h=2) for t in tm[b]]
                cosn = rp[:, 0:32].unsqueeze(1).unsqueeze(1).to_broadcast(shp)
                sinn = rp[:, 32:64].unsqueeze(1).unsqueeze(1).to_broadcast(shp)
                rope(lambda hf, src=src: src[:, :, :, hf, :], lambda hf, dst=dst: dst[:, :, :, hf, :], cosn, sinn, shp, tv,
                     d_kraw[b], d_kvo, d_rp, d_tm[b])
                P.op("act", lambda e, kvo=kvo: e.copy(out=kvo[:, 0, 128:256], in_=zA[:, 384:512]), reads=[dzA], writes=[d_kvo])
                P.op("act", lambda e, kvo=kvo: e.copy(out=kvo[:, 1:3, 128:256], in_=zB[:, 0:256].rearrange("p (j f) -> p j f", j=2)),
                     reads=[dzB], writes=[d_kvo])
                P.dma("sp", lambda e, tt=tt, kvo=kvo: e.dma_start(out=o_cmp[tt * 128:(tt + 1) * 128, :], in_=kvo[:, 0, :]), reads=[d_kvo])
                P.dma("sp", lambda e, tt=tt, kvo=kvo: e.dma_start(out=o_sel[tt * 128:(tt + 1) * 128, :], in_=kvo[:, 1, :]), reads=[d_kvo])
                if tt >= NT - 4:
                    P.dma("sp", lambda e, tt=tt, kvo=kvo: e.dma_start(out=o_win[(tt - NT + 4) * 128:(tt - NT + 5) * 128, :], in_=kvo[:, 2, :]),
                          reads=[d_kvo])
                P.op("pool", lambda e, kvo=kvo, b=b: e.tensor_copy(out=vsb[b][:, :, 0:64], in_=kvo[:, 1, 128:256].rearrange("p (h d) -> p h d", h=2)),
                     reads=[d_kvo], writes=[d_vsb[b]])
                P.dma("sp", lambda e, tt=tt, b=b: e.dma_start(out=vs_d[tt * 128:(tt + 1) * 128, :], in_=vsb[b][:].rearrange("p a b -> p (a b)")),
                      reads=[d_vsb[b]])
                P.op("pool", lambda e, kvo=kvo, b=b: e.tensor_copy(out=vwb[b][:, :, 0:64], in_=kvo[:, 2, 128:256].rearrange("p (h d) -> p h d", h=2)),
                     reads=[d_kvo], writes=[d_vwb[b]])
                P.dma("sp", lambda e, tt=tt, b=b: e.dma_start(out=vw_d[tt * 128:(tt + 1) * 128, :], in_=vwb[b][:].rearrange("p a b -> p (a b)")),
                      reads=[d_vwb[b]])
                P.op("pe", lambda e, kvo=kvo: e.transpose(out=zB[:, 256:384], in_=kvo[:, 1, 0:128], identity=ident[:]),
                     reads=[d_kvo, d_ident], writes=[d_zBt])
                P.op("pe", lambda e, kvo=kvo: e.transpose(out=zB[:, 384:512], in_=kvo[:, 2, 0:128], identity=ident[:]),
                     reads=[d_kvo, d_ident], writes=[d_zBt])
                P.op("act", lambda e, b=b: e.copy(out=kst[b][:], in_=zB[:, 256:384]), reads=[d_zBt], writes=[d_kst[b]])
                P.dma("sp", lambda e, tt=tt, b=b: e.dma_start(out=ksT_d[:, tt * 128:(tt + 1) * 128], in_=kst[b][:]), reads=[d_kst[b]])
                P.op("act", lambda e, b=b: e.copy(out=kwt[b][:], in_=zB[:, 384:512]), reads=[d_zBt], writes=[d_kwt[b]])
                P.dma("sp", lambda e, tt=tt, b=b: e.dma_start(out=kwT_d[:, tt * 128:(tt + 1) * 128], in_=kwt[b][:]), reads=[d_kwt[b]])
                for r in range(2):
                    P.op("pool", lambda e, r=r, b=b, kvo=kvo: e.tensor_tensor(out=cw[b][:, r, :], in0=kvo[:, 0, :], in1=wrep[:, r * 256:(r + 1) * 256],
                                                                             op=ALU.mult), reads=[d_kvo, d_wrep], writes=[d_cw[b]])
                for r in range(2):
                    for kv in range(2):
                        i4 = r * 2 + kv
                        P.op("pe", lambda e, r=r, kv=kv, i4=i4, b=b: e.matmul(out=zC[:, i4 * 8:(i4 + 1) * 8], lhsT=cw[b][:, r, kv * 128:(kv + 1) * 128],
                                                                             rhs=indab[:], start=True, stop=True),
                             reads=[d_cw[b], d_inda], writes=[dzC])
                P.op("act", lambda e, tt=tt: e.copy(out=FS[:, :, tt * 8:(tt + 1) * 8], in_=zC[:, 0:32].rearrange("p (a s) -> p a s", a=4)),
                     reads=[dzC], writes=[d_FS])
                srck = kraw[b][:, 384:896].rearrange("p (h two e) -> p h two e", h=4, two=2)
                dstk = rkr[b][:].rearrange("p h (two e) -> p h two e", two=2)
                shpk = [128, 4, 64]
                tvk = [t[:, 0:256].rearrange("p (h e) -> p h e", h=4) for t in tm[b]]
                cosk = rp[:, 64:128].unsqueeze(1).to_broadcast(shpk)
                sink = rp[:, 128:192].unsqueeze(1).to_broadcast(shpk)
                rope(lambda hf, srck=srck: srck[:, :, hf, :], lambda hf, dstk=dstk: dstk[:, :, hf, :], cosk, sink, shpk, tvk,
                     d_kraw[b], d_rkr[b], d_rp, d_tm[b])
                P.op("pool", lambda e, b=b: e.tensor_tensor(out=kd[b][:], in0=rkr[b][:], in1=kdsc[:].unsqueeze(2).to_broadcast([128, 4, 128]),
                                                           op=ALU.mult), reads=[d_rkr[b], d_kdsc], writes=[d_kd[b]])
                P.op("act", lambda e, b=b: e.copy(out=vb[b][:, 0:512], in_=zV0[:, :]), reads=[dzV0], writes=[d_vb[b]])
                P.op("act", lambda e, b=b: e.copy(out=vb[b][:, 512:1024], in_=zV1[:, :]), reads=[dzV1], writes=[d_vb[b]])
                j = tt % 8
                if j == 0:
                    P.op("dve", lambda e: e.tensor_scalar(out=Sown[:], in0=S[:].rearrange("p a b -> p (a b)"), scalar1=oh[:, 0:1], scalar2=None,
                                                          op0=ALU.mult), reads=[d_S, d_oh], writes=[d_Sown])
                else:
                    P.op("dve", lambda e, j=j: e.scalar_tensor_tensor(out=Sown[:], in0=S[:].rearrange("p a b -> p (a b)"), scalar=oh[:, j:j + 1],
                                                                     in1=Sown[:], op0=ALU.mult, op1=ALU.add),
                         reads=[d_S, d_oh, d_Sown], writes=[d_Sown])
                if j == 7:
                    P.dma("sp", lambda e, i=tt // 8: e.dma_start(out=sown_d[i], in_=Sown[:]), reads=[d_Sown])
                for h in range(4):
                    zz, dz = (zV0, dzV0) if h < 2 else (zV1, dzV1)
                    c0 = (h % 2) * 256
                    P.op("pe", lambda e, h=h, zz=zz, c0=c0, b=b: e.matmul(out=zz[:, c0:c0 + 256], lhsT=kd[b][:, h, :], rhs=vb[b][:, h * 256:(h + 1) * 256],
                                                                         start=True, stop=True),
                         reads=[d_kd[b], d_vb[b]], writes=[dz])
                    P.op("dve", lambda e, h=h, zz=zz, c0=c0: e.scalar_tensor_tensor(out=S[:, h, :], in0=S[:, h, :], scalar=float(math.exp(128 * LG[h])),
                                                                                   in1=zz[:, c0:c0 + 256], op0=ALU.mult, op1=ALU.add),
                         reads=[d_S, dz], writes=[d_S])
            if ntA:
                P.dma("sp", lambda e: e.dma_start(out=o_state.rearrange("h k v -> k h v"), in_=S[:]), reads=[d_S])
                ctmp, cv = tmpS, Sown
                P.op("pool", lambda e: e.memset(cmpKT[:], 0.0), writes=[d_cmpKT])
                P.op("pool", lambda e: e.memset(cv[:], 0.0), reads=[], writes=[d_Sown])
                P.op("dve", lambda e: e.tensor_tensor(out=ctmp[:, 0:1023], in0=FS[:, 0, 0:1023], in1=FS[:, 2, 1:1024], op=ALU.add),
                     reads=[d_FS], writes=[d_tmpS])
                P.op("dve", lambda e: e.tensor_scalar(out=cmpKT[:, 0:1023], in0=ctmp[:, 0:1023], scalar1=bcol[:, 0:1], scalar2=None, op0=ALU.add),
                     reads=[d_tmpS, d_bcol], writes=[d_cmpKT])
                P.op("dve", lambda e: e.tensor_tensor(out=ctmp[:, 0:1023], in0=FS[:, 1, 0:1023], in1=FS[:, 3, 1:1024], op=ALU.add),
                     reads=[d_FS], writes=[d_tmpS])
                P.op("dve", lambda e: e.tensor_scalar(out=cv[:, 0:1023], in0=ctmp[:, 0:1023], scalar1=bcol[:, 1:2], scalar2=None, op0=ALU.add),
                     reads=[d_tmpS, d_bcol], writes=[d_Sown])
                for jt in range(8):
                    P.op("pe", lambda e, jt=jt: e.transpose(out=pT[:, jt * 128:(jt + 1) * 128], in_=cv[:, jt * 128:(jt + 1) * 128], identity=ident[:]),
                         reads=[d_Sown, d_ident], writes=[d_pTs[jt // 4]])
                P.op("act", lambda e: e.copy(out=cmpVA[:, :, :, 0:64], in_=pT[:].rearrange("p (j h d) -> p j h d", j=8, h=2)),
                     reads=d_pTs, writes=[d_cmpVA])
            P.barrier()

        if stage >= 2:
            phase_b1()
        if stage >= 4:
            phase_s1()
        if stage >= 3:
            groups = []
            for gq in range(4):
                def load_a(hT4, d_hT4, onT4, d_onT4, yrT4, d_yrT4, gq=gq):
                    for j in range(4):
                        i_ = 4 * gq + j
                        P.dma("sp", lambda e, i_=i_, j=j: e.dma_start(out=hT4[:, :, j * 128:(j + 1) * 128], in_=hT_d[i_]), writes=[d_hT4])
                        P.dma("sp", lambda e, i_=i_, j=j: e.dma_start(out=onT4[:, :, j * 128:(j + 1) * 128], in_=onT_d[i_]), writes=[d_onT4])
                        P.dma("sp", lambda e, i_=i_, j=j: e.dma_start(out=yrT4[:, :, j * 128:(j + 1) * 128], in_=yrT_d[i_]), writes=[d_yrT4])

                def tiles_a(gtaS, d_gtaS, gq=gq):
                    return [(x_own[(4 * gq + j) * 128:(4 * gq + j + 1) * 128, :], 128, modP4[:, 1, :], d_modP,
                             x1_d[(4 * gq + j) * 128:(4 * gq + j + 1) * 128, :]) for j in range(4)]

                def tiles_b(modS3, d_modS3, gq=gq):
                    return [(x1_d[(4 * gq + j) * 128:(4 * gq + j + 1) * 128, :], 128, G2P, d_G2P, modP4[:, 2, :], d_modP, modP4[:, 3, :], d_modP,
                             y_own[(4 * gq + j) * 128:(4 * gq + j + 1) * 128, :], shcol[:, 1, :]) for j in range(4)]
                groups.append({"ntok": 512, "load_a": load_a, "tiles_a": tiles_a, "tiles_b": tiles_b})
            if stage >= 4:
                def load_as(hT4, d_hT4, onT4, d_onT4, yrT4, d_yrT4):
                    P.dma("sp", lambda e: e.dma_start(out=hT4[:, :, 0:SB], in_=hTs_d), writes=[d_hT4])
                    P.dma("sp", lambda e: e.dma_start(out=onT4[:, :, 0:SB], in_=onTs_d), writes=[d_onT4])
                    P.dma("sp", lambda e: e.dma_start(out=yrT4[:, :, 0:SB], in_=yrTs_d), writes=[d_yrT4])
                groups.append({"ntok": SB, "load_a": load_as,
                               "tiles_a": lambda gtaS, d_gtaS: [(x_smp, SB, gtaS[:], d_gtaS, x1s_d)],
                               "tiles_b": lambda modS3, d_modS3: [(x1s_d, SB, modS3[:, 1, :], d_modS3, modS3[:, 0, :], d_modS3, modS3[:, 2, :], d_modS3, y_smp, None)]})
            phase_b2(groups)

        P.barrier()
        P.emit()
    return nc


def _rope_tables(pos):
    pos = np.asarray(pos, np.float32)
    out = np.zeros((len(pos), 192), np.float32)
    inv32 = (10000.0 ** (-np.arange(0, 64, 2, dtype=np.float32) / 64)).astype(np.float32)
    inv64 = (10000.0 ** (-np.arange(0, 128, 2, dtype=np.float32) / 128)).astype(np.float32)
    a32 = pos[:, None] * inv32[None, :]
    a64 = pos[:, None] * inv64[None, :]
    out[:, 0:32] = np.cos(a32)
    out[:, 32:64] = np.sin(a32)
    out[:, 64:128] = np.cos(a64)
    out[:, 128:192] = np.sin(a64)
    return out


def _consts(core):
    cs = {}
    cs["ident"] = np.eye(128, dtype=np.float32)
    sel5 = np.zeros((5, 132), np.float32)
    sel5[0, 0:128] = 1.0
    for s in range(4):
        sel5[1 + s, 128 + s] = 1.0
    cs["sel5"] = sel5
    inda = np.zeros((128, 8), np.float32)
    inda[np.arange(128), np.arange(128) // 16] = 1.0
    cs["inda"] = inda
    i = np.arange(128, dtype=np.float64)
    kdsc = np.zeros((128, 4), np.float32)
    for h in range(4):
        kdsc[:, h] = RET_SCALE * np.exp((127.0 - i) * LG[h])
    cs["kdsc"] = kdsc
    oh = np.zeros((128, 8), np.float32)
    oh[:, core] = 1.0
    cs["oh"] = oh
    return cs


def _core_tables(c):
    q = np.arange(128)
    n = np.arange(1024)
    j = np.arange(256)
    k = np.arange(128)
    cmask = np.zeros((NB, 128, 1024), np.float32)
    forceb = np.zeros((NB, 128, 256), np.float32)
    for i in range(NB):
        t = 8 * i + c
        qpos = 128 * t + q
        ok = (16 * n[None, :] + 31 <= qpos[:, None]) & (n[None, :] <= 1022)
        cmask[i] = np.where(ok, 0.0, NEG)
        valid = 64 * j[None, :] <= qpos[:, None]
        qb = qpos[:, None] // 64
        forced = (j[None, :] == 0) | (j[None, :] == qb) | (j[None, :] == qb - 1)
        forceb[i] = np.where(valid, np.where(forced, 1.0e4, 0.0), -1.0e30)
    dmask = np.zeros((128, 8, 128), np.float32)
    for jj in range(8):
        if jj == c:
            dmask[:, jj, :] = np.where(k[None, :] <= q[:, None], 0.0, NEG)
        elif jj > c:
            dmask[:, jj, :] = NEG
    wmask = np.zeros((128, 24, 128), np.float32)
    for v in range(2):
        for m in range(12):
            diff = 128 * (m - 4 - c) + k[None, :] - q[:, None]
            ok = (diff <= 0) & (diff > -512)
            if v == 0 and m < 4:
                ok = np.zeros_like(ok)
            wmask[:, v * 12 + m, :] = np.where(ok, 0.0, NEG)
    return {"cmask": cmask, "forceb": forceb, "dmask": dmask.reshape(128, 1024), "wmask": wmask.reshape(128, 24 * 128)}


def _shared_tables():
    nl = np.arange(128)
    blk = np.arange(256)
    ovp = np.zeros((128, 8, 256), np.float32)
    for jt in range(8):
        n = 128 * jt + nl
        ovp[:, jt, :] = ((4 * blk[None, :] - 1 <= n[:, None]) & (n[:, None] <= 4 * blk[None, :] + 3)).astype(np.float32)
    i = np.arange(128)
    dtab = np.zeros((128, 4, 128), np.float32)
    qdt = np.zeros((128, 4, 128), np.float32)
    for h in range(4):
        diff = i[None, :] - i[:, None]
        dtab[:, h, :] = np.where(diff >= 0, np.exp(np.maximum(diff, 0) * LG[h]), 0.0)
        qdt[:, h, :] = np.exp((i[None, :] + 1.0) * LG[h])
    p = np.arange(128)
    ovs = np.zeros((128, 8, 257), np.float32)
    for s_ in range(8):
        n = 8 * p + s_
        ovs[:, s_, 0:256] = ((4 * blk[None, :] - 1 <= n[:, None]) & (n[:, None] <= 4 * blk[None, :] + 3)).astype(np.float32)
    ovs[:, :, 256] = 1.0
    smallc = np.zeros((1, 1024), np.float32)
    smallc[0, 127] = NEG
    smallc[0, 128] = NEG
    for b in range(4):
        for s_ in range(4):
            smallc[0, 256 + 4 * b + s_] = 0.0 if s_ == b else NEG
    smallc[0, 272:276] = 1.0
    smallc[0, 276 + 0] = 1.0e4
    smallc[0, 276 + 255] = 1.0e4
    ohm = np.zeros((128, 4, 4), np.float32)
    for b in range(4):
        ohm[:, b, b] = 1.0
    return {"ovp": ovp.reshape(128, 2048), "dtab": dtab.reshape(128, 512), "qdt": qdt.reshape(128, 512),
            "ovs": ovs.reshape(128, 8 * 257), "smallc": smallc, "ohm": ohm.reshape(128, 16), "oh4": np.eye(4, dtype=np.float32),
            "rope_smp": _rope_tables(np.full(4, P_PAST))}


def _prep(inputs, stage=99):
    f = lambda a: np.ascontiguousarray(np.asarray(a, dtype=np.float32))
    x_all = f(inputs["x_prompt"]).reshape(T, D)
    w_cmp = f(inputs["w_cmp"])[0]
    b_cmp = f(inputs["b_cmp"])[0]
    wrep = np.zeros((128, 512), np.float32)
    pm = np.arange(128) % 16
    for r in range(2):
        blk = w_cmp[:, :, r * 16 + pm, :]
        wrep[:, r * 256:(r + 1) * 256] = blk.transpose(2, 0, 1, 3).reshape(128, 256)
    bcol = np.ascontiguousarray(b_cmp.reshape(2, 128).T)
    rope_all = _rope_tables(np.arange(T))
    shared = {
        "x_all": x_all, "rope_all": rope_all,
        "w_ada": f(inputs["w_ada"])[0], "b_ada": f(inputs["b_ada"]).reshape(1, 6 * D),
        "w_in": f(inputs["w_in"])[0],
        "norm_mix": f(inputs["norm_mix"]).reshape(1, D), "norm_mlp": f(inputs["norm_mlp"]).reshape(1, D),
        "wrep": wrep, "bcol": bcol,
        "w_bn": f(inputs["w_branch_nsa"])[0], "w_br": f(inputs["w_branch_ret"])[0], "w_out": f(inputs["w_out"])[0],
        "w_up": f(inputs["w_up"])[0], "w_down": f(inputs["w_down"])[0], "norm_final": f(inputs["norm_final"]).reshape(1, D),
        "gnw": f(inputs["ret_gn_w"]).reshape(1, D), "gnb": f(inputs["ret_gn_b"]).reshape(1, D),
    }
    shared.update(_shared_tables())
    shared["wrow"] = np.ascontiguousarray(w_cmp.reshape(2, 2, 2, 16, 64).transpose(2, 3, 0, 1, 4).reshape(1, 2 * 16 * 256))
    shared["brow"] = np.ascontiguousarray(b_cmp.reshape(1, 256))
    shared["cache_cmp"] = f(inputs["cache_cmp_kv"]).reshape(5120 * 8, 4096)
    shared["cache_sel"] = f(inputs["cache_sel_kv"]).reshape(5120 * 8, 4096)
    ptab_all = np.ascontiguousarray(np.asarray(inputs["page_table"], dtype=np.int32))
    cwin = f(inputs["cache_win_kv"])[0].reshape(32, 512, 256)
    sret = f(inputs["state_ret"])[0]
    maps = []
    cp = f(inputs["c_prompt"]).reshape(1, D)
    cs_ = f(inputs["c_sample"])
    for c in range(NCORE):
        m = dict(shared)
        m.update(_consts(c))
        m.update(_core_tables(c))
        tiles = [8 * i + c for i in range(NB)]
        m["x_own"] = np.ascontiguousarray(x_all.reshape(NT, 128, D)[tiles].reshape(NB * 128, D))
        m["rope_own"] = np.ascontiguousarray(rope_all.reshape(NT, 128, 192)[tiles].reshape(NB * 128, 192))
        m["ptab"] = ptab_all[4 * c:4 * c + 4]
        m["cache_win"] = cwin[4 * c:4 * c + 4]
        m["state_s"] = sret[4 * c:4 * c + 4]
        m["x_smp"] = f(inputs["x_sample"]).reshape(32, D)[4 * c:4 * c + 4]
        m["c_all"] = np.ascontiguousarray(np.concatenate([cp, cs_[4 * c:4 * c + 4]], axis=0).reshape(5, 8, 128).transpose(2, 1, 0).reshape(128, 40))
        maps.append(m)
    return maps


_NC_CACHE = {}


def run(inputs, stage=99):
    if stage not in _NC_CACHE:
        _NC_CACHE[stage] = build(stage)
    nc = _NC_CACHE[stage]
    maps = _prep(inputs, stage)
    res = run_bass_kernel_spmd(nc, maps, core_ids=list(range(NCORE)))
    return res.results


def kernel(**inputs):
    r = run(inputs)
    r0 = r[0]
    y = np.zeros((NT, 128, D), np.float32)
    for c in range(NCORE):
        yo = np.asarray(r[c]["y_own"]).reshape(NB, 128, D)
        for i in range(NB):
            y[8 * i + c] = yo[i]
    y_prompt = y.reshape(1, T, D)
    y_sample = np.concatenate([np.asarray(r[c]["y_smp"]) for c in range(NCORE)], axis=0).reshape(32, 1, D)
    new_cmp_p = np.asarray(r0["o_cmp"]).reshape(1, 1, T, 2, 2, 64)
    new_sel_p = np.asarray(r0["o_sel"]).reshape(1, 1, T, 2, 2, 64)
    new_win_p = np.asarray(r0["o_win"]).reshape(1, 1, 512, 2, 2, 64)
    new_state_p = np.asarray(r0["o_state"]).reshape(1, 1, 4, 128, 256)
    cat = lambda k: np.concatenate([np.asarray(r[c][k]) for c in range(NCORE)], axis=0)
    new_cmp_s = cat("o_cmp_s").reshape(1, 32, 1, 2, 2, 64)
    new_sel_s = cat("o_sel_s").reshape(1, 32, 1, 2, 2, 64)
    new_win_s = cat("o_win_s").reshape(1, 32, 512, 2, 2, 64)
    new_state_s = cat("o_state_s").reshape(1, 32, 4, 128, 256)
    return (y_prompt, y_sample, new_cmp_p, new_sel_p, new_win_p, new_state_p, new_cmp_s, new_sel_s, new_win_s, new_state_s)
```

```python
import math
import os
from contextlib import ExitStack

import numpy as np
import ml_dtypes

import concourse.bass as bass
import concourse.mybir as mybir
from concourse.bass_utils import run_bass_kernel_spmd

F32 = mybir.dt.float32
BF16 = mybir.dt.bfloat16
I32 = mybir.dt.int32
ALU = mybir.AluOpType
ACTF = mybir.ActivationFunctionType
AX = mybir.AxisListType

ENGS = ("pe", "act", "dve", "pool", "sp")
NSLOT = 12

D = 1024
T = 16384
NT = T // 128
NB = 16
NCORE = 8
SB = 4
P_PAST = 16384
W_IN = 6424
C_Q, C_KVC, C_KVS, C_KVW, C_G, C_RQ, C_RK, C_RV, C_RG, C_GA, C_GB = (
    0, 512, 768, 1024, 1280, 1304, 1816, 2328, 3352, 4376, 5400)
NEG = -30000.0
RMS_EPS = 1e-6
GN_EPS = 1e-5
LG = [math.log1p(-2.0 ** (-5.0 - h)) for h in range(4)]
RET_SCALE = 128 ** -0.5


class Dep:
    __slots__ = ("w", "r", "excl")

    def __init__(self, excl=False):
        self.w = None
        self.r = []
        self.excl = excl


class Prog:
    def __init__(self, nc, stack):
        self.nc = nc
        self.stack = stack
        self.ops = {e: [] for e in ENGS}
        self.cnt = {e: 0 for e in ENGS}
        self.esem = {e: stack.enter_context(nc.semaphore("es_" + e)) for e in ENGS}
        self.dq = {}
        for q in ("sp", "act", "pool"):
            sl = [stack.enter_context(nc.semaphore("ds_%s_%d" % (q, i))) for i in range(NSLOT)]
            self.dq[q] = {"sems": sl, "cnt": [0] * NSLOT, "next": 0}
        self.waited = {e: {} for e in ENGS}
        self.nuid = 0

    def sb(self, shape, dt=F32, stack=None):
        self.nuid += 1
        return (stack or self.stack).enter_context(self.nc.sbuf_tensor("sb%d" % self.nuid, list(shape), dt))

    def ps(self, shape, dt=F32, stack=None):
        self.nuid += 1
        return (stack or self.stack).enter_context(self.nc.psum_tensor("ps%d" % self.nuid, list(shape), dt))

    def _need(self, eng, tok, waits):
        if tok is None:
            return
        kind, key, val = tok
        if kind == "e" and key == eng and eng == "pe":
            return
        k = (kind, key)
        if self.waited[eng].get(k, 0) >= val:
            return
        self.waited[eng][k] = val
        waits[k] = max(waits.get(k, 0), val)

    def _gather(self, eng, reads, writes):
        waits = {}
        for d in reads:
            self._need(eng, d.w, waits)
            if d.excl:
                for t in d.r:
                    if not (t[0] == "e" and t[1] == eng):
                        self._need(eng, t, waits)
        for d in writes:
            self._need(eng, d.w, waits)
            for t in d.r:
                self._need(eng, t, waits)
        return waits

    def _commit(self, tok, reads, writes):
        for d in reads:
            if tok[0] == "e":
                d.r = [t for t in d.r if not (t[0] == "e" and t[1] == tok[1])]
            d.r.append(tok)
        for d in writes:
            d.w = tok
            d.r = []

    def _semof(self, k):
        kind, key = k
        if kind == "e":
            return self.esem[key]
        q, i = key
        return self.dq[q]["sems"][i]

    def op(self, eng, fn, reads=(), writes=()):
        waits = self._gather(eng, reads, writes)
        self.cnt[eng] += 1
        tok = ("e", eng, self.cnt[eng])
        self.ops[eng].append(([(self._semof(k), v) for k, v in waits.items()], fn, (self.esem[eng], 1)))
        self._commit(tok, reads, writes)
        return tok

    def dma(self, q, fn, reads=(), writes=()):
        st = self.dq[q]
        i = st["next"]
        st["next"] = (i + 1) % NSLOT
        waits = self._gather(q, reads, writes)
        if st["cnt"][i] > 0:
            self._need(q, ("d", (q, i), st["cnt"][i]), waits)
        st["cnt"][i] += 16
        tok = ("d", (q, i), st["cnt"][i])
        self.ops[q].append(([(self._semof(k), v) for k, v in waits.items()], fn, (st["sems"][i], 16)))
        self._commit(tok, reads, writes)
        return tok

    def barrier(self):
        toks = [("e", e, self.cnt[e]) for e in ENGS if self.cnt[e] > 0]
        for q, st in self.dq.items():
            for i in range(NSLOT):
                if st["cnt"][i] > 0:
                    toks.append(("d", (q, i), st["cnt"][i]))
        for eng in ENGS:
            waits = {}
            for t in toks:
                if t[0] == "e" and t[1] == eng:
                    continue
                self._need(eng, t, waits)
            if waits:
                self.ops[eng].append(([(self._semof(k), v) for k, v in waits.items()], None, None))

    def emit(self):
        nc = self.nc
        ops = self.ops
        with nc.Block() as block:
            def run(handle, lst):
                for waits, fn, inc in lst:
                    for sem, v in waits:
                        handle.wait_ge(sem, v)
                    if fn is not None:
                        fn(handle).then_inc(inc[0], inc[1])

            @block.tensor
            def _(e):
                run(e, ops["pe"])

            @block.scalar
            def _(e):
                run(e, ops["act"])

            @block.vector
            def _(e):
                run(e, ops["dve"])

            @block.gpsimd
            def _(e):
                run(e, ops["pool"])

            @block.sync
            def _(e):
                run(e, ops["sp"])


def bc(ap, shape):
    return ap.to_broadcast(list(shape))


def build(stage=99):
    nc = bass.Bass("TRN2", target_bir_lowering=False)

    def din(name, shape, dt=F32):
        return nc.dram_tensor(name, list(shape), dt, kind="ExternalInput").ap()

    def dout(name, shape, dt=F32):
        return nc.dram_tensor(name, list(shape), dt, kind="ExternalOutput").ap()

    def dscr(name, shape, dt=F32):
        return nc.dram_tensor(name, list(shape), dt, kind="Internal").ap()

    x_all = din("x_all", [T, D])
    c_all = din("c_all", [128, 40])
    rope_all = din("rope_all", [T, 192])
    w_ada = din("w_ada", [D, 6 * D])
    b_ada = din("b_ada", [1, 6 * D])
    w_in = din("w_in", [D, W_IN])
    norm_mix = din("norm_mix", [1, D])
    norm_mlp = din("norm_mlp", [1, D])
    ident_d = din("ident", [128, 128])
    sel5_d = din("sel5", [5, 132])
    inda_d = din("inda", [128, 8])
    wrep_d = din("wrep", [128, 512])
    bcol_d = din("bcol", [128, 2])
    kdsc_d = din("kdsc", [128, 4])
    oh_d = din("oh", [128, 8])

    x_own = din("x_own", [NB * 128, D])
    rope_own = din("rope_own", [NB * 128, 192])
    cmask_d = din("cmask", [NB, 128, 1024])
    forceb_d = din("forceb", [NB, 128, 256])
    dmask_d = din("dmask", [128, 8 * 128])
    wmask_d = din("wmask", [128, 24 * 128])
    ovp_d = din("ovp", [128, 8 * 256])
    dtab_d = din("dtab", [128, 512])
    qdt_d = din("qdt", [128, 512])
    gnw_d = din("gnw", [1, D])
    gnb_d = din("gnb", [1, D])
    DBG = (stage == 2)
    DBG_NBLK = int(os.environ.get('K_NBLK', '3'))
    PART = int(os.environ.get('K_PART', '9'))
    SUB = int(os.environ.get('K_SUB', '9'))
    SUB2 = int(os.environ.get('K_SUB2', '9'))
    hT_d = dscr("hT_d", [NB, 128, 8, 128], BF16)
    yrT_d = dscr("yrT_d", [NB, 128, 8, 128], BF16)
    onT_d = dscr("onT_d", [NB, 128, 4, 128], BF16)
    if DBG:
        dbg_selb = dout("dbg_selb", [NB, 2, 128, 256])
        dbg_on = dout("dbg_on", [NB, 128, 512])
        dbg_yr = dout("dbg_yr", [NB, 128, 1024])

    w_bn = din("w_bn", [512, D])
    w_br = din("w_br", [D, D])
    w_out = din("w_out", [D, D])
    w_up = din("w_up", [D, 4 * D])
    w_down = din("w_down", [4 * D, D])
    norm_final_d = din("norm_final", [1, D])
    x_smp = din("x_smp", [SB, D])
    x1_d = dscr("x1_d", [NB * 128, D])
    x1s_d = dscr("x1s_d", [SB, D])
    hTs_d = dscr("hTs_d", [128, 8, SB], BF16)
    onTs_d = dscr("onTs_d", [128, 4, SB], BF16)
    yrTs_d = dscr("yrTs_d", [128, 8, SB], BF16)
    y_own = dout("y_own", [NB * 128, D])
    y_smp = dout("y_smp", [SB, D])

    smallc_d = din("smallc", [1, 1024])
    oh4_d = din("oh4", [SB, 4])
    ohm_d = din("ohm", [128, 16])
    ovs_d = din("ovs", [128, 8 * 257])
    wrow_d = din("wrow", [1, 2 * 16 * 256])
    brow_d = din("brow", [1, 256])
    rope_smp = din("rope_smp", [SB, 192])
    ptab = din("ptab", [SB, 128], I32)
    cache_cmp = din("cache_cmp", [5120 * 8, 4096])
    cache_sel = din("cache_sel", [5120 * 8, 4096])
    cache_win = din("cache_win", [SB, 512, 256])
    state_s = din("state_s", [SB, 4, 128, 256])
    os_d = dscr("os_d", [SB, 3, 8, 64])
    dbg_ons = dout("dbg_ons", [SB, 512])
    o_cmp_s = dout("o_cmp_s", [SB, 256])
    o_sel_s = dout("o_sel_s", [SB, 256])
    o_win_s = dout("o_win_s", [SB, 512, 256])
    o_state_s = dout("o_state_s", [SB, 4, 128, 256])

    o_cmp = dout("o_cmp", [T, 256])
    o_sel = dout("o_sel", [T, 256])
    o_win = dout("o_win", [512, 256])
    o_state = dout("o_state", [4, 128, 256])

    kwT_d = dscr("kwT_d", [128, T], BF16)
    vw_d = dscr("vw_d", [T, 130], BF16)
    ksT_d = dscr("ksT_d", [128, T], BF16)
    vs_d = dscr("vs_d", [T, 130], BF16)
    sown_d = dscr("sown_d", [NB, 128, 1024])
    mods_d = dscr("mods_d", [SB, 6 * D])

    with ExitStack() as st:
        P = Prog(nc, st)
        ident = P.sb([128, 128]); d_ident = Dep()
        identb = P.sb([128, 128], BF16); d_identb = Dep()
        modP4 = P.sb([128, 4, D]); d_modP = Dep()
        G1P = P.sb([128, D]); d_G1P = Dep()
        G2P = P.sb([128, D]); d_G2P = Dep()
        cmpKT = P.sb([128, 1024], BF16); d_cmpKT = Dep()
        cmpVA = P.sb([128, 8, 2, 65], BF16); d_cmpVA = Dep()
        epsT = P.sb([128, 1]); d_eps = Dep()

        banks = [P.ps([128, 512]) for _ in range(6)]
        pT = P.ps([128, 1024]); d_pTs = [Dep(True), Dep(True)]
        d_bank = [Dep(True) for _ in range(6)]

        P.dma("sp", lambda e: e.dma_start(out=ident[:], in_=ident_d), writes=[d_ident])
        P.op("dve", lambda e: e.tensor_copy(out=identb[:], in_=ident[:]), reads=[d_ident], writes=[d_identb])
        P.op("dve", lambda e: e.memset(epsT[:], RMS_EPS), writes=[d_eps])
        P.op("pool", lambda e: e.memset(cmpVA[:].rearrange("p a b c -> p (a b c)"), 1.0), writes=[d_cmpVA])

        with ExitStack() as s0:
            cT = P.sb([128, 8, 5], F32, s0); d_cT = Dep()
            scT = P.sb([128, 8, 5], BF16, s0); d_scT = Dep()
            sel5 = P.sb([5, 132], F32, s0); d_sel5 = Dep()
            mod5 = P.sb([5, 6 * D], F32, s0); d_mod5 = Dep()
            bada = P.sb([5, 6 * D], F32, s0); d_bada = Dep()
            gmix = P.sb([128, D], F32, s0); d_gmix = Dep()
            gmlp = P.sb([128, D], F32, s0); d_gmlp = Dep()
            wa = [P.sb([128, 8, 512], BF16, s0) for _ in range(2)]; d_wa = [Dep(), Dep()]

            P.dma("sp", lambda e: e.dma_start(out=cT[:].rearrange("p c b -> p (c b)"), in_=c_all), writes=[d_cT])
            P.dma("sp", lambda e: e.dma_start(out=sel5[:], in_=sel5_d), writes=[d_sel5])
            P.dma("sp", lambda e: e.dma_start(out=bada[:], in_=b_ada.partition_broadcast(5)[:, 0, :]), writes=[d_bada])
            P.dma("sp", lambda e: e.dma_start(out=gmix[:], in_=norm_mix.partition_broadcast(128)[:, 0, :]), writes=[d_gmix])
            P.dma("sp", lambda e: e.dma_start(out=gmlp[:], in_=norm_mlp.partition_broadcast(128)[:, 0, :]), writes=[d_gmlp])
            P.op("act", lambda e: e.activation(out=scT[:], in_=cT[:], func=ACTF.Silu), reads=[d_cT], writes=[d_scT])
            for g in range(12):
                b = g % 2
                P.dma("pool", lambda e, g=g, b=b: e.dma_start(
                    out=wa[b][:], in_=w_ada[:, g * 512:(g + 1) * 512].rearrange("(c p) n -> p c n", p=128)),
                    writes=[d_wa[b]])
                for k in range(8):
                    P.op("pe", lambda e, k=k, b=b: e.matmul(out=banks[0][0:5, :], lhsT=scT[:, k, :], rhs=wa[b][:, k, :],
                                                           start=(k == 0), stop=(k == 7)),
                         reads=[d_scT, d_wa[b]], writes=[d_bank[0]])
                P.op("dve", lambda e, g=g: e.tensor_tensor(out=mod5[:, g * 512:(g + 1) * 512], in0=banks[0][0:5, :],
                                                          in1=bada[:, g * 512:(g + 1) * 512], op=ALU.add),
                     reads=[d_bank[0], d_bada], writes=[d_mod5])
            P.dma("sp", lambda e: e.dma_start(out=mods_d, in_=mod5[1:5, :]), reads=[d_mod5])
            slot = {0: 0, 2: 1, 3: 2, 5: 3}
            for g in range(12):
                ch, hf_ = g // 2, (g % 2) * 512
                P.op("pe", lambda e, g=g: e.matmul(out=banks[1][:, :], lhsT=sel5[:, 0:128], rhs=mod5[:, g * 512:(g + 1) * 512],
                                                   start=True, stop=True), reads=[d_sel5, d_mod5], writes=[d_bank[1]])
                if ch in slot:
                    P.op("act", lambda e, ch=ch, hf_=hf_: e.copy(out=modP4[:, slot[ch], hf_:hf_ + 512], in_=banks[1][:, :]),
                         reads=[d_bank[1]], writes=[d_modP])
                else:
                    Gt, dG, gn, dgn = (G1P, d_G1P, gmix, d_gmix) if ch == 1 else (G2P, d_G2P, gmlp, d_gmlp)
                    P.op("dve", lambda e, Gt=Gt, gn=gn, hf_=hf_: e.scalar_tensor_tensor(out=Gt[:, hf_:hf_ + 512], in0=banks[1][:, :], scalar=1.0,
                                                                                         in1=gn[:, hf_:hf_ + 512], op0=ALU.add, op1=ALU.mult),
                         reads=[d_bank[1], dgn], writes=[dG])
            P.barrier()

        shcol = P.sb([128, 2, 8], F32); d_shcol = Dep()
        for si, mi in enumerate((0, 2)):
            for c in range(8):
                P.op("pe", lambda e, c=c, mi=mi: e.transpose(out=pT[:, c * 128:(c + 1) * 128], in_=modP4[:, mi, c * 128:(c + 1) * 128], identity=ident[:]),
                     reads=[d_modP, d_ident], writes=[d_pTs[c // 4]])
            P.op("act", lambda e, si=si: e.copy(out=shcol[:, si, :], in_=pT[:].rearrange("p (c t) -> p c t", t=128)[:, :, 0]), reads=d_pTs, writes=[d_shcol])

        def norm_mod_T(xt, d_xt, ntok, G, d_G, shift, d_shift, wk, hT, d_hT, shift_col=None, d_shift_col=None):
            junk, d_junk, ss, d_ss, rstd, d_rstd, hm, d_hm, hf, d_hf = wk
            P.op("act", lambda e: e.activation(out=junk[0:ntok, :], in_=xt[0:ntok, :], func=ACTF.Square, accum_out=ss[0:ntok, :]),
                 reads=[d_xt], writes=[d_junk, d_ss])
            P.op("act", lambda e: e.activation(out=rstd[0:ntok, :], in_=ss[0:ntok, :], func=ACTF.Sqrt, bias=epsT[0:ntok, :], scale=1.0 / D),
                 reads=[d_ss, d_eps], writes=[d_rstd])
            P.op("dve", lambda e: e.reciprocal(out=rstd[0:ntok, :], in_=rstd[0:ntok, :]), reads=[d_rstd], writes=[d_rstd])
            P.op("dve", lambda e: e.scalar_tensor_tensor(out=hm[0:ntok, :], in0=xt[0:ntok, :], scalar=rstd[0:ntok, 0:1], in1=G[0:ntok, :],
                                                        op0=ALU.mult, op1=ALU.mult),
                 reads=[d_xt, d_rstd, d_G], writes=[d_hm])
            if shift_col is None:
                P.op("pool", lambda e: e.tensor_tensor(out=hf[0:ntok, :], in0=hm[0:ntok, :], in1=shift[0:ntok, :], op=ALU.add),
                     reads=[d_hm, d_shift], writes=[d_hf])
            for c in range(8):
                P.op("pe", lambda e, c=c: e.transpose(out=pT[:, c * 128:c * 128 + ntok], in_=hf[0:ntok, c * 128:(c + 1) * 128],
                                                      identity=ident[0:ntok, 0:ntok]),
                     reads=[d_hf, d_ident], writes=[d_pTs[c // 4]])
            pv = pT[:].rearrange("p (c t) -> p c t", t=128)
            if shift_col is None:
                P.op("act", lambda e: e.copy(out=hT[:, 0:4, 0:ntok], in_=pv[:, 0:4, 0:ntok]), reads=[d_pTs[0]], writes=[d_hT])
                P.op("dve", lambda e: e.tensor_copy(out=hT[:, 4:8, 0:ntok], in_=pv[:, 4:8, 0:ntok]), reads=[d_pTs[1]], writes=[d_hT])
            else:
                for c in range(8):
                    P.op("act", lambda e, c=c: e.activation(out=hT[:, c, 0:ntok], in_=pv[:, c, 0:ntok], func=ACTF.Identity, bias=shift_col[:, c:c + 1]),
                         reads=[d_pTs[c // 4], d_shift_col], writes=[d_hT])

        junk_sh = P.sb([128, D], BF16); d_junk_sh = Dep()

        def mk_wk(stack, n=128):
            hm_ = P.sb([n, D], F32, stack); d_hm_ = Dep()
            return (junk_sh, d_junk_sh, P.sb([n, 1], F32, stack), Dep(), P.sb([n, 1], F32, stack), Dep(),
                    hm_, d_hm_, hm_, d_hm_)

        def rope(src4, dst4, cos, sin, shp, tmps, d_src, d_dst, d_rp, d_tmps):
            ta, tb, tc, td = tmps
            cb, sbb = cos, sin
            P.op("dve", lambda e: e.tensor_tensor(out=ta, in0=src4(0), in1=cb, op=ALU.mult), reads=[d_src, d_rp], writes=[d_tmps[0]])
            P.op("dve", lambda e: e.tensor_tensor(out=tb, in0=src4(1), in1=sbb, op=ALU.mult), reads=[d_src, d_rp], writes=[d_tmps[1]])
            P.op("dve", lambda e: e.tensor_tensor(out=tc, in0=src4(0), in1=sbb, op=ALU.mult), reads=[d_src, d_rp], writes=[d_tmps[2]])
            P.op("dve", lambda e: e.tensor_tensor(out=td, in0=src4(1), in1=cb, op=ALU.mult), reads=[d_src, d_rp], writes=[d_tmps[3]])
            P.op("pool", lambda e: e.tensor_tensor(out=dst4(0), in0=ta, in1=tb, op=ALU.subtract), reads=[d_tmps[0], d_tmps[1]], writes=[d_dst])
            P.op("pool", lambda e: e.tensor_tensor(out=dst4(1), in0=tc, in1=td, op=ALU.add), reads=[d_tmps[2], d_tmps[3]], writes=[d_dst])

        def make_attn(pts, d_pts):
            scb = [banks[0], banks[1], banks[4], banks[5]]
            d_scb = [d_bank[0], d_bank[1], d_bank[4], d_bank[5]]
            slot = [0]
            pend = []
            cfg = {"nslots": 2, "depth": 1}

            def flush(keep=0):
                while len(pend) > keep:
                    pend.pop(0)()

            def set_mode(nslots, depth):
                flush()
                cfg["nslots"] = min(nslots, len(pts))
                cfg["depth"] = depth
                slot[0] = 0

            def attn(nq, ng, QTh, d_Q, tiles, o4v, d_o4, pslc=None, first=True, last=True, pvT=False):
                ncols = nq * ng
                nt_ = len(tiles)
                for ti, tl in enumerate(tiles):
                    s_ = slot[0]; slot[0] = (slot[0] + 1) % cfg["nslots"]
                    nk = tl["nk"]
                    sc, d_sc, pt, d_pt = scb[s_], d_scb[s_], pts[s_], d_pts[s_]
                    nb = len(tl["biases"])
                    P.op("pe", lambda e, sc=sc, tl=tl, nk=nk, nb=nb: e.matmul(out=sc[0:nk, 0:ncols], lhsT=tl["KT"], rhs=QTh,
                                                                           start=True, stop=(nb == 0)),
                         reads=[d_Q] + tl["deps"], writes=[d_sc])
                    for bi, (bl, br_) in enumerate(tl["biases"]):
                        P.op("pe", lambda e, sc=sc, bl=bl, br_=br_, nk=nk, bi=bi, nb=nb: e.matmul(
                            out=sc[0:nk, 0:ncols], lhsT=bl, rhs=br_, start=False, stop=(bi == nb - 1)),
                            reads=tl["deps"], writes=[d_sc])
                    P.op("act", lambda e, sc=sc, pt=pt, nk=nk: e.activation(out=pt[0:nk, 0:ncols], in_=sc[0:nk, 0:ncols], func=ACTF.Exp),
                         reads=[d_sc], writes=[d_pt])

                    def pv(pt=pt, d_pt=d_pt, tl=tl, nk=nk, ti=ti):
                        if pvT:
                            P.op("pe", lambda e: e.matmul(out=o4v, lhsT=tl["V"], rhs=pt[0:nk, 0:ncols], start=(first and ti == 0), stop=(last and ti == nt_ - 1)),
                                 reads=[d_pt] + tl["deps"], writes=[d_o4])
                        else:
                            for g in range(ng):
                                P.op("pe", lambda e, g=g: e.matmul(
                                    out=o4v[:, g, :], lhsT=pt[0:nk, g * nq:(g + 1) * nq], rhs=tl["V"],
                                    start=(first and ti == 0 and g == 0), stop=(last and ti == nt_ - 1), skip_group_check=True),
                                    reads=[d_pt] + tl["deps"], writes=[d_o4])
                        if pslc is not None:
                            psv, d_ps, gpb = pslc
                            for g in range(ng):
                                P.op("pe", lambda e, g=g: e.matmul(
                                    out=psv(g), lhsT=pt[0:nk, g * nq:(g + 1) * nq], rhs=tl["Ov"],
                                    start=(ti == 0 and g % gpb == 0), stop=(ti == nt_ - 1), skip_group_check=True),
                                    reads=[d_pt] + tl["deps"], writes=[d_ps[g // gpb]])
                    flush(cfg["depth"] - 1)
                    pend.append(pv)

            attn.flush = flush
            attn.set_mode = set_mode
            return attn

        def phase_b1():
            with ExitStack() as sB:
                WB = P.sb([128, 8, 4376], BF16, sB); d_WB = Dep()
                wsegs = []
                for g in range(4):
                    wsegs += [(C_Q + g * 64, 64, g * 128), (C_Q + (4 + g) * 64, 64, g * 128 + 64)]
                wsegs += [(C_RQ, 512, 512), (C_RK, 512, 1024), (C_RV, 1024, 1536), (C_RG, 1024, 2560), (C_G, 24, 3584),
                          (C_KVC, 128, 3608), (C_KVS, 128, 3736), (C_KVW, 128, 3864),
                          (C_KVC + 128, 128, 3992), (C_KVS + 128, 128, 4120), (C_KVW + 128, 128, 4248)]
                for (src, n, dst) in wsegs:
                    P.dma("pool", lambda e, src=src, n=n, dst=dst: e.dma_start(
                        out=WB[:, :, dst:dst + n], in_=w_in[:, src:src + n].rearrange("(c p) n -> p c n", p=128)), writes=[d_WB])
                dmaskB = P.sb([128, 8, 128], BF16, sB); d_dmask = Dep()
                wmaskB = P.sb([128, 24, 128], BF16, sB); d_wmask = Dep()
                OvP = P.sb([128, 8, 256], BF16, sB); d_OvP = Dep()
                dtab = P.sb([128, 4, 128], F32, sB); d_dtab = Dep()
                qdt = P.sb([128, 4, 128], F32, sB); d_qdt = Dep()
                gnw = P.sb([128, D], F32, sB); d_gnw = Dep()
                gnb = P.sb([128, D], F32, sB); d_gnb = Dep()
                Irep = P.sb([128, 4, 128], BF16, sB); d_Irep = Dep()
                epsG = P.sb([128, 1], F32, sB); d_epsG = Dep()
                P.dma("pool", lambda e: e.dma_start(out=dmaskB[:].rearrange("p a b -> p (a b)"), in_=dmask_d), writes=[d_dmask])
                P.dma("pool", lambda e: e.dma_start(out=wmaskB[:].rearrange("p a b -> p (a b)"), in_=wmask_d), writes=[d_wmask])
                P.dma("pool", lambda e: e.dma_start(out=OvP[:].rearrange("p a b -> p (a b)"), in_=ovp_d), writes=[d_OvP])
                P.dma("sp", lambda e: e.dma_start(out=dtab[:].rearrange("p a b -> p (a b)"), in_=dtab_d), writes=[d_dtab])
                P.dma("sp", lambda e: e.dma_start(out=qdt[:].rearrange("p a b -> p (a b)"), in_=qdt_d), writes=[d_qdt])
                P.dma("sp", lambda e: e.dma_start(out=gnw[:], in_=gnw_d.partition_broadcast(128)[:, 0, :]), writes=[d_gnw])
                P.dma("sp", lambda e: e.dma_start(out=gnb[:], in_=gnb_d.partition_broadcast(128)[:, 0, :]), writes=[d_gnb])
                for g in range(4):
                    P.op("dve", lambda e, g=g: e.tensor_copy(out=Irep[:, g, :], in_=ident[:]), reads=[d_ident], writes=[d_Irep])
                P.op("dve", lambda e: e.memset(epsG[:], GN_EPS), writes=[d_epsG])
                IrepF = Irep[:].rearrange("p a b -> p (a b)")

                xt = P.sb([128, D], F32, sB); d_xt = Dep()
                rp = P.sb([128, 192], F32, sB); d_rp = Dep()
                wk = mk_wk(sB)
                hT = P.sb([128, 8, 128], BF16, sB); d_hT = Dep()
                tmq = [P.sb([128, 256], F32, sB) for _ in range(4)]; d_tmq = [Dep() for _ in range(4)]
                qr = P.sb([128, 512], F32, sB); d_qr = Dep()
                rqr = P.sb([128, 4, 128], F32, sB); d_rqr = Dep()
                rkr = P.sb([128, 4, 128], F32, sB); d_rkr = Dep()
                sg = P.sb([128, 24], F32, sB); d_sg = Dep()
                QT = P.sb([128, 4, 128], BF16, sB); d_QT = Dep()
                rqT = P.sb([128, 4, 128], BF16, sB); d_rqT = Dep()
                rqdT = P.sb([128, 4, 128], BF16, sB); d_rqdT = Dep()
                rkT = P.sb([128, 4, 128], BF16, sB); d_rkT = Dep()
                vb = P.sb([128, 1024], BF16, sB); d_vb = Dep()
                sgate = P.sb([128, 1024], F32, sB); d_sgate = Dep()
                attD = P.sb([128, 4, 128], BF16, sB); d_attD = Dep()
                SownB = P.sb([128, 1024], BF16, sB); d_SownB = Dep()
                osb = P.sb([128, 4, 256], F32, sB); d_osb = Dep()
                st4 = P.sb([128, 16], F32, sB); d_st4 = Dep()
                yret = osb[:].rearrange("p a b -> p (a b)"); d_yret = d_osb
                yrT = P.sb([128, 8, 128], BF16, sB); d_yrT = Dep()
                onsa = P.sb([128, 8, 64], F32, sB); d_onsa = Dep()
                tmpo = P.sb([128, 4, 64], F32, sB); d_tmpo = Dep()
                onT = P.sb([128, 4, 128], BF16, sB); d_onT = Dep()
                rden = P.sb([128, 8], F32, sB); d_rden = Dep()
                wgt = P.sb([128, 4], F32, sB); d_wgt = Dep()
                cmaskB = P.sb([128, 1024], BF16, sB); d_cmask = Dep()
                forceb = P.sb([128, 256], F32, sB); d_forceb = Dep()
                pacc = P.sb([128, 256], F32, sB); d_pacc = Dep()
                wk2 = P.sb([128, 256], F32, sB); d_wk2 = Dep()
                m16 = P.sb([128, 16], F32, sB); d_m16 = Dep()
                selb = [P.sb([128, 256], F32, sB) for _ in range(2)]; d_selb = [Dep(), Dep()]
                selbX = [[P.sb([128, 1024], BF16, sB) for _ in range(2)] for _ in range(2)]
                d_selbX = [[Dep(), Dep()], [Dep(), Dep()]]
                KsG = [P.sb([128, 1024], BF16, sB) for _ in range(2)]; d_KsG = [Dep(), Dep()]
                VsG = [P.sb([128, 8, 130], BF16, sB) for _ in range(2)]; d_VsG = [Dep(), Dep()]
                KwWin = P.sb([128, 1536], BF16, sB); d_KwWin = Dep()
                VwWin = P.sb([128, 12, 130], BF16, sB); d_VwWin = Dep()
                pts = [P.sb([128, 512], BF16, sB) for _ in range(4)]; d_pts = [Dep() for _ in range(4)]
                P.op("pool", lambda e: e.memset(KwWin[:], 0.0), writes=[d_KwWin])
                P.op("pool", lambda e: e.memset(VwWin[:].rearrange("p a b -> p (a b)"), 0.0), writes=[d_VwWin])
                attn = make_attn(pts, d_pts)

                o4 = [banks[2][0:65, :], banks[3][0:65, :]]
                d_o4 = [d_bank[2], d_bank[3]]
                oTs = [P.sb([65, 512], F32, sB) for _ in range(2)]; d_oTs = [Dep(), Dep()]
                sgv = sg[:].rearrange("p (h b) -> p h b", b=3)

                def evac(kvh, br_i, first):
                    attn.flush()
                    P.op("act", lambda e: e.copy(out=oTs[kvh][:], in_=o4[kvh]), reads=[d_o4[kvh]], writes=[d_oTs[kvh]])
                    for g in range(4):
                        P.op("pe", lambda e, g=g: e.transpose(out=pT[:, kvh * 512 + g * 65:kvh * 512 + (g + 1) * 65], in_=oTs[kvh][0:65, g * 128:(g + 1) * 128],
                                                              identity=ident[0:65, 0:65]), reads=[d_oTs[kvh], d_ident], writes=[d_pTs[kvh]])
                    o4v = pT[:, kvh * 512:kvh * 512 + 260].rearrange("p (g c) -> p g c", g=4)
                    d_o = d_pTs[kvh]
                    P.op("dve", lambda e: e.tensor_scalar(out=rden[:, kvh * 4:(kvh + 1) * 4], in0=o4v[:, :, 64], scalar1=1e-30, scalar2=None, op0=ALU.max),
                         reads=[d_o], writes=[d_rden])
                    P.op("dve", lambda e: e.reciprocal(out=rden[:, kvh * 4:(kvh + 1) * 4], in_=rden[:, kvh * 4:(kvh + 1) * 4]), reads=[d_rden], writes=[d_rden])
                    P.op("dve", lambda e: e.tensor_tensor(out=wgt[:], in0=rden[:, kvh * 4:(kvh + 1) * 4], in1=sgv[:, kvh * 4:(kvh + 1) * 4, br_i], op=ALU.mult),
                         reads=[d_rden, d_sg], writes=[d_wgt])
                    wb_ = wgt[:].unsqueeze(2).to_broadcast([128, 4, 64])
                    if first:
                        P.op("dve", lambda e: e.tensor_tensor(out=onsa[:, kvh * 4:(kvh + 1) * 4, :], in0=o4v[:, :, 0:64], in1=wb_, op=ALU.mult),
                             reads=[d_o, d_wgt], writes=[d_onsa])
                    else:
                        P.op("dve", lambda e: e.tensor_tensor(out=tmpo[:], in0=o4v[:, :, 0:64], in1=wb_, op=ALU.mult),
                             reads=[d_o, d_wgt], writes=[d_tmpo])
                        P.op("pool", lambda e: e.tensor_tensor(out=onsa[:, kvh * 4:(kvh + 1) * 4, :], in0=onsa[:, kvh * 4:(kvh + 1) * 4, :], in1=tmpo[:], op=ALU.add),
                             reads=[d_tmpo, d_onsa], writes=[d_onsa])

                nblk = NB if stage >= 2 else 0
                if stage == 2:
                    nblk = DBG_NBLK
                for i in range(nblk):
                    P.dma("sp", lambda e, i=i: e.dma_start(out=xt[:], in_=x_own[i * 128:(i + 1) * 128, :]), writes=[d_xt])
                    P.dma("sp", lambda e, i=i: e.dma_start(out=rp[:], in_=rope_own[i * 128:(i + 1) * 128, :]), writes=[d_rp])
                    P.dma("pool", lambda e, i=i: e.dma_start(out=cmaskB[:], in_=cmask_d[i]), writes=[d_cmask])
                    P.dma("sp", lambda e, i=i: e.dma_start(out=forceb[:], in_=forceb_d[i]), writes=[d_forceb])
                    P.dma("pool", lambda e, i=i: e.dma_start(out=SownB[:], in_=sown_d[i]), writes=[d_SownB])
                    if i == 0:
                        P.dma("sp", lambda e: e.dma_start(out=KwWin[:, 512:1536], in_=kwT_d[:, 0:1024]), writes=[d_KwWin])
                        P.dma("sp", lambda e: e.dma_start(out=VwWin[:, 4:12, :], in_=vw_d[0:1024, :].rearrange("(m p) f -> p m f", p=128)), writes=[d_VwWin])
                    else:
                        P.dma("sp", lambda e, i=i: e.dma_start(out=KwWin[:], in_=kwT_d[:, (8 * i - 4) * 128:(8 * i + 8) * 128]), writes=[d_KwWin])
                        P.dma("sp", lambda e, i=i: e.dma_start(out=VwWin[:], in_=vw_d[(8 * i - 4) * 128:(8 * i + 8) * 128, :].rearrange("(m p) f -> p m f", p=128)),
                              writes=[d_VwWin])
                    norm_mod_T(xt, d_xt, 128, G1P, d_G1P, modP4[:, 0, :], d_modP, wk, hT, d_hT, shcol[:, 0, :], d_shcol)
                    P.dma("sp", lambda e, i=i: e.dma_start(out=hT_d[i], in_=hT[:]), reads=[d_hT])
                    if SUB < 1:
                        continue
                    for (bk, c0, n) in ((0, 0, 512), (1, 512, 512), (4, 1024, 512), (5, 3584, 24)):
                        for k in range(8):
                            P.op("pe", lambda e, bk=bk, k=k, c0=c0, n=n: e.matmul(out=banks[bk][:, 0:n], lhsT=hT[:, k, :], rhs=WB[:, k, c0:c0 + n],
                                                                                start=(k == 0), stop=(k == 7)),
                                 reads=[d_hT, d_WB], writes=[d_bank[bk]])
                    P.op("act", lambda e: e.activation(out=sg[:], in_=banks[5][:, 0:24], func=ACTF.Sigmoid), reads=[d_bank[5]], writes=[d_sg])
                    if SUB < 2:
                        continue
                    srcq = banks[0][:, 0:512].rearrange("p (h two e) -> p h two e", h=8, two=2)
                    dstq = qr[:].rearrange("p (h two e) -> p h two e", h=8, two=2)
                    shq = [128, 8, 32]
                    tvq = [t[:, 0:256].rearrange("p (h e) -> p h e", h=8) for t in tmq]
                    rope(lambda hf: srcq[:, :, hf, :], lambda hf: dstq[:, :, hf, :], rp[:, 0:32].unsqueeze(1).to_broadcast(shq),
                         rp[:, 32:64].unsqueeze(1).to_broadcast(shq), shq, tvq, d_bank[0], d_qr, d_rp, d_tmq)
                    shk = [128, 4, 64]
                    tvk = [t[:, 0:256].rearrange("p (h e) -> p h e", h=4) for t in tmq]
                    cosk = rp[:, 64:128].unsqueeze(1).to_broadcast(shk)
                    sink = rp[:, 128:192].unsqueeze(1).to_broadcast(shk)
                    srcrq = banks[1][:, 0:512].rearrange("p (h two e) -> p h two e", h=4, two=2)
                    dstrq = rqr[:].rearrange("p h (two e) -> p h two e", two=2)
                    rope(lambda hf: srcrq[:, :, hf, :], lambda hf: dstrq[:, :, hf, :], cosk, sink, shk, tvk, d_bank[1], d_rqr, d_rp, d_tmq)
                    srcrk = banks[4][:, 0:512].rearrange("p (h two e) -> p h two e", h=4, two=2)
                    dstrk = rkr[:].rearrange("p h (two e) -> p h two e", two=2)
                    rope(lambda hf: srcrk[:, :, hf, :], lambda hf: dstrk[:, :, hf, :], cosk, sink, shk, tvk, d_bank[4], d_rkr, d_rp, d_tmq)
                    if SUB < 3:
                        continue
                    P.op("pool", lambda e: e.tensor_scalar(out=qr[:], in0=qr[:], scalar1=0.125, scalar2=None, op0=ALU.mult), reads=[d_qr], writes=[d_qr])
                    P.op("pool", lambda e: e.tensor_scalar(out=rkr[:].rearrange("p a b -> p (a b)"), in0=rkr[:].rearrange("p a b -> p (a b)"),
                                                           scalar1=RET_SCALE, scalar2=None, op0=ALU.mult), reads=[d_rkr], writes=[d_rkr])
                    for g in range(4):
                        P.op("pe", lambda e, g=g: e.transpose(out=banks[3][:, g * 128:(g + 1) * 128], in_=qr[:, g * 128:(g + 1) * 128], identity=ident[:]),
                             reads=[d_qr, d_ident], writes=[d_bank[3]])
                    P.op("act", lambda e: e.copy(out=QT[:].rearrange("p a b -> p (a b)"), in_=banks[3][:, :]),
                         reads=[d_bank[3]], writes=[d_QT])
                    if SUB2 < 1:
                        continue
                    for h in range(4):
                        P.op("pe", lambda e, h=h: e.transpose(out=banks[3][:, h * 128:(h + 1) * 128], in_=rqr[:, h, :], identity=ident[:]),
                             reads=[d_rqr, d_ident], writes=[d_bank[3]])
                    P.op("act", lambda e: e.copy(out=rqT[:].rearrange("p a b -> p (a b)"), in_=banks[3][:, :]), reads=[d_bank[3]], writes=[d_rqT])
                    P.op("dve", lambda e: e.tensor_tensor(out=rqdT[:].rearrange("p a b -> p (a b)"), in0=banks[3][:, :],
                                                          in1=qdt[:].rearrange("p a b -> p (a b)"), op=ALU.mult),
                         reads=[d_bank[3], d_qdt], writes=[d_rqdT])
                    if SUB2 < 2:
                        continue
                    for h in range(4):
                        P.op("pe", lambda e, h=h: e.transpose(out=banks[3][:, h * 128:(h + 1) * 128], in_=rkr[:, h, :], identity=ident[:]),
                             reads=[d_rkr, d_ident], writes=[d_bank[3]])
                    P.op("act", lambda e: e.copy(out=rkT[:].rearrange("p a b -> p (a b)"), in_=banks[3][:, :]),
                         reads=[d_bank[3]], writes=[d_rkT])
                    if SUB < 4:
                        continue
                    for (bk, c0) in ((0, 1536), (1, 2048), (4, 2560), (5, 3072)):
                        for k in range(8):
                            P.op("pe", lambda e, bk=bk, k=k, c0=c0: e.matmul(out=banks[bk][:, :], lhsT=hT[:, k, :], rhs=WB[:, k, c0:c0 + 512],
                                                                           start=(k == 0), stop=(k == 7)),
                                 reads=[d_hT, d_WB], writes=[d_bank[bk]])
                    P.op("act", lambda e: e.copy(out=vb[:, 0:512], in_=banks[0][:, :]), reads=[d_bank[0]], writes=[d_vb])
                    P.op("act", lambda e: e.copy(out=vb[:, 512:1024], in_=banks[1][:, :]), reads=[d_bank[1]], writes=[d_vb])
                    P.op("act", lambda e: e.activation(out=sgate[:, 0:512], in_=banks[4][:, :], func=ACTF.Silu), reads=[d_bank[4]], writes=[d_sgate])
                    P.op("act", lambda e: e.activation(out=sgate[:, 512:1024], in_=banks[5][:, :], func=ACTF.Silu), reads=[d_bank[5]], writes=[d_sgate])
                    if PART < 2:
                        continue
                    for h in range(4):
                        P.op("pe", lambda e, h=h: e.matmul(out=banks[3][:, h * 128:(h + 1) * 128], lhsT=rkT[:, h, :], rhs=rqT[:, h, :], start=True, stop=True),
                             reads=[d_rkT, d_rqT], writes=[d_bank[3]])
                    P.op("dve", lambda e: e.tensor_tensor(out=attD[:].rearrange("p a b -> p (a b)"), in0=banks[3][:, :],
                                                          in1=dtab[:].rearrange("p a b -> p (a b)"), op=ALU.mult),
                         reads=[d_bank[3], d_dtab], writes=[d_attD])
                    for h in range(4):
                        bk = 4 + h // 2
                        c0 = (h % 2) * 256
                        P.op("pe", lambda e, h=h, bk=bk, c0=c0: e.matmul(out=banks[bk][:, c0:c0 + 256], lhsT=attD[:, h, :], rhs=vb[:, h * 256:(h + 1) * 256],
                                                                        start=True, stop=False),
                             reads=[d_attD, d_vb], writes=[d_bank[bk]])
                        P.op("pe", lambda e, h=h, bk=bk, c0=c0: e.matmul(out=banks[bk][:, c0:c0 + 256], lhsT=rqdT[:, h, :], rhs=SownB[:, h * 256:(h + 1) * 256],
                                                                        start=False, stop=True),
                             reads=[d_rqdT, d_SownB], writes=[d_bank[bk]])
                    for h in range(4):
                        bk = 4 + h // 2
                        c0 = (h % 2) * 256
                        P.op("act", lambda e, h=h, bk=bk, c0=c0: e.activation(out=osb[:, h, :], in_=banks[bk][:, c0:c0 + 256], func=ACTF.Identity,
                                                                             accum_out=st4[:, h:h + 1]),
                             reads=[d_bank[bk]], writes=[d_osb, d_st4])
                        P.op("act", lambda e, h=h, bk=bk, c0=c0: e.activation(out=junk_sh[:, 0:256], in_=banks[bk][:, c0:c0 + 256], func=ACTF.Square,
                                                                             accum_out=st4[:, 4 + h:5 + h]),
                             reads=[d_bank[bk]], writes=[d_junk_sh, d_st4])
                    P.op("dve", lambda e: e.tensor_scalar(out=st4[:, 8:12], in0=st4[:, 0:4], scalar1=1.0 / 256, scalar2=None, op0=ALU.mult),
                         reads=[d_st4], writes=[d_st4])
                    P.op("dve", lambda e: e.tensor_tensor(out=st4[:, 12:16], in0=st4[:, 8:12], in1=st4[:, 8:12], op=ALU.mult), reads=[d_st4], writes=[d_st4])
                    P.op("dve", lambda e: e.scalar_tensor_tensor(out=st4[:, 12:16], in0=st4[:, 4:8], scalar=1.0 / 256, in1=st4[:, 12:16],
                                                                 op0=ALU.mult, op1=ALU.subtract), reads=[d_st4], writes=[d_st4])
                    P.op("act", lambda e: e.activation(out=st4[:, 12:16], in_=st4[:, 12:16], func=ACTF.Sqrt, bias=epsG[:, 0:1], scale=1.0),
                         reads=[d_st4, d_epsG], writes=[d_st4])
                    P.op("dve", lambda e: e.reciprocal(out=st4[:, 12:16], in_=st4[:, 12:16]), reads=[d_st4], writes=[d_st4])
                    for h in range(4):
                        P.op("dve", lambda e, h=h: e.tensor_scalar(out=yret[:, h * 256:(h + 1) * 256], in0=osb[:, h, :], scalar1=st4[:, 8 + h:9 + h],
                                                                  scalar2=st4[:, 12 + h:13 + h], op0=ALU.subtract, op1=ALU.mult),
                             reads=[d_osb, d_st4], writes=[d_yret])
                    P.op("pool", lambda e: e.tensor_tensor(out=yret[:], in0=yret[:], in1=gnw[:], op=ALU.mult), reads=[d_yret, d_gnw], writes=[d_yret])
                    P.op("pool", lambda e: e.tensor_tensor(out=yret[:], in0=yret[:], in1=gnb[:], op=ALU.add), reads=[d_yret, d_gnb], writes=[d_yret])
                    P.op("pool", lambda e: e.tensor_tensor(out=yret[:], in0=yret[:], in1=sgate[:], op=ALU.mult), reads=[d_yret, d_sgate], writes=[d_yret])
                    for c in range(8):
                        P.op("pe", lambda e, c=c: e.transpose(out=pT[:, c * 128:(c + 1) * 128], in_=yret[:, c * 128:(c + 1) * 128], identity=ident[:]),
                             reads=[d_yret, d_ident], writes=[d_pTs[c // 4]])
                    P.op("act", lambda e: e.copy(out=yrT[:].rearrange("p a b -> p (a b)"), in_=pT[:]), reads=d_pTs, writes=[d_yrT])
                    P.dma("sp", lambda e, i=i: e.dma_start(out=yrT_d[i], in_=yrT[:]), reads=[d_yrT])
                    if DBG:
                        P.dma("sp", lambda e, i=i: e.dma_start(out=dbg_yr[i], in_=yret[:]), reads=[d_yret])

                    if PART < 3:
                        continue
                    attn.set_mode(2, 1)
                    for kvh in range(2):
                        QTh = QT[kvh * 64:(kvh + 1) * 64, :, :].rearrange("p a b -> p (a b)")
                        tiles = []
                        for jt in range(8):
                            tiles.append({"KT": cmpKT[kvh * 64:(kvh + 1) * 64, jt * 128:(jt + 1) * 128], "V": cmpVA[:, jt, kvh, :], "nk": 128,
                                          "biases": [(cmaskB[:, jt * 128:(jt + 1) * 128], IrepF)], "Ov": OvP[:, jt, :],
                                          "deps": [d_cmpKT, d_cmpVA, d_cmask, d_Irep, d_OvP]})
                        psv = lambda g: banks[4 + g // 2][:, (g % 2) * 256:(g % 2) * 256 + 256]
                        attn(128, 4, QTh, d_QT, tiles, o4[kvh], d_o4[kvh], pslc=(psv, [d_bank[4], d_bank[5]], 2), pvT=True)
                        evac(kvh, 0, True)
                        P.op("dve", lambda e, kvh=kvh: e.tensor_scalar(out=pacc[:], in0=psv(0), scalar1=rden[:, kvh * 4:kvh * 4 + 1], scalar2=None, op0=ALU.mult),
                             reads=[d_bank[4], d_rden], writes=[d_pacc])
                        for g in range(1, 4):
                            P.op("dve", lambda e, kvh=kvh, g=g: e.scalar_tensor_tensor(out=pacc[:], in0=psv(g), scalar=rden[:, kvh * 4 + g:kvh * 4 + g + 1],
                                                                                      in1=pacc[:], op0=ALU.mult, op1=ALU.add),
                                 reads=[d_bank[4 + g // 2], d_rden, d_pacc], writes=[d_pacc])
                        P.op("dve", lambda e: e.tensor_tensor(out=pacc[:], in0=pacc[:], in1=forceb[:], op=ALU.add), reads=[d_pacc, d_forceb], writes=[d_pacc])
                        P.op("dve", lambda e: e.max(out=m16[:, 0:8], in_=pacc[:]), reads=[d_pacc], writes=[d_m16])
                        P.op("dve", lambda e: e.match_replace(out=wk2[:], in_to_replace=m16[:, 0:8], in_values=pacc[:], imm_value=-3.0e38),
                             reads=[d_pacc, d_m16], writes=[d_wk2])
                        P.op("dve", lambda e: e.max(out=m16[:, 8:16], in_=wk2[:]), reads=[d_wk2], writes=[d_m16])
                        P.op("dve", lambda e, kvh=kvh: e.tensor_scalar(out=selb[kvh][:], in0=pacc[:], scalar1=m16[:, 15:16], scalar2=NEG,
                                                                      op0=ALU.is_lt, op1=ALU.mult),
                             reads=[d_pacc, d_m16], writes=[d_selb[kvh]])
                    if DBG:
                        P.dma("sp", lambda e, i=i: e.dma_start(out=dbg_selb[i, 0], in_=selb[0][:]), reads=[d_selb[0]])
                        P.dma("sp", lambda e, i=i: e.dma_start(out=dbg_selb[i, 1], in_=selb[1][:]), reads=[d_selb[1]])
                    if PART < 4:
                        continue
                    attn.set_mode(4, 3)
                    ngk = i + 1
                    for gk in range(ngk):
                        b = gk % 2
                        P.dma("sp", lambda e, gk=gk, b=b: e.dma_start(out=KsG[b][:], in_=ksT_d[:, gk * 1024:(gk + 1) * 1024]), writes=[d_KsG[b]])
                        P.dma("sp", lambda e, gk=gk, b=b: e.dma_start(out=VsG[b][:], in_=vs_d[gk * 1024:(gk + 1) * 1024, :].rearrange("(m p) f -> p m f", p=128)),
                              writes=[d_VsG[b]])
                        for kvh in range(2):
                            P.op("pool", lambda e, kvh=kvh, b=b, gk=gk: e.tensor_copy(
                                out=selbX[kvh][b][:].rearrange("p (a c) -> p a c", c=64),
                                in_=selb[kvh][:, gk * 16:(gk + 1) * 16].unsqueeze(2).to_broadcast([128, 16, 64])),
                                reads=[d_selb[kvh]], writes=[d_selbX[kvh][b]])
                        for kvh in range(2):
                            QTh = QT[kvh * 64:(kvh + 1) * 64, :, :].rearrange("p a b -> p (a b)")
                            tiles = []
                            for j in range(8):
                                bs = [(selbX[kvh][b][:, j * 128:(j + 1) * 128], IrepF)]
                                if gk == i:
                                    bs.append((dmaskB[:, j, :], IrepF))
                                tiles.append({"KT": KsG[b][kvh * 64:(kvh + 1) * 64, j * 128:(j + 1) * 128], "V": VsG[b][:, j, kvh * 65:(kvh + 1) * 65],
                                              "nk": 128, "biases": bs, "deps": [d_KsG[b], d_VsG[b], d_selbX[kvh][b], d_Irep, d_dmask]})
                            attn(128, 4, QTh, d_QT, tiles, o4[kvh], d_o4[kvh], first=(gk == 0), last=(gk == ngk - 1), pvT=True)
                    evac(0, 1, False)
                    evac(1, 1, False)
                    if PART < 5:
                        continue
                    for kvh in range(2):
                        QTh = QT[kvh * 64:(kvh + 1) * 64, :, :].rearrange("p a b -> p (a b)")
                        tiles = []
                        for m in range(12):
                            wm = m if i == 0 else 12 + m
                            tiles.append({"KT": KwWin[kvh * 64:(kvh + 1) * 64, m * 128:(m + 1) * 128], "V": VwWin[:, m, kvh * 65:(kvh + 1) * 65], "nk": 128,
                                          "biases": [(wmaskB[:, wm, :], IrepF)], "deps": [d_KwWin, d_VwWin, d_wmask, d_Irep]})
                        attn(128, 4, QTh, d_QT, tiles, o4[kvh], d_o4[kvh], pvT=True)
                        evac(kvh, 2, False)
                    onf = onsa[:].rearrange("p h d -> p (h d)")
                    for c in range(4):
                        P.op("pe", lambda e, c=c: e.transpose(out=pT[:, c * 128:(c + 1) * 128], in_=onf[:, c * 128:(c + 1) * 128], identity=ident[:]),
                             reads=[d_onsa, d_ident], writes=[d_pTs[0]])
                    P.op("act", lambda e: e.copy(out=onT[:].rearrange("p a b -> p (a b)"), in_=pT[:, 0:512]), reads=[d_pTs[0]], writes=[d_onT])
                    P.dma("sp", lambda e, i=i: e.dma_start(out=onT_d[i], in_=onT[:]), reads=[d_onT])
                    if DBG:
                        P.dma("sp", lambda e, i=i: e.dma_start(out=dbg_on[i], in_=onsa[:].rearrange("p h d -> p (h d)")), reads=[d_onsa])
                P.barrier()

        def phase_s1():
            with ExitStack() as sS:
                sR = ExitStack()
                smallc = P.sb([1, 1024], F32, sS); smallb = P.sb([1, 1024], BF16, sS); d_small = Dep()
                P.dma("sp", lambda e: e.dma_start(out=smallc[:], in_=smallc_d), writes=[d_small])
                P.op("dve", lambda e: e.tensor_copy(out=smallb[:], in_=smallc[:]), reads=[d_small], writes=[d_small])
                cm7row, wm0row, ones4 = smallb[0:1, 0:128], smallb[0:1, 128:256], smallb[0:1, 272:276]
                forceS = smallc[0:1, 276:532]
                oh4 = P.sb([SB, 4], F32, sS); d_oh4 = Dep()
                P.dma("sp", lambda e: e.dma_start(out=oh4[:], in_=oh4_d), writes=[d_oh4])
                OvS = P.sb([128, 8, 257], BF16, sS); d_OvS = Dep()
                P.dma("pool", lambda e: e.dma_start(out=OvS[:].rearrange("p a b -> p (a b)"), in_=ovs_d), writes=[d_OvS])
                wbc = P.sb([128, 2, 16, 256], F32, sS); d_wbc = Dep()
                P.dma("sp", lambda e: e.dma_start(out=wbc[:].rearrange("p a b c -> p (a b c)"), in_=wrow_d.partition_broadcast(128)[:, 0, :]), writes=[d_wbc])
                brow = P.sb([128, 256], F32, sS); d_brow = Dep()
                P.dma("sp", lambda e: e.dma_start(out=brow[:], in_=brow_d.partition_broadcast(128)[:, 0, :]), writes=[d_brow])
                QTs = P.sb([128, 4, SB], BF16, sS); d_QTs = Dep()
                KnT = P.sb([128, 2, SB], BF16, sS); d_KnT = Dep()
                VnA = P.sb([SB, 2, 2, 65], BF16, sS); d_VnA = Dep()
                sg = P.sb([SB, 24], F32, sS); d_sg = Dep()
                ohm = P.sb([128, 4, 4], F32, sR); d_ohm = Dep()
                P.dma("sp", lambda e: e.dma_start(out=ohm[:].rearrange("p a b -> p (a b)"), in_=ohm_d), writes=[d_ohm])
                gnw = P.sb([SB, D], F32, sR); d_gnw = Dep()
                gnb = P.sb([SB, D], F32, sR); d_gnb = Dep()
                P.dma("sp", lambda e: e.dma_start(out=gnw[:], in_=gnw_d.partition_broadcast(SB)[:, 0, :]), writes=[d_gnw])
                P.dma("sp", lambda e: e.dma_start(out=gnb[:], in_=gnb_d.partition_broadcast(SB)[:, 0, :]), writes=[d_gnb])
                epsG = P.sb([SB, 1], F32, sR); d_epsG = Dep()
                P.op("dve", lambda e: e.memset(epsG[:], GN_EPS), writes=[d_epsG])
                xs = P.sb([SB, D], F32, sR); d_xs = Dep()
                rps = P.sb([SB, 192], F32, sR); d_rps = Dep()
                msS = P.sb([SB, 2, D], F32, sR); d_msS = Dep()
                gm4 = P.sb([SB, D], F32, sR); d_gm4 = Dep()
                P.dma("sp", lambda e: e.dma_start(out=xs[:], in_=x_smp), writes=[d_xs])
                P.dma("sp", lambda e: e.dma_start(out=rps[:], in_=rope_smp), writes=[d_rps])
                P.dma("sp", lambda e: e.dma_start(out=msS[:], in_=mods_d[:, 0:2 * D].rearrange("s (a d) -> s a d", a=2)), writes=[d_msS])
                P.dma("sp", lambda e: e.dma_start(out=gm4[:], in_=norm_mix.partition_broadcast(SB)[:, 0, :]), writes=[d_gm4])
                P.op("dve", lambda e: e.scalar_tensor_tensor(out=msS[:, 1, :], in0=msS[:, 1, :], scalar=1.0, in1=gm4[:], op0=ALU.add, op1=ALU.mult),
                     reads=[d_msS, d_gm4], writes=[d_msS])
                wkS = mk_wk(sR, SB)
                hTs = P.sb([128, 8, SB], BF16, sR); d_hTs = Dep()
                norm_mod_T(xs, d_xs, SB, msS[:, 1, :], d_msS, msS[:, 0, :], d_msS, wkS, hTs, d_hTs)
                P.dma("sp", lambda e: e.dma_start(out=hTs_d, in_=hTs[:]), reads=[d_hTs])
                zs = P.sb([SB, 4376], F32, sR); d_zs = Dep()
                wch = [P.sb([128, 8, 512], BF16, sR) for _ in range(2)]; d_wch = [Dep(), Dep()]
                segs = []
                for g in range(4):
                    segs += [(C_Q + g * 64, 64), (C_Q + (4 + g) * 64, 64)]
                segs += [(C_RQ, 512), (C_RK, 512), (C_RV, 512), (C_RV + 512, 512), (C_RG, 512), (C_RG + 512, 512), (C_G, 24),
                         (C_KVC, 128), (C_KVS, 128), (C_KVW, 128), (C_KVC + 128, 128), (C_KVS + 128, 128), (C_KVW + 128, 128)]
                chunks = []
                cur, curw = [], 0
                for sg_ in segs:
                    if curw + sg_[1] > 512:
                        chunks.append(cur); cur, curw = [], 0
                    cur.append(sg_); curw += sg_[1]
                chunks.append(cur)
                zoff = 0
                for ci, ch in enumerate(chunks):
                    b = ci % 2
                    o_ = 0
                    for (src, n_) in ch:
                        P.dma("pool", lambda e, b=b, o_=o_, src=src, n_=n_: e.dma_start(
                            out=wch[b][:, :, o_:o_ + n_], in_=w_in[:, src:src + n_].rearrange("(c p) n -> p c n", p=128)), writes=[d_wch[b]])
                        o_ += n_
                    for k in range(8):
                        P.op("pe", lambda e, b=b, k=k, o_=o_: e.matmul(out=banks[b][0:SB, 0:o_], lhsT=hTs[:, k, :], rhs=wch[b][:, k, 0:o_], start=(k == 0), stop=(k == 7)),
                             reads=[d_hTs, d_wch[b]], writes=[d_bank[b]])
                    P.op("act", lambda e, b=b, o_=o_, zoff=zoff: e.copy(out=zs[:, zoff:zoff + o_], in_=banks[b][0:SB, 0:o_]), reads=[d_bank[b]], writes=[d_zs])
                    zoff += o_
                tms = [P.sb([SB, 256], F32, sR) for _ in range(4)]; d_tms = [Dep() for _ in range(4)]
                qr = P.sb([SB, 512], F32, sR); d_qr = Dep()
                rqr = P.sb([SB, 4, 128], F32, sR); d_rqr = Dep()
                rkr = P.sb([SB, 4, 128], F32, sR); d_rkr = Dep()
                kvo = P.sb([SB, 3, 256], F32, sR); d_kvo = Dep()
                sgate = P.sb([SB, 1024], F32, sR); d_sgate = Dep()
                vbs = P.sb([SB, 1024], BF16, sR); d_vbs = Dep()
                shq = [SB, 8, 32]
                srcq = zs[:, 0:512].rearrange("p (h two e) -> p h two e", h=8, two=2)
                dstq = qr[:].rearrange("p (h two e) -> p h two e", h=8, two=2)
                tvq = [t[:, 0:256].rearrange("p (h e) -> p h e", h=8) for t in tms]
                cos32 = rps[:, 0:32]; sin32 = rps[:, 32:64]
                rope(lambda hf: srcq[:, :, hf, :], lambda hf: dstq[:, :, hf, :], cos32.unsqueeze(1).to_broadcast(shq), sin32.unsqueeze(1).to_broadcast(shq),
                     shq, tvq, d_zs, d_qr, d_rps, d_tms)
                P.op("pool", lambda e: e.tensor_scalar(out=qr[:], in0=qr[:], scalar1=0.125, scalar2=None, op0=ALU.mult), reads=[d_qr], writes=[d_qr])
                shk = [SB, 4, 64]
                tvk = [t[:, 0:256].rearrange("p (h e) -> p h e", h=4) for t in tms]
                cosk = rps[:, 64:128].unsqueeze(1).to_broadcast(shk); sink = rps[:, 128:192].unsqueeze(1).to_broadcast(shk)
                for (c0, dstt, d_dst) in ((512, rqr, d_rqr), (1024, rkr, d_rkr)):
                    src_ = zs[:, c0:c0 + 512].rearrange("p (h two e) -> p h two e", h=4, two=2)
                    dst_ = dstt[:].rearrange("p h (two e) -> p h two e", two=2)
                    rope(lambda hf, src_=src_: src_[:, :, hf, :], lambda hf, dst_=dst_: dst_[:, :, hf, :], cosk, sink, shk, tvk, d_zs, d_dst, d_rps, d_tms)
                P.op("pool", lambda e: e.tensor_scalar(out=rkr[:].rearrange("p a b -> p (a b)"), in0=rkr[:].rearrange("p a b -> p (a b)"),
                                                       scalar1=RET_SCALE, scalar2=None, op0=ALU.mult), reads=[d_rkr], writes=[d_rkr])
                shn = [SB, 3, 2, 32]
                srcn = zs[:, 3608:3992].rearrange("p (j h two e) -> p j h two e", j=3, h=2, two=2)
                dstn = kvo[:, :, 0:128].rearrange("p j (h two e) -> p j h two e", h=2, two=2)
                tvn = [t[:, 0:192].rearrange("p (j h e) -> p j h e", j=3, h=2) for t in tms]
                rope(lambda hf: srcn[:, :, :, hf, :], lambda hf: dstn[:, :, :, hf, :], cos32.unsqueeze(1).unsqueeze(1).to_broadcast(shn),
                     sin32.unsqueeze(1).unsqueeze(1).to_broadcast(shn), shn, tvn, d_zs, d_kvo, d_rps, d_tms)
                P.op("act", lambda e: e.copy(out=kvo[:, :, 128:256], in_=zs[:, 3992:4376].rearrange("p (j f) -> p j f", j=3)), reads=[d_zs], writes=[d_kvo])
                P.op("act", lambda e: e.activation(out=sg[:], in_=zs[:, 3584:3608], func=ACTF.Sigmoid), reads=[d_zs], writes=[d_sg])
                P.op("act", lambda e: e.activation(out=sgate[:], in_=zs[:, 2560:3584], func=ACTF.Silu), reads=[d_zs], writes=[d_sgate])
                P.op("act", lambda e: e.copy(out=vbs[:], in_=zs[:, 1536:2560]), reads=[d_zs], writes=[d_vbs])
                P.dma("sp", lambda e: e.dma_start(out=o_cmp_s, in_=kvo[:, 0, :]), reads=[d_kvo])
                P.dma("sp", lambda e: e.dma_start(out=o_sel_s, in_=kvo[:, 1, :]), reads=[d_kvo])
                P.dma("sp", lambda e: e.dma_start(out=o_win_s[:, 511, :], in_=kvo[:, 2, :]), reads=[d_kvo])
                for b in range(SB):
                    P.dma("sp", lambda e, b=b: e.dma_start(out=o_win_s[b, 0:511, :], in_=cache_win[b, 1:512, :]))
                rqTs = P.sb([128, 4, SB], BF16, sR); d_rqTs = Dep()
                rqdTs = P.sb([128, 4, SB], F32, sR); d_rqdTs = Dep()
                rkTs = P.sb([128, 4, SB], BF16, sR); d_rkTs = Dep()
                for (srcs, dstT, d_srcs, d_dstT) in (([qr[:, g * 128:(g + 1) * 128] for g in range(4)], QTs, d_qr, d_QTs),
                                                     ([rqr[:, h, :] for h in range(4)], rqTs, d_rqr, d_rqTs),
                                                     ([rkr[:, h, :] for h in range(4)], rkTs, d_rkr, d_rkTs),
                                                     ([kvo[:, 1, 0:128], kvo[:, 2, 0:128]], KnT, d_kvo, d_KnT)):
                    for j, sap in enumerate(srcs):
                        P.op("pe", lambda e, j=j, sap=sap: e.transpose(out=banks[3][:, j * SB:(j + 1) * SB], in_=sap, identity=ident[0:SB, 0:SB]),
                             reads=[d_srcs, d_ident], writes=[d_bank[3]])
                    nn = len(srcs) * SB
                    P.op("act", lambda e, dstT=dstT, nn=nn: e.copy(out=dstT[:].rearrange("p a b -> p (a b)"), in_=banks[3][:, 0:nn]), reads=[d_bank[3]], writes=[d_dstT])
                for h in range(4):
                    P.op("dve", lambda e, h=h: e.tensor_scalar(out=rqdTs[:, h, :], in0=rqTs[:, h, :], scalar1=float(math.exp(LG[h])), scalar2=None, op0=ALU.mult),
                         reads=[d_rqTs], writes=[d_rqdTs])
                P.op("pool", lambda e: e.memset(VnA[:].rearrange("p a b c -> p (a b c)"), 1.0), writes=[d_VnA])
                P.op("pool", lambda e: e.tensor_copy(out=VnA[:, :, :, 0:64], in_=kvo[:, 1:3, 128:256].rearrange("p j (h d) -> p j h d", h=2)),
                     reads=[d_kvo], writes=[d_VnA])
                attDs = P.sb([SB, 4, SB], BF16, sR); d_attDs = Dep()
                for h in range(4):
                    P.op("pe", lambda e, h=h: e.matmul(out=banks[3][0:SB, h * SB:(h + 1) * SB], lhsT=rkTs[:, h, :], rhs=rqTs[:, h, :], start=True, stop=True),
                         reads=[d_rkTs, d_rqTs], writes=[d_bank[3]])
                P.op("dve", lambda e: e.tensor_tensor(out=attDs[:], in0=banks[3][0:SB, 0:16].rearrange("p (h i) -> p h i", h=4),
                                                      in1=oh4[:].unsqueeze(1).to_broadcast([SB, 4, SB]), op=ALU.mult), reads=[d_bank[3], d_oh4], writes=[d_attDs])
                Sb = [P.sb([128, 4, 256], F32, sR) for _ in range(SB)]; d_Sb = [Dep() for _ in range(SB)]
                Sbb = [P.sb([128, 4, 256], BF16, sR) for _ in range(SB)]; d_Sbb = [Dep() for _ in range(SB)]
                rqm = [P.sb([128, 4, SB], BF16, sR) for _ in range(SB)]; d_rqm = [Dep() for _ in range(SB)]
                kdm = [P.sb([SB, 4, 128], BF16, sR) for _ in range(SB)]; d_kdm = [Dep() for _ in range(SB)]
                for b in range(SB):
                    P.dma("sp", lambda e, b=b: e.dma_start(out=Sb[b][:], in_=state_s[b].rearrange("h k v -> k h v")), writes=[d_Sb[b]])
                    P.op("pool", lambda e, b=b: e.tensor_copy(out=Sbb[b][:].rearrange("p a b -> p (a b)"), in_=Sb[b][:].rearrange("p a b -> p (a b)")),
                         reads=[d_Sb[b]], writes=[d_Sbb[b]])
                    P.op("dve", lambda e, b=b: e.tensor_tensor(out=rqm[b][:], in0=rqdTs[:], in1=ohm[:, b, :].unsqueeze(1).to_broadcast([128, 4, SB]), op=ALU.mult),
                         reads=[d_rqdTs, d_ohm], writes=[d_rqm[b]])
                    P.op("dve", lambda e, b=b: e.tensor_scalar(out=kdm[b][:].rearrange("p a b -> p (a b)"), in0=rkr[:].rearrange("p a b -> p (a b)"),
                                                              scalar1=oh4[:, b:b + 1], scalar2=None, op0=ALU.mult), reads=[d_rkr, d_oh4], writes=[d_kdm[b]])
                for h in range(4):
                    bk = 4 + h // 2
                    c0 = (h % 2) * 256
                    P.op("pe", lambda e, h=h, bk=bk, c0=c0: e.matmul(out=banks[bk][0:SB, c0:c0 + 256], lhsT=attDs[:, h, :], rhs=vbs[:, h * 256:(h + 1) * 256],
                                                                    start=True, stop=False), reads=[d_attDs, d_vbs], writes=[d_bank[bk]])
                    for b in range(SB):
                        P.op("pe", lambda e, h=h, bk=bk, c0=c0, b=b: e.matmul(out=banks[bk][0:SB, c0:c0 + 256], lhsT=rqm[b][:, h, :], rhs=Sbb[b][:, h, :],
                                                                             start=False, stop=(b == SB - 1)), reads=[d_rqm[b], d_Sbb[b]], writes=[d_bank[bk]])
                osb = P.sb([SB, 4, 256], F32, sR); d_osb = Dep()
                st4 = P.sb([SB, 16], F32, sR); d_st4 = Dep()
                yret = P.sb([SB, 1024], F32, sR); d_yret = Dep()
                for h in range(4):
                    bk = 4 + h // 2
                    c0 = (h % 2) * 256
                    P.op("act", lambda e, h=h, bk=bk, c0=c0: e.activation(out=osb[:, h, :], in_=banks[bk][0:SB, c0:c0 + 256], func=ACTF.Identity, accum_out=st4[:, h:h + 1]),
                         reads=[d_bank[bk]], writes=[d_osb, d_st4])
                    P.op("act", lambda e, h=h, bk=bk, c0=c0: e.activation(out=junk_sh[0:SB, 0:256], in_=banks[bk][0:SB, c0:c0 + 256], func=ACTF.Square,
                                                                         accum_out=st4[:, 4 + h:5 + h]), reads=[d_bank[bk]], writes=[d_junk_sh, d_st4])
                P.op("dve", lambda e: e.tensor_scalar(out=st4[:, 8:12], in0=st4[:, 0:4], scalar1=1.0 / 256, scalar2=None, op0=ALU.mult), reads=[d_st4], writes=[d_st4])
                P.op("dve", lambda e: e.tensor_tensor(out=st4[:, 12:16], in0=st4[:, 8:12], in1=st4[:, 8:12], op=ALU.mult), reads=[d_st4], writes=[d_st4])
                P.op("dve", lambda e: e.scalar_tensor_tensor(out=st4[:, 12:16], in0=st4[:, 4:8], scalar=1.0 / 256, in1=st4[:, 12:16], op0=ALU.mult, op1=ALU.subtract),
                     reads=[d_st4], writes=[d_st4])
                P.op("act", lambda e: e.activation(out=st4[:, 12:16], in_=st4[:, 12:16], func=ACTF.Sqrt, bias=epsG[:, 0:1], scale=1.0), reads=[d_st4, d_epsG], writes=[d_st4])
                P.op("dve", lambda e: e.reciprocal(out=st4[:, 12:16], in_=st4[:, 12:16]), reads=[d_st4], writes=[d_st4])
                for h in range(4):
                    P.op("dve", lambda e, h=h: e.tensor_scalar(out=yret[:, h * 256:(h + 1) * 256], in0=osb[:, h, :], scalar1=st4[:, 8 + h:9 + h],
                                                              scalar2=st4[:, 12 + h:13 + h], op0=ALU.subtract, op1=ALU.mult), reads=[d_osb, d_st4], writes=[d_yret])
                P.op("pool", lambda e: e.tensor_tensor(out=yret[:], in0=yret[:], in1=gnw[:], op=ALU.mult), reads=[d_yret, d_gnw], writes=[d_yret])
                P.op("pool", lambda e: e.tensor_tensor(out=yret[:], in0=yret[:], in1=gnb[:], op=ALU.add), reads=[d_yret, d_gnb], writes=[d_yret])
                P.op("pool", lambda e: e.tensor_tensor(out=yret[:], in0=yret[:], in1=sgate[:], op=ALU.mult), reads=[d_yret, d_sgate], writes=[d_yret])
                yrTs = P.sb([128, 8, SB], BF16, sR); d_yrTs = Dep()
                for c in range(8):
                    P.op("pe", lambda e, c=c: e.transpose(out=banks[3][:, c * SB:(c + 1) * SB], in_=yret[:, c * 128:(c + 1) * 128], identity=ident[0:SB, 0:SB]),
                         reads=[d_yret, d_ident], writes=[d_bank[3]])
                P.op("act", lambda e: e.copy(out=yrTs[:].rearrange("p a b -> p (a b)"), in_=banks[3][:, 0:8 * SB]), reads=[d_bank[3]], writes=[d_yrTs])
                P.dma("sp", lambda e: e.dma_start(out=yrTs_d, in_=yrTs[:]), reads=[d_yrTs])
                snew = [P.sb([128, 256], F32, sR) for _ in range(2)]; d_snew = [Dep(), Dep()]
                for b in range(SB):
                    for h in range(4):
                        bb = (b * 4 + h) % 2
                        P.op("pe", lambda e, b=b, h=h, bb=bb: e.matmul(out=banks[4 + bb][:, 0:256], lhsT=kdm[b][:, h, :], rhs=vbs[:, h * 256:(h + 1) * 256],
                                                                      start=True, stop=True), reads=[d_kdm[b], d_vbs], writes=[d_bank[4 + bb]])
                        P.op("dve", lambda e, b=b, h=h, bb=bb: e.scalar_tensor_tensor(out=snew[bb][:], in0=Sb[b][:, h, :], scalar=float(math.exp(LG[h])),
                                                                                     in1=banks[4 + bb][:, 0:256], op0=ALU.mult, op1=ALU.add),
                             reads=[d_Sb[b], d_bank[4 + bb]], writes=[d_snew[bb]])
                        P.dma("sp", lambda e, b=b, h=h, bb=bb: e.dma_start(out=o_state_s[b, h], in_=snew[bb][:]), reads=[d_snew[bb]])

                P.barrier()
                sR.close()
                pts = [P.sb([128, 512], BF16, sS) for _ in range(2)]; d_pts = [Dep(), Dep()]
                attn = make_attn(pts, d_pts)
                G = [P.sb([128, 16, 256], F32, sS) for _ in range(2)]; d_G = [Dep(), Dep()]
                prod = [P.sb([128, 16, 256], F32, sS) for _ in range(2)]; d_prod = [Dep(), Dep()]
                Fs = P.sb([128, 8, 256], F32, sS); d_Fs = Dep()
                S2 = P.sb([128, 9, 256], F32, sS); d_S2 = Dep()
                cmpKTs = P.sb([128, 8, 128], BF16, sS); d_cmpKTs = Dep()
                cmpVAs = P.sb([128, 8, 2, 65], BF16, sS); d_cmpVAs = Dep()
                P.op("pool", lambda e: e.memset(cmpVAs[:].rearrange("p a b c -> p (a b c)"), 1.0), writes=[d_cmpVAs])
                KT16 = [P.sb([128, 16, 128], BF16, sS) for _ in range(2)]; d_KT16 = [Dep(), Dep()]
                V16 = [P.sb([128, 16, 2, 65], BF16, sS) for _ in range(2)]; d_V16 = [Dep(), Dep()]
                for t_ in V16:
                    P.op("pool", lambda e, t_=t_: e.memset(t_[:].rearrange("p a b c -> p (a b c)"), 1.0), writes=d_V16)
                Wt = P.sb([128, 4, 256], F32, sS); d_Wt = Dep()
                KTw = P.sb([128, 4, 128], BF16, sS); d_KTw = Dep()
                VwA = P.sb([128, 4, 2, 65], BF16, sS); d_VwA = Dep()
                P.op("pool", lambda e: e.memset(VwA[:].rearrange("p a b c -> p (a b c)"), 1.0), writes=[d_VwA])
                ptb = P.sb([128, 2], I32, sS); d_ptb = Dep()
                idx8 = P.sb([128, 9], I32, sS); d_idx8 = Dep()
                og = P.sb([SB, 65], F32, sS); d_og = Dep()
                rd1 = P.sb([SB, 2], F32, sS); d_rd1 = Dep()
                pn = P.sb([SB, 256], F32, sS); d_pn = Dep()
                srow = P.sb([1, 256], F32, sS); d_srow = Dep()
                wk2 = P.sb([1, 256], F32, sS); d_wk2 = Dep()
                m16 = P.sb([1, 16], F32, sS); d_m16 = Dep()
                selbB = [P.sb([1, 256], BF16, sS) for _ in range(2)]; d_selbB = [Dep(), Dep()]
                o4s = [banks[2][0:SB, 0:65].rearrange("p (g c) -> p g c", g=1), banks[3][0:SB, 0:65].rearrange("p (g c) -> p g c", g=1)]
                d_o4s = [d_bank[2], d_bank[3]]
                ones4f = oh4
                onec = P.sb([SB, 1], F32, sS); d_onec = Dep()
                P.op("dve", lambda e: e.memset(onec[:], 1.0), writes=[d_onec])

                def evac_s(b, kvh, br_i):
                    attn.flush()
                    o4v, d_o = o4s[kvh], d_o4s[kvh]
                    P.op("dve", lambda e: e.tensor_scalar(out=rd1[:, 0:1], in0=o4v[:, 0, 64:65], scalar1=1e-30, scalar2=None, op0=ALU.max), reads=[d_o], writes=[d_rd1])
                    P.op("dve", lambda e: e.reciprocal(out=rd1[:, 0:1], in_=rd1[:, 0:1]), reads=[d_rd1], writes=[d_rd1])
                    P.op("dve", lambda e: e.tensor_scalar(out=og[:, 0:64], in0=o4v[:, 0, 0:64], scalar1=rd1[:, 0:1], scalar2=None, op0=ALU.mult),
                         reads=[d_o, d_rd1], writes=[d_og])
                    P.dma("sp", lambda e: e.dma_start(out=os_d[b, br_i, kvh * 4:(kvh + 1) * 4, :], in_=og[:, 0:64]), reads=[d_og])

                for b in range(SB):
                    P.dma("sp", lambda e, b=b: e.dma_start(out=ptb[:, 0:1], in_=ptab[b].rearrange("(p o) -> p o", o=1)), writes=[d_ptb])
                    P.dma("sp", lambda e, b=b: e.dma_start(out=ptb[0:127, 1:2], in_=ptab[b, 1:128].rearrange("(p o) -> p o", o=1)), writes=[d_ptb])
                    P.dma("sp", lambda e, b=b: e.dma_start(out=ptb[127:128, 1:2], in_=ptab[b, 127:128].rearrange("(p o) -> p o", o=1)), writes=[d_ptb])
                    for k in range(8):
                        P.op("dve", lambda e, k=k: e.tensor_scalar(out=idx8[:, k:k + 1], in0=ptb[:, 0:1], scalar1=8, scalar2=k, op0=ALU.mult, op1=ALU.add),
                             reads=[d_ptb], writes=[d_idx8])
                    P.op("dve", lambda e: e.tensor_scalar(out=idx8[:, 8:9], in0=ptb[:, 1:2], scalar1=8, scalar2=0, op0=ALU.mult, op1=ALU.add),
                         reads=[d_ptb], writes=[d_idx8])
                    for k in range(9):
                        gb_ = k % 2
                        P.dma("pool", lambda e, k=k, gb_=gb_: e.indirect_dma_start(
                            out=G[gb_][:].rearrange("p a b -> p (a b)"), out_offset=None, in_=cache_cmp[:, :],
                            in_offset=bass.IndirectOffsetOnAxis(ap=idx8[:, k:k + 1], axis=0)), reads=[d_idx8], writes=[d_G[gb_]])
                        for r in range(2):
                            if k == 8 and r == 0:
                                continue
                            P.op("pool", lambda e, gb_=gb_, r=r: e.tensor_tensor(out=prod[r][:], in0=G[gb_][:], in1=wbc[:, r, :, :], op=ALU.mult),
                                 reads=[d_G[gb_], d_wbc], writes=[d_prod[r]])
                            dstF = Fs[:, k, :] if r == 0 else S2[:, k, :]
                            P.op("dve", lambda e, r=r, dstF=dstF: e.tensor_reduce(out=dstF, in_=prod[r][:].rearrange("p j f -> p f j"), axis=AX.X, op=ALU.add),
                                 reads=[d_prod[r]], writes=[d_Fs if r == 0 else d_S2])
                    P.op("dve", lambda e: e.tensor_tensor(out=Fs[:], in0=Fs[:], in1=S2[:, 1:9, :], op=ALU.add), reads=[d_Fs, d_S2], writes=[d_Fs])
                    P.op("dve", lambda e: e.tensor_tensor(out=Fs[:], in0=Fs[:], in1=brow[:].unsqueeze(1).to_broadcast([128, 8, 256]), op=ALU.add),
                         reads=[d_Fs, d_brow], writes=[d_Fs])
                    for s_ in range(8):
                        P.op("pe", lambda e, s_=s_: e.transpose(out=pT[:, s_ * 128:(s_ + 1) * 128], in_=Fs[:, s_, 0:128], identity=ident[:]),
                             reads=[d_Fs, d_ident], writes=[d_pTs[s_ // 4]])
                    P.op("act", lambda e: e.copy(out=cmpKTs[:].rearrange("p a b -> p (a b)"), in_=pT[:]), reads=d_pTs, writes=[d_cmpKTs])
                    P.op("pool", lambda e: e.tensor_copy(out=cmpVAs[:, :, :, 0:64], in_=Fs[:, :, 128:256].rearrange("p s (h d) -> p s h d", h=2)),
                         reads=[d_Fs], writes=[d_cmpVAs])
                    for kvh in range(2):
                        QTh = QTs[kvh * 64:(kvh + 1) * 64, :, b]
                        tiles = []
                        for s_ in range(8):
                            bs = [(cm7row, ones4)] if s_ == 7 else []
                            tiles.append({"KT": cmpKTs[kvh * 64:(kvh + 1) * 64, s_, :], "V": cmpVAs[:, s_, kvh, :], "nk": 128, "biases": bs,
                                          "Ov": OvS[:, s_, :], "deps": [d_cmpKTs, d_cmpVAs, d_small, d_OvS]})
                        psvS = lambda g: banks[4][0:SB, 0:257]
                        attn(SB, 1, QTh, d_QTs, tiles, o4s[kvh], d_o4s[kvh], pslc=(psvS, [d_bank[4]], 1))
                        evac_s(b, kvh, 0)
                        P.op("dve", lambda e: e.tensor_scalar(out=rd1[:, 1:2], in0=banks[4][0:SB, 256:257], scalar1=1e-30, scalar2=None, op0=ALU.max),
                             reads=[d_bank[4]], writes=[d_rd1])
                        P.op("dve", lambda e: e.reciprocal(out=rd1[:, 1:2], in_=rd1[:, 1:2]), reads=[d_rd1], writes=[d_rd1])
                        P.op("dve", lambda e: e.tensor_scalar(out=pn[:], in0=banks[4][0:SB, 0:256], scalar1=rd1[:, 1:2], scalar2=None, op0=ALU.mult),
                             reads=[d_bank[4], d_rd1], writes=[d_pn])
                        P.op("pe", lambda e: e.matmul(out=banks[5][0:1, 0:256], lhsT=onec[:, 0:1], rhs=pn[:], start=True, stop=True),
                             reads=[d_onec, d_pn], writes=[d_bank[5]])
                        P.op("dve", lambda e: e.tensor_tensor(out=srow[:], in0=banks[5][0:1, 0:256], in1=forceS, op=ALU.add), reads=[d_bank[5], d_small], writes=[d_srow])
                        P.op("dve", lambda e: e.max(out=m16[:, 0:8], in_=srow[:]), reads=[d_srow], writes=[d_m16])
                        P.op("dve", lambda e: e.match_replace(out=wk2[:], in_to_replace=m16[:, 0:8], in_values=srow[:], imm_value=-3.0e38),
                             reads=[d_srow, d_m16], writes=[d_wk2])
                        P.op("dve", lambda e: e.max(out=m16[:, 8:16], in_=wk2[:]), reads=[d_wk2], writes=[d_m16])
                        P.op("dve", lambda e, kvh=kvh: e.tensor_scalar(out=selbB[kvh][:], in0=srow[:], scalar1=m16[:, 14:15], scalar2=NEG, op0=ALU.is_lt, op1=ALU.mult),
                             reads=[d_srow, d_m16], writes=[d_selbB[kvh]])
                    for k in range(8):
                        gb_ = k % 2
                        P.dma("pool", lambda e, k=k, gb_=gb_: e.indirect_dma_start(
                            out=G[gb_][:].rearrange("p a b -> p (a b)"), out_offset=None, in_=cache_sel[:, :],
                            in_offset=bass.IndirectOffsetOnAxis(ap=idx8[:, k:k + 1], axis=0)), reads=[d_idx8], writes=[d_G[gb_]])
                        for j in range(16):
                            if j % 4 == 0:
                                tb_, d_tb = (banks[4], d_bank[4]) if (j // 4) % 2 == 0 else (banks[5], d_bank[5])
                            P.op("pe", lambda e, gb_=gb_, j=j, tb_=tb_: e.transpose(out=tb_[:, (j % 4) * 128:(j % 4 + 1) * 128], in_=G[gb_][:, j, 0:128], identity=ident[:]),
                                 reads=[d_G[gb_], d_ident], writes=[d_tb])
                            if j % 4 == 3:
                                P.op("act", lambda e, gb_=gb_, j=j, tb_=tb_: e.copy(out=KT16[gb_][:, j - 3:j + 1, :].rearrange("p a b -> p (a b)"), in_=tb_[:, :]),
                                     reads=[d_tb], writes=[d_KT16[gb_]])
                        P.op("pool", lambda e, gb_=gb_: e.tensor_copy(out=V16[gb_][:, :, :, 0:64], in_=G[gb_][:, :, 128:256].rearrange("p j (h d) -> p j h d", h=2)),
                             reads=[d_G[gb_]], writes=[d_V16[gb_]])
                        for kvh in range(2):
                            QTh = QTs[kvh * 64:(kvh + 1) * 64, :, b]
                            tiles = []
                            for j in range(16):
                                hi = 1 if (16 * k + j) >= 64 else 0
                                tiles.append({"KT": KT16[gb_][kvh * 64:(kvh + 1) * 64, j, :], "V": V16[gb_][:, j, kvh, :], "nk": 128,
                                              "biases": [(selbB[kvh][0:1, hi:256:2], ones4)], "deps": [d_KT16[gb_], d_V16[gb_], d_selbB[kvh], d_small]})
                            attn(SB, 1, QTh, d_QTs, tiles, o4s[kvh], d_o4s[kvh], first=(k == 0), last=False)
                    for kvh in range(2):
                        QTh = QTs[kvh * 64:(kvh + 1) * 64, :, b]
                        tiles = [{"KT": KnT[kvh * 64:(kvh + 1) * 64, 0, :], "V": VnA[:, 0, kvh, :], "nk": SB,
                                  "biases": [(smallb[0:1, 256 + 4 * b:260 + 4 * b], ones4)], "deps": [d_KnT, d_VnA, d_small]}]
                        attn(SB, 1, QTh, d_QTs, tiles, o4s[kvh], d_o4s[kvh], first=False, last=True)
                        evac_s(b, kvh, 1)
                    P.dma("sp", lambda e, b=b: e.dma_start(out=Wt[:], in_=cache_win[b].rearrange("(m p) f -> p m f", p=128)), writes=[d_Wt])
                    for m in range(4):
                        P.op("pe", lambda e, m=m: e.transpose(out=banks[4][:, m * 128:(m + 1) * 128], in_=Wt[:, m, 0:128], identity=ident[:]),
                             reads=[d_Wt, d_ident], writes=[d_bank[4]])
                    P.op("act", lambda e: e.copy(out=KTw[:].rearrange("p a b -> p (a b)"), in_=banks[4][:, :]), reads=[d_bank[4]], writes=[d_KTw])
                    P.op("pool", lambda e: e.tensor_copy(out=VwA[:, :, :, 0:64], in_=Wt[:, :, 128:256].rearrange("p m (h d) -> p m h d", h=2)),
                         reads=[d_Wt], writes=[d_VwA])
                    for kvh in range(2):
                        QTh = QTs[kvh * 64:(kvh + 1) * 64, :, b]
                        tiles = []
                        for m in range(4):
                            bs = [(wm0row, ones4)] if m == 0 else []
                            tiles.append({"KT": KTw[kvh * 64:(kvh + 1) * 64, m, :], "V": VwA[:, m, kvh, :], "nk": 128, "biases": bs,
                                          "deps": [d_KTw, d_VwA, d_small]})
                        tiles.append({"KT": KnT[kvh * 64:(kvh + 1) * 64, 1, :], "V": VnA[:, 1, kvh, :], "nk": SB,
                                      "biases": [(smallb[0:1, 256 + 4 * b:260 + 4 * b], ones4)], "deps": [d_KnT, d_VnA, d_small]})
                        attn(SB, 1, QTh, d_QTs, tiles, o4s[kvh], d_o4s[kvh])
                        evac_s(b, kvh, 2)
                P.barrier()
                osall = P.sb([SB, 3, 8, 64], F32, sS); d_osall = Dep()
                onsa = P.sb([SB, 8, 64], F32, sS); d_onsa = Dep()
                tmpo = P.sb([SB, 8, 64], F32, sS); d_tmpo = Dep()
                P.dma("sp", lambda e: e.dma_start(out=osall[:].rearrange("p a b c -> p (a b c)"), in_=os_d.rearrange("b r h d -> b (r h d)")), writes=[d_osall])
                sgv = sg[:].rearrange("p (h r) -> p h r", r=3)
                for r in range(3):
                    dst_, d_dst_ = (onsa, d_onsa) if r == 0 else (tmpo, d_tmpo)
                    P.op("dve", lambda e, r=r, dst_=dst_: e.tensor_tensor(out=dst_[:], in0=osall[:, r, :, :], in1=sgv[:, :, r].unsqueeze(2).to_broadcast([SB, 8, 64]), op=ALU.mult),
                         reads=[d_osall, d_sg], writes=[d_dst_])
                    if r > 0:
                        P.op("dve", lambda e: e.tensor_tensor(out=onsa[:], in0=onsa[:], in1=tmpo[:], op=ALU.add), reads=[d_onsa, d_tmpo], writes=[d_onsa])
                onTs = P.sb([128, 4, SB], BF16, sS); d_onTs = Dep()
                onf = onsa[:].rearrange("p h d -> p (h d)")
                P.dma("sp", lambda e: e.dma_start(out=dbg_ons, in_=onf), reads=[d_onsa])
                for c in range(4):
                    P.op("pe", lambda e, c=c: e.transpose(out=banks[3][:, c * SB:(c + 1) * SB], in_=onf[:, c * 128:(c + 1) * 128], identity=ident[0:SB, 0:SB]),
                         reads=[d_onsa, d_ident], writes=[d_bank[3]])
                P.op("act", lambda e: e.copy(out=onTs[:].rearrange("p a b -> p (a b)"), in_=banks[3][:, 0:4 * SB]), reads=[d_bank[3]], writes=[d_onTs])
                P.dma("sp", lambda e: e.dma_start(out=onTs_d, in_=onTs[:]), reads=[d_onTs])
                P.barrier()

        def phase_b2(groups):
            with ExitStack() as s2:
                Wgab = P.sb([128, 8, 2048], BF16, s2); d_Wgab = Dep()
                Wbn = P.sb([128, 4, 1024], BF16, s2); d_Wbn = Dep()
                Wbr = P.sb([128, 8, 1024], BF16, s2); d_Wbr = Dep()
                Wout = P.sb([128, 8, 1024], BF16, s2); d_Wout = Dep()
                P.dma("pool", lambda e: e.dma_start(out=Wgab[:], in_=w_in[:, C_GA:C_GA + 2048].rearrange("(c p) n -> p c n", p=128)), writes=[d_Wgab])
                P.dma("pool", lambda e: e.dma_start(out=Wbn[:], in_=w_bn.rearrange("(c p) n -> p c n", p=128)), writes=[d_Wbn])
                P.dma("pool", lambda e: e.dma_start(out=Wbr[:], in_=w_br.rearrange("(c p) n -> p c n", p=128)), writes=[d_Wbr])
                P.dma("pool", lambda e: e.dma_start(out=Wout[:], in_=w_out.rearrange("(c p) n -> p c n", p=128)), writes=[d_Wout])
                hT4 = P.sb([128, 8, 512], BF16, s2); d_hT4 = Dep()
                onT4 = P.sb([128, 4, 512], BF16, s2); d_onT4 = Dep()
                yrT4 = P.sb([128, 8, 512], BF16, s2); d_yrT4 = Dep()
                mixT = P.sb([128, 8, 512], BF16, s2); d_mixT = Dep()
                sga = [P.sb([128, 512], F32, s2) for _ in range(2)]; d_sga = [Dep(), Dep()]
                sgb = [P.sb([128, 512], F32, s2) for _ in range(2)]; d_sgb = [Dep(), Dep()]
                t1 = [P.sb([128, 512], F32, s2) for _ in range(2)]; d_t1 = [Dep(), Dep()]
                t2 = [P.sb([128, 512], F32, s2) for _ in range(2)]; d_t2 = [Dep(), Dep()]
                xa = [P.sb([128, D], F32, s2) for _ in range(2)]; d_xa = [Dep(), Dep()]
                x1t = [P.sb([128, D], F32, s2) for _ in range(2)]; d_x1t = [Dep(), Dep()]
                gtaS = P.sb([SB, D], F32, s2); d_gtaS = Dep()
                P.dma("sp", lambda e: e.dma_start(out=gtaS[:], in_=mods_d[:, 2 * D:3 * D]), writes=[d_gtaS])
                for grp in groups:
                    n = grp["ntok"]
                    grp["load_a"](hT4, d_hT4, onT4, d_onT4, yrT4, d_yrT4)
                    for fc in range(8):
                        b = fc % 2
                        for (bk, W, dW, kc, c0, src, dsrc) in ((0, Wgab, d_Wgab, 8, fc * 128, hT4, d_hT4), (1, Wgab, d_Wgab, 8, 1024 + fc * 128, hT4, d_hT4),
                                                              (2, Wbn, d_Wbn, 4, fc * 128, onT4, d_onT4), (3, Wbr, d_Wbr, 8, fc * 128, yrT4, d_yrT4)):
                            for k in range(kc):
                                P.op("pe", lambda e, bk=bk, W=W, k=k, c0=c0, src=src, kc=kc, n=n: e.matmul(
                                    out=banks[bk][:, 0:n], lhsT=W[:, k, c0:c0 + 128], rhs=src[:, k, 0:n], start=(k == 0), stop=(k == kc - 1)),
                                    reads=[dW, dsrc], writes=[d_bank[bk]])
                        P.op("act", lambda e, b=b, n=n: e.activation(out=sga[b][:, 0:n], in_=banks[0][:, 0:n], func=ACTF.Sigmoid), reads=[d_bank[0]], writes=[d_sga[b]])
                        P.op("act", lambda e, b=b, n=n: e.activation(out=sgb[b][:, 0:n], in_=banks[1][:, 0:n], func=ACTF.Sigmoid), reads=[d_bank[1]], writes=[d_sgb[b]])
                        P.op("dve", lambda e, b=b, n=n: e.tensor_tensor(out=t1[b][:, 0:n], in0=banks[2][:, 0:n], in1=sga[b][:, 0:n], op=ALU.mult),
                             reads=[d_bank[2], d_sga[b]], writes=[d_t1[b]])
                        P.op("dve", lambda e, b=b, n=n: e.tensor_tensor(out=t2[b][:, 0:n], in0=banks[3][:, 0:n], in1=sgb[b][:, 0:n], op=ALU.mult),
                             reads=[d_bank[3], d_sgb[b]], writes=[d_t2[b]])
                        P.op("pool", lambda e, b=b, fc=fc, n=n: e.tensor_tensor(out=mixT[:, fc, 0:n], in0=t1[b][:, 0:n], in1=t2[b][:, 0:n], op=ALU.add),
                             reads=[d_t1[b], d_t2[b]], writes=[d_mixT])
                    off = 0
                    for ti, (xsrc, nt_, gta, d_gta, x1dst) in enumerate(grp["tiles_a"](gtaS, d_gtaS)):
                        b = ti % 2
                        P.dma("sp", lambda e, b=b, xsrc=xsrc, nt_=nt_: e.dma_start(out=xa[b][0:nt_, :], in_=xsrc), writes=[d_xa[b]])
                        for half in range(2):
                            bk = 4 + half
                            for k in range(8):
                                P.op("pe", lambda e, bk=bk, k=k, off=off, nt_=nt_, half=half: e.matmul(
                                    out=banks[bk][0:nt_, :], lhsT=mixT[:, k, off:off + nt_], rhs=Wout[:, k, half * 512:(half + 1) * 512],
                                    start=(k == 0), stop=(k == 7)), reads=[d_mixT, d_Wout], writes=[d_bank[bk]])
                            P.op("dve", lambda e, bk=bk, b=b, nt_=nt_, half=half, gta=gta: e.tensor_tensor(
                                out=x1t[b][0:nt_, half * 512:(half + 1) * 512], in0=banks[bk][0:nt_, :], in1=gta[0:nt_, half * 512:(half + 1) * 512], op=ALU.mult),
                                reads=[d_bank[bk], d_gta], writes=[d_x1t[b]])
                        P.op("pool", lambda e, b=b, nt_=nt_: e.tensor_tensor(out=x1t[b][0:nt_, :], in0=x1t[b][0:nt_, :], in1=xa[b][0:nt_, :], op=ALU.add),
                             reads=[d_x1t[b], d_xa[b]], writes=[d_x1t[b]])
                        P.dma("sp", lambda e, b=b, nt_=nt_, x1dst=x1dst: e.dma_start(out=x1dst, in_=x1t[b][0:nt_, :]), reads=[d_x1t[b]])
                        off += nt_
                P.barrier()
            with ExitStack() as s3:
                Wup = P.sb([128, 8, 4096], BF16, s3); d_Wup = Dep()
                Wdn = P.sb([128, 32, 1024], BF16, s3); d_Wdn = Dep()
                for q4 in range(4):
                    P.dma("pool", lambda e, q4=q4: e.dma_start(out=Wup[:, :, q4 * 1024:(q4 + 1) * 1024],
                                                              in_=w_up[:, q4 * 1024:(q4 + 1) * 1024].rearrange("(c p) n -> p c n", p=128)), writes=[d_Wup])
                    P.dma("pool", lambda e, q4=q4: e.dma_start(out=Wdn[:, q4 * 8:(q4 + 1) * 8, :],
                                                              in_=w_down[q4 * 1024:(q4 + 1) * 1024, :].rearrange("(c p) n -> p c n", p=128)), writes=[d_Wdn])
                nfb = P.sb([128, D], F32, s3); d_nfb = Dep()
                P.dma("sp", lambda e: e.dma_start(out=nfb[:], in_=norm_final_d.partition_broadcast(128)[:, 0, :]), writes=[d_nfb])
                h2T4 = P.sb([128, 8, 256], BF16, s3); d_h2T4 = Dep()
                uT = P.sb([128, 32, 256], BF16, s3); d_uT = Dep()
                x1in = [P.sb([128, D], F32, s3) for _ in range(2)]; d_x1in = [Dep(), Dep()]
                rl = [P.sb([128, 256], BF16, s3) for _ in range(2)]; d_rl = [Dep(), Dep()]
                x2t_ = P.sb([128, D], F32, s3); x2t = [x2t_, x2t_]; d_x2t_ = Dep(); d_x2t = [d_x2t_, d_x2t_]
                ss2 = P.sb([128, 2], F32, s3); d_ss2 = Dep()
                wk3 = mk_wk(s3)
                modS3 = P.sb([SB, 3, D], BF16, s3); d_modS3 = Dep()
                P.dma("pool", lambda e: e.dma_start(out=modS3[:], in_=mods_d[:, 3 * D:6 * D].rearrange("s (a d) -> s a d", a=3)), writes=[d_modS3])
                P.dma("sp", lambda e: e.dma_start(out=x2t[1][0:SB, :], in_=norm_mlp.partition_broadcast(SB)[:, 0, :]), writes=[d_x2t[1]])
                P.op("dve", lambda e: e.scalar_tensor_tensor(out=modS3[:, 1, :], in0=modS3[:, 1, :], scalar=1.0, in1=x2t[1][0:SB, :], op0=ALU.add, op1=ALU.mult),
                     reads=[d_modS3, d_x2t[1]], writes=[d_modS3])
                for grp in groups:
                    tl_all = grp["tiles_b"](modS3, d_modS3)
                    for s0_ in range(0, len(tl_all), 2):
                        tl_b = tl_all[s0_:s0_ + 2]
                        n = sum(t_[1] for t_ in tl_b)
                        off = 0
                        for ti, (x1src, nt_, G2, d_G2, shf, d_shf, gtf, d_gtf, ydst, shc) in enumerate(tl_b):
                            P.dma("sp", lambda e, ti=ti, x1src=x1src, nt_=nt_: e.dma_start(out=x1in[ti][0:nt_, :], in_=x1src), writes=[d_x1in[ti]])
                            norm_mod_T(x1in[ti], d_x1in[ti], nt_, G2, d_G2, shf, d_shf, wk3, h2T4[:, :, off:off + nt_], d_h2T4, shc, d_shcol)
                            off += nt_
                        for fc in range(32):
                            b = fc % 2
                            bk = fc % 4
                            for k in range(8):
                                P.op("pe", lambda e, bk=bk, k=k, fc=fc, n=n: e.matmul(out=banks[bk][:, 0:n], lhsT=Wup[:, k, fc * 128:(fc + 1) * 128], rhs=h2T4[:, k, 0:n],
                                                                                    start=(k == 0), stop=(k == 7)), reads=[d_Wup, d_h2T4], writes=[d_bank[bk]])
                            P.op("act", lambda e, bk=bk, b=b, n=n: e.activation(out=rl[b][:, 0:n], in_=banks[bk][:, 0:n], func=ACTF.Relu), reads=[d_bank[bk]], writes=[d_rl[b]])
                            P.op("pool", lambda e, b=b, fc=fc, n=n: e.tensor_tensor(out=uT[:, fc, 0:n], in0=rl[b][:, 0:n], in1=rl[b][:, 0:n], op=ALU.mult),
                                 reads=[d_rl[b]], writes=[d_uT])
                        off = 0
                        for ti, (x1src, nt_, G2, d_G2, shf, d_shf, gtf, d_gtf, ydst, shc) in enumerate(tl_b):
                            b = ti % 2
                            for half in range(2):
                                bk = 4 + half
                                for k in range(32):
                                    P.op("pe", lambda e, bk=bk, k=k, off=off, nt_=nt_, half=half: e.matmul(
                                        out=banks[bk][0:nt_, :], lhsT=uT[:, k, off:off + nt_], rhs=Wdn[:, k, half * 512:(half + 1) * 512],
                                        start=(k == 0), stop=(k == 31)), reads=[d_uT, d_Wdn], writes=[d_bank[bk]])
                                P.op("dve", lambda e, bk=bk, b=b, nt_=nt_, half=half, gtf=gtf: e.tensor_tensor(
                                    out=x2t[b][0:nt_, half * 512:(half + 1) * 512], in0=banks[bk][0:nt_, :], in1=gtf[0:nt_, half * 512:(half + 1) * 512], op=ALU.mult),
                                    reads=[d_bank[bk], d_gtf], writes=[d_x2t[b]])
                            P.op("pool", lambda e, b=b, ti=ti, nt_=nt_: e.tensor_tensor(out=x2t[b][0:nt_, :], in0=x2t[b][0:nt_, :], in1=x1in[ti][0:nt_, :], op=ALU.add),
                                 reads=[d_x2t[b], d_x1in[ti]], writes=[d_x2t[b]])
                            P.op("act", lambda e, b=b, nt_=nt_: e.activation(out=junk_sh[0:nt_, :], in_=x2t[b][0:nt_, :], func=ACTF.Square, accum_out=ss2[0:nt_, 0:1]),
                                 reads=[d_x2t[b]], writes=[d_junk_sh, d_ss2])
                            P.op("act", lambda e, nt_=nt_: e.activation(out=ss2[0:nt_, 1:2], in_=ss2[0:nt_, 0:1], func=ACTF.Sqrt, bias=epsT[0:nt_, :], scale=1.0 / D),
                                 reads=[d_ss2, d_eps], writes=[d_ss2])
                            P.op("dve", lambda e, nt_=nt_: e.reciprocal(out=ss2[0:nt_, 1:2], in_=ss2[0:nt_, 1:2]), reads=[d_ss2], writes=[d_ss2])
                            P.op("dve", lambda e, b=b, nt_=nt_: e.scalar_tensor_tensor(out=x2t[b][0:nt_, :], in0=x2t[b][0:nt_, :], scalar=ss2[0:nt_, 1:2], in1=nfb[0:nt_, :],
                                                                                     op0=ALU.mult, op1=ALU.mult), reads=[d_x2t[b], d_ss2, d_nfb], writes=[d_x2t[b]])
                            P.dma("sp", lambda e, b=b, nt_=nt_, ydst=ydst: e.dma_start(out=ydst, in_=x2t[b][0:nt_, :]), reads=[d_x2t[b]])
                            off += nt_
                P.barrier()

        with ExitStack() as sA:
            WA = P.sb([128, 8, 2304], BF16, sA); d_WA = Dep()
            segs = [(C_KVC, 128, 0), (C_KVS, 128, 128), (C_KVW, 128, 256), (C_KVC + 128, 128, 384),
                    (C_KVS + 128, 128, 512), (C_KVW + 128, 128, 640), (C_RK, 512, 768), (C_RV, 1024, 1280)]
            for (src, n, dst) in segs:
                P.dma("pool", lambda e, src=src, n=n, dst=dst: e.dma_start(
                    out=WA[:, :, dst:dst + n], in_=w_in[:, src:src + n].rearrange("(c p) n -> p c n", p=128)), writes=[d_WA])
            inda = P.sb([128, 8], F32, sA); indab = P.sb([128, 8], BF16, sA); d_inda = Dep()
            wrep = P.sb([128, 512], F32, sA); d_wrep = Dep()
            bcol = P.sb([128, 2], F32, sA); d_bcol = Dep()
            kdsc = P.sb([128, 4], F32, sA); d_kdsc = Dep()
            oh = P.sb([128, 8], F32, sA); d_oh = Dep()
            P.dma("sp", lambda e: e.dma_start(out=inda[:], in_=inda_d), writes=[d_inda])
            P.op("dve", lambda e: e.tensor_copy(out=indab[:], in_=inda[:]), reads=[d_inda], writes=[d_inda])
            P.dma("sp", lambda e: e.dma_start(out=wrep[:], in_=wrep_d), writes=[d_wrep])
            P.dma("sp", lambda e: e.dma_start(out=bcol[:], in_=bcol_d), writes=[d_bcol])
            P.dma("sp", lambda e: e.dma_start(out=kdsc[:], in_=kdsc_d), writes=[d_kdsc])
            P.dma("sp", lambda e: e.dma_start(out=oh[:], in_=oh_d), writes=[d_oh])
            FS = P.sb([128, 4, 1024], F32, sA); d_FS = Dep()
            S = P.sb([128, 4, 256], F32, sA); d_S = Dep()
            Sown = P.sb([128, 1024], F32, sA); d_Sown = Dep()
            tmpS = P.sb([128, 1024], F32, sA); d_tmpS = Dep()
            P.op("pool", lambda e: e.memset(S[:].rearrange("p a b -> p (a b)"), 0.0), writes=[d_S])
            NBUF = 2
            xts = [P.sb([128, D], F32, sA) for _ in range(NBUF)]; d_xts = [Dep() for _ in range(NBUF)]
            rps = [P.sb([128, 192], F32, sA) for _ in range(NBUF)]; d_rps = [Dep() for _ in range(NBUF)]
            wks = [mk_wk(sA) for _ in range(NBUF)]
            hTs = [P.sb([128, 8, 128], BF16, sA) for _ in range(NBUF)]; d_hTs = [Dep() for _ in range(NBUF)]
            kvos = [P.sb([128, 3, 256], F32, sA) for _ in range(NBUF)]; d_kvos = [Dep() for _ in range(NBUF)]
            kraw = [P.sb([128, 896], F32, sA) for _ in range(NBUF)]; d_kraw = [Dep() for _ in range(NBUF)]
            tm = [[P.sb([128, 256], F32, sA) for _ in range(4)] for _ in range(NBUF)]
            d_tm = [[Dep() for _ in range(4)] for _ in range(NBUF)]
            rkr = [P.sb([128, 4, 128], F32, sA) for _ in range(NBUF)]; d_rkr = [Dep() for _ in range(NBUF)]
            kd = [P.sb([128, 4, 128], BF16, sA) for _ in range(NBUF)]; d_kd = [Dep() for _ in range(NBUF)]
            vb = [P.sb([128, 1024], BF16, sA) for _ in range(NBUF)]; d_vb = [Dep() for _ in range(NBUF)]
            cw = [P.sb([128, 2, 256], BF16, sA) for _ in range(NBUF)]; d_cw = [Dep() for _ in range(NBUF)]
            kwt = [P.sb([128, 128], BF16, sA) for _ in range(NBUF)]; d_kwt = [Dep() for _ in range(NBUF)]
            kst = [P.sb([128, 128], BF16, sA) for _ in range(NBUF)]; d_kst = [Dep() for _ in range(NBUF)]
            vwb = [P.sb([128, 2, 65], BF16, sA) for _ in range(NBUF)]; d_vwb = [Dep() for _ in range(NBUF)]
            vsb = [P.sb([128, 2, 65], BF16, sA) for _ in range(NBUF)]; d_vsb = [Dep() for _ in range(NBUF)]
            for b_ in range(NBUF):
                P.op("pool", lambda e, b_=b_: e.memset(vwb[b_][:].rearrange("p a b -> p (a b)"), 1.0), writes=[d_vwb[b_]])
                P.op("pool", lambda e, b_=b_: e.memset(vsb[b_][:].rearrange("p a b -> p (a b)"), 1.0), writes=[d_vsb[b_]])
            zA, zB, zK, zV0, zV1, zC = banks
            dzA, dzB, dzK, dzV0, dzV1, dzC = d_bank
            d_zBt = dzB

            ntA = NT if stage >= 1 else 0
            if 'K_NTA' in os.environ:
                ntA = int(os.environ['K_NTA'])
            if stage == 2:
                ntA = 8 * DBG_NBLK
                P.op("pool", lambda e: e.memset(FS[:].rearrange("p a b -> p (a b)"), 0.0), writes=[d_FS])
            for tt in range(ntA):
                b = tt % NBUF
                xt, d_xt, rp, d_rp, hT, d_hT, kvo, d_kvo = xts[b], d_xts[b], rps[b], d_rps[b], hTs[b], d_hTs[b], kvos[b], d_kvos[b]
                P.dma("sp", lambda e, tt=tt, xt=xt: e.dma_start(out=xt[:], in_=x_all[tt * 128:(tt + 1) * 128, :]), writes=[d_xt])
                P.dma("sp", lambda e, tt=tt, rp=rp: e.dma_start(out=rp[:], in_=rope_all[tt * 128:(tt + 1) * 128, :]), writes=[d_rp])
                norm_mod_T(xt, d_xt, 128, G1P, d_G1P, modP4[:, 0, :], d_modP, wks[b], hT, d_hT, shcol[:, 0, :], d_shcol)
                for (zz, dz, c0, n) in ((zA, dzA, 0, 512), (zB, dzB, 512, 256), (zK, dzK, 768, 512), (zV0, dzV0, 1280, 512), (zV1, dzV1, 1792, 512)):
                    for k in range(8):
                        P.op("pe", lambda e, zz=zz, k=k, c0=c0, n=n, hT=hT: e.matmul(out=zz[:, 0:n], lhsT=hT[:, k, :], rhs=WA[:, k, c0:c0 + n],
                                                                                    start=(k == 0), stop=(k == 7)),
                             reads=[d_hT, d_WA], writes=[dz])
                P.op("act", lambda e, b=b: e.copy(out=kraw[b][:, 0:384], in_=zA[:, 0:384]), reads=[dzA], writes=[d_kraw[b]])
                P.op("act", lambda e, b=b: e.copy(out=kraw[b][:, 384:896], in_=zK[:, 0:512]), reads=[dzK], writes=[d_kraw[b]])
                src = kraw[b][:, 0:384].rearrange("p (j h two e) -> p j h two e", j=3, h=2, two=2)
                dst = kvo[:, :, 0:128].rearrange("p j (h two e) -> p j h two e", h=2, two=2)
                shp = [128, 3, 2, 32]
                tv = [t[:, 0:192].rearrange("p (j h e) -> p j h e", j=3, h=2) for t in tm[b]]
                cosn = rp[:, 0:32].unsqueeze(1).unsqueeze(1).to_broadcast(shp)
                sinn = rp[:, 32:64].unsqueeze(1).unsqueeze(1).to_broadcast(shp)
                rope(lambda hf, src=src: src[:, :, :, hf, :], lambda hf, dst=dst: dst[:, :, :, hf, :], cosn, sinn, shp, tv,
                     d_kraw[b], d_kvo, d_rp, d_tm[b])
                P.op("act", lambda e, kvo=kvo: e.copy(out=kvo[:, 0, 128:256], in_=zA[:, 384:512]), reads=[dzA], writes=[d_kvo])
                P.op("act", lambda e, kvo=kvo: e.copy(out=kvo[:, 1:3, 128:256], in_=zB[:, 0:256].rearrange("p (j f) -> p j f", j=2)),
                     reads=[dzB], writes=[d_kvo])
                P.dma("sp", lambda e, tt=tt, kvo=kvo: e.dma_start(out=o_cmp[tt * 128:(tt + 1) * 128, :], in_=kvo[:, 0, :]), reads=[d_kvo])
                P.dma("sp", lambda e, tt=tt, kvo=kvo: e.dma_start(out=o_sel[tt * 128:(tt + 1) * 128, :], in_=kvo[:, 1, :]), reads=[d_kvo])
                if tt >= NT - 4:
                    P.dma("sp", lambda e, tt=tt, kvo=kvo: e.dma_start(out=o_win[(tt - NT + 4) * 128:(tt - NT + 5) * 128, :], in_=kvo[:, 2, :]),
                          reads=[d_kvo])
                P.op("pool", lambda e, kvo=kvo, b=b: e.tensor_copy(out=vsb[b][:, :, 0:64], in_=kvo[:, 1, 128:256].rearrange("p (h d) -> p h d", h=2)),
                     reads=[d_kvo], writes=[d_vsb[b]])
                P.dma("sp", lambda e, tt=tt, b=b: e.dma_start(out=vs_d[tt * 128:(tt + 1) * 128, :], in_=vsb[b][:].rearrange("p a b -> p (a b)")),
                      reads=[d_vsb[b]])
                P.op("pool", lambda e, kvo=kvo, b=b: e.tensor_copy(out=vwb[b][:, :, 0:64], in_=kvo[:, 2, 128:256].rearrange("p (h d) -> p h d", h=2)),
                     reads=[d_kvo], writes=[d_vwb[b]])
                P.dma("sp", lambda e, tt=tt, b=b: e.dma_start(out=vw_d[tt * 128:(tt + 1) * 128, :], in_=vwb[b][:].rearrange("p a b -> p (a b)")),
                      reads=[d_vwb[b]])
                P.op("pe", lambda e, kvo=kvo: e.transpose(out=zB[:, 256:384], in_=kvo[:, 1, 0:128], identity=ident[:]),
                     reads=[d_kvo, d_ident], writes=[d_zBt])
                P.op("pe", lambda e, kvo=kvo: e.transpose(out=zB[:, 384:512], in_=kvo[:, 2, 0:128], identity=ident[:]),
                     reads=[d_kvo, d_ident], writes=[d_zBt])
                P.op("act", lambda e, b=b: e.copy(out=kst[b][:], in_=zB[:, 256:384]), reads=[d_zBt], writes=[d_kst[b]])
                P.dma("sp", lambda e, tt=tt, b=b: e.dma_start(out=ksT_d[:, tt * 128:(tt + 1) * 128], in_=kst[b][:]), reads=[d_kst[b]])
                P.op("act", lambda e, b=b: e.copy(out=kwt[b][:], in_=zB[:, 384:512]), reads=[d_zBt], writes=[d_kwt[b]])
                P.dma("sp", lambda e, tt=tt, b=b: e.dma_start(out=kwT_d[:, tt * 128:(tt + 1) * 128], in_=kwt[b][:]), reads=[d_kwt[b]])
                for r in range(2):
                    P.op("pool", lambda e, r=r, b=b, kvo=kvo: e.tensor_tensor(out=cw[b][:, r, :], in0=kvo[:, 0, :], in1=wrep[:, r * 256:(r + 1) * 256],
                                                                             op=ALU.mult), reads=[d_kvo, d_wrep], writes=[d_cw[b]])
                for r in range(2):
                    for kv in range(2):
                        i4 = r * 2 + kv
                        P.op("pe", lambda e, r=r, kv=kv, i4=i4, b=b: e.matmul(out=zC[:, i4 * 8:(i4 + 1) * 8], lhsT=cw[b][:, r, kv * 128:(kv + 1) * 128],
                                                                             rhs=indab[:], start=True, stop=True),
                             reads=[d_cw[b], d_inda], writes=[dzC])
                P.op("act", lambda e, tt=tt: e.copy(out=FS[:, :, tt * 8:(tt + 1) * 8], in_=zC[:, 0:32].rearrange("p (a s) -> p a s", a=4)),
                     reads=[dzC], writes=[d_FS])
                srck = kraw[b][:, 384:896].rearrange("p (h two e) -> p h two e", h=4, two=2)
                dstk = rkr[b][:].rearrange("p h (two e) -> p h two e", two=2)
                shpk = [128, 4, 64]
                tvk = [t[:, 0:256].rearrange("p (h e) -> p h e", h=4) for t in tm[b]]
                cosk = rp[:, 64:128].unsqueeze(1).to_broadcast(shpk)
                sink = rp[:, 128:192].unsqueeze(1).to_broadcast(shpk)
                rope(lambda hf, srck=srck: srck[:, :, hf, :], lambda hf, dstk=dstk: dstk[:, :, hf, :], cosk, sink, shpk, tvk,
                     d_kraw[b], d_rkr[b], d_rp, d_tm[b])
                P.op("pool", lambda e, b=b: e.tensor_tensor(out=kd[b][:], in0=rkr[b][:], in1=kdsc[:].unsqueeze(2).to_broadcast([128, 4, 128]),
                                                           op=ALU.mult), reads=[d_rkr[b], d_kdsc], writes=[d_kd[b]])
                P.op("act", lambda e, b=b: e.copy(out=vb[b][:, 0:512], in_=zV0[:, :]), reads=[dzV0], writes=[d_vb[b]])
                P.op("act", lambda e, b=b: e.copy(out=vb[b][:, 512:1024], in_=zV1[:, :]), reads=[dzV1], writes=[d_vb[b]])
                j = tt % 8
                if j == 0:
                    P.op("dve", lambda e: e.tensor_scalar(out=Sown[:], in0=S[:].rearrange("p a b -> p (a b)"), scalar1=oh[:, 0:1], scalar2=None,
                                                          op0=ALU.mult), reads=[d_S, d_oh], writes=[d_Sown])
                else:
                    P.op("dve", lambda e, j=j: e.scalar_tensor_tensor(out=Sown[:], in0=S[:].rearrange("p a b -> p (a b)"), scalar=oh[:, j:j + 1],
                                                                     in1=Sown[:], op0=ALU.mult, op1=ALU.add),
                         reads=[d_S, d_oh, d_Sown], writes=[d_Sown])
                if j == 7:
                    P.dma("sp", lambda e, i=tt // 8: e.dma_start(out=sown_d[i], in_=Sown[:]), reads=[d_Sown])
                for h in range(4):
                    zz, dz = (zV0, dzV0) if h < 2 else (zV1, dzV1)
                    c0 = (h % 2) * 256
                    P.op("pe", lambda e, h=h, zz=zz, c0=c0, b=b: e.matmul(out=zz[:, c0:c0 + 256], lhsT=kd[b][:, h, :], rhs=vb[b][:, h * 256:(h + 1) * 256],
                                                                         start=True, stop=True),
                         reads=[d_kd[b], d_vb[b]], writes=[dz])
                    P.op("dve", lambda e, h=h, zz=zz, c0=c0: e.scalar_tensor_tensor(out=S[:, h, :], in0=S[:, h, :], scalar=float(math.exp(128 * LG[h])),
                                                                                   in1=zz[:, c0:c0 + 256], op0=ALU.mult, op1=ALU.add),
                         reads=[d_S, dz], writes=[d_S])
            if ntA:
                P.dma("sp", lambda e: e.dma_start(out=o_state.rearrange("h k v -> k h v"), in_=S[:]), reads=[d_S])
                ctmp, cv = tmpS, Sown
                P.op("pool", lambda e: e.memset(cmpKT[:], 0.0), writes=[d_cmpKT])
                P.op("pool", lambda e: e.memset(cv[:], 0.0), reads=[], writes=[d_Sown])
                P.op("dve", lambda e: e.tensor_tensor(out=ctmp[:, 0:1023], in0=FS[:, 0, 0:1023], in1=FS[:, 2, 1:1024], op=ALU.add),
                     reads=[d_FS], writes=[d_tmpS])
                P.op("dve", lambda e: e.tensor_scalar(out=cmpKT[:, 0:1023], in0=ctmp[:, 0:1023], scalar1=bcol[:, 0:1], scalar2=None, op0=ALU.add),
                     reads=[d_tmpS, d_bcol], writes=[d_cmpKT])
                P.op("dve", lambda e: e.tensor_tensor(out=ctmp[:, 0:1023], in0=FS[:, 1, 0:1023], in1=FS[:, 3, 1:1024], op=ALU.add),
                     reads=[d_FS], writes=[d_tmpS])
                P.op("dve", lambda e: e.tensor_scalar(out=cv[:, 0:1023], in0=ctmp[:, 0:1023], scalar1=bcol[:, 1:2], scalar2=None, op0=ALU.add),
                     reads=[d_tmpS, d_bcol], writes=[d_Sown])
                for jt in range(8):
                    P.op("pe", lambda e, jt=jt: e.transpose(out=pT[:, jt * 128:(jt + 1) * 128], in_=cv[:, jt * 128:(jt + 1) * 128], identity=ident[:]),
                         reads=[d_Sown, d_ident], writes=[d_pTs[jt // 4]])
                P.op("act", lambda e: e.copy(out=cmpVA[:, :, :, 0:64], in_=pT[:].rearrange("p (j h d) -> p j h d", j=8, h=2)),
                     reads=d_pTs, writes=[d_cmpVA])
            P.barrier()

        if stage >= 2:
            phase_b1()
        if stage >= 4:
            phase_s1()
        if stage >= 3:
            groups = []
            for gq in range(4):
                def load_a(hT4, d_hT4, onT4, d_onT4, yrT4, d_yrT4, gq=gq):
                    for j in range(4):
                        i_ = 4 * gq + j
                        P.dma("sp", lambda e, i_=i_, j=j: e.dma_start(out=hT4[:, :, j * 128:(j + 1) * 128], in_=hT_d[i_]), writes=[d_hT4])
                        P.dma("sp", lambda e, i_=i_, j=j: e.dma_start(out=onT4[:, :, j * 128:(j + 1) * 128], in_=onT_d[i_]), writes=[d_onT4])
                        P.dma("sp", lambda e, i_=i_, j=j: e.dma_start(out=yrT4[:, :, j * 128:(j + 1) * 128], in_=yrT_d[i_]), writes=[d_yrT4])

                def tiles_a(gtaS, d_gtaS, gq=gq):
                    return [(x_own[(4 * gq + j) * 128:(4 * gq + j + 1) * 128, :], 128, modP4[:, 1, :], d_modP,
                             x1_d[(4 * gq + j) * 128:(4 * gq + j + 1) * 128, :]) for j in range(4)]

                def tiles_b(modS3, d_modS3, gq=gq):
                    return [(x1_d[(4 * gq + j) * 128:(4 * gq + j + 1) * 128, :], 128, G2P, d_G2P, modP4[:, 2, :], d_modP, modP4[:, 3, :], d_modP,
                             y_own[(4 * gq + j) * 128:(4 * gq + j + 1) * 128, :], shcol[:, 1, :]) for j in range(4)]
                groups.append({"ntok": 512, "load_a": load_a, "tiles_a": tiles_a, "tiles_b": tiles_b})
            if stage >= 4:
                def load_as(hT4, d_hT4, onT4, d_onT4, yrT4, d_yrT4):
                    P.dma("sp", lambda e: e.dma_start(out=hT4[:, :, 0:SB], in_=hTs_d), writes=[d_hT4])
                    P.dma("sp", lambda e: e.dma_start(out=onT4[:, :, 0:SB], in_=onTs_d), writes=[d_onT4])
                    P.dma("sp", lambda e: e.dma_start(out=yrT4[:, :, 0:SB], in_=yrTs_d), writes=[d_yrT4])
                groups.append({"ntok": SB, "load_a": load_as,
                               "tiles_a": lambda gtaS, d_gtaS: [(x_smp, SB, gtaS[:], d_gtaS, x1s_d)],
                               "tiles_b": lambda modS3, d_modS3: [(x1s_d, SB, modS3[:, 1, :], d_modS3, modS3[:, 0, :], d_modS3, modS3[:, 2, :], d_modS3, y_smp, None)]})
            phase_b2(groups)

        P.barrier()
        P.emit()
    return nc


def _rope_tables(pos):
    pos = np.asarray(pos, np.float32)
    out = np.zeros((len(pos), 192), np.float32)
    inv32 = (10000.0 ** (-np.arange(0, 64, 2, dtype=np.float32) / 64)).astype(np.float32)
    inv64 = (10000.0 ** (-np.arange(0, 128, 2, dtype=np.float32) / 128)).astype(np.float32)
    a32 = pos[:, None] * inv32[None, :]
    a64 = pos[:, None] * inv64[None, :]
    out[:, 0:32] = np.cos(a32)
    out[:, 32:64] = np.sin(a32)
    out[:, 64:128] = np.cos(a64)
    out[:, 128:192] = np.sin(a64)
    return out


def _consts(core):
    cs = {}
    cs["ident"] = np.eye(128, dtype=np.float32)
    sel5 = np.zeros((5, 132), np.float32)
    sel5[0, 0:128] = 1.0
    for s in range(4):
        sel5[1 + s, 128 + s] = 1.0
    cs["sel5"] = sel5
    inda = np.zeros((128, 8), np.float32)
    inda[np.arange(128), np.arange(128) // 16] = 1.0
    cs["inda"] = inda
    i = np.arange(128, dtype=np.float64)
    kdsc = np.zeros((128, 4), np.float32)
    for h in range(4):
        kdsc[:, h] = RET_SCALE * np.exp((127.0 - i) * LG[h])
    cs["kdsc"] = kdsc
    oh = np.zeros((128, 8), np.float32)
    oh[:, core] = 1.0
    cs["oh"] = oh
    return cs


def _core_tables(c):
    q = np.arange(128)
    n = np.arange(1024)
    j = np.arange(256)
    k = np.arange(128)
    cmask = np.zeros((NB, 128, 1024), np.float32)
    forceb = np.zeros((NB, 128, 256), np.float32)
    for i in range(NB):
        t = 8 * i + c
        qpos = 128 * t + q
        ok = (16 * n[None, :] + 31 <= qpos[:, None]) & (n[None, :] <= 1022)
        cmask[i] = np.where(ok, 0.0, NEG)
        valid = 64 * j[None, :] <= qpos[:, None]
        qb = qpos[:, None] // 64
        forced = (j[None, :] == 0) | (j[None, :] == qb) | (j[None, :] == qb - 1)
        forceb[i] = np.where(valid, np.where(forced, 1.0e4, 0.0), -1.0e30)
    dmask = np.zeros((128, 8, 128), np.float32)
    for jj in range(8):
        if jj == c:
            dmask[:, jj, :] = np.where(k[None, :] <= q[:, None], 0.0, NEG)
        elif jj > c:
            dmask[:, jj, :] = NEG
    wmask = np.zeros((128, 24, 128), np.float32)
    for v in range(2):
        for m in range(12):
            diff = 128 * (m - 4 - c) + k[None, :] - q[:, None]
            ok = (diff <= 0) & (diff > -512)
            if v == 0 and m < 4:
                ok = np.zeros_like(ok)
            wmask[:, v * 12 + m, :] = np.where(ok, 0.0, NEG)
    return {"cmask": cmask, "forceb": forceb, "dmask": dmask.reshape(128, 1024), "wmask": wmask.reshape(128, 24 * 128)}


def _shared_tables():
    nl = np.arange(128)
    blk = np.arange(256)
    ovp = np.zeros((128, 8, 256), np.float32)
    for jt in range(8):
        n = 128 * jt + nl
        ovp[:, jt, :] = ((4 * blk[None, :] - 1 <= n[:, None]) & (n[:, None] <= 4 * blk[None, :] + 3)).astype(np.float32)
    i = np.arange(128)
    dtab = np.zeros((128, 4, 128), np.float32)
    qdt = np.zeros((128, 4, 128), np.float32)
    for h in range(4):
        diff = i[None, :] - i[:, None]
        dtab[:, h, :] = np.where(diff >= 0, np.exp(np.maximum(diff, 0) * LG[h]), 0.0)
        qdt[:, h, :] = np.exp((i[None, :] + 1.0) * LG[h])
    p = np.arange(128)
    ovs = np.zeros((128, 8, 257), np.float32)
    for s_ in range(8):
        n = 8 * p + s_
        ovs[:, s_, 0:256] = ((4 * blk[None, :] - 1 <= n[:, None]) & (n[:, None] <= 4 * blk[None, :] + 3)).astype(np.float32)
    ovs[:, :, 256] = 1.0
    smallc = np.zeros((1, 1024), np.float32)
    smallc[0, 127] = NEG
    smallc[0, 128] = NEG
    for b in range(4):
        for s_ in range(4):
            smallc[0, 256 + 4 * b + s_] = 0.0 if s_ == b else NEG
    smallc[0, 272:276] = 1.0
    smallc[0, 276 + 0] = 1.0e4
    smallc[0, 276 + 255] = 1.0e4
    ohm = np.zeros((128, 4, 4), np.float32)
    for b in range(4):
        ohm[:, b, b] = 1.0
    return {"ovp": ovp.reshape(128, 2048), "dtab": dtab.reshape(128, 512), "qdt": qdt.reshape(128, 512),
            "ovs": ovs.reshape(128, 8 * 257), "smallc": smallc, "ohm": ohm.reshape(128, 16), "oh4": np.eye(4, dtype=np.float32),
            "rope_smp": _rope_tables(np.full(4, P_PAST))}


def _prep(inputs, stage=99):
    f = lambda a: np.ascontiguousarray(np.asarray(a, dtype=np.float32))
    x_all = f(inputs["x_prompt"]).reshape(T, D)
    w_cmp = f(inputs["w_cmp"])[0]
    b_cmp = f(inputs["b_cmp"])[0]
    wrep = np.zeros((128, 512), np.float32)
    pm = np.arange(128) % 16
    for r in range(2):
        blk = w_cmp[:, :, r * 16 + pm, :]
        wrep[:, r * 256:(r + 1) * 256] = blk.transpose(2, 0, 1, 3).reshape(128, 256)
    bcol = np.ascontiguousarray(b_cmp.reshape(2, 128).T)
    rope_all = _rope_tables(np.arange(T))
    shared = {
        "x_all": x_all, "rope_all": rope_all,
        "w_ada": f(inputs["w_ada"])[0], "b_ada": f(inputs["b_ada"]).reshape(1, 6 * D),
        "w_in": f(inputs["w_in"])[0],
        "norm_mix": f(inputs["norm_mix"]).reshape(1, D), "norm_mlp": f(inputs["norm_mlp"]).reshape(1, D),
        "wrep": wrep, "bcol": bcol,
        "w_bn": f(inputs["w_branch_nsa"])[0], "w_br": f(inputs["w_branch_ret"])[0], "w_out": f(inputs["w_out"])[0],
        "w_up": f(inputs["w_up"])[0], "w_down": f(inputs["w_down"])[0], "norm_final": f(inputs["norm_final"]).reshape(1, D),
        "gnw": f(inputs["ret_gn_w"]).reshape(1, D), "gnb": f(inputs["ret_gn_b"]).reshape(1, D),
    }
    shared.update(_shared_tables())
    shared["wrow"] = np.ascontiguousarray(w_cmp.reshape(2, 2, 2, 16, 64).transpose(2, 3, 0, 1, 4).reshape(1, 2 * 16 * 256))
    shared["brow"] = np.ascontiguousarray(b_cmp.reshape(1, 256))
    shared["cache_cmp"] = f(inputs["cache_cmp_kv"]).reshape(5120 * 8, 4096)
    shared["cache_sel"] = f(inputs["cache_sel_kv"]).reshape(5120 * 8, 4096)
    ptab_all = np.ascontiguousarray(np.asarray(inputs["page_table"], dtype=np.int32))
    cwin = f(inputs["cache_win_kv"])[0].reshape(32, 512, 256)
    sret = f(inputs["state_ret"])[0]
    maps = []
    cp = f(inputs["c_prompt"]).reshape(1, D)
    cs_ = f(inputs["c_sample"])
    for c in range(NCORE):
        m = dict(shared)
        m.update(_consts(c))
        m.update(_core_tables(c))
        tiles = [8 * i + c for i in range(NB)]
        m["x_own"] = np.ascontiguousarray(x_all.reshape(NT, 128, D)[tiles].reshape(NB * 128, D))
        m["rope_own"] = np.ascontiguousarray(rope_all.reshape(NT, 128, 192)[tiles].reshape(NB * 128, 192))
        m["ptab"] = ptab_all[4 * c:4 * c + 4]
        m["cache_win"] = cwin[4 * c:4 * c + 4]
        m["state_s"] = sret[4 * c:4 * c + 4]
        m["x_smp"] = f(inputs["x_sample"]).reshape(32, D)[4 * c:4 * c + 4]
        m["c_all"] = np.ascontiguousarray(np.concatenate([cp, cs_[4 * c:4 * c + 4]], axis=0).reshape(5, 8, 128).transpose(2, 1, 0).reshape(128, 40))
        maps.append(m)
    return maps


_NC_CACHE = {}


def run(inputs, stage=99):
    if stage not in _NC_CACHE:
        _NC_CACHE[stage] = build(stage)
    nc = _NC_CACHE[stage]
    maps = _prep(inputs, stage)
    res = run_bass_kernel_spmd(nc, maps, core_ids=list(range(NCORE)))
    return res.results


def kernel(**inputs):
    r = run(inputs)
    r0 = r[0]
    y = np.zeros((NT, 128, D), np.float32)
    for c in range(NCORE):
        yo = np.asarray(r[c]["y_own"]).reshape(NB, 128, D)
        for i in range(NB):
            y[8 * i + c] = yo[i]
    y_prompt = y.reshape(1, T, D)
    y_sample = np.concatenate([np.asarray(r[c]["y_smp"]) for c in range(NCORE)], axis=0).reshape(32, 1, D)
    new_cmp_p = np.asarray(r0["o_cmp"]).reshape(1, 1, T, 2, 2, 64)
    new_sel_p = np.asarray(r0["o_sel"]).reshape(1, 1, T, 2, 2, 64)
    new_win_p = np.asarray(r0["o_win"]).reshape(1, 1, 512, 2, 2, 64)
    new_state_p = np.asarray(r0["o_state"]).reshape(1, 1, 4, 128, 256)
    cat = lambda k: np.concatenate([np.asarray(r[c][k]) for c in range(NCORE)], axis=0)
    new_cmp_s = cat("o_cmp_s").reshape(1, 32, 1, 2, 2, 64)
    new_sel_s = cat("o_sel_s").reshape(1, 32, 1, 2, 2, 64)
    new_win_s = cat("o_win_s").reshape(1, 32, 512, 2, 2, 64)
    new_state_s = cat("o_state_s").reshape(1, 32, 4, 128, 256)
    return (y_prompt, y_sample, new_cmp_p, new_sel_p, new_win_p, new_state_p, new_cmp_s, new_sel_s, new_win_s, new_state_s)
```
